# Optimizing a Trainium2 kernel written in Bass

```python
import math
import jax
import jax.numpy as jnp
from jax import lax
import numpy as np

D_MODEL = 1024
BATCH = 16
SEQ = 256
DEPTH = 2
DEC_BATCH = 4
DEC_SEQ = 1024
PAST_LEN = 256

GRID_W = 64
HEAD_DIM = 64
D_MIX = D_MODEL
GROUP_W = D_MIX // 4
H_A = GROUP_W // HEAD_DIM
H_B = GROUP_W // HEAD_DIM
KV_B = H_B // 2
H_C = GROUP_W // HEAD_DIM
KV_C = H_C // 2
D_HY = GROUP_W
HY_ORDER = 2
HY_IN = (HY_ORDER + 1) * D_HY
SHORT_CONV = 3
FILTER_EMB = 33
FILTER_FW = 64
HY_FAST_DECAY = 0.3
HY_SLOW_DECAY = 1.5
HY_TARGET = 0.01
D_FF = 256 * ((8 * D_MODEL // 3 + 255) // 256)
N_MOD = 9
MLSTM_CHUNK = 64
Q_BLOCK = 128
WINDOW = 128
ROPE_BASE = 10000.0
EPS = 1e-6
IN_SIZES = (GROUP_W, GROUP_W, GROUP_W, GROUP_W, 2 * H_A, 2 * H_A,
            GROUP_W, KV_B * HEAD_DIM, KV_B * HEAD_DIM,
            GROUP_W, KV_C * HEAD_DIM, KV_C * HEAD_DIM,
            HY_IN)
N_IN = sum(IN_SIZES)
F32 = jnp.float32

kernel_name = 'hybrid_diffusion_trunk_step'


def rmsnorm(x, g):
    xf = x.astype(F32)
    y = xf * lax.rsqrt(jnp.mean(xf * xf, -1, keepdims=True) + EPS)
    return (y * g.astype(F32)).astype(x.dtype)


def swiglu(h, wg, wu, wd):
    return (jax.nn.silu(h @ wg) * (h @ wu)) @ wd


def split_in(u):
    idx = np.cumsum(IN_SIZES)[:-1].tolist()
    return jnp.split(u, idx, axis=-1)


def _heads(a, nh):
    return a.reshape(a.shape[0], a.shape[1], nh, HEAD_DIM)


def _sink(s):
    return s.astype(F32).reshape(KV_C, H_C // KV_C, 1, 1)


def grid_positions(n):
    rows = n // GRID_W
    r, c = jnp.meshgrid(jnp.arange(rows), jnp.arange(GRID_W), indexing='ij')
    return r.reshape(-1).astype(F32), c.reshape(-1).astype(F32)


def rope_1d(x, pos):
    half = x.shape[-1] // 2
    inv = ROPE_BASE ** (-jnp.arange(half, dtype=F32) / half)
    ang = pos[:, None] * inv[None, :]
    cos = jnp.cos(ang)[None, :, None, :]
    sin = jnp.sin(ang)[None, :, None, :]
    xf = x.astype(F32)
    x1, x2 = xf[..., :half], xf[..., half:]
    return jnp.concatenate([x1 * cos - x2 * sin, x2 * cos + x1 * sin], -1).astype(x.dtype)


def rope_2d(x, row, col):
    h = x.shape[-1] // 2
    return jnp.concatenate([rope_1d(x[..., :h], row), rope_1d(x[..., h:], col)], -1)


def attn_probs(s, sink):
    if sink is None:
        return jax.nn.softmax(s, axis=-1)
    m = jnp.maximum(jnp.max(s, -1, keepdims=True), sink)
    e = jnp.exp(s - m)
    return e / (jnp.sum(e, -1, keepdims=True) + jnp.exp(sink - m))


def attend_blocks(q, k, v, sink):
    b, n, h, d = q.shape
    kv = k.shape[2]
    g = h // kv
    nb = n // Q_BLOCK
    qb = (q * d ** -0.5).reshape(b, nb, Q_BLOCK, kv, g, d).transpose(1, 0, 2, 3, 4, 5)

    def one(qblk):
        s = jnp.einsum('bqkgd,bmkd->bkgqm', qblk, k).astype(F32)
        p = attn_probs(s, sink)
        return jnp.einsum('bkgqm,bmkd->bqkgd', p.astype(v.dtype), v)

    o = lax.map(one, qb)
    return o.transpose(1, 0, 2, 3, 4, 5).reshape(b, n, h * d)


def swa_banded(q, k, v, k_ctx, v_ctx, sink):
    b, n, h, d = q.shape
    kv = k.shape[2]
    g = h // kv
    nb = n // Q_BLOCK
    side = -(-WINDOW // Q_BLOCK)
    padn = side * Q_BLOCK

    def band(a):
        ap = jnp.pad(a, ((0, 0), (padn, padn), (0, 0), (0, 0))).reshape(b, nb + 2 * side, Q_BLOCK, kv, d)
        return jnp.concatenate([ap[:, j:j + nb] for j in range(2 * side + 1)], axis=2)

    kw, vw = band(k), band(v)
    mw = (2 * side + 1) * Q_BLOCK
    qpos = jnp.arange(nb)[:, None] * Q_BLOCK + jnp.arange(Q_BLOCK)[None, :]
    kpos = jnp.arange(nb)[:, None] * Q_BLOCK - padn + jnp.arange(mw)[None, :]
    kp = kpos[:, None, :]
    valid = (jnp.abs(kp - qpos[:, :, None]) <= WINDOW) & (kp >= 0) & (kp < n)
    qb = (q * d ** -0.5).reshape(b, nb, Q_BLOCK, kv, g, d)
    s_ctx = jnp.einsum('bnqkgd,bmkd->bnkgqm', qb, k_ctx).astype(F32)
    s_win = jnp.einsum('bnqkgd,bnmkd->bnkgqm', qb, kw).astype(F32)
    s_win = jnp.where(valid[None, :, None, None], s_win, -jnp.inf)
    p = attn_probs(jnp.concatenate([s_ctx, s_win], -1), sink)
    lc = k_ctx.shape[1]
    o = (jnp.einsum('bnkgqm,bmkd->bnqkgd', p[..., :lc].astype(v.dtype), v_ctx)
         + jnp.einsum('bnkgqm,bnmkd->bnqkgd', p[..., lc:].astype(v.dtype), vw))
    return o.reshape(b, n, h * d)


def mlstm_chunked(q, k, v, ig, fg, state):
    b, h, L, dk = q.shape
    nc = L // MLSTM_CHUNK
    k = k * (dk ** -0.5)
    logf = jax.nn.log_sigmoid(fg)

    def chunks(a):
        return jnp.moveaxis(a.reshape(b, h, nc, MLSTM_CHUNK, *a.shape[3:]), 2, 0)

    tri = jnp.arange(MLSTM_CHUNK)[:, None] >= jnp.arange(MLSTM_CHUNK)[None, :]

    def step(carry, inp):
        C, n, m = carry
        qc, kc, vc, ic, lf = inp
        bcum = jnp.cumsum(lf, -1)
        dmat = jnp.where(tri, bcum[..., :, None] - bcum[..., None, :] + ic[..., None, :], -jnp.inf)
        inter = bcum + m[..., None]
        mt = jnp.maximum(jnp.max(dmat, -1), inter)
        s = jnp.einsum('bhtd,bhsd->bhts', qc, kc) * jnp.exp(dmat - mt[..., None])
        w_inter = jnp.exp(inter - mt)
        num = jnp.einsum('bhts,bhse->bhte', s, vc) + w_inter[..., None] * jnp.einsum('bhtd,bhde->bhte', qc, C)
        den = jnp.sum(s, -1) + w_inter * jnp.einsum('bhtd,bhd->bht', qc, n)
        hc = num / jnp.maximum(jnp.abs(den), jnp.exp(-mt))[..., None]
        btot = bcum[..., -1]
        wlog = btot[..., None] - bcum + ic
        m_new = jnp.maximum(btot + m, jnp.max(wlog, -1))
        wk = jnp.exp(wlog - m_new[..., None])
        decay = jnp.exp(btot + m - m_new)
        C_new = decay[..., None, None] * C + jnp.einsum('bhs,bhsd,bhse->bhde', wk, kc, vc)
        n_new = decay[..., None] * n + jnp.einsum('bhs,bhsd->bhd', wk, kc)
        return (C_new, n_new, m_new), hc

    final, hs = lax.scan(step, state, (chunks(q), chunks(k), chunks(v), chunks(ig), chunks(logf)))
    hs = jnp.moveaxis(hs, 0, 2).reshape(b, h, L, v.shape[-1])
    return hs, final


def mlstm_bidir(q, k, v, o, ig, fg, state_f, state_b, p):
    b, L, _ = q.shape

    def heads(a):
        return a.astype(F32).reshape(b, L, H_A, HEAD_DIM).transpose(0, 2, 1, 3)

    gb = p['b_gates'].astype(F32)

    def gates(a, bias):
        return (a.astype(F32) + bias).reshape(b, L, 2, H_A).transpose(2, 0, 3, 1)

    igh = gates(ig, gb[:2 * H_A])
    fgh = gates(fg, gb[2 * H_A:])
    qh, kh, vh = heads(q), heads(k), heads(v)
    h_f, st_f = mlstm_chunked(qh, kh, vh, igh[0], fgh[0], state_f)

    def rev(a):
        return jnp.flip(a, axis=2)

    h_b, st_b = mlstm_chunked(rev(qh), rev(kh), rev(vh), rev(igh[1]), rev(fgh[1]), state_b)
    hs = (h_f + rev(h_b)).transpose(0, 2, 1, 3)
    hs = hs * lax.rsqrt(jnp.mean(hs * hs, -1, keepdims=True) + EPS)
    y = hs.reshape(b, L, GROUP_W) * p['g_mlstm'].astype(F32) * jax.nn.sigmoid(o.astype(F32))
    return y.astype(q.dtype), st_f, st_b


def hyena_filter(L, p):
    pos = jnp.arange(L, dtype=F32)
    bands = (FILTER_EMB - 1) // 2
    t01 = pos / max(L - 1, 1)
    ang = (2.0 * math.pi / L) * pos[:, None] * jnp.linspace(1e-4, bands - 1, bands, dtype=F32)[None, :]
    feats = jnp.concatenate([t01[:, None], jnp.cos(ang), -jnp.sin(ang)], -1)
    fr = p['filt_freq'].astype(F32)
    z = jnp.sin(fr * (feats @ p['filt_w1'].astype(F32) + p['filt_b1'].astype(F32)))
    z = jnp.sin(fr * (z @ p['filt_w2'].astype(F32) + p['filt_b2'].astype(F32)))
    filt = z @ p['filt_w3'].astype(F32) + p['filt_b3'].astype(F32)
    centre = L // 2
    dist = jnp.abs(pos - centre) / max(centre, 1)
    deltas = jnp.abs(jnp.linspace(math.log(HY_TARGET) / HY_SLOW_DECAY, math.log(HY_TARGET) / HY_FAST_DECAY, D_HY, dtype=F32))
    return filt * jnp.exp(-dist[:, None] * deltas[None, :])


def hyena_mixer(u, p):
    b, L, _ = u.shape
    half = SHORT_CONV // 2
    up = jnp.pad(u, ((0, 0), (half, half), (0, 0)))
    cw = p['conv_w']
    uc = sum(cw[j] * up[:, j:j + L] for j in range(SHORT_CONV)) + p['conv_b']
    x0, x1, v = jnp.split(uc.astype(F32), HY_ORDER + 1, axis=-1)
    z = x1 * v
    n_fft = 2 * L
    zf = jnp.fft.rfft(z, n=n_fft, axis=1)
    hf = jnp.fft.rfft(hyena_filter(L, p), n=n_fft, axis=0)
    yf = jnp.fft.irfft(zf * hf[None], n=n_fft, axis=1)
    y = yf[:, L // 2:L // 2 + L] + z * p['hyena_bias'].astype(F32)
    return (x0 * y).astype(u.dtype)


def mix_context(h, p):
    b = h.shape[0]
    aq, ak, av, ao, ai, af, bq, bk, bv, cq, ck, cv, du = split_in(h @ p['w_in'])
    zero = (jnp.zeros((b, H_A, HEAD_DIM, HEAD_DIM), F32), jnp.zeros((b, H_A, HEAD_DIM), F32), jnp.zeros((b, H_A), F32))
    ya, st_f, st_b = mlstm_bidir(aq, ak, av, ao, ai, af, zero, zero, p)
    kB = rmsnorm(_heads(bk, KV_B), p['g_knorm'])
    vB = _heads(bv, KV_B)
    yb = attend_blocks(rmsnorm(_heads(bq, H_B), p['g_qnorm']), kB, vB, None)
    kC = _heads(ck, KV_C)
    vC = _heads(cv, KV_C)
    yc = attend_blocks(_heads(cq, H_C), kC, vC, _sink(p['sinks']))
    yd = hyena_mixer(du, p)
    y = jnp.concatenate([ya, yb, yc, yd], -1) @ p['w_out']
    sC = jnp.stack([st_f[0], st_b[0]], axis=1)
    sn = jnp.stack([st_f[1], st_b[1]], axis=1)
    sm = jnp.stack([st_f[2], st_b[2]], axis=1)
    return y, (sC, sn, sm, kB, vB, kC, vC)


def mix_latent(h, cache, p):
    b, n, _ = h.shape
    row, col = grid_positions(n)
    sC, sn, sm, gk, gv, sk, sv = cache
    aq, ak, av, ao, ai, af, bq, bk, bv, cq, ck, cv, du = split_in(h @ p['w_in'])
    st_f = (sC[:, 0].astype(F32), sn[:, 0].astype(F32), sm[:, 0].astype(F32))
    st_b = (sC[:, 1].astype(F32), sn[:, 1].astype(F32), sm[:, 1].astype(F32))
    ya, _, _ = mlstm_bidir(aq, ak, av, ao, ai, af, st_f, st_b, p)
    qB = rope_2d(rmsnorm(_heads(bq, H_B), p['g_qnorm']), row, col)
    kB = rope_2d(rmsnorm(_heads(bk, KV_B), p['g_knorm']), row, col)
    vB = _heads(bv, KV_B)
    yb = attend_blocks(qB, jnp.concatenate([gk.astype(kB.dtype), kB], 1),
                       jnp.concatenate([gv.astype(vB.dtype), vB], 1), None)
    qC = rope_2d(_heads(cq, H_C), row, col)
    kC = rope_2d(_heads(ck, KV_C), row, col)
    vC = _heads(cv, KV_C)
    yc = swa_banded(qC, kC, vC, sk.astype(kC.dtype), sv.astype(vC.dtype), _sink(p['sinks']))
    yd = hyena_mixer(du, p)
    return jnp.concatenate([ya, yb, yc, yd], -1) @ p['w_out'], None


def trunk_layer(x, mod, p, mixer):
    def part(i):
        return mod[:, i][:, None, :]

    h = rmsnorm(x, p['g_ff1']) * (1 + part(1)) + part(0)
    x = x + 0.5 * part(2) * swiglu(h, p['w1_gate'], p['w1_up'], p['w1_down'])
    h = rmsnorm(x, p['g_mix']) * (1 + part(4)) + part(3)
    y, ctx_state = mixer(h)
    x = x + part(5) * y
    h = rmsnorm(x, p['g_ff2']) * (1 + part(7)) + part(6)
    x = x + 0.5 * part(8) * swiglu(h, p['w2_gate'], p['w2_up'], p['w2_down'])
    return x, ctx_state


def setup_inputs(seed: int = 0) -> dict:
    key = jax.random.key(seed)
    keys = list(jax.random.split(key, 64))

    def nrm(shape, scale=1.0):
        return jax.random.normal(keys.pop(), shape, F32) * scale

    def gain(shape):
        return 1.0 + nrm(shape, 0.02)

    L = DEPTH
    i_bias = nrm((L, 2 * H_A), 0.1)
    f_bias = jnp.tile(jnp.linspace(3.0, 6.0, H_A), 2)[None, :] + nrm((L, 2 * H_A), 0.1)
    return {
        'x_prompt': nrm((BATCH, SEQ, D_MODEL)),
        'x_sample': nrm((DEC_BATCH, DEC_SEQ, D_MODEL)),
        'state_mlstm_C': nrm((DEC_BATCH, DEPTH, 2, H_A, HEAD_DIM, HEAD_DIM), 0.3),
        'state_mlstm_n': nrm((DEC_BATCH, DEPTH, 2, H_A, HEAD_DIM), 0.3),
        'state_mlstm_m': nrm((DEC_BATCH, DEPTH, 2, H_A)),
        'cache_gattn_k': nrm((DEC_BATCH, DEPTH, PAST_LEN, KV_B, HEAD_DIM)),
        'cache_gattn_v': nrm((DEC_BATCH, DEPTH, PAST_LEN, KV_B, HEAD_DIM)),
        'cache_swa_k': nrm((DEC_BATCH, DEPTH, PAST_LEN, KV_C, HEAD_DIM)),
        'cache_swa_v': nrm((DEC_BATCH, DEPTH, PAST_LEN, KV_C, HEAD_DIM)),
        'c': nrm((DEC_BATCH, D_MODEL)),
        'c_ctx': nrm((D_MODEL,)),
        'w_ada': nrm((L, D_MODEL, N_MOD * D_MODEL), 0.5 * D_MODEL ** -0.5),
        'b_ada': nrm((L, N_MOD * D_MODEL), 0.01),
        'g_ff1': gain((L, D_MODEL)),
        'w1_gate': nrm((L, D_MODEL, D_FF), D_MODEL ** -0.5),
        'w1_up': nrm((L, D_MODEL, D_FF), D_MODEL ** -0.5),
        'w1_down': nrm((L, D_FF, D_MODEL), D_FF ** -0.5),
        'g_mix': gain((L, D_MODEL)),
        'w_in': nrm((L, D_MODEL, N_IN), D_MODEL ** -0.5),
        'b_gates': jnp.concatenate([i_bias, f_bias], -1),
        'g_mlstm': gain((L, GROUP_W)),
        'g_qnorm': gain((L, HEAD_DIM)),
        'g_knorm': gain((L, HEAD_DIM)),
        'sinks': nrm((L, H_C), 0.5),
        'conv_w': nrm((L, SHORT_CONV, HY_IN), 0.5),
        'conv_b': nrm((L, HY_IN), 0.02),
        'filt_w1': nrm((L, FILTER_EMB, FILTER_FW), FILTER_EMB ** -0.5),
        'filt_b1': nrm((L, FILTER_FW), 0.1),
        'filt_w2': nrm((L, FILTER_FW, FILTER_FW), FILTER_FW ** -0.5),
        'filt_b2': nrm((L, FILTER_FW), 0.1),
        'filt_w3': nrm((L, FILTER_FW, D_HY), 0.05 * FILTER_FW ** -0.5),
        'filt_b3': nrm((L, D_HY), 0.01),
        'filt_freq': 1.0 + nrm((L, FILTER_FW), 0.1),
        'hyena_bias': nrm((L, D_HY), 0.1),
        'w_out': nrm((L, D_MIX, D_MODEL), D_MIX ** -0.5),
        'g_ff2': gain((L, D_MODEL)),
        'w2_gate': nrm((L, D_MODEL, D_FF), D_MODEL ** -0.5),
        'w2_up': nrm((L, D_MODEL, D_FF), D_MODEL ** -0.5),
        'w2_down': nrm((L, D_FF, D_MODEL), D_FF ** -0.5),
        'g_final': gain((D_MODEL,)),
    }


def reference(x_prompt, x_sample, state_mlstm_C, state_mlstm_n, state_mlstm_m,
              cache_gattn_k, cache_gattn_v, cache_swa_k, cache_swa_v, c, c_ctx,
              w_ada, b_ada, g_ff1, w1_gate, w1_up, w1_down, g_mix, w_in, b_gates,
              g_mlstm, g_qnorm, g_knorm, sinks, conv_w, conv_b, filt_w1, filt_b1,
              filt_w2, filt_b2, filt_w3, filt_b3, filt_freq, hyena_bias, w_out,
              g_ff2, w2_gate, w2_up, w2_down, g_final):
    new_C, new_n, new_m, new_gk, new_gv, new_sk, new_sv = [], [], [], [], [], [], []
    xp, xs = x_prompt, x_sample
    for l in range(DEPTH):
        p = {
            'g_ff1': g_ff1[l], 'w1_gate': w1_gate[l], 'w1_up': w1_up[l], 'w1_down': w1_down[l],
            'g_mix': g_mix[l], 'w_in': w_in[l], 'b_gates': b_gates[l], 'g_mlstm': g_mlstm[l],
            'g_qnorm': g_qnorm[l], 'g_knorm': g_knorm[l], 'sinks': sinks[l],
            'conv_w': conv_w[l], 'conv_b': conv_b[l],
            'filt_w1': filt_w1[l], 'filt_b1': filt_b1[l], 'filt_w2': filt_w2[l], 'filt_b2': filt_b2[l],
            'filt_w3': filt_w3[l], 'filt_b3': filt_b3[l], 'filt_freq': filt_freq[l],
            'hyena_bias': hyena_bias[l], 'w_out': w_out[l],
            'g_ff2': g_ff2[l], 'w2_gate': w2_gate[l], 'w2_up': w2_up[l], 'w2_down': w2_down[l],
        }
        mod_ctx = (jax.nn.silu(c_ctx) @ w_ada[l] + b_ada[l]).reshape(1, N_MOD, -1)
        mod_lat = (jax.nn.silu(c) @ w_ada[l] + b_ada[l]).reshape(c.shape[0], N_MOD, -1)
        xp, st = trunk_layer(xp, mod_ctx, p, lambda h: mix_context(h, p))
        new_C.append(st[0]); new_n.append(st[1]); new_m.append(st[2])
        new_gk.append(st[3]); new_gv.append(st[4]); new_sk.append(st[5]); new_sv.append(st[6])
        cache_l = (state_mlstm_C[:, l], state_mlstm_n[:, l], state_mlstm_m[:, l],
                   cache_gattn_k[:, l], cache_gattn_v[:, l], cache_swa_k[:, l], cache_swa_v[:, l])
        xs, _ = trunk_layer(xs, mod_lat, p, lambda h: mix_latent(h, cache_l, p))
    y_prompt = rmsnorm(xp, g_final)
    y_sample = rmsnorm(xs, g_final)
    return (y_prompt, y_sample,
            jnp.stack(new_C, axis=1), jnp.stack(new_n, axis=1), jnp.stack(new_m, axis=1),
            jnp.stack(new_gk, axis=1), jnp.stack(new_gv, axis=1),
            jnp.stack(new_sk, axis=1), jnp.stack(new_sv, axis=1))
```

```python
import os
import math
import numpy as np
import concourse.bass as bass
import concourse.mybir as mybir
from concourse.bass_utils import run_bass_kernel_spmd

F32 = mybir.dt.float32
BF16 = mybir.dt.bfloat16
I32 = mybir.dt.int32
AF = mybir.ActivationFunctionType
ALU = mybir.AluOpType
AX = mybir.AxisListType

D = 1024
T = 1024
DFF = 2816
NFF = 22
NL = 2
NIN = 2832
EPS = 1e-6
NEG = -30000.0
BIG = 29952.0
LN8 = math.log(0.125)
SLOT = 8 * 656
STAGE = int(os.environ.get("MK_STAGE", "99"))
DEBUG = bool(os.environ.get("MK_DEBUG"))
CUT = int(os.environ.get("MK_CUT", "99"))


class R:
    __slots__ = ("name", "w", "rd", "excl")

    def __init__(self, name="", excl=False):
        self.name = name
        self.w = None
        self.rd = []
        self.excl = excl


def RL(n, name=""):
    return [R("%s%d" % (name, i)) for i in range(n)]


class Sched:
    NLANES = {"sp": 8, "pool": 6}

    def __init__(self, nc):
        self.nc = nc
        self.eng = {"pe": nc.tensor, "act": nc.scalar, "dve": nc.vector,
                    "pool": nc.gpsimd, "sp": nc.sync}
        self.sem = {}
        self.cnt = {}
        for k in self.eng:
            self.sem[k] = nc.alloc_semaphore(name="s_" + k)
            self.cnt[k] = 0
        self.lanes = {}
        for q, n in self.NLANES.items():
            self.lanes[q] = []
            for i in range(n):
                key = "d_%s%d" % (q, i)
                self.sem[key] = nc.alloc_semaphore(name=key)
                self.cnt[key] = 0
                self.lanes[q].append(key)
        self.lane_rr = {q: 0 for q in self.NLANES}
        self.waited = {k: {} for k in self.eng}
        self.nwaits = 0
        self.nops = 0

    def _wait(self, e, key, val):
        if key == "pe" and e == "pe":
            return
        w = self.waited[e]
        if w.get(key, 0) >= val:
            return
        self.eng[e].wait_ge(self.sem[key], val)
        w[key] = val
        self.nwaits += 1

    def _deps(self, e, reads, writes):
        deps = {}
        for r in reads:
            if r.w is not None:
                k, v = r.w
                if deps.get(k, 0) < v:
                    deps[k] = v
            if r.excl:
                for (k, v) in r.rd:
                    if k != e and deps.get(k, 0) < v:
                        deps[k] = v
        for w in writes:
            if w.w is not None:
                k, v = w.w
                if deps.get(k, 0) < v:
                    deps[k] = v
            for (k, v) in w.rd:
                if deps.get(k, 0) < v:
                    deps[k] = v
        for k, v in deps.items():
            self._wait(e, k, v)

    def _commit(self, tok, reads, writes):
        for r in reads:
            r.rd.append(tok)
            if len(r.rd) > 48:
                mx = {}
                for k, v in r.rd:
                    if mx.get(k, 0) < v:
                        mx[k] = v
                r.rd = list(mx.items())
        for w in writes:
            w.w = tok
            w.rd = []

    def op(self, e, fn, reads=(), writes=()):
        self._deps(e, reads, writes)
        inst = fn(self.eng[e])
        self.cnt[e] += 1
        inst.then_inc(self.sem[e], 1)
        self._commit((e, self.cnt[e]), reads, writes)
        self.nops += 1

    def dma(self, q, out, in_, reads=(), writes=()):
        lanes = self.lanes[q]
        key = lanes[self.lane_rr[q] % len(lanes)]
        self.lane_rr[q] += 1
        self._wait(q, key, self.cnt[key])
        self._deps(q, reads, writes)
        inst = self.eng[q].dma_start(out=out, in_=in_)
        self.cnt[key] += 16
        inst.then_inc(self.sem[key], 16)
        self._commit((key, self.cnt[key]), reads, writes)
        self.nops += 1

    def barrier(self):
        for e in self.eng:
            for k in self.sem:
                if self.cnt[k] > 0:
                    self._wait(e, k, self.cnt[k])

    def finish(self):
        for k in self.sem:
            if self.cnt[k] > 0 and k != "sp":
                self._wait("sp", k, self.cnt[k])


def act(S, out, in_, func, reads, writes, bias=0.0, scale=1.0):
    S.op("act", lambda e: e.activation(out=out, in_=in_, func=func, bias=bias, scale=scale), reads, writes)


def tt(S, eng, out, a, b, op, reads, writes):
    S.op(eng, lambda e: e.tensor_tensor(out=out, in0=a, in1=b, op=op), reads, writes)


def ts(S, eng, out, a, s1, op0, reads, writes, s2=None, op1=None):
    if op1 is None:
        S.op(eng, lambda e: e.tensor_scalar(out=out, in0=a, scalar1=s1, scalar2=None, op0=op0), reads, writes)
    else:
        S.op(eng, lambda e: e.tensor_scalar(out=out, in0=a, scalar1=s1, scalar2=s2, op0=op0, op1=op1), reads, writes)


def cp(S, eng, out, in_, reads, writes):
    S.op(eng, lambda e: e.tensor_copy(out=out, in_=in_), reads, writes)


def recip(S, out, in_, reads, writes):
    S.op("dve", lambda e: e.reciprocal(out=out, in_=in_), reads, writes)


def mm(S, groups, reads, writes):
    def fn(e):
        inst = None
        for (o, l, r, st, sp_) in groups:
            inst = e.matmul(o, lhsT=l, rhs=r, start=st, stop=sp_)
        return inst
    S.op("pe", fn, reads, writes)


class Prog:
    def __init__(self):
        nc = bass.Bass("TRN2", target_bir_lowering=False)
        self.nc = nc
        self.S = Sched(nc)
        self.din = {}
        self.dout = {}
        self._build()

    def inp(self, name, shape):
        t = self.nc.dram_tensor(name, list(shape), F32, kind="ExternalInput").ap()
        self.din[name] = tuple(shape)
        return t

    def outp(self, name, shape):
        t = self.nc.dram_tensor(name, list(shape), F32, kind="ExternalOutput").ap()
        self.dout[name] = tuple(shape)
        return t

    def sb(self, name, shape, dt=F32):
        return self.nc.alloc_sbuf_tensor("sb_" + name, list(shape), dt)

    def arena_reset(self, to=0):
        self.aoff = to
        self.S.barrier()

    def carve(self, shape, dt=F32):
        n = 1
        for s in shape[1:]:
            n *= s
        nb = n * (4 if dt in (F32, I32) else 2)
        nb = (nb + 31) // 32 * 32
        off = self.aoff
        self.aoff += nb
        assert self.aoff <= self.ARENA_BYTES, (self.aoff, self.ARENA_BYTES)
        v = self.arena[:, off // 2:(off + nb) // 2]
        if dt != BF16:
            v = v.bitcast(dt)
        v = v[:, 0:n]
        if len(shape) == 3:
            v = v.rearrange("p (a b) -> p a b", a=shape[1])
        elif len(shape) == 4:
            v = v.rearrange("p (a b c) -> p a b c", a=shape[1], b=shape[2])
        return v

    def bank(self):
        return self.pbank()

    def wload(self, parts):
        i = self.slot_rr % len(self.slots)
        self.slot_rr += 1
        sl, r = self.slots[i], self.slotr[i]
        for (dst_fn, src) in parts:
            self.S.dma("pool", dst_fn(sl), src, writes=[r])
        return sl, r

    def slot_view(self, sl, kk):
        return sl[:, 0:kk * self.slot_w].rearrange("p (k n) -> p k n", k=kk)

    def _build(self):
        nc, S = self.nc, self.S
        inp, outp, sb = self.inp, self.outp, self.sb
        xT_d = inp("xT", [D, T])
        cvec_d = inp("cvec", [128, 8])
        w_ada = inp("w_ada", [NL, D, 9 * D])
        b_adaT = inp("b_adaT", [NL, 128, 72])
        gvec_d = inp("gvec", [128, NL * 24 + 8])
        wd = {}
        for nm, shp in (("w1_gate", [NL, D, DFF]), ("w1_up", [NL, D, DFF]), ("w1_down", [NL, DFF, D]),
                        ("w_in", [NL, D, NIN]), ("w_out", [NL, D, D]),
                        ("w2_gate", [NL, D, DFF]), ("w2_up", [NL, D, DFF]), ("w2_down", [NL, DFF, D])):
            wd[nm] = inp(nm, shp)
        yT_d = outp("yT", [D, T])

        xT = sb("xT", [128, 8, T], F32)
        hT = sb("hT", [128, 8, T], BF16)
        self.ARENA_BYTES = 84 * 1024
        self.arena = sb("arena", [128, self.ARENA_BYTES // 2], BF16)
        self.slot_w = 656
        self.slots = [sb("slot%d" % i, [128, SLOT], BF16) for i in range(4)]
        self.slotr = RL(4, "slot")
        self.slot_rr = 0
        self.ps = [nc.alloc_psum_tensor("ps%d" % i, [128, 512], F32) for i in range(8)]
        self.psr = [R("ps%d" % i, excl=True) for i in range(8)]
        self.bank_rr = 0
        self.bank_pool = list(range(8))
        ones_bf = sb("ones_bf", [128, 128], BF16)
        cvec = sb("cvec_sb", [128, 8], F32)
        sc_bf = sb("sc_bf", [128, 8], BF16)
        modT = sb("modT", [128, NL, 72], F32)
        badaT = sb("badaT", [128, NL, 72], F32)
        gvec = sb("gvec_sb", [128, NL * 24 + 8], F32)
        Acoef = sb("Acoef", [128, NL, 3, 8], F32)
        Gcoef = sb("Gcoef", [128, NL, 3, 8], F32)
        sqb = sb("sqb", [128, 4, 512], BF16)
        f32s = sb("f32s", [128, 3, 512], F32)
        rstd = sb("rstd", [128, 512], F32)
        r_ones, r_cvec, r_sc, r_gvec, r_rstd = R("ones"), R("cvec"), R("sc"), R("gvec"), R("rstd")
        r_mod = RL(NL, "mod")
        r_bada = R("bada")
        r_coef = RL(NL, "coef")
        r_sq = RL(4, "sq")
        r_f32s = RL(3, "f32s")
        xr = [[R("x%d_%d" % (k, c)) for c in range(2)] for k in range(8)]
        hr = [[R("h%d_%d" % (k, c)) for c in range(2)] for k in range(8)]
        self.sq_rr = 0
        self.f32_rr = 0

        def CH(c):
            return slice(c * 512, (c + 1) * 512)

        xv = xT_d.rearrange("(k p) t -> p k t", p=128)
        for k in range(8):
            S.dma("sp", xT[:, k, :], xv[:, k, :], writes=[xr[k][0], xr[k][1]])
        S.dma("sp", cvec[:], cvec_d, writes=[r_cvec])
        S.dma("sp", gvec[:], gvec_d, writes=[r_gvec])
        for l in range(NL):
            S.dma("sp", badaT[:, l, :], b_adaT[l], writes=[r_bada])
        S.op("dve", lambda e: e.memset(ones_bf[:], 1.0), writes=[r_ones])
        act(S, sc_bf[:], cvec[:], AF.Silu, [r_cvec], [r_sc])

        for l in range(NL):
            pb, pr = self.bank()
            wv = w_ada[l].rearrange("(k p) n -> p k n", p=128)
            for cg in range(18):
                sl, sr = self.wload([(lambda sl_: self.slot_view(sl_, 8)[:, :, 0:512], wv[:, :, cg * 512:(cg + 1) * 512])])
                s3 = self.slot_view(sl, 8)
                groups = []
                for j in range(4):
                    col = cg * 4 + j
                    for k in range(8):
                        groups.append((pb[:, col:col + 1], s3[:, k, j * 128:(j + 1) * 128], sc_bf[:, k:k + 1],
                                       k == 0, k == 7))
                mm(S, groups, [sr, r_sc], [pr])
            tt(S, "dve", modT[:, l, :], pb[:, 0:72], badaT[:, l, :], ALU.add, [pr, r_bada], [r_mod[l]])
            for s in range(3):
                ts(S, "dve", Acoef[:, l, s, :], modT[:, l, (3 * s + 1) * 8:(3 * s + 2) * 8], 1.0, ALU.add,
                   [r_mod[l]], [r_coef[l]])
                tt(S, "dve", Acoef[:, l, s, :], Acoef[:, l, s, :], gvec[:, l * 24 + s * 8:l * 24 + s * 8 + 8],
                   ALU.mult, [r_coef[l], r_gvec], [r_coef[l]])
                ts(S, "dve", Gcoef[:, l, s, :], modT[:, l, (3 * s + 2) * 8:(3 * s + 3) * 8],
                   0.5 if s != 1 else 1.0, ALU.mult, [r_mod[l]], [r_coef[l]])

        def rms_rstd(c, src_fn, src_regs, nk, inv_n, ones_l):
            pb, pr = self.bank()
            for k in range(nk):
                i = self.sq_rr % 4
                self.sq_rr += 1
                act(S, sqb[:, i, :], src_fn(k), AF.Square, [src_regs[k]], [r_sq[i]])
                mm(S, [(pb[:], ones_l, sqb[:, i, :], k == 0, k == nk - 1)], [r_sq[i], r_ones], [pr])
            act(S, rstd[:], pb[:], AF.Sqrt, [pr], [r_rstd], bias=EPS, scale=inv_n)
            recip(S, rstd[:], rstd[:], [r_rstd], [r_rstd])

        def norm_mod(l, s):
            for c in range(2):
                rms_rstd(c, lambda k: xT[:, k, CH(c)], [xr[k][c] for k in range(8)], 8, 1.0 / D, ones_bf[:])
                for k in range(8):
                    i = self.f32_rr % 3
                    self.f32_rr += 1
                    tt(S, "dve", f32s[:, i, :], xT[:, k, CH(c)], rstd[:], ALU.mult,
                       [xr[k][c], r_rstd], [r_f32s[i]])
                    act(S, hT[:, k, CH(c)], f32s[:, i, :], AF.Identity, [r_f32s[i], r_coef[l], r_mod[l]],
                        [hr[k][c]], bias=modT[:, l, 3 * s * 8 + k:3 * s * 8 + k + 1],
                        scale=Acoef[:, l, s, k:k + 1])

        def resid_add(l, s, dt_, c, pb, pr):
            i = self.f32_rr % 3
            self.f32_rr += 1
            act(S, f32s[:, i, :], pb[:], AF.Copy, [pr, r_coef[l]], [r_f32s[i]], scale=Gcoef[:, l, s, dt_:dt_ + 1])
            tt(S, "dve", xT[:, dt_, CH(c)], xT[:, dt_, CH(c)], f32s[:, i, :], ALU.add,
               [xr[dt_][c], r_f32s[i]], [xr[dt_][c]])

        def ffn(l, s, wg, wu, wdn):
            self.arena_reset()
            aT = self.carve([128, NFF, T], BF16)
            ar = [[R("a%d_%d" % (j, c)) for c in range(2)] for j in range(NFF)]
            sg = [self.carve([128, 512], F32) for _ in range(3)]
            r_sg = RL(3, "sg")
            sg_rr = 0
            norm_mod(l, s)
            wgv = wg[l].rearrange("(k p) n -> p k n", p=128)
            wuv = wu[l].rearrange("(k p) n -> p k n", p=128)
            for g in range(NFF // 2):
                c0 = g * 256
                sl, sr = self.wload([(lambda sl_: self.slot_view(sl_, 8)[:, :, 0:256], wgv[:, :, c0:c0 + 256]),
                                     (lambda sl_: self.slot_view(sl_, 8)[:, :, 256:512], wuv[:, :, c0:c0 + 256])])
                s3 = self.slot_view(sl, 8)
                for jj in range(2):
                    j = g * 2 + jj
                    for c in range(2):
                        pg, prg = self.bank()
                        pu, pru = self.bank()
                        groups = []
                        for k in range(8):
                            groups.append((pg[:], s3[:, k, jj * 128:(jj + 1) * 128], hT[:, k, CH(c)], k == 0, k == 7))
                        for k in range(8):
                            groups.append((pu[:], s3[:, k, 256 + jj * 128:256 + (jj + 1) * 128], hT[:, k, CH(c)],
                                           k == 0, k == 7))
                        mm(S, groups, [sr] + [hr[k][c] for k in range(8)], [prg, pru])
                        i = sg_rr % 3
                        sg_rr += 1
                        act(S, sg[i], pg[:], AF.Silu, [prg], [r_sg[i]])
                        tt(S, "dve", aT[:, j, CH(c)], sg[i], pu[:], ALU.mult, [r_sg[i], pru], [ar[j][c]])
            wdv = wdn[l].rearrange("(j p) n -> p j n", p=128)
            for dt_ in range(8):
                sl, sr = self.wload([(lambda sl_: sl_[:, 0:NFF * 128].rearrange("p (j n) -> p j n", j=NFF),
                                      wdv[:, :, dt_ * 128:(dt_ + 1) * 128])])
                s3 = sl[:, 0:NFF * 128].rearrange("p (j n) -> p j n", j=NFF)
                for c in range(2):
                    pb, pr = self.bank()
                    groups = [(pb[:], s3[:, j, :], aT[:, j, CH(c)], j == 0, j == NFF - 1) for j in range(NFF)]
                    mm(S, groups, [sr] + [ar[j][c] for j in range(NFF)], [pr])
                    resid_add(l, s, dt_, c, pb, pr)

        self.wd = wd
        self.ctx = dict(xT=xT, hT=hT, xr=xr, hr=hr, CH=CH, rms_rstd=rms_rstd, norm_mod=norm_mod,
                        resid_add=resid_add, ones_bf=ones_bf, r_ones=r_ones, gvec=gvec, r_gvec=r_gvec,
                        f32s=f32s, r_f32s=r_f32s, rstd=rstd, r_rstd=r_rstd, modT=modT, r_mod=r_mod, sqb=sqb, r_sq=r_sq)


        self.ARENA_BYTES = 84 * 1024
        LPW = 304
        self.LP = dict(BG=0, GM=16, GQK=272, SK=274, CV=278, HB=302)
        d = {}
        d["lp"] = inp("lp", [128, NL, LPW])
        d["cflags"] = inp("cflags", [128, 4])
        d["ident"] = inp("ident", [128, 128])
        d["blockones"] = inp("blockones", [128, 128])
        d["tri"] = inp("tri", [128, 4, 128])
        d["ropeP"] = inp("ropeP", [128, 128])
        d["ropeCS"] = inp("ropeCS", [128, 2, T])
        d["swamask"] = inp("swamask", [128, 8, 384])
        d["gmAB"] = inp("gmAB", [5, 2, T])
        d["fw1"] = inp("fw1", [33, NL, 64])
        d["fw2"] = inp("fw2", [64, NL, 64])
        d["fw3"] = inp("fw3", [64, NL, 256])
        d["fvec"] = inp("fvec", [64, NL, 3])
        d["fb3"] = inp("fb3", [1, NL, 256])
        d["featsT"] = inp("featsT", [33, T])
        d["window"] = inp("window", [128, 8, 256])
        d["dftF"] = inp("dftF", [2, T, T])
        d["dftG"] = inp("dftG", [2, T, T])
        d["sel"] = inp("sel", [4, 130])
        d["C0"] = inp("C0", [NL, 2, 128, 2, 65])
        d["m0rep"] = inp("m0rep", [128, NL, 2, 2])
        d["m0c"] = inp("m0c", [4, NL * 2])
        d["gkT"] = inp("gkT", [NL, 2, 128, 256])
        d["gvp"] = inp("gvp", [NL, 2, 128, 2, 192])
        d["skT"] = inp("skT", [NL, 2, 128, 256])
        d["svp"] = inp("svp", [NL, 2, 128, 2, 192])
        d["o_gk"] = outp("o_gk", [NL, 128, T])
        d["o_gv"] = outp("o_gv", [NL, T, 128])
        d["o_sk"] = outp("o_sk", [NL, 128, T])
        d["o_sv"] = outp("o_sv", [NL, T, 128])
        d["o_C"] = outp("o_C", [NL, 2, 4, 128, 2, 65])
        d["o_m"] = outp("o_m", [NL, 2, 4, 4])
        self.d = d
        if DEBUG:
            self.dbg_ymix = outp("dbg_ymix", [NL, 128, 8, T])
        k = {}
        k["lp"] = sb("lp", [128, NL, LPW]); k["cflags"] = sb("cflags", [128, 4])
        k["ident_f"] = sb("ident_f", [128, 128]); k["ident_bf"] = sb("ident_bf", [128, 128], BF16)
        k["blockones"] = sb("blockones", [128, 128], BF16)
        k["tri"] = sb("tri", [128, 4, 128]); k["ropeP"] = sb("ropeP", [128, 128])
        k["ones_f"] = sb("ones_f", [128, 128])
        k["fw1"] = sb("fw1", [33, NL, 64]); k["fw2"] = sb("fw2", [64, NL, 64]); k["fw3"] = sb("fw3", [64, NL, 256])
        k["fvec"] = sb("fvec", [64, NL, 3]); k["fb3"] = sb("fb3", [1, NL, 256]); k["fs"] = sb("fs", [64, NL, 4])
        k["sel"] = sb("sel", [4, 130]); k["m0rep"] = sb("m0rep", [128, NL, 2, 2]); k["m0c"] = sb("m0c", [4, NL * 2])
        k["Clo"] = sb("Clo", [128, 2, 2, 65]); k["Chi"] = sb("Chi", [128, 2, 2, 65])
        k["Cblo"] = sb("Cblo", [128, 2, 2, 66], BF16); k["Cbhi"] = sb("Cbhi", [128, 2, 2, 66], BF16)
        k["cvb"] = sb("cvb", [128, NL, 6, 2])
        self.k = k
        r_k = R("consts")
        self.r_k = r_k
        for nm in ("lp", "cflags", "tri", "ropeP", "fw1", "fw2", "fw3", "fvec", "fb3", "sel", "m0rep", "m0c"):
            S.dma("sp", k[nm][:], d[nm], writes=[r_k])
        S.dma("sp", k["ident_f"][:], d["ident"], writes=[r_k])
        S.dma("pool", k["ident_bf"][:], d["ident"], writes=[r_k])
        S.dma("pool", k["blockones"][:], d["blockones"], writes=[r_k])
        S.op("pool", lambda e: e.memset(k["ones_f"][:], 1.0), writes=[r_k])
        for nm in ("Clo", "Chi", "Cblo", "Cbhi"):
            S.op("pool", lambda e, nm=nm: e.memset(k[nm][:], 0.0), writes=[r_k])
        i2p = float(1.0 / (2 * math.pi))
        ts(S, "dve", k["fs"][:, :, 0:1], k["fvec"][:, :, 2:3], i2p, ALU.mult, [r_k], [r_k])
        tt(S, "dve", k["fs"][:, :, 1:2], k["fs"][:, :, 0:1], k["fvec"][:, :, 0:1], ALU.mult, [r_k], [r_k])
        tt(S, "dve", k["fs"][:, :, 2:3], k["fs"][:, :, 0:1], k["fvec"][:, :, 1:2], ALU.mult, [r_k], [r_k])
        CV = self.LP["CV"]
        for l in range(NL):
            cvv = k["lp"][:, l, CV:CV + 24].rearrange("p (a b) -> p a b", a=6)
            for (j, col) in ((0, 0), (1, 2)):
                ts(S, "dve", k["cvb"][:, l, :, j:j + 1], cvv[:, :, col:col + 1], k["cflags"][:, 2:3], ALU.mult,
                   [r_k], [r_k], s2=-1.0, op1=ALU.mult)

        for l in range(NL):
            ffn(l, 0, wd["w1_gate"], wd["w1_up"], wd["w1_down"])
            if STAGE >= 2:
                self.mixer(l)
            ffn(l, 2, wd["w2_gate"], wd["w2_up"], wd["w2_down"])

        gfo = NL * 24
        yv = yT_d.rearrange("(k p) t -> p k t", p=128)
        self.arena_reset()
        ost_ = self.carve([128, 2, 512], F32)
        ost = [ost_[:, 0, :], ost_[:, 1, :]]
        r_ost = RL(2, "ost")
        o_rr = 0
        for c in range(2):
            rms_rstd(c, lambda k: xT[:, k, CH(c)], [xr[k][c] for k in range(8)], 8, 1.0 / D, ones_bf[:])
            for k in range(8):
                i = self.f32_rr % 3
                self.f32_rr += 1
                tt(S, "dve", f32s[:, i, :], xT[:, k, CH(c)], rstd[:], ALU.mult, [xr[k][c], r_rstd], [r_f32s[i]])
                o = o_rr % 2
                o_rr += 1
                act(S, ost[o], f32s[:, i, :], AF.Copy, [r_f32s[i], r_gvec], [r_ost[o]],
                    scale=gvec[:, gfo + k:gfo + k + 1])
                S.dma("sp", yv[:, k, CH(c)], ost[o], reads=[r_ost[o]])
        S.finish()

    def mixer(self, l):
        S = self.S
        C = self.ctx
        hT, hr, CH = C["hT"], C["hr"], C["CH"]
        self.arena_reset()
        ymix = self.carve([128, 8, T], BF16)
        ymr = [[R("ym%d_%d" % (k, c)) for c in range(2)] for k in range(8)]
        base = self.aoff
        C["norm_mod"](l, 1)
        win = self.wd["w_in"][l].rearrange("(k p) n -> p k n", p=128)
        self.bank_pool = list(range(8))
        self.mix_mlstm(l, ymix, ymr, win)
        self.arena_reset(base)
        if STAGE >= 3:
            self.mix_attn(l, ymix, ymr, win, glob=True)
            self.arena_reset(base)
            self.mix_attn(l, ymix, ymr, win, glob=False)
            self.arena_reset(base)
        if STAGE >= 4:
            self.mix_hyena(l, ymix, ymr, win)
        self.bank_pool = list(range(8))
        if DEBUG:
            f32s, r_f32s = C["f32s"], C["r_f32s"]
            for k in range(8):
                for c in range(2):
                    i = self.f32_rr % 3
                    self.f32_rr += 1
                    act(S, f32s[:, i, :], ymix[:, k, CH(c)], AF.Copy, [ymr[k][c]], [r_f32s[i]])
                    S.dma("sp", self.dbg_ymix[l, :, k, c * 512:(c + 1) * 512], f32s[:, i, :], reads=[r_f32s[i]])
        wov = self.wd["w_out"][l].rearrange("(k p) n -> p k n", p=128)
        for half in range(2):
            sl, sr = self.wload([(lambda sl_: self.slot_view(sl_, 8)[:, :, 0:512], wov[:, :, half * 512:(half + 1) * 512])])
            s3 = self.slot_view(sl, 8)
            for j in range(4):
                dt_ = half * 4 + j
                for c in range(2):
                    pb, pr = self.bank()
                    groups = [(pb[:], s3[:, k, j * 128:(j + 1) * 128], ymix[:, k, CH(c)], k == 0, k == 7) for k in range(8)]
                    mm(S, groups, [sr] + [ymr[k][c] for k in range(8)], [pr])
                    C["resid_add"](l, 1, dt_, c, pb, pr)

    def pbank(self):
        i = self.bank_pool[self.bank_rr % len(self.bank_pool)]
        self.bank_rr += 1
        return self.ps[i], self.psr[i]

    def proj_fm(self, c, s3, col0, pb):
        hT, CH = self.ctx["hT"], self.ctx["CH"]
        return [(pb[:], s3[:, k, col0:col0 + 128], hT[:, k, CH(c)], k == 0, k == 7) for k in range(8)]

    def proj_tok(self, tt_, s3, col0, ncols, pb, pc0):
        hT = self.ctx["hT"]
        return [(pb[:, pc0:pc0 + ncols], hT[:, k, tt_ * 128:(tt_ + 1) * 128], s3[:, k, col0:col0 + ncols], k == 0, k == 7)
                for k in range(8)]

    def mix_mlstm(self, l, ymix, ymr, win):
        S, k_, d = self.S, self.k, self.d
        C = self.ctx
        hr, CH = C["hr"], C["CH"]
        LP = self.LP
        lp = k_["lp"]
        r_k = self.r_k
        hall = [hr[k][c] for k in range(8) for c in range(2)]
        sv8 = lambda sl_: self.slot_view(sl_, 8)
        slA1, srA1 = self.wload([(lambda sl_: sv8(sl_)[:, :, 0:512], win[:, :, 0:512])])
        slA2, srA2 = self.wload([(lambda sl_: sv8(sl_)[:, :, 0:512], win[:, :, 512:1024])])
        slG, srG = self.wload([(lambda sl_: sv8(sl_)[:, :, 0:16], win[:, :, 1024:1040])])
        sA1, sA2, sG = sv8(slA1), sv8(slA2), sv8(slG)
        if CUT == 1:
            return
        aqT = self.carve([128, 2, T], BF16)
        akp = self.carve([128, 4, T], BF16)
        ktok = self.carve([128, 8, 256], BF16)
        vaug = self.carve([128, 8, 4, 66], BF16)
        sgo = self.carve([128, 8, 256], BF16)
        gts = self.carve([128, 8, 16], F32)
        lf = self.carve([128, 8, 8], F32)
        cum = self.carve([128, 8, 16], F32)
        call = self.carve([128, 8, 8], F32)
        wall = self.carve([128, 8, 8], F32)
        wkl = self.carve([128, 8, 8], F32)
        wkall = self.carve([128, 8, 8], F32)
        wkm = self.carve([128, 8, 8], F32)
        dec = self.carve([128, 8, 4], F32)
        lfB = self.carve([128, 8, 128], F32)
        E = self.carve([128, 8, 128], F32)
        AT = self.carve([128, 2, 4, 128], BF16)
        hf = self.carve([128, 8, 256], F32)
        numt = self.carve([128, 2, 260], F32)
        tmpn = self.carve([128, 2, 260], F32)
        h64 = self.carve([128, 2, 256], F32)
        dsm = self.carve([128, 2, 8], F32)
        kwp = self.carve([128, 2, 4, 192], BF16)
        yatok = self.carve([128, 8, 256], BF16)
        snap = self.carve([128, 2, 4, 130], F32)
        e0 = self.carve([128, 2, 2], F32)
        ssall = self.carve([128, 8, 4], F32)
        sq1 = self.carve([128, 2, 256], F32)
        mst = self.carve([128, 64], F32)
        scl = self.carve([128, 2, 8], F32)
        r_aq, r_akp = RL(2, "aq"), RL(2, "akp")
        r_ktok, r_vaug, r_sgo, r_gts = RL(8, "ktok"), RL(8, "vaug"), RL(8, "sgo"), R("gts")
        r_gate = R("gate")
        r_lfB, r_E, r_AT = RL(2, "lfB"), RL(2, "E"), RL(2, "AT")
        r_hf = RL(8, "hf")
        r_num, r_tmpn, r_h64, r_dsm, r_kwp = RL(2, "num"), RL(2, "tmpn"), RL(2, "h64"), RL(2, "dsm"), RL(2, "kwp")
        r_C, r_Cb = RL(2, "C"), RL(2, "Cb")
        r_snap = [[R("snap") for _ in range(4)] for _ in range(2)]
        r_ms, r_scl = R("mst"), R("scl")
        r_ya = RL(8, "ya")
        r_ss = R("ss")
        r_sq1 = RL(2, "sq1")
        S.op("pool", lambda e: e.memset(akp, 0.0), writes=r_akp)
        S.op("pool", lambda e: e.memset(kwp, 0.0), writes=r_kwp)
        S.op("pool", lambda e: e.memset(vaug[:, :, :, 64:65], 1.0), writes=r_vaug)
        S.op("pool", lambda e: e.memset(hf, 0.0), writes=r_hf)
        if CUT == 2:
            return
        for t2 in range(2):
            for c in range(2):
                pb, pr = self.pbank()
                mm(S, self.proj_fm(c, sA1, t2 * 128, pb), [srA1] + hall, [pr])
                act(S, aqT[:, t2, CH(c)], pb[:], AF.Copy, [pr], [r_aq[c]])
        if CUT == 21:
            return
        for t2 in range(2):
            for c in range(2):
                pb, pr = self.pbank()
                mm(S, self.proj_fm(c, sA1, 256 + t2 * 128, pb), [srA1] + hall, [pr])
                act(S, akp[0:64, 2 * t2, CH(c)], pb[0:64, :], AF.Copy, [pr], [r_akp[c]])
                cp(S, "dve", akp[64:128, 2 * t2 + 1, CH(c)], pb[64:128, :], [pr], [r_akp[c]])
        if CUT == 22:
            return
        BG = LP["BG"]
        for t_ in range(8):
            p1, pr1 = self.pbank()
            p2, pr2 = self.pbank()
            g = self.proj_tok(t_, sA1, 256, 256, p1, 0) + self.proj_tok(t_, sA2, 0, 256, p1, 256)
            g += self.proj_tok(t_, sA2, 256, 256, p2, 0) + self.proj_tok(t_, sG, 0, 16, p2, 256)
            mm(S, g, [srA1, srA2, srG] + hall, [pr1, pr2])
            if CUT == 23:
                continue
            act(S, ktok[:, t_, :], p1[:, 0:256], AF.Copy, [pr1], [r_ktok[t_]])
            cp(S, "dve", vaug[:, t_, :, 0:64], p1[:, 256:512].rearrange("p (a b) -> p a b", a=4), [pr1], [r_vaug[t_]])
            if CUT == 24:
                continue
            act(S, sgo[:, t_, :], p2[:, 0:256], AF.Sigmoid, [pr2], [r_sgo[t_]])
            tt(S, "dve", gts[:, t_, :], p2[:, 256:272], lp[:, l, BG:BG + 16], ALU.add, [pr2, r_k], [r_gts])
        if CUT in (3, 23, 24):
            return
        ai, af = gts[:, :, 0:8], gts[:, :, 8:16]
        act(S, lf, af, AF.Exp, [r_gts], [r_gate], scale=-1.0)
        act(S, lf, lf, AF.Ln, [r_gate], [r_gate], bias=1.0)
        ts(S, "dve", lf, lf, -1.0, ALU.mult, [r_gate], [r_gate])
        tri = k_["tri"]
        pbc, prc = self.pbank()
        g = []
        for t_ in range(8):
            g.append((pbc[:, t_ * 16:t_ * 16 + 4], tri[:, 0, :], lf[:, t_, 0:4], True, True))
            g.append((pbc[:, t_ * 16 + 4:t_ * 16 + 8], tri[:, 1, :], lf[:, t_, 4:8], True, True))
            g.append((pbc[:, t_ * 16 + 8:t_ * 16 + 16], k_["ones_f"][:], lf[:, t_, 0:8], True, True))
        mm(S, g, [r_gate, r_k], [prc])
        cp(S, "dve", cum, pbc[:, 0:128].rearrange("p (a b) -> p a b", a=8), [prc], [r_gate])
        bc, bt = cum[:, :, 0:8], cum[:, :, 8:16]
        tt(S, "dve", call, ai, bc, ALU.subtract, [r_gts, r_gate], [r_gate])
        ts(S, "dve", call, call, LN8, ALU.add, [r_gate], [r_gate])
        act(S, wall, bc, AF.Exp, [r_gate], [r_gate])
        tt(S, "dve", wkl, call, bt, ALU.add, [r_gate], [r_gate])
        act(S, wkall, wkl, AF.Exp, [r_gate], [r_gate])
        ts(S, "dve", wkm, wkl, -LN8, ALU.add, [r_gate], [r_gate])
        for g_ in range(2):
            rows = slice(g_ * 64, (g_ + 1) * 64)
            act(S, dec[rows, :, :], cum[rows, :, 8 + g_:16:2], AF.Exp, [r_gate], [r_gate])
        if CUT == 4:
            return
        ident_f = k_["ident_f"]
        for dr in range(2):
            pbm, prm = self.pbank()
            g = [(pbm[0:4, t_:t_ + 1], lf[:, t_, dr * 4:dr * 4 + 4], k_["ones_f"][:, 0:1], True, True) for t_ in range(8)]
            mm(S, g, [r_gate, r_k], [prm])
            cp(S, "dve", mst[0:4, dr * 8:dr * 8 + 8], pbm[0:4, 0:8], [prm], [r_ms])
            for hh in range(2):
                pbt, prt = self.pbank()
                for q in range(4):
                    t_ = hh * 4 + q
                    S.op("pe", lambda e, t_=t_, q=q, pbt=pbt: e.transpose(pbt[0:4, q * 128:(q + 1) * 128],
                                                                          wkm[:, t_, dr * 4:dr * 4 + 4], ident_f[:]),
                         [r_gate, r_k], [prt])
                S.op("dve", lambda e, pbt=pbt, hh=hh: e.tensor_reduce(
                    out=mst[0:4, 16 + dr * 8 + hh * 4:16 + dr * 8 + hh * 4 + 4],
                    in_=pbt[0:4, :].rearrange("p (a b) -> p a b", a=4), axis=AX.X, op=ALU.max), [prt], [r_ms])
            bv_ = mst[0:4, dr * 8:dr * 8 + 8].rearrange("p (a b) -> p a b", a=4)
            av_ = mst[0:4, 16 + dr * 8:16 + dr * 8 + 8].rearrange("p (a b) -> p a b", a=4)
            fi, se = (0, 1) if dr == 0 else (1, 0)
            mf = mst[0:4, 32 + dr * 4:32 + dr * 4 + 4]
            ts(S, "dve", mf, bv_[:, :, fi], k_["m0c"][0:4, l * 2 + dr:l * 2 + dr + 1], ALU.add, [r_ms, r_k], [r_ms])
            tt(S, "dve", mf, mf, av_[:, :, fi], ALU.max, [r_ms], [r_ms])
            tt(S, "dve", mf, mf, bv_[:, :, se], ALU.add, [r_ms], [r_ms])
            tt(S, "dve", mf, mf, av_[:, :, se], ALU.max, [r_ms], [r_ms])
            S.dma("sp", d["o_m"][l, dr], mf, reads=[r_ms])
            en = mst[0:4, 40 + dr * 4:40 + dr * 4 + 4]
            act(S, en, mf, AF.Exp, [r_ms], [r_ms], scale=-1.0)
            rhs2 = mst[0:4, 48 + dr * 8:48 + dr * 8 + 8]
            tt(S, "dve", rhs2.rearrange("p (a b) -> p a b", a=4), en.unsqueeze(2).to_broadcast([4, 4, 2]),
               k_["sel"][0:4, 128:130].unsqueeze(1).to_broadcast([4, 4, 2]), ALU.mult, [r_ms, r_k], [r_ms])
            pbs, prs = self.pbank()
            mm(S, [(pbs[:, 0:8], k_["sel"][0:4, 0:128], rhs2, True, True)], [r_ms, r_k], [prs])
            cp(S, "dve", scl[:, dr, :], pbs[:, 0:8], [prs], [r_scl])
        if CUT == 5:
            return
        Clo, Chi, Cblo, Cbhi = k_["Clo"], k_["Chi"], k_["Cblo"], k_["Cbhi"]
        act(S, e0, k_["m0rep"][:, l, :, :], AF.Exp, [r_k], [r_gate])
        halves = ((slice(0, 64), Clo, Cblo), (slice(64, 128), Chi, Cbhi))
        for dr in range(2):
            for (rows, Cx, Cbx) in halves:
                S.dma("sp", Cx[rows, dr, :, :], d["C0"][l, dr, rows, :, :], writes=[r_C[dr]])
            for (rows, Cx, Cbx) in halves:
                tt(S, "dve", Cx[rows, dr, :, :], Cx[rows, dr, :, :], e0[rows, dr, :].unsqueeze(2).to_broadcast([64, 2, 65]),
                   ALU.mult, [r_C[dr], r_gate], [r_C[dr]])
                act(S, Cbx[rows, dr, :, 0:65], Cx[rows, dr, :, :], AF.Copy, [r_C[dr]], [r_Cb[dr]])
        if CUT == 6:
            return
        keep = k_["cflags"][:, 1:2]

        def process(dr, t_):
            tok = slice(t_ * 128, (t_ + 1) * 128)
            chs = slice(dr * 4, dr * 4 + 4)
            cp(S, "pool", lfB[:, chs, :], lf[:, t_, chs].unsqueeze(2).to_broadcast([128, 4, 128]), [r_gate], [r_lfB[dr]])
            pbe, pre = self.pbank()
            g = []
            for h in range(4):
                o = pbe[:, h * 128:(h + 1) * 128]
                g.append((o, lfB[:, dr * 4 + h, :], tri[:, dr, :], True, False))
                g.append((o, ident_f[:], tri[:, 2 + dr, :], False, True))
            mm(S, g, [r_lfB[dr], r_k], [pre])
            pbs_, prs_ = self.pbank()
            g = [(pbs_[:, h * 128:(h + 1) * 128], akp[:, h, tok], aqT[:, h // 2, tok], True, True) for h in range(4)]
            mm(S, g, r_akp + r_aq, [prs_])
            for h in range(4):
                act(S, E[:, dr * 4 + h, :], pbe[:, h * 128:(h + 1) * 128], AF.Exp, [pre, r_gate], [r_E[dr]],
                    bias=call[:, t_, dr * 4 + h:dr * 4 + h + 1])
            tt(S, "dve", AT[:, dr, :, :], pbs_[:].rearrange("p (a b) -> p a b", a=4), E[:, chs, :], ALU.mult,
               [prs_, r_E[dr]], [r_AT[dr]])
            pbi, pri = self.pbank()
            g = [(pbi[:, h * 65:(h + 1) * 65], AT[:, dr, h, :], vaug[:, t_, h, 0:65], True, True) for h in range(4)]
            mm(S, g, [r_AT[dr], r_vaug[t_]], [pri])
            pbx, prx = self.pbank()
            g = [(pbx[:, h * 65:(h + 1) * 65], aqT[:, h // 2, tok], (Cblo if h % 2 == 0 else Cbhi)[:, dr, h // 2, 0:65], True, True)
                 for h in range(4)]
            mm(S, g, r_aq + [r_Cb[dr]], [prx])
            v3 = lambda ap: ap.rearrange("p (a b) -> p a b", a=4)
            tt(S, "dve", v3(tmpn[:, dr, :]), v3(pbx[:, 0:260]), wall[:, t_, chs].unsqueeze(2).to_broadcast([128, 4, 65]),
               ALU.mult, [prx, r_gate], [r_tmpn[dr]])
            tt(S, "dve", numt[:, dr, :], pbi[:, 0:260], tmpn[:, dr, :], ALU.add, [pri, r_tmpn[dr]], [r_num[dr]])
            nv = v3(numt[:, dr, :])
            dn, rd = dsm[:, dr, 0:4], dsm[:, dr, 4:8]
            ts(S, "dve", dn, nv[:, :, 64], -1.0, ALU.mult, [r_num[dr]], [r_dsm[dr]], s2=1.0, op1=ALU.max)
            tt(S, "dve", dn, dn, nv[:, :, 64], ALU.max, [r_num[dr], r_dsm[dr]], [r_dsm[dr]])
            recip(S, rd, dn, [r_dsm[dr]], [r_dsm[dr]])
            tt(S, "dve", v3(h64[:, dr, :])[:, :, 0:64], nv[:, :, 0:64], rd.unsqueeze(2).to_broadcast([128, 4, 64]), ALU.mult,
               [r_num[dr], r_dsm[dr]], [r_h64[dr]])
            tt(S, "dve", hf[:, t_, :], hf[:, t_, :], h64[:, dr, :], ALU.add, [r_h64[dr], r_hf[t_]], [r_hf[t_]])
            tt(S, "pool", kwp[:, dr, :, 64:128], ktok[:, t_, :].rearrange("p (a b) -> p a b", a=4),
               wkall[:, t_, chs].unsqueeze(2).to_broadcast([128, 4, 64]), ALU.mult, [r_ktok[t_], r_gate], [r_kwp[dr]])
            pbd, prd = self.pbank()
            g = []
            for j in range(2):
                o = pbd[:, j * 65:(j + 1) * 65]
                g.append((o, kwp[:, dr, 2 * j, 64:192], vaug[:, t_, 2 * j, 0:65], True, False))
                g.append((o, kwp[:, dr, 2 * j + 1, 0:128], vaug[:, t_, 2 * j + 1, 0:65], False, True))
            mm(S, g, [r_kwp[dr], r_vaug[t_]], [prd])
            for (rows, Cx, Cbx) in halves:
                tt(S, "dve", Cx[rows, dr, :, :], Cx[rows, dr, :, :],
                   dec[rows, t_, dr * 2:dr * 2 + 2].unsqueeze(2).to_broadcast([64, 2, 65]), ALU.mult,
                   [r_C[dr], r_gate, r_Cb[dr]], [r_C[dr]])
                tt(S, "dve", Cx[rows, dr, :, :], Cx[rows, dr, :, :], pbd[rows, 0:130].rearrange("p (a b) -> p a b", a=2),
                   ALU.add, [r_C[dr], prd], [r_C[dr]])
            end = (t_ % 2 == 1) if dr == 0 else (t_ % 2 == 0)
            if end:
                sq_ = t_ // 2
                for (rows, Cx, Cbx) in halves:
                    tt(S, "dve", snap[rows, dr, sq_, :].rearrange("p (a b) -> p a b", a=2), Cx[rows, dr, :, :],
                       scl[rows, dr, sq_ * 2:sq_ * 2 + 2].unsqueeze(2).to_broadcast([64, 2, 65]), ALU.mult,
                       [r_C[dr], r_scl], [r_snap[dr][sq_]])
                S.dma("sp", d["o_C"][l, dr, sq_], snap[:, dr, sq_, :].rearrange("p (a b) -> p a b", a=2),
                      reads=[r_snap[dr][sq_]])
                for (rows, Cx, Cbx) in halves:
                    ts(S, "dve", Cx[rows, dr, :, :], Cx[rows, dr, :, :], keep[rows, :], ALU.mult, [r_C[dr], r_k], [r_C[dr]])
            for (rows, Cx, Cbx) in halves:
                act(S, Cbx[rows, dr, :, 0:65], Cx[rows, dr, :, :], AF.Copy, [r_C[dr]], [r_Cb[dr]])

        for i in range(8):
            process(0, i)
            process(1, 7 - i)
        if CUT == 7:
            return
        GM = LP["GM"]
        for t_ in range(8):
            b_ = t_ % 2
            tt(S, "dve", sq1[:, b_, :], hf[:, t_, :], hf[:, t_, :], ALU.mult, [r_hf[t_]], [r_sq1[b_]])
            S.op("dve", lambda e, t_=t_, b_=b_: e.tensor_reduce(out=ssall[:, t_, :],
                                                              in_=sq1[:, b_, :].rearrange("p (a b) -> p a b", a=4),
                                                              axis=AX.X, op=ALU.add), [r_sq1[b_]], [r_ss])
        act(S, ssall, ssall, AF.Sqrt, [r_ss], [r_ss], bias=EPS, scale=1.0 / 64)
        recip(S, ssall, ssall, [r_ss], [r_ss])
        ident_bf = k_["ident_bf"]
        for t_ in range(8):
            b_ = t_ % 2
            tt(S, "dve", sq1[:, b_, :].rearrange("p (a b) -> p a b", a=4), hf[:, t_, :].rearrange("p (a b) -> p a b", a=4),
               ssall[:, t_, :].unsqueeze(2).to_broadcast([128, 4, 64]), ALU.mult, [r_hf[t_], r_ss], [r_sq1[b_]])
            tt(S, "pool", h64[:, b_, :], sgo[:, t_, :], lp[:, l, GM:GM + 256], ALU.mult, [r_sgo[t_], r_k], [r_h64[b_]])
            tt(S, "dve", yatok[:, t_, :], sq1[:, b_, :], h64[:, b_, :], ALU.mult, [r_h64[b_], r_sq1[b_]], [r_ya[t_]])
            pbt, prt = self.pbank()
            pv = pbt[:].bitcast(BF16)
            for t2 in range(2):
                S.op("pe", lambda e, t2=t2, pv=pv, t_=t_: e.transpose(pv[:, t2 * 128:(t2 + 1) * 128],
                                                                      yatok[:, t_, t2 * 128:(t2 + 1) * 128], ident_bf[:]),
                     [r_ya[t_], r_k], [prt])
            c = t_ // 4
            for t2 in range(2):
                if t2 == 0:
                    act(S, ymix[:, t2, t_ * 128:(t_ + 1) * 128], pv[:, t2 * 128:(t2 + 1) * 128], AF.Copy, [prt], [ymr[t2][c]])
                else:
                    cp(S, "dve", ymix[:, t2, t_ * 128:(t_ + 1) * 128], pv[:, t2 * 128:(t2 + 1) * 128], [prt], [ymr[t2][c]])

    def mix_attn(self, l, ymix, ymr, win, glob):
        S, k_, d = self.S, self.k, self.d
        C = self.ctx
        hr, CH = C["hr"], C["CH"]
        LP = self.LP
        lp = k_["lp"]
        r_k = self.r_k
        hall = [hr[k][c] for k in range(8) for c in range(2)]
        sv8 = lambda sl_: self.slot_view(sl_, 8)
        q0 = 1040 if glob else 1552
        k0, v0 = q0 + 256, q0 + 384
        sl, sr = self.wload([(lambda sl_: sv8(sl_)[:, :, 0:256], win[:, :, q0:q0 + 256]),
                             (lambda sl_: sv8(sl_)[:, :, 256:384], win[:, :, k0:k0 + 128]),
                             (lambda sl_: sv8(sl_)[:, :, 384:448], win[:, :, k0 + 64:k0 + 128]),
                             (lambda sl_: sv8(sl_)[:, :, 448:512], win[:, :, k0:k0 + 64]),
                             (lambda sl_: sv8(sl_)[:, :, 512:640], win[:, :, v0:v0 + 128])])
        s3 = sv8(sl)
        ymb = 2 if glob else 4
        cs = self.carve([128, 2, T], F32)
        qpad = self.carve([128, 4, T], BF16)
        kfull = self.carve([128, 2, 1280], BF16)
        vpad = self.carve([128, 10, 2, 192], BF16)
        kout = self.carve([128, T], F32)
        vout = self.carve([128, 8, 128], F32)
        PT = self.carve([128, 3, 512], BF16)
        rden = self.carve([128, 2, 512], F32)
        raw = self.carve([128, 2, 512], F32)
        tb = self.carve([128, 2, 2, 512], F32)
        gm = self.carve([128, 2, T], BF16)
        smask = self.carve([128, 8, 384], BF16) if not glob else None
        es = self.carve([128, 4], F32)
        r_cs, r_qp, r_kf, r_vp, r_kout, r_vout = R("cs"), RL(2, "qp"), R("kf"), RL(10, "vp"), R("kout"), R("vout")
        r_PT, r_rden, r_raw, r_tb, r_gm, r_sm, r_es = RL(3, "PT"), RL(2, "rden"), RL(2, "raw"), RL(2, "tb"), R("gm"), R("sm"), R("es")
        S.dma("sp", cs, d["ropeCS"], writes=[r_cs])
        S.op("pool", lambda e: e.memset(qpad, 0.0), writes=r_qp)
        S.op("pool", lambda e: e.memset(vpad[:, 2:10, :, :], 0.0), writes=r_vp[2:])
        kTd, vpd = (d["gkT"], d["gvp"]) if glob else (d["skT"], d["svp"])
        for x in range(2):
            S.dma("pool", kfull[:, x, 0:256], kTd[l, x], writes=[r_kf])
            S.dma("pool", vpad[:, x, :, :], vpd[l, x], writes=[r_vp[x]])
        S.dma("pool", gm[0:5, :, :], d["gmAB"], writes=[r_gm])
        if not glob:
            S.dma("pool", smask, d["swamask"], writes=[r_sm])
            SK = LP["SK"]
            act(S, es, lp[:, l, SK:SK + 4], AF.Exp, [r_k], [r_es])
        GQK = LP["GQK"]
        ropeP = k_["ropeP"]
        for ti in range(4):
            col0 = ti * 128
            for c in range(2):
                pb, pr = self.pbank()
                mm(S, self.proj_fm(c, s3, col0, pb), [sr] + hall, [pr])
                b_ = (ti * 2 + c) % 2
                rw = raw[:, b_, :]
                if glob:
                    act(S, rw, pb[:], AF.Copy, [pr], [r_raw[b_]])
                    i = self.sq_rr % 4
                    self.sq_rr += 1
                    sqb, r_sq = C["sqb"], C["r_sq"]
                    act(S, sqb[:, i, :], rw, AF.Square, [r_raw[b_]], [r_sq[i]])
                    p2, pr2 = self.pbank()
                    mm(S, [(p2[:], k_["blockones"][:], sqb[:, i, :], True, True)], [r_sq[i], r_k], [pr2])
                    rstd, r_rstd = C["rstd"], C["r_rstd"]
                    act(S, rstd[:], p2[:], AF.Sqrt, [pr2], [r_rstd], bias=EPS, scale=1.0 / 64)
                    recip(S, rstd[:], rstd[:], [r_rstd], [r_rstd])
                    tt(S, "dve", rw, rw, rstd[:], ALU.mult, [r_raw[b_], r_rstd], [r_raw[b_]])
                    gcol = GQK + (0 if ti < 2 else 1)
                    act(S, rw, rw, AF.Copy, [r_raw[b_], r_k], [r_raw[b_]], scale=lp[:, l, gcol:gcol + 1])
                else:
                    act(S, rw, pb[:], AF.Copy, [pr], [r_raw[b_]])
                p3, pr3 = self.pbank()
                mm(S, [(p3[:], ropeP[:], rw, True, True)], [r_raw[b_], r_k], [pr3])
                ta, tb_ = tb[:, b_, 0, :], tb[:, b_, 1, :]
                tt(S, "pool", ta, rw, cs[:, 0, CH(c)], ALU.mult, [r_raw[b_], r_cs], [r_tb[b_]])
                tt(S, "dve", tb_, p3[:], cs[:, 1, CH(c)], ALU.mult, [pr3, r_cs], [r_tb[b_]])
                if ti < 2:
                    for g_ in range(2):
                        rows = slice(g_ * 64, (g_ + 1) * 64)
                        tt(S, "dve", qpad[rows, 2 * ti + g_, CH(c)], ta[rows, :], tb_[rows, :], ALU.add, [r_tb[b_]], [r_qp[c]])
                elif ti == 2:
                    tt(S, "dve", kout[:, CH(c)], ta, tb_, ALU.add, [r_tb[b_]], [r_kout])
                    act(S, kfull[:, 0, 256 + c * 512:256 + (c + 1) * 512], kout[:, CH(c)], AF.Copy, [r_kout], [r_kf])
                else:
                    tt(S, "dve", kfull[:, 1, 256 + c * 512:256 + (c + 1) * 512], ta, tb_, ALU.add, [r_tb[b_]], [r_kf])
        S.dma("sp", (d["o_gk"] if glob else d["o_sk"])[l], kout, reads=[r_kout])
        for t_ in range(8):
            pb, pr = self.pbank()
            mm(S, self.proj_tok(t_, s3, 512, 128, pb, 0), [sr] + hall, [pr])
            act(S, vout[:, t_, :], pb[:, 0:128], AF.Copy, [pr], [r_vout])
            cp(S, "dve", vpad[:, 2 + t_, :, 64:128], pb[:, 0:128].rearrange("p (a b) -> p a b", a=2), [pr], [r_vp[2 + t_]])
        S.dma("sp", (d["o_gv"] if glob else d["o_sv"])[l].rearrange("(t p) n -> p t n", p=128), vout, reads=[r_vout])
        self.bank_pool = [0, 1, 2, 3]
        ones_bf = C["ones_bf"]
        r_ones = C["r_ones"]
        ident_bf = k_["ident_bf"]
        ctxb = k_["cflags"][:, 0:1]
        pt_rr = 0
        acc_rr = 0
        for h in range(4):
            kv, g_ = h // 2, h % 2
            kx = 0 if g_ == kv else 1
            rows = slice(g_ * 64, (g_ + 1) * 64)
            vw = slice(64, 192) if g_ == 0 else slice(0, 128)
            for c in range(2):
                if glob:
                    tiles = [(mt, 0, 512) for mt in range(10)]
                else:
                    tiles = [(0, 0, 512), (1, 0, 512)]
                    for j in range(8):
                        lo, hi = max((j - 1) * 128, c * 512), min((j + 2) * 128, (c + 1) * 512)
                        if hi > lo:
                            tiles.append((2 + j, lo - c * 512, hi - c * 512))
                ai_ = 4 + 2 * (acc_rr % 2)
                acc_rr += 1
                pn, prn, pd, prd = self.ps[ai_], self.psr[ai_], self.ps[ai_ + 1], self.psr[ai_ + 1]
                pend = []

                def qk(idx):
                    mt, lo, hi = tiles[idx]
                    pb, pr = self.pbank()
                    qs = slice(c * 512 + lo, c * 512 + hi)
                    g = [(pb[:, lo:hi], kfull[:, kx, mt * 128:(mt + 1) * 128], qpad[:, h, qs], True, mt < 2)]
                    rd = [r_kf, r_qp[c]]
                    if mt >= 2:
                        if glob:
                            g.append((pb[:, lo:hi], gm[0:5, 0, (mt - 2) * 128:(mt - 1) * 128], gm[0:5, 1, qs], False, True))
                            rd.append(r_gm)
                        else:
                            j = mt - 2
                            m0_ = c * 512 + lo - (j - 1) * 128
                            g.append((pb[:, lo:hi], ident_bf[:], smask[:, j, m0_:m0_ + (hi - lo)], False, True))
                            rd += [r_sm, r_k]
                    mm(S, g, rd, [pr])
                    return pb, pr

                nt = len(tiles)
                look = 2
                for idx in range(min(look, nt)):
                    pend.append(qk(idx))
                for idx in range(nt):
                    mt, lo, hi = tiles[idx]
                    pb, pr = pend.pop(0)
                    if idx + look < nt:
                        pend.append(qk(idx + look))
                    pi = pt_rr % 3
                    pt_rr += 1
                    act(S, PT[:, pi, lo:hi], pb[:, lo:hi], AF.Exp, [pr, r_k], [r_PT[pi]],
                        bias=(ctxb if mt < 2 else 0.0), scale=0.125)
                    g = [(pn[:, lo:hi], vpad[:, mt, kv, vw], PT[:, pi, lo:hi], idx == 0, idx == nt - 1),
                         (pd[:, lo:hi], ones_bf[:], PT[:, pi, lo:hi], idx == 0, idx == nt - 1)]
                    mm(S, g, [r_vp[mt], r_PT[pi], r_ones], [prn, prd])
                ri = (h * 2 + c) % 2
                if glob:
                    recip(S, rden[rows, ri, :], pd[rows, :], [prd], [r_rden[ri]])
                else:
                    ts(S, "dve", rden[rows, ri, :], pd[rows, :], es[rows, h:h + 1], ALU.add, [prd, r_es], [r_rden[ri]])
                    recip(S, rden[rows, ri, :], rden[rows, ri, :], [r_rden[ri]], [r_rden[ri]])
                tt(S, "dve", ymix[rows, ymb + kv, CH(c)], pn[rows, :], rden[rows, ri, :], ALU.mult, [prn, r_rden[ri]],
                   [ymr[ymb + kv][c]])
        self.bank_pool = list(range(8))

    def mix_hyena(self, l, ymix, ymr, win):
        S, k_, d = self.S, self.k, self.d
        C = self.ctx
        hr, CH = C["hr"], C["CH"]
        LP = self.LP
        lp = k_["lp"]
        r_k = self.r_k
        hall = [hr[k][c] for k in range(8) for c in range(2)]
        sv8 = lambda sl_: self.slot_view(sl_, 8)
        TWO_PI = float(2 * math.pi)
        xoff = self.aoff
        feats = self.carve([128, T], F32)
        z1 = self.carve([128, T], F32)
        z2 = self.carve([128, T], F32)
        wnd = self.carve([128, 8, 256], F32)
        rr = self.carve([128, 512], F32)
        ii = self.carve([128, 512], I32)
        kf = self.carve([128, 512], F32)
        xend = self.aoff
        self.aoff = xoff
        raw = self.carve([128, 3, T], F32)
        uct = self.carve([128, 2, T], F32)
        pa = self.carve([128, 2, 256], F32)
        pq = self.carve([128, 2, 256], F32)
        yt = self.carve([128, 2, 256], F32)
        assert self.aoff <= xend
        self.aoff = xend
        r_X = R("X")
        x0 = self.carve([128, 2, T], F32)
        zf = self.carve([128, 2, T], F32)
        zbf = self.carve([128, 2, T], BF16)
        zh = self.carve([128, 8, 512], BF16)
        ZH = self.carve([128, 2, 512], F32)
        Y = self.carve([128, 8, 2, 256], BF16)
        r_x0, r_zf, r_zbf = RL(2, "x0"), RL(2, "zf"), RL(2, "zbf")
        r_zh = RL(8, "zh")
        r_ZH, r_Y, r_pa, r_pq, r_yt = R("ZH"), RL(8, "Y"), R("pa"), R("pq"), RL(2, "yt")
        S.dma("sp", feats[0:33, :], d["featsT"], writes=[r_X])
        S.dma("sp", wnd, d["window"], writes=[r_X])
        fs = k_["fs"]

        def sin_layer(pb, pr, dst, bcol):
            ts(S, "dve", rr[0:64, :], pb[0:64, :], fs[:, l, 0:1], ALU.mult, [pr, r_k], [r_X], s2=fs[:, l, bcol:bcol + 1],
               op1=ALU.add)
            cp(S, "dve", ii[0:64, :], rr[0:64, :], [r_X], [r_X])
            cp(S, "dve", kf[0:64, :], ii[0:64, :], [r_X], [r_X])
            tt(S, "dve", rr[0:64, :], rr[0:64, :], kf[0:64, :], ALU.subtract, [r_X], [r_X])
            act(S, dst, rr[0:64, :], AF.Sin, [r_X], [r_X], scale=TWO_PI)

        for c in range(2):
            pb, pr = self.pbank()
            mm(S, [(pb[0:64, :], k_["fw1"][0:33, l, :], feats[0:33, CH(c)], True, True)], [r_X, r_k], [pr])
            sin_layer(pb, pr, z1[0:64, CH(c)], 1)
        for c in range(2):
            pb, pr = self.pbank()
            mm(S, [(pb[0:64, :], k_["fw2"][0:64, l, :], z1[0:64, CH(c)], True, True)], [r_X, r_k], [pr])
            sin_layer(pb, pr, z2[0:64, CH(c)], 2)
        for t_ in range(8):
            pb, pr = self.pbank()
            tok = slice(t_ * 128, (t_ + 1) * 128)
            mm(S, [(pb[:, 0:256], z2[0:64, tok], k_["fw3"][0:64, l, :], True, False),
                   (pb[:, 0:256], k_["ones_f"][0:1, :], k_["fb3"][0:1, l, :], False, True)], [r_X, r_k], [pr])
            tt(S, "dve", zh[:, t_, 256:512], pb[:, 0:256], wnd[:, t_, :], ALU.mult, [pr, r_X], [r_zh[t_]])
        slD1, srD1 = self.wload([(lambda sl_: sv8(sl_)[:, :, 0:512], win[:, :, 2064:2576])])
        slD2, srD2 = self.wload([(lambda sl_: sv8(sl_)[:, :, 0:256], win[:, :, 2576:2832])])
        sD1, sD2 = sv8(slD1), sv8(slD2)
        CV = LP["CV"]
        cvb = k_["cvb"]
        for ct in range(2):
            srcs = ((sD1, ct * 128, srD1), (sD1, 256 + ct * 128, srD1), (sD2, ct * 128, srD2))
            for ui, (s3, col, srr) in enumerate(srcs):
                for c in range(2):
                    pb, pr = self.pbank()
                    mm(S, self.proj_fm(c, s3, col, pb), [srr] + hall, [pr])
                    act(S, raw[:, ui, CH(c)], pb[:], AF.Copy, [pr], [r_X])
            for ui in range(3):
                tile = ui * 2 + ct
                cw = lambda j: lp[:, l, CV + tile * 4 + j:CV + tile * 4 + j + 1]
                u = raw[:, ui, :]
                if ui == 0:
                    dst, wr = x0[:, ct, :], [r_x0[ct]]
                else:
                    dst, wr = uct[:, ui - 1, :], [r_X]
                rd = [r_X, r_k]
                act(S, dst, u, AF.Identity, rd, wr, bias=cw(3), scale=cw(1))
                stt = lambda o, a, sc, b: S.op("dve", lambda e: e.scalar_tensor_tensor(out=o, in0=a, scalar=sc, in1=b,
                                                                                      op0=ALU.mult, op1=ALU.add), rd + wr, wr)
                stt(dst[:, 1:T], u[:, 0:T - 1], cw(0), dst[:, 1:T])
                stt(dst[:, 0:T - 1], u[:, 1:T], cw(2), dst[:, 0:T - 1])
                stt(dst[:, 256:T:256], u[:, 255:T - 1:256], cvb[:, l, tile, 0:1], dst[:, 256:T:256])
                stt(dst[:, 255:T - 1:256], u[:, 256:T:256], cvb[:, l, tile, 1:2], dst[:, 255:T - 1:256])
            tt(S, "dve", zf[:, ct, :], uct[:, 0, :], uct[:, 1, :], ALU.mult, [r_X], [r_zf[ct]])
            act(S, zbf[:, ct, :], zf[:, ct, :], AF.Copy, [r_zf[ct]], [r_zbf[ct]])
        ident_bf = k_["ident_bf"]
        for t_ in range(8):
            pb, pr = self.pbank()
            pv = pb[:].bitcast(BF16)
            for ct in range(2):
                S.op("pe", lambda e, ct=ct, pv=pv, t_=t_: e.transpose(pv[:, ct * 128:(ct + 1) * 128],
                                                                      zbf[:, ct, t_ * 128:(t_ + 1) * 128], ident_bf[:]),
                     [r_zbf[ct], r_k], [pr])
            cp(S, "dve", zh[:, t_, 0:256], pv[:, 0:256], [pr], [r_zh[t_]])
        Fd, Gd = d["dftF"], d["dftG"]
        fv = lambda m: Fd[m].rearrange("(k p) n -> p k n", p=128)
        gv = lambda m: Gd[m].rearrange("(k p) n -> p k n", p=128)
        for q in range(4):
            sl, sr = self.wload([(lambda sl_: sv8(sl_)[:, :, 0:256], fv(0)[:, :, q * 256:(q + 1) * 256]),
                                 (lambda sl_: sv8(sl_)[:, :, 256:512], fv(1)[:, :, q * 256:(q + 1) * 256])])
            s3 = sv8(sl)
            for fj in range(2):
                ft = q * 2 + fj
                pre_, prr = self.pbank()
                pim, pri = self.pbank()
                g = [(pre_[:], s3[:, t_, fj * 128:(fj + 1) * 128], zh[:, t_, :], t_ == 0, t_ == 7) for t_ in range(8)]
                g += [(pim[:], s3[:, t_, 256 + fj * 128:256 + (fj + 1) * 128], zh[:, t_, :], t_ == 0, t_ == 7) for t_ in range(8)]
                mm(S, g, [sr] + r_zh, [prr, pri])
                act(S, ZH[:, 0, :], pre_[:], AF.Copy, [prr], [r_ZH])
                act(S, ZH[:, 1, :], pim[:], AF.Copy, [pri], [r_ZH])
                Zr, Hr, Zi, Hi = ZH[:, 0, 0:256], ZH[:, 0, 256:512], ZH[:, 1, 0:256], ZH[:, 1, 256:512]
                tt(S, "dve", pa[:, 0, :], Zr, Hr, ALU.mult, [r_ZH], [r_pa] + ([r_X] if ft == 0 else []))
                tt(S, "dve", pa[:, 1, :], Zi, Hi, ALU.mult, [r_ZH], [r_pa])
                tt(S, "dve", Y[:, ft, 0, :], pa[:, 0, :], pa[:, 1, :], ALU.subtract, [r_pa], [r_Y[ft]])
                tt(S, "pool", pq[:, 0, :], Zr, Hi, ALU.mult, [r_ZH], [r_pq] + ([r_X] if ft == 0 else []))
                tt(S, "pool", pq[:, 1, :], Zi, Hr, ALU.mult, [r_ZH], [r_pq])
                tt(S, "pool", Y[:, ft, 1, :], pq[:, 0, :], pq[:, 1, :], ALU.add, [r_pq], [r_Y[ft]])
        HB = LP["HB"]
        for q in range(4):
            sl, sr = self.wload([(lambda sl_: sv8(sl_)[:, :, 0:256], gv(0)[:, :, q * 256:(q + 1) * 256]),
                                 (lambda sl_: sv8(sl_)[:, :, 256:512], gv(1)[:, :, q * 256:(q + 1) * 256])])
            s3 = sv8(sl)
            ns = slice(q * 256, (q + 1) * 256)
            for ct in range(2):
                pb, pr = self.pbank()
                g = []
                for ft in range(8):
                    g.append((pb[:, 0:256], Y[:, ft, 0, ct * 128:(ct + 1) * 128], s3[:, ft, 0:256], ft == 0, False))
                    g.append((pb[:, 0:256], Y[:, ft, 1, ct * 128:(ct + 1) * 128], s3[:, ft, 256:512], False, ft == 7))
                mm(S, g, [sr] + r_Y, [pr])
                b_ = ct
                act(S, yt[:, b_, :], pb[:, 0:256], AF.Copy, [pr], [r_yt[b_]] + ([r_X] if q == 0 else []))
                S.op("dve", lambda e, b_=b_, ct=ct: e.scalar_tensor_tensor(out=yt[:, b_, :], in0=zf[:, ct, ns],
                                                                         scalar=lp[:, l, HB + ct:HB + ct + 1], in1=yt[:, b_, :],
                                                                         op0=ALU.mult, op1=ALU.add),
                     [r_zf[ct], r_yt[b_], r_k], [r_yt[b_]])
                tt(S, "dve", ymix[:, 6 + ct, ns], x0[:, ct, ns], yt[:, b_, :], ALU.mult, [r_x0[ct], r_yt[b_]],
                   [ymr[6 + ct][q // 2]])


def fm(v):
    return np.ascontiguousarray(np.asarray(v, np.float32).reshape(8, 128).T)


def _consts(kind):
    c = {}
    p = np.arange(128)
    t = np.arange(T)
    ident = np.eye(128, dtype=np.float32)
    c["ident"] = ident
    bo = np.zeros((128, 128), np.float32)
    bo[:64, :64] = 1
    bo[64:, 64:] = 1
    c["blockones"] = bo
    r_, t_ = np.meshgrid(p, p, indexing="ij")
    tri = np.zeros((128, 4, 128), np.float32)
    tri[:, 0, :] = (r_ <= t_)
    tri[:, 1, :] = (r_ >= t_)
    tri[:, 2, :] = np.where(r_ <= t_, 0.0, NEG)
    tri[:, 3, :] = np.where(r_ >= t_, 0.0, NEG)
    c["tri"] = tri
    P = np.zeros((128, 128), np.float32)
    for b in range(0, 128, 32):
        for i in range(16):
            P[b + i + 16, b + i] = -1.0
            P[b + i, b + i + 16] = 1.0
    c["ropeP"] = P
    cs = np.zeros((128, 2, T), np.float32)
    if kind == "s":
        dd = p % 64
        inv = (10000.0 ** (-(dd % 16).astype(np.float32) / np.float32(16))).astype(np.float32)
        row = (t // 64).astype(np.float32)
        col = (t % 64).astype(np.float32)
        pos = np.where((dd // 32)[:, None] == 0, row[None, :], col[None, :]).astype(np.float32)
        ang = (pos * inv[:, None]).astype(np.float32)
        cs[:, 0, :] = np.cos(ang)
        cs[:, 1, :] = np.sin(ang)
    else:
        cs[:, 0, :] = 1.0
    c["ropeCS"] = cs
    sm = np.full((128, 8, 384), NEG, np.float32)
    for j in range(8):
        m = j * 128 + p[:, None]
        q = (j - 1) * 128 + np.arange(384)[None, :]
        inr = (q >= 0) & (q < T)
        if kind == "s":
            ok = (np.abs(m - q) <= 128) & inr
        else:
            ok = ((m // 256) == (q // 256)) & inr
        sm[:, j, :] = np.where(ok, 0.0, NEG)
    c["swamask"] = sm
    gm = np.zeros((5, 2, T), np.float32)
    if kind == "p":
        gm[0, 0, :] = 1.0
        gm[0, 1, :] = -BIG
        for s_ in range(4):
            gm[1 + s_, 0, :] = (t // 256 == s_)
            gm[1 + s_, 1, :] = BIG * (t // 256 == s_)
    c["gmAB"] = gm
    c["cflags"] = np.tile(np.array([[0.0, 1.0, 0.0, 0.0]] if kind == "s" else [[NEG, 0.0, 1.0, 0.0]], np.float32), (128, 1))
    sel = np.zeros((4, 130), np.float32)
    for h in range(4):
        sel[h, 0:128] = ((p >= 64).astype(int) == (h % 2))
        sel[h, 128 + h // 2] = 1.0
    c["sel"] = sel
    L = T if kind == "s" else 256
    rep = T // L
    pos = np.arange(L, dtype=np.float32)
    t01 = pos / np.float32(max(L - 1, 1))
    lin = np.linspace(1e-4, 15.0, 16, dtype=np.float32)
    ang = (np.float32(2.0 * math.pi / L) * pos[:, None] * lin[None, :]).astype(np.float32)
    feats = np.concatenate([t01[:, None], np.cos(ang), -np.sin(ang)], -1).astype(np.float32)
    c["featsT"] = np.ascontiguousarray(np.tile(feats, (rep, 1)).T)
    centre = L // 2
    dist = np.abs(pos - centre) / np.float32(max(centre, 1))
    deltas = np.abs(np.linspace(math.log(0.01) / 1.5, math.log(0.01) / 0.3, 256, dtype=np.float32))
    wnd = np.exp(-dist[:, None] * deltas[None, :]).astype(np.float32)
    c["window"] = np.ascontiguousarray(np.tile(wnd, (rep, 1)).reshape(8, 128, 256).transpose(1, 0, 2))
    N = 2 * L
    tt_ = np.arange(L, dtype=np.float64)
    ff = np.arange(L, dtype=np.float64)
    th = math.pi * (2 * ff + 1) / N
    Fc = np.cos(tt_[:, None] * th[None, :])
    Fs = -np.sin(tt_[:, None] * th[None, :])
    Gc = (2.0 / N) * np.cos(th[:, None] * (tt_[None, :] + L // 2))
    Gs = -(2.0 / N) * np.sin(th[:, None] * (tt_[None, :] + L // 2))
    dF = np.zeros((2, T, T), np.float32)
    dG = np.zeros((2, T, T), np.float32)
    for s_ in range(rep):
        sl = slice(s_ * L, (s_ + 1) * L)
        dF[0, sl, sl] = Fc
        dF[1, sl, sl] = Fs
        dG[0, sl, sl] = Gc
        dG[1, sl, sl] = Gs
    c["dftF"] = dF
    c["dftG"] = dG
    return c


def host_inputs(inp, cores=None):
    f = lambda a: np.ascontiguousarray(np.asarray(a, dtype=np.float32))
    A = {k: np.asarray(v) for k, v in inp.items()}
    shared = {}
    for nm in ("w_ada", "w1_gate", "w1_up", "w1_down", "w_in", "w_out", "w2_gate", "w2_up", "w2_down"):
        shared[nm] = f(A[nm])
    shared["b_adaT"] = f(A["b_ada"].reshape(NL, 72, 128).transpose(0, 2, 1))
    gv = []
    for l in range(NL):
        gv += [fm(A["g_ff1"][l]), fm(A["g_mix"][l]), fm(A["g_ff2"][l])]
    gv.append(fm(A["g_final"]))
    shared["gvec"] = f(np.concatenate(gv, axis=1))
    lp = np.zeros((128, NL, 304), np.float32)
    p = np.arange(128)
    for l in range(NL):
        lp[:, l, 0:16] = A["b_gates"][l][None, :]
        lp[:, l, 16:272] = A["g_mlstm"][l][None, :]
        lp[:, l, 272] = A["g_qnorm"][l][p % 64]
        lp[:, l, 273] = A["g_knorm"][l][p % 64]
        lp[:, l, 274:278] = A["sinks"][l][None, :]
        for i in range(6):
            ch = i * 128 + p
            lp[:, l, 278 + i * 4 + 0] = A["conv_w"][l][0, ch]
            lp[:, l, 278 + i * 4 + 1] = A["conv_w"][l][1, ch]
            lp[:, l, 278 + i * 4 + 2] = A["conv_w"][l][2, ch]
            lp[:, l, 278 + i * 4 + 3] = A["conv_b"][l][ch]
        for ct in range(2):
            lp[:, l, 302 + ct] = A["hyena_bias"][l][ct * 128 + p]
    shared["lp"] = lp
    shared["fw1"] = f(A["filt_w1"].transpose(1, 0, 2))
    shared["fw2"] = f(A["filt_w2"].transpose(1, 0, 2))
    shared["fw3"] = f(A["filt_w3"].transpose(1, 0, 2))
    shared["fvec"] = f(np.stack([A["filt_b1"], A["filt_b2"], A["filt_freq"]], -1).transpose(1, 0, 2))
    shared["fb3"] = f(A["filt_b3"][None, :, :])
    cst = {"s": _consts("s"), "p": _consts("p")}
    maps = []
    xs, xp = A["x_sample"], A["x_prompt"]
    for core in (range(8) if cores is None else cores):
        m = dict(shared)
        kind = "s" if core < 4 else "p"
        m.update(cst[kind])
        C0 = np.zeros((NL, 2, 128, 2, 65), np.float32)
        m0rep = np.zeros((128, NL, 2, 2), np.float32)
        m0c = np.zeros((4, NL * 2), np.float32)
        kT = {n: np.zeros((NL, 2, 128, 256), np.float32) for n in ("gkT", "skT")}
        vp = {n: np.zeros((NL, 2, 128, 2, 192), np.float32) for n in ("gvp", "svp")}
        if core < 4:
            b = core
            m["xT"] = f(xs[b].T)
            m["cvec"] = fm(A["c"][b])
            sC, sn, smm = A["state_mlstm_C"][b], A["state_mlstm_n"][b], A["state_mlstm_m"][b]
            for g_ in range(2):
                for pr_ in range(2):
                    h = 2 * pr_ + g_
                    C0[:, :, g_ * 64:(g_ + 1) * 64, pr_, 0:64] = sC[:, :, h]
                    C0[:, :, g_ * 64:(g_ + 1) * 64, pr_, 64] = sn[:, :, h]
                    m0rep[g_ * 64:(g_ + 1) * 64, :, :, pr_] = smm[None, :, :, h]
            for l in range(NL):
                for dr in range(2):
                    m0c[:, l * 2 + dr] = smm[l, dr, :]
            for (kn, vn, ck_, cv_) in (("gkT", "gvp", "cache_gattn_k", "cache_gattn_v"), ("skT", "svp", "cache_swa_k", "cache_swa_v")):
                ck, cv = A[ck_][b], A[cv_][b]
                t1 = ck.transpose(0, 2, 3, 1).reshape(NL, 128, 256)
                t2 = ck[:, :, ::-1, :].transpose(0, 2, 3, 1).reshape(NL, 128, 256)
                kT[kn][:, 0] = t1
                kT[kn][:, 1] = t2
                vp[vn][:, :, :, :, 64:128] = cv.reshape(NL, 2, 128, 2, 64)
        else:
            j = core - 4
            m["xT"] = f(xp[4 * j:4 * j + 4].reshape(T, D).T)
            m["cvec"] = fm(A["c_ctx"])
        m["C0"], m["m0rep"], m["m0c"] = C0, m0rep, m0c
        m.update(kT)
        m.update(vp)
        maps.append(m)
    return maps


_PROG = None


def get_prog():
    global _PROG
    if _PROG is None:
        _PROG = Prog()
    return _PROG


def run_device(inputs, trace=False, cores=None):
    prog = get_prog()
    maps = host_inputs(inputs, cores)
    maps = [{k: np.ascontiguousarray(v, dtype=np.float32) for k, v in m.items() if k in prog.din} for m in maps]
    for m in maps:
        for k, shp in prog.din.items():
            assert m[k].shape == shp, (k, m[k].shape, shp)
    res = run_bass_kernel_spmd(prog.nc, maps, core_ids=list(range(len(maps))), trace=trace)
    return res


def assemble(res):
    r = res.results
    yp = np.zeros((16, 256, D), np.float32)
    ys = np.zeros((4, T, D), np.float32)
    nC = np.zeros((16, NL, 2, 4, 64, 64), np.float32)
    nn = np.zeros((16, NL, 2, 4, 64), np.float32)
    nm = np.zeros((16, NL, 2, 4), np.float32)
    kv = {n: np.zeros((16, NL, 256, 2, 64), np.float32) for n in ("o_gk", "o_gv", "o_sk", "o_sv")}
    for core in range(8):
        y = np.ascontiguousarray(r[core]["yT"].T)
        if core < 4:
            ys[core] = y
            continue
        j = core - 4
        yp[4 * j:4 * j + 4] = y.reshape(4, 256, D)
        for n in ("o_gk", "o_sk"):
            a = r[core][n].reshape(NL, 2, 64, 4, 256)
            kv[n][4 * j:4 * j + 4] = a.transpose(3, 0, 4, 1, 2)
        for n in ("o_gv", "o_sv"):
            a = r[core][n].reshape(NL, 4, 256, 2, 64)
            kv[n][4 * j:4 * j + 4] = a.transpose(1, 0, 2, 3, 4)
        oc = r[core]["o_C"].reshape(NL, 2, 4, 2, 64, 2, 65)
        oc = oc.transpose(2, 0, 1, 5, 3, 4, 6).reshape(4, NL, 2, 4, 64, 65)
        nC[4 * j:4 * j + 4] = oc[..., 0:64]
        nn[4 * j:4 * j + 4] = oc[..., 64]
        nm[4 * j:4 * j + 4] = r[core]["o_m"].transpose(3, 0, 1, 2)
    return (yp, ys, nC, nn, nm, kv["o_gk"], kv["o_gv"], kv["o_sk"], kv["o_sv"])


def kernel(**inputs):
    res = run_device(inputs)
    return assemble(res)
```

```python
import os
import math
import numpy as np
import concourse.bass as bass
import concourse.mybir as mybir
from concourse.bass_utils import run_bass_kernel_spmd

F32 = mybir.dt.float32
BF16 = mybir.dt.bfloat16
I32 = mybir.dt.int32
AF = mybir.ActivationFunctionType
ALU = mybir.AluOpType
AX = mybir.AxisListType

D = 1024
T = 1024
DFF = 2816
NFF = 22
NL = 2
NIN = 2832
EPS = 1e-6
NEG = -30000.0
BIG = 29952.0
LN8 = math.log(0.125)
SLOT = 8 * 640
STAGE = int(os.environ.get("MK_STAGE", "99"))
DEBUG = bool(os.environ.get("MK_DEBUG"))
CUT = int(os.environ.get("MK_CUT", "99"))


class R:
    __slots__ = ("name", "w", "rd", "excl")

    def __init__(self, name="", excl=False):
        self.name = name
        self.w = None
        self.rd = []
        self.excl = excl


def RL(n, name=""):
    return [R("%s%d" % (name, i)) for i in range(n)]


class Sched:
    NLANES = {"sp": 8, "pool": 3, "poolw": 5}

    def __init__(self, nc):
        self.nc = nc
        self.eng = {"pe": nc.tensor, "act": nc.scalar, "dve": nc.vector,
                    "pool": nc.gpsimd, "sp": nc.sync}
        self.sem = {}
        self.cnt = {}
        for k in self.eng:
            self.sem[k] = nc.alloc_semaphore(name="s_" + k)
            self.cnt[k] = 0
        self.lanes = {}
        for q, n in self.NLANES.items():
            self.lanes[q] = []
            for i in range(n):
                key = "d_%s%d" % (q, i)
                self.sem[key] = nc.alloc_semaphore(name=key)
                self.cnt[key] = 0
                self.lanes[q].append(key)
        self.lane_rr = {q: 0 for q in self.NLANES}
        self.eng_of = {"sp": "sp", "pool": "pool", "poolw": "pool"}
        self.waited = {k: {} for k in self.eng}
        self.nwaits = 0
        self.nops = 0

    def _wait(self, e, key, val):
        if key == "pe" and e == "pe":
            return
        w = self.waited[e]
        if w.get(key, 0) >= val:
            return
        self.eng[e].wait_ge(self.sem[key], val)
        w[key] = val
        self.nwaits += 1

    def _deps(self, e, reads, writes):
        deps = {}
        for r in reads:
            if r.w is not None:
                k, v = r.w
                if deps.get(k, 0) < v:
                    deps[k] = v
            if r.excl:
                for (k, v) in r.rd:
                    if k != e and deps.get(k, 0) < v:
                        deps[k] = v
        for w in writes:
            if w.w is not None:
                k, v = w.w
                if deps.get(k, 0) < v:
                    deps[k] = v
            for (k, v) in w.rd:
                if deps.get(k, 0) < v:
                    deps[k] = v
        for k, v in deps.items():
            self._wait(e, k, v)

    def _commit(self, tok, reads, writes):
        for r in reads:
            r.rd.append(tok)
            if len(r.rd) > 48:
                mx = {}
                for k, v in r.rd:
                    if mx.get(k, 0) < v:
                        mx[k] = v
                r.rd = list(mx.items())
        for w in writes:
            w.w = tok
            w.rd = []

    def op(self, e, fn, reads=(), writes=()):
        self._deps(e, reads, writes)
        inst = fn(self.eng[e])
        self.cnt[e] += 1
        inst.then_inc(self.sem[e], 1)
        self._commit((e, self.cnt[e]), reads, writes)
        self.nops += 1

    def dma(self, q, out, in_, reads=(), writes=()):
        lanes = self.lanes[q]
        e = self.eng_of[q]
        key = lanes[self.lane_rr[q] % len(lanes)]
        self.lane_rr[q] += 1
        self._wait(e, key, self.cnt[key])
        self._deps(e, reads, writes)
        inst = self.eng[e].dma_start(out=out, in_=in_)
        self.cnt[key] += 16
        inst.then_inc(self.sem[key], 16)
        self._commit((key, self.cnt[key]), reads, writes)
        self.nops += 1

    def barrier(self):
        for e in self.eng:
            for k in self.sem:
                if self.cnt[k] > 0 and not k.startswith("d_poolw"):
                    self._wait(e, k, self.cnt[k])

    def finish(self):
        for k in self.sem:
            if self.cnt[k] > 0 and k != "sp":
                self._wait("sp", k, self.cnt[k])


def act(S, out, in_, func, reads, writes, bias=0.0, scale=1.0):
    S.op("act", lambda e: e.activation(out=out, in_=in_, func=func, bias=bias, scale=scale), reads, writes)


def tt(S, eng, out, a, b, op, reads, writes):
    S.op(eng, lambda e: e.tensor_tensor(out=out, in0=a, in1=b, op=op), reads, writes)


def ts(S, eng, out, a, s1, op0, reads, writes, s2=None, op1=None):
    if op1 is None:
        S.op(eng, lambda e: e.tensor_scalar(out=out, in0=a, scalar1=s1, scalar2=None, op0=op0), reads, writes)
    else:
        S.op(eng, lambda e: e.tensor_scalar(out=out, in0=a, scalar1=s1, scalar2=s2, op0=op0, op1=op1), reads, writes)


def cp(S, eng, out, in_, reads, writes):
    S.op(eng, lambda e: e.tensor_copy(out=out, in_=in_), reads, writes)


def recip(S, out, in_, reads, writes):
    S.op("dve", lambda e: e.reciprocal(out=out, in_=in_), reads, writes)


def mm(S, groups, reads, writes):
    def fn(e):
        inst = None
        for (o, l, r, st, sp_) in groups:
            inst = e.matmul(o, lhsT=l, rhs=r, start=st, stop=sp_)
        return inst
    S.op("pe", fn, reads, writes)


class Prog:
    def __init__(self):
        nc = bass.Bass("TRN2", target_bir_lowering=False)
        self.nc = nc
        self.S = Sched(nc)
        self.din = {}
        self.dout = {}
        self._build()

    def inp(self, name, shape):
        t = self.nc.dram_tensor(name, list(shape), F32, kind="ExternalInput").ap()
        self.din[name] = tuple(shape)
        return t

    def outp(self, name, shape):
        t = self.nc.dram_tensor(name, list(shape), F32, kind="ExternalOutput").ap()
        self.dout[name] = tuple(shape)
        return t

    def sb(self, name, shape, dt=F32):
        return self.nc.alloc_sbuf_tensor("sb_" + name, list(shape), dt)

    def arena_reset(self, to=0):
        self.aoff = to
        self.S.barrier()

    def carve(self, shape, dt=F32):
        n = 1
        for s in shape[1:]:
            n *= s
        nb = n * (4 if dt in (F32, I32) else 2)
        nb = (nb + 31) // 32 * 32
        off = self.aoff
        self.aoff += nb
        assert self.aoff <= self.ARENA_BYTES, (self.aoff, self.ARENA_BYTES)
        v = self.arena[:, off // 2:(off + nb) // 2]
        if dt != BF16:
            v = v.bitcast(dt)
        v = v[:, 0:n]
        if len(shape) == 3:
            v = v.rearrange("p (a b) -> p a b", a=shape[1])
        elif len(shape) == 4:
            v = v.rearrange("p (a b c) -> p a b c", a=shape[1], b=shape[2])
        return v

    def bank(self):
        return self.pbank()

    def prefetch(self, key, parts):
        self.pre[key] = self.wload(parts)

    def wload(self, parts, key=None):
        if key is not None and key in self.pre:
            return self.pre.pop(key)
        i = self.slot_rr % len(self.slots)
        self.slot_rr += 1
        sl, r = self.slots[i], self.slotr[i]
        for (dst_fn, src) in parts:
            self.S.dma("poolw", dst_fn(sl), src, writes=[r])
        return sl, r

    def slot_view(self, sl, kk):
        return sl[:, 0:kk * self.slot_w].rearrange("p (k n) -> p k n", k=kk)

    def _build(self):
        nc, S = self.nc, self.S
        inp, outp, sb = self.inp, self.outp, self.sb
        xT_d = inp("xT", [D, T])
        cvec_d = inp("cvec", [128, 8])
        w_ada = inp("w_ada", [NL, D, 9 * D])
        b_adaT = inp("b_adaT", [NL, 128, 72])
        gvec_d = inp("gvec", [128, NL * 24 + 8])
        wd = {}
        for nm, shp in (("w1_gate", [NL, D, DFF]), ("w1_up", [NL, D, DFF]), ("w1_down", [NL, DFF, D]),
                        ("w_in", [NL, D, NIN]), ("w_out", [NL, D, D]),
                        ("w2_gate", [NL, D, DFF]), ("w2_up", [NL, D, DFF]), ("w2_down", [NL, DFF, D])):
            wd[nm] = inp(nm, shp)
        yT_d = outp("yT", [D, T])

        xT = sb("xT", [128, 8, T], F32)
        hT = sb("hT", [128, 8, T], BF16)
        self.ARENA_BYTES = 84 * 1024
        self.arena = sb("arena", [128, self.ARENA_BYTES // 2], BF16)
        self.slot_w = 640
        self.slots = [sb("slot%d" % i, [128, SLOT], BF16) for i in range(4)]
        self.slotr = RL(4, "slot")
        self.slot_rr = 0
        self.pre = {}
        self.ps = [nc.alloc_psum_tensor("ps%d" % i, [128, 512], F32) for i in range(8)]
        self.psr = [R("ps%d" % i, excl=True) for i in range(8)]
        self.bank_rr = 0
        self.bank_pool = list(range(8))
        ones_bf = sb("ones_bf", [128, 128], BF16)
        cvec = sb("cvec_sb", [128, 8], F32)
        sc_bf = sb("sc_bf", [128, 8], BF16)
        modT = sb("modT", [128, NL, 72], F32)
        badaT = sb("badaT", [128, NL, 72], F32)
        gvec = sb("gvec_sb", [128, NL * 24 + 8], F32)
        Acoef = sb("Acoef", [128, NL, 3, 8], F32)
        Gcoef = sb("Gcoef", [128, NL, 3, 8], F32)
        sqb = sb("sqb", [128, 2, 512], BF16)
        f32s = sb("f32s", [128, 3, 512], F32)
        rstd = sb("rstd", [128, 512], F32)
        r_ones, r_cvec, r_sc, r_gvec, r_rstd = R("ones"), R("cvec"), R("sc"), R("gvec"), R("rstd")
        r_mod = RL(NL, "mod")
        r_bada = R("bada")
        r_coef = RL(NL, "coef")
        r_sq = RL(2, "sq")
        r_f32s = RL(3, "f32s")
        xr = [[R("x%d_%d" % (k, c)) for c in range(2)] for k in range(8)]
        hr = [[R("h%d_%d" % (k, c)) for c in range(2)] for k in range(8)]
        self.sq_rr = 0
        self.f32_rr = 0

        def CH(c):
            return slice(c * 512, (c + 1) * 512)

        xv = xT_d.rearrange("(k p) t -> p k t", p=128)
        for k in range(8):
            S.dma("sp", xT[:, k, :], xv[:, k, :], writes=[xr[k][0], xr[k][1]])
        S.dma("sp", cvec[:], cvec_d, writes=[r_cvec])
        S.dma("sp", gvec[:], gvec_d, writes=[r_gvec])
        for l in range(NL):
            S.dma("sp", badaT[:, l, :], b_adaT[l], writes=[r_bada])
        S.op("dve", lambda e: e.memset(ones_bf[:], 1.0), writes=[r_ones])
        act(S, sc_bf[:], cvec[:], AF.Silu, [r_cvec], [r_sc])

        mslots = [sb("mslot%d" % i, [128, 8, 256], BF16) for i in range(2)]
        r_ms = RL(2, "mslot")
        r_modls = [[R("mod%d_%d" % (l, s_)) for s_ in range(3)] for l in range(NL)]
        jobs = [(l, q) for l in range(NL) for q in range(36)]
        st = {"dma": 0, "mm": 0, "fin": set()}

        def mod_dma(j):
            l, q = jobs[j]
            wv = w_ada[l].rearrange("(k p) n -> p k n", p=128)
            S.dma("poolw", mslots[j % 2][:], wv[:, :, q * 256:(q + 1) * 256], writes=[r_ms[j % 2]])

        def mod_mm(j):
            l, q = jobs[j]
            pb, pr = self.bank()
            groups = []
            for jj in range(2):
                for k in range(8):
                    groups.append((pb[:, jj:jj + 1], mslots[j % 2][:, k, jj * 128:(jj + 1) * 128], sc_bf[:, k:k + 1],
                                   k == 0, k == 7))
            mm(S, groups, [r_ms[j % 2], r_sc], [pr])
            s_ = q // 12
            tt(S, "dve", modT[:, l, q * 2:q * 2 + 2], pb[:, 0:2], badaT[:, l, q * 2:q * 2 + 2], ALU.add,
               [pr, r_bada], [r_modls[l][s_]])

        def mod_pump(n=1):
            for _ in range(n):
                if st["dma"] < len(jobs) and st["dma"] - st["mm"] < 2:
                    mod_dma(st["dma"])
                    st["dma"] += 1
                if st["mm"] < st["dma"] - 1 or (st["dma"] == len(jobs) and st["mm"] < st["dma"]):
                    mod_mm(st["mm"])
                    st["mm"] += 1

        def mod_require(l, s_):
            last = l * 36 + s_ * 12 + 11
            while st["mm"] <= last:
                mod_pump()
            if (l, s_) in st["fin"]:
                return
            st["fin"].add((l, s_))
            r = r_modls[l][s_]
            ts(S, "dve", Acoef[:, l, s_, :], modT[:, l, (3 * s_ + 1) * 8:(3 * s_ + 2) * 8], 1.0, ALU.add, [r], [r])
            tt(S, "dve", Acoef[:, l, s_, :], Acoef[:, l, s_, :], gvec[:, l * 24 + s_ * 8:l * 24 + s_ * 8 + 8],
               ALU.mult, [r, r_gvec], [r])
            ts(S, "dve", Gcoef[:, l, s_, :], modT[:, l, (3 * s_ + 2) * 8:(3 * s_ + 3) * 8],
               0.5 if s_ != 1 else 1.0, ALU.mult, [r], [r])

        self.mod_pump = mod_pump
        self.mod_require = mod_require

        def rms_rstd(c, src_fn, src_regs, nk, inv_n, ones_l):
            pb, pr = self.bank()
            for k in range(nk):
                i = self.sq_rr % 2
                self.sq_rr += 1
                act(S, sqb[:, i, :], src_fn(k), AF.Square, [src_regs[k]], [r_sq[i]])
                mm(S, [(pb[:], ones_l, sqb[:, i, :], k == 0, k == nk - 1)], [r_sq[i], r_ones], [pr])
            act(S, rstd[:], pb[:], AF.Sqrt, [pr], [r_rstd], bias=EPS, scale=inv_n)
            recip(S, rstd[:], rstd[:], [r_rstd], [r_rstd])

        def norm_mod(l, s):
            for c in range(2):
                rms_rstd(c, lambda k: xT[:, k, CH(c)], [xr[k][c] for k in range(8)], 8, 1.0 / D, ones_bf[:])
                for k in range(8):
                    i = self.f32_rr % 3
                    self.f32_rr += 1
                    tt(S, "dve", f32s[:, i, :], xT[:, k, CH(c)], rstd[:], ALU.mult,
                       [xr[k][c], r_rstd], [r_f32s[i]])
                    act(S, hT[:, k, CH(c)], f32s[:, i, :], AF.Identity, [r_f32s[i], r_modls[l][s]],
                        [hr[k][c]], bias=modT[:, l, 3 * s * 8 + k:3 * s * 8 + k + 1],
                        scale=Acoef[:, l, s, k:k + 1])

        def resid_add(l, s, dt_, c, pb, pr):
            i = self.f32_rr % 3
            self.f32_rr += 1
            act(S, f32s[:, i, :], pb[:], AF.Copy, [pr, r_modls[l][s]], [r_f32s[i]], scale=Gcoef[:, l, s, dt_:dt_ + 1])
            tt(S, "dve", xT[:, dt_, CH(c)], xT[:, dt_, CH(c)], f32s[:, i, :], ALU.add,
               [xr[dt_][c], r_f32s[i]], [xr[dt_][c]])

        def ffn(l, s, wg, wu, wdn):
            self.arena_reset()
            aT = self.carve([128, NFF, T], BF16)
            ar = [[R("a%d_%d" % (j, c)) for c in range(2)] for j in range(NFF)]
            sg = [self.carve([128, 512], F32) for _ in range(3)]
            r_sg = RL(3, "sg")
            sg_rr = 0
            mod_require(l, s)
            norm_mod(l, s)
            wgv = wg[l].rearrange("(k p) n -> p k n", p=128)
            wuv = wu[l].rearrange("(k p) n -> p k n", p=128)
            for g in range(NFF // 2):
                c0 = g * 256
                sl, sr = self.wload(self.parts_gu(wg, wu, l, g), key=("gu", l, s, g))
                s3 = self.slot_view(sl, 8)
                if l == 0 and s == 0:
                    mod_pump(1)
                for jj in range(2):
                    j = g * 2 + jj
                    for c in range(2):
                        pg, prg = self.bank()
                        pu, pru = self.bank()
                        groups = []
                        for k in range(8):
                            groups.append((pg[:], s3[:, k, jj * 128:(jj + 1) * 128], hT[:, k, CH(c)], k == 0, k == 7))
                        for k in range(8):
                            groups.append((pu[:], s3[:, k, 256 + jj * 128:256 + (jj + 1) * 128], hT[:, k, CH(c)],
                                           k == 0, k == 7))
                        mm(S, groups, [sr] + [hr[k][c] for k in range(8)], [prg, pru])
                        i = sg_rr % 3
                        sg_rr += 1
                        act(S, sg[i], pg[:], AF.Silu, [prg], [r_sg[i]])
                        tt(S, "dve", aT[:, j, CH(c)], sg[i], pu[:], ALU.mult, [r_sg[i], pru], [ar[j][c]])
            wdv = wdn[l].rearrange("(j p) n -> p j n", p=128)
            for dt_ in range(8):
                sl, sr = self.wload([(lambda sl_: sl_[:, 0:NFF * 128].rearrange("p (j n) -> p j n", j=NFF),
                                      wdv[:, :, dt_ * 128:(dt_ + 1) * 128])])
                s3 = sl[:, 0:NFF * 128].rearrange("p (j n) -> p j n", j=NFF)
                for c in range(2):
                    pb, pr = self.bank()
                    groups = [(pb[:], s3[:, j, :], aT[:, j, CH(c)], j == 0, j == NFF - 1) for j in range(NFF)]
                    mm(S, groups, [sr] + [ar[j][c] for j in range(NFF)], [pr])
                    resid_add(l, s, dt_, c, pb, pr)

        self.wd = wd
        self.ctx = dict(xT=xT, hT=hT, xr=xr, hr=hr, CH=CH, rms_rstd=rms_rstd, norm_mod=norm_mod,
                        resid_add=resid_add, ones_bf=ones_bf, r_ones=r_ones, gvec=gvec, r_gvec=r_gvec,
                        f32s=f32s, r_f32s=r_f32s, rstd=rstd, r_rstd=r_rstd, modT=modT, sqb=sqb, r_sq=r_sq)


        self.ARENA_BYTES = 84 * 1024
        LPW = 304
        self.LP = dict(BG=0, GM=16, GQK=272, SK=274, CV=278, HB=302)
        d = {}
        d["lp"] = inp("lp", [128, NL, LPW])
        d["cflags"] = inp("cflags", [128, 4])
        d["ident"] = inp("ident", [128, 128])
        d["blockones"] = inp("blockones", [128, 128])
        d["tri"] = inp("tri", [128, 4, 128])
        d["ropeP"] = inp("ropeP", [128, 128])
        d["ropeCS"] = inp("ropeCS", [128, 2, T])
        d["swamask"] = inp("swamask", [128, 8, 384])
        d["gmAB"] = inp("gmAB", [5, 2, T])
        d["fw1"] = inp("fw1", [33, NL, 64])
        d["fw2"] = inp("fw2", [64, NL, 64])
        d["fw3"] = inp("fw3", [64, NL, 256])
        d["fvec"] = inp("fvec", [64, NL, 3])
        d["fb3"] = inp("fb3", [1, NL, 256])
        d["featsT"] = inp("featsT", [33, T])
        d["window"] = inp("window", [128, 8, 256])
        d["dftF"] = inp("dftF", [2, T, T])
        d["dftG"] = inp("dftG", [2, T, T])
        d["sel"] = inp("sel", [4, 130])
        d["C0"] = inp("C0", [NL, 2, 128, 2, 65])
        d["m0rep"] = inp("m0rep", [128, NL, 2, 2])
        d["m0c"] = inp("m0c", [4, NL * 2])
        d["gkT"] = inp("gkT", [NL, 2, 128, 256])
        d["gvp"] = inp("gvp", [NL, 2, 128, 2, 192])
        d["skT"] = inp("skT", [NL, 2, 128, 256])
        d["svp"] = inp("svp", [NL, 2, 128, 2, 192])
        d["o_gk"] = outp("o_gk", [NL, 128, T])
        d["o_gv"] = outp("o_gv", [NL, T, 128])
        d["o_sk"] = outp("o_sk", [NL, 128, T])
        d["o_sv"] = outp("o_sv", [NL, T, 128])
        d["o_C"] = outp("o_C", [NL, 2, 4, 128, 2, 65])
        d["o_m"] = outp("o_m", [NL, 2, 4, 4])
        self.d = d
        if DEBUG:
            self.dbg_ymix = outp("dbg_ymix", [NL, 128, 8, T])
        k = {}
        k["lp"] = sb("lp", [128, NL, LPW]); k["cflags"] = sb("cflags", [128, 4])
        k["ident_f"] = sb("ident_f", [128, 128]); k["ident_bf"] = sb("ident_bf", [128, 128], BF16)
        k["blockones"] = sb("blockones", [128, 128], BF16)
        k["tri"] = sb("tri", [128, 4, 128]); k["ropeP"] = sb("ropeP", [128, 128])
        k["ones_f"] = sb("ones_f", [128, 128])
        k["fw1"] = sb("fw1", [33, NL, 64]); k["fw2"] = sb("fw2", [64, NL, 64]); k["fw3"] = sb("fw3", [64, NL, 256])
        k["fvec"] = sb("fvec", [64, NL, 3]); k["fb3"] = sb("fb3", [1, NL, 256]); k["fs"] = sb("fs", [64, NL, 4])
        k["sel"] = sb("sel", [4, 130]); k["m0rep"] = sb("m0rep", [128, NL, 2, 2]); k["m0c"] = sb("m0c", [4, NL * 2])
        k["Clo"] = sb("Clo", [128, 2, 2, 65]); k["Chi"] = sb("Chi", [128, 2, 2, 65])
        k["Cblo"] = sb("Cblo", [128, 2, 2, 66], BF16); k["Cbhi"] = sb("Cbhi", [128, 2, 2, 66], BF16)
        k["cvb"] = sb("cvb", [128, NL, 6, 2])
        self.k = k
        r_k = R("consts")
        self.r_k = r_k
        for nm in ("lp", "cflags", "tri", "ropeP", "fw1", "fw2", "fw3", "fvec", "fb3", "sel", "m0rep", "m0c"):
            S.dma("sp", k[nm][:], d[nm], writes=[r_k])
        S.dma("sp", k["ident_f"][:], d["ident"], writes=[r_k])
        S.dma("pool", k["ident_bf"][:], d["ident"], writes=[r_k])
        S.dma("pool", k["blockones"][:], d["blockones"], writes=[r_k])
        S.op("pool", lambda e: e.memset(k["ones_f"][:], 1.0), writes=[r_k])
        for nm in ("Clo", "Chi", "Cblo", "Cbhi"):
            S.op("pool", lambda e, nm=nm: e.memset(k[nm][:], 0.0), writes=[r_k])
        i2p = float(1.0 / (2 * math.pi))
        ts(S, "dve", k["fs"][:, :, 0:1], k["fvec"][:, :, 2:3], i2p, ALU.mult, [r_k], [r_k])
        tt(S, "dve", k["fs"][:, :, 1:2], k["fs"][:, :, 0:1], k["fvec"][:, :, 0:1], ALU.mult, [r_k], [r_k])
        tt(S, "dve", k["fs"][:, :, 2:3], k["fs"][:, :, 0:1], k["fvec"][:, :, 1:2], ALU.mult, [r_k], [r_k])
        CV = self.LP["CV"]
        for l in range(NL):
            cvv = k["lp"][:, l, CV:CV + 24].rearrange("p (a b) -> p a b", a=6)
            for (j, col) in ((0, 0), (1, 2)):
                ts(S, "dve", k["cvb"][:, l, :, j:j + 1], cvv[:, :, col:col + 1], k["cflags"][:, 2:3], ALU.mult,
                   [r_k], [r_k], s2=-1.0, op1=ALU.mult)

        for l in range(NL):
            ffn(l, 0, wd["w1_gate"], wd["w1_up"], wd["w1_down"])
            self.prefetch(("A1", l), self.parts_win(l, 0, 512))
            self.prefetch(("A2", l), self.parts_win(l, 512, 512))
            self.prefetch(("G", l), self.parts_win(l, 1024, 16))
            self.mixer(l)
            for g in range(2):
                self.prefetch(("gu", l, 2, g), self.parts_gu(wd["w2_gate"], wd["w2_up"], l, g))
            ffn(l, 2, wd["w2_gate"], wd["w2_up"], wd["w2_down"])
            if l + 1 < NL:
                for g in range(2):
                    self.prefetch(("gu", l + 1, 0, g), self.parts_gu(wd["w1_gate"], wd["w1_up"], l + 1, g))

        gfo = NL * 24
        yv = yT_d.rearrange("(k p) t -> p k t", p=128)
        self.arena_reset()
        ost_ = self.carve([128, 2, 512], F32)
        ost = [ost_[:, 0, :], ost_[:, 1, :]]
        r_ost = RL(2, "ost")
        o_rr = 0
        for c in range(2):
            rms_rstd(c, lambda k: xT[:, k, CH(c)], [xr[k][c] for k in range(8)], 8, 1.0 / D, ones_bf[:])
            for k in range(8):
                i = self.f32_rr % 3
                self.f32_rr += 1
                tt(S, "dve", f32s[:, i, :], xT[:, k, CH(c)], rstd[:], ALU.mult, [xr[k][c], r_rstd], [r_f32s[i]])
                o = o_rr % 2
                o_rr += 1
                act(S, ost[o], f32s[:, i, :], AF.Copy, [r_f32s[i], r_gvec], [r_ost[o]],
                    scale=gvec[:, gfo + k:gfo + k + 1])
                S.dma("sp", yv[:, k, CH(c)], ost[o], reads=[r_ost[o]])
        S.finish()

    def mixer(self, l):
        S = self.S
        C = self.ctx
        hT, hr, CH = C["hT"], C["hr"], C["CH"]
        self.arena_reset()
        ymix = self.carve([128, 8, T], BF16)
        ymr = [[R("ym%d_%d" % (k, c)) for c in range(2)] for k in range(8)]
        base = self.aoff
        self.mod_require(l, 1)
        C["norm_mod"](l, 1)
        win = self.wd["w_in"][l].rearrange("(k p) n -> p k n", p=128)
        self.bank_pool = list(range(8))
        self.mix_mlstm(l, ymix, ymr, win)
        self.prefetch(("attn", l, True), self.parts_attn(l, True))
        self.arena_reset(base)
        if STAGE >= 3:
            self.mix_attn(l, ymix, ymr, win, glob=True)
            self.prefetch(("attn", l, False), self.parts_attn(l, False))
            self.arena_reset(base)
            self.mix_attn(l, ymix, ymr, win, glob=False)
            self.prefetch(("D1", l), self.parts_win(l, 2064, 512))
            self.prefetch(("D2", l), self.parts_win(l, 2576, 256))
            self.arena_reset(base)
        if STAGE >= 4:
            self.mix_hyena(l, ymix, ymr, win)
        self.prefetch(("wout", l, 0), self.parts_wout(l, 0))
        self.prefetch(("wout", l, 1), self.parts_wout(l, 1))
        self.bank_pool = list(range(8))
        if DEBUG:
            f32s, r_f32s = C["f32s"], C["r_f32s"]
            for k in range(8):
                for c in range(2):
                    i = self.f32_rr % 3
                    self.f32_rr += 1
                    act(S, f32s[:, i, :], ymix[:, k, CH(c)], AF.Copy, [ymr[k][c]], [r_f32s[i]])
                    S.dma("sp", self.dbg_ymix[l, :, k, c * 512:(c + 1) * 512], f32s[:, i, :], reads=[r_f32s[i]])
        wov = self.wd["w_out"][l].rearrange("(k p) n -> p k n", p=128)
        for half in range(2):
            sl, sr = self.wload(self.parts_wout(l, half), key=("wout", l, half))
            s3 = self.slot_view(sl, 8)
            for j in range(4):
                dt_ = half * 4 + j
                for c in range(2):
                    pb, pr = self.bank()
                    groups = [(pb[:], s3[:, k, j * 128:(j + 1) * 128], ymix[:, k, CH(c)], k == 0, k == 7) for k in range(8)]
                    mm(S, groups, [sr] + [ymr[k][c] for k in range(8)], [pr])
                    C["resid_add"](l, 1, dt_, c, pb, pr)

    def parts_attn(self, l, glob):
        win = self.wd["w_in"][l].rearrange("(k p) n -> p k n", p=128)
        sv8 = lambda sl_: self.slot_view(sl_, 8)
        q0 = 1040 if glob else 1552
        k0, v0 = q0 + 256, q0 + 384
        return [(lambda sl_: sv8(sl_)[:, :, 0:256], win[:, :, q0:q0 + 256]),
                (lambda sl_: sv8(sl_)[:, :, 256:384], win[:, :, k0:k0 + 128]),
                (lambda sl_: sv8(sl_)[:, :, 384:448], win[:, :, k0 + 64:k0 + 128]),
                (lambda sl_: sv8(sl_)[:, :, 448:512], win[:, :, k0:k0 + 64]),
                (lambda sl_: sv8(sl_)[:, :, 512:640], win[:, :, v0:v0 + 128])]

    def parts_win(self, l, c0, n):
        win = self.wd["w_in"][l].rearrange("(k p) n -> p k n", p=128)
        return [(lambda sl_: self.slot_view(sl_, 8)[:, :, 0:n], win[:, :, c0:c0 + n])]

    def parts_wout(self, l, half):
        wov = self.wd["w_out"][l].rearrange("(k p) n -> p k n", p=128)
        return [(lambda sl_: self.slot_view(sl_, 8)[:, :, 0:512], wov[:, :, half * 512:(half + 1) * 512])]

    def parts_gu(self, wg, wu, l, g):
        wgv = wg[l].rearrange("(k p) n -> p k n", p=128)
        wuv = wu[l].rearrange("(k p) n -> p k n", p=128)
        c0 = g * 256
        return [(lambda sl_: self.slot_view(sl_, 8)[:, :, 0:256], wgv[:, :, c0:c0 + 256]),
                (lambda sl_: self.slot_view(sl_, 8)[:, :, 256:512], wuv[:, :, c0:c0 + 256])]

    def pbank(self):
        i = self.bank_pool[self.bank_rr % len(self.bank_pool)]
        self.bank_rr += 1
        return self.ps[i], self.psr[i]

    def proj_fm(self, c, s3, col0, pb):
        hT, CH = self.ctx["hT"], self.ctx["CH"]
        return [(pb[:], s3[:, k, col0:col0 + 128], hT[:, k, CH(c)], k == 0, k == 7) for k in range(8)]

    def proj_tok(self, tt_, s3, col0, ncols, pb, pc0):
        hT = self.ctx["hT"]
        return [(pb[:, pc0:pc0 + ncols], hT[:, k, tt_ * 128:(tt_ + 1) * 128], s3[:, k, col0:col0 + ncols], k == 0, k == 7)
                for k in range(8)]

    def mix_mlstm(self, l, ymix, ymr, win):
        S, k_, d = self.S, self.k, self.d
        C = self.ctx
        hr, CH = C["hr"], C["CH"]
        LP = self.LP
        lp = k_["lp"]
        r_k = self.r_k
        hall = [hr[k][c] for k in range(8) for c in range(2)]
        sv8 = lambda sl_: self.slot_view(sl_, 8)
        slA1, srA1 = self.wload(self.parts_win(l, 0, 512), key=("A1", l))
        slA2, srA2 = self.wload(self.parts_win(l, 512, 512), key=("A2", l))
        slG, srG = self.wload(self.parts_win(l, 1024, 16), key=("G", l))
        sA1, sA2, sG = sv8(slA1), sv8(slA2), sv8(slG)
        if CUT == 1:
            return
        aqT = self.carve([128, 2, T], BF16)
        akp = self.carve([128, 4, T], BF16)
        ktok = self.carve([128, 8, 256], BF16)
        vaug = self.carve([128, 8, 4, 66], BF16)
        sgo = self.carve([128, 8, 256], BF16)
        gts = self.carve([128, 8, 16], F32)
        lf = self.carve([128, 8, 8], F32)
        cum = self.carve([128, 8, 16], F32)
        call = self.carve([128, 8, 8], F32)
        wall = self.carve([128, 8, 8], F32)
        wkl = self.carve([128, 8, 8], F32)
        wkall = self.carve([128, 8, 8], F32)
        wkm = self.carve([128, 8, 8], F32)
        dec = self.carve([128, 8, 4], F32)
        lfB = self.carve([128, 8, 128], F32)
        E = self.carve([128, 8, 128], F32)
        AT = self.carve([128, 2, 4, 128], BF16)
        hf = self.carve([128, 8, 256], F32)
        numt = self.carve([128, 2, 260], F32)
        tmpn = self.carve([128, 2, 260], F32)
        h64 = self.carve([128, 2, 256], F32)
        dsm = self.carve([128, 2, 8], F32)
        kwp = self.carve([128, 2, 4, 192], BF16)
        yatok = self.carve([128, 8, 256], BF16)
        snap = self.carve([128, 2, 4, 130], F32)
        e0 = self.carve([128, 2, 2], F32)
        ssall = self.carve([128, 8, 4], F32)
        sq1 = self.carve([128, 2, 256], F32)
        mst = self.carve([128, 64], F32)
        scl = self.carve([128, 2, 8], F32)
        r_aq, r_akp = RL(2, "aq"), RL(2, "akp")
        r_ktok, r_vaug, r_sgo, r_gts = RL(8, "ktok"), RL(8, "vaug"), RL(8, "sgo"), R("gts")
        r_gate = R("gate")
        r_lfB, r_E, r_AT = RL(2, "lfB"), RL(2, "E"), RL(2, "AT")
        r_hf = RL(8, "hf")
        r_num, r_tmpn, r_h64, r_dsm, r_kwp = RL(2, "num"), RL(2, "tmpn"), RL(2, "h64"), RL(2, "dsm"), RL(2, "kwp")
        r_C, r_Cb = RL(2, "C"), RL(2, "Cb")
        r_snap = [[R("snap") for _ in range(4)] for _ in range(2)]
        r_ms, r_scl = R("mst"), R("scl")
        r_ya = RL(8, "ya")
        r_ss = R("ss")
        r_sq1 = RL(2, "sq1")
        S.op("pool", lambda e: e.memset(akp, 0.0), writes=r_akp)
        S.op("pool", lambda e: e.memset(kwp, 0.0), writes=r_kwp)
        S.op("pool", lambda e: e.memset(vaug[:, :, :, 64:65], 1.0), writes=r_vaug)
        S.op("pool", lambda e: e.memset(hf, 0.0), writes=r_hf)
        if CUT == 2:
            return
        for t2 in range(2):
            for c in range(2):
                pb, pr = self.pbank()
                mm(S, self.proj_fm(c, sA1, t2 * 128, pb), [srA1] + hall, [pr])
                act(S, aqT[:, t2, CH(c)], pb[:], AF.Copy, [pr], [r_aq[c]])
        if CUT == 21:
            return
        for t2 in range(2):
            for c in range(2):
                pb, pr = self.pbank()
                mm(S, self.proj_fm(c, sA1, 256 + t2 * 128, pb), [srA1] + hall, [pr])
                act(S, akp[0:64, 2 * t2, CH(c)], pb[0:64, :], AF.Copy, [pr], [r_akp[c]])
                cp(S, "dve", akp[64:128, 2 * t2 + 1, CH(c)], pb[64:128, :], [pr], [r_akp[c]])
        if CUT == 22:
            return
        BG = LP["BG"]
        for t_ in range(8):
            p1, pr1 = self.pbank()
            p2, pr2 = self.pbank()
            g = self.proj_tok(t_, sA1, 256, 256, p1, 0) + self.proj_tok(t_, sA2, 0, 256, p1, 256)
            g += self.proj_tok(t_, sA2, 256, 256, p2, 0) + self.proj_tok(t_, sG, 0, 16, p2, 256)
            mm(S, g, [srA1, srA2, srG] + hall, [pr1, pr2])
            if CUT == 23:
                continue
            act(S, ktok[:, t_, :], p1[:, 0:256], AF.Copy, [pr1], [r_ktok[t_]])
            cp(S, "dve", vaug[:, t_, :, 0:64], p1[:, 256:512].rearrange("p (a b) -> p a b", a=4), [pr1], [r_vaug[t_]])
            if CUT == 24:
                continue
            act(S, sgo[:, t_, :], p2[:, 0:256], AF.Sigmoid, [pr2], [r_sgo[t_]])
            tt(S, "dve", gts[:, t_, :], p2[:, 256:272], lp[:, l, BG:BG + 16], ALU.add, [pr2, r_k], [r_gts])
        if CUT in (3, 23, 24):
            return
        ai, af = gts[:, :, 0:8], gts[:, :, 8:16]
        act(S, lf, af, AF.Exp, [r_gts], [r_gate], scale=-1.0)
        act(S, lf, lf, AF.Ln, [r_gate], [r_gate], bias=1.0)
        ts(S, "dve", lf, lf, -1.0, ALU.mult, [r_gate], [r_gate])
        tri = k_["tri"]
        pbc, prc = self.pbank()
        g = []
        for t_ in range(8):
            g.append((pbc[:, t_ * 16:t_ * 16 + 4], tri[:, 0, :], lf[:, t_, 0:4], True, True))
            g.append((pbc[:, t_ * 16 + 4:t_ * 16 + 8], tri[:, 1, :], lf[:, t_, 4:8], True, True))
            g.append((pbc[:, t_ * 16 + 8:t_ * 16 + 16], k_["ones_f"][:], lf[:, t_, 0:8], True, True))
        mm(S, g, [r_gate, r_k], [prc])
        cp(S, "dve", cum, pbc[:, 0:128].rearrange("p (a b) -> p a b", a=8), [prc], [r_gate])
        bc, bt = cum[:, :, 0:8], cum[:, :, 8:16]
        tt(S, "dve", call, ai, bc, ALU.subtract, [r_gts, r_gate], [r_gate])
        ts(S, "dve", call, call, LN8, ALU.add, [r_gate], [r_gate])
        act(S, wall, bc, AF.Exp, [r_gate], [r_gate])
        tt(S, "dve", wkl, call, bt, ALU.add, [r_gate], [r_gate])
        act(S, wkall, wkl, AF.Exp, [r_gate], [r_gate])
        ts(S, "dve", wkm, wkl, -LN8, ALU.add, [r_gate], [r_gate])
        for g_ in range(2):
            rows = slice(g_ * 64, (g_ + 1) * 64)
            act(S, dec[rows, :, :], cum[rows, :, 8 + g_:16:2], AF.Exp, [r_gate], [r_gate])
        if CUT == 4:
            return
        ident_f = k_["ident_f"]
        for dr in range(2):
            pbm, prm = self.pbank()
            g = [(pbm[0:4, t_:t_ + 1], lf[:, t_, dr * 4:dr * 4 + 4], k_["ones_f"][:, 0:1], True, True) for t_ in range(8)]
            mm(S, g, [r_gate, r_k], [prm])
            cp(S, "dve", mst[0:4, dr * 8:dr * 8 + 8], pbm[0:4, 0:8], [prm], [r_ms])
            for hh in range(2):
                pbt, prt = self.pbank()
                for q in range(4):
                    t_ = hh * 4 + q
                    S.op("pe", lambda e, t_=t_, q=q, pbt=pbt: e.transpose(pbt[0:4, q * 128:(q + 1) * 128],
                                                                          wkm[:, t_, dr * 4:dr * 4 + 4], ident_f[:]),
                         [r_gate, r_k], [prt])
                S.op("dve", lambda e, pbt=pbt, hh=hh: e.tensor_reduce(
                    out=mst[0:4, 16 + dr * 8 + hh * 4:16 + dr * 8 + hh * 4 + 4],
                    in_=pbt[0:4, :].rearrange("p (a b) -> p a b", a=4), axis=AX.X, op=ALU.max), [prt], [r_ms])
            bv_ = mst[0:4, dr * 8:dr * 8 + 8].rearrange("p (a b) -> p a b", a=4)
            av_ = mst[0:4, 16 + dr * 8:16 + dr * 8 + 8].rearrange("p (a b) -> p a b", a=4)
            fi, se = (0, 1) if dr == 0 else (1, 0)
            mf = mst[0:4, 32 + dr * 4:32 + dr * 4 + 4]
            ts(S, "dve", mf, bv_[:, :, fi], k_["m0c"][0:4, l * 2 + dr:l * 2 + dr + 1], ALU.add, [r_ms, r_k], [r_ms])
            tt(S, "dve", mf, mf, av_[:, :, fi], ALU.max, [r_ms], [r_ms])
            tt(S, "dve", mf, mf, bv_[:, :, se], ALU.add, [r_ms], [r_ms])
            tt(S, "dve", mf, mf, av_[:, :, se], ALU.max, [r_ms], [r_ms])
            S.dma("sp", d["o_m"][l, dr], mf, reads=[r_ms])
            en = mst[0:4, 40 + dr * 4:40 + dr * 4 + 4]
            act(S, en, mf, AF.Exp, [r_ms], [r_ms], scale=-1.0)
            rhs2 = mst[0:4, 48 + dr * 8:48 + dr * 8 + 8]
            tt(S, "dve", rhs2.rearrange("p (a b) -> p a b", a=4), en.unsqueeze(2).to_broadcast([4, 4, 2]),
               k_["sel"][0:4, 128:130].unsqueeze(1).to_broadcast([4, 4, 2]), ALU.mult, [r_ms, r_k], [r_ms])
            pbs, prs = self.pbank()
            mm(S, [(pbs[:, 0:8], k_["sel"][0:4, 0:128], rhs2, True, True)], [r_ms, r_k], [prs])
            cp(S, "dve", scl[:, dr, :], pbs[:, 0:8], [prs], [r_scl])
        if CUT == 5:
            return
        Clo, Chi, Cblo, Cbhi = k_["Clo"], k_["Chi"], k_["Cblo"], k_["Cbhi"]
        act(S, e0, k_["m0rep"][:, l, :, :], AF.Exp, [r_k], [r_gate])
        halves = ((slice(0, 64), Clo, Cblo), (slice(64, 128), Chi, Cbhi))
        for dr in range(2):
            for (rows, Cx, Cbx) in halves:
                S.dma("sp", Cx[rows, dr, :, :], d["C0"][l, dr, rows, :, :], writes=[r_C[dr]])
            for (rows, Cx, Cbx) in halves:
                tt(S, "dve", Cx[rows, dr, :, :], Cx[rows, dr, :, :], e0[rows, dr, :].unsqueeze(2).to_broadcast([64, 2, 65]),
                   ALU.mult, [r_C[dr], r_gate], [r_C[dr]])
                act(S, Cbx[rows, dr, :, 0:65], Cx[rows, dr, :, :], AF.Copy, [r_C[dr]], [r_Cb[dr]])
        if CUT == 6:
            return
        keep = k_["cflags"][:, 1:2]

        def process(dr, t_):
            tok = slice(t_ * 128, (t_ + 1) * 128)
            chs = slice(dr * 4, dr * 4 + 4)
            cp(S, "pool", lfB[:, chs, :], lf[:, t_, chs].unsqueeze(2).to_broadcast([128, 4, 128]), [r_gate], [r_lfB[dr]])
            yield
            pbe, pre = self.pbank()
            g = []
            for h in range(4):
                o = pbe[:, h * 128:(h + 1) * 128]
                g.append((o, lfB[:, dr * 4 + h, :], tri[:, dr, :], True, False))
                g.append((o, ident_f[:], tri[:, 2 + dr, :], False, True))
            mm(S, g, [r_lfB[dr], r_k], [pre])
            yield
            pbs_, prs_ = self.pbank()
            g = [(pbs_[:, h * 128:(h + 1) * 128], akp[:, h, tok], aqT[:, h // 2, tok], True, True) for h in range(4)]
            mm(S, g, r_akp + r_aq, [prs_])
            yield
            for h in range(4):
                act(S, E[:, dr * 4 + h, :], pbe[:, h * 128:(h + 1) * 128], AF.Exp, [pre, r_gate], [r_E[dr]],
                    bias=call[:, t_, dr * 4 + h:dr * 4 + h + 1])
                yield
            tt(S, "dve", AT[:, dr, :, :], pbs_[:].rearrange("p (a b) -> p a b", a=4), E[:, chs, :], ALU.mult,
               [prs_, r_E[dr]], [r_AT[dr]])
            yield
            pbi, pri = self.pbank()
            g = [(pbi[:, h * 65:(h + 1) * 65], AT[:, dr, h, :], vaug[:, t_, h, 0:65], True, True) for h in range(4)]
            mm(S, g, [r_AT[dr], r_vaug[t_]], [pri])
            yield
            pbx, prx = self.pbank()
            g = [(pbx[:, h * 65:(h + 1) * 65], aqT[:, h // 2, tok], (Cblo if h % 2 == 0 else Cbhi)[:, dr, h // 2, 0:65], True, True)
                 for h in range(4)]
            mm(S, g, r_aq + [r_Cb[dr]], [prx])
            yield
            v3 = lambda ap: ap.rearrange("p (a b) -> p a b", a=4)
            tt(S, "dve", v3(tmpn[:, dr, :]), v3(pbx[:, 0:260]), wall[:, t_, chs].unsqueeze(2).to_broadcast([128, 4, 65]),
               ALU.mult, [prx, r_gate], [r_tmpn[dr]])
            yield
            tt(S, "dve", numt[:, dr, :], pbi[:, 0:260], tmpn[:, dr, :], ALU.add, [pri, r_tmpn[dr]], [r_num[dr]])
            yield
            nv = v3(numt[:, dr, :])
            dn, rd = dsm[:, dr, 0:4], dsm[:, dr, 4:8]
            ts(S, "dve", dn, nv[:, :, 64], -1.0, ALU.mult, [r_num[dr]], [r_dsm[dr]], s2=1.0, op1=ALU.max)
            yield
            tt(S, "dve", dn, dn, nv[:, :, 64], ALU.max, [r_num[dr], r_dsm[dr]], [r_dsm[dr]])
            yield
            recip(S, rd, dn, [r_dsm[dr]], [r_dsm[dr]])
            yield
            tt(S, "dve", v3(h64[:, dr, :])[:, :, 0:64], nv[:, :, 0:64], rd.unsqueeze(2).to_broadcast([128, 4, 64]), ALU.mult,
               [r_num[dr], r_dsm[dr]], [r_h64[dr]])
            yield
            tt(S, "dve", hf[:, t_, :], hf[:, t_, :], h64[:, dr, :], ALU.add, [r_h64[dr], r_hf[t_]], [r_hf[t_]])
            yield
            self.mod_pump()
            yield
            tt(S, "pool", kwp[:, dr, :, 64:128], ktok[:, t_, :].rearrange("p (a b) -> p a b", a=4),
               wkall[:, t_, chs].unsqueeze(2).to_broadcast([128, 4, 64]), ALU.mult, [r_ktok[t_], r_gate], [r_kwp[dr]])
            yield
            pbd, prd = self.pbank()
            g = []
            for j in range(2):
                o = pbd[:, j * 65:(j + 1) * 65]
                g.append((o, kwp[:, dr, 2 * j, 64:192], vaug[:, t_, 2 * j, 0:65], True, False))
                g.append((o, kwp[:, dr, 2 * j + 1, 0:128], vaug[:, t_, 2 * j + 1, 0:65], False, True))
            mm(S, g, [r_kwp[dr], r_vaug[t_]], [prd])
            yield
            for (rows, Cx, Cbx) in halves:
                tt(S, "dve", Cx[rows, dr, :, :], Cx[rows, dr, :, :],
                   dec[rows, t_, dr * 2:dr * 2 + 2].unsqueeze(2).to_broadcast([64, 2, 65]), ALU.mult,
                   [r_C[dr], r_gate, r_Cb[dr]], [r_C[dr]])
                yield
                tt(S, "dve", Cx[rows, dr, :, :], Cx[rows, dr, :, :], pbd[rows, 0:130].rearrange("p (a b) -> p a b", a=2),
                   ALU.add, [r_C[dr], prd], [r_C[dr]])
                yield
            end = (t_ % 2 == 1) if dr == 0 else (t_ % 2 == 0)
            if end:
                sq_ = t_ // 2
                for (rows, Cx, Cbx) in halves:
                    tt(S, "dve", snap[rows, dr, sq_, :].rearrange("p (a b) -> p a b", a=2), Cx[rows, dr, :, :],
                       scl[rows, dr, sq_ * 2:sq_ * 2 + 2].unsqueeze(2).to_broadcast([64, 2, 65]), ALU.mult,
                       [r_C[dr], r_scl], [r_snap[dr][sq_]])
                    yield
                S.dma("sp", d["o_C"][l, dr, sq_], snap[:, dr, sq_, :].rearrange("p (a b) -> p a b", a=2),
                      reads=[r_snap[dr][sq_]])
                yield
                for (rows, Cx, Cbx) in halves:
                    ts(S, "dve", Cx[rows, dr, :, :], Cx[rows, dr, :, :], keep[rows, :], ALU.mult, [r_C[dr], r_k], [r_C[dr]])
                    yield
            for (rows, Cx, Cbx) in halves:
                act(S, Cbx[rows, dr, :, 0:65], Cx[rows, dr, :, :], AF.Copy, [r_C[dr]], [r_Cb[dr]])
                yield

        for i in range(8):
            gens = [process(0, i), process(1, 7 - i)]
            while gens:
                for g__ in list(gens):
                    try:
                        next(g__)
                    except StopIteration:
                        gens.remove(g__)
            self.mod_pump()
        if CUT == 7:
            return
        GM = LP["GM"]
        for t_ in range(8):
            b_ = t_ % 2
            tt(S, "dve", sq1[:, b_, :], hf[:, t_, :], hf[:, t_, :], ALU.mult, [r_hf[t_]], [r_sq1[b_]])
            S.op("dve", lambda e, t_=t_, b_=b_: e.tensor_reduce(out=ssall[:, t_, :],
                                                              in_=sq1[:, b_, :].rearrange("p (a b) -> p a b", a=4),
                                                              axis=AX.X, op=ALU.add), [r_sq1[b_]], [r_ss])
        act(S, ssall, ssall, AF.Sqrt, [r_ss], [r_ss], bias=EPS, scale=1.0 / 64)
        recip(S, ssall, ssall, [r_ss], [r_ss])
        ident_bf = k_["ident_bf"]
        for t_ in range(8):
            b_ = t_ % 2
            tt(S, "dve", sq1[:, b_, :].rearrange("p (a b) -> p a b", a=4), hf[:, t_, :].rearrange("p (a b) -> p a b", a=4),
               ssall[:, t_, :].unsqueeze(2).to_broadcast([128, 4, 64]), ALU.mult, [r_hf[t_], r_ss], [r_sq1[b_]])
            tt(S, "pool", h64[:, b_, :], sgo[:, t_, :], lp[:, l, GM:GM + 256], ALU.mult, [r_sgo[t_], r_k], [r_h64[b_]])
            tt(S, "dve", yatok[:, t_, :], sq1[:, b_, :], h64[:, b_, :], ALU.mult, [r_h64[b_], r_sq1[b_]], [r_ya[t_]])
            pbt, prt = self.pbank()
            pv = pbt[:].bitcast(BF16)
            for t2 in range(2):
                S.op("pe", lambda e, t2=t2, pv=pv, t_=t_: e.transpose(pv[:, t2 * 128:(t2 + 1) * 128],
                                                                      yatok[:, t_, t2 * 128:(t2 + 1) * 128], ident_bf[:]),
                     [r_ya[t_], r_k], [prt])
            c = t_ // 4
            for t2 in range(2):
                if t2 == 0:
                    act(S, ymix[:, t2, t_ * 128:(t_ + 1) * 128], pv[:, t2 * 128:(t2 + 1) * 128], AF.Copy, [prt], [ymr[t2][c]])
                else:
                    cp(S, "dve", ymix[:, t2, t_ * 128:(t_ + 1) * 128], pv[:, t2 * 128:(t2 + 1) * 128], [prt], [ymr[t2][c]])

    def mix_attn(self, l, ymix, ymr, win, glob):
        S, k_, d = self.S, self.k, self.d
        C = self.ctx
        hr, CH = C["hr"], C["CH"]
        LP = self.LP
        lp = k_["lp"]
        r_k = self.r_k
        hall = [hr[k][c] for k in range(8) for c in range(2)]
        sv8 = lambda sl_: self.slot_view(sl_, 8)
        q0 = 1040 if glob else 1552
        k0, v0 = q0 + 256, q0 + 384
        sl, sr = self.wload(self.parts_attn(l, glob), key=("attn", l, glob))
        s3 = sv8(sl)
        ymb = 2 if glob else 4
        cs = self.carve([128, 2, T], F32)
        qpad = self.carve([128, 4, T], BF16)
        kfull = self.carve([128, 2, 1280], BF16)
        vpad = self.carve([128, 10, 2, 192], BF16)
        kout = self.carve([128, T], F32)
        vout = self.carve([128, 8, 128], F32)
        PT = self.carve([128, 3, 512], BF16)
        rden = self.carve([128, 2, 512], F32)
        raw = self.carve([128, 2, 512], F32)
        tb = self.carve([128, 2, 512], F32)
        gm = self.carve([128, 2, T], BF16)
        smask = self.carve([128, 8, 384], BF16) if not glob else None
        es = self.carve([128, 4], F32)
        r_cs, r_qp, r_kf, r_vp, r_kout, r_vout = R("cs"), RL(2, "qp"), R("kf"), RL(10, "vp"), R("kout"), R("vout")
        r_PT, r_rden, r_raw, r_tb, r_gm, r_sm, r_es = RL(3, "PT"), RL(2, "rden"), RL(2, "raw"), RL(2, "tb"), R("gm"), R("sm"), R("es")
        S.dma("sp", cs, d["ropeCS"], writes=[r_cs])
        S.op("pool", lambda e: e.memset(qpad, 0.0), writes=r_qp)
        S.op("pool", lambda e: e.memset(vpad[:, 2:10, :, :], 0.0), writes=r_vp[2:])
        kTd, vpd = (d["gkT"], d["gvp"]) if glob else (d["skT"], d["svp"])
        for x in range(2):
            S.dma("pool", kfull[:, x, 0:256], kTd[l, x], writes=[r_kf])
            S.dma("pool", vpad[:, x, :, :], vpd[l, x], writes=[r_vp[x]])
        if glob:
            S.dma("pool", gm[0:5, :, :], d["gmAB"], writes=[r_gm])
        if not glob:
            S.dma("pool", smask, d["swamask"], writes=[r_sm])
            SK = LP["SK"]
            act(S, es, lp[:, l, SK:SK + 4], AF.Exp, [r_k], [r_es])
        GQK = LP["GQK"]
        ropeP = k_["ropeP"]
        rstd2 = self.carve([128, 2, 512], F32)
        r_rstd2 = RL(2, "rstd2")

        def prelude(ti, c, b_):
            col0 = ti * 128
            pb, pr = self.pbank()
            mm(S, self.proj_fm(c, s3, col0, pb), [sr] + hall, [pr])
            yield
            rw = raw[:, b_, :]
            if glob:
                act(S, rw, pb[:], AF.Copy, [pr], [r_raw[b_]])
                yield
                i = self.sq_rr % 2
                self.sq_rr += 1
                sqb, r_sq = C["sqb"], C["r_sq"]
                act(S, sqb[:, i, :], rw, AF.Square, [r_raw[b_]], [r_sq[i]])
                yield
                p2, pr2 = self.pbank()
                mm(S, [(p2[:], k_["blockones"][:], sqb[:, i, :], True, True)], [r_sq[i], r_k], [pr2])
                yield
                rstd, r_rstd = rstd2[:, b_, :], r_rstd2[b_]
                act(S, rstd, p2[:], AF.Sqrt, [pr2], [r_rstd], bias=EPS, scale=1.0 / 64)
                yield
                recip(S, rstd, rstd, [r_rstd], [r_rstd])
                yield
                tt(S, "dve", rw, rw, rstd, ALU.mult, [r_raw[b_], r_rstd], [r_raw[b_]])
                yield
                gcol = GQK + (0 if ti < 2 else 1)
                act(S, rw, rw, AF.Copy, [r_raw[b_], r_k], [r_raw[b_]], scale=lp[:, l, gcol:gcol + 1])
                yield
            else:
                act(S, rw, pb[:], AF.Copy, [pr], [r_raw[b_]])
                yield
            p3, pr3 = self.pbank()
            mm(S, [(p3[:], ropeP[:], rw, True, True)], [r_raw[b_], r_k], [pr3])
            yield
            ta, tb_ = rw, tb[:, b_, :]
            tt(S, "pool", ta, rw, cs[:, 0, CH(c)], ALU.mult, [r_raw[b_], r_cs], [r_raw[b_]])
            yield
            tt(S, "dve", tb_, p3[:], cs[:, 1, CH(c)], ALU.mult, [pr3, r_cs], [r_tb[b_]])
            yield
            if ti < 2:
                for g_ in range(2):
                    rows = slice(g_ * 64, (g_ + 1) * 64)
                    tt(S, "dve", qpad[rows, 2 * ti + g_, CH(c)], ta[rows, :], tb_[rows, :], ALU.add, [r_tb[b_], r_raw[b_]], [r_qp[c]])
                    yield
            elif ti == 2:
                tt(S, "dve", kout[:, CH(c)], ta, tb_, ALU.add, [r_tb[b_], r_raw[b_]], [r_kout])
                yield
                act(S, kfull[:, 0, 256 + c * 512:256 + (c + 1) * 512], kout[:, CH(c)], AF.Copy, [r_kout], [r_kf])
                yield
            else:
                tt(S, "dve", kfull[:, 1, 256 + c * 512:256 + (c + 1) * 512], ta, tb_, ALU.add, [r_tb[b_], r_raw[b_]], [r_kf])
                yield

        its = [(ti, c) for ti in range(4) for c in range(2)]
        for i0 in range(0, 8, 2):
            gens = [prelude(its[i0][0], its[i0][1], 0), prelude(its[i0 + 1][0], its[i0 + 1][1], 1)]
            while gens:
                for g__ in list(gens):
                    try:
                        next(g__)
                    except StopIteration:
                        gens.remove(g__)
        S.dma("sp", (d["o_gk"] if glob else d["o_sk"])[l], kout, reads=[r_kout])
        for t_ in range(8):
            pb, pr = self.pbank()
            mm(S, self.proj_tok(t_, s3, 512, 128, pb, 0), [sr] + hall, [pr])
            act(S, vout[:, t_, :], pb[:, 0:128], AF.Copy, [pr], [r_vout])
            cp(S, "dve", vpad[:, 2 + t_, :, 64:128], pb[:, 0:128].rearrange("p (a b) -> p a b", a=2), [pr], [r_vp[2 + t_]])
        S.dma("sp", (d["o_gv"] if glob else d["o_sv"])[l].rearrange("(t p) n -> p t n", p=128), vout, reads=[r_vout])
        self.bank_pool = [0, 1, 2, 3]
        ones_bf = C["ones_bf"]
        r_ones = C["r_ones"]
        ident_bf = k_["ident_bf"]
        ctxb = k_["cflags"][:, 0:1]
        pt_rr = 0
        acc_rr = 0
        for h in range(4):
            kv, g_ = h // 2, h % 2
            kx = 0 if g_ == kv else 1
            rows = slice(g_ * 64, (g_ + 1) * 64)
            vw = slice(64, 192) if g_ == 0 else slice(0, 128)
            for c in range(2):
                if glob:
                    tiles = [(mt, 0, 512) for mt in range(10)]
                else:
                    tiles = [(0, 0, 512), (1, 0, 512)]
                    for j in range(8):
                        lo, hi = max((j - 1) * 128, c * 512), min((j + 2) * 128, (c + 1) * 512)
                        if hi > lo:
                            tiles.append((2 + j, lo - c * 512, hi - c * 512))
                ai_ = 4 + 2 * (acc_rr % 2)
                acc_rr += 1
                pn, prn, pd, prd = self.ps[ai_], self.psr[ai_], self.ps[ai_ + 1], self.psr[ai_ + 1]
                pend = []

                def qk(idx):
                    mt, lo, hi = tiles[idx]
                    pb, pr = self.pbank()
                    qs = slice(c * 512 + lo, c * 512 + hi)
                    g = [(pb[:, lo:hi], kfull[:, kx, mt * 128:(mt + 1) * 128], qpad[:, h, qs], True, mt < 2)]
                    rd = [r_kf, r_qp[c]]
                    if mt >= 2:
                        if glob:
                            g.append((pb[:, lo:hi], gm[0:5, 0, (mt - 2) * 128:(mt - 1) * 128], gm[0:5, 1, qs], False, True))
                            rd.append(r_gm)
                        else:
                            j = mt - 2
                            m0_ = c * 512 + lo - (j - 1) * 128
                            g.append((pb[:, lo:hi], ident_bf[:], smask[:, j, m0_:m0_ + (hi - lo)], False, True))
                            rd += [r_sm, r_k]
                    mm(S, g, rd, [pr])
                    return pb, pr

                nt = len(tiles)
                look = 2
                for idx in range(min(look, nt)):
                    pend.append(qk(idx))
                for idx in range(nt):
                    mt, lo, hi = tiles[idx]
                    pb, pr = pend.pop(0)
                    if idx + look < nt:
                        pend.append(qk(idx + look))
                    pi = pt_rr % 3
                    pt_rr += 1
                    act(S, PT[:, pi, lo:hi], pb[:, lo:hi], AF.Exp, [pr, r_k], [r_PT[pi]],
                        bias=(ctxb if mt < 2 else 0.0), scale=0.125)
                    g = [(pn[:, lo:hi], vpad[:, mt, kv, vw], PT[:, pi, lo:hi], idx == 0, idx == nt - 1),
                         (pd[:, lo:hi], ones_bf[:], PT[:, pi, lo:hi], idx == 0, idx == nt - 1)]
                    mm(S, g, [r_vp[mt], r_PT[pi], r_ones], [prn, prd])
                ri = (h * 2 + c) % 2
                if glob:
                    recip(S, rden[rows, ri, :], pd[rows, :], [prd], [r_rden[ri]])
                else:
                    ts(S, "dve", rden[rows, ri, :], pd[rows, :], es[rows, h:h + 1], ALU.add, [prd, r_es], [r_rden[ri]])
                    recip(S, rden[rows, ri, :], rden[rows, ri, :], [r_rden[ri]], [r_rden[ri]])
                tt(S, "dve", ymix[rows, ymb + kv, CH(c)], pn[rows, :], rden[rows, ri, :], ALU.mult, [prn, r_rden[ri]],
                   [ymr[ymb + kv][c]])
        self.bank_pool = list(range(8))

    def mix_hyena(self, l, ymix, ymr, win):
        S, k_, d = self.S, self.k, self.d
        C = self.ctx
        hr, CH = C["hr"], C["CH"]
        LP = self.LP
        lp = k_["lp"]
        r_k = self.r_k
        hall = [hr[k][c] for k in range(8) for c in range(2)]
        sv8 = lambda sl_: self.slot_view(sl_, 8)
        TWO_PI = float(2 * math.pi)
        xoff = self.aoff
        feats = self.carve([128, T], F32)
        z1 = self.carve([128, T], F32)
        z2 = self.carve([128, T], F32)
        wnd = self.carve([128, 8, 256], F32)
        rr = self.carve([128, 512], F32)
        ii = self.carve([128, 512], I32)
        kf = self.carve([128, 512], F32)
        xend = self.aoff
        self.aoff = xoff
        raw = self.carve([128, 3, T], F32)
        uct = self.carve([128, 2, T], F32)
        pa = self.carve([128, 2, 256], F32)
        pq = self.carve([128, 2, 256], F32)
        yt = self.carve([128, 2, 256], F32)
        assert self.aoff <= xend
        self.aoff = xend
        r_X = R("X")
        x0 = self.carve([128, 2, T], F32)
        zf = self.carve([128, 2, T], F32)
        zbf = self.carve([128, 2, T], BF16)
        zh = self.carve([128, 8, 512], BF16)
        ZH = self.carve([128, 2, 512], F32)
        Y = self.carve([128, 8, 2, 256], BF16)
        r_x0, r_zf, r_zbf = RL(2, "x0"), RL(2, "zf"), RL(2, "zbf")
        r_zh = RL(8, "zh")
        r_ZH, r_Y, r_pa, r_pq, r_yt = R("ZH"), RL(8, "Y"), R("pa"), R("pq"), RL(2, "yt")
        S.dma("sp", feats[0:33, :], d["featsT"], writes=[r_X])
        S.dma("sp", wnd, d["window"], writes=[r_X])
        fs = k_["fs"]

        def sin_layer(pb, pr, dst, bcol):
            ts(S, "dve", rr[0:64, :], pb[0:64, :], fs[:, l, 0:1], ALU.mult, [pr, r_k], [r_X], s2=fs[:, l, bcol:bcol + 1],
               op1=ALU.add)
            cp(S, "dve", ii[0:64, :], rr[0:64, :], [r_X], [r_X])
            cp(S, "dve", kf[0:64, :], ii[0:64, :], [r_X], [r_X])
            tt(S, "dve", rr[0:64, :], rr[0:64, :], kf[0:64, :], ALU.subtract, [r_X], [r_X])
            act(S, dst, rr[0:64, :], AF.Sin, [r_X], [r_X], scale=TWO_PI)

        for c in range(2):
            pb, pr = self.pbank()
            mm(S, [(pb[0:64, :], k_["fw1"][0:33, l, :], feats[0:33, CH(c)], True, True)], [r_X, r_k], [pr])
            sin_layer(pb, pr, z1[0:64, CH(c)], 1)
        for c in range(2):
            pb, pr = self.pbank()
            mm(S, [(pb[0:64, :], k_["fw2"][0:64, l, :], z1[0:64, CH(c)], True, True)], [r_X, r_k], [pr])
            sin_layer(pb, pr, z2[0:64, CH(c)], 2)
        for t_ in range(8):
            pb, pr = self.pbank()
            tok = slice(t_ * 128, (t_ + 1) * 128)
            mm(S, [(pb[:, 0:256], z2[0:64, tok], k_["fw3"][0:64, l, :], True, False),
                   (pb[:, 0:256], k_["ones_f"][0:1, :], k_["fb3"][0:1, l, :], False, True)], [r_X, r_k], [pr])
            tt(S, "dve", zh[:, t_, 256:512], pb[:, 0:256], wnd[:, t_, :], ALU.mult, [pr, r_X], [r_zh[t_]])
        slD1, srD1 = self.wload(self.parts_win(l, 2064, 512), key=("D1", l))
        slD2, srD2 = self.wload(self.parts_win(l, 2576, 256), key=("D2", l))
        sD1, sD2 = sv8(slD1), sv8(slD2)
        CV = LP["CV"]
        cvb = k_["cvb"]
        for ct in range(2):
            srcs = ((sD1, ct * 128, srD1), (sD1, 256 + ct * 128, srD1), (sD2, ct * 128, srD2))
            for ui, (s3, col, srr) in enumerate(srcs):
                for c in range(2):
                    pb, pr = self.pbank()
                    mm(S, self.proj_fm(c, s3, col, pb), [srr] + hall, [pr])
                    act(S, raw[:, ui, CH(c)], pb[:], AF.Copy, [pr], [r_X])
            for ui in range(3):
                tile = ui * 2 + ct
                cw = lambda j: lp[:, l, CV + tile * 4 + j:CV + tile * 4 + j + 1]
                u = raw[:, ui, :]
                if ui == 0:
                    dst, wr = x0[:, ct, :], [r_x0[ct]]
                else:
                    dst, wr = uct[:, ui - 1, :], [r_X]
                rd = [r_X, r_k]
                act(S, dst, u, AF.Identity, rd, wr, bias=cw(3), scale=cw(1))
                stt = lambda o, a, sc, b: S.op("dve", lambda e: e.scalar_tensor_tensor(out=o, in0=a, scalar=sc, in1=b,
                                                                                      op0=ALU.mult, op1=ALU.add), rd + wr, wr)
                stt(dst[:, 1:T], u[:, 0:T - 1], cw(0), dst[:, 1:T])
                stt(dst[:, 0:T - 1], u[:, 1:T], cw(2), dst[:, 0:T - 1])
                stt(dst[:, 256:T:256], u[:, 255:T - 1:256], cvb[:, l, tile, 0:1], dst[:, 256:T:256])
                stt(dst[:, 255:T - 1:256], u[:, 256:T:256], cvb[:, l, tile, 1:2], dst[:, 255:T - 1:256])
            tt(S, "dve", zf[:, ct, :], uct[:, 0, :], uct[:, 1, :], ALU.mult, [r_X], [r_zf[ct]])
            act(S, zbf[:, ct, :], zf[:, ct, :], AF.Copy, [r_zf[ct]], [r_zbf[ct]])
        ident_bf = k_["ident_bf"]
        for t_ in range(8):
            pb, pr = self.pbank()
            pv = pb[:].bitcast(BF16)
            for ct in range(2):
                S.op("pe", lambda e, ct=ct, pv=pv, t_=t_: e.transpose(pv[:, ct * 128:(ct + 1) * 128],
                                                                      zbf[:, ct, t_ * 128:(t_ + 1) * 128], ident_bf[:]),
                     [r_zbf[ct], r_k], [pr])
            cp(S, "dve", zh[:, t_, 0:256], pv[:, 0:256], [pr], [r_zh[t_]])
        Fd, Gd = d["dftF"], d["dftG"]
        fv = lambda m: Fd[m].rearrange("(k p) n -> p k n", p=128)
        gv = lambda m: Gd[m].rearrange("(k p) n -> p k n", p=128)
        for q in range(4):
            sl, sr = self.wload([(lambda sl_: sv8(sl_)[:, :, 0:256], fv(0)[:, :, q * 256:(q + 1) * 256]),
                                 (lambda sl_: sv8(sl_)[:, :, 256:512], fv(1)[:, :, q * 256:(q + 1) * 256])])
            s3 = sv8(sl)
            self.mod_pump(2)
            for fj in range(2):
                ft = q * 2 + fj
                pre_, prr = self.pbank()
                pim, pri = self.pbank()
                g = [(pre_[:], s3[:, t_, fj * 128:(fj + 1) * 128], zh[:, t_, :], t_ == 0, t_ == 7) for t_ in range(8)]
                g += [(pim[:], s3[:, t_, 256 + fj * 128:256 + (fj + 1) * 128], zh[:, t_, :], t_ == 0, t_ == 7) for t_ in range(8)]
                mm(S, g, [sr] + r_zh, [prr, pri])
                act(S, ZH[:, 0, :], pre_[:], AF.Copy, [prr], [r_ZH])
                act(S, ZH[:, 1, :], pim[:], AF.Copy, [pri], [r_ZH])
                Zr, Hr, Zi, Hi = ZH[:, 0, 0:256], ZH[:, 0, 256:512], ZH[:, 1, 0:256], ZH[:, 1, 256:512]
                tt(S, "dve", pa[:, 0, :], Zr, Hr, ALU.mult, [r_ZH], [r_pa] + ([r_X] if ft == 0 else []))
                tt(S, "dve", pa[:, 1, :], Zi, Hi, ALU.mult, [r_ZH], [r_pa])
                tt(S, "dve", Y[:, ft, 0, :], pa[:, 0, :], pa[:, 1, :], ALU.subtract, [r_pa], [r_Y[ft]])
                tt(S, "pool", pq[:, 0, :], Zr, Hi, ALU.mult, [r_ZH], [r_pq] + ([r_X] if ft == 0 else []))
                tt(S, "pool", pq[:, 1, :], Zi, Hr, ALU.mult, [r_ZH], [r_pq])
                tt(S, "pool", Y[:, ft, 1, :], pq[:, 0, :], pq[:, 1, :], ALU.add, [r_pq], [r_Y[ft]])
        HB = LP["HB"]
        for q in range(4):
            sl, sr = self.wload([(lambda sl_: sv8(sl_)[:, :, 0:256], gv(0)[:, :, q * 256:(q + 1) * 256]),
                                 (lambda sl_: sv8(sl_)[:, :, 256:512], gv(1)[:, :, q * 256:(q + 1) * 256])])
            s3 = sv8(sl)
            ns = slice(q * 256, (q + 1) * 256)
            self.mod_pump(2)
            for ct in range(2):
                pb, pr = self.pbank()
                g = []
                for ft in range(8):
                    g.append((pb[:, 0:256], Y[:, ft, 0, ct * 128:(ct + 1) * 128], s3[:, ft, 0:256], ft == 0, False))
                    g.append((pb[:, 0:256], Y[:, ft, 1, ct * 128:(ct + 1) * 128], s3[:, ft, 256:512], False, ft == 7))
                mm(S, g, [sr] + r_Y, [pr])
                b_ = ct
                act(S, yt[:, b_, :], pb[:, 0:256], AF.Copy, [pr], [r_yt[b_]] + ([r_X] if q == 0 else []))
                S.op("dve", lambda e, b_=b_, ct=ct: e.scalar_tensor_tensor(out=yt[:, b_, :], in0=zf[:, ct, ns],
                                                                         scalar=lp[:, l, HB + ct:HB + ct + 1], in1=yt[:, b_, :],
                                                                         op0=ALU.mult, op1=ALU.add),
                     [r_zf[ct], r_yt[b_], r_k], [r_yt[b_]])
                tt(S, "dve", ymix[:, 6 + ct, ns], x0[:, ct, ns], yt[:, b_, :], ALU.mult, [r_x0[ct], r_yt[b_]],
                   [ymr[6 + ct][q // 2]])


def fm(v):
    return np.ascontiguousarray(np.asarray(v, np.float32).reshape(8, 128).T)


def _consts(kind):
    c = {}
    p = np.arange(128)
    t = np.arange(T)
    ident = np.eye(128, dtype=np.float32)
    c["ident"] = ident
    bo = np.zeros((128, 128), np.float32)
    bo[:64, :64] = 1
    bo[64:, 64:] = 1
    c["blockones"] = bo
    r_, t_ = np.meshgrid(p, p, indexing="ij")
    tri = np.zeros((128, 4, 128), np.float32)
    tri[:, 0, :] = (r_ <= t_)
    tri[:, 1, :] = (r_ >= t_)
    tri[:, 2, :] = np.where(r_ <= t_, 0.0, NEG)
    tri[:, 3, :] = np.where(r_ >= t_, 0.0, NEG)
    c["tri"] = tri
    P = np.zeros((128, 128), np.float32)
    for b in range(0, 128, 32):
        for i in range(16):
            P[b + i + 16, b + i] = -1.0
            P[b + i, b + i + 16] = 1.0
    c["ropeP"] = P
    cs = np.zeros((128, 2, T), np.float32)
    if kind == "s":
        dd = p % 64
        inv = (10000.0 ** (-(dd % 16).astype(np.float32) / np.float32(16))).astype(np.float32)
        row = (t // 64).astype(np.float32)
        col = (t % 64).astype(np.float32)
        pos = np.where((dd // 32)[:, None] == 0, row[None, :], col[None, :]).astype(np.float32)
        ang = (pos * inv[:, None]).astype(np.float32)
        cs[:, 0, :] = np.cos(ang)
        cs[:, 1, :] = np.sin(ang)
    else:
        cs[:, 0, :] = 1.0
    c["ropeCS"] = cs
    sm = np.full((128, 8, 384), NEG, np.float32)
    for j in range(8):
        m = j * 128 + p[:, None]
        q = (j - 1) * 128 + np.arange(384)[None, :]
        inr = (q >= 0) & (q < T)
        if kind == "s":
            ok = (np.abs(m - q) <= 128) & inr
        else:
            ok = ((m // 256) == (q // 256)) & inr
        sm[:, j, :] = np.where(ok, 0.0, NEG)
    c["swamask"] = sm
    gm = np.zeros((5, 2, T), np.float32)
    if kind == "p":
        gm[0, 0, :] = 1.0
        gm[0, 1, :] = -BIG
        for s_ in range(4):
            gm[1 + s_, 0, :] = (t // 256 == s_)
            gm[1 + s_, 1, :] = BIG * (t // 256 == s_)
    c["gmAB"] = gm
    c["cflags"] = np.tile(np.array([[0.0, 1.0, 0.0, 0.0]] if kind == "s" else [[NEG, 0.0, 1.0, 0.0]], np.float32), (128, 1))
    sel = np.zeros((4, 130), np.float32)
    for h in range(4):
        sel[h, 0:128] = ((p >= 64).astype(int) == (h % 2))
        sel[h, 128 + h // 2] = 1.0
    c["sel"] = sel
    L = T if kind == "s" else 256
    rep = T // L
    pos = np.arange(L, dtype=np.float32)
    t01 = pos / np.float32(max(L - 1, 1))
    lin = np.linspace(1e-4, 15.0, 16, dtype=np.float32)
    ang = (np.float32(2.0 * math.pi / L) * pos[:, None] * lin[None, :]).astype(np.float32)
    feats = np.concatenate([t01[:, None], np.cos(ang), -np.sin(ang)], -1).astype(np.float32)
    c["featsT"] = np.ascontiguousarray(np.tile(feats, (rep, 1)).T)
    centre = L // 2
    dist = np.abs(pos - centre) / np.float32(max(centre, 1))
    deltas = np.abs(np.linspace(math.log(0.01) / 1.5, math.log(0.01) / 0.3, 256, dtype=np.float32))
    wnd = np.exp(-dist[:, None] * deltas[None, :]).astype(np.float32)
    c["window"] = np.ascontiguousarray(np.tile(wnd, (rep, 1)).reshape(8, 128, 256).transpose(1, 0, 2))
    N = 2 * L
    tt_ = np.arange(L, dtype=np.float64)
    ff = np.arange(L, dtype=np.float64)
    th = math.pi * (2 * ff + 1) / N
    Fc = np.cos(tt_[:, None] * th[None, :])
    Fs = -np.sin(tt_[:, None] * th[None, :])
    Gc = (2.0 / N) * np.cos(th[:, None] * (tt_[None, :] + L // 2))
    Gs = -(2.0 / N) * np.sin(th[:, None] * (tt_[None, :] + L // 2))
    dF = np.zeros((2, T, T), np.float32)
    dG = np.zeros((2, T, T), np.float32)
    for s_ in range(rep):
        sl = slice(s_ * L, (s_ + 1) * L)
        dF[0, sl, sl] = Fc
        dF[1, sl, sl] = Fs
        dG[0, sl, sl] = Gc
        dG[1, sl, sl] = Gs
    c["dftF"] = dF
    c["dftG"] = dG
    return c


def host_inputs(inp, cores=None):
    f = lambda a: np.ascontiguousarray(np.asarray(a, dtype=np.float32))
    A = {k: np.asarray(v) for k, v in inp.items()}
    shared = {}
    for nm in ("w_ada", "w1_gate", "w1_up", "w1_down", "w_in", "w_out", "w2_gate", "w2_up", "w2_down"):
        shared[nm] = f(A[nm])
    shared["b_adaT"] = f(A["b_ada"].reshape(NL, 72, 128).transpose(0, 2, 1))
    gv = []
    for l in range(NL):
        gv += [fm(A["g_ff1"][l]), fm(A["g_mix"][l]), fm(A["g_ff2"][l])]
    gv.append(fm(A["g_final"]))
    shared["gvec"] = f(np.concatenate(gv, axis=1))
    lp = np.zeros((128, NL, 304), np.float32)
    p = np.arange(128)
    for l in range(NL):
        lp[:, l, 0:16] = A["b_gates"][l][None, :]
        lp[:, l, 16:272] = A["g_mlstm"][l][None, :]
        lp[:, l, 272] = A["g_qnorm"][l][p % 64]
        lp[:, l, 273] = A["g_knorm"][l][p % 64]
        lp[:, l, 274:278] = A["sinks"][l][None, :]
        for i in range(6):
            ch = i * 128 + p
            lp[:, l, 278 + i * 4 + 0] = A["conv_w"][l][0, ch]
            lp[:, l, 278 + i * 4 + 1] = A["conv_w"][l][1, ch]
            lp[:, l, 278 + i * 4 + 2] = A["conv_w"][l][2, ch]
            lp[:, l, 278 + i * 4 + 3] = A["conv_b"][l][ch]
        for ct in range(2):
            lp[:, l, 302 + ct] = A["hyena_bias"][l][ct * 128 + p]
    shared["lp"] = lp
    shared["fw1"] = f(A["filt_w1"].transpose(1, 0, 2))
    shared["fw2"] = f(A["filt_w2"].transpose(1, 0, 2))
    shared["fw3"] = f(A["filt_w3"].transpose(1, 0, 2))
    shared["fvec"] = f(np.stack([A["filt_b1"], A["filt_b2"], A["filt_freq"]], -1).transpose(1, 0, 2))
    shared["fb3"] = f(A["filt_b3"][None, :, :])
    cst = {"s": _consts("s"), "p": _consts("p")}
    maps = []
    xs, xp = A["x_sample"], A["x_prompt"]
    for core in (range(8) if cores is None else cores):
        m = dict(shared)
        kind = "s" if core < 4 else "p"
        m.update(cst[kind])
        C0 = np.zeros((NL, 2, 128, 2, 65), np.float32)
        m0rep = np.zeros((128, NL, 2, 2), np.float32)
        m0c = np.zeros((4, NL * 2), np.float32)
        kT = {n: np.zeros((NL, 2, 128, 256), np.float32) for n in ("gkT", "skT")}
        vp = {n: np.zeros((NL, 2, 128, 2, 192), np.float32) for n in ("gvp", "svp")}
        if core < 4:
            b = core
            m["xT"] = f(xs[b].T)
            m["cvec"] = fm(A["c"][b])
            sC, sn, smm = A["state_mlstm_C"][b], A["state_mlstm_n"][b], A["state_mlstm_m"][b]
            for g_ in range(2):
                for pr_ in range(2):
                    h = 2 * pr_ + g_
                    C0[:, :, g_ * 64:(g_ + 1) * 64, pr_, 0:64] = sC[:, :, h]
                    C0[:, :, g_ * 64:(g_ + 1) * 64, pr_, 64] = sn[:, :, h]
                    m0rep[g_ * 64:(g_ + 1) * 64, :, :, pr_] = smm[None, :, :, h]
            for l in range(NL):
                for dr in range(2):
                    m0c[:, l * 2 + dr] = smm[l, dr, :]
            for (kn, vn, ck_, cv_) in (("gkT", "gvp", "cache_gattn_k", "cache_gattn_v"), ("skT", "svp", "cache_swa_k", "cache_swa_v")):
                ck, cv = A[ck_][b], A[cv_][b]
                t1 = ck.transpose(0, 2, 3, 1).reshape(NL, 128, 256)
                t2 = ck[:, :, ::-1, :].transpose(0, 2, 3, 1).reshape(NL, 128, 256)
                kT[kn][:, 0] = t1
                kT[kn][:, 1] = t2
                vp[vn][:, :, :, :, 64:128] = cv.reshape(NL, 2, 128, 2, 64)
        else:
            j = core - 4
            m["xT"] = f(xp[4 * j:4 * j + 4].reshape(T, D).T)
            m["cvec"] = fm(A["c_ctx"])
        m["C0"], m["m0rep"], m["m0c"] = C0, m0rep, m0c
        m.update(kT)
        m.update(vp)
        maps.append(m)
    return maps


_PROG = None


def get_prog():
    global _PROG
    if _PROG is None:
        _PROG = Prog()
    return _PROG


def run_device(inputs, trace=False, cores=None):
    prog = get_prog()
    maps = host_inputs(inputs, cores)
    maps = [{k: np.ascontiguousarray(v, dtype=np.float32) for k, v in m.items() if k in prog.din} for m in maps]
    for m in maps:
        for k, shp in prog.din.items():
            assert m[k].shape == shp, (k, m[k].shape, shp)
    res = run_bass_kernel_spmd(prog.nc, maps, core_ids=list(range(len(maps))), trace=trace)
    return res


def assemble(res):
    r = res.results
    yp = np.zeros((16, 256, D), np.float32)
    ys = np.zeros((4, T, D), np.float32)
    nC = np.zeros((16, NL, 2, 4, 64, 64), np.float32)
    nn = np.zeros((16, NL, 2, 4, 64), np.float32)
    nm = np.zeros((16, NL, 2, 4), np.float32)
    kv = {n: np.zeros((16, NL, 256, 2, 64), np.float32) for n in ("o_gk", "o_gv", "o_sk", "o_sv")}
    for core in range(8):
        y = np.ascontiguousarray(r[core]["yT"].T)
        if core < 4:
            ys[core] = y
            continue
        j = core - 4
        yp[4 * j:4 * j + 4] = y.reshape(4, 256, D)
        for n in ("o_gk", "o_sk"):
            a = r[core][n].reshape(NL, 2, 64, 4, 256)
            kv[n][4 * j:4 * j + 4] = a.transpose(3, 0, 4, 1, 2)
        for n in ("o_gv", "o_sv"):
            a = r[core][n].reshape(NL, 4, 256, 2, 64)
            kv[n][4 * j:4 * j + 4] = a.transpose(1, 0, 2, 3, 4)
        oc = r[core]["o_C"].reshape(NL, 2, 4, 2, 64, 2, 65)
        oc = oc.transpose(2, 0, 1, 5, 3, 4, 6).reshape(4, NL, 2, 4, 64, 65)
        nC[4 * j:4 * j + 4] = oc[..., 0:64]
        nn[4 * j:4 * j + 4] = oc[..., 64]
        nm[4 * j:4 * j + 4] = r[core]["o_m"].transpose(3, 0, 1, 2)
    return (yp, ys, nC, nn, nm, kv["o_gk"], kv["o_gv"], kv["o_sk"], kv["o_sv"])


def kernel(**inputs):
    res = run_device(inputs)
    return assemble(res)
```

```python
import os
import math
import numpy as np
import concourse.bass as bass
import concourse.mybir as mybir
from concourse.bass_utils import run_bass_kernel_spmd

F32 = mybir.dt.float32
BF16 = mybir.dt.bfloat16
I32 = mybir.dt.int32
AF = mybir.ActivationFunctionType
ALU = mybir.AluOpType
AX = mybir.AxisListType

D = 1024
T = 1024
DFF = 2816
NFF = 22
NL = 2
NIN = 2832
EPS = 1e-6
NEG = -30000.0
BIG = 29952.0
LN8 = math.log(0.125)
SLOT = 8 * 640
STAGE = int(os.environ.get("MK_STAGE", "99"))
DEBUG = bool(os.environ.get("MK_DEBUG"))
CUT = int(os.environ.get("MK_CUT", "99"))


class R:
    __slots__ = ("name", "w", "rd", "excl")

    def __init__(self, name="", excl=False):
        self.name = name
        self.w = None
        self.rd = []
        self.excl = excl


def RL(n, name=""):
    return [R("%s%d" % (name, i)) for i in range(n)]


class Sched:
    NLANES = {"sp": 8, "pool": 3, "poolw": 5}

    def __init__(self, nc):
        self.nc = nc
        self.eng = {"pe": nc.tensor, "act": nc.scalar, "dve": nc.vector,
                    "pool": nc.gpsimd, "sp": nc.sync}
        self.sem = {}
        self.cnt = {}
        for k in self.eng:
            self.sem[k] = nc.alloc_semaphore(name="s_" + k)
            self.cnt[k] = 0
        self.lanes = {}
        for q, n in self.NLANES.items():
            self.lanes[q] = []
            for i in range(n):
                key = "d_%s%d" % (q, i)
                self.sem[key] = nc.alloc_semaphore(name=key)
                self.cnt[key] = 0
                self.lanes[q].append(key)
        self.lane_rr = {q: 0 for q in self.NLANES}
        self.eng_of = {"sp": "sp", "pool": "pool", "poolw": "pool"}
        self.waited = {k: {} for k in self.eng}
        self.nwaits = 0
        self.nops = 0

    def _wait(self, e, key, val):
        if key == "pe" and e == "pe":
            return
        w = self.waited[e]
        if w.get(key, 0) >= val:
            return
        self.eng[e].wait_ge(self.sem[key], val)
        w[key] = val
        self.nwaits += 1

    def _deps(self, e, reads, writes):
        deps = {}
        for r in reads:
            if r.w is not None:
                k, v = r.w
                if deps.get(k, 0) < v:
                    deps[k] = v
            if r.excl:
                for (k, v) in r.rd:
                    if k != e and deps.get(k, 0) < v:
                        deps[k] = v
        for w in writes:
            if w.w is not None:
                k, v = w.w
                if deps.get(k, 0) < v:
                    deps[k] = v
            for (k, v) in w.rd:
                if deps.get(k, 0) < v:
                    deps[k] = v
        for k, v in deps.items():
            self._wait(e, k, v)

    def _commit(self, tok, reads, writes):
        for r in reads:
            r.rd.append(tok)
            if len(r.rd) > 48:
                mx = {}
                for k, v in r.rd:
                    if mx.get(k, 0) < v:
                        mx[k] = v
                r.rd = list(mx.items())
        for w in writes:
            w.w = tok
            w.rd = []

    def op(self, e, fn, reads=(), writes=()):
        self._deps(e, reads, writes)
        inst = fn(self.eng[e])
        self.cnt[e] += 1
        inst.then_inc(self.sem[e], 1)
        self._commit((e, self.cnt[e]), reads, writes)
        self.nops += 1

    def dma(self, q, out, in_, reads=(), writes=()):
        lanes = self.lanes[q]
        e = self.eng_of[q]
        key = lanes[self.lane_rr[q] % len(lanes)]
        self.lane_rr[q] += 1
        self._wait(e, key, self.cnt[key])
        self._deps(e, reads, writes)
        inst = self.eng[e].dma_start(out=out, in_=in_)
        self.cnt[key] += 16
        inst.then_inc(self.sem[key], 16)
        self._commit((key, self.cnt[key]), reads, writes)
        self.nops += 1

    def barrier(self):
        for e in self.eng:
            for k in self.sem:
                if self.cnt[k] > 0 and not k.startswith("d_poolw"):
                    self._wait(e, k, self.cnt[k])

    def finish(self):
        for k in self.sem:
            if self.cnt[k] > 0 and k != "sp":
                self._wait("sp", k, self.cnt[k])


def act(S, out, in_, func, reads, writes, bias=0.0, scale=1.0):
    S.op("act", lambda e: e.activation(out=out, in_=in_, func=func, bias=bias, scale=scale), reads, writes)


def tt(S, eng, out, a, b, op, reads, writes):
    S.op(eng, lambda e: e.tensor_tensor(out=out, in0=a, in1=b, op=op), reads, writes)


def ts(S, eng, out, a, s1, op0, reads, writes, s2=None, op1=None):
    if op1 is None:
        S.op(eng, lambda e: e.tensor_scalar(out=out, in0=a, scalar1=s1, scalar2=None, op0=op0), reads, writes)
    else:
        S.op(eng, lambda e: e.tensor_scalar(out=out, in0=a, scalar1=s1, scalar2=s2, op0=op0, op1=op1), reads, writes)


def cp(S, eng, out, in_, reads, writes):
    S.op(eng, lambda e: e.tensor_copy(out=out, in_=in_), reads, writes)


def recip(S, out, in_, reads, writes):
    S.op("dve", lambda e: e.reciprocal(out=out, in_=in_), reads, writes)


def mm(S, groups, reads, writes):
    def fn(e):
        inst = None
        for (o, l, r, st, sp_) in groups:
            inst = e.matmul(o, lhsT=l, rhs=r, start=st, stop=sp_)
        return inst
    S.op("pe", fn, reads, writes)


class Prog:
    def __init__(self):
        nc = bass.Bass("TRN2", target_bir_lowering=False)
        self.nc = nc
        self.S = Sched(nc)
        self.din = {}
        self.dout = {}
        self._build()

    def inp(self, name, shape):
        t = self.nc.dram_tensor(name, list(shape), F32, kind="ExternalInput").ap()
        self.din[name] = tuple(shape)
        return t

    def outp(self, name, shape):
        t = self.nc.dram_tensor(name, list(shape), F32, kind="ExternalOutput").ap()
        self.dout[name] = tuple(shape)
        return t

    def sb(self, name, shape, dt=F32):
        return self.nc.alloc_sbuf_tensor("sb_" + name, list(shape), dt)

    def arena_reset(self, to=0):
        self.aoff = to
        self.S.barrier()

    def carve(self, shape, dt=F32):
        n = 1
        for s in shape[1:]:
            n *= s
        nb = n * (4 if dt in (F32, I32) else 2)
        nb = (nb + 31) // 32 * 32
        off = self.aoff
        self.aoff += nb
        assert self.aoff <= self.ARENA_BYTES, (self.aoff, self.ARENA_BYTES)
        v = self.arena[:, off // 2:(off + nb) // 2]
        if dt != BF16:
            v = v.bitcast(dt)
        v = v[:, 0:n]
        if len(shape) == 3:
            v = v.rearrange("p (a b) -> p a b", a=shape[1])
        elif len(shape) == 4:
            v = v.rearrange("p (a b c) -> p a b c", a=shape[1], b=shape[2])
        return v

    def bank(self):
        return self.pbank()

    def prefetch(self, key, parts):
        self.pre[key] = self.wload(parts)

    def wload(self, parts, key=None):
        if key is not None and key in self.pre:
            return self.pre.pop(key)
        i = self.slot_rr % len(self.slots)
        self.slot_rr += 1
        sl, r = self.slots[i], self.slotr[i]
        for (dst_fn, src) in parts:
            self.S.dma("poolw", dst_fn(sl), src, writes=[r])
        return sl, r

    def slot_view(self, sl, kk):
        return sl[:, 0:kk * self.slot_w].rearrange("p (k n) -> p k n", k=kk)

    def _build(self):
        nc, S = self.nc, self.S
        inp, outp, sb = self.inp, self.outp, self.sb
        xT_d = inp("xT", [D, T])
        cvec_d = inp("cvec", [128, 8])
        w_ada = inp("w_ada", [NL, D, 9 * D])
        b_adaT = inp("b_adaT", [NL, 128, 72])
        gvec_d = inp("gvec", [128, NL * 24 + 8])
        wd = {}
        for nm, shp in (("w1_gate", [NL, D, DFF]), ("w1_up", [NL, D, DFF]), ("w1_down", [NL, DFF, D]),
                        ("w_in", [NL, D, NIN]), ("w_out", [NL, D, D]),
                        ("w2_gate", [NL, D, DFF]), ("w2_up", [NL, D, DFF]), ("w2_down", [NL, DFF, D])):
            wd[nm] = inp(nm, shp)
        yT_d = outp("yT", [D, T])

        xT = sb("xT", [128, 8, T], F32)
        hT = sb("hT", [128, 8, T], BF16)
        self.ARENA_BYTES = 84 * 1024
        self.arena = sb("arena", [128, self.ARENA_BYTES // 2], BF16)
        self.slot_w = 640
        self.slots = [sb("slot%d" % i, [128, SLOT], BF16) for i in range(4)]
        self.slotr = RL(4, "slot")
        self.slot_rr = 0
        self.pre = {}
        self.ps = [nc.alloc_psum_tensor("ps%d" % i, [128, 512], F32) for i in range(8)]
        self.psr = [R("ps%d" % i, excl=True) for i in range(8)]
        self.bank_rr = 0
        self.bank_pool = list(range(8))
        ones_bf = sb("ones_bf", [128, 128], BF16)
        cvec = sb("cvec_sb", [128, 8], F32)
        sc_bf = sb("sc_bf", [128, 8], BF16)
        modT = sb("modT", [128, NL, 72], F32)
        badaT = sb("badaT", [128, NL, 72], F32)
        gvec = sb("gvec_sb", [128, NL * 24 + 8], F32)
        Acoef = sb("Acoef", [128, NL, 3, 8], F32)
        Gcoef = sb("Gcoef", [128, NL, 3, 8], F32)
        sqb = sb("sqb", [128, 2, 512], BF16)
        f32s = sb("f32s", [128, 3, 512], F32)
        rstd = sb("rstd", [128, 512], F32)
        r_ones, r_cvec, r_sc, r_gvec, r_rstd = R("ones"), R("cvec"), R("sc"), R("gvec"), R("rstd")
        r_mod = RL(NL, "mod")
        r_bada = R("bada")
        r_coef = RL(NL, "coef")
        r_sq = RL(2, "sq")
        r_f32s = RL(3, "f32s")
        xr = [[R("x%d_%d" % (k, c)) for c in range(2)] for k in range(8)]
        hr = [[R("h%d_%d" % (k, c)) for c in range(2)] for k in range(8)]
        self.sq_rr = 0
        self.f32_rr = 0

        def CH(c):
            return slice(c * 512, (c + 1) * 512)

        xv = xT_d.rearrange("(k p) t -> p k t", p=128)
        for k in range(8):
            S.dma("sp", xT[:, k, :], xv[:, k, :], writes=[xr[k][0], xr[k][1]])
        S.dma("sp", cvec[:], cvec_d, writes=[r_cvec])
        S.dma("sp", gvec[:], gvec_d, writes=[r_gvec])
        for l in range(NL):
            S.dma("sp", badaT[:, l, :], b_adaT[l], writes=[r_bada])
        S.op("dve", lambda e: e.memset(ones_bf[:], 1.0), writes=[r_ones])
        act(S, sc_bf[:], cvec[:], AF.Silu, [r_cvec], [r_sc])

        mslots = [sb("mslot%d" % i, [128, 8, 256], BF16) for i in range(2)]
        r_ms = RL(2, "mslot")
        r_modls = [[R("mod%d_%d" % (l, s_)) for s_ in range(3)] for l in range(NL)]
        jobs = [(l, q) for l in range(NL) for q in range(36)]
        st = {"dma": 0, "mm": 0, "fin": set()}

        def mod_dma(j):
            l, q = jobs[j]
            wv = w_ada[l].rearrange("(k p) n -> p k n", p=128)
            S.dma("poolw", mslots[j % 2][:], wv[:, :, q * 256:(q + 1) * 256], writes=[r_ms[j % 2]])

        def mod_mm(j):
            l, q = jobs[j]
            pb, pr = self.bank()
            groups = []
            for jj in range(2):
                for k in range(8):
                    groups.append((pb[:, jj:jj + 1], mslots[j % 2][:, k, jj * 128:(jj + 1) * 128], sc_bf[:, k:k + 1],
                                   k == 0, k == 7))
            mm(S, groups, [r_ms[j % 2], r_sc], [pr])
            s_ = q // 12
            tt(S, "dve", modT[:, l, q * 2:q * 2 + 2], pb[:, 0:2], badaT[:, l, q * 2:q * 2 + 2], ALU.add,
               [pr, r_bada], [r_modls[l][s_]])

        def mod_pump(n=1):
            for _ in range(n):
                if st["dma"] < len(jobs) and st["dma"] - st["mm"] < 2:
                    mod_dma(st["dma"])
                    st["dma"] += 1
                if st["mm"] < st["dma"] - 1 or (st["dma"] == len(jobs) and st["mm"] < st["dma"]):
                    mod_mm(st["mm"])
                    st["mm"] += 1

        def mod_require(l, s_):
            last = l * 36 + s_ * 12 + 11
            while st["mm"] <= last:
                mod_pump()
            if (l, s_) in st["fin"]:
                return
            st["fin"].add((l, s_))
            r = r_modls[l][s_]
            ts(S, "dve", Acoef[:, l, s_, :], modT[:, l, (3 * s_ + 1) * 8:(3 * s_ + 2) * 8], 1.0, ALU.add, [r], [r])
            tt(S, "dve", Acoef[:, l, s_, :], Acoef[:, l, s_, :], gvec[:, l * 24 + s_ * 8:l * 24 + s_ * 8 + 8],
               ALU.mult, [r, r_gvec], [r])
            ts(S, "dve", Gcoef[:, l, s_, :], modT[:, l, (3 * s_ + 2) * 8:(3 * s_ + 3) * 8],
               0.5 if s_ != 1 else 1.0, ALU.mult, [r], [r])

        self.mod_pump = mod_pump
        self.mod_require = mod_require

        def rms_rstd(c, src_fn, src_regs, nk, inv_n, ones_l):
            pb, pr = self.bank()
            for k in range(nk):
                i = self.sq_rr % 2
                self.sq_rr += 1
                act(S, sqb[:, i, :], src_fn(k), AF.Square, [src_regs[k]], [r_sq[i]])
                mm(S, [(pb[:], ones_l, sqb[:, i, :], k == 0, k == nk - 1)], [r_sq[i], r_ones], [pr])
            act(S, rstd[:], pb[:], AF.Ln, [pr], [r_rstd], bias=EPS, scale=inv_n)
            act(S, rstd[:], rstd[:], AF.Exp, [r_rstd], [r_rstd], scale=-0.5)

        def norm_mod(l, s):
            for c in range(2):
                rms_rstd(c, lambda k: xT[:, k, CH(c)], [xr[k][c] for k in range(8)], 8, 1.0 / D, ones_bf[:])
                for k in range(8):
                    i = self.f32_rr % 3
                    self.f32_rr += 1
                    tt(S, "dve", f32s[:, i, :], xT[:, k, CH(c)], rstd[:], ALU.mult,
                       [xr[k][c], r_rstd], [r_f32s[i]])
                    act(S, hT[:, k, CH(c)], f32s[:, i, :], AF.Identity, [r_f32s[i], r_modls[l][s]],
                        [hr[k][c]], bias=modT[:, l, 3 * s * 8 + k:3 * s * 8 + k + 1],
                        scale=Acoef[:, l, s, k:k + 1])

        def resid_add(l, s, dt_, c, pb, pr):
            i = self.f32_rr % 3
            self.f32_rr += 1
            act(S, f32s[:, i, :], pb[:], AF.Copy, [pr, r_modls[l][s]], [r_f32s[i]], scale=Gcoef[:, l, s, dt_:dt_ + 1])
            tt(S, "dve", xT[:, dt_, CH(c)], xT[:, dt_, CH(c)], f32s[:, i, :], ALU.add,
               [xr[dt_][c], r_f32s[i]], [xr[dt_][c]])

        def ffn(l, s, wg, wu, wdn):
            self.arena_reset()
            aT = self.carve([128, NFF, T], BF16)
            ar = [[R("a%d_%d" % (j, c)) for c in range(2)] for j in range(NFF)]
            sg = [self.carve([128, 512], F32) for _ in range(3)]
            r_sg = RL(3, "sg")
            sg_rr = 0
            mod_require(l, s)
            norm_mod(l, s)
            wgv = wg[l].rearrange("(k p) n -> p k n", p=128)
            wuv = wu[l].rearrange("(k p) n -> p k n", p=128)
            for g in range(NFF // 2):
                c0 = g * 256
                sl, sr = self.wload(self.parts_gu(wg, wu, l, g), key=("gu", l, s, g))
                s3 = self.slot_view(sl, 8)
                if l == 0 and s == 0:
                    mod_pump(1)
                for jj in range(2):
                    j = g * 2 + jj
                    for c in range(2):
                        pg, prg = self.bank()
                        pu, pru = self.bank()
                        groups = []
                        for k in range(8):
                            groups.append((pg[:], s3[:, k, jj * 128:(jj + 1) * 128], hT[:, k, CH(c)], k == 0, k == 7))
                        for k in range(8):
                            groups.append((pu[:], s3[:, k, 256 + jj * 128:256 + (jj + 1) * 128], hT[:, k, CH(c)],
                                           k == 0, k == 7))
                        mm(S, groups, [sr] + [hr[k][c] for k in range(8)], [prg, pru])
                        i = sg_rr % 3
                        sg_rr += 1
                        act(S, sg[i], pg[:], AF.Silu, [prg], [r_sg[i]])
                        tt(S, "dve", aT[:, j, CH(c)], sg[i], pu[:], ALU.mult, [r_sg[i], pru], [ar[j][c]])
            wdv = wdn[l].rearrange("(j p) n -> p j n", p=128)
            for dt_ in range(8):
                sl, sr = self.wload([(lambda sl_: sl_[:, 0:NFF * 128].rearrange("p (j n) -> p j n", j=NFF),
                                      wdv[:, :, dt_ * 128:(dt_ + 1) * 128])])
                s3 = sl[:, 0:NFF * 128].rearrange("p (j n) -> p j n", j=NFF)
                for c in range(2):
                    pb, pr = self.bank()
                    groups = [(pb[:], s3[:, j, :], aT[:, j, CH(c)], j == 0, j == NFF - 1) for j in range(NFF)]
                    mm(S, groups, [sr] + [ar[j][c] for j in range(NFF)], [pr])
                    resid_add(l, s, dt_, c, pb, pr)

        self.wd = wd
        self.ctx = dict(xT=xT, hT=hT, xr=xr, hr=hr, CH=CH, rms_rstd=rms_rstd, norm_mod=norm_mod,
                        resid_add=resid_add, ones_bf=ones_bf, r_ones=r_ones, gvec=gvec, r_gvec=r_gvec,
                        f32s=f32s, r_f32s=r_f32s, rstd=rstd, r_rstd=r_rstd, modT=modT, sqb=sqb, r_sq=r_sq)


        self.ARENA_BYTES = 84 * 1024
        LPW = 304
        self.LP = dict(BG=0, GM=16, GQK=272, SK=274, CV=278, HB=302)
        d = {}
        d["lp"] = inp("lp", [128, NL, LPW])
        d["cflags"] = inp("cflags", [128, 4])
        d["ident"] = inp("ident", [128, 128])
        d["blockones"] = inp("blockones", [128, 128])
        d["tri"] = inp("tri", [128, 4, 128])
        d["ropeP"] = inp("ropeP", [128, 128])
        d["ropeCS"] = inp("ropeCS", [128, 2, T])
        d["swamask"] = inp("swamask", [128, 8, 384])
        d["gmAB"] = inp("gmAB", [5, 2, T])
        d["fw1"] = inp("fw1", [33, NL, 64])
        d["fw2"] = inp("fw2", [64, NL, 64])
        d["fw3"] = inp("fw3", [64, NL, 256])
        d["fvec"] = inp("fvec", [64, NL, 3])
        d["fb3"] = inp("fb3", [1, NL, 256])
        d["featsT"] = inp("featsT", [33, T])
        d["window"] = inp("window", [128, 8, 256])
        d["dftF"] = inp("dftF", [2, T, T])
        d["dftG"] = inp("dftG", [2, T, T])
        d["sel"] = inp("sel", [4, 130])
        d["C0"] = inp("C0", [NL, 2, 128, 2, 65])
        d["m0rep"] = inp("m0rep", [128, NL, 2, 2])
        d["m0c"] = inp("m0c", [4, NL * 2])
        d["gkT"] = inp("gkT", [NL, 2, 128, 256])
        d["gvp"] = inp("gvp", [NL, 2, 128, 2, 192])
        d["skT"] = inp("skT", [NL, 2, 128, 256])
        d["svp"] = inp("svp", [NL, 2, 128, 2, 192])
        d["o_gk"] = outp("o_gk", [NL, 128, T])
        d["o_gv"] = outp("o_gv", [NL, T, 128])
        d["o_sk"] = outp("o_sk", [NL, 128, T])
        d["o_sv"] = outp("o_sv", [NL, T, 128])
        d["o_C"] = outp("o_C", [NL, 2, 4, 128, 2, 65])
        d["o_m"] = outp("o_m", [NL, 2, 4, 4])
        self.d = d
        if DEBUG:
            self.dbg_ymix = outp("dbg_ymix", [NL, 128, 8, T])
        k = {}
        k["lp"] = sb("lp", [128, NL, LPW]); k["cflags"] = sb("cflags", [128, 4])
        k["ident_f"] = sb("ident_f", [128, 128]); k["ident_bf"] = sb("ident_bf", [128, 128], BF16)
        k["blockones"] = sb("blockones", [128, 128], BF16)
        k["tri"] = sb("tri", [128, 4, 128]); k["ropeP"] = sb("ropeP", [128, 128])
        k["ones_f"] = sb("ones_f", [128, 128])
        k["fw1"] = sb("fw1", [33, NL, 64]); k["fw2"] = sb("fw2", [64, NL, 64]); k["fw3"] = sb("fw3", [64, NL, 256])
        k["fvec"] = sb("fvec", [64, NL, 3]); k["fb3"] = sb("fb3", [1, NL, 256]); k["fs"] = sb("fs", [64, NL, 4])
        k["sel"] = sb("sel", [4, 130]); k["m0rep"] = sb("m0rep", [128, NL, 2, 2]); k["m0c"] = sb("m0c", [4, NL * 2])
        k["Clo"] = sb("Clo", [128, 2, 2, 65]); k["Chi"] = sb("Chi", [128, 2, 2, 65])
        k["Cblo"] = sb("Cblo", [128, 2, 2, 66], BF16); k["Cbhi"] = sb("Cbhi", [128, 2, 2, 66], BF16)
        k["cvb"] = sb("cvb", [128, NL, 6, 2])
        self.k = k
        r_k = R("consts")
        self.r_k = r_k
        for nm in ("lp", "cflags", "tri", "ropeP", "fw1", "fw2", "fw3", "fvec", "fb3", "sel", "m0rep", "m0c"):
            S.dma("sp", k[nm][:], d[nm], writes=[r_k])
        S.dma("sp", k["ident_f"][:], d["ident"], writes=[r_k])
        S.dma("pool", k["ident_bf"][:], d["ident"], writes=[r_k])
        S.dma("pool", k["blockones"][:], d["blockones"], writes=[r_k])
        S.op("pool", lambda e: e.memset(k["ones_f"][:], 1.0), writes=[r_k])
        for nm in ("Clo", "Chi", "Cblo", "Cbhi"):
            S.op("pool", lambda e, nm=nm: e.memset(k[nm][:], 0.0), writes=[r_k])
        i2p = float(1.0 / (2 * math.pi))
        ts(S, "dve", k["fs"][:, :, 0:1], k["fvec"][:, :, 2:3], i2p, ALU.mult, [r_k], [r_k])
        tt(S, "dve", k["fs"][:, :, 1:2], k["fs"][:, :, 0:1], k["fvec"][:, :, 0:1], ALU.mult, [r_k], [r_k])
        tt(S, "dve", k["fs"][:, :, 2:3], k["fs"][:, :, 0:1], k["fvec"][:, :, 1:2], ALU.mult, [r_k], [r_k])
        CV = self.LP["CV"]
        for l in range(NL):
            cvv = k["lp"][:, l, CV:CV + 24].rearrange("p (a b) -> p a b", a=6)
            for (j, col) in ((0, 0), (1, 2)):
                ts(S, "dve", k["cvb"][:, l, :, j:j + 1], cvv[:, :, col:col + 1], k["cflags"][:, 2:3], ALU.mult,
                   [r_k], [r_k], s2=-1.0, op1=ALU.mult)

        for l in range(NL):
            ffn(l, 0, wd["w1_gate"], wd["w1_up"], wd["w1_down"])
            self.prefetch(("A1", l), self.parts_win(l, 0, 512))
            self.prefetch(("A2", l), self.parts_win(l, 512, 512))
            self.prefetch(("G", l), self.parts_win(l, 1024, 16))
            self.mixer(l)
            for g in range(2):
                self.prefetch(("gu", l, 2, g), self.parts_gu(wd["w2_gate"], wd["w2_up"], l, g))
            ffn(l, 2, wd["w2_gate"], wd["w2_up"], wd["w2_down"])
            if l + 1 < NL:
                for g in range(2):
                    self.prefetch(("gu", l + 1, 0, g), self.parts_gu(wd["w1_gate"], wd["w1_up"], l + 1, g))

        gfo = NL * 24
        yv = yT_d.rearrange("(k p) t -> p k t", p=128)
        self.arena_reset()
        ost_ = self.carve([128, 2, 512], F32)
        ost = [ost_[:, 0, :], ost_[:, 1, :]]
        r_ost = RL(2, "ost")
        o_rr = 0
        for c in range(2):
            rms_rstd(c, lambda k: xT[:, k, CH(c)], [xr[k][c] for k in range(8)], 8, 1.0 / D, ones_bf[:])
            for k in range(8):
                i = self.f32_rr % 3
                self.f32_rr += 1
                tt(S, "dve", f32s[:, i, :], xT[:, k, CH(c)], rstd[:], ALU.mult, [xr[k][c], r_rstd], [r_f32s[i]])
                o = o_rr % 2
                o_rr += 1
                act(S, ost[o], f32s[:, i, :], AF.Copy, [r_f32s[i], r_gvec], [r_ost[o]],
                    scale=gvec[:, gfo + k:gfo + k + 1])
                S.dma("sp", yv[:, k, CH(c)], ost[o], reads=[r_ost[o]])
        S.finish()

    def mixer(self, l):
        S = self.S
        C = self.ctx
        hT, hr, CH = C["hT"], C["hr"], C["CH"]
        self.arena_reset()
        ymix = self.carve([128, 8, T], BF16)
        ymr = [[R("ym%d_%d" % (k, c)) for c in range(2)] for k in range(8)]
        base = self.aoff
        self.mod_require(l, 1)
        C["norm_mod"](l, 1)
        win = self.wd["w_in"][l].rearrange("(k p) n -> p k n", p=128)
        self.bank_pool = list(range(8))
        self.mix_mlstm(l, ymix, ymr, win)
        self.prefetch(("attn", l, True), self.parts_attn(l, True))
        self.arena_reset(base)
        if STAGE >= 3:
            self.mix_attn(l, ymix, ymr, win, glob=True)
            self.prefetch(("attn", l, False), self.parts_attn(l, False))
            self.arena_reset(base)
            self.mix_attn(l, ymix, ymr, win, glob=False)
            self.prefetch(("D1", l), self.parts_win(l, 2064, 512))
            self.prefetch(("D2", l), self.parts_win(l, 2576, 256))
            self.arena_reset(base)
        if STAGE >= 4:
            self.mix_hyena(l, ymix, ymr, win)
        self.prefetch(("wout", l, 0), self.parts_wout(l, 0))
        self.prefetch(("wout", l, 1), self.parts_wout(l, 1))
        self.bank_pool = list(range(8))
        if DEBUG:
            f32s, r_f32s = C["f32s"], C["r_f32s"]
            for k in range(8):
                for c in range(2):
                    i = self.f32_rr % 3
                    self.f32_rr += 1
                    act(S, f32s[:, i, :], ymix[:, k, CH(c)], AF.Copy, [ymr[k][c]], [r_f32s[i]])
                    S.dma("sp", self.dbg_ymix[l, :, k, c * 512:(c + 1) * 512], f32s[:, i, :], reads=[r_f32s[i]])
        wov = self.wd["w_out"][l].rearrange("(k p) n -> p k n", p=128)
        for half in range(2):
            sl, sr = self.wload(self.parts_wout(l, half), key=("wout", l, half))
            s3 = self.slot_view(sl, 8)
            for j in range(4):
                dt_ = half * 4 + j
                for c in range(2):
                    pb, pr = self.bank()
                    groups = [(pb[:], s3[:, k, j * 128:(j + 1) * 128], ymix[:, k, CH(c)], k == 0, k == 7) for k in range(8)]
                    mm(S, groups, [sr] + [ymr[k][c] for k in range(8)], [pr])
                    C["resid_add"](l, 1, dt_, c, pb, pr)

    def parts_attn(self, l, glob):
        win = self.wd["w_in"][l].rearrange("(k p) n -> p k n", p=128)
        sv8 = lambda sl_: self.slot_view(sl_, 8)
        q0 = 1040 if glob else 1552
        k0, v0 = q0 + 256, q0 + 384
        return [(lambda sl_: sv8(sl_)[:, :, 0:256], win[:, :, q0:q0 + 256]),
                (lambda sl_: sv8(sl_)[:, :, 256:384], win[:, :, k0:k0 + 128]),
                (lambda sl_: sv8(sl_)[:, :, 384:448], win[:, :, k0 + 64:k0 + 128]),
                (lambda sl_: sv8(sl_)[:, :, 448:512], win[:, :, k0:k0 + 64]),
                (lambda sl_: sv8(sl_)[:, :, 512:640], win[:, :, v0:v0 + 128])]

    def parts_win(self, l, c0, n):
        win = self.wd["w_in"][l].rearrange("(k p) n -> p k n", p=128)
        return [(lambda sl_: self.slot_view(sl_, 8)[:, :, 0:n], win[:, :, c0:c0 + n])]

    def parts_wout(self, l, half):
        wov = self.wd["w_out"][l].rearrange("(k p) n -> p k n", p=128)
        return [(lambda sl_: self.slot_view(sl_, 8)[:, :, 0:512], wov[:, :, half * 512:(half + 1) * 512])]

    def parts_gu(self, wg, wu, l, g):
        wgv = wg[l].rearrange("(k p) n -> p k n", p=128)
        wuv = wu[l].rearrange("(k p) n -> p k n", p=128)
        c0 = g * 256
        return [(lambda sl_: self.slot_view(sl_, 8)[:, :, 0:256], wgv[:, :, c0:c0 + 256]),
                (lambda sl_: self.slot_view(sl_, 8)[:, :, 256:512], wuv[:, :, c0:c0 + 256])]

    def pbank(self):
        i = self.bank_pool[self.bank_rr % len(self.bank_pool)]
        self.bank_rr += 1
        return self.ps[i], self.psr[i]

    def proj_fm(self, c, s3, col0, pb):
        hT, CH = self.ctx["hT"], self.ctx["CH"]
        return [(pb[:], s3[:, k, col0:col0 + 128], hT[:, k, CH(c)], k == 0, k == 7) for k in range(8)]

    def proj_tok(self, tt_, s3, col0, ncols, pb, pc0):
        hT = self.ctx["hT"]
        return [(pb[:, pc0:pc0 + ncols], hT[:, k, tt_ * 128:(tt_ + 1) * 128], s3[:, k, col0:col0 + ncols], k == 0, k == 7)
                for k in range(8)]

    def mix_mlstm(self, l, ymix, ymr, win):
        S, k_, d = self.S, self.k, self.d
        C = self.ctx
        hr, CH = C["hr"], C["CH"]
        LP = self.LP
        lp = k_["lp"]
        r_k = self.r_k
        hall = [hr[k][c] for k in range(8) for c in range(2)]
        sv8 = lambda sl_: self.slot_view(sl_, 8)
        slA1, srA1 = self.wload(self.parts_win(l, 0, 512), key=("A1", l))
        slA2, srA2 = self.wload(self.parts_win(l, 512, 512), key=("A2", l))
        slG, srG = self.wload(self.parts_win(l, 1024, 16), key=("G", l))
        sA1, sA2, sG = sv8(slA1), sv8(slA2), sv8(slG)
        if CUT == 1:
            return
        aqT = self.carve([128, 2, T], BF16)
        akp = self.carve([128, 4, T], BF16)
        ktok = self.carve([128, 8, 256], BF16)
        vaug = self.carve([128, 8, 4, 66], BF16)
        sgo = self.carve([128, 8, 256], BF16)
        gts = self.carve([128, 8, 16], F32)
        lf = self.carve([128, 8, 8], F32)
        cum = self.carve([128, 8, 16], F32)
        call = self.carve([128, 8, 8], F32)
        wall = self.carve([128, 8, 8], F32)
        wkl = self.carve([128, 8, 8], F32)
        wkall = self.carve([128, 8, 8], F32)
        wkm = self.carve([128, 8, 8], F32)
        dec = self.carve([128, 8, 4], F32)
        lfB = self.carve([128, 8, 128], F32)
        E = self.carve([128, 8, 128], F32)
        AT = self.carve([128, 2, 4, 128], BF16)
        hf = self.carve([128, 8, 256], F32)
        numt = self.carve([128, 2, 260], F32)
        tmpn = self.carve([128, 2, 260], F32)
        h64 = self.carve([128, 2, 256], F32)
        dsm = self.carve([128, 2, 8], F32)
        kwp = self.carve([128, 2, 4, 192], BF16)
        yatok = self.carve([128, 8, 256], BF16)
        snap = self.carve([128, 2, 4, 130], F32)
        e0 = self.carve([128, 2, 2], F32)
        ssall = self.carve([128, 8, 4], F32)
        sq1 = self.carve([128, 2, 256], F32)
        mst = self.carve([128, 64], F32)
        scl = self.carve([128, 2, 8], F32)
        r_aq, r_akp = RL(2, "aq"), RL(2, "akp")
        r_ktok, r_vaug, r_sgo, r_gts = RL(8, "ktok"), RL(8, "vaug"), RL(8, "sgo"), R("gts")
        r_gate = R("gate")
        r_lfB, r_E, r_AT = RL(2, "lfB"), RL(2, "E"), RL(2, "AT")
        r_hf = RL(8, "hf")
        r_num, r_tmpn, r_h64, r_dsm, r_kwp = RL(2, "num"), RL(2, "tmpn"), RL(2, "h64"), RL(2, "dsm"), RL(2, "kwp")
        r_C, r_Cb = RL(2, "C"), RL(2, "Cb")
        r_snap = [[R("snap") for _ in range(4)] for _ in range(2)]
        r_ms, r_scl = R("mst"), R("scl")
        r_ya = RL(8, "ya")
        r_ss = R("ss")
        r_sq1 = RL(2, "sq1")
        S.op("pool", lambda e: e.memset(akp, 0.0), writes=r_akp)
        S.op("pool", lambda e: e.memset(kwp, 0.0), writes=r_kwp)
        S.op("pool", lambda e: e.memset(vaug[:, :, :, 64:65], 1.0), writes=r_vaug)
        S.op("pool", lambda e: e.memset(hf, 0.0), writes=r_hf)
        if CUT == 2:
            return
        for t2 in range(2):
            for c in range(2):
                pb, pr = self.pbank()
                mm(S, self.proj_fm(c, sA1, t2 * 128, pb), [srA1] + hall, [pr])
                act(S, aqT[:, t2, CH(c)], pb[:], AF.Copy, [pr], [r_aq[c]])
        if CUT == 21:
            return
        for t2 in range(2):
            for c in range(2):
                pb, pr = self.pbank()
                mm(S, self.proj_fm(c, sA1, 256 + t2 * 128, pb), [srA1] + hall, [pr])
                act(S, akp[0:64, 2 * t2, CH(c)], pb[0:64, :], AF.Copy, [pr], [r_akp[c]])
                cp(S, "dve", akp[64:128, 2 * t2 + 1, CH(c)], pb[64:128, :], [pr], [r_akp[c]])
        if CUT == 22:
            return
        BG = LP["BG"]
        for t_ in range(8):
            p1, pr1 = self.pbank()
            p2, pr2 = self.pbank()
            g = self.proj_tok(t_, sA1, 256, 256, p1, 0) + self.proj_tok(t_, sA2, 0, 256, p1, 256)
            g += self.proj_tok(t_, sA2, 256, 256, p2, 0) + self.proj_tok(t_, sG, 0, 16, p2, 256)
            mm(S, g, [srA1, srA2, srG] + hall, [pr1, pr2])
            if CUT == 23:
                continue
            act(S, ktok[:, t_, :], p1[:, 0:256], AF.Copy, [pr1], [r_ktok[t_]])
            cp(S, "dve", vaug[:, t_, :, 0:64], p1[:, 256:512].rearrange("p (a b) -> p a b", a=4), [pr1], [r_vaug[t_]])
            if CUT == 24:
                continue
            act(S, sgo[:, t_, :], p2[:, 0:256], AF.Sigmoid, [pr2], [r_sgo[t_]])
            tt(S, "dve", gts[:, t_, :], p2[:, 256:272], lp[:, l, BG:BG + 16], ALU.add, [pr2, r_k], [r_gts])
        if CUT in (3, 23, 24):
            return
        ai, af = gts[:, :, 0:8], gts[:, :, 8:16]
        act(S, lf, af, AF.Exp, [r_gts], [r_gate], scale=-1.0)
        act(S, lf, lf, AF.Ln, [r_gate], [r_gate], bias=1.0)
        ts(S, "dve", lf, lf, -1.0, ALU.mult, [r_gate], [r_gate])
        tri = k_["tri"]
        pbc, prc = self.pbank()
        g = []
        for t_ in range(8):
            g.append((pbc[:, t_ * 16:t_ * 16 + 4], tri[:, 0, :], lf[:, t_, 0:4], True, True))
            g.append((pbc[:, t_ * 16 + 4:t_ * 16 + 8], tri[:, 1, :], lf[:, t_, 4:8], True, True))
            g.append((pbc[:, t_ * 16 + 8:t_ * 16 + 16], k_["ones_f"][:], lf[:, t_, 0:8], True, True))
        mm(S, g, [r_gate, r_k], [prc])
        cp(S, "dve", cum, pbc[:, 0:128].rearrange("p (a b) -> p a b", a=8), [prc], [r_gate])
        bc, bt = cum[:, :, 0:8], cum[:, :, 8:16]
        tt(S, "dve", call, ai, bc, ALU.subtract, [r_gts, r_gate], [r_gate])
        ts(S, "dve", call, call, LN8, ALU.add, [r_gate], [r_gate])
        act(S, wall, bc, AF.Exp, [r_gate], [r_gate])
        tt(S, "dve", wkl, call, bt, ALU.add, [r_gate], [r_gate])
        act(S, wkall, wkl, AF.Exp, [r_gate], [r_gate])
        ts(S, "dve", wkm, wkl, -LN8, ALU.add, [r_gate], [r_gate])
        for g_ in range(2):
            rows = slice(g_ * 64, (g_ + 1) * 64)
            act(S, dec[rows, :, :], cum[rows, :, 8 + g_:16:2], AF.Exp, [r_gate], [r_gate])
        if CUT == 4:
            return
        ident_f = k_["ident_f"]
        for dr in range(2):
            pbm, prm = self.pbank()
            g = [(pbm[0:4, t_:t_ + 1], lf[:, t_, dr * 4:dr * 4 + 4], k_["ones_f"][:, 0:1], True, True) for t_ in range(8)]
            mm(S, g, [r_gate, r_k], [prm])
            cp(S, "dve", mst[0:4, dr * 8:dr * 8 + 8], pbm[0:4, 0:8], [prm], [r_ms])
            for hh in range(2):
                pbt, prt = self.pbank()
                for q in range(4):
                    t_ = hh * 4 + q
                    S.op("pe", lambda e, t_=t_, q=q, pbt=pbt: e.transpose(pbt[0:4, q * 128:(q + 1) * 128],
                                                                          wkm[:, t_, dr * 4:dr * 4 + 4], ident_f[:]),
                         [r_gate, r_k], [prt])
                S.op("dve", lambda e, pbt=pbt, hh=hh: e.tensor_reduce(
                    out=mst[0:4, 16 + dr * 8 + hh * 4:16 + dr * 8 + hh * 4 + 4],
                    in_=pbt[0:4, :].rearrange("p (a b) -> p a b", a=4), axis=AX.X, op=ALU.max), [prt], [r_ms])
            bv_ = mst[0:4, dr * 8:dr * 8 + 8].rearrange("p (a b) -> p a b", a=4)
            av_ = mst[0:4, 16 + dr * 8:16 + dr * 8 + 8].rearrange("p (a b) -> p a b", a=4)
            fi, se = (0, 1) if dr == 0 else (1, 0)
            mf = mst[0:4, 32 + dr * 4:32 + dr * 4 + 4]
            ts(S, "dve", mf, bv_[:, :, fi], k_["m0c"][0:4, l * 2 + dr:l * 2 + dr + 1], ALU.add, [r_ms, r_k], [r_ms])
            tt(S, "dve", mf, mf, av_[:, :, fi], ALU.max, [r_ms], [r_ms])
            tt(S, "dve", mf, mf, bv_[:, :, se], ALU.add, [r_ms], [r_ms])
            tt(S, "dve", mf, mf, av_[:, :, se], ALU.max, [r_ms], [r_ms])
            S.dma("sp", d["o_m"][l, dr], mf, reads=[r_ms])
            en = mst[0:4, 40 + dr * 4:40 + dr * 4 + 4]
            act(S, en, mf, AF.Exp, [r_ms], [r_ms], scale=-1.0)
            rhs2 = mst[0:4, 48 + dr * 8:48 + dr * 8 + 8]
            tt(S, "dve", rhs2.rearrange("p (a b) -> p a b", a=4), en.unsqueeze(2).to_broadcast([4, 4, 2]),
               k_["sel"][0:4, 128:130].unsqueeze(1).to_broadcast([4, 4, 2]), ALU.mult, [r_ms, r_k], [r_ms])
            pbs, prs = self.pbank()
            mm(S, [(pbs[:, 0:8], k_["sel"][0:4, 0:128], rhs2, True, True)], [r_ms, r_k], [prs])
            cp(S, "dve", scl[:, dr, :], pbs[:, 0:8], [prs], [r_scl])
        if CUT == 5:
            return
        Clo, Chi, Cblo, Cbhi = k_["Clo"], k_["Chi"], k_["Cblo"], k_["Cbhi"]
        act(S, e0, k_["m0rep"][:, l, :, :], AF.Exp, [r_k], [r_gate])
        halves = ((slice(0, 64), Clo, Cblo), (slice(64, 128), Chi, Cbhi))
        for dr in range(2):
            for (rows, Cx, Cbx) in halves:
                S.dma("sp", Cx[rows, dr, :, :], d["C0"][l, dr, rows, :, :], writes=[r_C[dr]])
            for (rows, Cx, Cbx) in halves:
                tt(S, "dve", Cx[rows, dr, :, :], Cx[rows, dr, :, :], e0[rows, dr, :].unsqueeze(2).to_broadcast([64, 2, 65]),
                   ALU.mult, [r_C[dr], r_gate], [r_C[dr]])
                act(S, Cbx[rows, dr, :, 0:65], Cx[rows, dr, :, :], AF.Copy, [r_C[dr]], [r_Cb[dr]])
        if CUT == 6:
            return
        keep = k_["cflags"][:, 1:2]

        def process(dr, t_):
            tok = slice(t_ * 128, (t_ + 1) * 128)
            chs = slice(dr * 4, dr * 4 + 4)
            cp(S, "pool", lfB[:, chs, :], lf[:, t_, chs].unsqueeze(2).to_broadcast([128, 4, 128]), [r_gate], [r_lfB[dr]])
            yield
            pbe, pre = self.pbank()
            g = []
            for h in range(4):
                o = pbe[:, h * 128:(h + 1) * 128]
                g.append((o, lfB[:, dr * 4 + h, :], tri[:, dr, :], True, False))
                g.append((o, ident_f[:], tri[:, 2 + dr, :], False, True))
            mm(S, g, [r_lfB[dr], r_k], [pre])
            yield
            pbs_, prs_ = self.pbank()
            g = [(pbs_[:, h * 128:(h + 1) * 128], akp[:, h, tok], aqT[:, h // 2, tok], True, True) for h in range(4)]
            mm(S, g, r_akp + r_aq, [prs_])
            yield
            for h in range(4):
                act(S, E[:, dr * 4 + h, :], pbe[:, h * 128:(h + 1) * 128], AF.Exp, [pre, r_gate], [r_E[dr]],
                    bias=call[:, t_, dr * 4 + h:dr * 4 + h + 1])
                yield
            tt(S, "dve", AT[:, dr, :, :], pbs_[:].rearrange("p (a b) -> p a b", a=4), E[:, chs, :], ALU.mult,
               [prs_, r_E[dr]], [r_AT[dr]])
            yield
            pbi, pri = self.pbank()
            g = [(pbi[:, h * 65:(h + 1) * 65], AT[:, dr, h, :], vaug[:, t_, h, 0:65], True, True) for h in range(4)]
            mm(S, g, [r_AT[dr], r_vaug[t_]], [pri])
            yield
            pbx, prx = self.pbank()
            g = [(pbx[:, h * 65:(h + 1) * 65], aqT[:, h // 2, tok], (Cblo if h % 2 == 0 else Cbhi)[:, dr, h // 2, 0:65], True, True)
                 for h in range(4)]
            mm(S, g, r_aq + [r_Cb[dr]], [prx])
            yield
            v3 = lambda ap: ap.rearrange("p (a b) -> p a b", a=4)
            tt(S, "dve", v3(tmpn[:, dr, :]), v3(pbx[:, 0:260]), wall[:, t_, chs].unsqueeze(2).to_broadcast([128, 4, 65]),
               ALU.mult, [prx, r_gate], [r_tmpn[dr]])
            yield
            tt(S, "dve", numt[:, dr, :], pbi[:, 0:260], tmpn[:, dr, :], ALU.add, [pri, r_tmpn[dr]], [r_num[dr]])
            yield
            nv = v3(numt[:, dr, :])
            dn, rd = dsm[:, dr, 0:4], dsm[:, dr, 4:8]
            ts(S, "dve", dn, nv[:, :, 64], -1.0, ALU.mult, [r_num[dr]], [r_dsm[dr]], s2=1.0, op1=ALU.max)
            yield
            tt(S, "dve", dn, dn, nv[:, :, 64], ALU.max, [r_num[dr], r_dsm[dr]], [r_dsm[dr]])
            yield
            recip(S, rd, dn, [r_dsm[dr]], [r_dsm[dr]])
            yield
            tt(S, "dve", v3(h64[:, dr, :])[:, :, 0:64], nv[:, :, 0:64], rd.unsqueeze(2).to_broadcast([128, 4, 64]), ALU.mult,
               [r_num[dr], r_dsm[dr]], [r_h64[dr]])
            yield
            tt(S, "dve", hf[:, t_, :], hf[:, t_, :], h64[:, dr, :], ALU.add, [r_h64[dr], r_hf[t_]], [r_hf[t_]])
            yield
            self.mod_pump()
            yield
            tt(S, "pool", kwp[:, dr, :, 64:128], ktok[:, t_, :].rearrange("p (a b) -> p a b", a=4),
               wkall[:, t_, chs].unsqueeze(2).to_broadcast([128, 4, 64]), ALU.mult, [r_ktok[t_], r_gate], [r_kwp[dr]])
            yield
            pbd, prd = self.pbank()
            g = []
            for j in range(2):
                o = pbd[:, j * 65:(j + 1) * 65]
                g.append((o, kwp[:, dr, 2 * j, 64:192], vaug[:, t_, 2 * j, 0:65], True, False))
                g.append((o, kwp[:, dr, 2 * j + 1, 0:128], vaug[:, t_, 2 * j + 1, 0:65], False, True))
            mm(S, g, [r_kwp[dr], r_vaug[t_]], [prd])
            yield
            for (rows, Cx, Cbx) in halves:
                tt(S, "dve", Cx[rows, dr, :, :], Cx[rows, dr, :, :],
                   dec[rows, t_, dr * 2:dr * 2 + 2].unsqueeze(2).to_broadcast([64, 2, 65]), ALU.mult,
                   [r_C[dr], r_gate, r_Cb[dr]], [r_C[dr]])
                yield
                tt(S, "dve", Cx[rows, dr, :, :], Cx[rows, dr, :, :], pbd[rows, 0:130].rearrange("p (a b) -> p a b", a=2),
                   ALU.add, [r_C[dr], prd], [r_C[dr]])
                yield
            end = (t_ % 2 == 1) if dr == 0 else (t_ % 2 == 0)
            if end:
                sq_ = t_ // 2
                for (rows, Cx, Cbx) in halves:
                    tt(S, "dve", snap[rows, dr, sq_, :].rearrange("p (a b) -> p a b", a=2), Cx[rows, dr, :, :],
                       scl[rows, dr, sq_ * 2:sq_ * 2 + 2].unsqueeze(2).to_broadcast([64, 2, 65]), ALU.mult,
                       [r_C[dr], r_scl], [r_snap[dr][sq_]])
                    yield
                S.dma("sp", d["o_C"][l, dr, sq_], snap[:, dr, sq_, :].rearrange("p (a b) -> p a b", a=2),
                      reads=[r_snap[dr][sq_]])
                yield
                for (rows, Cx, Cbx) in halves:
                    ts(S, "dve", Cx[rows, dr, :, :], Cx[rows, dr, :, :], keep[rows, :], ALU.mult, [r_C[dr], r_k], [r_C[dr]])
                    yield
            for (rows, Cx, Cbx) in halves:
                act(S, Cbx[rows, dr, :, 0:65], Cx[rows, dr, :, :], AF.Copy, [r_C[dr]], [r_Cb[dr]])
                yield

        for i in range(8):
            gens = [process(0, i), process(1, 7 - i)]
            while gens:
                for g__ in list(gens):
                    try:
                        next(g__)
                    except StopIteration:
                        gens.remove(g__)
            self.mod_pump()
        if CUT == 7:
            return
        GM = LP["GM"]
        for t_ in range(8):
            b_ = t_ % 2
            tt(S, "dve", sq1[:, b_, :], hf[:, t_, :], hf[:, t_, :], ALU.mult, [r_hf[t_]], [r_sq1[b_]])
            S.op("dve", lambda e, t_=t_, b_=b_: e.tensor_reduce(out=ssall[:, t_, :],
                                                              in_=sq1[:, b_, :].rearrange("p (a b) -> p a b", a=4),
                                                              axis=AX.X, op=ALU.add), [r_sq1[b_]], [r_ss])
        act(S, ssall, ssall, AF.Ln, [r_ss], [r_ss], bias=EPS, scale=1.0 / 64)
        act(S, ssall, ssall, AF.Exp, [r_ss], [r_ss], scale=-0.5)
        ident_bf = k_["ident_bf"]
        for t_ in range(8):
            b_ = t_ % 2
            tt(S, "dve", sq1[:, b_, :].rearrange("p (a b) -> p a b", a=4), hf[:, t_, :].rearrange("p (a b) -> p a b", a=4),
               ssall[:, t_, :].unsqueeze(2).to_broadcast([128, 4, 64]), ALU.mult, [r_hf[t_], r_ss], [r_sq1[b_]])
            tt(S, "pool", h64[:, b_, :], sgo[:, t_, :], lp[:, l, GM:GM + 256], ALU.mult, [r_sgo[t_], r_k], [r_h64[b_]])
            tt(S, "dve", yatok[:, t_, :], sq1[:, b_, :], h64[:, b_, :], ALU.mult, [r_h64[b_], r_sq1[b_]], [r_ya[t_]])
            pbt, prt = self.pbank()
            pv = pbt[:].bitcast(BF16)
            for t2 in range(2):
                S.op("pe", lambda e, t2=t2, pv=pv, t_=t_: e.transpose(pv[:, t2 * 128:(t2 + 1) * 128],
                                                                      yatok[:, t_, t2 * 128:(t2 + 1) * 128], ident_bf[:]),
                     [r_ya[t_], r_k], [prt])
            c = t_ // 4
            for t2 in range(2):
                if t2 == 0:
                    act(S, ymix[:, t2, t_ * 128:(t_ + 1) * 128], pv[:, t2 * 128:(t2 + 1) * 128], AF.Copy, [prt], [ymr[t2][c]])
                else:
                    cp(S, "dve", ymix[:, t2, t_ * 128:(t_ + 1) * 128], pv[:, t2 * 128:(t2 + 1) * 128], [prt], [ymr[t2][c]])

    def mix_attn(self, l, ymix, ymr, win, glob):
        S, k_, d = self.S, self.k, self.d
        C = self.ctx
        hr, CH = C["hr"], C["CH"]
        LP = self.LP
        lp = k_["lp"]
        r_k = self.r_k
        hall = [hr[k][c] for k in range(8) for c in range(2)]
        sv8 = lambda sl_: self.slot_view(sl_, 8)
        q0 = 1040 if glob else 1552
        k0, v0 = q0 + 256, q0 + 384
        sl, sr = self.wload(self.parts_attn(l, glob), key=("attn", l, glob))
        s3 = sv8(sl)
        ymb = 2 if glob else 4
        cs = self.carve([128, 2, T], F32)
        qpad = self.carve([128, 4, T], BF16)
        kfull = self.carve([128, 2, 1280], BF16)
        vpad = self.carve([128, 10, 2, 192], BF16)
        kout = self.carve([128, T], F32)
        vout = self.carve([128, 8, 128], F32)
        PT = self.carve([128, 3, 512], BF16)
        rden = self.carve([128, 2, 512], F32)
        raw = self.carve([128, 2, 512], F32)
        tb = self.carve([128, 2, 512], F32)
        gm = self.carve([128, 2, T], BF16)
        smask = self.carve([128, 8, 384], BF16) if not glob else None
        es = self.carve([128, 4], F32)
        r_cs, r_qp, r_kf, r_vp, r_kout, r_vout = R("cs"), RL(2, "qp"), R("kf"), RL(10, "vp"), R("kout"), R("vout")
        r_PT, r_rden, r_raw, r_tb, r_gm, r_sm, r_es = RL(3, "PT"), RL(2, "rden"), RL(2, "raw"), RL(2, "tb"), R("gm"), R("sm"), R("es")
        S.dma("sp", cs, d["ropeCS"], writes=[r_cs])
        S.op("pool", lambda e: e.memset(qpad, 0.0), writes=r_qp)
        S.op("pool", lambda e: e.memset(vpad[:, 2:10, :, :], 0.0), writes=r_vp[2:])
        kTd, vpd = (d["gkT"], d["gvp"]) if glob else (d["skT"], d["svp"])
        for x in range(2):
            S.dma("pool", kfull[:, x, 0:256], kTd[l, x], writes=[r_kf])
            S.dma("pool", vpad[:, x, :, :], vpd[l, x], writes=[r_vp[x]])
        if glob:
            S.dma("pool", gm[0:5, :, :], d["gmAB"], writes=[r_gm])
        if not glob:
            S.dma("pool", smask, d["swamask"], writes=[r_sm])
            SK = LP["SK"]
            act(S, es, lp[:, l, SK:SK + 4], AF.Exp, [r_k], [r_es])
        GQK = LP["GQK"]
        ropeP = k_["ropeP"]
        rstd2 = self.carve([128, 2, 512], F32)
        r_rstd2 = RL(2, "rstd2")

        def prelude(ti, c, b_):
            col0 = ti * 128
            pb, pr = self.pbank()
            mm(S, self.proj_fm(c, s3, col0, pb), [sr] + hall, [pr])
            yield
            rw = raw[:, b_, :]
            if glob:
                act(S, rw, pb[:], AF.Copy, [pr], [r_raw[b_]])
                yield
                i = self.sq_rr % 2
                self.sq_rr += 1
                sqb, r_sq = C["sqb"], C["r_sq"]
                act(S, sqb[:, i, :], rw, AF.Square, [r_raw[b_]], [r_sq[i]])
                yield
                p2, pr2 = self.pbank()
                mm(S, [(p2[:], k_["blockones"][:], sqb[:, i, :], True, True)], [r_sq[i], r_k], [pr2])
                yield
                rstd, r_rstd = rstd2[:, b_, :], r_rstd2[b_]
                act(S, rstd, p2[:], AF.Ln, [pr2], [r_rstd], bias=EPS, scale=1.0 / 64)
                yield
                act(S, rstd, rstd, AF.Exp, [r_rstd], [r_rstd], scale=-0.5)
                yield
                tt(S, "dve", rw, rw, rstd, ALU.mult, [r_raw[b_], r_rstd], [r_raw[b_]])
                yield
                gcol = GQK + (0 if ti < 2 else 1)
                act(S, rw, rw, AF.Copy, [r_raw[b_], r_k], [r_raw[b_]], scale=lp[:, l, gcol:gcol + 1])
                yield
            else:
                act(S, rw, pb[:], AF.Copy, [pr], [r_raw[b_]])
                yield
            p3, pr3 = self.pbank()
            mm(S, [(p3[:], ropeP[:], rw, True, True)], [r_raw[b_], r_k], [pr3])
            yield
            ta, tb_ = rw, tb[:, b_, :]
            tt(S, "pool", ta, rw, cs[:, 0, CH(c)], ALU.mult, [r_raw[b_], r_cs], [r_raw[b_]])
            yield
            tt(S, "dve", tb_, p3[:], cs[:, 1, CH(c)], ALU.mult, [pr3, r_cs], [r_tb[b_]])
            yield
            if ti < 2:
                for g_ in range(2):
                    rows = slice(g_ * 64, (g_ + 1) * 64)
                    tt(S, "dve", qpad[rows, 2 * ti + g_, CH(c)], ta[rows, :], tb_[rows, :], ALU.add, [r_tb[b_], r_raw[b_]], [r_qp[c]])
                    yield
            elif ti == 2:
                tt(S, "dve", kout[:, CH(c)], ta, tb_, ALU.add, [r_tb[b_], r_raw[b_]], [r_kout])
                yield
                act(S, kfull[:, 0, 256 + c * 512:256 + (c + 1) * 512], kout[:, CH(c)], AF.Copy, [r_kout], [r_kf])
                yield
            else:
                tt(S, "dve", kfull[:, 1, 256 + c * 512:256 + (c + 1) * 512], ta, tb_, ALU.add, [r_tb[b_], r_raw[b_]], [r_kf])
                yield

        its = [(ti, c) for ti in range(4) for c in range(2)]
        for i0 in range(0, 8, 2):
            gens = [prelude(its[i0][0], its[i0][1], 0), prelude(its[i0 + 1][0], its[i0 + 1][1], 1)]
            while gens:
                for g__ in list(gens):
                    try:
                        next(g__)
                    except StopIteration:
                        gens.remove(g__)
        S.dma("sp", (d["o_gk"] if glob else d["o_sk"])[l], kout, reads=[r_kout])
        for t_ in range(8):
            pb, pr = self.pbank()
            mm(S, self.proj_tok(t_, s3, 512, 128, pb, 0), [sr] + hall, [pr])
            act(S, vout[:, t_, :], pb[:, 0:128], AF.Copy, [pr], [r_vout])
            cp(S, "dve", vpad[:, 2 + t_, :, 64:128], pb[:, 0:128].rearrange("p (a b) -> p a b", a=2), [pr], [r_vp[2 + t_]])
        S.dma("sp", (d["o_gv"] if glob else d["o_sv"])[l].rearrange("(t p) n -> p t n", p=128), vout, reads=[r_vout])
        self.bank_pool = [0, 1, 2, 3]
        ones_bf = C["ones_bf"]
        r_ones = C["r_ones"]
        ident_bf = k_["ident_bf"]
        ctxb = k_["cflags"][:, 0:1]
        pt_rr = 0
        acc_rr = 0
        for h in range(4):
            kv, g_ = h // 2, h % 2
            kx = 0 if g_ == kv else 1
            rows = slice(g_ * 64, (g_ + 1) * 64)
            vw = slice(64, 192) if g_ == 0 else slice(0, 128)
            for c in range(2):
                if glob:
                    tiles = [(mt, 0, 512) for mt in range(10)]
                else:
                    tiles = [(0, 0, 512), (1, 0, 512)]
                    for j in range(8):
                        lo, hi = max((j - 1) * 128, c * 512), min((j + 2) * 128, (c + 1) * 512)
                        if hi > lo:
                            tiles.append((2 + j, lo - c * 512, hi - c * 512))
                ai_ = 4 + 2 * (acc_rr % 2)
                acc_rr += 1
                pn, prn, pd, prd = self.ps[ai_], self.psr[ai_], self.ps[ai_ + 1], self.psr[ai_ + 1]
                pend = []

                def qk(idx):
                    mt, lo, hi = tiles[idx]
                    pb, pr = self.pbank()
                    qs = slice(c * 512 + lo, c * 512 + hi)
                    g = [(pb[:, lo:hi], kfull[:, kx, mt * 128:(mt + 1) * 128], qpad[:, h, qs], True, mt < 2)]
                    rd = [r_kf, r_qp[c]]
                    if mt >= 2:
                        if glob:
                            g.append((pb[:, lo:hi], gm[0:5, 0, (mt - 2) * 128:(mt - 1) * 128], gm[0:5, 1, qs], False, True))
                            rd.append(r_gm)
                        else:
                            j = mt - 2
                            m0_ = c * 512 + lo - (j - 1) * 128
                            g.append((pb[:, lo:hi], ident_bf[:], smask[:, j, m0_:m0_ + (hi - lo)], False, True))
                            rd += [r_sm, r_k]
                    mm(S, g, rd, [pr])
                    return pb, pr

                nt = len(tiles)
                look = 2
                for idx in range(min(look, nt)):
                    pend.append(qk(idx))
                for idx in range(nt):
                    mt, lo, hi = tiles[idx]
                    pb, pr = pend.pop(0)
                    if idx + look < nt:
                        pend.append(qk(idx + look))
                    pi = pt_rr % 3
                    pt_rr += 1
                    act(S, PT[:, pi, lo:hi], pb[:, lo:hi], AF.Exp, [pr, r_k], [r_PT[pi]],
                        bias=(ctxb if mt < 2 else 0.0), scale=0.125)
                    g = [(pn[:, lo:hi], vpad[:, mt, kv, vw], PT[:, pi, lo:hi], idx == 0, idx == nt - 1),
                         (pd[:, lo:hi], ones_bf[:], PT[:, pi, lo:hi], idx == 0, idx == nt - 1)]
                    mm(S, g, [r_vp[mt], r_PT[pi], r_ones], [prn, prd])
                ri = (h * 2 + c) % 2
                if glob:
                    recip(S, rden[rows, ri, :], pd[rows, :], [prd], [r_rden[ri]])
                else:
                    ts(S, "dve", rden[rows, ri, :], pd[rows, :], es[rows, h:h + 1], ALU.add, [prd, r_es], [r_rden[ri]])
                    recip(S, rden[rows, ri, :], rden[rows, ri, :], [r_rden[ri]], [r_rden[ri]])
                tt(S, "dve", ymix[rows, ymb + kv, CH(c)], pn[rows, :], rden[rows, ri, :], ALU.mult, [prn, r_rden[ri]],
                   [ymr[ymb + kv][c]])
        self.bank_pool = list(range(8))

    def mix_hyena(self, l, ymix, ymr, win):
        S, k_, d = self.S, self.k, self.d
        C = self.ctx
        hr, CH = C["hr"], C["CH"]
        LP = self.LP
        lp = k_["lp"]
        r_k = self.r_k
        hall = [hr[k][c] for k in range(8) for c in range(2)]
        sv8 = lambda sl_: self.slot_view(sl_, 8)
        TWO_PI = float(2 * math.pi)
        xoff = self.aoff
        feats = self.carve([128, T], F32)
        z1 = self.carve([128, T], F32)
        z2 = self.carve([128, T], F32)
        wnd = self.carve([128, 8, 256], F32)
        rr = self.carve([128, 512], F32)
        ii = self.carve([128, 512], I32)
        kf = self.carve([128, 512], F32)
        xend = self.aoff
        self.aoff = xoff
        raw = self.carve([128, 3, T], F32)
        uct = self.carve([128, 2, T], F32)
        pa = self.carve([128, 2, 256], F32)
        pq = self.carve([128, 2, 256], F32)
        yt = self.carve([128, 2, 256], F32)
        assert self.aoff <= xend
        self.aoff = xend
        r_X = R("X")
        x0 = self.carve([128, 2, T], F32)
        zf = self.carve([128, 2, T], F32)
        zbf = self.carve([128, 2, T], BF16)
        zh = self.carve([128, 8, 512], BF16)
        ZH = self.carve([128, 2, 512], F32)
        Y = self.carve([128, 8, 2, 256], BF16)
        pa2 = self.carve([128, 2, 256], F32)
        yt2 = ZH[:, :, 0:256]
        r_x0, r_zf, r_zbf = RL(2, "x0"), RL(2, "zf"), RL(2, "zbf")
        r_zh = RL(8, "zh")
        r_ZH, r_Y, r_pa, r_pq, r_yt = R("ZH"), RL(8, "Y"), R("pa"), R("pq"), RL(2, "yt")
        S.dma("sp", feats[0:33, :], d["featsT"], writes=[r_X])
        S.dma("sp", wnd, d["window"], writes=[r_X])
        fs = k_["fs"]

        def sin_layer(pb, pr, dst, bcol):
            ts(S, "dve", rr[0:64, :], pb[0:64, :], fs[:, l, 0:1], ALU.mult, [pr, r_k], [r_X], s2=fs[:, l, bcol:bcol + 1],
               op1=ALU.add)
            cp(S, "dve", ii[0:64, :], rr[0:64, :], [r_X], [r_X])
            cp(S, "dve", kf[0:64, :], ii[0:64, :], [r_X], [r_X])
            tt(S, "dve", rr[0:64, :], rr[0:64, :], kf[0:64, :], ALU.subtract, [r_X], [r_X])
            act(S, dst, rr[0:64, :], AF.Sin, [r_X], [r_X], scale=TWO_PI)

        for c in range(2):
            pb, pr = self.pbank()
            mm(S, [(pb[0:64, :], k_["fw1"][0:33, l, :], feats[0:33, CH(c)], True, True)], [r_X, r_k], [pr])
            sin_layer(pb, pr, z1[0:64, CH(c)], 1)
        for c in range(2):
            pb, pr = self.pbank()
            mm(S, [(pb[0:64, :], k_["fw2"][0:64, l, :], z1[0:64, CH(c)], True, True)], [r_X, r_k], [pr])
            sin_layer(pb, pr, z2[0:64, CH(c)], 2)
        for t_ in range(8):
            pb, pr = self.pbank()
            tok = slice(t_ * 128, (t_ + 1) * 128)
            mm(S, [(pb[:, 0:256], z2[0:64, tok], k_["fw3"][0:64, l, :], True, False),
                   (pb[:, 0:256], k_["ones_f"][0:1, :], k_["fb3"][0:1, l, :], False, True)], [r_X, r_k], [pr])
            tt(S, "dve", zh[:, t_, 256:512], pb[:, 0:256], wnd[:, t_, :], ALU.mult, [pr, r_X], [r_zh[t_]])
        slD1, srD1 = self.wload(self.parts_win(l, 2064, 512), key=("D1", l))
        slD2, srD2 = self.wload(self.parts_win(l, 2576, 256), key=("D2", l))
        sD1, sD2 = sv8(slD1), sv8(slD2)
        Fd, Gd = d["dftF"], d["dftG"]
        fv = lambda m: Fd[m].rearrange("(k p) n -> p k n", p=128)
        gv = lambda m: Gd[m].rearrange("(k p) n -> p k n", p=128)

        def parts_dft(vw, q):
            return [(lambda sl_: sv8(sl_)[:, :, 0:256], vw(0)[:, :, q * 256:(q + 1) * 256]),
                    (lambda sl_: sv8(sl_)[:, :, 256:512], vw(1)[:, :, q * 256:(q + 1) * 256])]

        def zip_run(gens):
            gens = list(gens)
            while gens:
                for g__ in list(gens):
                    try:
                        next(g__)
                    except StopIteration:
                        gens.remove(g__)

        for q in range(2):
            self.prefetch(("F", l, q), parts_dft(fv, q))
        CV = LP["CV"]
        cvb = k_["cvb"]
        r_raw = RL(3, "hraw")
        r_uct = RL(2, "uct")

        def conv_chain(ct, ui):
            s3, col, srr = ((sD1, ct * 128, srD1), (sD1, 256 + ct * 128, srD1), (sD2, ct * 128, srD2))[ui]
            rr_ = r_raw[ui]
            fx = [r_X] if ct == 0 else []
            for c in range(2):
                pb, pr = self.pbank()
                mm(S, self.proj_fm(c, s3, col, pb), [srr] + hall, [pr])
                yield
                act(S, raw[:, ui, CH(c)], pb[:], AF.Copy, [pr], [rr_] + fx)
                yield
            tile = ui * 2 + ct
            cw = lambda j: lp[:, l, CV + tile * 4 + j:CV + tile * 4 + j + 1]
            u = raw[:, ui, :]
            if ui == 0:
                dst, wr = x0[:, ct, :], [r_x0[ct]]
            else:
                dst, wr = uct[:, ui - 1, :], [r_uct[ui - 1]] + fx
            rd = [rr_, r_k]
            act(S, dst, u, AF.Identity, rd, wr, bias=cw(3), scale=cw(1))
            yield
            for (o, a_, sc) in ((dst[:, 1:T], u[:, 0:T - 1], cw(0)), (dst[:, 0:T - 1], u[:, 1:T], cw(2)),
                                (dst[:, 256:T:256], u[:, 255:T - 1:256], cvb[:, l, tile, 0:1]),
                                (dst[:, 255:T - 1:256], u[:, 256:T:256], cvb[:, l, tile, 1:2])):
                S.op("dve", lambda e, o=o, a_=a_, sc=sc: e.scalar_tensor_tensor(out=o, in0=a_, scalar=sc, in1=o,
                                                                              op0=ALU.mult, op1=ALU.add), rd + wr[:1], wr[:1])
                yield

        for ct in range(2):
            zip_run([conv_chain(ct, ui) for ui in range(3)])
            tt(S, "dve", zf[:, ct, :], uct[:, 0, :], uct[:, 1, :], ALU.mult, r_uct, [r_zf[ct]])
            act(S, zbf[:, ct, :], zf[:, ct, :], AF.Copy, [r_zf[ct]], [r_zbf[ct]])
        for q in range(2, 4):
            self.prefetch(("F", l, q), parts_dft(fv, q))
        ident_bf = k_["ident_bf"]
        for t_ in range(8):
            pb, pr = self.pbank()
            pv = pb[:].bitcast(BF16)
            for ct in range(2):
                S.op("pe", lambda e, ct=ct, pv=pv, t_=t_: e.transpose(pv[:, ct * 128:(ct + 1) * 128],
                                                                      zbf[:, ct, t_ * 128:(t_ + 1) * 128], ident_bf[:]),
                     [r_zbf[ct], r_k], [pr])
            cp(S, "dve", zh[:, t_, 0:256], pv[:, 0:256], [pr], [r_zh[t_]])
        ZHs = [ZH, zbf.rearrange("p a b -> p (a b)").bitcast(F32).rearrange("p (a b) -> p a b", a=2)]
        r_ZHs = [r_ZH, R("ZHb")]
        PAs, PQs = [pa, pa2], [pq, yt]
        r_PA, r_PQ = RL(2, "PA"), RL(2, "PQ")
        seen = set()

        def ft_chain(ft, fj, s3, sr):
            bi = ft % 2
            Zb, rz = ZHs[bi], r_ZHs[bi]
            PA, PQ, rpa, rpq = PAs[bi], PQs[bi], r_PA[bi], r_PQ[bi]
            fresh = bi not in seen
            seen.add(bi)
            fz = (r_zbf if (fresh and bi == 1) else [])
            fx = ([r_X] if fresh else [])
            pre_, prr = self.pbank()
            pim, pri = self.pbank()
            g = [(pre_[:], s3[:, t_, fj * 128:(fj + 1) * 128], zh[:, t_, :], t_ == 0, t_ == 7) for t_ in range(8)]
            g += [(pim[:], s3[:, t_, 256 + fj * 128:256 + (fj + 1) * 128], zh[:, t_, :], t_ == 0, t_ == 7) for t_ in range(8)]
            mm(S, g, [sr] + r_zh, [prr, pri])
            yield
            act(S, Zb[:, 0, :], pre_[:], AF.Copy, [prr], [rz] + fz)
            yield
            act(S, Zb[:, 1, :], pim[:], AF.Copy, [pri], [rz])
            yield
            Zr, Hr, Zi, Hi = Zb[:, 0, 0:256], Zb[:, 0, 256:512], Zb[:, 1, 0:256], Zb[:, 1, 256:512]
            tt(S, "dve", PA[:, 0, :], Zr, Hr, ALU.mult, [rz], [rpa] + fx)
            yield
            tt(S, "pool", PQ[:, 0, :], Zr, Hi, ALU.mult, [rz], [rpq] + fx)
            yield
            tt(S, "dve", PA[:, 1, :], Zi, Hi, ALU.mult, [rz], [rpa])
            yield
            tt(S, "pool", PQ[:, 1, :], Zi, Hr, ALU.mult, [rz], [rpq])
            yield
            tt(S, "dve", Y[:, ft, 0, :], PA[:, 0, :], PA[:, 1, :], ALU.subtract, [rpa], [r_Y[ft]])
            yield
            tt(S, "pool", Y[:, ft, 1, :], PQ[:, 0, :], PQ[:, 1, :], ALU.add, [rpq], [r_Y[ft]])
            yield

        for q in range(4):
            sl, sr = self.wload(parts_dft(fv, q), key=("F", l, q))
            s3 = sv8(sl)
            self.mod_pump(2)
            zip_run([ft_chain(q * 2 + fj, fj, s3, sr) for fj in range(2)])
        HB = LP["HB"]
        for q in range(2):
            self.prefetch(("G", l, q), parts_dft(gv, q))
        for q in range(4):
            sl, sr = self.wload(parts_dft(gv, q), key=("G", l, q))
            if q + 2 < 4:
                self.prefetch(("G", l, q + 2), parts_dft(gv, q + 2))
            s3 = sv8(sl)
            ns = slice(q * 256, (q + 1) * 256)
            self.mod_pump(2)
            for ct in range(2):
                pb, pr = self.pbank()
                g = []
                for ft in range(8):
                    g.append((pb[:, 0:256], Y[:, ft, 0, ct * 128:(ct + 1) * 128], s3[:, ft, 0:256], ft == 0, False))
                    g.append((pb[:, 0:256], Y[:, ft, 1, ct * 128:(ct + 1) * 128], s3[:, ft, 256:512], False, ft == 7))
                mm(S, g, [sr] + r_Y, [pr])
                ytb, ryt = (yt2[:, ct, :], r_yt[ct])
                act(S, ytb, pb[:, 0:256], AF.Copy, [pr], [ryt] + ([r_PA[1], r_ZHs[0]] if q == 0 else []))
                S.op("dve", lambda e, ytb=ytb, ct=ct: e.scalar_tensor_tensor(out=ytb, in0=zf[:, ct, ns],
                                                                           scalar=lp[:, l, HB + ct:HB + ct + 1], in1=ytb,
                                                                           op0=ALU.mult, op1=ALU.add),
                     [r_zf[ct], ryt, r_k], [ryt])
                tt(S, "dve", ymix[:, 6 + ct, ns], x0[:, ct, ns], ytb, ALU.mult, [r_x0[ct], ryt], [ymr[6 + ct][q // 2]])


def fm(v):
    return np.ascontiguousarray(np.asarray(v, np.float32).reshape(8, 128).T)


def _consts(kind):
    c = {}
    p = np.arange(128)
    t = np.arange(T)
    ident = np.eye(128, dtype=np.float32)
    c["ident"] = ident
    bo = np.zeros((128, 128), np.float32)
    bo[:64, :64] = 1
    bo[64:, 64:] = 1
    c["blockones"] = bo
    r_, t_ = np.meshgrid(p, p, indexing="ij")
    tri = np.zeros((128, 4, 128), np.float32)
    tri[:, 0, :] = (r_ <= t_)
    tri[:, 1, :] = (r_ >= t_)
    tri[:, 2, :] = np.where(r_ <= t_, 0.0, NEG)
    tri[:, 3, :] = np.where(r_ >= t_, 0.0, NEG)
    c["tri"] = tri
    P = np.zeros((128, 128), np.float32)
    for b in range(0, 128, 32):
        for i in range(16):
            P[b + i + 16, b + i] = -1.0
            P[b + i, b + i + 16] = 1.0
    c["ropeP"] = P
    cs = np.zeros((128, 2, T), np.float32)
    if kind == "s":
        dd = p % 64
        inv = (10000.0 ** (-(dd % 16).astype(np.float32) / np.float32(16))).astype(np.float32)
        row = (t // 64).astype(np.float32)
        col = (t % 64).astype(np.float32)
        pos = np.where((dd // 32)[:, None] == 0, row[None, :], col[None, :]).astype(np.float32)
        ang = (pos * inv[:, None]).astype(np.float32)
        cs[:, 0, :] = np.cos(ang)
        cs[:, 1, :] = np.sin(ang)
    else:
        cs[:, 0, :] = 1.0
    c["ropeCS"] = cs
    sm = np.full((128, 8, 384), NEG, np.float32)
    for j in range(8):
        m = j * 128 + p[:, None]
        q = (j - 1) * 128 + np.arange(384)[None, :]
        inr = (q >= 0) & (q < T)
        if kind == "s":
            ok = (np.abs(m - q) <= 128) & inr
        else:
            ok = ((m // 256) == (q // 256)) & inr
        sm[:, j, :] = np.where(ok, 0.0, NEG)
    c["swamask"] = sm
    gm = np.zeros((5, 2, T), np.float32)
    if kind == "p":
        gm[0, 0, :] = 1.0
        gm[0, 1, :] = -BIG
        for s_ in range(4):
            gm[1 + s_, 0, :] = (t // 256 == s_)
            gm[1 + s_, 1, :] = BIG * (t // 256 == s_)
    c["gmAB"] = gm
    c["cflags"] = np.tile(np.array([[0.0, 1.0, 0.0, 0.0]] if kind == "s" else [[NEG, 0.0, 1.0, 0.0]], np.float32), (128, 1))
    sel = np.zeros((4, 130), np.float32)
    for h in range(4):
        sel[h, 0:128] = ((p >= 64).astype(int) == (h % 2))
        sel[h, 128 + h // 2] = 1.0
    c["sel"] = sel
    L = T if kind == "s" else 256
    rep = T // L
    pos = np.arange(L, dtype=np.float32)
    t01 = pos / np.float32(max(L - 1, 1))
    lin = np.linspace(1e-4, 15.0, 16, dtype=np.float32)
    ang = (np.float32(2.0 * math.pi / L) * pos[:, None] * lin[None, :]).astype(np.float32)
    feats = np.concatenate([t01[:, None], np.cos(ang), -np.sin(ang)], -1).astype(np.float32)
    c["featsT"] = np.ascontiguousarray(np.tile(feats, (rep, 1)).T)
    centre = L // 2
    dist = np.abs(pos - centre) / np.float32(max(centre, 1))
    deltas = np.abs(np.linspace(math.log(0.01) / 1.5, math.log(0.01) / 0.3, 256, dtype=np.float32))
    wnd = np.exp(-dist[:, None] * deltas[None, :]).astype(np.float32)
    c["window"] = np.ascontiguousarray(np.tile(wnd, (rep, 1)).reshape(8, 128, 256).transpose(1, 0, 2))
    N = 2 * L
    tt_ = np.arange(L, dtype=np.float64)
    ff = np.arange(L, dtype=np.float64)
    th = math.pi * (2 * ff + 1) / N
    Fc = np.cos(tt_[:, None] * th[None, :])
    Fs = -np.sin(tt_[:, None] * th[None, :])
    Gc = (2.0 / N) * np.cos(th[:, None] * (tt_[None, :] + L // 2))
    Gs = -(2.0 / N) * np.sin(th[:, None] * (tt_[None, :] + L // 2))
    dF = np.zeros((2, T, T), np.float32)
    dG = np.zeros((2, T, T), np.float32)
    for s_ in range(rep):
        sl = slice(s_ * L, (s_ + 1) * L)
        dF[0, sl, sl] = Fc
        dF[1, sl, sl] = Fs
        dG[0, sl, sl] = Gc
        dG[1, sl, sl] = Gs
    c["dftF"] = dF
    c["dftG"] = dG
    return c


def host_inputs(inp, cores=None):
    f = lambda a: np.ascontiguousarray(np.asarray(a, dtype=np.float32))
    A = {k: np.asarray(v) for k, v in inp.items()}
    shared = {}
    for nm in ("w_ada", "w1_gate", "w1_up", "w1_down", "w_in", "w_out", "w2_gate", "w2_up", "w2_down"):
        shared[nm] = f(A[nm])
    shared["b_adaT"] = f(A["b_ada"].reshape(NL, 72, 128).transpose(0, 2, 1))
    gv = []
    for l in range(NL):
        gv += [fm(A["g_ff1"][l]), fm(A["g_mix"][l]), fm(A["g_ff2"][l])]
    gv.append(fm(A["g_final"]))
    shared["gvec"] = f(np.concatenate(gv, axis=1))
    lp = np.zeros((128, NL, 304), np.float32)
    p = np.arange(128)
    for l in range(NL):
        lp[:, l, 0:16] = A["b_gates"][l][None, :]
        lp[:, l, 16:272] = A["g_mlstm"][l][None, :]
        lp[:, l, 272] = A["g_qnorm"][l][p % 64]
        lp[:, l, 273] = A["g_knorm"][l][p % 64]
        lp[:, l, 274:278] = A["sinks"][l][None, :]
        for i in range(6):
            ch = i * 128 + p
            lp[:, l, 278 + i * 4 + 0] = A["conv_w"][l][0, ch]
            lp[:, l, 278 + i * 4 + 1] = A["conv_w"][l][1, ch]
            lp[:, l, 278 + i * 4 + 2] = A["conv_w"][l][2, ch]
            lp[:, l, 278 + i * 4 + 3] = A["conv_b"][l][ch]
        for ct in range(2):
            lp[:, l, 302 + ct] = A["hyena_bias"][l][ct * 128 + p]
    shared["lp"] = lp
    shared["fw1"] = f(A["filt_w1"].transpose(1, 0, 2))
    shared["fw2"] = f(A["filt_w2"].transpose(1, 0, 2))
    shared["fw3"] = f(A["filt_w3"].transpose(1, 0, 2))
    shared["fvec"] = f(np.stack([A["filt_b1"], A["filt_b2"], A["filt_freq"]], -1).transpose(1, 0, 2))
    shared["fb3"] = f(A["filt_b3"][None, :, :])
    cst = {"s": _consts("s"), "p": _consts("p")}
    maps = []
    xs, xp = A["x_sample"], A["x_prompt"]
    for core in (range(8) if cores is None else cores):
        m = dict(shared)
        kind = "s" if core < 4 else "p"
        m.update(cst[kind])
        C0 = np.zeros((NL, 2, 128, 2, 65), np.float32)
        m0rep = np.zeros((128, NL, 2, 2), np.float32)
        m0c = np.zeros((4, NL * 2), np.float32)
        kT = {n: np.zeros((NL, 2, 128, 256), np.float32) for n in ("gkT", "skT")}
        vp = {n: np.zeros((NL, 2, 128, 2, 192), np.float32) for n in ("gvp", "svp")}
        if core < 4:
            b = core
            m["xT"] = f(xs[b].T)
            m["cvec"] = fm(A["c"][b])
            sC, sn, smm = A["state_mlstm_C"][b], A["state_mlstm_n"][b], A["state_mlstm_m"][b]
            for g_ in range(2):
                for pr_ in range(2):
                    h = 2 * pr_ + g_
                    C0[:, :, g_ * 64:(g_ + 1) * 64, pr_, 0:64] = sC[:, :, h]
                    C0[:, :, g_ * 64:(g_ + 1) * 64, pr_, 64] = sn[:, :, h]
                    m0rep[g_ * 64:(g_ + 1) * 64, :, :, pr_] = smm[None, :, :, h]
            for l in range(NL):
                for dr in range(2):
                    m0c[:, l * 2 + dr] = smm[l, dr, :]
            for (kn, vn, ck_, cv_) in (("gkT", "gvp", "cache_gattn_k", "cache_gattn_v"), ("skT", "svp", "cache_swa_k", "cache_swa_v")):
                ck, cv = A[ck_][b], A[cv_][b]
                t1 = ck.transpose(0, 2, 3, 1).reshape(NL, 128, 256)
                t2 = ck[:, :, ::-1, :].transpose(0, 2, 3, 1).reshape(NL, 128, 256)
                kT[kn][:, 0] = t1
                kT[kn][:, 1] = t2
                vp[vn][:, :, :, :, 64:128] = cv.reshape(NL, 2, 128, 2, 64)
        else:
            j = core - 4
            m["xT"] = f(xp[4 * j:4 * j + 4].reshape(T, D).T)
            m["cvec"] = fm(A["c_ctx"])
        m["C0"], m["m0rep"], m["m0c"] = C0, m0rep, m0c
        m.update(kT)
        m.update(vp)
        maps.append(m)
    return maps


_PROG = None


def get_prog():
    global _PROG
    if _PROG is None:
        _PROG = Prog()
    return _PROG


def run_device(inputs, trace=False, cores=None):
    prog = get_prog()
    maps = host_inputs(inputs, cores)
    maps = [{k: np.ascontiguousarray(v, dtype=np.float32) for k, v in m.items() if k in prog.din} for m in maps]
    for m in maps:
        for k, shp in prog.din.items():
            assert m[k].shape == shp, (k, m[k].shape, shp)
    res = run_bass_kernel_spmd(prog.nc, maps, core_ids=list(range(len(maps))), trace=trace)
    return res


def assemble(res):
    r = res.results
    yp = np.zeros((16, 256, D), np.float32)
    ys = np.zeros((4, T, D), np.float32)
    nC = np.zeros((16, NL, 2, 4, 64, 64), np.float32)
    nn = np.zeros((16, NL, 2, 4, 64), np.float32)
    nm = np.zeros((16, NL, 2, 4), np.float32)
    kv = {n: np.zeros((16, NL, 256, 2, 64), np.float32) for n in ("o_gk", "o_gv", "o_sk", "o_sv")}
    for core in range(8):
        y = np.ascontiguousarray(r[core]["yT"].T)
        if core < 4:
            ys[core] = y
            continue
        j = core - 4
        yp[4 * j:4 * j + 4] = y.reshape(4, 256, D)
        for n in ("o_gk", "o_sk"):
            a = r[core][n].reshape(NL, 2, 64, 4, 256)
            kv[n][4 * j:4 * j + 4] = a.transpose(3, 0, 4, 1, 2)
        for n in ("o_gv", "o_sv"):
            a = r[core][n].reshape(NL, 4, 256, 2, 64)
            kv[n][4 * j:4 * j + 4] = a.transpose(1, 0, 2, 3, 4)
        oc = r[core]["o_C"].reshape(NL, 2, 4, 2, 64, 2, 65)
        oc = oc.transpose(2, 0, 1, 5, 3, 4, 6).reshape(4, NL, 2, 4, 64, 65)
        nC[4 * j:4 * j + 4] = oc[..., 0:64]
        nn[4 * j:4 * j + 4] = oc[..., 64]
        nm[4 * j:4 * j + 4] = r[core]["o_m"].transpose(3, 0, 1, 2)
    return (yp, ys, nC, nn, nm, kv["o_gk"], kv["o_gv"], kv["o_sk"], kv["o_sv"])


def kernel(**inputs):
    res = run_device(inputs)
    return assemble(res)
```

```python
import os
import math
import numpy as np
import concourse.bass as bass
import concourse.mybir as mybir
from concourse.bass_utils import run_bass_kernel_spmd

F32 = mybir.dt.float32
BF16 = mybir.dt.bfloat16
I32 = mybir.dt.int32
AF = mybir.ActivationFunctionType
ALU = mybir.AluOpType
AX = mybir.AxisListType

D = 1024
T = 1024
DFF = 2816
NFF = 22
NL = 2
NIN = 2832
EPS = 1e-6
NEG = -30000.0
BIG = 29952.0
LN8 = math.log(0.125)
SLOT = 8 * 640
STAGE = int(os.environ.get("MK_STAGE", "99"))
DEBUG = bool(os.environ.get("MK_DEBUG"))
CUT = int(os.environ.get("MK_CUT", "99"))


class R:
    __slots__ = ("name", "w", "rd", "excl")

    def __init__(self, name="", excl=False):
        self.name = name
        self.w = None
        self.rd = []
        self.excl = excl


def RL(n, name=""):
    return [R("%s%d" % (name, i)) for i in range(n)]


class Sched:
    NLANES = {"sp": 8, "pool": 3, "poolw": 5}

    def __init__(self, nc):
        self.nc = nc
        self.eng = {"pe": nc.tensor, "act": nc.scalar, "dve": nc.vector,
                    "pool": nc.gpsimd, "sp": nc.sync}
        self.sem = {}
        self.cnt = {}
        for k in self.eng:
            self.sem[k] = nc.alloc_semaphore(name="s_" + k)
            self.cnt[k] = 0
        self.lanes = {}
        for q, n in self.NLANES.items():
            self.lanes[q] = []
            for i in range(n):
                key = "d_%s%d" % (q, i)
                self.sem[key] = nc.alloc_semaphore(name=key)
                self.cnt[key] = 0
                self.lanes[q].append(key)
        self.lane_rr = {q: 0 for q in self.NLANES}
        self.eng_of = {"sp": "sp", "pool": "pool", "poolw": "pool"}
        self.waited = {k: {} for k in self.eng}
        self.nwaits = 0
        self.nops = 0

    def _wait(self, e, key, val):
        if key == "pe" and e == "pe":
            return
        w = self.waited[e]
        if w.get(key, 0) >= val:
            return
        self.eng[e].wait_ge(self.sem[key], val)
        w[key] = val
        self.nwaits += 1

    def _deps(self, e, reads, writes):
        deps = {}
        for r in reads:
            if r.w is not None:
                k, v = r.w
                if deps.get(k, 0) < v:
                    deps[k] = v
            if r.excl:
                for (k, v) in r.rd:
                    if k != e and deps.get(k, 0) < v:
                        deps[k] = v
        for w in writes:
            if w.w is not None:
                k, v = w.w
                if deps.get(k, 0) < v:
                    deps[k] = v
            for (k, v) in w.rd:
                if deps.get(k, 0) < v:
                    deps[k] = v
        for k, v in deps.items():
            self._wait(e, k, v)

    def _commit(self, tok, reads, writes):
        for r in reads:
            r.rd.append(tok)
            if len(r.rd) > 48:
                mx = {}
                for k, v in r.rd:
                    if mx.get(k, 0) < v:
                        mx[k] = v
                r.rd = list(mx.items())
        for w in writes:
            w.w = tok
            w.rd = []

    def op(self, e, fn, reads=(), writes=()):
        self._deps(e, reads, writes)
        inst = fn(self.eng[e])
        self.cnt[e] += 1
        inst.then_inc(self.sem[e], 1)
        self._commit((e, self.cnt[e]), reads, writes)
        self.nops += 1

    def dma(self, q, out, in_, reads=(), writes=()):
        lanes = self.lanes[q]
        e = self.eng_of[q]
        key = lanes[self.lane_rr[q] % len(lanes)]
        self.lane_rr[q] += 1
        self._wait(e, key, self.cnt[key])
        self._deps(e, reads, writes)
        inst = self.eng[e].dma_start(out=out, in_=in_)
        self.cnt[key] += 16
        inst.then_inc(self.sem[key], 16)
        self._commit((key, self.cnt[key]), reads, writes)
        self.nops += 1

    def barrier(self):
        for e in self.eng:
            for k in self.sem:
                if self.cnt[k] > 0 and not k.startswith("d_poolw"):
                    self._wait(e, k, self.cnt[k])

    def finish(self):
        for k in self.sem:
            if self.cnt[k] > 0 and k != "sp":
                self._wait("sp", k, self.cnt[k])


def act(S, out, in_, func, reads, writes, bias=0.0, scale=1.0):
    S.op("act", lambda e: e.activation(out=out, in_=in_, func=func, bias=bias, scale=scale), reads, writes)


def tt(S, eng, out, a, b, op, reads, writes):
    S.op(eng, lambda e: e.tensor_tensor(out=out, in0=a, in1=b, op=op), reads, writes)


def ts(S, eng, out, a, s1, op0, reads, writes, s2=None, op1=None):
    if op1 is None:
        S.op(eng, lambda e: e.tensor_scalar(out=out, in0=a, scalar1=s1, scalar2=None, op0=op0), reads, writes)
    else:
        S.op(eng, lambda e: e.tensor_scalar(out=out, in0=a, scalar1=s1, scalar2=s2, op0=op0, op1=op1), reads, writes)


def cp(S, eng, out, in_, reads, writes):
    S.op(eng, lambda e: e.tensor_copy(out=out, in_=in_), reads, writes)


def recip(S, out, in_, reads, writes):
    S.op("dve", lambda e: e.reciprocal(out=out, in_=in_), reads, writes)


def mm(S, groups, reads, writes):
    def fn(e):
        inst = None
        for (o, l, r, st, sp_) in groups:
            inst = e.matmul(o, lhsT=l, rhs=r, start=st, stop=sp_)
        return inst
    S.op("pe", fn, reads, writes)


class Prog:
    def __init__(self):
        nc = bass.Bass("TRN2", target_bir_lowering=False)
        self.nc = nc
        self.S = Sched(nc)
        self.din = {}
        self.dout = {}
        self._build()

    def inp(self, name, shape):
        t = self.nc.dram_tensor(name, list(shape), F32, kind="ExternalInput").ap()
        self.din[name] = tuple(shape)
        return t

    def outp(self, name, shape):
        t = self.nc.dram_tensor(name, list(shape), F32, kind="ExternalOutput").ap()
        self.dout[name] = tuple(shape)
        return t

    def sb(self, name, shape, dt=F32):
        return self.nc.alloc_sbuf_tensor("sb_" + name, list(shape), dt)

    def arena_reset(self, to=0):
        self.aoff = to
        self.S.barrier()

    def carve(self, shape, dt=F32):
        n = 1
        for s in shape[1:]:
            n *= s
        nb = n * (4 if dt in (F32, I32) else 2)
        nb = (nb + 31) // 32 * 32
        off = self.aoff
        self.aoff += nb
        assert self.aoff <= self.ARENA_BYTES, (self.aoff, self.ARENA_BYTES)
        v = self.arena[:, off // 2:(off + nb) // 2]
        if dt != BF16:
            v = v.bitcast(dt)
        v = v[:, 0:n]
        if len(shape) == 3:
            v = v.rearrange("p (a b) -> p a b", a=shape[1])
        elif len(shape) == 4:
            v = v.rearrange("p (a b c) -> p a b c", a=shape[1], b=shape[2])
        return v

    def bank(self):
        return self.pbank()

    def prefetch(self, key, parts):
        self.pre[key] = self.wload(parts)

    def wload(self, parts, key=None):
        if key is not None and key in self.pre:
            return self.pre.pop(key)
        i = self.slot_rr % len(self.slots)
        self.slot_rr += 1
        sl, r = self.slots[i], self.slotr[i]
        for (dst_fn, src) in parts:
            self.S.dma("poolw", dst_fn(sl), src, writes=[r])
        return sl, r

    def slot_view(self, sl, kk):
        return sl[:, 0:kk * self.slot_w].rearrange("p (k n) -> p k n", k=kk)

    def _build(self):
        nc, S = self.nc, self.S
        inp, outp, sb = self.inp, self.outp, self.sb
        xT_d = inp("xT", [D, T])
        cvec_d = inp("cvec", [128, 8])
        w_ada = inp("w_ada", [NL, D, 9 * D])
        b_adaT = inp("b_adaT", [NL, 128, 72])
        gvec_d = inp("gvec", [128, NL * 24 + 8])
        wd = {}
        for nm, shp in (("w1_gate", [NL, D, DFF]), ("w1_up", [NL, D, DFF]), ("w1_down", [NL, DFF, D]),
                        ("w_in", [NL, D, NIN]), ("w_out", [NL, D, D]),
                        ("w2_gate", [NL, D, DFF]), ("w2_up", [NL, D, DFF]), ("w2_down", [NL, DFF, D])):
            wd[nm] = inp(nm, shp)
        yT_d = outp("yT", [D, T])

        xT = sb("xT", [128, 8, T], F32)
        hT = sb("hT", [128, 8, T], BF16)
        self.ARENA_BYTES = 84 * 1024
        self.arena = sb("arena", [128, self.ARENA_BYTES // 2], BF16)
        self.slot_w = 640
        self.slots = [sb("slot%d" % i, [128, SLOT], BF16) for i in range(4)]
        self.slotr = RL(4, "slot")
        self.slot_rr = 0
        self.pre = {}
        self.ps = [nc.alloc_psum_tensor("ps%d" % i, [128, 512], F32) for i in range(8)]
        self.psr = [R("ps%d" % i, excl=True) for i in range(8)]
        self.bank_rr = 0
        self.bank_pool = list(range(8))
        ones_bf = sb("ones_bf", [128, 128], BF16)
        cvec = sb("cvec_sb", [128, 8], F32)
        sc_bf = sb("sc_bf", [128, 8], BF16)
        modT = sb("modT", [128, NL, 72], F32)
        badaT = sb("badaT", [128, NL, 72], F32)
        gvec = sb("gvec_sb", [128, NL * 24 + 8], F32)
        Acoef = sb("Acoef", [128, NL, 3, 8], F32)
        Gcoef = sb("Gcoef", [128, NL, 3, 8], F32)
        sqb = sb("sqb", [128, 2, 512], BF16)
        f32s = sb("f32s", [128, 3, 512], F32)
        rstd = sb("rstd", [128, 512], F32)
        r_ones, r_cvec, r_sc, r_gvec, r_rstd = R("ones"), R("cvec"), R("sc"), R("gvec"), R("rstd")
        r_mod = RL(NL, "mod")
        r_bada = R("bada")
        r_coef = RL(NL, "coef")
        r_sq = RL(2, "sq")
        r_f32s = RL(3, "f32s")
        xr = [[R("x%d_%d" % (k, c)) for c in range(2)] for k in range(8)]
        hr = [[R("h%d_%d" % (k, c)) for c in range(2)] for k in range(8)]
        self.sq_rr = 0
        self.f32_rr = 0

        def CH(c):
            return slice(c * 512, (c + 1) * 512)

        xv = xT_d.rearrange("(k p) t -> p k t", p=128)
        for k in range(8):
            S.dma("sp", xT[:, k, :], xv[:, k, :], writes=[xr[k][0], xr[k][1]])
        S.dma("sp", cvec[:], cvec_d, writes=[r_cvec])
        S.dma("sp", gvec[:], gvec_d, writes=[r_gvec])
        for l in range(NL):
            S.dma("sp", badaT[:, l, :], b_adaT[l], writes=[r_bada])
        S.op("dve", lambda e: e.memset(ones_bf[:], 1.0), writes=[r_ones])
        act(S, sc_bf[:], cvec[:], AF.Silu, [r_cvec], [r_sc])

        mslots = [sb("mslot%d" % i, [128, 8, 256], BF16) for i in range(2)]
        r_ms = RL(2, "mslot")
        r_modls = [[R("mod%d_%d" % (l, s_)) for s_ in range(3)] for l in range(NL)]
        jobs = [(l, q) for l in range(NL) for q in range(36)]
        st = {"dma": 0, "mm": 0, "fin": set()}

        def mod_dma(j):
            l, q = jobs[j]
            wv = w_ada[l].rearrange("(k p) n -> p k n", p=128)
            S.dma("poolw", mslots[j % 2][:], wv[:, :, q * 256:(q + 1) * 256], writes=[r_ms[j % 2]])

        def mod_mm(j):
            l, q = jobs[j]
            pb, pr = self.bank()
            groups = []
            for jj in range(2):
                for k in range(8):
                    groups.append((pb[:, jj:jj + 1], mslots[j % 2][:, k, jj * 128:(jj + 1) * 128], sc_bf[:, k:k + 1],
                                   k == 0, k == 7))
            mm(S, groups, [r_ms[j % 2], r_sc], [pr])
            s_ = q // 12
            tt(S, "dve", modT[:, l, q * 2:q * 2 + 2], pb[:, 0:2], badaT[:, l, q * 2:q * 2 + 2], ALU.add,
               [pr, r_bada], [r_modls[l][s_]])

        def mod_pump(n=1):
            for _ in range(n):
                if st["dma"] < len(jobs) and st["dma"] - st["mm"] < 2:
                    mod_dma(st["dma"])
                    st["dma"] += 1
                if st["mm"] < st["dma"] - 1 or (st["dma"] == len(jobs) and st["mm"] < st["dma"]):
                    mod_mm(st["mm"])
                    st["mm"] += 1

        def mod_require(l, s_):
            last = l * 36 + s_ * 12 + 11
            while st["mm"] <= last:
                mod_pump()
            if (l, s_) in st["fin"]:
                return
            st["fin"].add((l, s_))
            r = r_modls[l][s_]
            ts(S, "dve", Acoef[:, l, s_, :], modT[:, l, (3 * s_ + 1) * 8:(3 * s_ + 2) * 8], 1.0, ALU.add, [r], [r])
            tt(S, "dve", Acoef[:, l, s_, :], Acoef[:, l, s_, :], gvec[:, l * 24 + s_ * 8:l * 24 + s_ * 8 + 8],
               ALU.mult, [r, r_gvec], [r])
            ts(S, "dve", Gcoef[:, l, s_, :], modT[:, l, (3 * s_ + 2) * 8:(3 * s_ + 3) * 8],
               0.5 if s_ != 1 else 1.0, ALU.mult, [r], [r])

        self.mod_pump = mod_pump
        self.mod_require = mod_require

        def rms_rstd(c, src_fn, src_regs, nk, inv_n, ones_l):
            pb, pr = self.bank()
            for k in range(nk):
                i = self.sq_rr % 2
                self.sq_rr += 1
                act(S, sqb[:, i, :], src_fn(k), AF.Square, [src_regs[k]], [r_sq[i]])
                mm(S, [(pb[:], ones_l, sqb[:, i, :], k == 0, k == nk - 1)], [r_sq[i], r_ones], [pr])
            act(S, rstd[:], pb[:], AF.Ln, [pr], [r_rstd], bias=EPS, scale=inv_n)
            act(S, rstd[:], rstd[:], AF.Exp, [r_rstd], [r_rstd], scale=-0.5)

        def norm_mod(l, s):
            for c in range(2):
                rms_rstd(c, lambda k: xT[:, k, CH(c)], [xr[k][c] for k in range(8)], 8, 1.0 / D, ones_bf[:])
                for k in range(8):
                    i = self.f32_rr % 3
                    self.f32_rr += 1
                    tt(S, "dve", f32s[:, i, :], xT[:, k, CH(c)], rstd[:], ALU.mult,
                       [xr[k][c], r_rstd], [r_f32s[i]])
                    act(S, hT[:, k, CH(c)], f32s[:, i, :], AF.Identity, [r_f32s[i], r_modls[l][s]],
                        [hr[k][c]], bias=modT[:, l, 3 * s * 8 + k:3 * s * 8 + k + 1],
                        scale=Acoef[:, l, s, k:k + 1])

        def resid_add(l, s, dt_, c, pb, pr):
            i = self.f32_rr % 3
            self.f32_rr += 1
            act(S, f32s[:, i, :], pb[:], AF.Copy, [pr, r_modls[l][s]], [r_f32s[i]], scale=Gcoef[:, l, s, dt_:dt_ + 1])
            tt(S, "dve", xT[:, dt_, CH(c)], xT[:, dt_, CH(c)], f32s[:, i, :], ALU.add,
               [xr[dt_][c], r_f32s[i]], [xr[dt_][c]])

        def ffn(l, s, wg, wu, wdn):
            self.arena_reset()
            aT = self.carve([128, NFF, T], BF16)
            ar = [[R("a%d_%d" % (j, c)) for c in range(2)] for j in range(NFF)]
            sg = [self.carve([128, 512], F32) for _ in range(3)]
            r_sg = RL(3, "sg")
            sg_rr = 0
            mod_require(l, s)
            norm_mod(l, s)
            wgv = wg[l].rearrange("(k p) n -> p k n", p=128)
            wuv = wu[l].rearrange("(k p) n -> p k n", p=128)
            for g in range(NFF // 2):
                c0 = g * 256
                sl, sr = self.wload(self.parts_gu(wg, wu, l, g), key=("gu", l, s, g))
                s3 = self.slot_view(sl, 8)
                if l == 0:
                    mod_pump(1)
                for jj in range(2):
                    j = g * 2 + jj
                    for c in range(2):
                        pg, prg = self.bank()
                        pu, pru = self.bank()
                        groups = []
                        for k in range(8):
                            groups.append((pg[:], s3[:, k, jj * 128:(jj + 1) * 128], hT[:, k, CH(c)], k == 0, k == 7))
                        for k in range(8):
                            groups.append((pu[:], s3[:, k, 256 + jj * 128:256 + (jj + 1) * 128], hT[:, k, CH(c)],
                                           k == 0, k == 7))
                        mm(S, groups, [sr] + [hr[k][c] for k in range(8)], [prg, pru])
                        i = sg_rr % 3
                        sg_rr += 1
                        act(S, sg[i], pg[:], AF.Silu, [prg], [r_sg[i]])
                        tt(S, "dve", aT[:, j, CH(c)], sg[i], pu[:], ALU.mult, [r_sg[i], pru], [ar[j][c]])
            wdv = wdn[l].rearrange("(j p) n -> p j n", p=128)
            for dt_ in range(8):
                sl, sr = self.wload([(lambda sl_: sl_[:, 0:NFF * 128].rearrange("p (j n) -> p j n", j=NFF),
                                      wdv[:, :, dt_ * 128:(dt_ + 1) * 128])])
                if dt_ == 7:
                    if s == 0:
                        self.prefetch(("A1", l), self.parts_win(l, 0, 512))
                        self.prefetch(("A2", l), self.parts_win(l, 512, 512))
                        self.prefetch(("G", l), self.parts_win(l, 1024, 16))
                    elif l + 1 < NL:
                        for g_ in range(2):
                            self.prefetch(("gu", l + 1, 0, g_), self.parts_gu(wd["w1_gate"], wd["w1_up"], l + 1, g_))
                s3 = sl[:, 0:NFF * 128].rearrange("p (j n) -> p j n", j=NFF)
                for c in range(2):
                    pb, pr = self.bank()
                    groups = [(pb[:], s3[:, j, :], aT[:, j, CH(c)], j == 0, j == NFF - 1) for j in range(NFF)]
                    mm(S, groups, [sr] + [ar[j][c] for j in range(NFF)], [pr])
                    resid_add(l, s, dt_, c, pb, pr)

        self.wd = wd
        self.ctx = dict(xT=xT, hT=hT, xr=xr, hr=hr, CH=CH, rms_rstd=rms_rstd, norm_mod=norm_mod,
                        resid_add=resid_add, ones_bf=ones_bf, r_ones=r_ones, gvec=gvec, r_gvec=r_gvec,
                        f32s=f32s, r_f32s=r_f32s, rstd=rstd, r_rstd=r_rstd, modT=modT, sqb=sqb, r_sq=r_sq)


        self.ARENA_BYTES = 84 * 1024
        LPW = 304
        self.LP = dict(BG=0, GM=16, GQK=272, SK=274, CV=278, HB=302)
        d = {}
        d["lp"] = inp("lp", [128, NL, LPW])
        d["cflags"] = inp("cflags", [128, 4])
        d["ident"] = inp("ident", [128, 128])
        d["blockones"] = inp("blockones", [128, 128])
        d["tri"] = inp("tri", [128, 4, 128])
        d["ropeP"] = inp("ropeP", [128, 128])
        d["ropeCS"] = inp("ropeCS", [128, 2, T])
        d["swamask"] = inp("swamask", [128, 8, 384])
        d["gmAB"] = inp("gmAB", [5, 2, T])
        d["fw1"] = inp("fw1", [33, NL, 64])
        d["fw2"] = inp("fw2", [64, NL, 64])
        d["fw3"] = inp("fw3", [64, NL, 256])
        d["fvec"] = inp("fvec", [64, NL, 3])
        d["fb3"] = inp("fb3", [1, NL, 256])
        d["featsT"] = inp("featsT", [33, T])
        d["window"] = inp("window", [128, 8, 256])
        d["dftF"] = inp("dftF", [2, T, T])
        d["dftG"] = inp("dftG", [2, T, T])
        d["sel"] = inp("sel", [4, 130])
        d["C0"] = inp("C0", [NL, 2, 128, 2, 65])
        d["m0rep"] = inp("m0rep", [128, NL, 2, 2])
        d["m0c"] = inp("m0c", [4, NL * 2])
        d["gkT"] = inp("gkT", [NL, 2, 128, 256])
        d["gvp"] = inp("gvp", [NL, 2, 128, 2, 192])
        d["skT"] = inp("skT", [NL, 2, 128, 256])
        d["svp"] = inp("svp", [NL, 2, 128, 2, 192])
        d["o_gk"] = outp("o_gk", [NL, 128, T])
        d["o_gv"] = outp("o_gv", [NL, T, 128])
        d["o_sk"] = outp("o_sk", [NL, 128, T])
        d["o_sv"] = outp("o_sv", [NL, T, 128])
        d["o_C"] = outp("o_C", [NL, 2, 4, 128, 2, 65])
        d["o_m"] = outp("o_m", [NL, 2, 4, 4])
        self.d = d
        if DEBUG:
            self.dbg_ymix = outp("dbg_ymix", [NL, 128, 8, T])
        k = {}
        k["lp"] = sb("lp", [128, NL, LPW]); k["cflags"] = sb("cflags", [128, 4])
        k["ident_f"] = sb("ident_f", [128, 128]); k["ident_bf"] = sb("ident_bf", [128, 128], BF16)
        k["blockones"] = sb("blockones", [128, 128], BF16)
        k["tri"] = sb("tri", [128, 4, 128]); k["ropeP"] = sb("ropeP", [128, 128])
        k["ones_f"] = sb("ones_f", [128, 128])
        k["fw1"] = sb("fw1", [33, NL, 64]); k["fw2"] = sb("fw2", [64, NL, 64]); k["fw3"] = sb("fw3", [64, NL, 256])
        k["fvec"] = sb("fvec", [64, NL, 3]); k["fb3"] = sb("fb3", [1, NL, 256]); k["fs"] = sb("fs", [64, NL, 4])
        k["sel"] = sb("sel", [4, 130]); k["m0rep"] = sb("m0rep", [128, NL, 2, 2]); k["m0c"] = sb("m0c", [4, NL * 2])
        k["Clo"] = sb("Clo", [128, 2, 2, 65]); k["Chi"] = sb("Chi", [128, 2, 2, 65])
        k["Cblo"] = sb("Cblo", [128, 2, 2, 66], BF16); k["Cbhi"] = sb("Cbhi", [128, 2, 2, 66], BF16)
        k["cvb"] = sb("cvb", [128, NL, 6, 2])
        self.k = k
        r_k = R("consts")
        self.r_k = r_k
        for nm in ("lp", "cflags", "tri", "ropeP", "fw1", "fw2", "fw3", "fvec", "fb3", "sel", "m0rep", "m0c"):
            S.dma("sp", k[nm][:], d[nm], writes=[r_k])
        S.dma("sp", k["ident_f"][:], d["ident"], writes=[r_k])
        S.dma("pool", k["ident_bf"][:], d["ident"], writes=[r_k])
        S.dma("pool", k["blockones"][:], d["blockones"], writes=[r_k])
        S.op("pool", lambda e: e.memset(k["ones_f"][:], 1.0), writes=[r_k])
        for nm in ("Clo", "Chi", "Cblo", "Cbhi"):
            S.op("pool", lambda e, nm=nm: e.memset(k[nm][:], 0.0), writes=[r_k])
        i2p = float(1.0 / (2 * math.pi))
        ts(S, "dve", k["fs"][:, :, 0:1], k["fvec"][:, :, 2:3], i2p, ALU.mult, [r_k], [r_k])
        tt(S, "dve", k["fs"][:, :, 1:2], k["fs"][:, :, 0:1], k["fvec"][:, :, 0:1], ALU.mult, [r_k], [r_k])
        tt(S, "dve", k["fs"][:, :, 2:3], k["fs"][:, :, 0:1], k["fvec"][:, :, 1:2], ALU.mult, [r_k], [r_k])
        CV = self.LP["CV"]
        for l in range(NL):
            cvv = k["lp"][:, l, CV:CV + 24].rearrange("p (a b) -> p a b", a=6)
            for (j, col) in ((0, 0), (1, 2)):
                ts(S, "dve", k["cvb"][:, l, :, j:j + 1], cvv[:, :, col:col + 1], k["cflags"][:, 2:3], ALU.mult,
                   [r_k], [r_k], s2=-1.0, op1=ALU.mult)

        for l in range(NL):
            ffn(l, 0, wd["w1_gate"], wd["w1_up"], wd["w1_down"])
            self.mixer(l)
            ffn(l, 2, wd["w2_gate"], wd["w2_up"], wd["w2_down"])

        gfo = NL * 24
        yv = yT_d.rearrange("(k p) t -> p k t", p=128)
        self.arena_reset()
        ost_ = self.carve([128, 2, 512], F32)
        ost = [ost_[:, 0, :], ost_[:, 1, :]]
        r_ost = RL(2, "ost")
        o_rr = 0
        for c in range(2):
            rms_rstd(c, lambda k: xT[:, k, CH(c)], [xr[k][c] for k in range(8)], 8, 1.0 / D, ones_bf[:])
            for k in range(8):
                i = self.f32_rr % 3
                self.f32_rr += 1
                tt(S, "dve", f32s[:, i, :], xT[:, k, CH(c)], rstd[:], ALU.mult, [xr[k][c], r_rstd], [r_f32s[i]])
                o = o_rr % 2
                o_rr += 1
                act(S, ost[o], f32s[:, i, :], AF.Copy, [r_f32s[i], r_gvec], [r_ost[o]],
                    scale=gvec[:, gfo + k:gfo + k + 1])
                S.dma("sp", yv[:, k, CH(c)], ost[o], reads=[r_ost[o]])
        S.finish()

    def mixer(self, l):
        S = self.S
        C = self.ctx
        hT, hr, CH = C["hT"], C["hr"], C["CH"]
        self.arena_reset()
        ymix = self.carve([128, 8, T], BF16)
        ymr = [[R("ym%d_%d" % (k, c)) for c in range(2)] for k in range(8)]
        base = self.aoff
        self.mod_require(l, 1)
        C["norm_mod"](l, 1)
        win = self.wd["w_in"][l].rearrange("(k p) n -> p k n", p=128)
        self.bank_pool = list(range(8))
        self.mix_mlstm(l, ymix, ymr, win)
        self.arena_reset(base)
        if STAGE >= 3:
            self.mix_attn(l, ymix, ymr, win, glob=True)
            self.arena_reset(base)
            self.mix_attn(l, ymix, ymr, win, glob=False)
            self.arena_reset(base)
        if STAGE >= 4:
            self.mix_hyena(l, ymix, ymr, win)
        self.prefetch(("wout", l, 0), self.parts_wout(l, 0))
        self.prefetch(("wout", l, 1), self.parts_wout(l, 1))
        wd_ = self.wd
        for g in range(2):
            self.prefetch(("gu", l, 2, g), self.parts_gu(wd_["w2_gate"], wd_["w2_up"], l, g))
        self.bank_pool = list(range(8))
        if DEBUG:
            f32s, r_f32s = C["f32s"], C["r_f32s"]
            for k in range(8):
                for c in range(2):
                    i = self.f32_rr % 3
                    self.f32_rr += 1
                    act(S, f32s[:, i, :], ymix[:, k, CH(c)], AF.Copy, [ymr[k][c]], [r_f32s[i]])
                    S.dma("sp", self.dbg_ymix[l, :, k, c * 512:(c + 1) * 512], f32s[:, i, :], reads=[r_f32s[i]])
        wov = self.wd["w_out"][l].rearrange("(k p) n -> p k n", p=128)
        for half in range(2):
            sl, sr = self.wload(self.parts_wout(l, half), key=("wout", l, half))
            s3 = self.slot_view(sl, 8)
            for j in range(4):
                dt_ = half * 4 + j
                for c in range(2):
                    pb, pr = self.bank()
                    groups = [(pb[:], s3[:, k, j * 128:(j + 1) * 128], ymix[:, k, CH(c)], k == 0, k == 7) for k in range(8)]
                    mm(S, groups, [sr] + [ymr[k][c] for k in range(8)], [pr])
                    C["resid_add"](l, 1, dt_, c, pb, pr)

    def parts_attn(self, l, glob):
        win = self.wd["w_in"][l].rearrange("(k p) n -> p k n", p=128)
        sv8 = lambda sl_: self.slot_view(sl_, 8)
        q0 = 1040 if glob else 1552
        k0, v0 = q0 + 256, q0 + 384
        return [(lambda sl_: sv8(sl_)[:, :, 0:256], win[:, :, q0:q0 + 256]),
                (lambda sl_: sv8(sl_)[:, :, 256:384], win[:, :, k0:k0 + 128]),
                (lambda sl_: sv8(sl_)[:, :, 384:448], win[:, :, k0 + 64:k0 + 128]),
                (lambda sl_: sv8(sl_)[:, :, 448:512], win[:, :, k0:k0 + 64]),
                (lambda sl_: sv8(sl_)[:, :, 512:640], win[:, :, v0:v0 + 128])]

    def parts_win(self, l, c0, n):
        win = self.wd["w_in"][l].rearrange("(k p) n -> p k n", p=128)
        return [(lambda sl_: self.slot_view(sl_, 8)[:, :, 0:n], win[:, :, c0:c0 + n])]

    def parts_wout(self, l, half):
        wov = self.wd["w_out"][l].rearrange("(k p) n -> p k n", p=128)
        return [(lambda sl_: self.slot_view(sl_, 8)[:, :, 0:512], wov[:, :, half * 512:(half + 1) * 512])]

    def parts_gu(self, wg, wu, l, g):
        wgv = wg[l].rearrange("(k p) n -> p k n", p=128)
        wuv = wu[l].rearrange("(k p) n -> p k n", p=128)
        c0 = g * 256
        return [(lambda sl_: self.slot_view(sl_, 8)[:, :, 0:256], wgv[:, :, c0:c0 + 256]),
                (lambda sl_: self.slot_view(sl_, 8)[:, :, 256:512], wuv[:, :, c0:c0 + 256])]

    def pbank(self):
        i = self.bank_pool[self.bank_rr % len(self.bank_pool)]
        self.bank_rr += 1
        return self.ps[i], self.psr[i]

    def proj_fm(self, c, s3, col0, pb):
        hT, CH = self.ctx["hT"], self.ctx["CH"]
        return [(pb[:], s3[:, k, col0:col0 + 128], hT[:, k, CH(c)], k == 0, k == 7) for k in range(8)]

    def proj_tok(self, tt_, s3, col0, ncols, pb, pc0):
        hT = self.ctx["hT"]
        return [(pb[:, pc0:pc0 + ncols], hT[:, k, tt_ * 128:(tt_ + 1) * 128], s3[:, k, col0:col0 + ncols], k == 0, k == 7)
                for k in range(8)]

    def mix_mlstm(self, l, ymix, ymr, win):
        S, k_, d = self.S, self.k, self.d
        C = self.ctx
        hr, CH = C["hr"], C["CH"]
        LP = self.LP
        lp = k_["lp"]
        r_k = self.r_k
        hall = [hr[k][c] for k in range(8) for c in range(2)]
        sv8 = lambda sl_: self.slot_view(sl_, 8)
        slA1, srA1 = self.wload(self.parts_win(l, 0, 512), key=("A1", l))
        slA2, srA2 = self.wload(self.parts_win(l, 512, 512), key=("A2", l))
        slG, srG = self.wload(self.parts_win(l, 1024, 16), key=("G", l))
        sA1, sA2, sG = sv8(slA1), sv8(slA2), sv8(slG)
        self.prefetch(("attn", l, True), self.parts_attn(l, True))
        if CUT == 1:
            return
        aqT = self.carve([128, 2, T], BF16)
        akp = self.carve([128, 4, T], BF16)
        ktok = self.carve([128, 8, 256], BF16)
        vaug = self.carve([128, 8, 4, 66], BF16)
        sgo = self.carve([128, 8, 256], BF16)
        gts = self.carve([128, 8, 16], F32)
        lf = self.carve([128, 8, 8], F32)
        cum = self.carve([128, 8, 16], F32)
        call = self.carve([128, 8, 8], F32)
        wall = self.carve([128, 8, 8], F32)
        wkl = self.carve([128, 8, 8], F32)
        wkall = self.carve([128, 8, 8], F32)
        wkm = self.carve([128, 8, 8], F32)
        dec = self.carve([128, 8, 4], F32)
        lfB = self.carve([128, 8, 128], F32)
        E = self.carve([128, 8, 128], F32)
        AT = self.carve([128, 2, 4, 128], BF16)
        hf = self.carve([128, 8, 256], F32)
        numt = self.carve([128, 2, 260], F32)
        tmpn = self.carve([128, 2, 260], F32)
        h64 = self.carve([128, 2, 256], F32)
        dsm = self.carve([128, 2, 8], F32)
        kwp = self.carve([128, 2, 4, 192], BF16)
        yatok = self.carve([128, 8, 256], BF16)
        snap = self.carve([128, 2, 4, 130], F32)
        e0 = self.carve([128, 2, 2], F32)
        ssall = self.carve([128, 8, 4], F32)
        sq1 = self.carve([128, 2, 256], F32)
        mst = self.carve([128, 64], F32)
        scl = self.carve([128, 2, 8], F32)
        r_aq, r_akp = RL(2, "aq"), RL(2, "akp")
        r_ktok, r_vaug, r_sgo, r_gts = RL(8, "ktok"), RL(8, "vaug"), RL(8, "sgo"), R("gts")
        r_gate = R("gate")
        r_lfB, r_E, r_AT = RL(2, "lfB"), RL(2, "E"), RL(2, "AT")
        r_hf = RL(8, "hf")
        r_num, r_tmpn, r_h64, r_dsm, r_kwp = RL(2, "num"), RL(2, "tmpn"), RL(2, "h64"), RL(2, "dsm"), RL(2, "kwp")
        r_C, r_Cb = RL(2, "C"), RL(2, "Cb")
        r_snap = [[R("snap") for _ in range(4)] for _ in range(2)]
        r_ms, r_scl = R("mst"), R("scl")
        r_ya = RL(8, "ya")
        r_ss = R("ss")
        r_sq1 = RL(2, "sq1")
        S.op("pool", lambda e: e.memset(akp, 0.0), writes=r_akp)
        S.op("pool", lambda e: e.memset(kwp, 0.0), writes=r_kwp)
        S.op("pool", lambda e: e.memset(vaug[:, :, :, 64:65], 1.0), writes=r_vaug)
        S.op("pool", lambda e: e.memset(hf, 0.0), writes=r_hf)
        if CUT == 2:
            return
        for t2 in range(2):
            for c in range(2):
                pb, pr = self.pbank()
                mm(S, self.proj_fm(c, sA1, t2 * 128, pb), [srA1] + hall, [pr])
                act(S, aqT[:, t2, CH(c)], pb[:], AF.Copy, [pr], [r_aq[c]])
        if CUT == 21:
            return
        for t2 in range(2):
            for c in range(2):
                pb, pr = self.pbank()
                mm(S, self.proj_fm(c, sA1, 256 + t2 * 128, pb), [srA1] + hall, [pr])
                act(S, akp[0:64, 2 * t2, CH(c)], pb[0:64, :], AF.Copy, [pr], [r_akp[c]])
                cp(S, "dve", akp[64:128, 2 * t2 + 1, CH(c)], pb[64:128, :], [pr], [r_akp[c]])
        if CUT == 22:
            return
        BG = LP["BG"]
        for t_ in range(8):
            p1, pr1 = self.pbank()
            p2, pr2 = self.pbank()
            g = self.proj_tok(t_, sA1, 256, 256, p1, 0) + self.proj_tok(t_, sA2, 0, 256, p1, 256)
            g += self.proj_tok(t_, sA2, 256, 256, p2, 0) + self.proj_tok(t_, sG, 0, 16, p2, 256)
            mm(S, g, [srA1, srA2, srG] + hall, [pr1, pr2])
            if CUT == 23:
                continue
            act(S, ktok[:, t_, :], p1[:, 0:256], AF.Copy, [pr1], [r_ktok[t_]])
            cp(S, "dve", vaug[:, t_, :, 0:64], p1[:, 256:512].rearrange("p (a b) -> p a b", a=4), [pr1], [r_vaug[t_]])
            if CUT == 24:
                continue
            act(S, sgo[:, t_, :], p2[:, 0:256], AF.Sigmoid, [pr2], [r_sgo[t_]])
            tt(S, "dve", gts[:, t_, :], p2[:, 256:272], lp[:, l, BG:BG + 16], ALU.add, [pr2, r_k], [r_gts])
        if CUT in (3, 23, 24):
            return
        ai, af = gts[:, :, 0:8], gts[:, :, 8:16]
        act(S, lf, af, AF.Exp, [r_gts], [r_gate], scale=-1.0)
        act(S, lf, lf, AF.Ln, [r_gate], [r_gate], bias=1.0)
        ts(S, "dve", lf, lf, -1.0, ALU.mult, [r_gate], [r_gate])
        tri = k_["tri"]
        pbc, prc = self.pbank()
        g = []
        for t_ in range(8):
            g.append((pbc[:, t_ * 16:t_ * 16 + 4], tri[:, 0, :], lf[:, t_, 0:4], True, True))
            g.append((pbc[:, t_ * 16 + 4:t_ * 16 + 8], tri[:, 1, :], lf[:, t_, 4:8], True, True))
            g.append((pbc[:, t_ * 16 + 8:t_ * 16 + 16], k_["ones_f"][:], lf[:, t_, 0:8], True, True))
        mm(S, g, [r_gate, r_k], [prc])
        cp(S, "dve", cum, pbc[:, 0:128].rearrange("p (a b) -> p a b", a=8), [prc], [r_gate])
        bc, bt = cum[:, :, 0:8], cum[:, :, 8:16]
        tt(S, "dve", call, ai, bc, ALU.subtract, [r_gts, r_gate], [r_gate])
        ts(S, "dve", call, call, LN8, ALU.add, [r_gate], [r_gate])
        act(S, wall, bc, AF.Exp, [r_gate], [r_gate])
        tt(S, "dve", wkl, call, bt, ALU.add, [r_gate], [r_gate])
        act(S, wkall, wkl, AF.Exp, [r_gate], [r_gate])
        ts(S, "dve", wkm, wkl, -LN8, ALU.add, [r_gate], [r_gate])
        for g_ in range(2):
            rows = slice(g_ * 64, (g_ + 1) * 64)
            act(S, dec[rows, :, :], cum[rows, :, 8 + g_:16:2], AF.Exp, [r_gate], [r_gate])
        if CUT == 4:
            return
        ident_f = k_["ident_f"]
        for dr in range(2):
            pbm, prm = self.pbank()
            g = [(pbm[0:4, t_:t_ + 1], lf[:, t_, dr * 4:dr * 4 + 4], k_["ones_f"][:, 0:1], True, True) for t_ in range(8)]
            mm(S, g, [r_gate, r_k], [prm])
            cp(S, "dve", mst[0:4, dr * 8:dr * 8 + 8], pbm[0:4, 0:8], [prm], [r_ms])
            for hh in range(2):
                pbt, prt = self.pbank()
                for q in range(4):
                    t_ = hh * 4 + q
                    S.op("pe", lambda e, t_=t_, q=q, pbt=pbt: e.transpose(pbt[0:4, q * 128:(q + 1) * 128],
                                                                          wkm[:, t_, dr * 4:dr * 4 + 4], ident_f[:]),
                         [r_gate, r_k], [prt])
                S.op("dve", lambda e, pbt=pbt, hh=hh: e.tensor_reduce(
                    out=mst[0:4, 16 + dr * 8 + hh * 4:16 + dr * 8 + hh * 4 + 4],
                    in_=pbt[0:4, :].rearrange("p (a b) -> p a b", a=4), axis=AX.X, op=ALU.max), [prt], [r_ms])
            bv_ = mst[0:4, dr * 8:dr * 8 + 8].rearrange("p (a b) -> p a b", a=4)
            av_ = mst[0:4, 16 + dr * 8:16 + dr * 8 + 8].rearrange("p (a b) -> p a b", a=4)
            fi, se = (0, 1) if dr == 0 else (1, 0)
            mf = mst[0:4, 32 + dr * 4:32 + dr * 4 + 4]
            ts(S, "dve", mf, bv_[:, :, fi], k_["m0c"][0:4, l * 2 + dr:l * 2 + dr + 1], ALU.add, [r_ms, r_k], [r_ms])
            tt(S, "dve", mf, mf, av_[:, :, fi], ALU.max, [r_ms], [r_ms])
            tt(S, "dve", mf, mf, bv_[:, :, se], ALU.add, [r_ms], [r_ms])
            tt(S, "dve", mf, mf, av_[:, :, se], ALU.max, [r_ms], [r_ms])
            S.dma("sp", d["o_m"][l, dr], mf, reads=[r_ms])
            en = mst[0:4, 40 + dr * 4:40 + dr * 4 + 4]
            act(S, en, mf, AF.Exp, [r_ms], [r_ms], scale=-1.0)
            rhs2 = mst[0:4, 48 + dr * 8:48 + dr * 8 + 8]
            tt(S, "dve", rhs2.rearrange("p (a b) -> p a b", a=4), en.unsqueeze(2).to_broadcast([4, 4, 2]),
               k_["sel"][0:4, 128:130].unsqueeze(1).to_broadcast([4, 4, 2]), ALU.mult, [r_ms, r_k], [r_ms])
            pbs, prs = self.pbank()
            mm(S, [(pbs[:, 0:8], k_["sel"][0:4, 0:128], rhs2, True, True)], [r_ms, r_k], [prs])
            cp(S, "dve", scl[:, dr, :], pbs[:, 0:8], [prs], [r_scl])
        if CUT == 5:
            return
        Clo, Chi, Cblo, Cbhi = k_["Clo"], k_["Chi"], k_["Cblo"], k_["Cbhi"]
        act(S, e0, k_["m0rep"][:, l, :, :], AF.Exp, [r_k], [r_gate])
        halves = ((slice(0, 64), Clo, Cblo), (slice(64, 128), Chi, Cbhi))
        for dr in range(2):
            for (rows, Cx, Cbx) in halves:
                S.dma("sp", Cx[rows, dr, :, :], d["C0"][l, dr, rows, :, :], writes=[r_C[dr]])
            for (rows, Cx, Cbx) in halves:
                tt(S, "dve", Cx[rows, dr, :, :], Cx[rows, dr, :, :], e0[rows, dr, :].unsqueeze(2).to_broadcast([64, 2, 65]),
                   ALU.mult, [r_C[dr], r_gate], [r_C[dr]])
                act(S, Cbx[rows, dr, :, 0:65], Cx[rows, dr, :, :], AF.Copy, [r_C[dr]], [r_Cb[dr]])
        if CUT == 6:
            return
        keep = k_["cflags"][:, 1:2]

        def process(dr, t_):
            tok = slice(t_ * 128, (t_ + 1) * 128)
            chs = slice(dr * 4, dr * 4 + 4)
            cp(S, "pool", lfB[:, chs, :], lf[:, t_, chs].unsqueeze(2).to_broadcast([128, 4, 128]), [r_gate], [r_lfB[dr]])
            yield
            pbe, pre = self.pbank()
            g = []
            for h in range(4):
                o = pbe[:, h * 128:(h + 1) * 128]
                g.append((o, lfB[:, dr * 4 + h, :], tri[:, dr, :], True, False))
                g.append((o, ident_f[:], tri[:, 2 + dr, :], False, True))
            mm(S, g, [r_lfB[dr], r_k], [pre])
            yield
            pbs_, prs_ = self.pbank()
            g = [(pbs_[:, h * 128:(h + 1) * 128], akp[:, h, tok], aqT[:, h // 2, tok], True, True) for h in range(4)]
            mm(S, g, r_akp + r_aq, [prs_])
            yield
            for h in range(4):
                act(S, E[:, dr * 4 + h, :], pbe[:, h * 128:(h + 1) * 128], AF.Exp, [pre, r_gate], [r_E[dr]],
                    bias=call[:, t_, dr * 4 + h:dr * 4 + h + 1])
                yield
            tt(S, "dve", AT[:, dr, :, :], pbs_[:].rearrange("p (a b) -> p a b", a=4), E[:, chs, :], ALU.mult,
               [prs_, r_E[dr]], [r_AT[dr]])
            yield
            pbi, pri = self.pbank()
            g = [(pbi[:, h * 65:(h + 1) * 65], AT[:, dr, h, :], vaug[:, t_, h, 0:65], True, True) for h in range(4)]
            mm(S, g, [r_AT[dr], r_vaug[t_]], [pri])
            yield
            pbx, prx = self.pbank()
            g = [(pbx[:, h * 65:(h + 1) * 65], aqT[:, h // 2, tok], (Cblo if h % 2 == 0 else Cbhi)[:, dr, h // 2, 0:65], True, True)
                 for h in range(4)]
            mm(S, g, r_aq + [r_Cb[dr]], [prx])
            yield
            v3 = lambda ap: ap.rearrange("p (a b) -> p a b", a=4)
            tt(S, "dve", v3(tmpn[:, dr, :]), v3(pbx[:, 0:260]), wall[:, t_, chs].unsqueeze(2).to_broadcast([128, 4, 65]),
               ALU.mult, [prx, r_gate], [r_tmpn[dr]])
            yield
            tt(S, "dve", numt[:, dr, :], pbi[:, 0:260], tmpn[:, dr, :], ALU.add, [pri, r_tmpn[dr]], [r_num[dr]])
            yield
            nv = v3(numt[:, dr, :])
            dn, rd = dsm[:, dr, 0:4], dsm[:, dr, 4:8]
            ts(S, "dve", dn, nv[:, :, 64], -1.0, ALU.mult, [r_num[dr]], [r_dsm[dr]], s2=1.0, op1=ALU.max)
            yield
            tt(S, "dve", dn, dn, nv[:, :, 64], ALU.max, [r_num[dr], r_dsm[dr]], [r_dsm[dr]])
            yield
            recip(S, rd, dn, [r_dsm[dr]], [r_dsm[dr]])
            yield
            tt(S, "dve", v3(h64[:, dr, :])[:, :, 0:64], nv[:, :, 0:64], rd.unsqueeze(2).to_broadcast([128, 4, 64]), ALU.mult,
               [r_num[dr], r_dsm[dr]], [r_h64[dr]])
            yield
            tt(S, "dve", hf[:, t_, :], hf[:, t_, :], h64[:, dr, :], ALU.add, [r_h64[dr], r_hf[t_]], [r_hf[t_]])
            yield
            self.mod_pump()
            yield
            tt(S, "pool", kwp[:, dr, :, 64:128], ktok[:, t_, :].rearrange("p (a b) -> p a b", a=4),
               wkall[:, t_, chs].unsqueeze(2).to_broadcast([128, 4, 64]), ALU.mult, [r_ktok[t_], r_gate], [r_kwp[dr]])
            yield
            pbd, prd = self.pbank()
            g = []
            for j in range(2):
                o = pbd[:, j * 65:(j + 1) * 65]
                g.append((o, kwp[:, dr, 2 * j, 64:192], vaug[:, t_, 2 * j, 0:65], True, False))
                g.append((o, kwp[:, dr, 2 * j + 1, 0:128], vaug[:, t_, 2 * j + 1, 0:65], False, True))
            mm(S, g, [r_kwp[dr], r_vaug[t_]], [prd])
            yield
            for (rows, Cx, Cbx) in halves:
                tt(S, "dve", Cx[rows, dr, :, :], Cx[rows, dr, :, :],
                   dec[rows, t_, dr * 2:dr * 2 + 2].unsqueeze(2).to_broadcast([64, 2, 65]), ALU.mult,
                   [r_C[dr], r_gate, r_Cb[dr]], [r_C[dr]])
                yield
                tt(S, "dve", Cx[rows, dr, :, :], Cx[rows, dr, :, :], pbd[rows, 0:130].rearrange("p (a b) -> p a b", a=2),
                   ALU.add, [r_C[dr], prd], [r_C[dr]])
                yield
            end = (t_ % 2 == 1) if dr == 0 else (t_ % 2 == 0)
            if end:
                sq_ = t_ // 2
                for (rows, Cx, Cbx) in halves:
                    tt(S, "dve", snap[rows, dr, sq_, :].rearrange("p (a b) -> p a b", a=2), Cx[rows, dr, :, :],
                       scl[rows, dr, sq_ * 2:sq_ * 2 + 2].unsqueeze(2).to_broadcast([64, 2, 65]), ALU.mult,
                       [r_C[dr], r_scl], [r_snap[dr][sq_]])
                    yield
                S.dma("sp", d["o_C"][l, dr, sq_], snap[:, dr, sq_, :].rearrange("p (a b) -> p a b", a=2),
                      reads=[r_snap[dr][sq_]])
                yield
                for (rows, Cx, Cbx) in halves:
                    ts(S, "dve", Cx[rows, dr, :, :], Cx[rows, dr, :, :], keep[rows, :], ALU.mult, [r_C[dr], r_k], [r_C[dr]])
                    yield
            for (rows, Cx, Cbx) in halves:
                act(S, Cbx[rows, dr, :, 0:65], Cx[rows, dr, :, :], AF.Copy, [r_C[dr]], [r_Cb[dr]])
                yield

        for i in range(8):
            gens = [process(0, i), process(1, 7 - i)]
            while gens:
                for g__ in list(gens):
                    try:
                        next(g__)
                    except StopIteration:
                        gens.remove(g__)
            self.mod_pump()
        if CUT == 7:
            return
        GM = LP["GM"]
        for t_ in range(8):
            b_ = t_ % 2
            tt(S, "dve", sq1[:, b_, :], hf[:, t_, :], hf[:, t_, :], ALU.mult, [r_hf[t_]], [r_sq1[b_]])
            S.op("dve", lambda e, t_=t_, b_=b_: e.tensor_reduce(out=ssall[:, t_, :],
                                                              in_=sq1[:, b_, :].rearrange("p (a b) -> p a b", a=4),
                                                              axis=AX.X, op=ALU.add), [r_sq1[b_]], [r_ss])
        act(S, ssall, ssall, AF.Ln, [r_ss], [r_ss], bias=EPS, scale=1.0 / 64)
        act(S, ssall, ssall, AF.Exp, [r_ss], [r_ss], scale=-0.5)
        ident_bf = k_["ident_bf"]
        for t_ in range(8):
            b_ = t_ % 2
            tt(S, "dve", sq1[:, b_, :].rearrange("p (a b) -> p a b", a=4), hf[:, t_, :].rearrange("p (a b) -> p a b", a=4),
               ssall[:, t_, :].unsqueeze(2).to_broadcast([128, 4, 64]), ALU.mult, [r_hf[t_], r_ss], [r_sq1[b_]])
            tt(S, "pool", h64[:, b_, :], sgo[:, t_, :], lp[:, l, GM:GM + 256], ALU.mult, [r_sgo[t_], r_k], [r_h64[b_]])
            tt(S, "dve", yatok[:, t_, :], sq1[:, b_, :], h64[:, b_, :], ALU.mult, [r_h64[b_], r_sq1[b_]], [r_ya[t_]])
            pbt, prt = self.pbank()
            pv = pbt[:].bitcast(BF16)
            for t2 in range(2):
                S.op("pe", lambda e, t2=t2, pv=pv, t_=t_: e.transpose(pv[:, t2 * 128:(t2 + 1) * 128],
                                                                      yatok[:, t_, t2 * 128:(t2 + 1) * 128], ident_bf[:]),
                     [r_ya[t_], r_k], [prt])
            c = t_ // 4
            for t2 in range(2):
                if t2 == 0:
                    act(S, ymix[:, t2, t_ * 128:(t_ + 1) * 128], pv[:, t2 * 128:(t2 + 1) * 128], AF.Copy, [prt], [ymr[t2][c]])
                else:
                    cp(S, "dve", ymix[:, t2, t_ * 128:(t_ + 1) * 128], pv[:, t2 * 128:(t2 + 1) * 128], [prt], [ymr[t2][c]])

    def mix_attn(self, l, ymix, ymr, win, glob):
        S, k_, d = self.S, self.k, self.d
        C = self.ctx
        hr, CH = C["hr"], C["CH"]
        LP = self.LP
        lp = k_["lp"]
        r_k = self.r_k
        hall = [hr[k][c] for k in range(8) for c in range(2)]
        sv8 = lambda sl_: self.slot_view(sl_, 8)
        q0 = 1040 if glob else 1552
        k0, v0 = q0 + 256, q0 + 384
        sl, sr = self.wload(self.parts_attn(l, glob), key=("attn", l, glob))
        if glob:
            self.prefetch(("attn", l, False), self.parts_attn(l, False))
        else:
            self.prefetch(("D1", l), self.parts_win(l, 2064, 512))
            self.prefetch(("D2", l), self.parts_win(l, 2576, 256))
        s3 = sv8(sl)
        ymb = 2 if glob else 4
        cs = self.carve([128, 2, T], F32)
        qpad = self.carve([128, 4, T], BF16)
        kfull = self.carve([128, 2, 1280], BF16)
        vpad = self.carve([128, 10, 2, 192], BF16)
        kout = self.carve([128, T], F32)
        vout = self.carve([128, 8, 128], F32)
        PT = self.carve([128, 3, 512], BF16)
        rden = self.carve([128, 2, 512], F32)
        raw = self.carve([128, 2, 512], F32)
        tb = self.carve([128, 2, 512], F32)
        gm = self.carve([128, 2, T], BF16)
        smask = self.carve([128, 8, 384], BF16) if not glob else None
        es = self.carve([128, 4], F32)
        r_cs, r_qp, r_kf, r_vp, r_kout, r_vout = R("cs"), RL(2, "qp"), R("kf"), RL(10, "vp"), R("kout"), R("vout")
        r_PT, r_rden, r_raw, r_tb, r_gm, r_sm, r_es = RL(3, "PT"), RL(2, "rden"), RL(2, "raw"), RL(2, "tb"), R("gm"), R("sm"), R("es")
        S.dma("sp", cs, d["ropeCS"], writes=[r_cs])
        S.op("pool", lambda e: e.memset(qpad, 0.0), writes=r_qp)
        S.op("pool", lambda e: e.memset(vpad[:, 2:10, :, :], 0.0), writes=r_vp[2:])
        kTd, vpd = (d["gkT"], d["gvp"]) if glob else (d["skT"], d["svp"])
        for x in range(2):
            S.dma("pool", kfull[:, x, 0:256], kTd[l, x], writes=[r_kf])
            S.dma("pool", vpad[:, x, :, :], vpd[l, x], writes=[r_vp[x]])
        if glob:
            S.dma("pool", gm[0:5, :, :], d["gmAB"], writes=[r_gm])
        if not glob:
            S.dma("pool", smask, d["swamask"], writes=[r_sm])
            SK = LP["SK"]
            act(S, es, lp[:, l, SK:SK + 4], AF.Exp, [r_k], [r_es])
        GQK = LP["GQK"]
        ropeP = k_["ropeP"]
        rstd2 = self.carve([128, 2, 512], F32)
        r_rstd2 = RL(2, "rstd2")

        def prelude(ti, c, b_):
            col0 = ti * 128
            pb, pr = self.pbank()
            mm(S, self.proj_fm(c, s3, col0, pb), [sr] + hall, [pr])
            yield
            rw = raw[:, b_, :]
            if glob:
                act(S, rw, pb[:], AF.Copy, [pr], [r_raw[b_]])
                yield
                i = self.sq_rr % 2
                self.sq_rr += 1
                sqb, r_sq = C["sqb"], C["r_sq"]
                act(S, sqb[:, i, :], rw, AF.Square, [r_raw[b_]], [r_sq[i]])
                yield
                p2, pr2 = self.pbank()
                mm(S, [(p2[:], k_["blockones"][:], sqb[:, i, :], True, True)], [r_sq[i], r_k], [pr2])
                yield
                rstd, r_rstd = rstd2[:, b_, :], r_rstd2[b_]
                act(S, rstd, p2[:], AF.Ln, [pr2], [r_rstd], bias=EPS, scale=1.0 / 64)
                yield
                act(S, rstd, rstd, AF.Exp, [r_rstd], [r_rstd], scale=-0.5)
                yield
                tt(S, "dve", rw, rw, rstd, ALU.mult, [r_raw[b_], r_rstd], [r_raw[b_]])
                yield
                gcol = GQK + (0 if ti < 2 else 1)
                act(S, rw, rw, AF.Copy, [r_raw[b_], r_k], [r_raw[b_]], scale=lp[:, l, gcol:gcol + 1])
                yield
            else:
                act(S, rw, pb[:], AF.Copy, [pr], [r_raw[b_]])
                yield
            p3, pr3 = self.pbank()
            mm(S, [(p3[:], ropeP[:], rw, True, True)], [r_raw[b_], r_k], [pr3])
            yield
            ta, tb_ = rw, tb[:, b_, :]
            tt(S, "pool", ta, rw, cs[:, 0, CH(c)], ALU.mult, [r_raw[b_], r_cs], [r_raw[b_]])
            yield
            tt(S, "dve", tb_, p3[:], cs[:, 1, CH(c)], ALU.mult, [pr3, r_cs], [r_tb[b_]])
            yield
            if ti < 2:
                for g_ in range(2):
                    rows = slice(g_ * 64, (g_ + 1) * 64)
                    tt(S, "dve", qpad[rows, 2 * ti + g_, CH(c)], ta[rows, :], tb_[rows, :], ALU.add, [r_tb[b_], r_raw[b_]], [r_qp[c]])
                    yield
            elif ti == 2:
                tt(S, "dve", kout[:, CH(c)], ta, tb_, ALU.add, [r_tb[b_], r_raw[b_]], [r_kout])
                yield
                act(S, kfull[:, 0, 256 + c * 512:256 + (c + 1) * 512], kout[:, CH(c)], AF.Copy, [r_kout], [r_kf])
                yield
            else:
                tt(S, "dve", kfull[:, 1, 256 + c * 512:256 + (c + 1) * 512], ta, tb_, ALU.add, [r_tb[b_], r_raw[b_]], [r_kf])
                yield

        its = [(ti, c) for ti in range(4) for c in range(2)]
        for i0 in range(0, 8, 2):
            gens = [prelude(its[i0][0], its[i0][1], 0), prelude(its[i0 + 1][0], its[i0 + 1][1], 1)]
            while gens:
                for g__ in list(gens):
                    try:
                        next(g__)
                    except StopIteration:
                        gens.remove(g__)
        S.dma("sp", (d["o_gk"] if glob else d["o_sk"])[l], kout, reads=[r_kout])
        for t_ in range(8):
            pb, pr = self.pbank()
            mm(S, self.proj_tok(t_, s3, 512, 128, pb, 0), [sr] + hall, [pr])
            act(S, vout[:, t_, :], pb[:, 0:128], AF.Copy, [pr], [r_vout])
            cp(S, "dve", vpad[:, 2 + t_, :, 64:128], pb[:, 0:128].rearrange("p (a b) -> p a b", a=2), [pr], [r_vp[2 + t_]])
        S.dma("sp", (d["o_gv"] if glob else d["o_sv"])[l].rearrange("(t p) n -> p t n", p=128), vout, reads=[r_vout])
        self.bank_pool = [0, 1, 2, 3]
        ones_bf = C["ones_bf"]
        r_ones = C["r_ones"]
        ident_bf = k_["ident_bf"]
        ctxb = k_["cflags"][:, 0:1]
        pt_rr = 0
        acc_rr = 0
        for h in range(4):
            kv, g_ = h // 2, h % 2
            kx = 0 if g_ == kv else 1
            rows = slice(g_ * 64, (g_ + 1) * 64)
            vw = slice(64, 192) if g_ == 0 else slice(0, 128)
            for c in range(2):
                if glob:
                    tiles = [(mt, 0, 512) for mt in range(10)]
                else:
                    tiles = [(0, 0, 512), (1, 0, 512)]
                    for j in range(8):
                        lo, hi = max((j - 1) * 128, c * 512), min((j + 2) * 128, (c + 1) * 512)
                        if hi > lo:
                            tiles.append((2 + j, lo - c * 512, hi - c * 512))
                ai_ = 4 + 2 * (acc_rr % 2)
                acc_rr += 1
                pn, prn, pd, prd = self.ps[ai_], self.psr[ai_], self.ps[ai_ + 1], self.psr[ai_ + 1]
                pend = []

                def qk(idx):
                    mt, lo, hi = tiles[idx]
                    pb, pr = self.pbank()
                    qs = slice(c * 512 + lo, c * 512 + hi)
                    g = [(pb[:, lo:hi], kfull[:, kx, mt * 128:(mt + 1) * 128], qpad[:, h, qs], True, mt < 2)]
                    rd = [r_kf, r_qp[c]]
                    if mt >= 2:
                        if glob:
                            g.append((pb[:, lo:hi], gm[0:5, 0, (mt - 2) * 128:(mt - 1) * 128], gm[0:5, 1, qs], False, True))
                            rd.append(r_gm)
                        else:
                            j = mt - 2
                            m0_ = c * 512 + lo - (j - 1) * 128
                            g.append((pb[:, lo:hi], ident_bf[:], smask[:, j, m0_:m0_ + (hi - lo)], False, True))
                            rd += [r_sm, r_k]
                    mm(S, g, rd, [pr])
                    return pb, pr

                nt = len(tiles)
                look = 2
                for idx in range(min(look, nt)):
                    pend.append(qk(idx))
                for idx in range(nt):
                    mt, lo, hi = tiles[idx]
                    pb, pr = pend.pop(0)
                    if idx + look < nt:
                        pend.append(qk(idx + look))
                    pi = pt_rr % 3
                    pt_rr += 1
                    act(S, PT[:, pi, lo:hi], pb[:, lo:hi], AF.Exp, [pr, r_k], [r_PT[pi]],
                        bias=(ctxb if mt < 2 else 0.0), scale=0.125)
                    g = [(pn[:, lo:hi], vpad[:, mt, kv, vw], PT[:, pi, lo:hi], idx == 0, idx == nt - 1),
                         (pd[:, lo:hi], ones_bf[:], PT[:, pi, lo:hi], idx == 0, idx == nt - 1)]
                    mm(S, g, [r_vp[mt], r_PT[pi], r_ones], [prn, prd])
                ri = (h * 2 + c) % 2
                if glob:
                    recip(S, rden[rows, ri, :], pd[rows, :], [prd], [r_rden[ri]])
                else:
                    ts(S, "dve", rden[rows, ri, :], pd[rows, :], es[rows, h:h + 1], ALU.add, [prd, r_es], [r_rden[ri]])
                    recip(S, rden[rows, ri, :], rden[rows, ri, :], [r_rden[ri]], [r_rden[ri]])
                tt(S, "dve", ymix[rows, ymb + kv, CH(c)], pn[rows, :], rden[rows, ri, :], ALU.mult, [prn, r_rden[ri]],
                   [ymr[ymb + kv][c]])
        self.bank_pool = list(range(8))

    def mix_hyena(self, l, ymix, ymr, win):
        S, k_, d = self.S, self.k, self.d
        C = self.ctx
        hr, CH = C["hr"], C["CH"]
        LP = self.LP
        lp = k_["lp"]
        r_k = self.r_k
        hall = [hr[k][c] for k in range(8) for c in range(2)]
        sv8 = lambda sl_: self.slot_view(sl_, 8)
        TWO_PI = float(2 * math.pi)
        xoff = self.aoff
        feats = self.carve([128, T], F32)
        z1 = self.carve([128, T], F32)
        z2 = self.carve([128, T], F32)
        wnd = self.carve([128, 8, 256], F32)
        rr = self.carve([128, 512], F32)
        ii = self.carve([128, 512], I32)
        kf = self.carve([128, 512], F32)
        xend = self.aoff
        self.aoff = xoff
        raw = self.carve([128, 3, T], F32)
        uct = self.carve([128, 2, T], F32)
        pa = self.carve([128, 2, 256], F32)
        pq = self.carve([128, 2, 256], F32)
        yt = self.carve([128, 2, 256], F32)
        assert self.aoff <= xend
        self.aoff = xend
        r_X = R("X")
        x0 = self.carve([128, 2, T], F32)
        zf = self.carve([128, 2, T], F32)
        zbf = self.carve([128, 2, T], BF16)
        zh = self.carve([128, 8, 512], BF16)
        ZH = self.carve([128, 2, 512], F32)
        Y = self.carve([128, 8, 2, 256], BF16)
        pa2 = self.carve([128, 2, 256], F32)
        yt2 = ZH[:, :, 0:256]
        r_x0, r_zf, r_zbf = RL(2, "x0"), RL(2, "zf"), RL(2, "zbf")
        r_zh = RL(8, "zh")
        r_ZH, r_Y, r_pa, r_pq, r_yt = R("ZH"), RL(8, "Y"), R("pa"), R("pq"), RL(2, "yt")
        S.dma("sp", feats[0:33, :], d["featsT"], writes=[r_X])
        S.dma("sp", wnd, d["window"], writes=[r_X])
        fs = k_["fs"]

        def sin_layer(pb, pr, dst, bcol):
            ts(S, "dve", rr[0:64, :], pb[0:64, :], fs[:, l, 0:1], ALU.mult, [pr, r_k], [r_X], s2=fs[:, l, bcol:bcol + 1],
               op1=ALU.add)
            cp(S, "dve", ii[0:64, :], rr[0:64, :], [r_X], [r_X])
            cp(S, "dve", kf[0:64, :], ii[0:64, :], [r_X], [r_X])
            tt(S, "dve", rr[0:64, :], rr[0:64, :], kf[0:64, :], ALU.subtract, [r_X], [r_X])
            act(S, dst, rr[0:64, :], AF.Sin, [r_X], [r_X], scale=TWO_PI)

        for c in range(2):
            pb, pr = self.pbank()
            mm(S, [(pb[0:64, :], k_["fw1"][0:33, l, :], feats[0:33, CH(c)], True, True)], [r_X, r_k], [pr])
            sin_layer(pb, pr, z1[0:64, CH(c)], 1)
        for c in range(2):
            pb, pr = self.pbank()
            mm(S, [(pb[0:64, :], k_["fw2"][0:64, l, :], z1[0:64, CH(c)], True, True)], [r_X, r_k], [pr])
            sin_layer(pb, pr, z2[0:64, CH(c)], 2)
        for t_ in range(8):
            pb, pr = self.pbank()
            tok = slice(t_ * 128, (t_ + 1) * 128)
            mm(S, [(pb[:, 0:256], z2[0:64, tok], k_["fw3"][0:64, l, :], True, False),
                   (pb[:, 0:256], k_["ones_f"][0:1, :], k_["fb3"][0:1, l, :], False, True)], [r_X, r_k], [pr])
            tt(S, "dve", zh[:, t_, 256:512], pb[:, 0:256], wnd[:, t_, :], ALU.mult, [pr, r_X], [r_zh[t_]])
        slD1, srD1 = self.wload(self.parts_win(l, 2064, 512), key=("D1", l))
        slD2, srD2 = self.wload(self.parts_win(l, 2576, 256), key=("D2", l))
        sD1, sD2 = sv8(slD1), sv8(slD2)
        Fd, Gd = d["dftF"], d["dftG"]
        fv = lambda m: Fd[m].rearrange("(k p) n -> p k n", p=128)
        gv = lambda m: Gd[m].rearrange("(k p) n -> p k n", p=128)

        def parts_dft(vw, q):
            return [(lambda sl_: sv8(sl_)[:, :, 0:256], vw(0)[:, :, q * 256:(q + 1) * 256]),
                    (lambda sl_: sv8(sl_)[:, :, 256:512], vw(1)[:, :, q * 256:(q + 1) * 256])]

        def zip_run(gens):
            gens = list(gens)
            while gens:
                for g__ in list(gens):
                    try:
                        next(g__)
                    except StopIteration:
                        gens.remove(g__)

        for q in range(2):
            self.prefetch(("F", l, q), parts_dft(fv, q))
        CV = LP["CV"]
        cvb = k_["cvb"]
        r_raw = RL(3, "hraw")
        r_uct = RL(2, "uct")

        def conv_chain(ct, ui):
            s3, col, srr = ((sD1, ct * 128, srD1), (sD1, 256 + ct * 128, srD1), (sD2, ct * 128, srD2))[ui]
            rr_ = r_raw[ui]
            fx = [r_X] if ct == 0 else []
            for c in range(2):
                pb, pr = self.pbank()
                mm(S, self.proj_fm(c, s3, col, pb), [srr] + hall, [pr])
                yield
                act(S, raw[:, ui, CH(c)], pb[:], AF.Copy, [pr], [rr_] + fx)
                yield
            tile = ui * 2 + ct
            cw = lambda j: lp[:, l, CV + tile * 4 + j:CV + tile * 4 + j + 1]
            u = raw[:, ui, :]
            if ui == 0:
                dst, wr = x0[:, ct, :], [r_x0[ct]]
            else:
                dst, wr = uct[:, ui - 1, :], [r_uct[ui - 1]] + fx
            rd = [rr_, r_k]
            act(S, dst, u, AF.Identity, rd, wr, bias=cw(3), scale=cw(1))
            yield
            for (o, a_, sc) in ((dst[:, 1:T], u[:, 0:T - 1], cw(0)), (dst[:, 0:T - 1], u[:, 1:T], cw(2)),
                                (dst[:, 256:T:256], u[:, 255:T - 1:256], cvb[:, l, tile, 0:1]),
                                (dst[:, 255:T - 1:256], u[:, 256:T:256], cvb[:, l, tile, 1:2])):
                S.op("dve", lambda e, o=o, a_=a_, sc=sc: e.scalar_tensor_tensor(out=o, in0=a_, scalar=sc, in1=o,
                                                                              op0=ALU.mult, op1=ALU.add), rd + wr[:1], wr[:1])
                yield

        for ct in range(2):
            zip_run([conv_chain(ct, ui) for ui in range(3)])
            tt(S, "dve", zf[:, ct, :], uct[:, 0, :], uct[:, 1, :], ALU.mult, r_uct, [r_zf[ct]])
            act(S, zbf[:, ct, :], zf[:, ct, :], AF.Copy, [r_zf[ct]], [r_zbf[ct]])
        for q in range(2, 4):
            self.prefetch(("F", l, q), parts_dft(fv, q))
        ident_bf = k_["ident_bf"]
        for t_ in range(8):
            pb, pr = self.pbank()
            pv = pb[:].bitcast(BF16)
            for ct in range(2):
                S.op("pe", lambda e, ct=ct, pv=pv, t_=t_: e.transpose(pv[:, ct * 128:(ct + 1) * 128],
                                                                      zbf[:, ct, t_ * 128:(t_ + 1) * 128], ident_bf[:]),
                     [r_zbf[ct], r_k], [pr])
            cp(S, "dve", zh[:, t_, 0:256], pv[:, 0:256], [pr], [r_zh[t_]])
        ZHs = [ZH, zbf.rearrange("p a b -> p (a b)").bitcast(F32).rearrange("p (a b) -> p a b", a=2)]
        r_ZHs = [r_ZH, R("ZHb")]
        PAs, PQs = [pa, pa2], [pq, yt]
        r_PA, r_PQ = RL(2, "PA"), RL(2, "PQ")
        seen = set()

        def ft_chain(ft, fj, s3, sr):
            bi = ft % 2
            Zb, rz = ZHs[bi], r_ZHs[bi]
            PA, PQ, rpa, rpq = PAs[bi], PQs[bi], r_PA[bi], r_PQ[bi]
            fresh = bi not in seen
            seen.add(bi)
            fz = (r_zbf if (fresh and bi == 1) else [])
            fx = ([r_X] if fresh else [])
            pre_, prr = self.pbank()
            pim, pri = self.pbank()
            g = [(pre_[:], s3[:, t_, fj * 128:(fj + 1) * 128], zh[:, t_, :], t_ == 0, t_ == 7) for t_ in range(8)]
            g += [(pim[:], s3[:, t_, 256 + fj * 128:256 + (fj + 1) * 128], zh[:, t_, :], t_ == 0, t_ == 7) for t_ in range(8)]
            mm(S, g, [sr] + r_zh, [prr, pri])
            yield
            act(S, Zb[:, 0, :], pre_[:], AF.Copy, [prr], [rz] + fz)
            yield
            act(S, Zb[:, 1, :], pim[:], AF.Copy, [pri], [rz])
            yield
            Zr, Hr, Zi, Hi = Zb[:, 0, 0:256], Zb[:, 0, 256:512], Zb[:, 1, 0:256], Zb[:, 1, 256:512]
            tt(S, "dve", PA[:, 0, :], Zr, Hr, ALU.mult, [rz], [rpa] + fx)
            yield
            tt(S, "pool", PQ[:, 0, :], Zr, Hi, ALU.mult, [rz], [rpq] + fx)
            yield
            tt(S, "dve", PA[:, 1, :], Zi, Hi, ALU.mult, [rz], [rpa])
            yield
            tt(S, "pool", PQ[:, 1, :], Zi, Hr, ALU.mult, [rz], [rpq])
            yield
            tt(S, "dve", Y[:, ft, 0, :], PA[:, 0, :], PA[:, 1, :], ALU.subtract, [rpa], [r_Y[ft]])
            yield
            tt(S, "pool", Y[:, ft, 1, :], PQ[:, 0, :], PQ[:, 1, :], ALU.add, [rpq], [r_Y[ft]])
            yield

        for q in range(4):
            sl, sr = self.wload(parts_dft(fv, q), key=("F", l, q))
            s3 = sv8(sl)
            zip_run([ft_chain(q * 2 + fj, fj, s3, sr) for fj in range(2)])
        HB = LP["HB"]
        for q in range(2):
            self.prefetch(("G", l, q), parts_dft(gv, q))
        for q in range(4):
            sl, sr = self.wload(parts_dft(gv, q), key=("G", l, q))
            if q + 2 < 4:
                self.prefetch(("G", l, q + 2), parts_dft(gv, q + 2))
            s3 = sv8(sl)
            ns = slice(q * 256, (q + 1) * 256)
            for ct in range(2):
                pb, pr = self.pbank()
                g = []
                for ft in range(8):
                    g.append((pb[:, 0:256], Y[:, ft, 0, ct * 128:(ct + 1) * 128], s3[:, ft, 0:256], ft == 0, False))
                    g.append((pb[:, 0:256], Y[:, ft, 1, ct * 128:(ct + 1) * 128], s3[:, ft, 256:512], False, ft == 7))
                mm(S, g, [sr] + r_Y, [pr])
                ytb, ryt = (yt2[:, ct, :], r_yt[ct])
                act(S, ytb, pb[:, 0:256], AF.Copy, [pr], [ryt] + ([r_PA[1], r_ZHs[0]] if q == 0 else []))
                S.op("dve", lambda e, ytb=ytb, ct=ct: e.scalar_tensor_tensor(out=ytb, in0=zf[:, ct, ns],
                                                                           scalar=lp[:, l, HB + ct:HB + ct + 1], in1=ytb,
                                                                           op0=ALU.mult, op1=ALU.add),
                     [r_zf[ct], ryt, r_k], [ryt])
                tt(S, "dve", ymix[:, 6 + ct, ns], x0[:, ct, ns], ytb, ALU.mult, [r_x0[ct], ryt], [ymr[6 + ct][q // 2]])


def fm(v):
    return np.ascontiguousarray(np.asarray(v, np.float32).reshape(8, 128).T)


def _consts(kind):
    c = {}
    p = np.arange(128)
    t = np.arange(T)
    ident = np.eye(128, dtype=np.float32)
    c["ident"] = ident
    bo = np.zeros((128, 128), np.float32)
    bo[:64, :64] = 1
    bo[64:, 64:] = 1
    c["blockones"] = bo
    r_, t_ = np.meshgrid(p, p, indexing="ij")
    tri = np.zeros((128, 4, 128), np.float32)
    tri[:, 0, :] = (r_ <= t_)
    tri[:, 1, :] = (r_ >= t_)
    tri[:, 2, :] = np.where(r_ <= t_, 0.0, NEG)
    tri[:, 3, :] = np.where(r_ >= t_, 0.0, NEG)
    c["tri"] = tri
    P = np.zeros((128, 128), np.float32)
    for b in range(0, 128, 32):
        for i in range(16):
            P[b + i + 16, b + i] = -1.0
            P[b + i, b + i + 16] = 1.0
    c["ropeP"] = P
    cs = np.zeros((128, 2, T), np.float32)
    if kind == "s":
        dd = p % 64
        inv = (10000.0 ** (-(dd % 16).astype(np.float32) / np.float32(16))).astype(np.float32)
        row = (t // 64).astype(np.float32)
        col = (t % 64).astype(np.float32)
        pos = np.where((dd // 32)[:, None] == 0, row[None, :], col[None, :]).astype(np.float32)
        ang = (pos * inv[:, None]).astype(np.float32)
        cs[:, 0, :] = np.cos(ang)
        cs[:, 1, :] = np.sin(ang)
    else:
        cs[:, 0, :] = 1.0
    c["ropeCS"] = cs
    sm = np.full((128, 8, 384), NEG, np.float32)
    for j in range(8):
        m = j * 128 + p[:, None]
        q = (j - 1) * 128 + np.arange(384)[None, :]
        inr = (q >= 0) & (q < T)
        if kind == "s":
            ok = (np.abs(m - q) <= 128) & inr
        else:
            ok = ((m // 256) == (q // 256)) & inr
        sm[:, j, :] = np.where(ok, 0.0, NEG)
    c["swamask"] = sm
    gm = np.zeros((5, 2, T), np.float32)
    if kind == "p":
        gm[0, 0, :] = 1.0
        gm[0, 1, :] = -BIG
        for s_ in range(4):
            gm[1 + s_, 0, :] = (t // 256 == s_)
            gm[1 + s_, 1, :] = BIG * (t // 256 == s_)
    c["gmAB"] = gm
    c["cflags"] = np.tile(np.array([[0.0, 1.0, 0.0, 0.0]] if kind == "s" else [[NEG, 0.0, 1.0, 0.0]], np.float32), (128, 1))
    sel = np.zeros((4, 130), np.float32)
    for h in range(4):
        sel[h, 0:128] = ((p >= 64).astype(int) == (h % 2))
        sel[h, 128 + h // 2] = 1.0
    c["sel"] = sel
    L = T if kind == "s" else 256
    rep = T // L
    pos = np.arange(L, dtype=np.float32)
    t01 = pos / np.float32(max(L - 1, 1))
    lin = np.linspace(1e-4, 15.0, 16, dtype=np.float32)
    ang = (np.float32(2.0 * math.pi / L) * pos[:, None] * lin[None, :]).astype(np.float32)
    feats = np.concatenate([t01[:, None], np.cos(ang), -np.sin(ang)], -1).astype(np.float32)
    c["featsT"] = np.ascontiguousarray(np.tile(feats, (rep, 1)).T)
    centre = L // 2
    dist = np.abs(pos - centre) / np.float32(max(centre, 1))
    deltas = np.abs(np.linspace(math.log(0.01) / 1.5, math.log(0.01) / 0.3, 256, dtype=np.float32))
    wnd = np.exp(-dist[:, None] * deltas[None, :]).astype(np.float32)
    c["window"] = np.ascontiguousarray(np.tile(wnd, (rep, 1)).reshape(8, 128, 256).transpose(1, 0, 2))
    N = 2 * L
    tt_ = np.arange(L, dtype=np.float64)
    ff = np.arange(L, dtype=np.float64)
    th = math.pi * (2 * ff + 1) / N
    Fc = np.cos(tt_[:, None] * th[None, :])
    Fs = -np.sin(tt_[:, None] * th[None, :])
    Gc = (2.0 / N) * np.cos(th[:, None] * (tt_[None, :] + L // 2))
    Gs = -(2.0 / N) * np.sin(th[:, None] * (tt_[None, :] + L // 2))
    dF = np.zeros((2, T, T), np.float32)
    dG = np.zeros((2, T, T), np.float32)
    for s_ in range(rep):
        sl = slice(s_ * L, (s_ + 1) * L)
        dF[0, sl, sl] = Fc
        dF[1, sl, sl] = Fs
        dG[0, sl, sl] = Gc
        dG[1, sl, sl] = Gs
    c["dftF"] = dF
    c["dftG"] = dG
    return c


def host_inputs(inp, cores=None):
    f = lambda a: np.ascontiguousarray(np.asarray(a, dtype=np.float32))
    A = {k: np.asarray(v) for k, v in inp.items()}
    shared = {}
    for nm in ("w_ada", "w1_gate", "w1_up", "w1_down", "w_in", "w_out", "w2_gate", "w2_up", "w2_down"):
        shared[nm] = f(A[nm])
    shared["b_adaT"] = f(A["b_ada"].reshape(NL, 72, 128).transpose(0, 2, 1))
    gv = []
    for l in range(NL):
        gv += [fm(A["g_ff1"][l]), fm(A["g_mix"][l]), fm(A["g_ff2"][l])]
    gv.append(fm(A["g_final"]))
    shared["gvec"] = f(np.concatenate(gv, axis=1))
    lp = np.zeros((128, NL, 304), np.float32)
    p = np.arange(128)
    for l in range(NL):
        lp[:, l, 0:16] = A["b_gates"][l][None, :]
        lp[:, l, 16:272] = A["g_mlstm"][l][None, :]
        lp[:, l, 272] = A["g_qnorm"][l][p % 64]
        lp[:, l, 273] = A["g_knorm"][l][p % 64]
        lp[:, l, 274:278] = A["sinks"][l][None, :]
        for i in range(6):
            ch = i * 128 + p
            lp[:, l, 278 + i * 4 + 0] = A["conv_w"][l][0, ch]
            lp[:, l, 278 + i * 4 + 1] = A["conv_w"][l][1, ch]
            lp[:, l, 278 + i * 4 + 2] = A["conv_w"][l][2, ch]
            lp[:, l, 278 + i * 4 + 3] = A["conv_b"][l][ch]
        for ct in range(2):
            lp[:, l, 302 + ct] = A["hyena_bias"][l][ct * 128 + p]
    shared["lp"] = lp
    shared["fw1"] = f(A["filt_w1"].transpose(1, 0, 2))
    shared["fw2"] = f(A["filt_w2"].transpose(1, 0, 2))
    shared["fw3"] = f(A["filt_w3"].transpose(1, 0, 2))
    shared["fvec"] = f(np.stack([A["filt_b1"], A["filt_b2"], A["filt_freq"]], -1).transpose(1, 0, 2))
    shared["fb3"] = f(A["filt_b3"][None, :, :])
    cst = {"s": _consts("s"), "p": _consts("p")}
    maps = []
    xs, xp = A["x_sample"], A["x_prompt"]
    for core in (range(8) if cores is None else cores):
        m = dict(shared)
        kind = "s" if core < 4 else "p"
        m.update(cst[kind])
        C0 = np.zeros((NL, 2, 128, 2, 65), np.float32)
        m0rep = np.zeros((128, NL, 2, 2), np.float32)
        m0c = np.zeros((4, NL * 2), np.float32)
        kT = {n: np.zeros((NL, 2, 128, 256), np.float32) for n in ("gkT", "skT")}
        vp = {n: np.zeros((NL, 2, 128, 2, 192), np.float32) for n in ("gvp", "svp")}
        if core < 4:
            b = core
            m["xT"] = f(xs[b].T)
            m["cvec"] = fm(A["c"][b])
            sC, sn, smm = A["state_mlstm_C"][b], A["state_mlstm_n"][b], A["state_mlstm_m"][b]
            for g_ in range(2):
                for pr_ in range(2):
                    h = 2 * pr_ + g_
                    C0[:, :, g_ * 64:(g_ + 1) * 64, pr_, 0:64] = sC[:, :, h]
                    C0[:, :, g_ * 64:(g_ + 1) * 64, pr_, 64] = sn[:, :, h]
                    m0rep[g_ * 64:(g_ + 1) * 64, :, :, pr_] = smm[None, :, :, h]
            for l in range(NL):
                for dr in range(2):
                    m0c[:, l * 2 + dr] = smm[l, dr, :]
            for (kn, vn, ck_, cv_) in (("gkT", "gvp", "cache_gattn_k", "cache_gattn_v"), ("skT", "svp", "cache_swa_k", "cache_swa_v")):
                ck, cv = A[ck_][b], A[cv_][b]
                t1 = ck.transpose(0, 2, 3, 1).reshape(NL, 128, 256)
                t2 = ck[:, :, ::-1, :].transpose(0, 2, 3, 1).reshape(NL, 128, 256)
                kT[kn][:, 0] = t1
                kT[kn][:, 1] = t2
                vp[vn][:, :, :, :, 64:128] = cv.reshape(NL, 2, 128, 2, 64)
        else:
            j = core - 4
            m["xT"] = f(xp[4 * j:4 * j + 4].reshape(T, D).T)
            m["cvec"] = fm(A["c_ctx"])
        m["C0"], m["m0rep"], m["m0c"] = C0, m0rep, m0c
        m.update(kT)
        m.update(vp)
        maps.append(m)
    return maps


_PROG = None


def get_prog():
    global _PROG
    if _PROG is None:
        _PROG = Prog()
    return _PROG


def run_device(inputs, trace=False, cores=None):
    prog = get_prog()
    maps = host_inputs(inputs, cores)
    maps = [{k: np.ascontiguousarray(v, dtype=np.float32) for k, v in m.items() if k in prog.din} for m in maps]
    for m in maps:
        for k, shp in prog.din.items():
            assert m[k].shape == shp, (k, m[k].shape, shp)
    res = run_bass_kernel_spmd(prog.nc, maps, core_ids=list(range(len(maps))), trace=trace)
    return res


def assemble(res):
    r = res.results
    yp = np.zeros((16, 256, D), np.float32)
    ys = np.zeros((4, T, D), np.float32)
    nC = np.zeros((16, NL, 2, 4, 64, 64), np.float32)
    nn = np.zeros((16, NL, 2, 4, 64), np.float32)
    nm = np.zeros((16, NL, 2, 4), np.float32)
    kv = {n: np.zeros((16, NL, 256, 2, 64), np.float32) for n in ("o_gk", "o_gv", "o_sk", "o_sv")}
    for core in range(8):
        y = np.ascontiguousarray(r[core]["yT"].T)
        if core < 4:
            ys[core] = y
            continue
        j = core - 4
        yp[4 * j:4 * j + 4] = y.reshape(4, 256, D)
        for n in ("o_gk", "o_sk"):
            a = r[core][n].reshape(NL, 2, 64, 4, 256)
            kv[n][4 * j:4 * j + 4] = a.transpose(3, 0, 4, 1, 2)
        for n in ("o_gv", "o_sv"):
            a = r[core][n].reshape(NL, 4, 256, 2, 64)
            kv[n][4 * j:4 * j + 4] = a.transpose(1, 0, 2, 3, 4)
        oc = r[core]["o_C"].reshape(NL, 2, 4, 2, 64, 2, 65)
        oc = oc.transpose(2, 0, 1, 5, 3, 4, 6).reshape(4, NL, 2, 4, 64, 65)
        nC[4 * j:4 * j + 4] = oc[..., 0:64]
        nn[4 * j:4 * j + 4] = oc[..., 64]
        nm[4 * j:4 * j + 4] = r[core]["o_m"].transpose(3, 0, 1, 2)
    return (yp, ys, nC, nn, nm, kv["o_gk"], kv["o_gv"], kv["o_sk"], kv["o_sv"])


def kernel(**inputs):
    res = run_device(inputs)
    return assemble(res)
```

```python
import os
import math
import numpy as np
import concourse.bass as bass
import concourse.mybir as mybir
from concourse.bass_utils import run_bass_kernel_spmd

F32 = mybir.dt.float32
BF16 = mybir.dt.bfloat16
I32 = mybir.dt.int32
AF = mybir.ActivationFunctionType
ALU = mybir.AluOpType
AX = mybir.AxisListType

D = 1024
T = 1024
DFF = 2816
NFF = 22
NL = 2
NIN = 2832
EPS = 1e-6
NEG = -30000.0
BIG = 29952.0
LN8 = math.log(0.125)
SLOT = 8 * 640
STAGE = int(os.environ.get("MK_STAGE", "99"))
DEBUG = bool(os.environ.get("MK_DEBUG"))
CUT = int(os.environ.get("MK_CUT", "99"))


class R:
    __slots__ = ("name", "w", "rd", "excl")

    def __init__(self, name="", excl=False):
        self.name = name
        self.w = None
        self.rd = []
        self.excl = excl


def RL(n, name=""):
    return [R("%s%d" % (name, i)) for i in range(n)]


class Sched:
    NLANES = {"sp": 8, "pool": 3, "poolw": 5}

    def __init__(self, nc):
        self.nc = nc
        self.eng = {"pe": nc.tensor, "act": nc.scalar, "dve": nc.vector,
                    "pool": nc.gpsimd, "sp": nc.sync}
        self.sem = {}
        self.cnt = {}
        for k in self.eng:
            self.sem[k] = nc.alloc_semaphore(name="s_" + k)
            self.cnt[k] = 0
        self.lanes = {}
        for q, n in self.NLANES.items():
            self.lanes[q] = []
            for i in range(n):
                key = "d_%s%d" % (q, i)
                self.sem[key] = nc.alloc_semaphore(name=key)
                self.cnt[key] = 0
                self.lanes[q].append(key)
        self.lane_rr = {q: 0 for q in self.NLANES}
        self.eng_of = {"sp": "sp", "pool": "pool", "poolw": "pool"}
        self.waited = {k: {} for k in self.eng}
        self.nwaits = 0
        self.nops = 0

    def _wait(self, e, key, val):
        if key == "pe" and e == "pe":
            return
        w = self.waited[e]
        if w.get(key, 0) >= val:
            return
        self.eng[e].wait_ge(self.sem[key], val)
        w[key] = val
        self.nwaits += 1

    def _deps(self, e, reads, writes):
        deps = {}
        for r in reads:
            if r.w is not None:
                k, v = r.w
                if deps.get(k, 0) < v:
                    deps[k] = v
            if r.excl:
                for (k, v) in r.rd:
                    if k != e and deps.get(k, 0) < v:
                        deps[k] = v
        for w in writes:
            if w.w is not None:
                k, v = w.w
                if deps.get(k, 0) < v:
                    deps[k] = v
            for (k, v) in w.rd:
                if deps.get(k, 0) < v:
                    deps[k] = v
        for k, v in deps.items():
            self._wait(e, k, v)

    def _commit(self, tok, reads, writes):
        for r in reads:
            r.rd.append(tok)
            if len(r.rd) > 48:
                mx = {}
                for k, v in r.rd:
                    if mx.get(k, 0) < v:
                        mx[k] = v
                r.rd = list(mx.items())
        for w in writes:
            w.w = tok
            w.rd = []

    def op(self, e, fn, reads=(), writes=()):
        self._deps(e, reads, writes)
        inst = fn(self.eng[e])
        self.cnt[e] += 1
        inst.then_inc(self.sem[e], 1)
        self._commit((e, self.cnt[e]), reads, writes)
        self.nops += 1

    def dma(self, q, out, in_, reads=(), writes=()):
        lanes = self.lanes[q]
        e = self.eng_of[q]
        key = lanes[self.lane_rr[q] % len(lanes)]
        self.lane_rr[q] += 1
        self._wait(e, key, self.cnt[key])
        self._deps(e, reads, writes)
        inst = self.eng[e].dma_start(out=out, in_=in_)
        self.cnt[key] += 16
        inst.then_inc(self.sem[key], 16)
        self._commit((key, self.cnt[key]), reads, writes)
        self.nops += 1

    def barrier(self):
        for e in self.eng:
            for k in self.sem:
                if self.cnt[k] > 0 and not k.startswith("d_poolw"):
                    self._wait(e, k, self.cnt[k])

    def finish(self):
        for k in self.sem:
            if self.cnt[k] > 0 and k != "sp":
                self._wait("sp", k, self.cnt[k])


def act(S, out, in_, func, reads, writes, bias=0.0, scale=1.0):
    S.op("act", lambda e: e.activation(out=out, in_=in_, func=func, bias=bias, scale=scale), reads, writes)


def tt(S, eng, out, a, b, op, reads, writes):
    S.op(eng, lambda e: e.tensor_tensor(out=out, in0=a, in1=b, op=op), reads, writes)


def ts(S, eng, out, a, s1, op0, reads, writes, s2=None, op1=None):
    if op1 is None:
        S.op(eng, lambda e: e.tensor_scalar(out=out, in0=a, scalar1=s1, scalar2=None, op0=op0), reads, writes)
    else:
        S.op(eng, lambda e: e.tensor_scalar(out=out, in0=a, scalar1=s1, scalar2=s2, op0=op0, op1=op1), reads, writes)


def cp(S, eng, out, in_, reads, writes):
    S.op(eng, lambda e: e.tensor_copy(out=out, in_=in_), reads, writes)


def recip(S, out, in_, reads, writes):
    S.op("dve", lambda e: e.reciprocal(out=out, in_=in_), reads, writes)


def mm(S, groups, reads, writes):
    def fn(e):
        inst = None
        for (o, l, r, st, sp_) in groups:
            inst = e.matmul(o, lhsT=l, rhs=r, start=st, stop=sp_)
        return inst
    S.op("pe", fn, reads, writes)


class Prog:
    def __init__(self):
        nc = bass.Bass("TRN2", target_bir_lowering=False)
        self.nc = nc
        self.S = Sched(nc)
        self.din = {}
        self.dout = {}
        self._build()

    def inp(self, name, shape):
        t = self.nc.dram_tensor(name, list(shape), F32, kind="ExternalInput").ap()
        self.din[name] = tuple(shape)
        return t

    def outp(self, name, shape):
        t = self.nc.dram_tensor(name, list(shape), F32, kind="ExternalOutput").ap()
        self.dout[name] = tuple(shape)
        return t

    def sb(self, name, shape, dt=F32):
        return self.nc.alloc_sbuf_tensor("sb_" + name, list(shape), dt)

    def arena_reset(self, to=0):
        self.aoff = to
        self.S.barrier()

    def carve(self, shape, dt=F32):
        n = 1
        for s in shape[1:]:
            n *= s
        nb = n * (4 if dt in (F32, I32) else 2)
        nb = (nb + 31) // 32 * 32
        off = self.aoff
        self.aoff += nb
        assert self.aoff <= self.ARENA_BYTES, (self.aoff, self.ARENA_BYTES)
        v = self.arena[:, off // 2:(off + nb) // 2]
        if dt != BF16:
            v = v.bitcast(dt)
        v = v[:, 0:n]
        if len(shape) == 3:
            v = v.rearrange("p (a b) -> p a b", a=shape[1])
        elif len(shape) == 4:
            v = v.rearrange("p (a b c) -> p a b c", a=shape[1], b=shape[2])
        return v

    def bank(self):
        return self.pbank()

    def prefetch(self, key, parts):
        self.pre[key] = self.wload(parts)

    def wload(self, parts, key=None):
        if key is not None and key in self.pre:
            return self.pre.pop(key)
        i = self.slot_rr % len(self.slots)
        self.slot_rr += 1
        sl, r = self.slots[i], self.slotr[i]
        for (dst_fn, src) in parts:
            self.S.dma("poolw", dst_fn(sl), src, writes=[r])
        return sl, r

    def slot_view(self, sl, kk):
        return sl[:, 0:kk * self.slot_w].rearrange("p (k n) -> p k n", k=kk)

    def _build(self):
        nc, S = self.nc, self.S
        inp, outp, sb = self.inp, self.outp, self.sb
        xT_d = inp("xT", [D, T])
        cvec_d = inp("cvec", [128, 8])
        w_ada = inp("w_ada", [NL, D, 9 * D])
        b_adaT = inp("b_adaT", [NL, 128, 72])
        gvec_d = inp("gvec", [128, NL * 24 + 8])
        wd = {}
        for nm, shp in (("w1_gate", [NL, D, DFF]), ("w1_up", [NL, D, DFF]), ("w1_down", [NL, DFF, D]),
                        ("w_in", [NL, D, NIN]), ("w_out", [NL, D, D]),
                        ("w2_gate", [NL, D, DFF]), ("w2_up", [NL, D, DFF]), ("w2_down", [NL, DFF, D])):
            wd[nm] = inp(nm, shp)
        yT_d = outp("yT", [D, T])

        xT = sb("xT", [128, 8, T], F32)
        hT = sb("hT", [128, 8, T], BF16)
        self.ARENA_BYTES = 84 * 1024
        self.arena = sb("arena", [128, self.ARENA_BYTES // 2], BF16)
        self.slot_w = 640
        self.slots = [sb("slot%d" % i, [128, SLOT], BF16) for i in range(4)]
        self.slotr = RL(4, "slot")
        self.slot_rr = 0
        self.pre = {}
        self.ps = [nc.alloc_psum_tensor("ps%d" % i, [128, 512], F32) for i in range(8)]
        self.psr = [R("ps%d" % i, excl=True) for i in range(8)]
        self.bank_rr = 0
        self.bank_pool = list(range(8))
        ones_bf = sb("ones_bf", [128, 128], BF16)
        cvec = sb("cvec_sb", [128, 8], F32)
        sc_bf = sb("sc_bf", [128, 8], BF16)
        modT = sb("modT", [128, NL, 72], F32)
        badaT = sb("badaT", [128, NL, 72], F32)
        gvec = sb("gvec_sb", [128, NL * 24 + 8], F32)
        Acoef = sb("Acoef", [128, NL, 3, 8], F32)
        Gcoef = sb("Gcoef", [128, NL, 3, 8], F32)
        sqb = sb("sqb", [128, 2, 512], BF16)
        f32s = sb("f32s", [128, 3, 512], F32)
        rstd = sb("rstd", [128, 512], F32)
        r_ones, r_cvec, r_sc, r_gvec, r_rstd = R("ones"), R("cvec"), R("sc"), R("gvec"), R("rstd")
        r_mod = RL(NL, "mod")
        r_bada = R("bada")
        r_coef = RL(NL, "coef")
        r_sq = RL(2, "sq")
        r_f32s = RL(3, "f32s")
        xr = [[R("x%d_%d" % (k, c)) for c in range(2)] for k in range(8)]
        hr = [[R("h%d_%d" % (k, c)) for c in range(2)] for k in range(8)]
        self.sq_rr = 0
        self.f32_rr = 0

        def CH(c):
            return slice(c * 512, (c + 1) * 512)

        xv = xT_d.rearrange("(k p) t -> p k t", p=128)
        for k in range(8):
            S.dma("sp", xT[:, k, :], xv[:, k, :], writes=[xr[k][0], xr[k][1]])
        S.dma("sp", cvec[:], cvec_d, writes=[r_cvec])
        S.dma("sp", gvec[:], gvec_d, writes=[r_gvec])
        for l in range(NL):
            S.dma("sp", badaT[:, l, :], b_adaT[l], writes=[r_bada])
        S.op("dve", lambda e: e.memset(ones_bf[:], 1.0), writes=[r_ones])
        act(S, sc_bf[:], cvec[:], AF.Silu, [r_cvec], [r_sc])

        mslots = [sb("mslot%d" % i, [128, 8, 256], BF16) for i in range(2)]
        r_ms = RL(2, "mslot")
        r_modls = [[R("mod%d_%d" % (l, s_)) for s_ in range(3)] for l in range(NL)]
        jobs = [(l, q) for l in range(NL) for q in range(36)]
        st = {"dma": 0, "mm": 0, "fin": set()}

        def mod_dma(j):
            l, q = jobs[j]
            wv = w_ada[l].rearrange("(k p) n -> p k n", p=128)
            S.dma("poolw", mslots[j % 2][:], wv[:, :, q * 256:(q + 1) * 256], writes=[r_ms[j % 2]])

        def mod_mm(j):
            l, q = jobs[j]
            pb, pr = self.bank()
            groups = []
            for jj in range(2):
                for k in range(8):
                    groups.append((pb[:, jj:jj + 1], mslots[j % 2][:, k, jj * 128:(jj + 1) * 128], sc_bf[:, k:k + 1],
                                   k == 0, k == 7))
            mm(S, groups, [r_ms[j % 2], r_sc], [pr])
            s_ = q // 12
            tt(S, "dve", modT[:, l, q * 2:q * 2 + 2], pb[:, 0:2], badaT[:, l, q * 2:q * 2 + 2], ALU.add,
               [pr, r_bada], [r_modls[l][s_]])

        def mod_pump(n=1):
            for _ in range(n):
                if st["dma"] < len(jobs) and st["dma"] - st["mm"] < 2:
                    mod_dma(st["dma"])
                    st["dma"] += 1
                if st["mm"] < st["dma"] - 1 or (st["dma"] == len(jobs) and st["mm"] < st["dma"]):
                    mod_mm(st["mm"])
                    st["mm"] += 1

        def mod_require(l, s_):
            last = l * 36 + s_ * 12 + 11
            while st["mm"] <= last:
                mod_pump()
            if (l, s_) in st["fin"]:
                return
            st["fin"].add((l, s_))
            r = r_modls[l][s_]
            ts(S, "dve", Acoef[:, l, s_, :], modT[:, l, (3 * s_ + 1) * 8:(3 * s_ + 2) * 8], 1.0, ALU.add, [r], [r])
            tt(S, "dve", Acoef[:, l, s_, :], Acoef[:, l, s_, :], gvec[:, l * 24 + s_ * 8:l * 24 + s_ * 8 + 8],
               ALU.mult, [r, r_gvec], [r])
            ts(S, "dve", Gcoef[:, l, s_, :], modT[:, l, (3 * s_ + 2) * 8:(3 * s_ + 3) * 8],
               0.5 if s_ != 1 else 1.0, ALU.mult, [r], [r])

        self.mod_pump = mod_pump
        self.mod_require = mod_require

        def rms_rstd(c, src_fn, src_regs, nk, inv_n, ones_l):
            pb, pr = self.bank()
            for k in range(nk):
                i = self.sq_rr % 2
                self.sq_rr += 1
                act(S, sqb[:, i, :], src_fn(k), AF.Square, [src_regs[k]], [r_sq[i]])
                mm(S, [(pb[:], ones_l, sqb[:, i, :], k == 0, k == nk - 1)], [r_sq[i], r_ones], [pr])
            act(S, rstd[:], pb[:], AF.Ln, [pr], [r_rstd], bias=EPS, scale=inv_n)
            act(S, rstd[:], rstd[:], AF.Exp, [r_rstd], [r_rstd], scale=-0.5)

        def norm_mod(l, s):
            for c in range(2):
                rms_rstd(c, lambda k: xT[:, k, CH(c)], [xr[k][c] for k in range(8)], 8, 1.0 / D, ones_bf[:])
                for k in range(8):
                    i = self.f32_rr % 3
                    self.f32_rr += 1
                    tt(S, "dve", f32s[:, i, :], xT[:, k, CH(c)], rstd[:], ALU.mult,
                       [xr[k][c], r_rstd], [r_f32s[i]])
                    act(S, hT[:, k, CH(c)], f32s[:, i, :], AF.Identity, [r_f32s[i], r_modls[l][s]],
                        [hr[k][c]], bias=modT[:, l, 3 * s * 8 + k:3 * s * 8 + k + 1],
                        scale=Acoef[:, l, s, k:k + 1])

        def resid_add(l, s, dt_, c, pb, pr):
            i = self.f32_rr % 3
            self.f32_rr += 1
            act(S, f32s[:, i, :], pb[:], AF.Copy, [pr, r_modls[l][s]], [r_f32s[i]], scale=Gcoef[:, l, s, dt_:dt_ + 1])
            tt(S, "dve", xT[:, dt_, CH(c)], xT[:, dt_, CH(c)], f32s[:, i, :], ALU.add,
               [xr[dt_][c], r_f32s[i]], [xr[dt_][c]])

        def ffn(l, s, wg, wu, wdn):
            self.arena_reset()
            aT = self.carve([128, NFF, T], BF16)
            ar = [[R("a%d_%d" % (j, c)) for c in range(2)] for j in range(NFF)]
            sg = [self.carve([128, 512], F32) for _ in range(3)]
            r_sg = RL(3, "sg")
            sg_rr = 0
            mod_require(l, s)
            norm_mod(l, s)
            wgv = wg[l].rearrange("(k p) n -> p k n", p=128)
            wuv = wu[l].rearrange("(k p) n -> p k n", p=128)
            for g in range(NFF // 2):
                c0 = g * 256
                sl, sr = self.wload(self.parts_gu(wg, wu, l, g), key=("gu", l, s, g))
                s3 = self.slot_view(sl, 8)
                if l == 0:
                    mod_pump(1)
                for jj in range(2):
                    j = g * 2 + jj
                    for c in range(2):
                        pg, prg = self.bank()
                        pu, pru = self.bank()
                        groups = []
                        for k in range(8):
                            groups.append((pg[:], s3[:, k, jj * 128:(jj + 1) * 128], hT[:, k, CH(c)], k == 0, k == 7))
                        for k in range(8):
                            groups.append((pu[:], s3[:, k, 256 + jj * 128:256 + (jj + 1) * 128], hT[:, k, CH(c)],
                                           k == 0, k == 7))
                        mm(S, groups, [sr] + [hr[k][c] for k in range(8)], [prg, pru])
                        i = sg_rr % 3
                        sg_rr += 1
                        act(S, sg[i], pg[:], AF.Silu, [prg], [r_sg[i]])
                        tt(S, "dve", aT[:, j, CH(c)], sg[i], pu[:], ALU.mult, [r_sg[i], pru], [ar[j][c]])
            wdv = wdn[l].rearrange("(j p) n -> p j n", p=128)
            for dt_ in range(8):
                sl, sr = self.wload([(lambda sl_: sl_[:, 0:NFF * 128].rearrange("p (j n) -> p j n", j=NFF),
                                      wdv[:, :, dt_ * 128:(dt_ + 1) * 128])])
                if dt_ == 7:
                    if s == 0:
                        self.prefetch(("A1", l), self.parts_win(l, 0, 512))
                        self.prefetch(("A2", l), self.parts_win(l, 512, 512))
                        self.prefetch(("G", l), self.parts_win(l, 1024, 16))
                    elif l + 1 < NL:
                        for g_ in range(2):
                            self.prefetch(("gu", l + 1, 0, g_), self.parts_gu(wd["w1_gate"], wd["w1_up"], l + 1, g_))
                s3 = sl[:, 0:NFF * 128].rearrange("p (j n) -> p j n", j=NFF)
                for c in range(2):
                    pb, pr = self.bank()
                    groups = [(pb[:], s3[:, j, :], aT[:, j, CH(c)], j == 0, j == NFF - 1) for j in range(NFF)]
                    mm(S, groups, [sr] + [ar[j][c] for j in range(NFF)], [pr])
                    resid_add(l, s, dt_, c, pb, pr)

        self.wd = wd
        self.ctx = dict(xT=xT, hT=hT, xr=xr, hr=hr, CH=CH, rms_rstd=rms_rstd, norm_mod=norm_mod,
                        resid_add=resid_add, ones_bf=ones_bf, r_ones=r_ones, gvec=gvec, r_gvec=r_gvec,
                        f32s=f32s, r_f32s=r_f32s, rstd=rstd, r_rstd=r_rstd, modT=modT, sqb=sqb, r_sq=r_sq)


        self.ARENA_BYTES = 84 * 1024
        LPW = 304
        self.LP = dict(BG=0, GM=16, GQK=272, SK=274, CV=278, HB=302)
        d = {}
        d["lp"] = inp("lp", [128, NL, LPW])
        d["cflags"] = inp("cflags", [128, 4])
        d["ident"] = inp("ident", [128, 128])
        d["blockones"] = inp("blockones", [128, 128])
        d["tri"] = inp("tri", [128, 4, 128])
        d["ropeP"] = inp("ropeP", [128, 128])
        d["ropeCS"] = inp("ropeCS", [128, 2, T])
        d["swamask"] = inp("swamask", [128, 8, 384])
        d["gmAB"] = inp("gmAB", [5, 2, T])
        d["fw1"] = inp("fw1", [33, NL, 64])
        d["fw2"] = inp("fw2", [64, NL, 64])
        d["fw3"] = inp("fw3", [64, NL, 256])
        d["fvec"] = inp("fvec", [64, NL, 3])
        d["fb3"] = inp("fb3", [1, NL, 256])
        d["featsT"] = inp("featsT", [33, T])
        d["window"] = inp("window", [128, 8, 256])
        d["dftF"] = inp("dftF", [2, T, T])
        d["dftG"] = inp("dftG", [2, T, T])
        d["sel"] = inp("sel", [4, 130])
        d["C0"] = inp("C0", [NL, 2, 128, 2, 65])
        d["m0rep"] = inp("m0rep", [128, NL, 2, 2])
        d["m0c"] = inp("m0c", [4, NL * 2])
        d["gkT"] = inp("gkT", [NL, 2, 128, 256])
        d["gvp"] = inp("gvp", [NL, 2, 128, 2, 192])
        d["skT"] = inp("skT", [NL, 2, 128, 256])
        d["svp"] = inp("svp", [NL, 2, 128, 2, 192])
        d["o_gk"] = outp("o_gk", [NL, 128, T])
        d["o_gv"] = outp("o_gv", [NL, T, 128])
        d["o_sk"] = outp("o_sk", [NL, 128, T])
        d["o_sv"] = outp("o_sv", [NL, T, 128])
        d["o_C"] = outp("o_C", [NL, 2, 4, 128, 2, 65])
        d["o_m"] = outp("o_m", [NL, 2, 4, 4])
        self.d = d
        if DEBUG:
            self.dbg_ymix = outp("dbg_ymix", [NL, 128, 8, T])
        k = {}
        k["lp"] = sb("lp", [128, NL, LPW]); k["cflags"] = sb("cflags", [128, 4])
        k["ident_f"] = sb("ident_f", [128, 128]); k["ident_bf"] = sb("ident_bf", [128, 128], BF16)
        k["blockones"] = sb("blockones", [128, 128], BF16)
        k["tri"] = sb("tri", [128, 4, 128]); k["ropeP"] = sb("ropeP", [128, 128])
        k["ones_f"] = sb("ones_f", [128, 128])
        k["fw1"] = sb("fw1", [33, NL, 64]); k["fw2"] = sb("fw2", [64, NL, 64]); k["fw3"] = sb("fw3", [64, NL, 256])
        k["fvec"] = sb("fvec", [64, NL, 3]); k["fb3"] = sb("fb3", [1, NL, 256]); k["fs"] = sb("fs", [64, NL, 4])
        k["sel"] = sb("sel", [4, 130]); k["m0rep"] = sb("m0rep", [128, NL, 2, 2]); k["m0c"] = sb("m0c", [4, NL * 2])
        k["Clo"] = sb("Clo", [128, 2, 2, 65]); k["Chi"] = sb("Chi", [128, 2, 2, 65])
        k["Cblo"] = sb("Cblo", [128, 2, 2, 66], BF16); k["Cbhi"] = sb("Cbhi", [128, 2, 2, 66], BF16)
        k["cvb"] = sb("cvb", [128, NL, 6, 2])
        self.k = k
        r_k = R("consts")
        self.r_k = r_k
        for nm in ("lp", "cflags", "tri", "ropeP", "fw1", "fw2", "fw3", "fvec", "fb3", "sel", "m0rep", "m0c"):
            S.dma("sp", k[nm][:], d[nm], writes=[r_k])
        S.dma("sp", k["ident_f"][:], d["ident"], writes=[r_k])
        S.dma("pool", k["ident_bf"][:], d["ident"], writes=[r_k])
        S.dma("pool", k["blockones"][:], d["blockones"], writes=[r_k])
        S.op("pool", lambda e: e.memset(k["ones_f"][:], 1.0), writes=[r_k])
        for nm in ("Clo", "Chi", "Cblo", "Cbhi"):
            S.op("pool", lambda e, nm=nm: e.memset(k[nm][:], 0.0), writes=[r_k])
        i2p = float(1.0 / (2 * math.pi))
        ts(S, "dve", k["fs"][:, :, 0:1], k["fvec"][:, :, 2:3], i2p, ALU.mult, [r_k], [r_k])
        tt(S, "dve", k["fs"][:, :, 1:2], k["fs"][:, :, 0:1], k["fvec"][:, :, 0:1], ALU.mult, [r_k], [r_k])
        tt(S, "dve", k["fs"][:, :, 2:3], k["fs"][:, :, 0:1], k["fvec"][:, :, 1:2], ALU.mult, [r_k], [r_k])
        CV = self.LP["CV"]
        for l in range(NL):
            cvv = k["lp"][:, l, CV:CV + 24].rearrange("p (a b) -> p a b", a=6)
            for (j, col) in ((0, 0), (1, 2)):
                ts(S, "dve", k["cvb"][:, l, :, j:j + 1], cvv[:, :, col:col + 1], k["cflags"][:, 2:3], ALU.mult,
                   [r_k], [r_k], s2=-1.0, op1=ALU.mult)

        for l in range(NL):
            ffn(l, 0, wd["w1_gate"], wd["w1_up"], wd["w1_down"])
            self.mixer(l)
            ffn(l, 2, wd["w2_gate"], wd["w2_up"], wd["w2_down"])

        gfo = NL * 24
        yv = yT_d.rearrange("(k p) t -> p k t", p=128)
        self.arena_reset()
        ost_ = self.carve([128, 2, 512], F32)
        ost = [ost_[:, 0, :], ost_[:, 1, :]]
        r_ost = RL(2, "ost")
        o_rr = 0
        for c in range(2):
            rms_rstd(c, lambda k: xT[:, k, CH(c)], [xr[k][c] for k in range(8)], 8, 1.0 / D, ones_bf[:])
            for k in range(8):
                i = self.f32_rr % 3
                self.f32_rr += 1
                tt(S, "dve", f32s[:, i, :], xT[:, k, CH(c)], rstd[:], ALU.mult, [xr[k][c], r_rstd], [r_f32s[i]])
                o = o_rr % 2
                o_rr += 1
                act(S, ost[o], f32s[:, i, :], AF.Copy, [r_f32s[i], r_gvec], [r_ost[o]],
                    scale=gvec[:, gfo + k:gfo + k + 1])
                S.dma("sp", yv[:, k, CH(c)], ost[o], reads=[r_ost[o]])
        S.finish()

    def mixer(self, l):
        S = self.S
        C = self.ctx
        hT, hr, CH = C["hT"], C["hr"], C["CH"]
        self.arena_reset()
        ymix = self.carve([128, 8, T], BF16)
        ymr = [[R("ym%d_%d" % (k, c)) for c in range(2)] for k in range(8)]
        base = self.aoff
        self.mod_require(l, 1)
        C["norm_mod"](l, 1)
        win = self.wd["w_in"][l].rearrange("(k p) n -> p k n", p=128)
        self.bank_pool = list(range(8))
        self.mix_mlstm(l, ymix, ymr, win)
        self.arena_reset(base)
        if STAGE >= 3:
            self.mix_attn(l, ymix, ymr, win, glob=True)
            self.arena_reset(base)
            self.mix_attn(l, ymix, ymr, win, glob=False)
            self.arena_reset(base)
        if STAGE >= 4:
            self.mix_hyena(l, ymix, ymr, win)
        self.prefetch(("wout", l, 0), self.parts_wout(l, 0))
        self.prefetch(("wout", l, 1), self.parts_wout(l, 1))
        wd_ = self.wd
        for g in range(2):
            self.prefetch(("gu", l, 2, g), self.parts_gu(wd_["w2_gate"], wd_["w2_up"], l, g))
        self.bank_pool = list(range(8))
        if DEBUG:
            f32s, r_f32s = C["f32s"], C["r_f32s"]
            for k in range(8):
                for c in range(2):
                    i = self.f32_rr % 3
                    self.f32_rr += 1
                    act(S, f32s[:, i, :], ymix[:, k, CH(c)], AF.Copy, [ymr[k][c]], [r_f32s[i]])
                    S.dma("sp", self.dbg_ymix[l, :, k, c * 512:(c + 1) * 512], f32s[:, i, :], reads=[r_f32s[i]])
        wov = self.wd["w_out"][l].rearrange("(k p) n -> p k n", p=128)
        for half in range(2):
            sl, sr = self.wload(self.parts_wout(l, half), key=("wout", l, half))
            s3 = self.slot_view(sl, 8)
            for j in range(4):
                dt_ = half * 4 + j
                for c in range(2):
                    pb, pr = self.bank()
                    groups = [(pb[:], s3[:, k, j * 128:(j + 1) * 128], ymix[:, k, CH(c)], k == 0, k == 7) for k in range(8)]
                    mm(S, groups, [sr] + [ymr[k][c] for k in range(8)], [pr])
                    C["resid_add"](l, 1, dt_, c, pb, pr)

    def parts_attn(self, l, glob):
        win = self.wd["w_in"][l].rearrange("(k p) n -> p k n", p=128)
        sv8 = lambda sl_: self.slot_view(sl_, 8)
        q0 = 1040 if glob else 1552
        k0, v0 = q0 + 256, q0 + 384
        return [(lambda sl_: sv8(sl_)[:, :, 0:256], win[:, :, q0:q0 + 256]),
                (lambda sl_: sv8(sl_)[:, :, 256:384], win[:, :, k0:k0 + 128]),
                (lambda sl_: sv8(sl_)[:, :, 384:448], win[:, :, k0 + 64:k0 + 128]),
                (lambda sl_: sv8(sl_)[:, :, 448:512], win[:, :, k0:k0 + 64]),
                (lambda sl_: sv8(sl_)[:, :, 512:640], win[:, :, v0:v0 + 128])]

    def parts_win(self, l, c0, n):
        win = self.wd["w_in"][l].rearrange("(k p) n -> p k n", p=128)
        return [(lambda sl_: self.slot_view(sl_, 8)[:, :, 0:n], win[:, :, c0:c0 + n])]

    def parts_wout(self, l, half):
        wov = self.wd["w_out"][l].rearrange("(k p) n -> p k n", p=128)
        return [(lambda sl_: self.slot_view(sl_, 8)[:, :, 0:512], wov[:, :, half * 512:(half + 1) * 512])]

    def parts_gu(self, wg, wu, l, g):
        wgv = wg[l].rearrange("(k p) n -> p k n", p=128)
        wuv = wu[l].rearrange("(k p) n -> p k n", p=128)
        c0 = g * 256
        return [(lambda sl_: self.slot_view(sl_, 8)[:, :, 0:256], wgv[:, :, c0:c0 + 256]),
                (lambda sl_: self.slot_view(sl_, 8)[:, :, 256:512], wuv[:, :, c0:c0 + 256])]

    def pbank(self):
        i = self.bank_pool[self.bank_rr % len(self.bank_pool)]
        self.bank_rr += 1
        return self.ps[i], self.psr[i]

    def proj_fm(self, c, s3, col0, pb):
        hT, CH = self.ctx["hT"], self.ctx["CH"]
        return [(pb[:], s3[:, k, col0:col0 + 128], hT[:, k, CH(c)], k == 0, k == 7) for k in range(8)]

    def proj_tok(self, tt_, s3, col0, ncols, pb, pc0):
        hT = self.ctx["hT"]
        return [(pb[:, pc0:pc0 + ncols], hT[:, k, tt_ * 128:(tt_ + 1) * 128], s3[:, k, col0:col0 + ncols], k == 0, k == 7)
                for k in range(8)]

    def mix_mlstm(self, l, ymix, ymr, win):
        S, k_, d = self.S, self.k, self.d
        C = self.ctx
        hr, CH = C["hr"], C["CH"]
        LP = self.LP
        lp = k_["lp"]
        r_k = self.r_k
        hall = [hr[k][c] for k in range(8) for c in range(2)]
        sv8 = lambda sl_: self.slot_view(sl_, 8)
        slA1, srA1 = self.wload(self.parts_win(l, 0, 512), key=("A1", l))
        slA2, srA2 = self.wload(self.parts_win(l, 512, 512), key=("A2", l))
        slG, srG = self.wload(self.parts_win(l, 1024, 16), key=("G", l))
        sA1, sA2, sG = sv8(slA1), sv8(slA2), sv8(slG)
        self.prefetch(("attn", l, True), self.parts_attn(l, True))
        if CUT == 1:
            return
        aqT = self.carve([128, 2, T], BF16)
        akp = self.carve([128, 4, T], BF16)
        ktok = self.carve([128, 8, 256], BF16)
        vaug = self.carve([128, 8, 4, 66], BF16)
        sgo = self.carve([128, 8, 256], BF16)
        gts = self.carve([128, 8, 16], F32)
        lf = self.carve([128, 8, 8], F32)
        cum = self.carve([128, 8, 16], F32)
        call = self.carve([128, 8, 8], F32)
        wall = self.carve([128, 8, 8], F32)
        wkl = self.carve([128, 8, 8], F32)
        wkall = self.carve([128, 8, 8], F32)
        wkm = self.carve([128, 8, 8], F32)
        dec = self.carve([128, 8, 4], F32)
        lfB = self.carve([128, 8, 128], F32)
        E = self.carve([128, 8, 128], F32)
        AT = self.carve([128, 2, 4, 128], BF16)
        hf = self.carve([128, 8, 256], F32)
        numt = self.carve([128, 2, 260], F32)
        tmpn = self.carve([128, 2, 260], F32)
        h64 = self.carve([128, 2, 256], F32)
        dsm = self.carve([128, 2, 8], F32)
        kwp = self.carve([128, 2, 4, 192], BF16)
        yatok = self.carve([128, 8, 256], BF16)
        snap = self.carve([128, 2, 4, 130], F32)
        e0 = self.carve([128, 2, 2], F32)
        ssall = self.carve([128, 8, 4], F32)
        sq1 = self.carve([128, 2, 256], F32)
        mst = self.carve([128, 64], F32)
        scl = self.carve([128, 2, 8], F32)
        r_aq, r_akp = RL(2, "aq"), RL(2, "akp")
        r_ktok, r_vaug, r_sgo, r_gts = RL(8, "ktok"), RL(8, "vaug"), RL(8, "sgo"), R("gts")
        r_gate = R("gate")
        r_lfB, r_E, r_AT = RL(2, "lfB"), RL(2, "E"), RL(2, "AT")
        r_hf = RL(8, "hf")
        r_num, r_tmpn, r_h64, r_dsm, r_kwp = RL(2, "num"), RL(2, "tmpn"), RL(2, "h64"), RL(2, "dsm"), RL(2, "kwp")
        r_C, r_Cb = RL(2, "C"), RL(2, "Cb")
        r_snap = [[R("snap") for _ in range(4)] for _ in range(2)]
        r_ms, r_scl = R("mst"), R("scl")
        r_ya = RL(8, "ya")
        r_ss = R("ss")
        r_sq1 = RL(2, "sq1")
        S.op("pool", lambda e: e.memset(akp, 0.0), writes=r_akp)
        S.op("pool", lambda e: e.memset(kwp, 0.0), writes=r_kwp)
        S.op("pool", lambda e: e.memset(vaug[:, :, :, 64:65], 1.0), writes=r_vaug)
        S.op("pool", lambda e: e.memset(hf, 0.0), writes=r_hf)
        if CUT == 2:
            return
        BG = LP["BG"]
        for t_ in range(8):
            p1, pr1 = self.pbank()
            p2, pr2 = self.pbank()
            g = self.proj_tok(t_, sA1, 256, 256, p1, 0) + self.proj_tok(t_, sA2, 0, 256, p1, 256)
            g += self.proj_tok(t_, sA2, 256, 256, p2, 0) + self.proj_tok(t_, sG, 0, 16, p2, 256)
            mm(S, g, [srA1, srA2, srG] + hall, [pr1, pr2])
            act(S, ktok[:, t_, :], p1[:, 0:256], AF.Copy, [pr1], [r_ktok[t_]])
            cp(S, "dve", vaug[:, t_, :, 0:64], p1[:, 256:512].rearrange("p (a b) -> p a b", a=4), [pr1], [r_vaug[t_]])
            act(S, sgo[:, t_, :], p2[:, 0:256], AF.Sigmoid, [pr2], [r_sgo[t_]])
            tt(S, "dve", gts[:, t_, :], p2[:, 256:272], lp[:, l, BG:BG + 16], ALU.add, [pr2, r_k], [r_gts])
        ai, af = gts[:, :, 0:8], gts[:, :, 8:16]
        tri = k_["tri"]
        bc, bt = cum[:, :, 0:8], cum[:, :, 8:16]
        ident_f = k_["ident_f"]
        Clo, Chi, Cblo, Cbhi = k_["Clo"], k_["Chi"], k_["Cblo"], k_["Cbhi"]
        halves = ((slice(0, 64), Clo, Cblo), (slice(64, 128), Chi, Cbhi))

        def fm_chain():
            for t2 in range(2):
                for c in range(2):
                    pb, pr = self.pbank()
                    mm(S, self.proj_fm(c, sA1, t2 * 128, pb), [srA1] + hall, [pr])
                    yield
                    act(S, aqT[:, t2, CH(c)], pb[:], AF.Copy, [pr], [r_aq[c]])
                    yield
            for t2 in range(2):
                for c in range(2):
                    pb, pr = self.pbank()
                    mm(S, self.proj_fm(c, sA1, 256 + t2 * 128, pb), [srA1] + hall, [pr])
                    yield
                    act(S, akp[0:64, 2 * t2, CH(c)], pb[0:64, :], AF.Copy, [pr], [r_akp[c]])
                    yield
                    cp(S, "dve", akp[64:128, 2 * t2 + 1, CH(c)], pb[64:128, :], [pr], [r_akp[c]])
                    yield

        def gate_chain():
            act(S, lf, af, AF.Exp, [r_gts], [r_gate], scale=-1.0)
            yield
            act(S, lf, lf, AF.Ln, [r_gate], [r_gate], bias=1.0)
            yield
            ts(S, "dve", lf, lf, -1.0, ALU.mult, [r_gate], [r_gate])
            yield
            pbc, prc = self.pbank()
            g = []
            for t_ in range(8):
                g.append((pbc[:, t_ * 16:t_ * 16 + 4], tri[:, 0, :], lf[:, t_, 0:4], True, True))
                g.append((pbc[:, t_ * 16 + 4:t_ * 16 + 8], tri[:, 1, :], lf[:, t_, 4:8], True, True))
                g.append((pbc[:, t_ * 16 + 8:t_ * 16 + 16], k_["ones_f"][:], lf[:, t_, 0:8], True, True))
            mm(S, g, [r_gate, r_k], [prc])
            yield
            cp(S, "dve", cum, pbc[:, 0:128].rearrange("p (a b) -> p a b", a=8), [prc], [r_gate])
            yield
            tt(S, "dve", call, ai, bc, ALU.subtract, [r_gts, r_gate], [r_gate])
            yield
            ts(S, "dve", call, call, LN8, ALU.add, [r_gate], [r_gate])
            yield
            act(S, wall, bc, AF.Exp, [r_gate], [r_gate])
            yield
            tt(S, "dve", wkl, call, bt, ALU.add, [r_gate], [r_gate])
            yield
            act(S, wkall, wkl, AF.Exp, [r_gate], [r_gate])
            yield
            ts(S, "dve", wkm, wkl, -LN8, ALU.add, [r_gate], [r_gate])
            yield
            for g_ in range(2):
                rows = slice(g_ * 64, (g_ + 1) * 64)
                act(S, dec[rows, :, :], cum[rows, :, 8 + g_:16:2], AF.Exp, [r_gate], [r_gate])
                yield
            for dr in range(2):
                pbm, prm = self.pbank()
                g = [(pbm[0:4, t_:t_ + 1], lf[:, t_, dr * 4:dr * 4 + 4], k_["ones_f"][:, 0:1], True, True) for t_ in range(8)]
                mm(S, g, [r_gate, r_k], [prm])
                yield
                cp(S, "dve", mst[0:4, dr * 8:dr * 8 + 8], pbm[0:4, 0:8], [prm], [r_ms])
                yield
                for hh in range(2):
                    pbt, prt = self.pbank()
                    for q in range(4):
                        t_ = hh * 4 + q
                        S.op("pe", lambda e, t_=t_, q=q, pbt=pbt: e.transpose(pbt[0:4, q * 128:(q + 1) * 128],
                                                                              wkm[:, t_, dr * 4:dr * 4 + 4], ident_f[:]),
                             [r_gate, r_k], [prt])
                        yield
                    S.op("dve", lambda e, pbt=pbt, hh=hh: e.tensor_reduce(
                        out=mst[0:4, 16 + dr * 8 + hh * 4:16 + dr * 8 + hh * 4 + 4],
                        in_=pbt[0:4, :].rearrange("p (a b) -> p a b", a=4), axis=AX.X, op=ALU.max), [prt], [r_ms])
                    yield
                bv_ = mst[0:4, dr * 8:dr * 8 + 8].rearrange("p (a b) -> p a b", a=4)
                av_ = mst[0:4, 16 + dr * 8:16 + dr * 8 + 8].rearrange("p (a b) -> p a b", a=4)
                fi, se = (0, 1) if dr == 0 else (1, 0)
                mf = mst[0:4, 32 + dr * 4:32 + dr * 4 + 4]
                ts(S, "dve", mf, bv_[:, :, fi], k_["m0c"][0:4, l * 2 + dr:l * 2 + dr + 1], ALU.add, [r_ms, r_k], [r_ms])
                yield
                tt(S, "dve", mf, mf, av_[:, :, fi], ALU.max, [r_ms], [r_ms])
                yield
                tt(S, "dve", mf, mf, bv_[:, :, se], ALU.add, [r_ms], [r_ms])
                yield
                tt(S, "dve", mf, mf, av_[:, :, se], ALU.max, [r_ms], [r_ms])
                yield
                S.dma("sp", d["o_m"][l, dr], mf, reads=[r_ms])
                yield
                en = mst[0:4, 40 + dr * 4:40 + dr * 4 + 4]
                act(S, en, mf, AF.Exp, [r_ms], [r_ms], scale=-1.0)
                yield
                rhs2 = mst[0:4, 48 + dr * 8:48 + dr * 8 + 8]
                tt(S, "dve", rhs2.rearrange("p (a b) -> p a b", a=4), en.unsqueeze(2).to_broadcast([4, 4, 2]),
                   k_["sel"][0:4, 128:130].unsqueeze(1).to_broadcast([4, 4, 2]), ALU.mult, [r_ms, r_k], [r_ms])
                yield
                pbs, prs = self.pbank()
                mm(S, [(pbs[:, 0:8], k_["sel"][0:4, 0:128], rhs2, True, True)], [r_ms, r_k], [prs])
                yield
                cp(S, "dve", scl[:, dr, :], pbs[:, 0:8], [prs], [r_scl])
                yield
            act(S, e0, k_["m0rep"][:, l, :, :], AF.Exp, [r_k], [r_gate])
            yield
            for dr in range(2):
                S.dma("sp", Clo[:, dr, :, :], d["C0"][l, dr, :, :, :], writes=[r_C[dr]])
                yield
                tt(S, "dve", Clo[:, dr, :, :], Clo[:, dr, :, :], e0[:, dr, :].unsqueeze(2).to_broadcast([128, 2, 65]),
                   ALU.mult, [r_C[dr], r_gate], [r_C[dr]])
                yield
                for (rows, Cx, Cbx) in halves:
                    act(S, Cbx[rows, dr, :, 0:65], Clo[rows, dr, :, :], AF.Copy, [r_C[dr]], [r_Cb[dr]])
                    yield

        gens = [fm_chain(), gate_chain()]
        while gens:
            for g__ in list(gens):
                try:
                    next(g__)
                except StopIteration:
                    gens.remove(g__)
        keep = k_["cflags"][:, 1:2]

        def process(dr, t_):
            tok = slice(t_ * 128, (t_ + 1) * 128)
            chs = slice(dr * 4, dr * 4 + 4)
            cp(S, "pool", lfB[:, chs, :], lf[:, t_, chs].unsqueeze(2).to_broadcast([128, 4, 128]), [r_gate], [r_lfB[dr]])
            yield
            pbe, pre = self.pbank()
            g = []
            for h in range(4):
                o = pbe[:, h * 128:(h + 1) * 128]
                g.append((o, lfB[:, dr * 4 + h, :], tri[:, dr, :], True, False))
                g.append((o, ident_f[:], tri[:, 2 + dr, :], False, True))
            mm(S, g, [r_lfB[dr], r_k], [pre])
            yield
            pbs_, prs_ = self.pbank()
            g = [(pbs_[:, h * 128:(h + 1) * 128], akp[:, h, tok], aqT[:, h // 2, tok], True, True) for h in range(4)]
            mm(S, g, r_akp + r_aq, [prs_])
            yield
            for h in range(4):
                act(S, E[:, dr * 4 + h, :], pbe[:, h * 128:(h + 1) * 128], AF.Exp, [pre, r_gate], [r_E[dr]],
                    bias=call[:, t_, dr * 4 + h:dr * 4 + h + 1])
                yield
            tt(S, "dve", AT[:, dr, :, :], pbs_[:].rearrange("p (a b) -> p a b", a=4), E[:, chs, :], ALU.mult,
               [prs_, r_E[dr]], [r_AT[dr]])
            yield
            pbi, pri = self.pbank()
            g = [(pbi[:, h * 65:(h + 1) * 65], AT[:, dr, h, :], vaug[:, t_, h, 0:65], True, True) for h in range(4)]
            mm(S, g, [r_AT[dr], r_vaug[t_]], [pri])
            yield
            pbx, prx = self.pbank()
            g = [(pbx[:, h * 65:(h + 1) * 65], aqT[:, h // 2, tok], (Cblo if h % 2 == 0 else Cbhi)[:, dr, h // 2, 0:65], True, True)
                 for h in range(4)]
            mm(S, g, r_aq + [r_Cb[dr]], [prx])
            yield
            v3 = lambda ap: ap.rearrange("p (a b) -> p a b", a=4)
            tt(S, "dve", v3(tmpn[:, dr, :]), v3(pbx[:, 0:260]), wall[:, t_, chs].unsqueeze(2).to_broadcast([128, 4, 65]),
               ALU.mult, [prx, r_gate], [r_tmpn[dr]])
            yield
            tt(S, "dve", numt[:, dr, :], pbi[:, 0:260], tmpn[:, dr, :], ALU.add, [pri, r_tmpn[dr]], [r_num[dr]])
            yield
            nv = v3(numt[:, dr, :])
            dn, rd = dsm[:, dr, 0:4], dsm[:, dr, 4:8]
            ts(S, "dve", dn, nv[:, :, 64], -1.0, ALU.mult, [r_num[dr]], [r_dsm[dr]], s2=1.0, op1=ALU.max)
            yield
            tt(S, "dve", dn, dn, nv[:, :, 64], ALU.max, [r_num[dr], r_dsm[dr]], [r_dsm[dr]])
            yield
            recip(S, rd, dn, [r_dsm[dr]], [r_dsm[dr]])
            yield
            tt(S, "dve", v3(h64[:, dr, :])[:, :, 0:64], nv[:, :, 0:64], rd.unsqueeze(2).to_broadcast([128, 4, 64]), ALU.mult,
               [r_num[dr], r_dsm[dr]], [r_h64[dr]])
            yield
            tt(S, "dve", hf[:, t_, :], hf[:, t_, :], h64[:, dr, :], ALU.add, [r_h64[dr], r_hf[t_]], [r_hf[t_]])
            yield
            self.mod_pump()
            yield
            tt(S, "pool", kwp[:, dr, :, 64:128], ktok[:, t_, :].rearrange("p (a b) -> p a b", a=4),
               wkall[:, t_, chs].unsqueeze(2).to_broadcast([128, 4, 64]), ALU.mult, [r_ktok[t_], r_gate], [r_kwp[dr]])
            yield
            pbd, prd = self.pbank()
            g = []
            for j in range(2):
                o = pbd[:, j * 65:(j + 1) * 65]
                g.append((o, kwp[:, dr, 2 * j, 64:192], vaug[:, t_, 2 * j, 0:65], True, False))
                g.append((o, kwp[:, dr, 2 * j + 1, 0:128], vaug[:, t_, 2 * j + 1, 0:65], False, True))
            mm(S, g, [r_kwp[dr], r_vaug[t_]], [prd])
            yield
            tt(S, "dve", Clo[:, dr, :, :], Clo[:, dr, :, :],
               dec[:, t_, dr * 2:dr * 2 + 2].unsqueeze(2).to_broadcast([128, 2, 65]), ALU.mult,
               [r_C[dr], r_gate], [r_C[dr]])
            yield
            tt(S, "dve", Clo[:, dr, :, :], Clo[:, dr, :, :], pbd[:, 0:130].rearrange("p (a b) -> p a b", a=2),
               ALU.add, [r_C[dr], prd], [r_C[dr]])
            yield
            end = (t_ % 2 == 1) if dr == 0 else (t_ % 2 == 0)
            if end:
                sq_ = t_ // 2
                tt(S, "dve", snap[:, dr, sq_, :].rearrange("p (a b) -> p a b", a=2), Clo[:, dr, :, :],
                   scl[:, dr, sq_ * 2:sq_ * 2 + 2].unsqueeze(2).to_broadcast([128, 2, 65]), ALU.mult,
                   [r_C[dr], r_scl], [r_snap[dr][sq_]])
                yield
                S.dma("sp", d["o_C"][l, dr, sq_], snap[:, dr, sq_, :].rearrange("p (a b) -> p a b", a=2),
                      reads=[r_snap[dr][sq_]])
                yield
                ts(S, "dve", Clo[:, dr, :, :], Clo[:, dr, :, :], keep, ALU.mult, [r_C[dr], r_k], [r_C[dr]])
                yield
            for (rows, Cx, Cbx) in halves:
                act(S, Cbx[rows, dr, :, 0:65], Clo[rows, dr, :, :], AF.Copy, [r_C[dr]], [r_Cb[dr]])
                yield

        for i in range(8):
            gens = [process(0, i), process(1, 7 - i)]
            while gens:
                for g__ in list(gens):
                    try:
                        next(g__)
                    except StopIteration:
                        gens.remove(g__)
            self.mod_pump()
        if CUT == 7:
            return
        GM = LP["GM"]
        for t_ in range(8):
            b_ = t_ % 2
            tt(S, "dve", sq1[:, b_, :], hf[:, t_, :], hf[:, t_, :], ALU.mult, [r_hf[t_]], [r_sq1[b_]])
            S.op("dve", lambda e, t_=t_, b_=b_: e.tensor_reduce(out=ssall[:, t_, :],
                                                              in_=sq1[:, b_, :].rearrange("p (a b) -> p a b", a=4),
                                                              axis=AX.X, op=ALU.add), [r_sq1[b_]], [r_ss])
        act(S, ssall, ssall, AF.Ln, [r_ss], [r_ss], bias=EPS, scale=1.0 / 64)
        act(S, ssall, ssall, AF.Exp, [r_ss], [r_ss], scale=-0.5)
        ident_bf = k_["ident_bf"]
        for t_ in range(8):
            b_ = t_ % 2
            tt(S, "dve", sq1[:, b_, :].rearrange("p (a b) -> p a b", a=4), hf[:, t_, :].rearrange("p (a b) -> p a b", a=4),
               ssall[:, t_, :].unsqueeze(2).to_broadcast([128, 4, 64]), ALU.mult, [r_hf[t_], r_ss], [r_sq1[b_]])
            tt(S, "pool", h64[:, b_, :], sgo[:, t_, :], lp[:, l, GM:GM + 256], ALU.mult, [r_sgo[t_], r_k], [r_h64[b_]])
            tt(S, "dve", yatok[:, t_, :], sq1[:, b_, :], h64[:, b_, :], ALU.mult, [r_h64[b_], r_sq1[b_]], [r_ya[t_]])
            pbt, prt = self.pbank()
            pv = pbt[:].bitcast(BF16)
            for t2 in range(2):
                S.op("pe", lambda e, t2=t2, pv=pv, t_=t_: e.transpose(pv[:, t2 * 128:(t2 + 1) * 128],
                                                                      yatok[:, t_, t2 * 128:(t2 + 1) * 128], ident_bf[:]),
                     [r_ya[t_], r_k], [prt])
            c = t_ // 4
            for t2 in range(2):
                if t2 == 0:
                    act(S, ymix[:, t2, t_ * 128:(t_ + 1) * 128], pv[:, t2 * 128:(t2 + 1) * 128], AF.Copy, [prt], [ymr[t2][c]])
                else:
                    cp(S, "dve", ymix[:, t2, t_ * 128:(t_ + 1) * 128], pv[:, t2 * 128:(t2 + 1) * 128], [prt], [ymr[t2][c]])

    def mix_attn(self, l, ymix, ymr, win, glob):
        S, k_, d = self.S, self.k, self.d
        C = self.ctx
        hr, CH = C["hr"], C["CH"]
        LP = self.LP
        lp = k_["lp"]
        r_k = self.r_k
        hall = [hr[k][c] for k in range(8) for c in range(2)]
        sv8 = lambda sl_: self.slot_view(sl_, 8)
        q0 = 1040 if glob else 1552
        k0, v0 = q0 + 256, q0 + 384
        sl, sr = self.wload(self.parts_attn(l, glob), key=("attn", l, glob))
        if glob:
            self.prefetch(("attn", l, False), self.parts_attn(l, False))
        else:
            self.prefetch(("D1", l), self.parts_win(l, 2064, 512))
            self.prefetch(("D2", l), self.parts_win(l, 2576, 256))
        s3 = sv8(sl)
        ymb = 2 if glob else 4
        cs = self.carve([128, 2, T], F32)
        qpad = self.carve([128, 4, T], BF16)
        kfull = self.carve([128, 2, 1280], BF16)
        vpad = self.carve([128, 10, 2, 192], BF16)
        kout = self.carve([128, T], F32)
        vout = self.carve([128, 8, 128], F32)
        PT = self.carve([128, 3, 512], BF16)
        rden = self.carve([128, 2, 512], F32)
        raw = self.carve([128, 2, 512], F32)
        tb = self.carve([128, 2, 512], F32)
        gm = self.carve([128, 2, T], BF16)
        smask = self.carve([128, 8, 384], BF16) if not glob else None
        es = self.carve([128, 4], F32)
        r_cs, r_qp, r_kf, r_vp, r_kout, r_vout = R("cs"), RL(2, "qp"), R("kf"), RL(10, "vp"), R("kout"), R("vout")
        r_PT, r_rden, r_raw, r_tb, r_gm, r_sm, r_es = RL(3, "PT"), RL(2, "rden"), RL(2, "raw"), RL(2, "tb"), R("gm"), R("sm"), R("es")
        S.dma("sp", cs, d["ropeCS"], writes=[r_cs])
        S.op("pool", lambda e: e.memset(qpad, 0.0), writes=r_qp)
        S.op("pool", lambda e: e.memset(vpad[:, 2:10, :, :], 0.0), writes=r_vp[2:])
        kTd, vpd = (d["gkT"], d["gvp"]) if glob else (d["skT"], d["svp"])
        for x in range(2):
            S.dma("pool", kfull[:, x, 0:256], kTd[l, x], writes=[r_kf])
            S.dma("pool", vpad[:, x, :, :], vpd[l, x], writes=[r_vp[x]])
        if glob:
            S.dma("pool", gm[0:5, :, :], d["gmAB"], writes=[r_gm])
        if not glob:
            S.dma("pool", smask, d["swamask"], writes=[r_sm])
            SK = LP["SK"]
            act(S, es, lp[:, l, SK:SK + 4], AF.Exp, [r_k], [r_es])
        GQK = LP["GQK"]
        ropeP = k_["ropeP"]
        rstd2 = self.carve([128, 2, 512], F32)
        r_rstd2 = RL(2, "rstd2")

        def prelude(ti, c, b_):
            col0 = ti * 128
            pb, pr = self.pbank()
            mm(S, self.proj_fm(c, s3, col0, pb), [sr] + hall, [pr])
            yield
            rw = raw[:, b_, :]
            if glob:
                act(S, rw, pb[:], AF.Copy, [pr], [r_raw[b_]])
                yield
                i = self.sq_rr % 2
                self.sq_rr += 1
                sqb, r_sq = C["sqb"], C["r_sq"]
                act(S, sqb[:, i, :], rw, AF.Square, [r_raw[b_]], [r_sq[i]])
                yield
                p2, pr2 = self.pbank()
                mm(S, [(p2[:], k_["blockones"][:], sqb[:, i, :], True, True)], [r_sq[i], r_k], [pr2])
                yield
                rstd, r_rstd = rstd2[:, b_, :], r_rstd2[b_]
                act(S, rstd, p2[:], AF.Ln, [pr2], [r_rstd], bias=EPS, scale=1.0 / 64)
                yield
                act(S, rstd, rstd, AF.Exp, [r_rstd], [r_rstd], scale=-0.5)
                yield
                tt(S, "dve", rw, rw, rstd, ALU.mult, [r_raw[b_], r_rstd], [r_raw[b_]])
                yield
                gcol = GQK + (0 if ti < 2 else 1)
                act(S, rw, rw, AF.Copy, [r_raw[b_], r_k], [r_raw[b_]], scale=lp[:, l, gcol:gcol + 1])
                yield
            else:
                act(S, rw, pb[:], AF.Copy, [pr], [r_raw[b_]])
                yield
            p3, pr3 = self.pbank()
            mm(S, [(p3[:], ropeP[:], rw, True, True)], [r_raw[b_], r_k], [pr3])
            yield
            ta, tb_ = rw, tb[:, b_, :]
            tt(S, "pool", ta, rw, cs[:, 0, CH(c)], ALU.mult, [r_raw[b_], r_cs], [r_raw[b_]])
            yield
            tt(S, "dve", tb_, p3[:], cs[:, 1, CH(c)], ALU.mult, [pr3, r_cs], [r_tb[b_]])
            yield
            if ti < 2:
                for g_ in range(2):
                    rows = slice(g_ * 64, (g_ + 1) * 64)
                    tt(S, "dve", qpad[rows, 2 * ti + g_, CH(c)], ta[rows, :], tb_[rows, :], ALU.add, [r_tb[b_], r_raw[b_]], [r_qp[c]])
                    yield
            elif ti == 2:
                tt(S, "dve", kout[:, CH(c)], ta, tb_, ALU.add, [r_tb[b_], r_raw[b_]], [r_kout])
                yield
                act(S, kfull[:, 0, 256 + c * 512:256 + (c + 1) * 512], kout[:, CH(c)], AF.Copy, [r_kout], [r_kf])
                yield
            else:
                tt(S, "dve", kfull[:, 1, 256 + c * 512:256 + (c + 1) * 512], ta, tb_, ALU.add, [r_tb[b_], r_raw[b_]], [r_kf])
                yield

        its = [(ti, c) for ti in range(4) for c in range(2)]
        for i0 in range(0, 8, 2):
            gens = [prelude(its[i0][0], its[i0][1], 0), prelude(its[i0 + 1][0], its[i0 + 1][1], 1)]
            while gens:
                for g__ in list(gens):
                    try:
                        next(g__)
                    except StopIteration:
                        gens.remove(g__)
        S.dma("sp", (d["o_gk"] if glob else d["o_sk"])[l], kout, reads=[r_kout])
        for t_ in range(8):
            pb, pr = self.pbank()
            mm(S, self.proj_tok(t_, s3, 512, 128, pb, 0), [sr] + hall, [pr])
            act(S, vout[:, t_, :], pb[:, 0:128], AF.Copy, [pr], [r_vout])
            cp(S, "dve", vpad[:, 2 + t_, :, 64:128], pb[:, 0:128].rearrange("p (a b) -> p a b", a=2), [pr], [r_vp[2 + t_]])
        S.dma("sp", (d["o_gv"] if glob else d["o_sv"])[l].rearrange("(t p) n -> p t n", p=128), vout, reads=[r_vout])
        self.bank_pool = [0, 1, 2, 3]
        ones_bf = C["ones_bf"]
        r_ones = C["r_ones"]
        ident_bf = k_["ident_bf"]
        ctxb = k_["cflags"][:, 0:1]
        pt_rr = 0
        acc_rr = 0
        for h in range(4):
            kv, g_ = h // 2, h % 2
            kx = 0 if g_ == kv else 1
            rows = slice(g_ * 64, (g_ + 1) * 64)
            vw = slice(64, 192) if g_ == 0 else slice(0, 128)
            for c in range(2):
                if glob:
                    tiles = [(mt, 0, 512) for mt in range(10)]
                else:
                    tiles = [(0, 0, 512), (1, 0, 512)]
                    for j in range(8):
                        lo, hi = max((j - 1) * 128, c * 512), min((j + 2) * 128, (c + 1) * 512)
                        if hi > lo:
                            tiles.append((2 + j, lo - c * 512, hi - c * 512))
                ai_ = 4 + 2 * (acc_rr % 2)
                acc_rr += 1
                pn, prn, pd, prd = self.ps[ai_], self.psr[ai_], self.ps[ai_ + 1], self.psr[ai_ + 1]
                pend = []

                def qk(idx):
                    mt, lo, hi = tiles[idx]
                    pb, pr = self.pbank()
                    qs = slice(c * 512 + lo, c * 512 + hi)
                    g = [(pb[:, lo:hi], kfull[:, kx, mt * 128:(mt + 1) * 128], qpad[:, h, qs], True, mt < 2)]
                    rd = [r_kf, r_qp[c]]
                    if mt >= 2:
                        if glob:
                            g.append((pb[:, lo:hi], gm[0:5, 0, (mt - 2) * 128:(mt - 1) * 128], gm[0:5, 1, qs], False, True))
                            rd.append(r_gm)
                        else:
                            j = mt - 2
                            m0_ = c * 512 + lo - (j - 1) * 128
                            g.append((pb[:, lo:hi], ident_bf[:], smask[:, j, m0_:m0_ + (hi - lo)], False, True))
                            rd += [r_sm, r_k]
                    mm(S, g, rd, [pr])
                    return pb, pr

                nt = len(tiles)
                look = 2
                for idx in range(min(look, nt)):
                    pend.append(qk(idx))
                for idx in range(nt):
                    mt, lo, hi = tiles[idx]
                    pb, pr = pend.pop(0)
                    if idx + look < nt:
                        pend.append(qk(idx + look))
                    pi = pt_rr % 3
                    pt_rr += 1
                    act(S, PT[:, pi, lo:hi], pb[:, lo:hi], AF.Exp, [pr, r_k], [r_PT[pi]],
                        bias=(ctxb if mt < 2 else 0.0), scale=0.125)
                    g = [(pn[:, lo:hi], vpad[:, mt, kv, vw], PT[:, pi, lo:hi], idx == 0, idx == nt - 1),
                         (pd[:, lo:hi], ones_bf[:], PT[:, pi, lo:hi], idx == 0, idx == nt - 1)]
                    mm(S, g, [r_vp[mt], r_PT[pi], r_ones], [prn, prd])
                ri = (h * 2 + c) % 2
                if glob:
                    recip(S, rden[rows, ri, :], pd[rows, :], [prd], [r_rden[ri]])
                else:
                    ts(S, "dve", rden[rows, ri, :], pd[rows, :], es[rows, h:h + 1], ALU.add, [prd, r_es], [r_rden[ri]])
                    recip(S, rden[rows, ri, :], rden[rows, ri, :], [r_rden[ri]], [r_rden[ri]])
                tt(S, "dve", ymix[rows, ymb + kv, CH(c)], pn[rows, :], rden[rows, ri, :], ALU.mult, [prn, r_rden[ri]],
                   [ymr[ymb + kv][c]])
        self.bank_pool = list(range(8))

    def mix_hyena(self, l, ymix, ymr, win):
        S, k_, d = self.S, self.k, self.d
        C = self.ctx
        hr, CH = C["hr"], C["CH"]
        LP = self.LP
        lp = k_["lp"]
        r_k = self.r_k
        hall = [hr[k][c] for k in range(8) for c in range(2)]
        sv8 = lambda sl_: self.slot_view(sl_, 8)
        TWO_PI = float(2 * math.pi)
        xoff = self.aoff
        feats = self.carve([128, T], F32)
        z1 = self.carve([128, T], F32)
        z2 = self.carve([128, T], F32)
        wnd = self.carve([128, 8, 256], F32)
        rr = self.carve([128, 512], F32)
        ii = self.carve([128, 512], I32)
        kf = self.carve([128, 512], F32)
        xend = self.aoff
        self.aoff = xoff
        raw = self.carve([128, 3, T], F32)
        uct = self.carve([128, 2, T], F32)
        pa = self.carve([128, 2, 256], F32)
        pq = self.carve([128, 2, 256], F32)
        yt = self.carve([128, 2, 256], F32)
        assert self.aoff <= xend
        self.aoff = xend
        r_X = R("X")
        x0 = self.carve([128, 2, T], F32)
        zf = self.carve([128, 2, T], F32)
        zbf = self.carve([128, 2, T], BF16)
        zh = self.carve([128, 8, 512], BF16)
        ZH = self.carve([128, 2, 512], F32)
        Y = self.carve([128, 8, 2, 256], BF16)
        pa2 = self.carve([128, 2, 256], F32)
        yt2 = ZH[:, :, 0:256]
        r_x0, r_zf, r_zbf = RL(2, "x0"), RL(2, "zf"), RL(2, "zbf")
        r_zh = RL(8, "zh")
        r_ZH, r_Y, r_pa, r_pq, r_yt = R("ZH"), RL(8, "Y"), R("pa"), R("pq"), RL(2, "yt")
        S.dma("sp", feats[0:33, :], d["featsT"], writes=[r_X])
        S.dma("sp", wnd, d["window"], writes=[r_X])
        fs = k_["fs"]

        def sin_layer(pb, pr, dst, bcol):
            ts(S, "dve", rr[0:64, :], pb[0:64, :], fs[:, l, 0:1], ALU.mult, [pr, r_k], [r_X], s2=fs[:, l, bcol:bcol + 1],
               op1=ALU.add)
            cp(S, "dve", ii[0:64, :], rr[0:64, :], [r_X], [r_X])
            cp(S, "dve", kf[0:64, :], ii[0:64, :], [r_X], [r_X])
            tt(S, "dve", rr[0:64, :], rr[0:64, :], kf[0:64, :], ALU.subtract, [r_X], [r_X])
            act(S, dst, rr[0:64, :], AF.Sin, [r_X], [r_X], scale=TWO_PI)

        for c in range(2):
            pb, pr = self.pbank()
            mm(S, [(pb[0:64, :], k_["fw1"][0:33, l, :], feats[0:33, CH(c)], True, True)], [r_X, r_k], [pr])
            sin_layer(pb, pr, z1[0:64, CH(c)], 1)
        for c in range(2):
            pb, pr = self.pbank()
            mm(S, [(pb[0:64, :], k_["fw2"][0:64, l, :], z1[0:64, CH(c)], True, True)], [r_X, r_k], [pr])
            sin_layer(pb, pr, z2[0:64, CH(c)], 2)
        for t_ in range(8):
            pb, pr = self.pbank()
            tok = slice(t_ * 128, (t_ + 1) * 128)
            mm(S, [(pb[:, 0:256], z2[0:64, tok], k_["fw3"][0:64, l, :], True, False),
                   (pb[:, 0:256], k_["ones_f"][0:1, :], k_["fb3"][0:1, l, :], False, True)], [r_X, r_k], [pr])
            tt(S, "dve", zh[:, t_, 256:512], pb[:, 0:256], wnd[:, t_, :], ALU.mult, [pr, r_X], [r_zh[t_]])
        slD1, srD1 = self.wload(self.parts_win(l, 2064, 512), key=("D1", l))
        slD2, srD2 = self.wload(self.parts_win(l, 2576, 256), key=("D2", l))
        sD1, sD2 = sv8(slD1), sv8(slD2)
        Fd, Gd = d["dftF"], d["dftG"]
        fv = lambda m: Fd[m].rearrange("(k p) n -> p k n", p=128)
        gv = lambda m: Gd[m].rearrange("(k p) n -> p k n", p=128)

        def parts_dft(vw, q):
            return [(lambda sl_: sv8(sl_)[:, :, 0:256], vw(0)[:, :, q * 256:(q + 1) * 256]),
                    (lambda sl_: sv8(sl_)[:, :, 256:512], vw(1)[:, :, q * 256:(q + 1) * 256])]

        def zip_run(gens):
            gens = list(gens)
            while gens:
                for g__ in list(gens):
                    try:
                        next(g__)
                    except StopIteration:
                        gens.remove(g__)

        for q in range(2):
            self.prefetch(("F", l, q), parts_dft(fv, q))
        CV = LP["CV"]
        cvb = k_["cvb"]
        r_raw = RL(3, "hraw")
        r_uct = RL(2, "uct")

        def conv_chain(ct, ui):
            s3, col, srr = ((sD1, ct * 128, srD1), (sD1, 256 + ct * 128, srD1), (sD2, ct * 128, srD2))[ui]
            rr_ = r_raw[ui]
            fx = [r_X] if ct == 0 else []
            for c in range(2):
                pb, pr = self.pbank()
                mm(S, self.proj_fm(c, s3, col, pb), [srr] + hall, [pr])
                yield
                act(S, raw[:, ui, CH(c)], pb[:], AF.Copy, [pr], [rr_] + fx)
                yield
            tile = ui * 2 + ct
            cw = lambda j: lp[:, l, CV + tile * 4 + j:CV + tile * 4 + j + 1]
            u = raw[:, ui, :]
            if ui == 0:
                dst, wr = x0[:, ct, :], [r_x0[ct]]
            else:
                dst, wr = uct[:, ui - 1, :], [r_uct[ui - 1]] + fx
            rd = [rr_, r_k]
            act(S, dst, u, AF.Identity, rd, wr, bias=cw(3), scale=cw(1))
            yield
            for (o, a_, sc) in ((dst[:, 1:T], u[:, 0:T - 1], cw(0)), (dst[:, 0:T - 1], u[:, 1:T], cw(2)),
                                (dst[:, 256:T:256], u[:, 255:T - 1:256], cvb[:, l, tile, 0:1]),
                                (dst[:, 255:T - 1:256], u[:, 256:T:256], cvb[:, l, tile, 1:2])):
                S.op("dve", lambda e, o=o, a_=a_, sc=sc: e.scalar_tensor_tensor(out=o, in0=a_, scalar=sc, in1=o,
                                                                              op0=ALU.mult, op1=ALU.add), rd + wr[:1], wr[:1])
                yield

        for ct in range(2):
            zip_run([conv_chain(ct, ui) for ui in range(3)])
            tt(S, "dve", zf[:, ct, :], uct[:, 0, :], uct[:, 1, :], ALU.mult, r_uct, [r_zf[ct]])
            act(S, zbf[:, ct, :], zf[:, ct, :], AF.Copy, [r_zf[ct]], [r_zbf[ct]])
        for q in range(2, 4):
            self.prefetch(("F", l, q), parts_dft(fv, q))
        ident_bf = k_["ident_bf"]
        for t_ in range(8):
            pb, pr = self.pbank()
            pv = pb[:].bitcast(BF16)
            for ct in range(2):
                S.op("pe", lambda e, ct=ct, pv=pv, t_=t_: e.transpose(pv[:, ct * 128:(ct + 1) * 128],
                                                                      zbf[:, ct, t_ * 128:(t_ + 1) * 128], ident_bf[:]),
                     [r_zbf[ct], r_k], [pr])
            cp(S, "dve", zh[:, t_, 0:256], pv[:, 0:256], [pr], [r_zh[t_]])
        ZHs = [ZH, zbf.rearrange("p a b -> p (a b)").bitcast(F32).rearrange("p (a b) -> p a b", a=2)]
        r_ZHs = [r_ZH, R("ZHb")]
        PAs, PQs = [pa, pa2], [pq, yt]
        r_PA, r_PQ = RL(2, "PA"), RL(2, "PQ")
        seen = set()

        def ft_chain(ft, fj, s3, sr):
            bi = ft % 2
            Zb, rz = ZHs[bi], r_ZHs[bi]
            PA, PQ, rpa, rpq = PAs[bi], PQs[bi], r_PA[bi], r_PQ[bi]
            fresh = bi not in seen
            seen.add(bi)
            fz = (r_zbf if (fresh and bi == 1) else [])
            fx = ([r_X] if fresh else [])
            pre_, prr = self.pbank()
            pim, pri = self.pbank()
            g = [(pre_[:], s3[:, t_, fj * 128:(fj + 1) * 128], zh[:, t_, :], t_ == 0, t_ == 7) for t_ in range(8)]
            g += [(pim[:], s3[:, t_, 256 + fj * 128:256 + (fj + 1) * 128], zh[:, t_, :], t_ == 0, t_ == 7) for t_ in range(8)]
            mm(S, g, [sr] + r_zh, [prr, pri])
            yield
            act(S, Zb[:, 0, :], pre_[:], AF.Copy, [prr], [rz] + fz)
            yield
            act(S, Zb[:, 1, :], pim[:], AF.Copy, [pri], [rz])
            yield
            Zr, Hr, Zi, Hi = Zb[:, 0, 0:256], Zb[:, 0, 256:512], Zb[:, 1, 0:256], Zb[:, 1, 256:512]
            tt(S, "dve", PA[:, 0, :], Zr, Hr, ALU.mult, [rz], [rpa] + fx)
            yield
            tt(S, "pool", PQ[:, 0, :], Zr, Hi, ALU.mult, [rz], [rpq] + fx)
            yield
            tt(S, "dve", PA[:, 1, :], Zi, Hi, ALU.mult, [rz], [rpa])
            yield
            tt(S, "pool", PQ[:, 1, :], Zi, Hr, ALU.mult, [rz], [rpq])
            yield
            tt(S, "dve", Y[:, ft, 0, :], PA[:, 0, :], PA[:, 1, :], ALU.subtract, [rpa], [r_Y[ft]])
            yield
            tt(S, "pool", Y[:, ft, 1, :], PQ[:, 0, :], PQ[:, 1, :], ALU.add, [rpq], [r_Y[ft]])
            yield

        for q in range(4):
            sl, sr = self.wload(parts_dft(fv, q), key=("F", l, q))
            s3 = sv8(sl)
            zip_run([ft_chain(q * 2 + fj, fj, s3, sr) for fj in range(2)])
        HB = LP["HB"]
        for q in range(2):
            self.prefetch(("G", l, q), parts_dft(gv, q))
        for q in range(4):
            sl, sr = self.wload(parts_dft(gv, q), key=("G", l, q))
            if q + 2 < 4:
                self.prefetch(("G", l, q + 2), parts_dft(gv, q + 2))
            s3 = sv8(sl)
            ns = slice(q * 256, (q + 1) * 256)
            for ct in range(2):
                pb, pr = self.pbank()
                g = []
                for ft in range(8):
                    g.append((pb[:, 0:256], Y[:, ft, 0, ct * 128:(ct + 1) * 128], s3[:, ft, 0:256], ft == 0, False))
                    g.append((pb[:, 0:256], Y[:, ft, 1, ct * 128:(ct + 1) * 128], s3[:, ft, 256:512], False, ft == 7))
                mm(S, g, [sr] + r_Y, [pr])
                ytb, ryt = (yt2[:, ct, :], r_yt[ct])
                act(S, ytb, pb[:, 0:256], AF.Copy, [pr], [ryt] + ([r_PA[1], r_ZHs[0]] if q == 0 else []))
                S.op("dve", lambda e, ytb=ytb, ct=ct: e.scalar_tensor_tensor(out=ytb, in0=zf[:, ct, ns],
                                                                           scalar=lp[:, l, HB + ct:HB + ct + 1], in1=ytb,
                                                                           op0=ALU.mult, op1=ALU.add),
                     [r_zf[ct], ryt, r_k], [ryt])
                tt(S, "dve", ymix[:, 6 + ct, ns], x0[:, ct, ns], ytb, ALU.mult, [r_x0[ct], ryt], [ymr[6 + ct][q // 2]])


def fm(v):
    return np.ascontiguousarray(np.asarray(v, np.float32).reshape(8, 128).T)


def _consts(kind):
    c = {}
    p = np.arange(128)
    t = np.arange(T)
    ident = np.eye(128, dtype=np.float32)
    c["ident"] = ident
    bo = np.zeros((128, 128), np.float32)
    bo[:64, :64] = 1
    bo[64:, 64:] = 1
    c["blockones"] = bo
    r_, t_ = np.meshgrid(p, p, indexing="ij")
    tri = np.zeros((128, 4, 128), np.float32)
    tri[:, 0, :] = (r_ <= t_)
    tri[:, 1, :] = (r_ >= t_)
    tri[:, 2, :] = np.where(r_ <= t_, 0.0, NEG)
    tri[:, 3, :] = np.where(r_ >= t_, 0.0, NEG)
    c["tri"] = tri
    P = np.zeros((128, 128), np.float32)
    for b in range(0, 128, 32):
        for i in range(16):
            P[b + i + 16, b + i] = -1.0
            P[b + i, b + i + 16] = 1.0
    c["ropeP"] = P
    cs = np.zeros((128, 2, T), np.float32)
    if kind == "s":
        dd = p % 64
        inv = (10000.0 ** (-(dd % 16).astype(np.float32) / np.float32(16))).astype(np.float32)
        row = (t // 64).astype(np.float32)
        col = (t % 64).astype(np.float32)
        pos = np.where((dd // 32)[:, None] == 0, row[None, :], col[None, :]).astype(np.float32)
        ang = (pos * inv[:, None]).astype(np.float32)
        cs[:, 0, :] = np.cos(ang)
        cs[:, 1, :] = np.sin(ang)
    else:
        cs[:, 0, :] = 1.0
    c["ropeCS"] = cs
    sm = np.full((128, 8, 384), NEG, np.float32)
    for j in range(8):
        m = j * 128 + p[:, None]
        q = (j - 1) * 128 + np.arange(384)[None, :]
        inr = (q >= 0) & (q < T)
        if kind == "s":
            ok = (np.abs(m - q) <= 128) & inr
        else:
            ok = ((m // 256) == (q // 256)) & inr
        sm[:, j, :] = np.where(ok, 0.0, NEG)
    c["swamask"] = sm
    gm = np.zeros((5, 2, T), np.float32)
    if kind == "p":
        gm[0, 0, :] = 1.0
        gm[0, 1, :] = -BIG
        for s_ in range(4):
            gm[1 + s_, 0, :] = (t // 256 == s_)
            gm[1 + s_, 1, :] = BIG * (t // 256 == s_)
    c["gmAB"] = gm
    c["cflags"] = np.tile(np.array([[0.0, 1.0, 0.0, 0.0]] if kind == "s" else [[NEG, 0.0, 1.0, 0.0]], np.float32), (128, 1))
    sel = np.zeros((4, 130), np.float32)
    for h in range(4):
        sel[h, 0:128] = ((p >= 64).astype(int) == (h % 2))
        sel[h, 128 + h // 2] = 1.0
    c["sel"] = sel
    L = T if kind == "s" else 256
    rep = T // L
    pos = np.arange(L, dtype=np.float32)
    t01 = pos / np.float32(max(L - 1, 1))
    lin = np.linspace(1e-4, 15.0, 16, dtype=np.float32)
    ang = (np.float32(2.0 * math.pi / L) * pos[:, None] * lin[None, :]).astype(np.float32)
    feats = np.concatenate([t01[:, None], np.cos(ang), -np.sin(ang)], -1).astype(np.float32)
    c["featsT"] = np.ascontiguousarray(np.tile(feats, (rep, 1)).T)
    centre = L // 2
    dist = np.abs(pos - centre) / np.float32(max(centre, 1))
    deltas = np.abs(np.linspace(math.log(0.01) / 1.5, math.log(0.01) / 0.3, 256, dtype=np.float32))
    wnd = np.exp(-dist[:, None] * deltas[None, :]).astype(np.float32)
    c["window"] = np.ascontiguousarray(np.tile(wnd, (rep, 1)).reshape(8, 128, 256).transpose(1, 0, 2))
    N = 2 * L
    tt_ = np.arange(L, dtype=np.float64)
    ff = np.arange(L, dtype=np.float64)
    th = math.pi * (2 * ff + 1) / N
    Fc = np.cos(tt_[:, None] * th[None, :])
    Fs = -np.sin(tt_[:, None] * th[None, :])
    Gc = (2.0 / N) * np.cos(th[:, None] * (tt_[None, :] + L // 2))
    Gs = -(2.0 / N) * np.sin(th[:, None] * (tt_[None, :] + L // 2))
    dF = np.zeros((2, T, T), np.float32)
    dG = np.zeros((2, T, T), np.float32)
    for s_ in range(rep):
        sl = slice(s_ * L, (s_ + 1) * L)
        dF[0, sl, sl] = Fc
        dF[1, sl, sl] = Fs
        dG[0, sl, sl] = Gc
        dG[1, sl, sl] = Gs
    c["dftF"] = dF
    c["dftG"] = dG
    return c


def host_inputs(inp, cores=None):
    f = lambda a: np.ascontiguousarray(np.asarray(a, dtype=np.float32))
    A = {k: np.asarray(v) for k, v in inp.items()}
    shared = {}
    for nm in ("w_ada", "w1_gate", "w1_up", "w1_down", "w_in", "w_out", "w2_gate", "w2_up", "w2_down"):
        shared[nm] = f(A[nm])
    shared["b_adaT"] = f(A["b_ada"].reshape(NL, 72, 128).transpose(0, 2, 1))
    gv = []
    for l in range(NL):
        gv += [fm(A["g_ff1"][l]), fm(A["g_mix"][l]), fm(A["g_ff2"][l])]
    gv.append(fm(A["g_final"]))
    shared["gvec"] = f(np.concatenate(gv, axis=1))
    lp = np.zeros((128, NL, 304), np.float32)
    p = np.arange(128)
    for l in range(NL):
        lp[:, l, 0:16] = A["b_gates"][l][None, :]
        lp[:, l, 16:272] = A["g_mlstm"][l][None, :]
        lp[:, l, 272] = A["g_qnorm"][l][p % 64]
        lp[:, l, 273] = A["g_knorm"][l][p % 64]
        lp[:, l, 274:278] = A["sinks"][l][None, :]
        for i in range(6):
            ch = i * 128 + p
            lp[:, l, 278 + i * 4 + 0] = A["conv_w"][l][0, ch]
            lp[:, l, 278 + i * 4 + 1] = A["conv_w"][l][1, ch]
            lp[:, l, 278 + i * 4 + 2] = A["conv_w"][l][2, ch]
            lp[:, l, 278 + i * 4 + 3] = A["conv_b"][l][ch]
        for ct in range(2):
            lp[:, l, 302 + ct] = A["hyena_bias"][l][ct * 128 + p]
    shared["lp"] = lp
    shared["fw1"] = f(A["filt_w1"].transpose(1, 0, 2))
    shared["fw2"] = f(A["filt_w2"].transpose(1, 0, 2))
    shared["fw3"] = f(A["filt_w3"].transpose(1, 0, 2))
    shared["fvec"] = f(np.stack([A["filt_b1"], A["filt_b2"], A["filt_freq"]], -1).transpose(1, 0, 2))
    shared["fb3"] = f(A["filt_b3"][None, :, :])
    cst = {"s": _consts("s"), "p": _consts("p")}
    maps = []
    xs, xp = A["x_sample"], A["x_prompt"]
    for core in (range(8) if cores is None else cores):
        m = dict(shared)
        kind = "s" if core < 4 else "p"
        m.update(cst[kind])
        C0 = np.zeros((NL, 2, 128, 2, 65), np.float32)
        m0rep = np.zeros((128, NL, 2, 2), np.float32)
        m0c = np.zeros((4, NL * 2), np.float32)
        kT = {n: np.zeros((NL, 2, 128, 256), np.float32) for n in ("gkT", "skT")}
        vp = {n: np.zeros((NL, 2, 128, 2, 192), np.float32) for n in ("gvp", "svp")}
        if core < 4:
            b = core
            m["xT"] = f(xs[b].T)
            m["cvec"] = fm(A["c"][b])
            sC, sn, smm = A["state_mlstm_C"][b], A["state_mlstm_n"][b], A["state_mlstm_m"][b]
            for g_ in range(2):
                for pr_ in range(2):
                    h = 2 * pr_ + g_
                    C0[:, :, g_ * 64:(g_ + 1) * 64, pr_, 0:64] = sC[:, :, h]
                    C0[:, :, g_ * 64:(g_ + 1) * 64, pr_, 64] = sn[:, :, h]
                    m0rep[g_ * 64:(g_ + 1) * 64, :, :, pr_] = smm[None, :, :, h]
            for l in range(NL):
                for dr in range(2):
                    m0c[:, l * 2 + dr] = smm[l, dr, :]
            for (kn, vn, ck_, cv_) in (("gkT", "gvp", "cache_gattn_k", "cache_gattn_v"), ("skT", "svp", "cache_swa_k", "cache_swa_v")):
                ck, cv = A[ck_][b], A[cv_][b]
                t1 = ck.transpose(0, 2, 3, 1).reshape(NL, 128, 256)
                t2 = ck[:, :, ::-1, :].transpose(0, 2, 3, 1).reshape(NL, 128, 256)
                kT[kn][:, 0] = t1
                kT[kn][:, 1] = t2
                vp[vn][:, :, :, :, 64:128] = cv.reshape(NL, 2, 128, 2, 64)
        else:
            j = core - 4
            m["xT"] = f(xp[4 * j:4 * j + 4].reshape(T, D).T)
            m["cvec"] = fm(A["c_ctx"])
        m["C0"], m["m0rep"], m["m0c"] = C0, m0rep, m0c
        m.update(kT)
        m.update(vp)
        maps.append(m)
    return maps


_PROG = None


def get_prog():
    global _PROG
    if _PROG is None:
        _PROG = Prog()
    return _PROG


def run_device(inputs, trace=False, cores=None):
    prog = get_prog()
    maps = host_inputs(inputs, cores)
    maps = [{k: np.ascontiguousarray(v, dtype=np.float32) for k, v in m.items() if k in prog.din} for m in maps]
    for m in maps:
        for k, shp in prog.din.items():
            assert m[k].shape == shp, (k, m[k].shape, shp)
    res = run_bass_kernel_spmd(prog.nc, maps, core_ids=list(range(len(maps))), trace=trace)
    return res


def assemble(res):
    r = res.results
    yp = np.zeros((16, 256, D), np.float32)
    ys = np.zeros((4, T, D), np.float32)
    nC = np.zeros((16, NL, 2, 4, 64, 64), np.float32)
    nn = np.zeros((16, NL, 2, 4, 64), np.float32)
    nm = np.zeros((16, NL, 2, 4), np.float32)
    kv = {n: np.zeros((16, NL, 256, 2, 64), np.float32) for n in ("o_gk", "o_gv", "o_sk", "o_sv")}
    for core in range(8):
        y = np.ascontiguousarray(r[core]["yT"].T)
        if core < 4:
            ys[core] = y
            continue
        j = core - 4
        yp[4 * j:4 * j + 4] = y.reshape(4, 256, D)
        for n in ("o_gk", "o_sk"):
            a = r[core][n].reshape(NL, 2, 64, 4, 256)
            kv[n][4 * j:4 * j + 4] = a.transpose(3, 0, 4, 1, 2)
        for n in ("o_gv", "o_sv"):
            a = r[core][n].reshape(NL, 4, 256, 2, 64)
            kv[n][4 * j:4 * j + 4] = a.transpose(1, 0, 2, 3, 4)
        oc = r[core]["o_C"].reshape(NL, 2, 4, 2, 64, 2, 65)
        oc = oc.transpose(2, 0, 1, 5, 3, 4, 6).reshape(4, NL, 2, 4, 64, 65)
        nC[4 * j:4 * j + 4] = oc[..., 0:64]
        nn[4 * j:4 * j + 4] = oc[..., 64]
        nm[4 * j:4 * j + 4] = r[core]["o_m"].transpose(3, 0, 1, 2)
    return (yp, ys, nC, nn, nm, kv["o_gk"], kv["o_gv"], kv["o_sk"], kv["o_sv"])


def kernel(**inputs):
    res = run_device(inputs)
    return assemble(res)
```

```python
import os
import math
import numpy as np
import concourse.bass as bass
import concourse.mybir as mybir
from concourse.bass_utils import run_bass_kernel_spmd

F32 = mybir.dt.float32
BF16 = mybir.dt.bfloat16
I32 = mybir.dt.int32
AF = mybir.ActivationFunctionType
ALU = mybir.AluOpType
AX = mybir.AxisListType

D = 1024
T = 1024
DFF = 2816
NFF = 22
NL = 2
NIN = 2832
EPS = 1e-6
NEG = -30000.0
BIG = 29952.0
LN8 = math.log(0.125)
SLOT = 8 * 640
STAGE = int(os.environ.get("MK_STAGE", "99"))
DEBUG = bool(os.environ.get("MK_DEBUG"))
CUT = int(os.environ.get("MK_CUT", "99"))


class R:
    __slots__ = ("name", "w", "rd", "excl")

    def __init__(self, name="", excl=False):
        self.name = name
        self.w = None
        self.rd = []
        self.excl = excl


def RL(n, name=""):
    return [R("%s%d" % (name, i)) for i in range(n)]


class Sched:
    NLANES = {"sp": 8, "pool": 3, "poolw": 5}

    def __init__(self, nc):
        self.nc = nc
        self.eng = {"pe": nc.tensor, "act": nc.scalar, "dve": nc.vector,
                    "pool": nc.gpsimd, "sp": nc.sync}
        self.sem = {}
        self.cnt = {}
        for k in self.eng:
            self.sem[k] = nc.alloc_semaphore(name="s_" + k)
            self.cnt[k] = 0
        self.lanes = {}
        for q, n in self.NLANES.items():
            self.lanes[q] = []
            for i in range(n):
                key = "d_%s%d" % (q, i)
                self.sem[key] = nc.alloc_semaphore(name=key)
                self.cnt[key] = 0
                self.lanes[q].append(key)
        self.lane_rr = {q: 0 for q in self.NLANES}
        self.eng_of = {"sp": "sp", "pool": "pool", "poolw": "pool"}
        self.waited = {k: {} for k in self.eng}
        self.nwaits = 0
        self.nops = 0

    def _wait(self, e, key, val):
        if key == "pe" and e == "pe":
            return
        w = self.waited[e]
        if w.get(key, 0) >= val:
            return
        self.eng[e].wait_ge(self.sem[key], val)
        w[key] = val
        self.nwaits += 1

    def _deps(self, e, reads, writes):
        deps = {}
        for r in reads:
            if r.w is not None:
                k, v = r.w
                if deps.get(k, 0) < v:
                    deps[k] = v
            if r.excl:
                for (k, v) in r.rd:
                    if k != e and deps.get(k, 0) < v:
                        deps[k] = v
        for w in writes:
            if w.w is not None:
                k, v = w.w
                if deps.get(k, 0) < v:
                    deps[k] = v
            for (k, v) in w.rd:
                if deps.get(k, 0) < v:
                    deps[k] = v
        for k, v in deps.items():
            self._wait(e, k, v)

    def _commit(self, tok, reads, writes):
        for r in reads:
            r.rd.append(tok)
            if len(r.rd) > 48:
                mx = {}
                for k, v in r.rd:
                    if mx.get(k, 0) < v:
                        mx[k] = v
                r.rd = list(mx.items())
        for w in writes:
            w.w = tok
            w.rd = []

    def op(self, e, fn, reads=(), writes=()):
        self._deps(e, reads, writes)
        inst = fn(self.eng[e])
        self.cnt[e] += 1
        inst.then_inc(self.sem[e], 1)
        self._commit((e, self.cnt[e]), reads, writes)
        self.nops += 1

    def dma(self, q, out, in_, reads=(), writes=()):
        lanes = self.lanes[q]
        e = self.eng_of[q]
        key = lanes[self.lane_rr[q] % len(lanes)]
        self.lane_rr[q] += 1
        self._wait(e, key, self.cnt[key])
        self._deps(e, reads, writes)
        inst = self.eng[e].dma_start(out=out, in_=in_)
        self.cnt[key] += 16
        inst.then_inc(self.sem[key], 16)
        self._commit((key, self.cnt[key]), reads, writes)
        self.nops += 1

    def barrier(self):
        for e in self.eng:
            for k in self.sem:
                if self.cnt[k] > 0 and not k.startswith("d_poolw"):
                    self._wait(e, k, self.cnt[k])

    def finish(self):
        for k in self.sem:
            if self.cnt[k] > 0 and k != "sp":
                self._wait("sp", k, self.cnt[k])


def act(S, out, in_, func, reads, writes, bias=0.0, scale=1.0):
    S.op("act", lambda e: e.activation(out=out, in_=in_, func=func, bias=bias, scale=scale), reads, writes)


def tt(S, eng, out, a, b, op, reads, writes):
    S.op(eng, lambda e: e.tensor_tensor(out=out, in0=a, in1=b, op=op), reads, writes)


def ts(S, eng, out, a, s1, op0, reads, writes, s2=None, op1=None):
    if op1 is None:
        S.op(eng, lambda e: e.tensor_scalar(out=out, in0=a, scalar1=s1, scalar2=None, op0=op0), reads, writes)
    else:
        S.op(eng, lambda e: e.tensor_scalar(out=out, in0=a, scalar1=s1, scalar2=s2, op0=op0, op1=op1), reads, writes)


def cp(S, eng, out, in_, reads, writes):
    S.op(eng, lambda e: e.tensor_copy(out=out, in_=in_), reads, writes)


def recip(S, out, in_, reads, writes):
    S.op("dve", lambda e: e.reciprocal(out=out, in_=in_), reads, writes)


def mm(S, groups, reads, writes):
    def fn(e):
        inst = None
        for (o, l, r, st, sp_) in groups:
            inst = e.matmul(o, lhsT=l, rhs=r, start=st, stop=sp_)
        return inst
    S.op("pe", fn, reads, writes)


class Prog:
    def __init__(self):
        nc = bass.Bass("TRN2", target_bir_lowering=False)
        self.nc = nc
        self.S = Sched(nc)
        self.din = {}
        self.dout = {}
        self._build()

    def inp(self, name, shape):
        t = self.nc.dram_tensor(name, list(shape), F32, kind="ExternalInput").ap()
        self.din[name] = tuple(shape)
        return t

    def outp(self, name, shape):
        t = self.nc.dram_tensor(name, list(shape), F32, kind="ExternalOutput").ap()
        self.dout[name] = tuple(shape)
        return t

    def sb(self, name, shape, dt=F32):
        return self.nc.alloc_sbuf_tensor("sb_" + name, list(shape), dt)

    def arena_reset(self, to=0):
        self.aoff = to
        self.S.barrier()

    def carve(self, shape, dt=F32):
        n = 1
        for s in shape[1:]:
            n *= s
        nb = n * (4 if dt in (F32, I32) else 2)
        nb = (nb + 31) // 32 * 32
        off = self.aoff
        self.aoff += nb
        assert self.aoff <= self.ARENA_BYTES, (self.aoff, self.ARENA_BYTES)
        v = self.arena[:, off // 2:(off + nb) // 2]
        if dt != BF16:
            v = v.bitcast(dt)
        v = v[:, 0:n]
        if len(shape) == 3:
            v = v.rearrange("p (a b) -> p a b", a=shape[1])
        elif len(shape) == 4:
            v = v.rearrange("p (a b c) -> p a b c", a=shape[1], b=shape[2])
        return v

    def bank(self):
        return self.pbank()

    def prefetch(self, key, parts):
        self.pre[key] = self.wload(parts)

    def wload(self, parts, key=None):
        if key is not None and key in self.pre:
            return self.pre.pop(key)
        i = self.slot_rr % len(self.slots)
        self.slot_rr += 1
        sl, r = self.slots[i], self.slotr[i]
        for (dst_fn, src) in parts:
            self.S.dma("poolw", dst_fn(sl), src, writes=[r])
        return sl, r

    def slot_view(self, sl, kk):
        return sl[:, 0:kk * self.slot_w].rearrange("p (k n) -> p k n", k=kk)

    def _build(self):
        nc, S = self.nc, self.S
        inp, outp, sb = self.inp, self.outp, self.sb
        xT_d = inp("xT", [D, T])
        cvec_d = inp("cvec", [128, 8])
        w_ada = inp("w_ada", [NL, D, 9 * D])
        b_adaT = inp("b_adaT", [NL, 128, 72])
        gvec_d = inp("gvec", [128, NL * 24 + 8])
        wd = {}
        for nm, shp in (("w1_gate", [NL, D, DFF]), ("w1_up", [NL, D, DFF]), ("w1_down", [NL, DFF, D]),
                        ("w_in", [NL, D, NIN]), ("w_out", [NL, D, D]),
                        ("w2_gate", [NL, D, DFF]), ("w2_up", [NL, D, DFF]), ("w2_down", [NL, DFF, D])):
            wd[nm] = inp(nm, shp)
        yT_d = outp("yT", [D, T])

        xT = sb("xT", [128, 8, T], F32)
        hT = sb("hT", [128, 8, T], BF16)
        self.ARENA_BYTES = 84 * 1024
        self.arena = sb("arena", [128, self.ARENA_BYTES // 2], BF16)
        self.slot_w = 640
        self.slots = [sb("slot%d" % i, [128, SLOT], BF16) for i in range(4)]
        self.slotr = RL(4, "slot")
        self.slot_rr = 0
        self.pre = {}
        self.ps = [nc.alloc_psum_tensor("ps%d" % i, [128, 512], F32) for i in range(8)]
        self.psr = [R("ps%d" % i, excl=True) for i in range(8)]
        self.bank_rr = 0
        self.bank_pool = list(range(8))
        ones_bf = sb("ones_bf", [128, 128], BF16)
        cvec = sb("cvec_sb", [128, 8], F32)
        sc_bf = sb("sc_bf", [128, 8], BF16)
        modT = sb("modT", [128, NL, 72], F32)
        badaT = sb("badaT", [128, NL, 72], F32)
        gvec = sb("gvec_sb", [128, NL * 24 + 8], F32)
        Acoef = sb("Acoef", [128, NL, 3, 8], F32)
        Gcoef = sb("Gcoef", [128, NL, 3, 8], F32)
        sqb = sb("sqb", [128, 2, 512], BF16)
        f32s = sb("f32s", [128, 3, 512], F32)
        rstd = sb("rstd", [128, 512], F32)
        r_ones, r_cvec, r_sc, r_gvec, r_rstd = R("ones"), R("cvec"), R("sc"), R("gvec"), R("rstd")
        r_mod = RL(NL, "mod")
        r_bada = R("bada")
        r_coef = RL(NL, "coef")
        r_sq = RL(2, "sq")
        r_f32s = RL(3, "f32s")
        xr = [[R("x%d_%d" % (k, c)) for c in range(2)] for k in range(8)]
        hr = [[R("h%d_%d" % (k, c)) for c in range(2)] for k in range(8)]
        self.sq_rr = 0
        self.f32_rr = 0

        def CH(c):
            return slice(c * 512, (c + 1) * 512)

        xv = xT_d.rearrange("(k p) t -> p k t", p=128)
        for k in range(8):
            S.dma("sp", xT[:, k, :], xv[:, k, :], writes=[xr[k][0], xr[k][1]])
        S.dma("sp", cvec[:], cvec_d, writes=[r_cvec])
        S.dma("sp", gvec[:], gvec_d, writes=[r_gvec])
        for l in range(NL):
            S.dma("sp", badaT[:, l, :], b_adaT[l], writes=[r_bada])
        S.op("dve", lambda e: e.memset(ones_bf[:], 1.0), writes=[r_ones])
        act(S, sc_bf[:], cvec[:], AF.Silu, [r_cvec], [r_sc])

        mslots = [sb("mslot%d" % i, [128, 8, 256], BF16) for i in range(2)]
        r_ms = RL(2, "mslot")
        r_modls = [[R("mod%d_%d" % (l, s_)) for s_ in range(3)] for l in range(NL)]
        jobs = [(l, q) for l in range(NL) for q in range(36)]
        st = {"dma": 0, "mm": 0, "fin": set()}

        def mod_dma(j):
            l, q = jobs[j]
            wv = w_ada[l].rearrange("(k p) n -> p k n", p=128)
            S.dma("poolw", mslots[j % 2][:], wv[:, :, q * 256:(q + 1) * 256], writes=[r_ms[j % 2]])

        def mod_mm(j, bank=None):
            l, q = jobs[j]
            if bank is None:
                pb, pr = self.bank()
                c0 = 0
            else:
                pb, pr, c0 = bank
            groups = []
            for jj in range(2):
                for k in range(8):
                    groups.append((pb[:, c0 + jj:c0 + jj + 1], mslots[j % 2][:, k, jj * 128:(jj + 1) * 128], sc_bf[:, k:k + 1],
                                   k == 0, k == 7))
            mm(S, groups, [r_ms[j % 2], r_sc], [pr])
            s_ = q // 12
            tt(S, "dve", modT[:, l, q * 2:q * 2 + 2], pb[:, c0:c0 + 2], badaT[:, l, q * 2:q * 2 + 2], ALU.add,
               [pr, r_bada], [r_modls[l][s_]])

        def mod_pump(n=1, bank=None):
            for _ in range(n):
                if st["dma"] < len(jobs) and st["dma"] - st["mm"] < 2:
                    mod_dma(st["dma"])
                    st["dma"] += 1
                if st["mm"] < st["dma"] - 1 or (st["dma"] == len(jobs) and st["mm"] < st["dma"]):
                    mod_mm(st["mm"], bank)
                    st["mm"] += 1

        def mod_require(l, s_):
            last = l * 36 + s_ * 12 + 11
            while st["mm"] <= last:
                mod_pump()
            if (l, s_) in st["fin"]:
                return
            st["fin"].add((l, s_))
            r = r_modls[l][s_]
            ts(S, "dve", Acoef[:, l, s_, :], modT[:, l, (3 * s_ + 1) * 8:(3 * s_ + 2) * 8], 1.0, ALU.add, [r], [r])
            tt(S, "dve", Acoef[:, l, s_, :], Acoef[:, l, s_, :], gvec[:, l * 24 + s_ * 8:l * 24 + s_ * 8 + 8],
               ALU.mult, [r, r_gvec], [r])
            ts(S, "dve", Gcoef[:, l, s_, :], modT[:, l, (3 * s_ + 2) * 8:(3 * s_ + 3) * 8],
               0.5 if s_ != 1 else 1.0, ALU.mult, [r], [r])

        self.mod_pump = mod_pump
        self.mod_require = mod_require

        def rms_rstd(c, src_fn, src_regs, nk, inv_n, ones_l):
            pb, pr = self.bank()
            for k in range(nk):
                i = self.sq_rr % 2
                self.sq_rr += 1
                act(S, sqb[:, i, :], src_fn(k), AF.Square, [src_regs[k]], [r_sq[i]])
                mm(S, [(pb[:], ones_l, sqb[:, i, :], k == 0, k == nk - 1)], [r_sq[i], r_ones], [pr])
            act(S, rstd[:], pb[:], AF.Ln, [pr], [r_rstd], bias=EPS, scale=inv_n)
            act(S, rstd[:], rstd[:], AF.Exp, [r_rstd], [r_rstd], scale=-0.5)

        def norm_mod(l, s):
            for c in range(2):
                rms_rstd(c, lambda k: xT[:, k, CH(c)], [xr[k][c] for k in range(8)], 8, 1.0 / D, ones_bf[:])
                for k in range(8):
                    i = self.f32_rr % 3
                    self.f32_rr += 1
                    tt(S, "dve", f32s[:, i, :], xT[:, k, CH(c)], rstd[:], ALU.mult,
                       [xr[k][c], r_rstd], [r_f32s[i]])
                    act(S, hT[:, k, CH(c)], f32s[:, i, :], AF.Identity, [r_f32s[i], r_modls[l][s]],
                        [hr[k][c]], bias=modT[:, l, 3 * s * 8 + k:3 * s * 8 + k + 1],
                        scale=Acoef[:, l, s, k:k + 1])

        def resid_add(l, s, dt_, c, pb, pr):
            i = self.f32_rr % 3
            self.f32_rr += 1
            act(S, f32s[:, i, :], pb[:], AF.Copy, [pr, r_modls[l][s]], [r_f32s[i]], scale=Gcoef[:, l, s, dt_:dt_ + 1])
            tt(S, "dve", xT[:, dt_, CH(c)], xT[:, dt_, CH(c)], f32s[:, i, :], ALU.add,
               [xr[dt_][c], r_f32s[i]], [xr[dt_][c]])

        def ffn(l, s, wg, wu, wdn):
            self.arena_reset()
            aT = self.carve([128, NFF, T], BF16)
            ar = [[R("a%d_%d" % (j, c)) for c in range(2)] for j in range(NFF)]
            sg = [self.carve([128, 512], F32) for _ in range(3)]
            r_sg = RL(3, "sg")
            sg_rr = 0
            mod_require(l, s)
            norm_mod(l, s)
            wgv = wg[l].rearrange("(k p) n -> p k n", p=128)
            wuv = wu[l].rearrange("(k p) n -> p k n", p=128)
            for g in range(NFF // 2):
                c0 = g * 256
                sl, sr = self.wload(self.parts_gu(wg, wu, l, g), key=("gu", l, s, g))
                s3 = self.slot_view(sl, 8)
                if l == 0:
                    mod_pump(1)
                for jj in range(2):
                    j = g * 2 + jj
                    for c in range(2):
                        pg, prg = self.bank()
                        pu, pru = self.bank()
                        groups = []
                        for k in range(8):
                            groups.append((pg[:], s3[:, k, jj * 128:(jj + 1) * 128], hT[:, k, CH(c)], k == 0, k == 7))
                        for k in range(8):
                            groups.append((pu[:], s3[:, k, 256 + jj * 128:256 + (jj + 1) * 128], hT[:, k, CH(c)],
                                           k == 0, k == 7))
                        mm(S, groups, [sr] + [hr[k][c] for k in range(8)], [prg, pru])
                        i = sg_rr % 3
                        sg_rr += 1
                        act(S, sg[i], pg[:], AF.Silu, [prg], [r_sg[i]])
                        tt(S, "dve", aT[:, j, CH(c)], sg[i], pu[:], ALU.mult, [r_sg[i], pru], [ar[j][c]])
            wdv = wdn[l].rearrange("(j p) n -> p j n", p=128)
            for dt_ in range(8):
                sl, sr = self.wload([(lambda sl_: sl_[:, 0:NFF * 128].rearrange("p (j n) -> p j n", j=NFF),
                                      wdv[:, :, dt_ * 128:(dt_ + 1) * 128])])
                if dt_ == 7:
                    if s == 0:
                        self.prefetch(("A1", l), self.parts_win(l, 0, 512))
                        self.prefetch(("A2", l), self.parts_win(l, 512, 512))
                        self.prefetch(("G", l), self.parts_win(l, 1024, 16))
                    elif l + 1 < NL:
                        for g_ in range(2):
                            self.prefetch(("gu", l + 1, 0, g_), self.parts_gu(wd["w1_gate"], wd["w1_up"], l + 1, g_))
                s3 = sl[:, 0:NFF * 128].rearrange("p (j n) -> p j n", j=NFF)
                for c in range(2):
                    pb, pr = self.bank()
                    groups = [(pb[:], s3[:, j, :], aT[:, j, CH(c)], j == 0, j == NFF - 1) for j in range(NFF)]
                    mm(S, groups, [sr] + [ar[j][c] for j in range(NFF)], [pr])
                    resid_add(l, s, dt_, c, pb, pr)

        self.wd = wd
        self.ctx = dict(xT=xT, hT=hT, xr=xr, hr=hr, CH=CH, rms_rstd=rms_rstd, norm_mod=norm_mod,
                        resid_add=resid_add, ones_bf=ones_bf, r_ones=r_ones, gvec=gvec, r_gvec=r_gvec,
                        f32s=f32s, r_f32s=r_f32s, rstd=rstd, r_rstd=r_rstd, modT=modT, sqb=sqb, r_sq=r_sq)


        self.ARENA_BYTES = 84 * 1024
        LPW = 304
        self.LP = dict(BG=0, GM=16, GQK=272, SK=274, CV=278, HB=302)
        d = {}
        d["lp"] = inp("lp", [128, NL, LPW])
        d["cflags"] = inp("cflags", [128, 4])
        d["ident"] = inp("ident", [128, 128])
        d["blockones"] = inp("blockones", [128, 128])
        d["tri"] = inp("tri", [128, 4, 128])
        d["ropeP"] = inp("ropeP", [128, 128])
        d["ropeCS"] = inp("ropeCS", [128, 2, T])
        d["swamask"] = inp("swamask", [128, 8, 384])
        d["gmAB"] = inp("gmAB", [5, 2, T])
        d["fw1"] = inp("fw1", [33, NL, 64])
        d["fw2"] = inp("fw2", [64, NL, 64])
        d["fw3"] = inp("fw3", [64, NL, 256])
        d["fvec"] = inp("fvec", [64, NL, 3])
        d["fb3"] = inp("fb3", [1, NL, 256])
        d["featsT"] = inp("featsT", [33, T])
        d["window"] = inp("window", [128, 8, 256])
        d["dftF"] = inp("dftF", [2, T, T])
        d["dftG"] = inp("dftG", [2, T, T])
        d["sel"] = inp("sel", [4, 130])
        d["C0"] = inp("C0", [NL, 2, 128, 2, 65])
        d["m0rep"] = inp("m0rep", [128, NL, 2, 2])
        d["m0c"] = inp("m0c", [4, NL * 2])
        d["gkT"] = inp("gkT", [NL, 2, 128, 256])
        d["gvp"] = inp("gvp", [NL, 2, 128, 2, 192])
        d["skT"] = inp("skT", [NL, 2, 128, 256])
        d["svp"] = inp("svp", [NL, 2, 128, 2, 192])
        d["o_gk"] = outp("o_gk", [NL, 128, T])
        d["o_gv"] = outp("o_gv", [NL, T, 128])
        d["o_sk"] = outp("o_sk", [NL, 128, T])
        d["o_sv"] = outp("o_sv", [NL, T, 128])
        d["o_C"] = outp("o_C", [NL, 2, 4, 128, 2, 65])
        d["o_m"] = outp("o_m", [NL, 2, 4, 4])
        self.d = d
        if DEBUG:
            self.dbg_ymix = outp("dbg_ymix", [NL, 128, 8, T])
        k = {}
        k["lp"] = sb("lp", [128, NL, LPW]); k["cflags"] = sb("cflags", [128, 4])
        k["ident_f"] = sb("ident_f", [128, 128]); k["ident_bf"] = sb("ident_bf", [128, 128], BF16)
        k["blockones"] = sb("blockones", [128, 128], BF16)
        k["tri"] = sb("tri", [128, 4, 128]); k["ropeP"] = sb("ropeP", [128, 128])
        k["ones_f"] = sb("ones_f", [128, 128])
        k["fw1"] = sb("fw1", [33, NL, 64]); k["fw2"] = sb("fw2", [64, NL, 64]); k["fw3"] = sb("fw3", [64, NL, 256])
        k["fvec"] = sb("fvec", [64, NL, 3]); k["fb3"] = sb("fb3", [1, NL, 256]); k["fs"] = sb("fs", [64, NL, 4])
        k["sel"] = sb("sel", [4, 130]); k["m0rep"] = sb("m0rep", [128, NL, 2, 2]); k["m0c"] = sb("m0c", [4, NL * 2])
        k["Clo"] = sb("Clo", [128, 2, 2, 65]); k["Chi"] = sb("Chi", [128, 2, 2, 65])
        k["Cblo"] = sb("Cblo", [128, 2, 2, 66], BF16); k["Cbhi"] = sb("Cbhi", [128, 2, 2, 66], BF16)
        k["cvb"] = sb("cvb", [128, NL, 6, 2])
        self.k = k
        r_k = R("consts")
        self.r_k = r_k
        for nm in ("lp", "cflags", "tri", "ropeP", "fw1", "fw2", "fw3", "fvec", "fb3", "sel", "m0rep", "m0c"):
            S.dma("sp", k[nm][:], d[nm], writes=[r_k])
        S.dma("sp", k["ident_f"][:], d["ident"], writes=[r_k])
        S.dma("pool", k["ident_bf"][:], d["ident"], writes=[r_k])
        S.dma("pool", k["blockones"][:], d["blockones"], writes=[r_k])
        S.op("pool", lambda e: e.memset(k["ones_f"][:], 1.0), writes=[r_k])
        for nm in ("Clo", "Chi", "Cblo", "Cbhi"):
            S.op("pool", lambda e, nm=nm: e.memset(k[nm][:], 0.0), writes=[r_k])
        i2p = float(1.0 / (2 * math.pi))
        ts(S, "dve", k["fs"][:, :, 0:1], k["fvec"][:, :, 2:3], i2p, ALU.mult, [r_k], [r_k])
        tt(S, "dve", k["fs"][:, :, 1:2], k["fs"][:, :, 0:1], k["fvec"][:, :, 0:1], ALU.mult, [r_k], [r_k])
        tt(S, "dve", k["fs"][:, :, 2:3], k["fs"][:, :, 0:1], k["fvec"][:, :, 1:2], ALU.mult, [r_k], [r_k])
        CV = self.LP["CV"]
        for l in range(NL):
            cvv = k["lp"][:, l, CV:CV + 24].rearrange("p (a b) -> p a b", a=6)
            for (j, col) in ((0, 0), (1, 2)):
                ts(S, "dve", k["cvb"][:, l, :, j:j + 1], cvv[:, :, col:col + 1], k["cflags"][:, 2:3], ALU.mult,
                   [r_k], [r_k], s2=-1.0, op1=ALU.mult)

        for l in range(NL):
            ffn(l, 0, wd["w1_gate"], wd["w1_up"], wd["w1_down"])
            self.mixer(l)
            ffn(l, 2, wd["w2_gate"], wd["w2_up"], wd["w2_down"])

        gfo = NL * 24
        yv = yT_d.rearrange("(k p) t -> p k t", p=128)
        self.arena_reset()
        ost_ = self.carve([128, 2, 512], F32)
        ost = [ost_[:, 0, :], ost_[:, 1, :]]
        r_ost = RL(2, "ost")
        o_rr = 0
        for c in range(2):
            rms_rstd(c, lambda k: xT[:, k, CH(c)], [xr[k][c] for k in range(8)], 8, 1.0 / D, ones_bf[:])
            for k in range(8):
                i = self.f32_rr % 3
                self.f32_rr += 1
                tt(S, "dve", f32s[:, i, :], xT[:, k, CH(c)], rstd[:], ALU.mult, [xr[k][c], r_rstd], [r_f32s[i]])
                o = o_rr % 2
                o_rr += 1
                act(S, ost[o], f32s[:, i, :], AF.Copy, [r_f32s[i], r_gvec], [r_ost[o]],
                    scale=gvec[:, gfo + k:gfo + k + 1])
                S.dma("sp", yv[:, k, CH(c)], ost[o], reads=[r_ost[o]])
        S.finish()

    def mixer(self, l):
        S = self.S
        C = self.ctx
        hT, hr, CH = C["hT"], C["hr"], C["CH"]
        self.arena_reset()
        ymix = self.carve([128, 8, T], BF16)
        ymr = [[R("ym%d_%d" % (k, c)) for c in range(2)] for k in range(8)]
        base = self.aoff
        self.mod_require(l, 1)
        C["norm_mod"](l, 1)
        win = self.wd["w_in"][l].rearrange("(k p) n -> p k n", p=128)
        self.bank_pool = list(range(8))
        self.mix_mlstm(l, ymix, ymr, win)
        self.arena_reset(base)
        if STAGE >= 3:
            self.mix_attn(l, ymix, ymr, win, glob=True)
            self.arena_reset(base)
            self.mix_attn(l, ymix, ymr, win, glob=False)
            self.arena_reset(base)
        if STAGE >= 4:
            self.mix_hyena(l, ymix, ymr, win)
        self.prefetch(("wout", l, 0), self.parts_wout(l, 0))
        self.prefetch(("wout", l, 1), self.parts_wout(l, 1))
        wd_ = self.wd
        for g in range(2):
            self.prefetch(("gu", l, 2, g), self.parts_gu(wd_["w2_gate"], wd_["w2_up"], l, g))
        self.bank_pool = list(range(8))
        if DEBUG:
            f32s, r_f32s = C["f32s"], C["r_f32s"]
            for k in range(8):
                for c in range(2):
                    i = self.f32_rr % 3
                    self.f32_rr += 1
                    act(S, f32s[:, i, :], ymix[:, k, CH(c)], AF.Copy, [ymr[k][c]], [r_f32s[i]])
                    S.dma("sp", self.dbg_ymix[l, :, k, c * 512:(c + 1) * 512], f32s[:, i, :], reads=[r_f32s[i]])
        wov = self.wd["w_out"][l].rearrange("(k p) n -> p k n", p=128)
        for half in range(2):
            sl, sr = self.wload(self.parts_wout(l, half), key=("wout", l, half))
            s3 = self.slot_view(sl, 8)
            for j in range(4):
                dt_ = half * 4 + j
                for c in range(2):
                    pb, pr = self.bank()
                    groups = [(pb[:], s3[:, k, j * 128:(j + 1) * 128], ymix[:, k, CH(c)], k == 0, k == 7) for k in range(8)]
                    mm(S, groups, [sr] + [ymr[k][c] for k in range(8)], [pr])
                    C["resid_add"](l, 1, dt_, c, pb, pr)

    def parts_attn(self, l, glob):
        win = self.wd["w_in"][l].rearrange("(k p) n -> p k n", p=128)
        sv8 = lambda sl_: self.slot_view(sl_, 8)
        q0 = 1040 if glob else 1552
        k0, v0 = q0 + 256, q0 + 384
        return [(lambda sl_: sv8(sl_)[:, :, 0:256], win[:, :, q0:q0 + 256]),
                (lambda sl_: sv8(sl_)[:, :, 256:384], win[:, :, k0:k0 + 128]),
                (lambda sl_: sv8(sl_)[:, :, 384:448], win[:, :, k0 + 64:k0 + 128]),
                (lambda sl_: sv8(sl_)[:, :, 448:512], win[:, :, k0:k0 + 64]),
                (lambda sl_: sv8(sl_)[:, :, 512:640], win[:, :, v0:v0 + 128])]

    def parts_win(self, l, c0, n):
        win = self.wd["w_in"][l].rearrange("(k p) n -> p k n", p=128)
        return [(lambda sl_: self.slot_view(sl_, 8)[:, :, 0:n], win[:, :, c0:c0 + n])]

    def parts_wout(self, l, half):
        wov = self.wd["w_out"][l].rearrange("(k p) n -> p k n", p=128)
        return [(lambda sl_: self.slot_view(sl_, 8)[:, :, 0:512], wov[:, :, half * 512:(half + 1) * 512])]

    def parts_gu(self, wg, wu, l, g):
        wgv = wg[l].rearrange("(k p) n -> p k n", p=128)
        wuv = wu[l].rearrange("(k p) n -> p k n", p=128)
        c0 = g * 256
        return [(lambda sl_: self.slot_view(sl_, 8)[:, :, 0:256], wgv[:, :, c0:c0 + 256]),
                (lambda sl_: self.slot_view(sl_, 8)[:, :, 256:512], wuv[:, :, c0:c0 + 256])]

    def pbank(self):
        i = self.bank_pool[self.bank_rr % len(self.bank_pool)]
        self.bank_rr += 1
        return self.ps[i], self.psr[i]

    def proj_fm(self, c, s3, col0, pb):
        hT, CH = self.ctx["hT"], self.ctx["CH"]
        return [(pb[:], s3[:, k, col0:col0 + 128], hT[:, k, CH(c)], k == 0, k == 7) for k in range(8)]

    def proj_tok(self, tt_, s3, col0, ncols, pb, pc0):
        hT = self.ctx["hT"]
        return [(pb[:, pc0:pc0 + ncols], hT[:, k, tt_ * 128:(tt_ + 1) * 128], s3[:, k, col0:col0 + ncols], k == 0, k == 7)
                for k in range(8)]

    def mix_mlstm(self, l, ymix, ymr, win):
        S, k_, d = self.S, self.k, self.d
        C = self.ctx
        hr, CH = C["hr"], C["CH"]
        LP = self.LP
        lp = k_["lp"]
        r_k = self.r_k
        hall = [hr[k][c] for k in range(8) for c in range(2)]
        sv8 = lambda sl_: self.slot_view(sl_, 8)
        slA1, srA1 = self.wload(self.parts_win(l, 0, 512), key=("A1", l))
        slA2, srA2 = self.wload(self.parts_win(l, 512, 512), key=("A2", l))
        slG, srG = self.wload(self.parts_win(l, 1024, 16), key=("G", l))
        sA1, sA2, sG = sv8(slA1), sv8(slA2), sv8(slG)
        self.prefetch(("attn", l, True), self.parts_attn(l, True))
        if CUT == 1:
            return
        aqT = self.carve([128, 2, T], BF16)
        akp = self.carve([128, 4, T], BF16)
        ktok = self.carve([128, 8, 256], BF16)
        vaug = self.carve([128, 8, 4, 66], BF16)
        sgo = self.carve([128, 8, 256], BF16)
        gts = self.carve([128, 8, 16], F32)
        lf = self.carve([128, 8, 8], F32)
        cum = self.carve([128, 8, 16], F32)
        call = self.carve([128, 8, 8], F32)
        wall = self.carve([128, 8, 8], F32)
        wkl = self.carve([128, 8, 8], F32)
        wkall = self.carve([128, 8, 8], F32)
        wkm = self.carve([128, 8, 8], F32)
        dec = self.carve([128, 8, 4], F32)
        lfB = self.carve([128, 8, 128], F32)
        E = self.carve([128, 8, 128], F32)
        AT = self.carve([128, 2, 4, 128], BF16)
        hf = self.carve([128, 8, 256], F32)
        numt = self.carve([128, 2, 260], F32)
        h64 = self.carve([128, 2, 256], F32)
        dsm = self.carve([128, 2, 8], F32)
        kwp = self.carve([128, 2, 4, 192], BF16)
        yatok = self.carve([128, 8, 256], BF16)
        snap = self.carve([128, 2, 4, 130], F32)
        e0 = self.carve([128, 2, 2], F32)
        ssall = self.carve([128, 8, 4], F32)
        sq1 = self.carve([128, 2, 256], F32)
        mst = self.carve([128, 64], F32)
        scl = self.carve([128, 2, 8], F32)
        r_aq, r_akp = RL(2, "aq"), RL(2, "akp")
        r_ktok, r_vaug, r_sgo, r_gts = RL(8, "ktok"), RL(8, "vaug"), RL(8, "sgo"), R("gts")
        r_gate = R("gate")
        r_lfB, r_E, r_AT = RL(2, "lfB"), RL(2, "E"), RL(2, "AT")
        r_hf = RL(8, "hf")
        r_num, r_tmpn, r_h64, r_dsm, r_kwp = RL(2, "num"), RL(2, "tmpn"), RL(2, "h64"), RL(2, "dsm"), RL(2, "kwp")
        r_C, r_Cb = RL(2, "C"), RL(2, "Cb")
        r_snap = [[R("snap") for _ in range(4)] for _ in range(2)]
        r_ms, r_scl = R("mst"), R("scl")
        r_ya = RL(8, "ya")
        r_ss = R("ss")
        r_sq1 = RL(2, "sq1")
        S.op("pool", lambda e: e.memset(akp, 0.0), writes=r_akp)
        S.op("pool", lambda e: e.memset(kwp, 0.0), writes=r_kwp)
        S.op("pool", lambda e: e.memset(vaug[:, :, :, 64:65], 1.0), writes=r_vaug)
        if CUT == 2:
            return
        for t2 in range(2):
            for c in range(2):
                pb, pr = self.pbank()
                mm(S, self.proj_fm(c, sA1, t2 * 128, pb), [srA1] + hall, [pr])
                act(S, aqT[:, t2, CH(c)], pb[:], AF.Copy, [pr], [r_aq[c]])
        if CUT == 21:
            return
        for t2 in range(2):
            for c in range(2):
                pb, pr = self.pbank()
                mm(S, self.proj_fm(c, sA1, 256 + t2 * 128, pb), [srA1] + hall, [pr])
                act(S, akp[0:64, 2 * t2, CH(c)], pb[0:64, :], AF.Copy, [pr], [r_akp[c]])
                cp(S, "dve", akp[64:128, 2 * t2 + 1, CH(c)], pb[64:128, :], [pr], [r_akp[c]])
        if CUT == 22:
            return
        BG = LP["BG"]
        for t_ in range(8):
            p1, pr1 = self.pbank()
            p2, pr2 = self.pbank()
            g = self.proj_tok(t_, sA1, 256, 256, p1, 0) + self.proj_tok(t_, sA2, 0, 256, p1, 256)
            g += self.proj_tok(t_, sA2, 256, 256, p2, 0) + self.proj_tok(t_, sG, 0, 16, p2, 256)
            mm(S, g, [srA1, srA2, srG] + hall, [pr1, pr2])
            if CUT == 23:
                continue
            act(S, ktok[:, t_, :], p1[:, 0:256], AF.Copy, [pr1], [r_ktok[t_]])
            cp(S, "dve", vaug[:, t_, :, 0:64], p1[:, 256:512].rearrange("p (a b) -> p a b", a=4), [pr1], [r_vaug[t_]])
            if CUT == 24:
                continue
            act(S, sgo[:, t_, :], p2[:, 0:256], AF.Sigmoid, [pr2], [r_sgo[t_]])
            tt(S, "dve", gts[:, t_, :], p2[:, 256:272], lp[:, l, BG:BG + 16], ALU.add, [pr2, r_k], [r_gts])
            self.mod_pump()
        if CUT in (3, 23, 24):
            return
        ai, af = gts[:, :, 0:8], gts[:, :, 8:16]
        act(S, lf, af, AF.Exp, [r_gts], [r_gate], scale=-1.0)
        act(S, lf, lf, AF.Ln, [r_gate], [r_gate], bias=1.0)
        ts(S, "dve", lf, lf, -1.0, ALU.mult, [r_gate], [r_gate])
        tri = k_["tri"]
        pbc, prc = self.pbank()
        g = []
        for t_ in range(8):
            g.append((pbc[:, t_ * 16:t_ * 16 + 4], tri[:, 0, :], lf[:, t_, 0:4], True, True))
            g.append((pbc[:, t_ * 16 + 4:t_ * 16 + 8], tri[:, 1, :], lf[:, t_, 4:8], True, True))
            g.append((pbc[:, t_ * 16 + 8:t_ * 16 + 16], k_["ones_f"][:], lf[:, t_, 0:8], True, True))
        mm(S, g, [r_gate, r_k], [prc])
        cp(S, "dve", cum, pbc[:, 0:128].rearrange("p (a b) -> p a b", a=8), [prc], [r_gate])
        bc, bt = cum[:, :, 0:8], cum[:, :, 8:16]
        tt(S, "dve", call, ai, bc, ALU.subtract, [r_gts, r_gate], [r_gate])
        ts(S, "dve", call, call, LN8, ALU.add, [r_gate], [r_gate])
        act(S, wall, bc, AF.Exp, [r_gate], [r_gate])
        tt(S, "dve", wkl, call, bt, ALU.add, [r_gate], [r_gate])
        act(S, wkall, wkl, AF.Exp, [r_gate], [r_gate])
        ts(S, "dve", wkm, wkl, -LN8, ALU.add, [r_gate], [r_gate])
        for g_ in range(2):
            rows = slice(g_ * 64, (g_ + 1) * 64)
            act(S, dec[rows, :, :], cum[rows, :, 8 + g_:16:2], AF.Exp, [r_gate], [r_gate])
        if CUT == 4:
            return
        ident_f = k_["ident_f"]
        for dr in range(2):
            pbm, prm = self.pbank()
            g = [(pbm[0:4, t_:t_ + 1], lf[:, t_, dr * 4:dr * 4 + 4], k_["ones_f"][:, 0:1], True, True) for t_ in range(8)]
            mm(S, g, [r_gate, r_k], [prm])
            cp(S, "dve", mst[0:4, dr * 8:dr * 8 + 8], pbm[0:4, 0:8], [prm], [r_ms])
            for hh in range(2):
                pbt, prt = self.pbank()
                for q in range(4):
                    t_ = hh * 4 + q
                    S.op("pe", lambda e, t_=t_, q=q, pbt=pbt: e.transpose(pbt[0:4, q * 128:(q + 1) * 128],
                                                                          wkm[:, t_, dr * 4:dr * 4 + 4], ident_f[:]),
                         [r_gate, r_k], [prt])
                S.op("dve", lambda e, pbt=pbt, hh=hh: e.tensor_reduce(
                    out=mst[0:4, 16 + dr * 8 + hh * 4:16 + dr * 8 + hh * 4 + 4],
                    in_=pbt[0:4, :].rearrange("p (a b) -> p a b", a=4), axis=AX.X, op=ALU.max), [prt], [r_ms])
            bv_ = mst[0:4, dr * 8:dr * 8 + 8].rearrange("p (a b) -> p a b", a=4)
            av_ = mst[0:4, 16 + dr * 8:16 + dr * 8 + 8].rearrange("p (a b) -> p a b", a=4)
            fi, se = (0, 1) if dr == 0 else (1, 0)
            mf = mst[0:4, 32 + dr * 4:32 + dr * 4 + 4]
            ts(S, "dve", mf, bv_[:, :, fi], k_["m0c"][0:4, l * 2 + dr:l * 2 + dr + 1], ALU.add, [r_ms, r_k], [r_ms])
            tt(S, "dve", mf, mf, av_[:, :, fi], ALU.max, [r_ms], [r_ms])
            tt(S, "dve", mf, mf, bv_[:, :, se], ALU.add, [r_ms], [r_ms])
            tt(S, "dve", mf, mf, av_[:, :, se], ALU.max, [r_ms], [r_ms])
            S.dma("sp", d["o_m"][l, dr], mf, reads=[r_ms])
            en = mst[0:4, 40 + dr * 4:40 + dr * 4 + 4]
            act(S, en, mf, AF.Exp, [r_ms], [r_ms], scale=-1.0)
            rhs2 = mst[0:4, 48 + dr * 8:48 + dr * 8 + 8]
            tt(S, "dve", rhs2.rearrange("p (a b) -> p a b", a=4), en.unsqueeze(2).to_broadcast([4, 4, 2]),
               k_["sel"][0:4, 128:130].unsqueeze(1).to_broadcast([4, 4, 2]), ALU.mult, [r_ms, r_k], [r_ms])
            pbs, prs = self.pbank()
            mm(S, [(pbs[:, 0:8], k_["sel"][0:4, 0:128], rhs2, True, True)], [r_ms, r_k], [prs])
            cp(S, "dve", scl[:, dr, :], pbs[:, 0:8], [prs], [r_scl])
        if CUT == 5:
            return
        Clo, Chi, Cblo, Cbhi = k_["Clo"], k_["Chi"], k_["Cblo"], k_["Cbhi"]
        act(S, e0, k_["m0rep"][:, l, :, :], AF.Exp, [r_k], [r_gate])
        halves = ((slice(0, 64), Clo, Cblo), (slice(64, 128), Chi, Cbhi))
        for dr in range(2):
            S.dma("sp", Clo[:, dr, :, :], d["C0"][l, dr, :, :, :], writes=[r_C[dr]])
            tt(S, "dve", Clo[:, dr, :, :], Clo[:, dr, :, :], e0[:, dr, :].unsqueeze(2).to_broadcast([128, 2, 65]),
               ALU.mult, [r_C[dr], r_gate], [r_C[dr]])
            for (rows, Cx, Cbx) in halves:
                act(S, Cbx[rows, dr, :, 0:65], Clo[rows, dr, :, :], AF.Copy, [r_C[dr]], [r_Cb[dr]])
        keep = k_["cflags"][:, 1:2]

        tmpn2 = self.carve([128, 2, 2, 260], F32)
        r_tmpn2 = [[R("tn00"), R("tn01")], [R("tn10"), R("tn11")]]
        v3 = lambda ap: ap.rearrange("p (a b) -> p a b", a=4)

        def state_gen(dr, t_):
            tok = slice(t_ * 128, (t_ + 1) * 128)
            chs = slice(dr * 4, dr * 4 + 4)
            b_ = t_ % 2
            pst, prst = self.ps[3 + 4 * dr], self.psr[3 + 4 * dr]
            tt(S, "pool", kwp[:, dr, :, 64:128], ktok[:, t_, :].rearrange("p (a b) -> p a b", a=4),
               wkall[:, t_, chs].unsqueeze(2).to_broadcast([128, 4, 64]), ALU.mult, [r_ktok[t_], r_gate], [r_kwp[dr]])
            yield
            g = [(pst[:, h * 65:(h + 1) * 65], aqT[:, h // 2, tok], (Cblo if h % 2 == 0 else Cbhi)[:, dr, h // 2, 0:65], True, True)
                 for h in range(4)]
            for j in range(2):
                o = pst[:, 260 + j * 65:260 + (j + 1) * 65]
                g.append((o, kwp[:, dr, 2 * j, 64:192], vaug[:, t_, 2 * j, 0:65], True, False))
                g.append((o, kwp[:, dr, 2 * j + 1, 0:128], vaug[:, t_, 2 * j + 1, 0:65], False, True))
            mm(S, g, r_aq + [r_Cb[dr], r_kwp[dr], r_vaug[t_]], [prst])
            yield
            tt(S, "dve", v3(tmpn2[:, dr, b_, :]), v3(pst[:, 0:260]), wall[:, t_, chs].unsqueeze(2).to_broadcast([128, 4, 65]),
               ALU.mult, [prst, r_gate], [r_tmpn2[dr][b_]])
            yield
            tt(S, "dve", Clo[:, dr, :, :], Clo[:, dr, :, :],
               dec[:, t_, dr * 2:dr * 2 + 2].unsqueeze(2).to_broadcast([128, 2, 65]), ALU.mult,
               [r_C[dr], r_gate], [r_C[dr]])
            yield
            tt(S, "dve", Clo[:, dr, :, :], Clo[:, dr, :, :], pst[:, 260:390].rearrange("p (a b) -> p a b", a=2),
               ALU.add, [r_C[dr], prst], [r_C[dr]])
            yield
            end = (t_ % 2 == 1) if dr == 0 else (t_ % 2 == 0)
            if end:
                sq_ = t_ // 2
                tt(S, "dve", snap[:, dr, sq_, :].rearrange("p (a b) -> p a b", a=2), Clo[:, dr, :, :],
                   scl[:, dr, sq_ * 2:sq_ * 2 + 2].unsqueeze(2).to_broadcast([128, 2, 65]), ALU.mult,
                   [r_C[dr], r_scl], [r_snap[dr][sq_]])
                yield
                S.dma("sp", d["o_C"][l, dr, sq_], snap[:, dr, sq_, :].rearrange("p (a b) -> p a b", a=2),
                      reads=[r_snap[dr][sq_]])
                yield
                ts(S, "dve", Clo[:, dr, :, :], Clo[:, dr, :, :], keep, ALU.mult, [r_C[dr], r_k], [r_C[dr]])
                yield
            for (rows, Cx, Cbx) in halves:
                act(S, Cbx[rows, dr, :, 0:65], Clo[rows, dr, :, :], AF.Copy, [r_C[dr]], [r_Cb[dr]])
                yield

        def out_gen(dr, t_):
            tok = slice(t_ * 128, (t_ + 1) * 128)
            chs = slice(dr * 4, dr * 4 + 4)
            b_ = t_ % 2
            pbe, pre = self.ps[0 + 4 * dr], self.psr[0 + 4 * dr]
            pbs_, prs_ = self.ps[1 + 4 * dr], self.psr[1 + 4 * dr]
            pbi, pri = self.ps[2 + 4 * dr], self.psr[2 + 4 * dr]
            cp(S, "pool", lfB[:, chs, :], lf[:, t_, chs].unsqueeze(2).to_broadcast([128, 4, 128]), [r_gate], [r_lfB[dr]])
            yield
            g = []
            for h in range(4):
                o = pbe[:, h * 128:(h + 1) * 128]
                g.append((o, lfB[:, dr * 4 + h, :], tri[:, dr, :], True, False))
                g.append((o, ident_f[:], tri[:, 2 + dr, :], False, True))
            mm(S, g, [r_lfB[dr], r_k], [pre])
            yield
            g = [(pbs_[:, h * 128:(h + 1) * 128], akp[:, h, tok], aqT[:, h // 2, tok], True, True) for h in range(4)]
            mm(S, g, r_akp + r_aq, [prs_])
            yield
            for h in range(4):
                act(S, E[:, dr * 4 + h, :], pbe[:, h * 128:(h + 1) * 128], AF.Exp, [pre, r_gate], [r_E[dr]],
                    bias=call[:, t_, dr * 4 + h:dr * 4 + h + 1])
                yield
            tt(S, "dve", AT[:, dr, :, :], pbs_[:].rearrange("p (a b) -> p a b", a=4), E[:, chs, :], ALU.mult,
               [prs_, r_E[dr]], [r_AT[dr]])
            yield
            g = [(pbi[:, h * 65:(h + 1) * 65], AT[:, dr, h, :], vaug[:, t_, h, 0:65], True, True) for h in range(4)]
            mm(S, g, [r_AT[dr], r_vaug[t_]], [pri])
            yield
            tt(S, "dve", numt[:, dr, :], pbi[:, 0:260], tmpn2[:, dr, b_, :], ALU.add, [pri, r_tmpn2[dr][b_]], [r_num[dr]])
            yield
            nv = v3(numt[:, dr, :])
            dn, rd = dsm[:, dr, 0:4], dsm[:, dr, 4:8]
            ts(S, "dve", dn, nv[:, :, 64], -1.0, ALU.mult, [r_num[dr]], [r_dsm[dr]], s2=1.0, op1=ALU.max)
            yield
            tt(S, "dve", dn, dn, nv[:, :, 64], ALU.max, [r_num[dr], r_dsm[dr]], [r_dsm[dr]])
            yield
            recip(S, rd, dn, [r_dsm[dr]], [r_dsm[dr]])
            yield
            first = (t_ <= 3) if dr == 0 else (t_ >= 4)
            if first:
                tt(S, "dve", v3(hf[:, t_, :]), nv[:, :, 0:64], rd.unsqueeze(2).to_broadcast([128, 4, 64]), ALU.mult,
                   [r_num[dr], r_dsm[dr]], [r_hf[t_]])
                yield
            else:
                tt(S, "dve", v3(h64[:, dr, :]), nv[:, :, 0:64], rd.unsqueeze(2).to_broadcast([128, 4, 64]), ALU.mult,
                   [r_num[dr], r_dsm[dr]], [r_h64[dr]])
                yield
                tt(S, "dve", hf[:, t_, :], hf[:, t_, :], h64[:, dr, :], ALU.add, [r_h64[dr], r_hf[t_]], [r_hf[t_]])
                yield

        def zip_run(gens):
            gens = list(gens)
            while gens:
                for g__ in list(gens):
                    try:
                        next(g__)
                    except StopIteration:
                        gens.remove(g__)

        def pump_gen(npump):
            for _ in range(npump):
                for _ in range(5):
                    yield
                self.mod_pump(bank=(self.ps[2], self.psr[2], 300))
                yield

        order = [list(range(8)), list(range(7, -1, -1))]
        zip_run([state_gen(0, order[0][0]), state_gen(1, order[1][0])])
        for i in range(8):
            gens = [out_gen(0, order[0][i]), out_gen(1, order[1][i])]
            if i + 1 < 8:
                gens = [state_gen(0, order[0][i + 1]), state_gen(1, order[1][i + 1])] + gens
            gens.append(pump_gen(3))
            zip_run(gens)
        GM = LP["GM"]
        for t_ in range(8):
            b_ = t_ % 2
            tt(S, "dve", sq1[:, b_, :], hf[:, t_, :], hf[:, t_, :], ALU.mult, [r_hf[t_]], [r_sq1[b_]])
            S.op("dve", lambda e, t_=t_, b_=b_: e.tensor_reduce(out=ssall[:, t_, :],
                                                              in_=sq1[:, b_, :].rearrange("p (a b) -> p a b", a=4),
                                                              axis=AX.X, op=ALU.add), [r_sq1[b_]], [r_ss])
        act(S, ssall, ssall, AF.Ln, [r_ss], [r_ss], bias=EPS, scale=1.0 / 64)
        act(S, ssall, ssall, AF.Exp, [r_ss], [r_ss], scale=-0.5)
        ident_bf = k_["ident_bf"]
        for t_ in range(8):
            b_ = t_ % 2
            tt(S, "dve", sq1[:, b_, :].rearrange("p (a b) -> p a b", a=4), hf[:, t_, :].rearrange("p (a b) -> p a b", a=4),
               ssall[:, t_, :].unsqueeze(2).to_broadcast([128, 4, 64]), ALU.mult, [r_hf[t_], r_ss], [r_sq1[b_]])
            tt(S, "pool", h64[:, b_, :], sgo[:, t_, :], lp[:, l, GM:GM + 256], ALU.mult, [r_sgo[t_], r_k], [r_h64[b_]])
            tt(S, "dve", yatok[:, t_, :], sq1[:, b_, :], h64[:, b_, :], ALU.mult, [r_h64[b_], r_sq1[b_]], [r_ya[t_]])
            self.mod_pump()
            pbt, prt = self.pbank()
            pv = pbt[:].bitcast(BF16)
            for t2 in range(2):
                S.op("pe", lambda e, t2=t2, pv=pv, t_=t_: e.transpose(pv[:, t2 * 128:(t2 + 1) * 128],
                                                                      yatok[:, t_, t2 * 128:(t2 + 1) * 128], ident_bf[:]),
                     [r_ya[t_], r_k], [prt])
            c = t_ // 4
            for t2 in range(2):
                if t2 == 0:
                    act(S, ymix[:, t2, t_ * 128:(t_ + 1) * 128], pv[:, t2 * 128:(t2 + 1) * 128], AF.Copy, [prt], [ymr[t2][c]])
                else:
                    cp(S, "dve", ymix[:, t2, t_ * 128:(t_ + 1) * 128], pv[:, t2 * 128:(t2 + 1) * 128], [prt], [ymr[t2][c]])

    def mix_attn(self, l, ymix, ymr, win, glob):
        S, k_, d = self.S, self.k, self.d
        C = self.ctx
        hr, CH = C["hr"], C["CH"]
        LP = self.LP
        lp = k_["lp"]
        r_k = self.r_k
        hall = [hr[k][c] for k in range(8) for c in range(2)]
        sv8 = lambda sl_: self.slot_view(sl_, 8)
        q0 = 1040 if glob else 1552
        k0, v0 = q0 + 256, q0 + 384
        sl, sr = self.wload(self.parts_attn(l, glob), key=("attn", l, glob))
        if glob:
            self.prefetch(("attn", l, False), self.parts_attn(l, False))
        else:
            self.prefetch(("D1", l), self.parts_win(l, 2064, 512))
            self.prefetch(("D2", l), self.parts_win(l, 2576, 256))
        s3 = sv8(sl)
        ymb = 2 if glob else 4
        cs = self.carve([128, 2, T], F32)
        qpad = self.carve([128, 4, T], BF16)
        kfull = self.carve([128, 2, 1280], BF16)
        vpad = self.carve([128, 10, 2, 192], BF16)
        kout = self.carve([128, T], F32)
        vout = self.carve([128, 8, 128], F32)
        PT = self.carve([128, 3, 512], BF16)
        rden = self.carve([128, 2, 512], F32)
        raw = self.carve([128, 2, 512], F32)
        tb = self.carve([128, 2, 512], F32)
        gm = self.carve([128, 2, T], BF16)
        smask = self.carve([128, 8, 384], BF16) if not glob else None
        es = self.carve([128, 4], F32)
        r_cs, r_qp, r_kf, r_vp, r_kout, r_vout = R("cs"), RL(2, "qp"), R("kf"), RL(10, "vp"), R("kout"), R("vout")
        r_PT, r_rden, r_raw, r_tb, r_gm, r_sm, r_es = RL(3, "PT"), RL(2, "rden"), RL(2, "raw"), RL(2, "tb"), R("gm"), R("sm"), R("es")
        S.dma("sp", cs, d["ropeCS"], writes=[r_cs])
        S.op("pool", lambda e: e.memset(qpad, 0.0), writes=r_qp)
        S.op("pool", lambda e: e.memset(vpad[:, 2:10, :, :], 0.0), writes=r_vp[2:])
        kTd, vpd = (d["gkT"], d["gvp"]) if glob else (d["skT"], d["svp"])
        for x in range(2):
            S.dma("pool", kfull[:, x, 0:256], kTd[l, x], writes=[r_kf])
            S.dma("pool", vpad[:, x, :, :], vpd[l, x], writes=[r_vp[x]])
        if glob:
            S.dma("pool", gm[0:5, :, :], d["gmAB"], writes=[r_gm])
        if not glob:
            S.dma("pool", smask, d["swamask"], writes=[r_sm])
            SK = LP["SK"]
            act(S, es, lp[:, l, SK:SK + 4], AF.Exp, [r_k], [r_es])
        GQK = LP["GQK"]
        ropeP = k_["ropeP"]
        rstd2 = self.carve([128, 2, 512], F32)
        r_rstd2 = RL(2, "rstd2")

        def prelude(ti, c, b_):
            col0 = ti * 128
            pb, pr = self.pbank()
            mm(S, self.proj_fm(c, s3, col0, pb), [sr] + hall, [pr])
            yield
            rw = raw[:, b_, :]
            if glob:
                act(S, rw, pb[:], AF.Copy, [pr], [r_raw[b_]])
                yield
                i = self.sq_rr % 2
                self.sq_rr += 1
                sqb, r_sq = C["sqb"], C["r_sq"]
                act(S, sqb[:, i, :], rw, AF.Square, [r_raw[b_]], [r_sq[i]])
                yield
                p2, pr2 = self.pbank()
                mm(S, [(p2[:], k_["blockones"][:], sqb[:, i, :], True, True)], [r_sq[i], r_k], [pr2])
                yield
                rstd, r_rstd = rstd2[:, b_, :], r_rstd2[b_]
                act(S, rstd, p2[:], AF.Ln, [pr2], [r_rstd], bias=EPS, scale=1.0 / 64)
                yield
                act(S, rstd, rstd, AF.Exp, [r_rstd], [r_rstd], scale=-0.5)
                yield
                tt(S, "dve", rw, rw, rstd, ALU.mult, [r_raw[b_], r_rstd], [r_raw[b_]])
                yield
                gcol = GQK + (0 if ti < 2 else 1)
                act(S, rw, rw, AF.Copy, [r_raw[b_], r_k], [r_raw[b_]], scale=lp[:, l, gcol:gcol + 1])
                yield
            else:
                act(S, rw, pb[:], AF.Copy, [pr], [r_raw[b_]])
                yield
            p3, pr3 = self.pbank()
            mm(S, [(p3[:], ropeP[:], rw, True, True)], [r_raw[b_], r_k], [pr3])
            yield
            ta, tb_ = rw, tb[:, b_, :]
            tt(S, "pool", ta, rw, cs[:, 0, CH(c)], ALU.mult, [r_raw[b_], r_cs], [r_raw[b_]])
            yield
            tt(S, "dve", tb_, p3[:], cs[:, 1, CH(c)], ALU.mult, [pr3, r_cs], [r_tb[b_]])
            yield
            if ti < 2:
                for g_ in range(2):
                    rows = slice(g_ * 64, (g_ + 1) * 64)
                    tt(S, "dve", qpad[rows, 2 * ti + g_, CH(c)], ta[rows, :], tb_[rows, :], ALU.add, [r_tb[b_], r_raw[b_]], [r_qp[c]])
                    yield
            elif ti == 2:
                tt(S, "dve", kout[:, CH(c)], ta, tb_, ALU.add, [r_tb[b_], r_raw[b_]], [r_kout])
                yield
                act(S, kfull[:, 0, 256 + c * 512:256 + (c + 1) * 512], kout[:, CH(c)], AF.Copy, [r_kout], [r_kf])
                yield
            else:
                tt(S, "dve", kfull[:, 1, 256 + c * 512:256 + (c + 1) * 512], ta, tb_, ALU.add, [r_tb[b_], r_raw[b_]], [r_kf])
                yield

        its = [(ti, c) for ti in range(4) for c in range(2)]
        for i0 in range(0, 8, 2):
            gens = [prelude(its[i0][0], its[i0][1], 0), prelude(its[i0 + 1][0], its[i0 + 1][1], 1)]
            while gens:
                for g__ in list(gens):
                    try:
                        next(g__)
                    except StopIteration:
                        gens.remove(g__)
        S.dma("sp", (d["o_gk"] if glob else d["o_sk"])[l], kout, reads=[r_kout])
        for t_ in range(8):
            pb, pr = self.pbank()
            mm(S, self.proj_tok(t_, s3, 512, 128, pb, 0), [sr] + hall, [pr])
            act(S, vout[:, t_, :], pb[:, 0:128], AF.Copy, [pr], [r_vout])
            cp(S, "dve", vpad[:, 2 + t_, :, 64:128], pb[:, 0:128].rearrange("p (a b) -> p a b", a=2), [pr], [r_vp[2 + t_]])
        S.dma("sp", (d["o_gv"] if glob else d["o_sv"])[l].rearrange("(t p) n -> p t n", p=128), vout, reads=[r_vout])
        self.bank_pool = [0, 1, 2, 3]
        ones_bf = C["ones_bf"]
        r_ones = C["r_ones"]
        ident_bf = k_["ident_bf"]
        ctxb = k_["cflags"][:, 0:1]
        pt_rr = 0
        acc_rr = 0
        for h in range(4):
            kv, g_ = h // 2, h % 2
            kx = 0 if g_ == kv else 1
            rows = slice(g_ * 64, (g_ + 1) * 64)
            vw = slice(64, 192) if g_ == 0 else slice(0, 128)
            for c in range(2):
                if glob:
                    tiles = [(mt, 0, 512) for mt in range(10)]
                else:
                    tiles = [(0, 0, 512), (1, 0, 512)]
                    for j in range(8):
                        lo, hi = max((j - 1) * 128, c * 512), min((j + 2) * 128, (c + 1) * 512)
                        if hi > lo:
                            tiles.append((2 + j, lo - c * 512, hi - c * 512))
                ai_ = 4 + 2 * (acc_rr % 2)
                acc_rr += 1
                pn, prn, pd, prd = self.ps[ai_], self.psr[ai_], self.ps[ai_ + 1], self.psr[ai_ + 1]
                pend = []

                def qk(idx):
                    mt, lo, hi = tiles[idx]
                    pb, pr = self.pbank()
                    qs = slice(c * 512 + lo, c * 512 + hi)
                    g = [(pb[:, lo:hi], kfull[:, kx, mt * 128:(mt + 1) * 128], qpad[:, h, qs], True, mt < 2)]
                    rd = [r_kf, r_qp[c]]
                    if mt >= 2:
                        if glob:
                            g.append((pb[:, lo:hi], gm[0:5, 0, (mt - 2) * 128:(mt - 1) * 128], gm[0:5, 1, qs], False, True))
                            rd.append(r_gm)
                        else:
                            j = mt - 2
                            m0_ = c * 512 + lo - (j - 1) * 128
                            g.append((pb[:, lo:hi], ident_bf[:], smask[:, j, m0_:m0_ + (hi - lo)], False, True))
                            rd += [r_sm, r_k]
                    mm(S, g, rd, [pr])
                    return pb, pr

                nt = len(tiles)
                look = 2
                for idx in range(min(look, nt)):
                    pend.append(qk(idx))
                for idx in range(nt):
                    mt, lo, hi = tiles[idx]
                    pb, pr = pend.pop(0)
                    if idx + look < nt:
                        pend.append(qk(idx + look))
                    pi = pt_rr % 3
                    pt_rr += 1
                    act(S, PT[:, pi, lo:hi], pb[:, lo:hi], AF.Exp, [pr, r_k], [r_PT[pi]],
                        bias=(ctxb if mt < 2 else 0.0), scale=0.125)
                    g = [(pn[:, lo:hi], vpad[:, mt, kv, vw], PT[:, pi, lo:hi], idx == 0, idx == nt - 1),
                         (pd[:, lo:hi], ones_bf[:], PT[:, pi, lo:hi], idx == 0, idx == nt - 1)]
                    mm(S, g, [r_vp[mt], r_PT[pi], r_ones], [prn, prd])
                ri = (h * 2 + c) % 2
                if glob:
                    recip(S, rden[rows, ri, :], pd[rows, :], [prd], [r_rden[ri]])
                else:
                    ts(S, "dve", rden[rows, ri, :], pd[rows, :], es[rows, h:h + 1], ALU.add, [prd, r_es], [r_rden[ri]])
                    recip(S, rden[rows, ri, :], rden[rows, ri, :], [r_rden[ri]], [r_rden[ri]])
                tt(S, "dve", ymix[rows, ymb + kv, CH(c)], pn[rows, :], rden[rows, ri, :], ALU.mult, [prn, r_rden[ri]],
                   [ymr[ymb + kv][c]])
        self.bank_pool = list(range(8))

    def mix_hyena(self, l, ymix, ymr, win):
        S, k_, d = self.S, self.k, self.d
        C = self.ctx
        hr, CH = C["hr"], C["CH"]
        LP = self.LP
        lp = k_["lp"]
        r_k = self.r_k
        hall = [hr[k][c] for k in range(8) for c in range(2)]
        sv8 = lambda sl_: self.slot_view(sl_, 8)
        TWO_PI = float(2 * math.pi)
        xoff = self.aoff
        feats = self.carve([128, T], F32)
        z1 = self.carve([128, T], F32)
        z2 = self.carve([128, T], F32)
        wnd = self.carve([128, 8, 256], F32)
        rr = self.carve([128, 512], F32)
        ii = self.carve([128, 512], I32)
        kf = self.carve([128, 512], F32)
        xend = self.aoff
        self.aoff = xoff
        raw = self.carve([128, 3, T], F32)
        uct = self.carve([128, 2, T], F32)
        pa = self.carve([128, 2, 256], F32)
        pq = self.carve([128, 2, 256], F32)
        yt = self.carve([128, 2, 256], F32)
        assert self.aoff <= xend
        self.aoff = xend
        r_X = R("X")
        x0 = self.carve([128, 2, T], F32)
        zf = self.carve([128, 2, T], F32)
        zbf = self.carve([128, 2, T], BF16)
        zh = self.carve([128, 8, 512], BF16)
        ZH = self.carve([128, 2, 512], F32)
        Y = self.carve([128, 8, 2, 256], BF16)
        pa2 = self.carve([128, 2, 256], F32)
        yt2 = ZH[:, :, 0:256]
        r_x0, r_zf, r_zbf = RL(2, "x0"), RL(2, "zf"), RL(2, "zbf")
        r_zh = RL(8, "zh")
        r_ZH, r_Y, r_pa, r_pq, r_yt = R("ZH"), RL(8, "Y"), R("pa"), R("pq"), RL(2, "yt")
        S.dma("sp", feats[0:33, :], d["featsT"], writes=[r_X])
        S.dma("sp", wnd, d["window"], writes=[r_X])
        fs = k_["fs"]

        def sin_layer(pb, pr, dst, bcol):
            ts(S, "dve", rr[0:64, :], pb[0:64, :], fs[:, l, 0:1], ALU.mult, [pr, r_k], [r_X], s2=fs[:, l, bcol:bcol + 1],
               op1=ALU.add)
            cp(S, "dve", ii[0:64, :], rr[0:64, :], [r_X], [r_X])
            cp(S, "dve", kf[0:64, :], ii[0:64, :], [r_X], [r_X])
            tt(S, "dve", rr[0:64, :], rr[0:64, :], kf[0:64, :], ALU.subtract, [r_X], [r_X])
            act(S, dst, rr[0:64, :], AF.Sin, [r_X], [r_X], scale=TWO_PI)

        for c in range(2):
            pb, pr = self.pbank()
            mm(S, [(pb[0:64, :], k_["fw1"][0:33, l, :], feats[0:33, CH(c)], True, True)], [r_X, r_k], [pr])
            sin_layer(pb, pr, z1[0:64, CH(c)], 1)
        for c in range(2):
            pb, pr = self.pbank()
            mm(S, [(pb[0:64, :], k_["fw2"][0:64, l, :], z1[0:64, CH(c)], True, True)], [r_X, r_k], [pr])
            sin_layer(pb, pr, z2[0:64, CH(c)], 2)
        for t_ in range(8):
            pb, pr = self.pbank()
            tok = slice(t_ * 128, (t_ + 1) * 128)
            mm(S, [(pb[:, 0:256], z2[0:64, tok], k_["fw3"][0:64, l, :], True, False),
                   (pb[:, 0:256], k_["ones_f"][0:1, :], k_["fb3"][0:1, l, :], False, True)], [r_X, r_k], [pr])
            tt(S, "dve", zh[:, t_, 256:512], pb[:, 0:256], wnd[:, t_, :], ALU.mult, [pr, r_X], [r_zh[t_]])
        slD1, srD1 = self.wload(self.parts_win(l, 2064, 512), key=("D1", l))
        slD2, srD2 = self.wload(self.parts_win(l, 2576, 256), key=("D2", l))
        sD1, sD2 = sv8(slD1), sv8(slD2)
        Fd, Gd = d["dftF"], d["dftG"]
        fv = lambda m: Fd[m].rearrange("(k p) n -> p k n", p=128)
        gv = lambda m: Gd[m].rearrange("(k p) n -> p k n", p=128)

        def parts_dft(vw, q):
            return [(lambda sl_: sv8(sl_)[:, :, 0:256], vw(0)[:, :, q * 256:(q + 1) * 256]),
                    (lambda sl_: sv8(sl_)[:, :, 256:512], vw(1)[:, :, q * 256:(q + 1) * 256])]

        def zip_run(gens):
            gens = list(gens)
            while gens:
                for g__ in list(gens):
                    try:
                        next(g__)
                    except StopIteration:
                        gens.remove(g__)

        for q in range(2):
            self.prefetch(("F", l, q), parts_dft(fv, q))
        CV = LP["CV"]
        cvb = k_["cvb"]
        r_raw = RL(3, "hraw")
        r_uct = RL(2, "uct")

        def conv_chain(ct, ui):
            s3, col, srr = ((sD1, ct * 128, srD1), (sD1, 256 + ct * 128, srD1), (sD2, ct * 128, srD2))[ui]
            rr_ = r_raw[ui]
            fx = [r_X] if ct == 0 else []
            for c in range(2):
                pb, pr = self.pbank()
                mm(S, self.proj_fm(c, s3, col, pb), [srr] + hall, [pr])
                yield
                act(S, raw[:, ui, CH(c)], pb[:], AF.Copy, [pr], [rr_] + fx)
                yield
            tile = ui * 2 + ct
            cw = lambda j: lp[:, l, CV + tile * 4 + j:CV + tile * 4 + j + 1]
            u = raw[:, ui, :]
            if ui == 0:
                dst, wr = x0[:, ct, :], [r_x0[ct]]
            else:
                dst, wr = uct[:, ui - 1, :], [r_uct[ui - 1]] + fx
            rd = [rr_, r_k]
            act(S, dst, u, AF.Identity, rd, wr, bias=cw(3), scale=cw(1))
            yield
            for (o, a_, sc) in ((dst[:, 1:T], u[:, 0:T - 1], cw(0)), (dst[:, 0:T - 1], u[:, 1:T], cw(2)),
                                (dst[:, 256:T:256], u[:, 255:T - 1:256], cvb[:, l, tile, 0:1]),
                                (dst[:, 255:T - 1:256], u[:, 256:T:256], cvb[:, l, tile, 1:2])):
                S.op("dve", lambda e, o=o, a_=a_, sc=sc: e.scalar_tensor_tensor(out=o, in0=a_, scalar=sc, in1=o,
                                                                              op0=ALU.mult, op1=ALU.add), rd + wr[:1], wr[:1])
                yield

        for ct in range(2):
            zip_run([conv_chain(ct, ui) for ui in range(3)])
            tt(S, "dve", zf[:, ct, :], uct[:, 0, :], uct[:, 1, :], ALU.mult, r_uct, [r_zf[ct]])
            act(S, zbf[:, ct, :], zf[:, ct, :], AF.Copy, [r_zf[ct]], [r_zbf[ct]])
        for q in range(2, 4):
            self.prefetch(("F", l, q), parts_dft(fv, q))
        ident_bf = k_["ident_bf"]
        for t_ in range(8):
            pb, pr = self.pbank()
            pv = pb[:].bitcast(BF16)
            for ct in range(2):
                S.op("pe", lambda e, ct=ct, pv=pv, t_=t_: e.transpose(pv[:, ct * 128:(ct + 1) * 128],
                                                                      zbf[:, ct, t_ * 128:(t_ + 1) * 128], ident_bf[:]),
                     [r_zbf[ct], r_k], [pr])
            cp(S, "dve", zh[:, t_, 0:256], pv[:, 0:256], [pr], [r_zh[t_]])
        ZHs = [ZH, zbf.rearrange("p a b -> p (a b)").bitcast(F32).rearrange("p (a b) -> p a b", a=2)]
        r_ZHs = [r_ZH, R("ZHb")]
        PAs, PQs = [pa, pa2], [pq, yt]
        r_PA, r_PQ = RL(2, "PA"), RL(2, "PQ")
        seen = set()

        def ft_chain(ft, fj, s3, sr):
            bi = ft % 2
            Zb, rz = ZHs[bi], r_ZHs[bi]
            PA, PQ, rpa, rpq = PAs[bi], PQs[bi], r_PA[bi], r_PQ[bi]
            fresh = bi not in seen
            seen.add(bi)
            fz = (r_zbf if (fresh and bi == 1) else [])
            fx = ([r_X] if fresh else [])
            pre_, prr = self.pbank()
            pim, pri = self.pbank()
            g = [(pre_[:], s3[:, t_, fj * 128:(fj + 1) * 128], zh[:, t_, :], t_ == 0, t_ == 7) for t_ in range(8)]
            g += [(pim[:], s3[:, t_, 256 + fj * 128:256 + (fj + 1) * 128], zh[:, t_, :], t_ == 0, t_ == 7) for t_ in range(8)]
            mm(S, g, [sr] + r_zh, [prr, pri])
            yield
            act(S, Zb[:, 0, :], pre_[:], AF.Copy, [prr], [rz] + fz)
            yield
            act(S, Zb[:, 1, :], pim[:], AF.Copy, [pri], [rz])
            yield
            Zr, Hr, Zi, Hi = Zb[:, 0, 0:256], Zb[:, 0, 256:512], Zb[:, 1, 0:256], Zb[:, 1, 256:512]
            tt(S, "dve", PA[:, 0, :], Zr, Hr, ALU.mult, [rz], [rpa] + fx)
            yield
            tt(S, "pool", PQ[:, 0, :], Zr, Hi, ALU.mult, [rz], [rpq] + fx)
            yield
            tt(S, "dve", PA[:, 1, :], Zi, Hi, ALU.mult, [rz], [rpa])
            yield
            tt(S, "pool", PQ[:, 1, :], Zi, Hr, ALU.mult, [rz], [rpq])
            yield
            tt(S, "dve", Y[:, ft, 0, :], PA[:, 0, :], PA[:, 1, :], ALU.subtract, [rpa], [r_Y[ft]])
            yield
            tt(S, "pool", Y[:, ft, 1, :], PQ[:, 0, :], PQ[:, 1, :], ALU.add, [rpq], [r_Y[ft]])
            yield

        for q in range(4):
            sl, sr = self.wload(parts_dft(fv, q), key=("F", l, q))
            s3 = sv8(sl)
            zip_run([ft_chain(q * 2 + fj, fj, s3, sr) for fj in range(2)])
        HB = LP["HB"]
        for q in range(2):
            self.prefetch(("G", l, q), parts_dft(gv, q))
        for q in range(4):
            sl, sr = self.wload(parts_dft(gv, q), key=("G", l, q))
            if q + 2 < 4:
                self.prefetch(("G", l, q + 2), parts_dft(gv, q + 2))
            s3 = sv8(sl)
            ns = slice(q * 256, (q + 1) * 256)
            for ct in range(2):
                pb, pr = self.pbank()
                g = []
                for ft in range(8):
                    g.append((pb[:, 0:256], Y[:, ft, 0, ct * 128:(ct + 1) * 128], s3[:, ft, 0:256], ft == 0, False))
                    g.append((pb[:, 0:256], Y[:, ft, 1, ct * 128:(ct + 1) * 128], s3[:, ft, 256:512], False, ft == 7))
                mm(S, g, [sr] + r_Y, [pr])
                ytb, ryt = (yt2[:, ct, :], r_yt[ct])
                act(S, ytb, pb[:, 0:256], AF.Copy, [pr], [ryt] + ([r_PA[1], r_ZHs[0]] if q == 0 else []))
                S.op("dve", lambda e, ytb=ytb, ct=ct: e.scalar_tensor_tensor(out=ytb, in0=zf[:, ct, ns],
                                                                           scalar=lp[:, l, HB + ct:HB + ct + 1], in1=ytb,
                                                                           op0=ALU.mult, op1=ALU.add),
                     [r_zf[ct], ryt, r_k], [ryt])
                tt(S, "dve", ymix[:, 6 + ct, ns], x0[:, ct, ns], ytb, ALU.mult, [r_x0[ct], ryt], [ymr[6 + ct][q // 2]])


def fm(v):
    return np.ascontiguousarray(np.asarray(v, np.float32).reshape(8, 128).T)


def _consts(kind):
    c = {}
    p = np.arange(128)
    t = np.arange(T)
    ident = np.eye(128, dtype=np.float32)
    c["ident"] = ident
    bo = np.zeros((128, 128), np.float32)
    bo[:64, :64] = 1
    bo[64:, 64:] = 1
    c["blockones"] = bo
    r_, t_ = np.meshgrid(p, p, indexing="ij")
    tri = np.zeros((128, 4, 128), np.float32)
    tri[:, 0, :] = (r_ <= t_)
    tri[:, 1, :] = (r_ >= t_)
    tri[:, 2, :] = np.where(r_ <= t_, 0.0, NEG)
    tri[:, 3, :] = np.where(r_ >= t_, 0.0, NEG)
    c["tri"] = tri
    P = np.zeros((128, 128), np.float32)
    for b in range(0, 128, 32):
        for i in range(16):
            P[b + i + 16, b + i] = -1.0
            P[b + i, b + i + 16] = 1.0
    c["ropeP"] = P
    cs = np.zeros((128, 2, T), np.float32)
    if kind == "s":
        dd = p % 64
        inv = (10000.0 ** (-(dd % 16).astype(np.float32) / np.float32(16))).astype(np.float32)
        row = (t // 64).astype(np.float32)
        col = (t % 64).astype(np.float32)
        pos = np.where((dd // 32)[:, None] == 0, row[None, :], col[None, :]).astype(np.float32)
        ang = (pos * inv[:, None]).astype(np.float32)
        cs[:, 0, :] = np.cos(ang)
        cs[:, 1, :] = np.sin(ang)
    else:
        cs[:, 0, :] = 1.0
    c["ropeCS"] = cs
    sm = np.full((128, 8, 384), NEG, np.float32)
    for j in range(8):
        m = j * 128 + p[:, None]
        q = (j - 1) * 128 + np.arange(384)[None, :]
        inr = (q >= 0) & (q < T)
        if kind == "s":
            ok = (np.abs(m - q) <= 128) & inr
        else:
            ok = ((m // 256) == (q // 256)) & inr
        sm[:, j, :] = np.where(ok, 0.0, NEG)
    c["swamask"] = sm
    gm = np.zeros((5, 2, T), np.float32)
    if kind == "p":
        gm[0, 0, :] = 1.0
        gm[0, 1, :] = -BIG
        for s_ in range(4):
            gm[1 + s_, 0, :] = (t // 256 == s_)
            gm[1 + s_, 1, :] = BIG * (t // 256 == s_)
    c["gmAB"] = gm
    c["cflags"] = np.tile(np.array([[0.0, 1.0, 0.0, 0.0]] if kind == "s" else [[NEG, 0.0, 1.0, 0.0]], np.float32), (128, 1))
    sel = np.zeros((4, 130), np.float32)
    for h in range(4):
        sel[h, 0:128] = ((p >= 64).astype(int) == (h % 2))
        sel[h, 128 + h // 2] = 1.0
    c["sel"] = sel
    L = T if kind == "s" else 256
    rep = T // L
    pos = np.arange(L, dtype=np.float32)
    t01 = pos / np.float32(max(L - 1, 1))
    lin = np.linspace(1e-4, 15.0, 16, dtype=np.float32)
    ang = (np.float32(2.0 * math.pi / L) * pos[:, None] * lin[None, :]).astype(np.float32)
    feats = np.concatenate([t01[:, None], np.cos(ang), -np.sin(ang)], -1).astype(np.float32)
    c["featsT"] = np.ascontiguousarray(np.tile(feats, (rep, 1)).T)
    centre = L // 2
    dist = np.abs(pos - centre) / np.float32(max(centre, 1))
    deltas = np.abs(np.linspace(math.log(0.01) / 1.5, math.log(0.01) / 0.3, 256, dtype=np.float32))
    wnd = np.exp(-dist[:, None] * deltas[None, :]).astype(np.float32)
    c["window"] = np.ascontiguousarray(np.tile(wnd, (rep, 1)).reshape(8, 128, 256).transpose(1, 0, 2))
    N = 2 * L
    tt_ = np.arange(L, dtype=np.float64)
    ff = np.arange(L, dtype=np.float64)
    th = math.pi * (2 * ff + 1) / N
    Fc = np.cos(tt_[:, None] * th[None, :])
    Fs = -np.sin(tt_[:, None] * th[None, :])
    Gc = (2.0 / N) * np.cos(th[:, None] * (tt_[None, :] + L // 2))
    Gs = -(2.0 / N) * np.sin(th[:, None] * (tt_[None, :] + L // 2))
    dF = np.zeros((2, T, T), np.float32)
    dG = np.zeros((2, T, T), np.float32)
    for s_ in range(rep):
        sl = slice(s_ * L, (s_ + 1) * L)
        dF[0, sl, sl] = Fc
        dF[1, sl, sl] = Fs
        dG[0, sl, sl] = Gc
        dG[1, sl, sl] = Gs
    c["dftF"] = dF
    c["dftG"] = dG
    return c


def host_inputs(inp, cores=None):
    f = lambda a: np.ascontiguousarray(np.asarray(a, dtype=np.float32))
    A = {k: np.asarray(v) for k, v in inp.items()}
    shared = {}
    for nm in ("w_ada", "w1_gate", "w1_up", "w1_down", "w_in", "w_out", "w2_gate", "w2_up", "w2_down"):
        shared[nm] = f(A[nm])
    shared["b_adaT"] = f(A["b_ada"].reshape(NL, 72, 128).transpose(0, 2, 1))
    gv = []
    for l in range(NL):
        gv += [fm(A["g_ff1"][l]), fm(A["g_mix"][l]), fm(A["g_ff2"][l])]
    gv.append(fm(A["g_final"]))
    shared["gvec"] = f(np.concatenate(gv, axis=1))
    lp = np.zeros((128, NL, 304), np.float32)
    p = np.arange(128)
    for l in range(NL):
        lp[:, l, 0:16] = A["b_gates"][l][None, :]
        lp[:, l, 16:272] = A["g_mlstm"][l][None, :]
        lp[:, l, 272] = A["g_qnorm"][l][p % 64]
        lp[:, l, 273] = A["g_knorm"][l][p % 64]
        lp[:, l, 274:278] = A["sinks"][l][None, :]
        for i in range(6):
            ch = i * 128 + p
            lp[:, l, 278 + i * 4 + 0] = A["conv_w"][l][0, ch]
            lp[:, l, 278 + i * 4 + 1] = A["conv_w"][l][1, ch]
            lp[:, l, 278 + i * 4 + 2] = A["conv_w"][l][2, ch]
            lp[:, l, 278 + i * 4 + 3] = A["conv_b"][l][ch]
        for ct in range(2):
            lp[:, l, 302 + ct] = A["hyena_bias"][l][ct * 128 + p]
    shared["lp"] = lp
    shared["fw1"] = f(A["filt_w1"].transpose(1, 0, 2))
    shared["fw2"] = f(A["filt_w2"].transpose(1, 0, 2))
    shared["fw3"] = f(A["filt_w3"].transpose(1, 0, 2))
    shared["fvec"] = f(np.stack([A["filt_b1"], A["filt_b2"], A["filt_freq"]], -1).transpose(1, 0, 2))
    shared["fb3"] = f(A["filt_b3"][None, :, :])
    cst = {"s": _consts("s"), "p": _consts("p")}
    maps = []
    xs, xp = A["x_sample"], A["x_prompt"]
    for core in (range(8) if cores is None else cores):
        m = dict(shared)
        kind = "s" if core < 4 else "p"
        m.update(cst[kind])
        C0 = np.zeros((NL, 2, 128, 2, 65), np.float32)
        m0rep = np.zeros((128, NL, 2, 2), np.float32)
        m0c = np.zeros((4, NL * 2), np.float32)
        kT = {n: np.zeros((NL, 2, 128, 256), np.float32) for n in ("gkT", "skT")}
        vp = {n: np.zeros((NL, 2, 128, 2, 192), np.float32) for n in ("gvp", "svp")}
        if core < 4:
            b = core
            m["xT"] = f(xs[b].T)
            m["cvec"] = fm(A["c"][b])
            sC, sn, smm = A["state_mlstm_C"][b], A["state_mlstm_n"][b], A["state_mlstm_m"][b]
            for g_ in range(2):
                for pr_ in range(2):
                    h = 2 * pr_ + g_
                    C0[:, :, g_ * 64:(g_ + 1) * 64, pr_, 0:64] = sC[:, :, h]
                    C0[:, :, g_ * 64:(g_ + 1) * 64, pr_, 64] = sn[:, :, h]
                    m0rep[g_ * 64:(g_ + 1) * 64, :, :, pr_] = smm[None, :, :, h]
            for l in range(NL):
                for dr in range(2):
                    m0c[:, l * 2 + dr] = smm[l, dr, :]
            for (kn, vn, ck_, cv_) in (("gkT", "gvp", "cache_gattn_k", "cache_gattn_v"), ("skT", "svp", "cache_swa_k", "cache_swa_v")):
                ck, cv = A[ck_][b], A[cv_][b]
                t1 = ck.transpose(0, 2, 3, 1).reshape(NL, 128, 256)
                t2 = ck[:, :, ::-1, :].transpose(0, 2, 3, 1).reshape(NL, 128, 256)
                kT[kn][:, 0] = t1
                kT[kn][:, 1] = t2
                vp[vn][:, :, :, :, 64:128] = cv.reshape(NL, 2, 128, 2, 64)
        else:
            j = core - 4
            m["xT"] = f(xp[4 * j:4 * j + 4].reshape(T, D).T)
            m["cvec"] = fm(A["c_ctx"])
        m["C0"], m["m0rep"], m["m0c"] = C0, m0rep, m0c
        m.update(kT)
        m.update(vp)
        maps.append(m)
    return maps


_PROG = None


def get_prog():
    global _PROG
    if _PROG is None:
        _PROG = Prog()
    return _PROG


def run_device(inputs, trace=False, cores=None):
    prog = get_prog()
    maps = host_inputs(inputs, cores)
    maps = [{k: np.ascontiguousarray(v, dtype=np.float32) for k, v in m.items() if k in prog.din} for m in maps]
    for m in maps:
        for k, shp in prog.din.items():
            assert m[k].shape == shp, (k, m[k].shape, shp)
    res = run_bass_kernel_spmd(prog.nc, maps, core_ids=list(range(len(maps))), trace=trace)
    return res


def assemble(res):
    r = res.results
    yp = np.zeros((16, 256, D), np.float32)
    ys = np.zeros((4, T, D), np.float32)
    nC = np.zeros((16, NL, 2, 4, 64, 64), np.float32)
    nn = np.zeros((16, NL, 2, 4, 64), np.float32)
    nm = np.zeros((16, NL, 2, 4), np.float32)
    kv = {n: np.zeros((16, NL, 256, 2, 64), np.float32) for n in ("o_gk", "o_gv", "o_sk", "o_sv")}
    for core in range(8):
        y = np.ascontiguousarray(r[core]["yT"].T)
        if core < 4:
            ys[core] = y
            continue
        j = core - 4
        yp[4 * j:4 * j + 4] = y.reshape(4, 256, D)
        for n in ("o_gk", "o_sk"):
            a = r[core][n].reshape(NL, 2, 64, 4, 256)
            kv[n][4 * j:4 * j + 4] = a.transpose(3, 0, 4, 1, 2)
        for n in ("o_gv", "o_sv"):
            a = r[core][n].reshape(NL, 4, 256, 2, 64)
            kv[n][4 * j:4 * j + 4] = a.transpose(1, 0, 2, 3, 4)
        oc = r[core]["o_C"].reshape(NL, 2, 4, 2, 64, 2, 65)
        oc = oc.transpose(2, 0, 1, 5, 3, 4, 6).reshape(4, NL, 2, 4, 64, 65)
        nC[4 * j:4 * j + 4] = oc[..., 0:64]
        nn[4 * j:4 * j + 4] = oc[..., 64]
        nm[4 * j:4 * j + 4] = r[core]["o_m"].transpose(3, 0, 1, 2)
    return (yp, ys, nC, nn, nm, kv["o_gk"], kv["o_gv"], kv["o_sk"], kv["o_sv"])


def kernel(**inputs):
    res = run_device(inputs)
    return assemble(res)
```

```python
import os
import math
import numpy as np
import concourse.bass as bass
import concourse.mybir as mybir
from concourse.bass_utils import run_bass_kernel_spmd

F32 = mybir.dt.float32
BF16 = mybir.dt.bfloat16
I32 = mybir.dt.int32
AF = mybir.ActivationFunctionType
ALU = mybir.AluOpType
AX = mybir.AxisListType

D = 1024
T = 1024
DFF = 2816
NFF = 22
NL = 2
NIN = 2832
EPS = 1e-6
NEG = -30000.0
BIG = 29952.0
LN8 = math.log(0.125)
SLOT = 8 * 640
STAGE = int(os.environ.get("MK_STAGE", "99"))
DEBUG = bool(os.environ.get("MK_DEBUG"))
CUT = int(os.environ.get("MK_CUT", "99"))


class R:
    __slots__ = ("name", "w", "rd", "excl")

    def __init__(self, name="", excl=False):
        self.name = name
        self.w = None
        self.rd = []
        self.excl = excl


def RL(n, name=""):
    return [R("%s%d" % (name, i)) for i in range(n)]


class Sched:
    NLANES = {"sp": 8, "pool": 3, "poolw": 5}

    def __init__(self, nc):
        self.nc = nc
        self.eng = {"pe": nc.tensor, "act": nc.scalar, "dve": nc.vector,
                    "pool": nc.gpsimd, "sp": nc.sync}
        self.sem = {}
        self.cnt = {}
        for k in self.eng:
            self.sem[k] = nc.alloc_semaphore(name="s_" + k)
            self.cnt[k] = 0
        self.lanes = {}
        for q, n in self.NLANES.items():
            self.lanes[q] = []
            for i in range(n):
                key = "d_%s%d" % (q, i)
                self.sem[key] = nc.alloc_semaphore(name=key)
                self.cnt[key] = 0
                self.lanes[q].append(key)
        self.lane_rr = {q: 0 for q in self.NLANES}
        self.eng_of = {"sp": "sp", "pool": "pool", "poolw": "pool"}
        self.waited = {k: {} for k in self.eng}
        self.nwaits = 0
        self.nops = 0

    def _wait(self, e, key, val):
        if key == "pe" and e == "pe":
            return
        w = self.waited[e]
        if w.get(key, 0) >= val:
            return
        self.eng[e].wait_ge(self.sem[key], val)
        w[key] = val
        self.nwaits += 1

    def _deps(self, e, reads, writes):
        deps = {}
        for r in reads:
            if r.w is not None:
                k, v = r.w
                if deps.get(k, 0) < v:
                    deps[k] = v
            if r.excl:
                for (k, v) in r.rd:
                    if k != e and deps.get(k, 0) < v:
                        deps[k] = v
        for w in writes:
            if w.w is not None:
                k, v = w.w
                if deps.get(k, 0) < v:
                    deps[k] = v
            for (k, v) in w.rd:
                if deps.get(k, 0) < v:
                    deps[k] = v
        for k, v in deps.items():
            self._wait(e, k, v)

    def _commit(self, tok, reads, writes):
        for r in reads:
            r.rd.append(tok)
            if len(r.rd) > 48:
                mx = {}
                for k, v in r.rd:
                    if mx.get(k, 0) < v:
                        mx[k] = v
                r.rd = list(mx.items())
        for w in writes:
            w.w = tok
            w.rd = []

    def op(self, e, fn, reads=(), writes=()):
        self._deps(e, reads, writes)
        inst = fn(self.eng[e])
        self.cnt[e] += 1
        inst.then_inc(self.sem[e], 1)
        self._commit((e, self.cnt[e]), reads, writes)
        self.nops += 1

    def dma(self, q, out, in_, reads=(), writes=()):
        lanes = self.lanes[q]
        e = self.eng_of[q]
        key = lanes[self.lane_rr[q] % len(lanes)]
        self.lane_rr[q] += 1
        self._wait(e, key, self.cnt[key])
        self._deps(e, reads, writes)
        inst = self.eng[e].dma_start(out=out, in_=in_)
        self.cnt[key] += 16
        inst.then_inc(self.sem[key], 16)
        self._commit((key, self.cnt[key]), reads, writes)
        self.nops += 1

    def barrier(self):
        for e in self.eng:
            for k in self.sem:
                if self.cnt[k] > 0 and not k.startswith("d_poolw"):
                    self._wait(e, k, self.cnt[k])

    def finish(self):
        for k in self.sem:
            if self.cnt[k] > 0 and k != "sp":
                self._wait("sp", k, self.cnt[k])


def act(S, out, in_, func, reads, writes, bias=0.0, scale=1.0):
    S.op("act", lambda e: e.activation(out=out, in_=in_, func=func, bias=bias, scale=scale), reads, writes)


def tt(S, eng, out, a, b, op, reads, writes):
    S.op(eng, lambda e: e.tensor_tensor(out=out, in0=a, in1=b, op=op), reads, writes)


def ts(S, eng, out, a, s1, op0, reads, writes, s2=None, op1=None):
    if op1 is None:
        S.op(eng, lambda e: e.tensor_scalar(out=out, in0=a, scalar1=s1, scalar2=None, op0=op0), reads, writes)
    else:
        S.op(eng, lambda e: e.tensor_scalar(out=out, in0=a, scalar1=s1, scalar2=s2, op0=op0, op1=op1), reads, writes)


def cp(S, eng, out, in_, reads, writes):
    S.op(eng, lambda e: e.tensor_copy(out=out, in_=in_), reads, writes)


def recip(S, out, in_, reads, writes):
    S.op("dve", lambda e: e.reciprocal(out=out, in_=in_), reads, writes)


def mm(S, groups, reads, writes):
    def fn(e):
        inst = None
        for (o, l, r, st, sp_) in groups:
            inst = e.matmul(o, lhsT=l, rhs=r, start=st, stop=sp_)
        return inst
    S.op("pe", fn, reads, writes)


class Prog:
    def __init__(self):
        nc = bass.Bass("TRN2", target_bir_lowering=False)
        self.nc = nc
        self.S = Sched(nc)
        self.din = {}
        self.dout = {}
        self._build()

    def inp(self, name, shape):
        t = self.nc.dram_tensor(name, list(shape), F32, kind="ExternalInput").ap()
        self.din[name] = tuple(shape)
        return t

    def outp(self, name, shape):
        t = self.nc.dram_tensor(name, list(shape), F32, kind="ExternalOutput").ap()
        self.dout[name] = tuple(shape)
        return t

    def sb(self, name, shape, dt=F32):
        return self.nc.alloc_sbuf_tensor("sb_" + name, list(shape), dt)

    def arena_reset(self, to=0):
        self.aoff = to
        self.S.barrier()

    def carve(self, shape, dt=F32):
        n = 1
        for s in shape[1:]:
            n *= s
        nb = n * (4 if dt in (F32, I32) else 2)
        nb = (nb + 31) // 32 * 32
        off = self.aoff
        self.aoff += nb
        assert self.aoff <= self.ARENA_BYTES, (self.aoff, self.ARENA_BYTES)
        v = self.arena[:, off // 2:(off + nb) // 2]
        if dt != BF16:
            v = v.bitcast(dt)
        v = v[:, 0:n]
        if len(shape) == 3:
            v = v.rearrange("p (a b) -> p a b", a=shape[1])
        elif len(shape) == 4:
            v = v.rearrange("p (a b c) -> p a b c", a=shape[1], b=shape[2])
        return v

    def bank(self):
        return self.pbank()

    def prefetch(self, key, parts):
        self.pre[key] = self.wload(parts)

    def wload(self, parts, key=None):
        if key is not None and key in self.pre:
            return self.pre.pop(key)
        i = self.slot_rr % len(self.slots)
        self.slot_rr += 1
        sl, r = self.slots[i], self.slotr[i]
        for (dst_fn, src) in parts:
            self.S.dma("poolw", dst_fn(sl), src, writes=[r])
        return sl, r

    def slot_view(self, sl, kk):
        return sl[:, 0:kk * self.slot_w].rearrange("p (k n) -> p k n", k=kk)

    def _build(self):
        nc, S = self.nc, self.S
        inp, outp, sb = self.inp, self.outp, self.sb
        xT_d = inp("xT", [D, T])
        cvec_d = inp("cvec", [128, 8])
        w_ada = inp("w_ada", [NL, D, 9 * D])
        b_adaT = inp("b_adaT", [NL, 128, 72])
        gvec_d = inp("gvec", [128, NL * 24 + 8])
        wd = {}
        for nm, shp in (("w1_gate", [NL, D, DFF]), ("w1_up", [NL, D, DFF]), ("w1_down", [NL, DFF, D]),
                        ("w_in", [NL, D, NIN]), ("w_out", [NL, D, D]),
                        ("w2_gate", [NL, D, DFF]), ("w2_up", [NL, D, DFF]), ("w2_down", [NL, DFF, D])):
            wd[nm] = inp(nm, shp)
        yT_d = outp("yT", [D, T])

        xT = sb("xT", [128, 8, T], F32)
        hT = sb("hT", [128, 8, T], BF16)
        self.ARENA_BYTES = 84 * 1024
        self.arena = sb("arena", [128, self.ARENA_BYTES // 2], BF16)
        self.slot_w = 640
        self.slots = [sb("slot%d" % i, [128, SLOT], BF16) for i in range(4)]
        self.slotr = RL(4, "slot")
        self.slot_rr = 0
        self.pre = {}
        self.ps = [nc.alloc_psum_tensor("ps%d" % i, [128, 512], F32) for i in range(8)]
        self.psr = [R("ps%d" % i, excl=True) for i in range(8)]
        self.bank_rr = 0
        self.bank_pool = list(range(8))
        ones_bf = sb("ones_bf", [128, 128], BF16)
        cvec = sb("cvec_sb", [128, 8], F32)
        sc_bf = sb("sc_bf", [128, 8], BF16)
        modT = sb("modT", [128, NL, 72], F32)
        badaT = sb("badaT", [128, NL, 72], F32)
        gvec = sb("gvec_sb", [128, NL * 24 + 8], F32)
        Acoef = sb("Acoef", [128, NL, 3, 8], F32)
        Gcoef = sb("Gcoef", [128, NL, 3, 8], F32)
        sqb = sb("sqb", [128, 2, 512], BF16)
        f32s = sb("f32s", [128, 3, 512], F32)
        rstd = sb("rstd", [128, 512], F32)
        r_ones, r_cvec, r_sc, r_gvec, r_rstd = R("ones"), R("cvec"), R("sc"), R("gvec"), R("rstd")
        r_mod = RL(NL, "mod")
        r_bada = R("bada")
        r_coef = RL(NL, "coef")
        r_sq = RL(2, "sq")
        r_f32s = RL(3, "f32s")
        xr = [[R("x%d_%d" % (k, c)) for c in range(2)] for k in range(8)]
        hr = [[R("h%d_%d" % (k, c)) for c in range(2)] for k in range(8)]
        self.sq_rr = 0
        self.f32_rr = 0

        def CH(c):
            return slice(c * 512, (c + 1) * 512)

        xv = xT_d.rearrange("(k p) t -> p k t", p=128)
        for k in range(8):
            S.dma("sp", xT[:, k, :], xv[:, k, :], writes=[xr[k][0], xr[k][1]])
        S.dma("sp", cvec[:], cvec_d, writes=[r_cvec])
        S.dma("sp", gvec[:], gvec_d, writes=[r_gvec])
        for l in range(NL):
            S.dma("sp", badaT[:, l, :], b_adaT[l], writes=[r_bada])
        S.op("dve", lambda e: e.memset(ones_bf[:], 1.0), writes=[r_ones])
        act(S, sc_bf[:], cvec[:], AF.Silu, [r_cvec], [r_sc])

        mslots = [sb("mslot%d" % i, [128, 8, 256], BF16) for i in range(2)]
        r_ms = RL(2, "mslot")
        r_modls = [[R("mod%d_%d" % (l, s_)) for s_ in range(3)] for l in range(NL)]
        jobs = [(l, q) for l in range(NL) for q in range(36)]
        st = {"dma": 0, "mm": 0, "fin": set()}

        def mod_dma(j):
            l, q = jobs[j]
            wv = w_ada[l].rearrange("(k p) n -> p k n", p=128)
            S.dma("poolw", mslots[j % 2][:], wv[:, :, q * 256:(q + 1) * 256], writes=[r_ms[j % 2]])

        def mod_mm(j, bank=None):
            l, q = jobs[j]
            if bank is None:
                pb, pr = self.bank()
                c0 = 0
            else:
                pb, pr, c0 = bank
            groups = []
            for jj in range(2):
                for k in range(8):
                    groups.append((pb[:, c0 + jj:c0 + jj + 1], mslots[j % 2][:, k, jj * 128:(jj + 1) * 128], sc_bf[:, k:k + 1],
                                   k == 0, k == 7))
            mm(S, groups, [r_ms[j % 2], r_sc], [pr])
            s_ = q // 12
            tt(S, "dve", modT[:, l, q * 2:q * 2 + 2], pb[:, c0:c0 + 2], badaT[:, l, q * 2:q * 2 + 2], ALU.add,
               [pr, r_bada], [r_modls[l][s_]])

        def mod_pump(n=1, bank=None):
            for _ in range(n):
                if st["dma"] < len(jobs) and st["dma"] - st["mm"] < 2:
                    mod_dma(st["dma"])
                    st["dma"] += 1
                if st["mm"] < st["dma"] - 1 or (st["dma"] == len(jobs) and st["mm"] < st["dma"]):
                    mod_mm(st["mm"], bank)
                    st["mm"] += 1

        def mod_require(l, s_):
            last = l * 36 + s_ * 12 + 11
            while st["mm"] <= last:
                mod_pump()
            if (l, s_) in st["fin"]:
                return
            st["fin"].add((l, s_))
            r = r_modls[l][s_]
            ts(S, "dve", Acoef[:, l, s_, :], modT[:, l, (3 * s_ + 1) * 8:(3 * s_ + 2) * 8], 1.0, ALU.add, [r], [r])
            tt(S, "dve", Acoef[:, l, s_, :], Acoef[:, l, s_, :], gvec[:, l * 24 + s_ * 8:l * 24 + s_ * 8 + 8],
               ALU.mult, [r, r_gvec], [r])
            ts(S, "dve", Gcoef[:, l, s_, :], modT[:, l, (3 * s_ + 2) * 8:(3 * s_ + 3) * 8],
               0.5 if s_ != 1 else 1.0, ALU.mult, [r], [r])

        self.mod_pump = mod_pump
        self.mod_require = mod_require

        def rms_rstd(c, src_fn, src_regs, nk, inv_n, ones_l):
            pb, pr = self.bank()
            for k in range(nk):
                i = self.sq_rr % 2
                self.sq_rr += 1
                act(S, sqb[:, i, :], src_fn(k), AF.Square, [src_regs[k]], [r_sq[i]])
                mm(S, [(pb[:], ones_l, sqb[:, i, :], k == 0, k == nk - 1)], [r_sq[i], r_ones], [pr])
            act(S, rstd[:], pb[:], AF.Ln, [pr], [r_rstd], bias=EPS, scale=inv_n)
            act(S, rstd[:], rstd[:], AF.Exp, [r_rstd], [r_rstd], scale=-0.5)

        def norm_mod(l, s):
            for c in range(2):
                rms_rstd(c, lambda k: xT[:, k, CH(c)], [xr[k][c] for k in range(8)], 8, 1.0 / D, ones_bf[:])
                for k in range(8):
                    i = self.f32_rr % 3
                    self.f32_rr += 1
                    tt(S, "dve", f32s[:, i, :], xT[:, k, CH(c)], rstd[:], ALU.mult,
                       [xr[k][c], r_rstd], [r_f32s[i]])
                    act(S, hT[:, k, CH(c)], f32s[:, i, :], AF.Identity, [r_f32s[i], r_modls[l][s]],
                        [hr[k][c]], bias=modT[:, l, 3 * s * 8 + k:3 * s * 8 + k + 1],
                        scale=Acoef[:, l, s, k:k + 1])

        def resid_add(l, s, dt_, c, pb, pr):
            i = self.f32_rr % 3
            self.f32_rr += 1
            act(S, f32s[:, i, :], pb[:], AF.Copy, [pr, r_modls[l][s]], [r_f32s[i]], scale=Gcoef[:, l, s, dt_:dt_ + 1])
            tt(S, "dve", xT[:, dt_, CH(c)], xT[:, dt_, CH(c)], f32s[:, i, :], ALU.add,
               [xr[dt_][c], r_f32s[i]], [xr[dt_][c]])

        def ffn(l, s, wg, wu, wdn):
            self.arena_reset()
            aT = self.carve([128, NFF, T], BF16)
            ar = [[R("a%d_%d" % (j, c)) for c in range(2)] for j in range(NFF)]
            sg = [self.carve([128, 512], F32) for _ in range(3)]
            r_sg = RL(3, "sg")
            sg_rr = 0
            mod_require(l, s)
            norm_mod(l, s)
            wgv = wg[l].rearrange("(k p) n -> p k n", p=128)
            wuv = wu[l].rearrange("(k p) n -> p k n", p=128)
            for g in range(NFF // 2):
                c0 = g * 256
                sl, sr = self.wload(self.parts_gu(wg, wu, l, g), key=("gu", l, s, g))
                s3 = self.slot_view(sl, 8)
                if l == 0:
                    mod_pump(1)
                for jj in range(2):
                    j = g * 2 + jj
                    for c in range(2):
                        pg, prg = self.bank()
                        pu, pru = self.bank()
                        groups = []
                        for k in range(8):
                            groups.append((pg[:], s3[:, k, jj * 128:(jj + 1) * 128], hT[:, k, CH(c)], k == 0, k == 7))
                        for k in range(8):
                            groups.append((pu[:], s3[:, k, 256 + jj * 128:256 + (jj + 1) * 128], hT[:, k, CH(c)],
                                           k == 0, k == 7))
                        mm(S, groups, [sr] + [hr[k][c] for k in range(8)], [prg, pru])
                        i = sg_rr % 3
                        sg_rr += 1
                        act(S, sg[i], pg[:], AF.Silu, [prg], [r_sg[i]])
                        tt(S, "dve", aT[:, j, CH(c)], sg[i], pu[:], ALU.mult, [r_sg[i], pru], [ar[j][c]])
            wdv = wdn[l].rearrange("(j p) n -> p j n", p=128)
            for dt_ in range(8):
                sl, sr = self.wload([(lambda sl_: sl_[:, 0:NFF * 128].rearrange("p (j n) -> p j n", j=NFF),
                                      wdv[:, :, dt_ * 128:(dt_ + 1) * 128])])
                if dt_ == 7:
                    if s == 0:
                        self.prefetch(("A1", l), self.parts_win(l, 0, 512))
                        self.prefetch(("A2", l), self.parts_win(l, 512, 512))
                        self.prefetch(("G", l), self.parts_win(l, 1024, 16))
                    elif l + 1 < NL:
                        for g_ in range(2):
                            self.prefetch(("gu", l + 1, 0, g_), self.parts_gu(wd["w1_gate"], wd["w1_up"], l + 1, g_))
                s3 = sl[:, 0:NFF * 128].rearrange("p (j n) -> p j n", j=NFF)
                for c in range(2):
                    pb, pr = self.bank()
                    groups = [(pb[:], s3[:, j, :], aT[:, j, CH(c)], j == 0, j == NFF - 1) for j in range(NFF)]
                    mm(S, groups, [sr] + [ar[j][c] for j in range(NFF)], [pr])
                    resid_add(l, s, dt_, c, pb, pr)

        self.wd = wd
        self.ctx = dict(xT=xT, hT=hT, xr=xr, hr=hr, CH=CH, rms_rstd=rms_rstd, norm_mod=norm_mod,
                        resid_add=resid_add, ones_bf=ones_bf, r_ones=r_ones, gvec=gvec, r_gvec=r_gvec,
                        f32s=f32s, r_f32s=r_f32s, rstd=rstd, r_rstd=r_rstd, modT=modT, sqb=sqb, r_sq=r_sq)


        self.ARENA_BYTES = 84 * 1024
        LPW = 304
        self.LP = dict(BG=0, GM=16, GQK=272, SK=274, CV=278, HB=302)
        d = {}
        d["lp"] = inp("lp", [128, NL, LPW])
        d["cflags"] = inp("cflags", [128, 4])
        d["ident"] = inp("ident", [128, 128])
        d["blockones"] = inp("blockones", [128, 128])
        d["tri"] = inp("tri", [128, 4, 128])
        d["ropeP"] = inp("ropeP", [128, 128])
        d["ropeCS"] = inp("ropeCS", [128, 2, T])
        d["swamask"] = inp("swamask", [128, 8, 384])
        d["gmAB"] = inp("gmAB", [5, 2, T])
        d["fw1"] = inp("fw1", [33, NL, 64])
        d["fw2"] = inp("fw2", [64, NL, 64])
        d["fw3"] = inp("fw3", [64, NL, 256])
        d["fvec"] = inp("fvec", [64, NL, 3])
        d["fb3"] = inp("fb3", [1, NL, 256])
        d["featsT"] = inp("featsT", [33, T])
        d["window"] = inp("window", [128, 8, 256])
        d["dftF"] = inp("dftF", [2, T, T])
        d["dftG"] = inp("dftG", [2, T, T])
        d["sel"] = inp("sel", [4, 130])
        d["C0"] = inp("C0", [NL, 2, 128, 2, 65])
        d["m0rep"] = inp("m0rep", [128, NL, 2, 2])
        d["m0c"] = inp("m0c", [4, NL * 2])
        d["gkT"] = inp("gkT", [NL, 2, 128, 256])
        d["gvp"] = inp("gvp", [NL, 2, 128, 2, 192])
        d["skT"] = inp("skT", [NL, 2, 128, 256])
        d["svp"] = inp("svp", [NL, 2, 128, 2, 192])
        d["o_gk"] = outp("o_gk", [NL, 128, T])
        d["o_gv"] = outp("o_gv", [NL, T, 128])
        d["o_sk"] = outp("o_sk", [NL, 128, T])
        d["o_sv"] = outp("o_sv", [NL, T, 128])
        d["o_C"] = outp("o_C", [NL, 2, 4, 128, 2, 65])
        d["o_m"] = outp("o_m", [NL, 2, 4, 4])
        self.d = d
        if DEBUG:
            self.dbg_ymix = outp("dbg_ymix", [NL, 128, 8, T])
        k = {}
        k["lp"] = sb("lp", [128, NL, LPW]); k["cflags"] = sb("cflags", [128, 4])
        k["ident_f"] = sb("ident_f", [128, 128]); k["ident_bf"] = sb("ident_bf", [128, 128], BF16)
        k["blockones"] = sb("blockones", [128, 128], BF16)
        k["tri"] = sb("tri", [128, 4, 128]); k["ropeP"] = sb("ropeP", [128, 128])
        k["ones_f"] = sb("ones_f", [128, 128])
        k["fw1"] = sb("fw1", [33, NL, 64]); k["fw2"] = sb("fw2", [64, NL, 64]); k["fw3"] = sb("fw3", [64, NL, 256])
        k["fvec"] = sb("fvec", [64, NL, 3]); k["fb3"] = sb("fb3", [1, NL, 256]); k["fs"] = sb("fs", [64, NL, 4])
        k["sel"] = sb("sel", [4, 130]); k["m0rep"] = sb("m0rep", [128, NL, 2, 2]); k["m0c"] = sb("m0c", [4, NL * 2])
        k["Clo"] = sb("Clo", [128, 2, 2, 65]); k["Chi"] = sb("Chi", [128, 2, 2, 65])
        k["Cblo"] = sb("Cblo", [128, 2, 2, 66], BF16); k["Cbhi"] = sb("Cbhi", [128, 2, 2, 66], BF16)
        k["cvb"] = sb("cvb", [128, NL, 6, 2])
        self.k = k
        r_k = R("consts")
        self.r_k = r_k
        for nm in ("lp", "cflags", "tri", "ropeP", "fw1", "fw2", "fw3", "fvec", "fb3", "sel", "m0rep", "m0c"):
            S.dma("sp", k[nm][:], d[nm], writes=[r_k])
        S.dma("sp", k["ident_f"][:], d["ident"], writes=[r_k])
        S.dma("pool", k["ident_bf"][:], d["ident"], writes=[r_k])
        S.dma("pool", k["blockones"][:], d["blockones"], writes=[r_k])
        S.op("pool", lambda e: e.memset(k["ones_f"][:], 1.0), writes=[r_k])
        for nm in ("Clo", "Chi", "Cblo", "Cbhi"):
            S.op("pool", lambda e, nm=nm: e.memset(k[nm][:], 0.0), writes=[r_k])
        i2p = float(1.0 / (2 * math.pi))
        ts(S, "dve", k["fs"][:, :, 0:1], k["fvec"][:, :, 2:3], i2p, ALU.mult, [r_k], [r_k])
        tt(S, "dve", k["fs"][:, :, 1:2], k["fs"][:, :, 0:1], k["fvec"][:, :, 0:1], ALU.mult, [r_k], [r_k])
        tt(S, "dve", k["fs"][:, :, 2:3], k["fs"][:, :, 0:1], k["fvec"][:, :, 1:2], ALU.mult, [r_k], [r_k])
        CV = self.LP["CV"]
        for l in range(NL):
            cvv = k["lp"][:, l, CV:CV + 24].rearrange("p (a b) -> p a b", a=6)
            for (j, col) in ((0, 0), (1, 2)):
                ts(S, "dve", k["cvb"][:, l, :, j:j + 1], cvv[:, :, col:col + 1], k["cflags"][:, 2:3], ALU.mult,
                   [r_k], [r_k], s2=-1.0, op1=ALU.mult)

        for l in range(NL):
            ffn(l, 0, wd["w1_gate"], wd["w1_up"], wd["w1_down"])
            self.mixer(l)
            ffn(l, 2, wd["w2_gate"], wd["w2_up"], wd["w2_down"])

        gfo = NL * 24
        yv = yT_d.rearrange("(k p) t -> p k t", p=128)
        self.arena_reset()
        ost_ = self.carve([128, 2, 512], F32)
        ost = [ost_[:, 0, :], ost_[:, 1, :]]
        r_ost = RL(2, "ost")
        o_rr = 0
        for c in range(2):
            rms_rstd(c, lambda k: xT[:, k, CH(c)], [xr[k][c] for k in range(8)], 8, 1.0 / D, ones_bf[:])
            for k in range(8):
                i = self.f32_rr % 3
                self.f32_rr += 1
                tt(S, "dve", f32s[:, i, :], xT[:, k, CH(c)], rstd[:], ALU.mult, [xr[k][c], r_rstd], [r_f32s[i]])
                o = o_rr % 2
                o_rr += 1
                act(S, ost[o], f32s[:, i, :], AF.Copy, [r_f32s[i], r_gvec], [r_ost[o]],
                    scale=gvec[:, gfo + k:gfo + k + 1])
                S.dma("sp", yv[:, k, CH(c)], ost[o], reads=[r_ost[o]])
        S.finish()

    def mixer(self, l):
        S = self.S
        C = self.ctx
        hT, hr, CH = C["hT"], C["hr"], C["CH"]
        self.arena_reset()
        ymix = self.carve([128, 8, T], BF16)
        ymr = [[R("ym%d_%d" % (k, c)) for c in range(2)] for k in range(8)]
        base = self.aoff
        self.mod_require(l, 1)
        C["norm_mod"](l, 1)
        win = self.wd["w_in"][l].rearrange("(k p) n -> p k n", p=128)
        self.bank_pool = list(range(8))
        self.mix_mlstm(l, ymix, ymr, win)
        self.arena_reset(base)
        if STAGE >= 3:
            self.mix_attn(l, ymix, ymr, win, glob=True)
            self.arena_reset(base)
            self.mix_attn(l, ymix, ymr, win, glob=False)
            self.arena_reset(base)
        if STAGE >= 4:
            self.mix_hyena(l, ymix, ymr, win)
        self.prefetch(("wout", l, 0), self.parts_wout(l, 0))
        self.prefetch(("wout", l, 1), self.parts_wout(l, 1))
        wd_ = self.wd
        for g in range(2):
            self.prefetch(("gu", l, 2, g), self.parts_gu(wd_["w2_gate"], wd_["w2_up"], l, g))
        self.bank_pool = list(range(8))
        if DEBUG:
            f32s, r_f32s = C["f32s"], C["r_f32s"]
            for k in range(8):
                for c in range(2):
                    i = self.f32_rr % 3
                    self.f32_rr += 1
                    act(S, f32s[:, i, :], ymix[:, k, CH(c)], AF.Copy, [ymr[k][c]], [r_f32s[i]])
                    S.dma("sp", self.dbg_ymix[l, :, k, c * 512:(c + 1) * 512], f32s[:, i, :], reads=[r_f32s[i]])
        wov = self.wd["w_out"][l].rearrange("(k p) n -> p k n", p=128)
        for half in range(2):
            sl, sr = self.wload(self.parts_wout(l, half), key=("wout", l, half))
            s3 = self.slot_view(sl, 8)
            for j in range(4):
                dt_ = half * 4 + j
                for c in range(2):
                    pb, pr = self.bank()
                    groups = [(pb[:], s3[:, k, j * 128:(j + 1) * 128], ymix[:, k, CH(c)], k == 0, k == 7) for k in range(8)]
                    mm(S, groups, [sr] + [ymr[k][c] for k in range(8)], [pr])
                    C["resid_add"](l, 1, dt_, c, pb, pr)

    def parts_attn(self, l, glob):
        win = self.wd["w_in"][l].rearrange("(k p) n -> p k n", p=128)
        sv8 = lambda sl_: self.slot_view(sl_, 8)
        q0 = 1040 if glob else 1552
        k0, v0 = q0 + 256, q0 + 384
        return [(lambda sl_: sv8(sl_)[:, :, 0:256], win[:, :, q0:q0 + 256]),
                (lambda sl_: sv8(sl_)[:, :, 256:384], win[:, :, k0:k0 + 128]),
                (lambda sl_: sv8(sl_)[:, :, 384:448], win[:, :, k0 + 64:k0 + 128]),
                (lambda sl_: sv8(sl_)[:, :, 448:512], win[:, :, k0:k0 + 64]),
                (lambda sl_: sv8(sl_)[:, :, 512:640], win[:, :, v0:v0 + 128])]

    def parts_win(self, l, c0, n):
        win = self.wd["w_in"][l].rearrange("(k p) n -> p k n", p=128)
        return [(lambda sl_: self.slot_view(sl_, 8)[:, :, 0:n], win[:, :, c0:c0 + n])]

    def parts_wout(self, l, half):
        wov = self.wd["w_out"][l].rearrange("(k p) n -> p k n", p=128)
        return [(lambda sl_: self.slot_view(sl_, 8)[:, :, 0:512], wov[:, :, half * 512:(half + 1) * 512])]

    def parts_gu(self, wg, wu, l, g):
        wgv = wg[l].rearrange("(k p) n -> p k n", p=128)
        wuv = wu[l].rearrange("(k p) n -> p k n", p=128)
        c0 = g * 256
        return [(lambda sl_: self.slot_view(sl_, 8)[:, :, 0:256], wgv[:, :, c0:c0 + 256]),
                (lambda sl_: self.slot_view(sl_, 8)[:, :, 256:512], wuv[:, :, c0:c0 + 256])]

    def pbank(self):
        i = self.bank_pool[self.bank_rr % len(self.bank_pool)]
        self.bank_rr += 1
        return self.ps[i], self.psr[i]

    def proj_fm(self, c, s3, col0, pb):
        hT, CH = self.ctx["hT"], self.ctx["CH"]
        return [(pb[:], s3[:, k, col0:col0 + 128], hT[:, k, CH(c)], k == 0, k == 7) for k in range(8)]

    def proj_tok(self, tt_, s3, col0, ncols, pb, pc0):
        hT = self.ctx["hT"]
        return [(pb[:, pc0:pc0 + ncols], hT[:, k, tt_ * 128:(tt_ + 1) * 128], s3[:, k, col0:col0 + ncols], k == 0, k == 7)
                for k in range(8)]

    def mix_mlstm(self, l, ymix, ymr, win):
        S, k_, d = self.S, self.k, self.d
        C = self.ctx
        hr, CH = C["hr"], C["CH"]
        LP = self.LP
        lp = k_["lp"]
        r_k = self.r_k
        hall = [hr[k][c] for k in range(8) for c in range(2)]
        sv8 = lambda sl_: self.slot_view(sl_, 8)
        slA1, srA1 = self.wload(self.parts_win(l, 0, 512), key=("A1", l))
        slA2, srA2 = self.wload(self.parts_win(l, 512, 512), key=("A2", l))
        slG, srG = self.wload(self.parts_win(l, 1024, 16), key=("G", l))
        sA1, sA2, sG = sv8(slA1), sv8(slA2), sv8(slG)
        self.prefetch(("attn", l, True), self.parts_attn(l, True))
        if CUT == 1:
            return
        aqT = self.carve([128, 2, T], BF16)
        akp = self.carve([128, 4, T], BF16)
        ktok = self.carve([128, 8, 256], BF16)
        vaug = self.carve([128, 8, 4, 66], BF16)
        sgo = self.carve([128, 8, 256], BF16)
        gts = self.carve([128, 8, 16], F32)
        lf = self.carve([128, 8, 8], F32)
        cum = self.carve([128, 8, 16], F32)
        call = self.carve([128, 8, 8], F32)
        wall = self.carve([128, 8, 8], F32)
        wkl = self.carve([128, 8, 8], F32)
        wkall = self.carve([128, 8, 8], F32)
        wkm = self.carve([128, 8, 8], F32)
        dec = self.carve([128, 8, 4], F32)
        lfB = self.carve([128, 8, 128], F32)
        E = self.carve([128, 8, 128], F32)
        AT = self.carve([128, 2, 4, 128], BF16)
        hf = self.carve([128, 8, 256], F32)
        numt = self.carve([128, 2, 260], F32)
        h64 = self.carve([128, 2, 256], F32)
        dsm = self.carve([128, 2, 8], F32)
        kwp = self.carve([128, 2, 4, 192], BF16)
        yatok = self.carve([128, 8, 256], BF16)
        snap = self.carve([128, 2, 4, 130], F32)
        e0 = self.carve([128, 2, 2], F32)
        ssall = self.carve([128, 8, 4], F32)
        sq1 = self.carve([128, 2, 256], F32)
        mst = self.carve([128, 64], F32)
        scl = self.carve([128, 2, 8], F32)
        r_aq, r_akp = RL(2, "aq"), RL(2, "akp")
        r_ktok, r_vaug, r_sgo, r_gts = RL(8, "ktok"), RL(8, "vaug"), RL(8, "sgo"), R("gts")
        r_gate = R("gate")
        r_lfB, r_E, r_AT = RL(2, "lfB"), RL(2, "E"), RL(2, "AT")
        r_hf = RL(8, "hf")
        r_num, r_tmpn, r_h64, r_dsm, r_kwp = RL(2, "num"), RL(2, "tmpn"), RL(2, "h64"), RL(2, "dsm"), RL(2, "kwp")
        r_C, r_Cb = RL(2, "C"), RL(2, "Cb")
        r_snap = [[R("snap") for _ in range(4)] for _ in range(2)]
        r_ms, r_scl = R("mst"), R("scl")
        r_ya = RL(8, "ya")
        r_ss = R("ss")
        r_sq1 = RL(2, "sq1")
        S.op("pool", lambda e: e.memset(akp, 0.0), writes=r_akp)
        S.op("pool", lambda e: e.memset(kwp, 0.0), writes=r_kwp)
        S.op("pool", lambda e: e.memset(vaug[:, :, :, 64:65], 1.0), writes=r_vaug)
        if CUT == 2:
            return
        for t2 in range(2):
            for c in range(2):
                pb, pr = self.pbank()
                mm(S, self.proj_fm(c, sA1, t2 * 128, pb), [srA1] + hall, [pr])
                act(S, aqT[:, t2, CH(c)], pb[:], AF.Copy, [pr], [r_aq[c]])
        if CUT == 21:
            return
        for t2 in range(2):
            for c in range(2):
                pb, pr = self.pbank()
                mm(S, self.proj_fm(c, sA1, 256 + t2 * 128, pb), [srA1] + hall, [pr])
                act(S, akp[0:64, 2 * t2, CH(c)], pb[0:64, :], AF.Copy, [pr], [r_akp[c]])
                cp(S, "dve", akp[64:128, 2 * t2 + 1, CH(c)], pb[64:128, :], [pr], [r_akp[c]])
        if CUT == 22:
            return
        BG = LP["BG"]
        for t_ in range(8):
            p1, pr1 = self.pbank()
            p2, pr2 = self.pbank()
            g = self.proj_tok(t_, sA1, 256, 256, p1, 0) + self.proj_tok(t_, sA2, 0, 256, p1, 256)
            g += self.proj_tok(t_, sA2, 256, 256, p2, 0) + self.proj_tok(t_, sG, 0, 16, p2, 256)
            mm(S, g, [srA1, srA2, srG] + hall, [pr1, pr2])
            if CUT == 23:
                continue
            act(S, ktok[:, t_, :], p1[:, 0:256], AF.Copy, [pr1], [r_ktok[t_]])
            cp(S, "dve", vaug[:, t_, :, 0:64], p1[:, 256:512].rearrange("p (a b) -> p a b", a=4), [pr1], [r_vaug[t_]])
            if CUT == 24:
                continue
            act(S, sgo[:, t_, :], p2[:, 0:256], AF.Sigmoid, [pr2], [r_sgo[t_]])
            tt(S, "dve", gts[:, t_, :], p2[:, 256:272], lp[:, l, BG:BG + 16], ALU.add, [pr2, r_k], [r_gts])
            self.mod_pump()
        if CUT in (3, 23, 24):
            return
        ai, af = gts[:, :, 0:8], gts[:, :, 8:16]
        act(S, lf, af, AF.Exp, [r_gts], [r_gate], scale=-1.0)
        act(S, lf, lf, AF.Ln, [r_gate], [r_gate], bias=1.0)
        ts(S, "dve", lf, lf, -1.0, ALU.mult, [r_gate], [r_gate])
        tri = k_["tri"]
        pbc, prc = self.pbank()
        g = []
        for t_ in range(8):
            g.append((pbc[:, t_ * 16:t_ * 16 + 4], tri[:, 0, :], lf[:, t_, 0:4], True, True))
            g.append((pbc[:, t_ * 16 + 4:t_ * 16 + 8], tri[:, 1, :], lf[:, t_, 4:8], True, True))
            g.append((pbc[:, t_ * 16 + 8:t_ * 16 + 16], k_["ones_f"][:], lf[:, t_, 0:8], True, True))
        mm(S, g, [r_gate, r_k], [prc])
        cp(S, "dve", cum, pbc[:, 0:128].rearrange("p (a b) -> p a b", a=8), [prc], [r_gate])
        bc, bt = cum[:, :, 0:8], cum[:, :, 8:16]
        tt(S, "dve", call, ai, bc, ALU.subtract, [r_gts, r_gate], [r_gate])
        ts(S, "dve", call, call, LN8, ALU.add, [r_gate], [r_gate])
        act(S, wall, bc, AF.Exp, [r_gate], [r_gate])
        tt(S, "dve", wkl, call, bt, ALU.add, [r_gate], [r_gate])
        act(S, wkall, wkl, AF.Exp, [r_gate], [r_gate])
        ts(S, "dve", wkm, wkl, -LN8, ALU.add, [r_gate], [r_gate])
        for g_ in range(2):
            rows = slice(g_ * 64, (g_ + 1) * 64)
            act(S, dec[rows, :, :], cum[rows, :, 8 + g_:16:2], AF.Exp, [r_gate], [r_gate])
        if CUT == 4:
            return
        ident_f = k_["ident_f"]
        for dr in range(2):
            pbm, prm = self.pbank()
            g = [(pbm[0:4, t_:t_ + 1], lf[:, t_, dr * 4:dr * 4 + 4], k_["ones_f"][:, 0:1], True, True) for t_ in range(8)]
            mm(S, g, [r_gate, r_k], [prm])
            cp(S, "dve", mst[0:4, dr * 8:dr * 8 + 8], pbm[0:4, 0:8], [prm], [r_ms])
            for hh in range(2):
                pbt, prt = self.pbank()
                for q in range(4):
                    t_ = hh * 4 + q
                    S.op("pe", lambda e, t_=t_, q=q, pbt=pbt: e.transpose(pbt[0:4, q * 128:(q + 1) * 128],
                                                                          wkm[:, t_, dr * 4:dr * 4 + 4], ident_f[:]),
                         [r_gate, r_k], [prt])
                S.op("dve", lambda e, pbt=pbt, hh=hh: e.tensor_reduce(
                    out=mst[0:4, 16 + dr * 8 + hh * 4:16 + dr * 8 + hh * 4 + 4],
                    in_=pbt[0:4, :].rearrange("p (a b) -> p a b", a=4), axis=AX.X, op=ALU.max), [prt], [r_ms])
            bv_ = mst[0:4, dr * 8:dr * 8 + 8].rearrange("p (a b) -> p a b", a=4)
            av_ = mst[0:4, 16 + dr * 8:16 + dr * 8 + 8].rearrange("p (a b) -> p a b", a=4)
            fi, se = (0, 1) if dr == 0 else (1, 0)
            mf = mst[0:4, 32 + dr * 4:32 + dr * 4 + 4]
            ts(S, "dve", mf, bv_[:, :, fi], k_["m0c"][0:4, l * 2 + dr:l * 2 + dr + 1], ALU.add, [r_ms, r_k], [r_ms])
            tt(S, "dve", mf, mf, av_[:, :, fi], ALU.max, [r_ms], [r_ms])
            tt(S, "dve", mf, mf, bv_[:, :, se], ALU.add, [r_ms], [r_ms])
            tt(S, "dve", mf, mf, av_[:, :, se], ALU.max, [r_ms], [r_ms])
            S.dma("sp", d["o_m"][l, dr], mf, reads=[r_ms])
            en = mst[0:4, 40 + dr * 4:40 + dr * 4 + 4]
            act(S, en, mf, AF.Exp, [r_ms], [r_ms], scale=-1.0)
            rhs2 = mst[0:4, 48 + dr * 8:48 + dr * 8 + 8]
            tt(S, "dve", rhs2.rearrange("p (a b) -> p a b", a=4), en.unsqueeze(2).to_broadcast([4, 4, 2]),
               k_["sel"][0:4, 128:130].unsqueeze(1).to_broadcast([4, 4, 2]), ALU.mult, [r_ms, r_k], [r_ms])
            pbs, prs = self.pbank()
            mm(S, [(pbs[:, 0:8], k_["sel"][0:4, 0:128], rhs2, True, True)], [r_ms, r_k], [prs])
            cp(S, "dve", scl[:, dr, :], pbs[:, 0:8], [prs], [r_scl])
        if CUT == 5:
            return
        Clo, Chi, Cblo, Cbhi = k_["Clo"], k_["Chi"], k_["Cblo"], k_["Cbhi"]
        act(S, e0, k_["m0rep"][:, l, :, :], AF.Exp, [r_k], [r_gate])
        halves = ((slice(0, 64), Clo, Cblo), (slice(64, 128), Chi, Cbhi))
        for dr in range(2):
            S.dma("sp", Clo[:, dr, :, :], d["C0"][l, dr, :, :, :], writes=[r_C[dr]])
            tt(S, "dve", Clo[:, dr, :, :], Clo[:, dr, :, :], e0[:, dr, :].unsqueeze(2).to_broadcast([128, 2, 65]),
               ALU.mult, [r_C[dr], r_gate], [r_C[dr]])
            for (rows, Cx, Cbx) in halves:
                act(S, Cbx[rows, dr, :, 0:65], Clo[rows, dr, :, :], AF.Copy, [r_C[dr]], [r_Cb[dr]])
        keep = k_["cflags"][:, 1:2]

        tmpn2 = self.carve([128, 2, 2, 260], F32)
        r_tmpn2 = [[R("tn00"), R("tn01")], [R("tn10"), R("tn11")]]
        v3 = lambda ap: ap.rearrange("p (a b) -> p a b", a=4)

        def state_gen(dr, t_):
            tok = slice(t_ * 128, (t_ + 1) * 128)
            chs = slice(dr * 4, dr * 4 + 4)
            b_ = t_ % 2
            pst, prst = self.ps[3 + 4 * dr], self.psr[3 + 4 * dr]
            tt(S, "pool", kwp[:, dr, :, 64:128], ktok[:, t_, :].rearrange("p (a b) -> p a b", a=4),
               wkall[:, t_, chs].unsqueeze(2).to_broadcast([128, 4, 64]), ALU.mult, [r_ktok[t_], r_gate], [r_kwp[dr]])
            yield
            g = [(pst[:, h * 65:(h + 1) * 65], aqT[:, h // 2, tok], (Cblo if h % 2 == 0 else Cbhi)[:, dr, h // 2, 0:65], True, True)
                 for h in range(4)]
            for j in range(2):
                o = pst[:, 260 + j * 65:260 + (j + 1) * 65]
                g.append((o, kwp[:, dr, 2 * j, 64:192], vaug[:, t_, 2 * j, 0:65], True, False))
                g.append((o, kwp[:, dr, 2 * j + 1, 0:128], vaug[:, t_, 2 * j + 1, 0:65], False, True))
            mm(S, g, r_aq + [r_Cb[dr], r_kwp[dr], r_vaug[t_]], [prst])
            yield
            tt(S, "dve", v3(tmpn2[:, dr, b_, :]), v3(pst[:, 0:260]), wall[:, t_, chs].unsqueeze(2).to_broadcast([128, 4, 65]),
               ALU.mult, [prst, r_gate], [r_tmpn2[dr][b_]])
            yield
            tt(S, "dve", Clo[:, dr, :, :], Clo[:, dr, :, :],
               dec[:, t_, dr * 2:dr * 2 + 2].unsqueeze(2).to_broadcast([128, 2, 65]), ALU.mult,
               [r_C[dr], r_gate], [r_C[dr]])
            yield
            tt(S, "dve", Clo[:, dr, :, :], Clo[:, dr, :, :], pst[:, 260:390].rearrange("p (a b) -> p a b", a=2),
               ALU.add, [r_C[dr], prst], [r_C[dr]])
            yield
            end = (t_ % 2 == 1) if dr == 0 else (t_ % 2 == 0)
            if end:
                sq_ = t_ // 2
                tt(S, "dve", snap[:, dr, sq_, :].rearrange("p (a b) -> p a b", a=2), Clo[:, dr, :, :],
                   scl[:, dr, sq_ * 2:sq_ * 2 + 2].unsqueeze(2).to_broadcast([128, 2, 65]), ALU.mult,
                   [r_C[dr], r_scl], [r_snap[dr][sq_]])
                yield
                S.dma("sp", d["o_C"][l, dr, sq_], snap[:, dr, sq_, :].rearrange("p (a b) -> p a b", a=2),
                      reads=[r_snap[dr][sq_]])
                yield
                ts(S, "dve", Clo[:, dr, :, :], Clo[:, dr, :, :], keep, ALU.mult, [r_C[dr], r_k], [r_C[dr]])
                yield
            for (rows, Cx, Cbx) in halves:
                act(S, Cbx[rows, dr, :, 0:65], Clo[rows, dr, :, :], AF.Copy, [r_C[dr]], [r_Cb[dr]])
                yield

        def out_gen(dr, t_):
            tok = slice(t_ * 128, (t_ + 1) * 128)
            chs = slice(dr * 4, dr * 4 + 4)
            b_ = t_ % 2
            pbe, pre = self.ps[0 + 4 * dr], self.psr[0 + 4 * dr]
            pbs_, prs_ = self.ps[1 + 4 * dr], self.psr[1 + 4 * dr]
            pbi, pri = self.ps[2 + 4 * dr], self.psr[2 + 4 * dr]
            cp(S, "pool", lfB[:, chs, :], lf[:, t_, chs].unsqueeze(2).to_broadcast([128, 4, 128]), [r_gate], [r_lfB[dr]])
            yield
            g = []
            for h in range(4):
                o = pbe[:, h * 128:(h + 1) * 128]
                g.append((o, lfB[:, dr * 4 + h, :], tri[:, dr, :], True, False))
                g.append((o, ident_f[:], tri[:, 2 + dr, :], False, True))
            mm(S, g, [r_lfB[dr], r_k], [pre])
            yield
            g = [(pbs_[:, h * 128:(h + 1) * 128], akp[:, h, tok], aqT[:, h // 2, tok], True, True) for h in range(4)]
            mm(S, g, r_akp + r_aq, [prs_])
            yield
            for h in range(4):
                act(S, E[:, dr * 4 + h, :], pbe[:, h * 128:(h + 1) * 128], AF.Exp, [pre, r_gate], [r_E[dr]],
                    bias=call[:, t_, dr * 4 + h:dr * 4 + h + 1])
                yield
            tt(S, "dve", AT[:, dr, :, :], pbs_[:].rearrange("p (a b) -> p a b", a=4), E[:, chs, :], ALU.mult,
               [prs_, r_E[dr]], [r_AT[dr]])
            yield
            g = [(pbi[:, h * 65:(h + 1) * 65], AT[:, dr, h, :], vaug[:, t_, h, 0:65], True, True) for h in range(4)]
            mm(S, g, [r_AT[dr], r_vaug[t_]], [pri])
            yield
            tt(S, "dve", numt[:, dr, :], pbi[:, 0:260], tmpn2[:, dr, b_, :], ALU.add, [pri, r_tmpn2[dr][b_]], [r_num[dr]])
            yield
            nv = v3(numt[:, dr, :])
            dn, rd = dsm[:, dr, 0:4], dsm[:, dr, 4:8]
            ts(S, "dve", dn, nv[:, :, 64], -1.0, ALU.mult, [r_num[dr]], [r_dsm[dr]], s2=1.0, op1=ALU.max)
            yield
            tt(S, "dve", dn, dn, nv[:, :, 64], ALU.max, [r_num[dr], r_dsm[dr]], [r_dsm[dr]])
            yield
            recip(S, rd, dn, [r_dsm[dr]], [r_dsm[dr]])
            yield
            first = (t_ <= 3) if dr == 0 else (t_ >= 4)
            if first:
                tt(S, "dve", v3(hf[:, t_, :]), nv[:, :, 0:64], rd.unsqueeze(2).to_broadcast([128, 4, 64]), ALU.mult,
                   [r_num[dr], r_dsm[dr]], [r_hf[t_]])
                yield
            else:
                tt(S, "dve", v3(h64[:, dr, :]), nv[:, :, 0:64], rd.unsqueeze(2).to_broadcast([128, 4, 64]), ALU.mult,
                   [r_num[dr], r_dsm[dr]], [r_h64[dr]])
                yield
                tt(S, "dve", hf[:, t_, :], hf[:, t_, :], h64[:, dr, :], ALU.add, [r_h64[dr], r_hf[t_]], [r_hf[t_]])
                yield

        def zip_run(gens):
            gens = list(gens)
            while gens:
                for g__ in list(gens):
                    try:
                        next(g__)
                    except StopIteration:
                        gens.remove(g__)

        def pump_gen(npump):
            for _ in range(npump):
                for _ in range(5):
                    yield
                self.mod_pump(bank=(self.ps[2], self.psr[2], 300))
                yield

        order = [list(range(8)), list(range(7, -1, -1))]
        zip_run([state_gen(0, order[0][0]), state_gen(1, order[1][0])])
        for i in range(8):
            gens = [out_gen(0, order[0][i]), out_gen(1, order[1][i])]
            if i + 1 < 8:
                gens = [state_gen(0, order[0][i + 1]), state_gen(1, order[1][i + 1])] + gens
            gens.append(pump_gen(3))
            zip_run(gens)
        GM = LP["GM"]
        for t_ in range(8):
            b_ = t_ % 2
            tt(S, "dve", sq1[:, b_, :], hf[:, t_, :], hf[:, t_, :], ALU.mult, [r_hf[t_]], [r_sq1[b_]])
            S.op("dve", lambda e, t_=t_, b_=b_: e.tensor_reduce(out=ssall[:, t_, :],
                                                              in_=sq1[:, b_, :].rearrange("p (a b) -> p a b", a=4),
                                                              axis=AX.X, op=ALU.add), [r_sq1[b_]], [r_ss])
        act(S, ssall, ssall, AF.Ln, [r_ss], [r_ss], bias=EPS, scale=1.0 / 64)
        act(S, ssall, ssall, AF.Exp, [r_ss], [r_ss], scale=-0.5)
        ident_bf = k_["ident_bf"]
        for t_ in range(8):
            b_ = t_ % 2
            tt(S, "dve", sq1[:, b_, :].rearrange("p (a b) -> p a b", a=4), hf[:, t_, :].rearrange("p (a b) -> p a b", a=4),
               ssall[:, t_, :].unsqueeze(2).to_broadcast([128, 4, 64]), ALU.mult, [r_hf[t_], r_ss], [r_sq1[b_]])
            tt(S, "pool", h64[:, b_, :], sgo[:, t_, :], lp[:, l, GM:GM + 256], ALU.mult, [r_sgo[t_], r_k], [r_h64[b_]])
            tt(S, "dve", yatok[:, t_, :], sq1[:, b_, :], h64[:, b_, :], ALU.mult, [r_h64[b_], r_sq1[b_]], [r_ya[t_]])
            self.mod_pump()
            pbt, prt = self.pbank()
            pv = pbt[:].bitcast(BF16)
            for t2 in range(2):
                S.op("pe", lambda e, t2=t2, pv=pv, t_=t_: e.transpose(pv[:, t2 * 128:(t2 + 1) * 128],
                                                                      yatok[:, t_, t2 * 128:(t2 + 1) * 128], ident_bf[:]),
                     [r_ya[t_], r_k], [prt])
            c = t_ // 4
            for t2 in range(2):
                if t2 == 0:
                    act(S, ymix[:, t2, t_ * 128:(t_ + 1) * 128], pv[:, t2 * 128:(t2 + 1) * 128], AF.Copy, [prt], [ymr[t2][c]])
                else:
                    cp(S, "dve", ymix[:, t2, t_ * 128:(t_ + 1) * 128], pv[:, t2 * 128:(t2 + 1) * 128], [prt], [ymr[t2][c]])

    def mix_attn(self, l, ymix, ymr, win, glob):
        S, k_, d = self.S, self.k, self.d
        C = self.ctx
        hr, CH = C["hr"], C["CH"]
        LP = self.LP
        lp = k_["lp"]
        r_k = self.r_k
        hall = [hr[k][c] for k in range(8) for c in range(2)]
        sv8 = lambda sl_: self.slot_view(sl_, 8)
        q0 = 1040 if glob else 1552
        k0, v0 = q0 + 256, q0 + 384
        sl, sr = self.wload(self.parts_attn(l, glob), key=("attn", l, glob))
        if glob:
            self.prefetch(("attn", l, False), self.parts_attn(l, False))
        else:
            self.prefetch(("D1", l), self.parts_win(l, 2064, 512))
            self.prefetch(("D2", l), self.parts_win(l, 2576, 256))
        s3 = sv8(sl)
        ymb = 2 if glob else 4
        cs = self.carve([128, 2, T], F32)
        qpad = self.carve([128, 4, T], BF16)
        kfull = self.carve([128, 2, 1280], BF16)
        vpad = self.carve([128, 10, 2, 192], BF16)
        kout = self.carve([128, T], F32)
        vout = self.carve([128, 8, 128], F32)
        PT = self.carve([128, 3, 512], BF16)
        rden = self.carve([128, 2, 512], F32)
        raw = self.carve([128, 2, 512], F32)
        tb = self.carve([128, 2, 512], F32)
        gm = self.carve([128, 2, T], BF16)
        smask = self.carve([128, 8, 384], BF16) if not glob else None
        es = self.carve([128, 4], F32)
        r_cs, r_qp, r_kf, r_vp, r_kout, r_vout = R("cs"), RL(2, "qp"), R("kf"), RL(10, "vp"), R("kout"), R("vout")
        r_PT, r_rden, r_raw, r_tb, r_gm, r_sm, r_es = RL(3, "PT"), RL(2, "rden"), RL(2, "raw"), RL(2, "tb"), R("gm"), R("sm"), R("es")
        S.dma("sp", cs, d["ropeCS"], writes=[r_cs])
        S.op("pool", lambda e: e.memset(qpad, 0.0), writes=r_qp)
        S.op("pool", lambda e: e.memset(vpad[:, 2:10, :, :], 0.0), writes=r_vp[2:])
        kTd, vpd = (d["gkT"], d["gvp"]) if glob else (d["skT"], d["svp"])
        for x in range(2):
            S.dma("pool", kfull[:, x, 0:256], kTd[l, x], writes=[r_kf])
            S.dma("pool", vpad[:, x, :, :], vpd[l, x], writes=[r_vp[x]])
        if glob:
            S.dma("pool", gm[0:5, :, :], d["gmAB"], writes=[r_gm])
        if not glob:
            S.dma("pool", smask, d["swamask"], writes=[r_sm])
            SK = LP["SK"]
            act(S, es, lp[:, l, SK:SK + 4], AF.Exp, [r_k], [r_es])
        GQK = LP["GQK"]
        ropeP = k_["ropeP"]
        rstd2 = self.carve([128, 2, 512], F32)
        r_rstd2 = RL(2, "rstd2")

        def prelude(ti, c, b_):
            col0 = ti * 128
            pb, pr = self.pbank()
            mm(S, self.proj_fm(c, s3, col0, pb), [sr] + hall, [pr])
            yield
            rw = raw[:, b_, :]
            if glob:
                act(S, rw, pb[:], AF.Copy, [pr], [r_raw[b_]])
                yield
                i = self.sq_rr % 2
                self.sq_rr += 1
                sqb, r_sq = C["sqb"], C["r_sq"]
                act(S, sqb[:, i, :], rw, AF.Square, [r_raw[b_]], [r_sq[i]])
                yield
                p2, pr2 = self.pbank()
                mm(S, [(p2[:], k_["blockones"][:], sqb[:, i, :], True, True)], [r_sq[i], r_k], [pr2])
                yield
                rstd, r_rstd = rstd2[:, b_, :], r_rstd2[b_]
                act(S, rstd, p2[:], AF.Ln, [pr2], [r_rstd], bias=EPS, scale=1.0 / 64)
                yield
                act(S, rstd, rstd, AF.Exp, [r_rstd], [r_rstd], scale=-0.5)
                yield
                tt(S, "dve", rw, rw, rstd, ALU.mult, [r_raw[b_], r_rstd], [r_raw[b_]])
                yield
                gcol = GQK + (0 if ti < 2 else 1)
                act(S, rw, rw, AF.Copy, [r_raw[b_], r_k], [r_raw[b_]], scale=lp[:, l, gcol:gcol + 1])
                yield
            else:
                act(S, rw, pb[:], AF.Copy, [pr], [r_raw[b_]])
                yield
            p3, pr3 = self.pbank()
            mm(S, [(p3[:], ropeP[:], rw, True, True)], [r_raw[b_], r_k], [pr3])
            yield
            ta, tb_ = rw, tb[:, b_, :]
            tt(S, "dve", ta, rw, cs[:, 0, CH(c)], ALU.mult, [r_raw[b_], r_cs], [r_raw[b_]])
            yield
            tt(S, "dve", tb_, p3[:], cs[:, 1, CH(c)], ALU.mult, [pr3, r_cs], [r_tb[b_]])
            yield
            if ti < 2:
                for g_ in range(2):
                    rows = slice(g_ * 64, (g_ + 1) * 64)
                    tt(S, "dve", qpad[rows, 2 * ti + g_, CH(c)], ta[rows, :], tb_[rows, :], ALU.add, [r_tb[b_], r_raw[b_]], [r_qp[c]])
                    yield
            elif ti == 2:
                tt(S, "dve", kout[:, CH(c)], ta, tb_, ALU.add, [r_tb[b_], r_raw[b_]], [r_kout])
                yield
                act(S, kfull[:, 0, 256 + c * 512:256 + (c + 1) * 512], kout[:, CH(c)], AF.Copy, [r_kout], [r_kf])
                yield
            else:
                tt(S, "dve", kfull[:, 1, 256 + c * 512:256 + (c + 1) * 512], ta, tb_, ALU.add, [r_tb[b_], r_raw[b_]], [r_kf])
                yield

        its = [(ti, c) for ti in range(4) for c in range(2)]
        for i0 in range(0, 8, 2):
            gens = [prelude(its[i0][0], its[i0][1], 0), prelude(its[i0 + 1][0], its[i0 + 1][1], 1)]
            while gens:
                for g__ in list(gens):
                    try:
                        next(g__)
                    except StopIteration:
                        gens.remove(g__)
        S.dma("sp", (d["o_gk"] if glob else d["o_sk"])[l], kout, reads=[r_kout])
        for t_ in range(8):
            pb, pr = self.pbank()
            mm(S, self.proj_tok(t_, s3, 512, 128, pb, 0), [sr] + hall, [pr])
            act(S, vout[:, t_, :], pb[:, 0:128], AF.Copy, [pr], [r_vout])
            cp(S, "dve", vpad[:, 2 + t_, :, 64:128], pb[:, 0:128].rearrange("p (a b) -> p a b", a=2), [pr], [r_vp[2 + t_]])
        S.dma("sp", (d["o_gv"] if glob else d["o_sv"])[l].rearrange("(t p) n -> p t n", p=128), vout, reads=[r_vout])
        self.bank_pool = [0, 1, 2, 3]
        ones_bf = C["ones_bf"]
        r_ones = C["r_ones"]
        ident_bf = k_["ident_bf"]
        ctxb = k_["cflags"][:, 0:1]
        pt_rr = 0
        acc_rr = 0
        for h in range(4):
            kv, g_ = h // 2, h % 2
            kx = 0 if g_ == kv else 1
            rows = slice(g_ * 64, (g_ + 1) * 64)
            vw = slice(64, 192) if g_ == 0 else slice(0, 128)
            for c in range(2):
                if glob:
                    tiles = [(mt, 0, 512) for mt in range(10)]
                else:
                    tiles = [(0, 0, 512), (1, 0, 512)]
                    for j in range(8):
                        lo, hi = max((j - 1) * 128, c * 512), min((j + 2) * 128, (c + 1) * 512)
                        if hi > lo:
                            tiles.append((2 + j, lo - c * 512, hi - c * 512))
                ai_ = 4 + 2 * (acc_rr % 2)
                acc_rr += 1
                pn, prn, pd, prd = self.ps[ai_], self.psr[ai_], self.ps[ai_ + 1], self.psr[ai_ + 1]
                pend = []

                def qk(idx):
                    mt, lo, hi = tiles[idx]
                    pb, pr = self.pbank()
                    qs = slice(c * 512 + lo, c * 512 + hi)
                    g = [(pb[:, lo:hi], kfull[:, kx, mt * 128:(mt + 1) * 128], qpad[:, h, qs], True, mt < 2)]
                    rd = [r_kf, r_qp[c]]
                    if mt >= 2:
                        if glob:
                            g.append((pb[:, lo:hi], gm[0:5, 0, (mt - 2) * 128:(mt - 1) * 128], gm[0:5, 1, qs], False, True))
                            rd.append(r_gm)
                        else:
                            j = mt - 2
                            m0_ = c * 512 + lo - (j - 1) * 128
                            g.append((pb[:, lo:hi], ident_bf[:], smask[:, j, m0_:m0_ + (hi - lo)], False, True))
                            rd += [r_sm, r_k]
                    mm(S, g, rd, [pr])
                    return pb, pr

                nt = len(tiles)
                look = 2
                for idx in range(min(look, nt)):
                    pend.append(qk(idx))
                for idx in range(nt):
                    mt, lo, hi = tiles[idx]
                    pb, pr = pend.pop(0)
                    if idx + look < nt:
                        pend.append(qk(idx + look))
                    pi = pt_rr % 3
                    pt_rr += 1
                    act(S, PT[:, pi, lo:hi], pb[:, lo:hi], AF.Exp, [pr, r_k], [r_PT[pi]],
                        bias=(ctxb if mt < 2 else 0.0), scale=0.125)
                    g = [(pn[:, lo:hi], vpad[:, mt, kv, vw], PT[:, pi, lo:hi], idx == 0, idx == nt - 1),
                         (pd[:, lo:hi], ones_bf[:], PT[:, pi, lo:hi], idx == 0, idx == nt - 1)]
                    mm(S, g, [r_vp[mt], r_PT[pi], r_ones], [prn, prd])
                ri = (h * 2 + c) % 2
                if glob:
                    recip(S, rden[rows, ri, :], pd[rows, :], [prd], [r_rden[ri]])
                else:
                    ts(S, "dve", rden[rows, ri, :], pd[rows, :], es[rows, h:h + 1], ALU.add, [prd, r_es], [r_rden[ri]])
                    recip(S, rden[rows, ri, :], rden[rows, ri, :], [r_rden[ri]], [r_rden[ri]])
                tt(S, "dve", ymix[rows, ymb + kv, CH(c)], pn[rows, :], rden[rows, ri, :], ALU.mult, [prn, r_rden[ri]],
                   [ymr[ymb + kv][c]])
        self.bank_pool = list(range(8))

    def mix_hyena(self, l, ymix, ymr, win):
        S, k_, d = self.S, self.k, self.d
        C = self.ctx
        hr, CH = C["hr"], C["CH"]
        LP = self.LP
        lp = k_["lp"]
        r_k = self.r_k
        hall = [hr[k][c] for k in range(8) for c in range(2)]
        sv8 = lambda sl_: self.slot_view(sl_, 8)
        TWO_PI = float(2 * math.pi)
        xoff = self.aoff
        feats = self.carve([128, T], F32)
        z1 = self.carve([128, T], F32)
        z2 = self.carve([128, T], F32)
        wnd = self.carve([128, 8, 256], F32)
        rr = self.carve([128, 512], F32)
        ii = self.carve([128, 512], I32)
        kf = self.carve([128, 512], F32)
        xend = self.aoff
        self.aoff = xoff
        raw = self.carve([128, 3, T], F32)
        uct = self.carve([128, 2, T], F32)
        pa = self.carve([128, 2, 256], F32)
        pq = self.carve([128, 2, 256], F32)
        yt = self.carve([128, 2, 256], F32)
        assert self.aoff <= xend
        self.aoff = xend
        r_X = R("X")
        x0 = self.carve([128, 2, T], F32)
        zf = self.carve([128, 2, T], F32)
        zbf = self.carve([128, 2, T], BF16)
        zh = self.carve([128, 8, 512], BF16)
        ZH = self.carve([128, 2, 512], F32)
        Y = self.carve([128, 8, 2, 256], BF16)
        pa2 = self.carve([128, 2, 256], F32)
        yt2 = ZH[:, :, 0:256]
        r_x0, r_zf, r_zbf = RL(2, "x0"), RL(2, "zf"), RL(2, "zbf")
        r_zh = RL(8, "zh")
        r_ZH, r_Y, r_pa, r_pq, r_yt = R("ZH"), RL(8, "Y"), R("pa"), R("pq"), RL(2, "yt")
        S.dma("sp", feats[0:33, :], d["featsT"], writes=[r_X])
        S.dma("sp", wnd, d["window"], writes=[r_X])
        fs = k_["fs"]

        def sin_layer(pb, pr, dst, bcol):
            ts(S, "dve", rr[0:64, :], pb[0:64, :], fs[:, l, 0:1], ALU.mult, [pr, r_k], [r_X], s2=fs[:, l, bcol:bcol + 1],
               op1=ALU.add)
            cp(S, "dve", ii[0:64, :], rr[0:64, :], [r_X], [r_X])
            cp(S, "dve", kf[0:64, :], ii[0:64, :], [r_X], [r_X])
            tt(S, "dve", rr[0:64, :], rr[0:64, :], kf[0:64, :], ALU.subtract, [r_X], [r_X])
            act(S, dst, rr[0:64, :], AF.Sin, [r_X], [r_X], scale=TWO_PI)

        def zip2(gens):
            gens = list(gens)
            while gens:
                for g__ in list(gens):
                    try:
                        next(g__)
                    except StopIteration:
                        gens.remove(g__)

        r_fc = [[R("fc%d%d" % (a_, c_)) for c_ in range(2)] for a_ in range(2)]

        def sin_chain(layer, c):
            cs_ = slice(c * 256, (c + 1) * 256)
            for hh in range(2):
                q_ = slice(c * 512 + hh * 256, c * 512 + (hh + 1) * 256)
                pb, pr = self.pbank()
                if layer == 0:
                    mm(S, [(pb[0:64, 0:256], k_["fw1"][0:33, l, :], feats[0:33, q_], True, True)], [r_X, r_k], [pr])
                else:
                    mm(S, [(pb[0:64, 0:256], k_["fw2"][0:64, l, :], z1[0:64, q_], True, True)], [r_fc[0][c], r_k], [pr])
                yield
                bcol = 1 + layer
                rg = R("tmp")
                ts(S, "dve", rr[0:64, cs_], pb[0:64, 0:256], fs[:, l, 0:1], ALU.mult, [pr, r_k], [r_fc[1][c]],
                   s2=fs[:, l, bcol:bcol + 1], op1=ALU.add)
                yield
                cp(S, "dve", ii[0:64, cs_], rr[0:64, cs_], [r_fc[1][c]], [r_fc[1][c]])
                yield
                cp(S, "dve", kf[0:64, cs_], ii[0:64, cs_], [r_fc[1][c]], [r_fc[1][c]])
                yield
                tt(S, "dve", rr[0:64, cs_], rr[0:64, cs_], kf[0:64, cs_], ALU.subtract, [r_fc[1][c]], [r_fc[1][c]])
                yield
                dst = (z1 if layer == 0 else z2)[0:64, q_]
                act(S, dst, rr[0:64, cs_], AF.Sin, [r_fc[1][c]], [r_fc[0][c] if layer == 0 else r_X], scale=TWO_PI)
                yield

        for c in range(2):
            S.op("dve", lambda e, c=c: e.memset(rr[0:64, c * 256:c * 256 + 1], 0.0), [r_X], [r_fc[1][c], r_fc[0][c]])
        zip2([sin_chain(0, 0), sin_chain(0, 1)])
        zip2([sin_chain(1, 0), sin_chain(1, 1)])
        for c in range(2):
            S.op("dve", lambda e, c=c: e.memset(rr[0:64, c * 256:c * 256 + 1], 0.0), [r_fc[1][c], r_fc[0][c]], [r_X])
        for t_ in range(8):
            pb, pr = self.pbank()
            tok = slice(t_ * 128, (t_ + 1) * 128)
            mm(S, [(pb[:, 0:256], z2[0:64, tok], k_["fw3"][0:64, l, :], True, False),
                   (pb[:, 0:256], k_["ones_f"][0:1, :], k_["fb3"][0:1, l, :], False, True)], [r_X, r_k], [pr])
            tt(S, "dve", zh[:, t_, 256:512], pb[:, 0:256], wnd[:, t_, :], ALU.mult, [pr, r_X], [r_zh[t_]])
        slD1, srD1 = self.wload(self.parts_win(l, 2064, 512), key=("D1", l))
        slD2, srD2 = self.wload(self.parts_win(l, 2576, 256), key=("D2", l))
        sD1, sD2 = sv8(slD1), sv8(slD2)
        Fd, Gd = d["dftF"], d["dftG"]
        fv = lambda m: Fd[m].rearrange("(k p) n -> p k n", p=128)
        gv = lambda m: Gd[m].rearrange("(k p) n -> p k n", p=128)

        def parts_dft(vw, q):
            return [(lambda sl_: sv8(sl_)[:, :, 0:256], vw(0)[:, :, q * 256:(q + 1) * 256]),
                    (lambda sl_: sv8(sl_)[:, :, 256:512], vw(1)[:, :, q * 256:(q + 1) * 256])]

        def zip_run(gens):
            gens = list(gens)
            while gens:
                for g__ in list(gens):
                    try:
                        next(g__)
                    except StopIteration:
                        gens.remove(g__)

        for q in range(2):
            self.prefetch(("F", l, q), parts_dft(fv, q))
        CV = LP["CV"]
        cvb = k_["cvb"]
        r_raw = RL(3, "hraw")
        r_uct = RL(2, "uct")

        def conv_chain(ct, ui):
            s3, col, srr = ((sD1, ct * 128, srD1), (sD1, 256 + ct * 128, srD1), (sD2, ct * 128, srD2))[ui]
            rr_ = r_raw[ui]
            fx = [r_X] if ct == 0 else []
            for c in range(2):
                pb, pr = self.pbank()
                mm(S, self.proj_fm(c, s3, col, pb), [srr] + hall, [pr])
                yield
                act(S, raw[:, ui, CH(c)], pb[:], AF.Copy, [pr], [rr_] + fx)
                yield
            tile = ui * 2 + ct
            cw = lambda j: lp[:, l, CV + tile * 4 + j:CV + tile * 4 + j + 1]
            u = raw[:, ui, :]
            if ui == 0:
                dst, wr = x0[:, ct, :], [r_x0[ct]]
            else:
                dst, wr = uct[:, ui - 1, :], [r_uct[ui - 1]] + fx
            rd = [rr_, r_k]
            act(S, dst, u, AF.Identity, rd, wr, bias=cw(3), scale=cw(1))
            yield
            for (o, a_, sc) in ((dst[:, 1:T], u[:, 0:T - 1], cw(0)), (dst[:, 0:T - 1], u[:, 1:T], cw(2)),
                                (dst[:, 256:T:256], u[:, 255:T - 1:256], cvb[:, l, tile, 0:1]),
                                (dst[:, 255:T - 1:256], u[:, 256:T:256], cvb[:, l, tile, 1:2])):
                S.op("dve", lambda e, o=o, a_=a_, sc=sc: e.scalar_tensor_tensor(out=o, in0=a_, scalar=sc, in1=o,
                                                                              op0=ALU.mult, op1=ALU.add), rd + wr[:1], wr[:1])
                yield

        for ct in range(2):
            zip_run([conv_chain(ct, ui) for ui in range(3)])
            tt(S, "dve", zf[:, ct, :], uct[:, 0, :], uct[:, 1, :], ALU.mult, r_uct, [r_zf[ct]])
            act(S, zbf[:, ct, :], zf[:, ct, :], AF.Copy, [r_zf[ct]], [r_zbf[ct]])
        for q in range(2, 4):
            self.prefetch(("F", l, q), parts_dft(fv, q))
        ident_bf = k_["ident_bf"]
        for t_ in range(8):
            pb, pr = self.pbank()
            pv = pb[:].bitcast(BF16)
            for ct in range(2):
                S.op("pe", lambda e, ct=ct, pv=pv, t_=t_: e.transpose(pv[:, ct * 128:(ct + 1) * 128],
                                                                      zbf[:, ct, t_ * 128:(t_ + 1) * 128], ident_bf[:]),
                     [r_zbf[ct], r_k], [pr])
            cp(S, "dve", zh[:, t_, 0:256], pv[:, 0:256], [pr], [r_zh[t_]])
        ZHs = [ZH, zbf.rearrange("p a b -> p (a b)").bitcast(F32).rearrange("p (a b) -> p a b", a=2)]
        r_ZHs = [r_ZH, R("ZHb")]
        PAs, PQs = [pa, pa2], [pq, yt]
        r_PA, r_PQ = RL(2, "PA"), RL(2, "PQ")
        seen = set()

        def ft_chain(ft, fj, s3, sr):
            bi = ft % 2
            Zb, rz = ZHs[bi], r_ZHs[bi]
            PA, PQ, rpa, rpq = PAs[bi], PQs[bi], r_PA[bi], r_PQ[bi]
            fresh = bi not in seen
            seen.add(bi)
            fz = (r_zbf if (fresh and bi == 1) else [])
            fx = ([r_X] if fresh else [])
            pre_, prr = self.pbank()
            pim, pri = self.pbank()
            g = [(pre_[:], s3[:, t_, fj * 128:(fj + 1) * 128], zh[:, t_, :], t_ == 0, t_ == 7) for t_ in range(8)]
            g += [(pim[:], s3[:, t_, 256 + fj * 128:256 + (fj + 1) * 128], zh[:, t_, :], t_ == 0, t_ == 7) for t_ in range(8)]
            mm(S, g, [sr] + r_zh, [prr, pri])
            yield
            act(S, Zb[:, 0, :], pre_[:], AF.Copy, [prr], [rz] + fz)
            yield
            act(S, Zb[:, 1, :], pim[:], AF.Copy, [pri], [rz])
            yield
            Zr, Hr, Zi, Hi = Zb[:, 0, 0:256], Zb[:, 0, 256:512], Zb[:, 1, 0:256], Zb[:, 1, 256:512]
            tt(S, "dve", PA[:, 0, :], Zr, Hr, ALU.mult, [rz], [rpa] + fx)
            yield
            tt(S, "pool", PQ[:, 0, :], Zr, Hi, ALU.mult, [rz], [rpq] + fx)
            yield
            tt(S, "dve", PA[:, 1, :], Zi, Hi, ALU.mult, [rz], [rpa])
            yield
            tt(S, "pool", PQ[:, 1, :], Zi, Hr, ALU.mult, [rz], [rpq])
            yield
            tt(S, "dve", Y[:, ft, 0, :], PA[:, 0, :], PA[:, 1, :], ALU.subtract, [rpa], [r_Y[ft]])
            yield
            tt(S, "pool", Y[:, ft, 1, :], PQ[:, 0, :], PQ[:, 1, :], ALU.add, [rpq], [r_Y[ft]])
            yield

        for q in range(4):
            sl, sr = self.wload(parts_dft(fv, q), key=("F", l, q))
            s3 = sv8(sl)
            zip_run([ft_chain(q * 2 + fj, fj, s3, sr) for fj in range(2)])
        HB = LP["HB"]
        for q in range(2):
            self.prefetch(("G", l, q), parts_dft(gv, q))
        for q in range(4):
            sl, sr = self.wload(parts_dft(gv, q), key=("G", l, q))
            if q + 2 < 4:
                self.prefetch(("G", l, q + 2), parts_dft(gv, q + 2))
            s3 = sv8(sl)
            ns = slice(q * 256, (q + 1) * 256)
            for ct in range(2):
                pb, pr = self.pbank()
                g = []
                for ft in range(8):
                    g.append((pb[:, 0:256], Y[:, ft, 0, ct * 128:(ct + 1) * 128], s3[:, ft, 0:256], ft == 0, False))
                    g.append((pb[:, 0:256], Y[:, ft, 1, ct * 128:(ct + 1) * 128], s3[:, ft, 256:512], False, ft == 7))
                mm(S, g, [sr] + r_Y, [pr])
                ytb, ryt = (yt2[:, ct, :], r_yt[ct])
                act(S, ytb, pb[:, 0:256], AF.Copy, [pr], [ryt] + ([r_PA[1], r_ZHs[0]] if q == 0 else []))
                S.op("dve", lambda e, ytb=ytb, ct=ct: e.scalar_tensor_tensor(out=ytb, in0=zf[:, ct, ns],
                                                                           scalar=lp[:, l, HB + ct:HB + ct + 1], in1=ytb,
                                                                           op0=ALU.mult, op1=ALU.add),
                     [r_zf[ct], ryt, r_k], [ryt])
                tt(S, "dve", ymix[:, 6 + ct, ns], x0[:, ct, ns], ytb, ALU.mult, [r_x0[ct], ryt], [ymr[6 + ct][q // 2]])


def fm(v):
    return np.ascontiguousarray(np.asarray(v, np.float32).reshape(8, 128).T)


def _consts(kind):
    c = {}
    p = np.arange(128)
    t = np.arange(T)
    ident = np.eye(128, dtype=np.float32)
    c["ident"] = ident
    bo = np.zeros((128, 128), np.float32)
    bo[:64, :64] = 1
    bo[64:, 64:] = 1
    c["blockones"] = bo
    r_, t_ = np.meshgrid(p, p, indexing="ij")
    tri = np.zeros((128, 4, 128), np.float32)
    tri[:, 0, :] = (r_ <= t_)
    tri[:, 1, :] = (r_ >= t_)
    tri[:, 2, :] = np.where(r_ <= t_, 0.0, NEG)
    tri[:, 3, :] = np.where(r_ >= t_, 0.0, NEG)
    c["tri"] = tri
    P = np.zeros((128, 128), np.float32)
    for b in range(0, 128, 32):
        for i in range(16):
            P[b + i + 16, b + i] = -1.0
            P[b + i, b + i + 16] = 1.0
    c["ropeP"] = P
    cs = np.zeros((128, 2, T), np.float32)
    if kind == "s":
        dd = p % 64
        inv = (10000.0 ** (-(dd % 16).astype(np.float32) / np.float32(16))).astype(np.float32)
        row = (t // 64).astype(np.float32)
        col = (t % 64).astype(np.float32)
        pos = np.where((dd // 32)[:, None] == 0, row[None, :], col[None, :]).astype(np.float32)
        ang = (pos * inv[:, None]).astype(np.float32)
        cs[:, 0, :] = np.cos(ang)
        cs[:, 1, :] = np.sin(ang)
    else:
        cs[:, 0, :] = 1.0
    c["ropeCS"] = cs
    sm = np.full((128, 8, 384), NEG, np.float32)
    for j in range(8):
        m = j * 128 + p[:, None]
        q = (j - 1) * 128 + np.arange(384)[None, :]
        inr = (q >= 0) & (q < T)
        if kind == "s":
            ok = (np.abs(m - q) <= 128) & inr
        else:
            ok = ((m // 256) == (q // 256)) & inr
        sm[:, j, :] = np.where(ok, 0.0, NEG)
    c["swamask"] = sm
    gm = np.zeros((5, 2, T), np.float32)
    if kind == "p":
        gm[0, 0, :] = 1.0
        gm[0, 1, :] = -BIG
        for s_ in range(4):
            gm[1 + s_, 0, :] = (t // 256 == s_)
            gm[1 + s_, 1, :] = BIG * (t // 256 == s_)
    c["gmAB"] = gm
    c["cflags"] = np.tile(np.array([[0.0, 1.0, 0.0, 0.0]] if kind == "s" else [[NEG, 0.0, 1.0, 0.0]], np.float32), (128, 1))
    sel = np.zeros((4, 130), np.float32)
    for h in range(4):
        sel[h, 0:128] = ((p >= 64).astype(int) == (h % 2))
        sel[h, 128 + h // 2] = 1.0
    c["sel"] = sel
    L = T if kind == "s" else 256
    rep = T // L
    pos = np.arange(L, dtype=np.float32)
    t01 = pos / np.float32(max(L - 1, 1))
    lin = np.linspace(1e-4, 15.0, 16, dtype=np.float32)
    ang = (np.float32(2.0 * math.pi / L) * pos[:, None] * lin[None, :]).astype(np.float32)
    feats = np.concatenate([t01[:, None], np.cos(ang), -np.sin(ang)], -1).astype(np.float32)
    c["featsT"] = np.ascontiguousarray(np.tile(feats, (rep, 1)).T)
    centre = L // 2
    dist = np.abs(pos - centre) / np.float32(max(centre, 1))
    deltas = np.abs(np.linspace(math.log(0.01) / 1.5, math.log(0.01) / 0.3, 256, dtype=np.float32))
    wnd = np.exp(-dist[:, None] * deltas[None, :]).astype(np.float32)
    c["window"] = np.ascontiguousarray(np.tile(wnd, (rep, 1)).reshape(8, 128, 256).transpose(1, 0, 2))
    N = 2 * L
    tt_ = np.arange(L, dtype=np.float64)
    ff = np.arange(L, dtype=np.float64)
    th = math.pi * (2 * ff + 1) / N
    Fc = np.cos(tt_[:, None] * th[None, :])
    Fs = -np.sin(tt_[:, None] * th[None, :])
    Gc = (2.0 / N) * np.cos(th[:, None] * (tt_[None, :] + L // 2))
    Gs = -(2.0 / N) * np.sin(th[:, None] * (tt_[None, :] + L // 2))
    dF = np.zeros((2, T, T), np.float32)
    dG = np.zeros((2, T, T), np.float32)
    for s_ in range(rep):
        sl = slice(s_ * L, (s_ + 1) * L)
        dF[0, sl, sl] = Fc
        dF[1, sl, sl] = Fs
        dG[0, sl, sl] = Gc
        dG[1, sl, sl] = Gs
    c["dftF"] = dF
    c["dftG"] = dG
    return c


def host_inputs(inp, cores=None):
    f = lambda a: np.ascontiguousarray(np.asarray(a, dtype=np.float32))
    A = {k: np.asarray(v) for k, v in inp.items()}
    shared = {}
    for nm in ("w_ada", "w1_gate", "w1_up", "w1_down", "w_in", "w_out", "w2_gate", "w2_up", "w2_down"):
        shared[nm] = f(A[nm])
    shared["b_adaT"] = f(A["b_ada"].reshape(NL, 72, 128).transpose(0, 2, 1))
    gv = []
    for l in range(NL):
        gv += [fm(A["g_ff1"][l]), fm(A["g_mix"][l]), fm(A["g_ff2"][l])]
    gv.append(fm(A["g_final"]))
    shared["gvec"] = f(np.concatenate(gv, axis=1))
    lp = np.zeros((128, NL, 304), np.float32)
    p = np.arange(128)
    for l in range(NL):
        lp[:, l, 0:16] = A["b_gates"][l][None, :]
        lp[:, l, 16:272] = A["g_mlstm"][l][None, :]
        lp[:, l, 272] = A["g_qnorm"][l][p % 64]
        lp[:, l, 273] = A["g_knorm"][l][p % 64]
        lp[:, l, 274:278] = A["sinks"][l][None, :]
        for i in range(6):
            ch = i * 128 + p
            lp[:, l, 278 + i * 4 + 0] = A["conv_w"][l][0, ch]
            lp[:, l, 278 + i * 4 + 1] = A["conv_w"][l][1, ch]
            lp[:, l, 278 + i * 4 + 2] = A["conv_w"][l][2, ch]
            lp[:, l, 278 + i * 4 + 3] = A["conv_b"][l][ch]
        for ct in range(2):
            lp[:, l, 302 + ct] = A["hyena_bias"][l][ct * 128 + p]
    shared["lp"] = lp
    shared["fw1"] = f(A["filt_w1"].transpose(1, 0, 2))
    shared["fw2"] = f(A["filt_w2"].transpose(1, 0, 2))
    shared["fw3"] = f(A["filt_w3"].transpose(1, 0, 2))
    shared["fvec"] = f(np.stack([A["filt_b1"], A["filt_b2"], A["filt_freq"]], -1).transpose(1, 0, 2))
    shared["fb3"] = f(A["filt_b3"][None, :, :])
    cst = {"s": _consts("s"), "p": _consts("p")}
    maps = []
    xs, xp = A["x_sample"], A["x_prompt"]
    for core in (range(8) if cores is None else cores):
        m = dict(shared)
        kind = "s" if core < 4 else "p"
        m.update(cst[kind])
        C0 = np.zeros((NL, 2, 128, 2, 65), np.float32)
        m0rep = np.zeros((128, NL, 2, 2), np.float32)
        m0c = np.zeros((4, NL * 2), np.float32)
        kT = {n: np.zeros((NL, 2, 128, 256), np.float32) for n in ("gkT", "skT")}
        vp = {n: np.zeros((NL, 2, 128, 2, 192), np.float32) for n in ("gvp", "svp")}
        if core < 4:
            b = core
            m["xT"] = f(xs[b].T)
            m["cvec"] = fm(A["c"][b])
            sC, sn, smm = A["state_mlstm_C"][b], A["state_mlstm_n"][b], A["state_mlstm_m"][b]
            for g_ in range(2):
                for pr_ in range(2):
                    h = 2 * pr_ + g_
                    C0[:, :, g_ * 64:(g_ + 1) * 64, pr_, 0:64] = sC[:, :, h]
                    C0[:, :, g_ * 64:(g_ + 1) * 64, pr_, 64] = sn[:, :, h]
                    m0rep[g_ * 64:(g_ + 1) * 64, :, :, pr_] = smm[None, :, :, h]
            for l in range(NL):
                for dr in range(2):
                    m0c[:, l * 2 + dr] = smm[l, dr, :]
            for (kn, vn, ck_, cv_) in (("gkT", "gvp", "cache_gattn_k", "cache_gattn_v"), ("skT", "svp", "cache_swa_k", "cache_swa_v")):
                ck, cv = A[ck_][b], A[cv_][b]
                t1 = ck.transpose(0, 2, 3, 1).reshape(NL, 128, 256)
                t2 = ck[:, :, ::-1, :].transpose(0, 2, 3, 1).reshape(NL, 128, 256)
                kT[kn][:, 0] = t1
                kT[kn][:, 1] = t2
                vp[vn][:, :, :, :, 64:128] = cv.reshape(NL, 2, 128, 2, 64)
        else:
            j = core - 4
            m["xT"] = f(xp[4 * j:4 * j + 4].reshape(T, D).T)
            m["cvec"] = fm(A["c_ctx"])
        m["C0"], m["m0rep"], m["m0c"] = C0, m0rep, m0c
        m.update(kT)
        m.update(vp)
        maps.append(m)
    return maps


_PROG = None


def get_prog():
    global _PROG
    if _PROG is None:
        _PROG = Prog()
    return _PROG


def run_device(inputs, trace=False, cores=None):
    prog = get_prog()
    maps = host_inputs(inputs, cores)
    maps = [{k: np.ascontiguousarray(v, dtype=np.float32) for k, v in m.items() if k in prog.din} for m in maps]
    for m in maps:
        for k, shp in prog.din.items():
            assert m[k].shape == shp, (k, m[k].shape, shp)
    res = run_bass_kernel_spmd(prog.nc, maps, core_ids=list(range(len(maps))), trace=trace)
    return res


def assemble(res):
    r = res.results
    yp = np.zeros((16, 256, D), np.float32)
    ys = np.zeros((4, T, D), np.float32)
    nC = np.zeros((16, NL, 2, 4, 64, 64), np.float32)
    nn = np.zeros((16, NL, 2, 4, 64), np.float32)
    nm = np.zeros((16, NL, 2, 4), np.float32)
    kv = {n: np.zeros((16, NL, 256, 2, 64), np.float32) for n in ("o_gk", "o_gv", "o_sk", "o_sv")}
    for core in range(8):
        y = np.ascontiguousarray(r[core]["yT"].T)
        if core < 4:
            ys[core] = y
            continue
        j = core - 4
        yp[4 * j:4 * j + 4] = y.reshape(4, 256, D)
        for n in ("o_gk", "o_sk"):
            a = r[core][n].reshape(NL, 2, 64, 4, 256)
            kv[n][4 * j:4 * j + 4] = a.transpose(3, 0, 4, 1, 2)
        for n in ("o_gv", "o_sv"):
            a = r[core][n].reshape(NL, 4, 256, 2, 64)
            kv[n][4 * j:4 * j + 4] = a.transpose(1, 0, 2, 3, 4)
        oc = r[core]["o_C"].reshape(NL, 2, 4, 2, 64, 2, 65)
        oc = oc.transpose(2, 0, 1, 5, 3, 4, 6).reshape(4, NL, 2, 4, 64, 65)
        nC[4 * j:4 * j + 4] = oc[..., 0:64]
        nn[4 * j:4 * j + 4] = oc[..., 64]
        nm[4 * j:4 * j + 4] = r[core]["o_m"].transpose(3, 0, 1, 2)
    return (yp, ys, nC, nn, nm, kv["o_gk"], kv["o_gv"], kv["o_sk"], kv["o_sv"])


def kernel(**inputs):
    res = run_device(inputs)
    return assemble(res)
```

```python
import os
import math
import numpy as np
import concourse.bass as bass
import concourse.mybir as mybir
from concourse.bass_utils import run_bass_kernel_spmd

F32 = mybir.dt.float32
BF16 = mybir.dt.bfloat16
I32 = mybir.dt.int32
AF = mybir.ActivationFunctionType
ALU = mybir.AluOpType
AX = mybir.AxisListType

D = 1024
T = 1024
DFF = 2816
NFF = 22
NL = 2
NIN = 2832
EPS = 1e-6
NEG = -30000.0
BIG = 29952.0
LN8 = math.log(0.125)
SLOT = 8 * 640
STAGE = int(os.environ.get("MK_STAGE", "99"))
DEBUG = bool(os.environ.get("MK_DEBUG"))
CUT = int(os.environ.get("MK_CUT", "99"))


class R:
    __slots__ = ("name", "w", "rd", "excl")

    def __init__(self, name="", excl=False):
        self.name = name
        self.w = None
        self.rd = []
        self.excl = excl


def RL(n, name=""):
    return [R("%s%d" % (name, i)) for i in range(n)]


class Sched:
    NLANES = {"sp": 8, "pool": 3, "poolw": 5}

    def __init__(self, nc):
        self.nc = nc
        self.eng = {"pe": nc.tensor, "act": nc.scalar, "dve": nc.vector,
                    "pool": nc.gpsimd, "sp": nc.sync}
        self.sem = {}
        self.cnt = {}
        for k in self.eng:
            self.sem[k] = nc.alloc_semaphore(name="s_" + k)
            self.cnt[k] = 0
        self.lanes = {}
        for q, n in self.NLANES.items():
            self.lanes[q] = []
            for i in range(n):
                key = "d_%s%d" % (q, i)
                self.sem[key] = nc.alloc_semaphore(name=key)
                self.cnt[key] = 0
                self.lanes[q].append(key)
        self.lane_rr = {q: 0 for q in self.NLANES}
        self.eng_of = {"sp": "sp", "pool": "pool", "poolw": "pool"}
        self.waited = {k: {} for k in self.eng}
        self.nwaits = 0
        self.nops = 0

    def _wait(self, e, key, val):
        if key == "pe" and e == "pe":
            return
        w = self.waited[e]
        if w.get(key, 0) >= val:
            return
        self.eng[e].wait_ge(self.sem[key], val)
        w[key] = val
        self.nwaits += 1

    def _deps(self, e, reads, writes):
        deps = {}
        for r in reads:
            if r.w is not None:
                k, v = r.w
                if deps.get(k, 0) < v:
                    deps[k] = v
            if r.excl:
                for (k, v) in r.rd:
                    if k != e and deps.get(k, 0) < v:
                        deps[k] = v
        for w in writes:
            if w.w is not None:
                k, v = w.w
                if deps.get(k, 0) < v:
                    deps[k] = v
            for (k, v) in w.rd:
                if deps.get(k, 0) < v:
                    deps[k] = v
        for k, v in deps.items():
            self._wait(e, k, v)

    def _commit(self, tok, reads, writes):
        for r in reads:
            r.rd.append(tok)
            if len(r.rd) > 48:
                mx = {}
                for k, v in r.rd:
                    if mx.get(k, 0) < v:
                        mx[k] = v
                r.rd = list(mx.items())
        for w in writes:
            w.w = tok
            w.rd = []

    def op(self, e, fn, reads=(), writes=()):
        self._deps(e, reads, writes)
        inst = fn(self.eng[e])
        self.cnt[e] += 1
        inst.then_inc(self.sem[e], 1)
        self._commit((e, self.cnt[e]), reads, writes)
        self.nops += 1

    def dma(self, q, out, in_, reads=(), writes=()):
        lanes = self.lanes[q]
        e = self.eng_of[q]
        key = lanes[self.lane_rr[q] % len(lanes)]
        self.lane_rr[q] += 1
        self._wait(e, key, self.cnt[key])
        self._deps(e, reads, writes)
        inst = self.eng[e].dma_start(out=out, in_=in_)
        self.cnt[key] += 16
        inst.then_inc(self.sem[key], 16)
        self._commit((key, self.cnt[key]), reads, writes)
        self.nops += 1

    def barrier(self):
        for e in self.eng:
            for k in self.sem:
                if self.cnt[k] > 0 and not k.startswith("d_poolw"):
                    self._wait(e, k, self.cnt[k])

    def finish(self):
        for k in self.sem:
            if self.cnt[k] > 0 and k != "sp":
                self._wait("sp", k, self.cnt[k])


def act(S, out, in_, func, reads, writes, bias=0.0, scale=1.0):
    S.op("act", lambda e: e.activation(out=out, in_=in_, func=func, bias=bias, scale=scale), reads, writes)


def tt(S, eng, out, a, b, op, reads, writes):
    S.op(eng, lambda e: e.tensor_tensor(out=out, in0=a, in1=b, op=op), reads, writes)


def ts(S, eng, out, a, s1, op0, reads, writes, s2=None, op1=None):
    if op1 is None:
        S.op(eng, lambda e: e.tensor_scalar(out=out, in0=a, scalar1=s1, scalar2=None, op0=op0), reads, writes)
    else:
        S.op(eng, lambda e: e.tensor_scalar(out=out, in0=a, scalar1=s1, scalar2=s2, op0=op0, op1=op1), reads, writes)


def cp(S, eng, out, in_, reads, writes):
    S.op(eng, lambda e: e.tensor_copy(out=out, in_=in_), reads, writes)


def recip(S, out, in_, reads, writes):
    S.op("dve", lambda e: e.reciprocal(out=out, in_=in_), reads, writes)


def mm(S, groups, reads, writes):
    def fn(e):
        inst = None
        for (o, l, r, st, sp_) in groups:
            inst = e.matmul(o, lhsT=l, rhs=r, start=st, stop=sp_)
        return inst
    S.op("pe", fn, reads, writes)


class Prog:
    def __init__(self):
        nc = bass.Bass("TRN2", target_bir_lowering=False)
        self.nc = nc
        self.S = Sched(nc)
        self.din = {}
        self.dout = {}
        self._build()

    def inp(self, name, shape):
        t = self.nc.dram_tensor(name, list(shape), F32, kind="ExternalInput").ap()
        self.din[name] = tuple(shape)
        return t

    def outp(self, name, shape):
        t = self.nc.dram_tensor(name, list(shape), F32, kind="ExternalOutput").ap()
        self.dout[name] = tuple(shape)
        return t

    def sb(self, name, shape, dt=F32):
        return self.nc.alloc_sbuf_tensor("sb_" + name, list(shape), dt)

    def arena_reset(self, to=0):
        self.aoff = to
        self.S.barrier()

    def carve(self, shape, dt=F32):
        n = 1
        for s in shape[1:]:
            n *= s
        nb = n * (4 if dt in (F32, I32) else 2)
        nb = (nb + 31) // 32 * 32
        off = self.aoff
        self.aoff += nb
        assert self.aoff <= self.ARENA_BYTES, (self.aoff, self.ARENA_BYTES)
        v = self.arena[:, off // 2:(off + nb) // 2]
        if dt != BF16:
            v = v.bitcast(dt)
        v = v[:, 0:n]
        if len(shape) == 3:
            v = v.rearrange("p (a b) -> p a b", a=shape[1])
        elif len(shape) == 4:
            v = v.rearrange("p (a b c) -> p a b c", a=shape[1], b=shape[2])
        return v

    def bank(self):
        return self.pbank()

    def prefetch(self, key, parts):
        self.pre[key] = self.wload(parts)

    def wload(self, parts, key=None):
        if key is not None and key in self.pre:
            return self.pre.pop(key)
        i = self.slot_rr % len(self.slots)
        self.slot_rr += 1
        sl, r = self.slots[i], self.slotr[i]
        for (dst_fn, src) in parts:
            self.S.dma("poolw", dst_fn(sl), src, writes=[r])
        return sl, r

    def slot_view(self, sl, kk):
        return sl[:, 0:kk * self.slot_w].rearrange("p (k n) -> p k n", k=kk)

    def _build(self):
        nc, S = self.nc, self.S
        inp, outp, sb = self.inp, self.outp, self.sb
        xT_d = inp("xT", [D, T])
        cvec_d = inp("cvec", [128, 8])
        w_ada = inp("w_ada", [NL, D, 9 * D])
        b_adaT = inp("b_adaT", [NL, 128, 72])
        gvec_d = inp("gvec", [128, NL * 24 + 8])
        wd = {}
        for nm, shp in (("w1_gate", [NL, D, DFF]), ("w1_up", [NL, D, DFF]), ("w1_down", [NL, DFF, D]),
                        ("w_in", [NL, D, NIN]), ("w_out", [NL, D, D]),
                        ("w2_gate", [NL, D, DFF]), ("w2_up", [NL, D, DFF]), ("w2_down", [NL, DFF, D])):
            wd[nm] = inp(nm, shp)
        yT_d = outp("yT", [D, T])

        xT = sb("xT", [128, 8, T], F32)
        hT = sb("hT", [128, 8, T], BF16)
        self.ARENA_BYTES = 84 * 1024
        self.arena = sb("arena", [128, self.ARENA_BYTES // 2], BF16)
        self.slot_w = 640
        self.slots = [sb("slot%d" % i, [128, SLOT], BF16) for i in range(4)]
        self.slotr = RL(4, "slot")
        self.slot_rr = 0
        self.pre = {}
        self.ps = [nc.alloc_psum_tensor("ps%d" % i, [128, 512], F32) for i in range(8)]
        self.psr = [R("ps%d" % i, excl=True) for i in range(8)]
        self.bank_rr = 0
        self.bank_pool = list(range(8))
        ones_bf = sb("ones_bf", [128, 128], BF16)
        cvec = sb("cvec_sb", [128, 8], F32)
        sc_bf = sb("sc_bf", [128, 8], BF16)
        modT = sb("modT", [128, NL, 72], F32)
        badaT = sb("badaT", [128, NL, 72], F32)
        gvec = sb("gvec_sb", [128, NL * 24 + 8], F32)
        Acoef = sb("Acoef", [128, NL, 3, 8], F32)
        Gcoef = sb("Gcoef", [128, NL, 3, 8], F32)
        sqb = sb("sqb", [128, 2, 512], BF16)
        f32s = sb("f32s", [128, 3, 512], F32)
        rstd = sb("rstd", [128, 512], F32)
        r_ones, r_cvec, r_sc, r_gvec, r_rstd = R("ones"), R("cvec"), R("sc"), R("gvec"), R("rstd")
        r_mod = RL(NL, "mod")
        r_bada = R("bada")
        r_coef = RL(NL, "coef")
        r_sq = RL(2, "sq")
        r_f32s = RL(3, "f32s")
        xr = [[R("x%d_%d" % (k, c)) for c in range(2)] for k in range(8)]
        hr = [[R("h%d_%d" % (k, c)) for c in range(2)] for k in range(8)]
        self.sq_rr = 0
        self.f32_rr = 0

        def CH(c):
            return slice(c * 512, (c + 1) * 512)

        xv = xT_d.rearrange("(k p) t -> p k t", p=128)
        for k in range(8):
            S.dma("sp", xT[:, k, :], xv[:, k, :], writes=[xr[k][0], xr[k][1]])
        S.dma("sp", cvec[:], cvec_d, writes=[r_cvec])
        S.dma("sp", gvec[:], gvec_d, writes=[r_gvec])
        for l in range(NL):
            S.dma("sp", badaT[:, l, :], b_adaT[l], writes=[r_bada])
        S.op("dve", lambda e: e.memset(ones_bf[:], 1.0), writes=[r_ones])
        act(S, sc_bf[:], cvec[:], AF.Silu, [r_cvec], [r_sc])

        mslots = [sb("mslot%d" % i, [128, 8, 256], BF16) for i in range(2)]
        r_ms = RL(2, "mslot")
        r_modls = [[R("mod%d_%d" % (l, s_)) for s_ in range(3)] for l in range(NL)]
        jobs = [(l, q) for l in range(NL) for q in range(36)]
        st = {"dma": 0, "mm": 0, "fin": set()}

        def mod_dma(j):
            l, q = jobs[j]
            wv = w_ada[l].rearrange("(k p) n -> p k n", p=128)
            S.dma("poolw", mslots[j % 2][:], wv[:, :, q * 256:(q + 1) * 256], writes=[r_ms[j % 2]])

        def mod_mm(j, bank=None):
            l, q = jobs[j]
            if bank is None:
                pb, pr = self.bank()
                c0 = 0
            else:
                pb, pr, c0 = bank
            groups = []
            for jj in range(2):
                for k in range(8):
                    groups.append((pb[:, c0 + jj:c0 + jj + 1], mslots[j % 2][:, k, jj * 128:(jj + 1) * 128], sc_bf[:, k:k + 1],
                                   k == 0, k == 7))
            mm(S, groups, [r_ms[j % 2], r_sc], [pr])
            s_ = q // 12
            tt(S, "dve", modT[:, l, q * 2:q * 2 + 2], pb[:, c0:c0 + 2], badaT[:, l, q * 2:q * 2 + 2], ALU.add,
               [pr, r_bada], [r_modls[l][s_]])

        def mod_pump(n=1, bank=None):
            for _ in range(n):
                if st["dma"] < len(jobs) and st["dma"] - st["mm"] < 2:
                    mod_dma(st["dma"])
                    st["dma"] += 1
                if st["mm"] < st["dma"] - 1 or (st["dma"] == len(jobs) and st["mm"] < st["dma"]):
                    mod_mm(st["mm"], bank)
                    st["mm"] += 1

        def mod_require(l, s_):
            last = l * 36 + s_ * 12 + 11
            while st["mm"] <= last:
                mod_pump()
            if (l, s_) in st["fin"]:
                return
            st["fin"].add((l, s_))
            r = r_modls[l][s_]
            ts(S, "dve", Acoef[:, l, s_, :], modT[:, l, (3 * s_ + 1) * 8:(3 * s_ + 2) * 8], 1.0, ALU.add, [r], [r])
            tt(S, "dve", Acoef[:, l, s_, :], Acoef[:, l, s_, :], gvec[:, l * 24 + s_ * 8:l * 24 + s_ * 8 + 8],
               ALU.mult, [r, r_gvec], [r])
            ts(S, "dve", Gcoef[:, l, s_, :], modT[:, l, (3 * s_ + 2) * 8:(3 * s_ + 3) * 8],
               0.5 if s_ != 1 else 1.0, ALU.mult, [r], [r])

        self.mod_pump = mod_pump
        self.mod_require = mod_require

        def rms_rstd(c, src_fn, src_regs, nk, inv_n, ones_l):
            pb, pr = self.bank()
            for k in range(nk):
                i = self.sq_rr % 2
                self.sq_rr += 1
                act(S, sqb[:, i, :], src_fn(k), AF.Square, [src_regs[k]], [r_sq[i]])
                mm(S, [(pb[:], ones_l, sqb[:, i, :], k == 0, k == nk - 1)], [r_sq[i], r_ones], [pr])
            act(S, rstd[:], pb[:], AF.Ln, [pr], [r_rstd], bias=EPS, scale=inv_n)
            act(S, rstd[:], rstd[:], AF.Exp, [r_rstd], [r_rstd], scale=-0.5)

        def norm_mod(l, s):
            for c in range(2):
                rms_rstd(c, lambda k: xT[:, k, CH(c)], [xr[k][c] for k in range(8)], 8, 1.0 / D, ones_bf[:])
                for k in range(8):
                    i = self.f32_rr % 3
                    self.f32_rr += 1
                    tt(S, "dve", f32s[:, i, :], xT[:, k, CH(c)], rstd[:], ALU.mult,
                       [xr[k][c], r_rstd], [r_f32s[i]])
                    act(S, hT[:, k, CH(c)], f32s[:, i, :], AF.Identity, [r_f32s[i], r_modls[l][s]],
                        [hr[k][c]], bias=modT[:, l, 3 * s * 8 + k:3 * s * 8 + k + 1],
                        scale=Acoef[:, l, s, k:k + 1])

        def resid_add(l, s, dt_, c, pb, pr):
            i = self.f32_rr % 3
            self.f32_rr += 1
            act(S, f32s[:, i, :], pb[:], AF.Copy, [pr, r_modls[l][s]], [r_f32s[i]], scale=Gcoef[:, l, s, dt_:dt_ + 1])
            tt(S, "dve", xT[:, dt_, CH(c)], xT[:, dt_, CH(c)], f32s[:, i, :], ALU.add,
               [xr[dt_][c], r_f32s[i]], [xr[dt_][c]])

        def ffn(l, s, wg, wu, wdn):
            self.arena_reset()
            aT = self.carve([128, NFF, T], BF16)
            ar = [[R("a%d_%d" % (j, c)) for c in range(2)] for j in range(NFF)]
            sg = [self.carve([128, 512], F32) for _ in range(3)]
            r_sg = RL(3, "sg")
            sg_rr = 0
            mod_require(l, s)
            norm_mod(l, s)
            wgv = wg[l].rearrange("(k p) n -> p k n", p=128)
            wuv = wu[l].rearrange("(k p) n -> p k n", p=128)
            for g in range(NFF // 2):
                c0 = g * 256
                sl, sr = self.wload(self.parts_gu(wg, wu, l, g), key=("gu", l, s, g))
                s3 = self.slot_view(sl, 8)
                if l == 0:
                    mod_pump(1)
                for jj in range(2):
                    j = g * 2 + jj
                    for c in range(2):
                        pg, prg = self.bank()
                        pu, pru = self.bank()
                        groups = []
                        for k in range(8):
                            groups.append((pg[:], s3[:, k, jj * 128:(jj + 1) * 128], hT[:, k, CH(c)], k == 0, k == 7))
                        for k in range(8):
                            groups.append((pu[:], s3[:, k, 256 + jj * 128:256 + (jj + 1) * 128], hT[:, k, CH(c)],
                                           k == 0, k == 7))
                        mm(S, groups, [sr] + [hr[k][c] for k in range(8)], [prg, pru])
                        i = sg_rr % 3
                        sg_rr += 1
                        act(S, sg[i], pg[:], AF.Silu, [prg], [r_sg[i]])
                        tt(S, "dve", aT[:, j, CH(c)], sg[i], pu[:], ALU.mult, [r_sg[i], pru], [ar[j][c]])
            wdv = wdn[l].rearrange("(j p) n -> p j n", p=128)
            for dt_ in range(8):
                sl, sr = self.wload([(lambda sl_: sl_[:, 0:NFF * 128].rearrange("p (j n) -> p j n", j=NFF),
                                      wdv[:, :, dt_ * 128:(dt_ + 1) * 128])])
                if dt_ == 7:
                    if s == 0:
                        self.prefetch(("A1", l), self.parts_win(l, 0, 512))
                        self.prefetch(("A2", l), self.parts_win(l, 512, 512))
                        self.prefetch(("G", l), self.parts_win(l, 1024, 16))
                    elif l + 1 < NL:
                        for g_ in range(2):
                            self.prefetch(("gu", l + 1, 0, g_), self.parts_gu(wd["w1_gate"], wd["w1_up"], l + 1, g_))
                s3 = sl[:, 0:NFF * 128].rearrange("p (j n) -> p j n", j=NFF)
                for c in range(2):
                    pb, pr = self.bank()
                    groups = [(pb[:], s3[:, j, :], aT[:, j, CH(c)], j == 0, j == NFF - 1) for j in range(NFF)]
                    mm(S, groups, [sr] + [ar[j][c] for j in range(NFF)], [pr])
                    resid_add(l, s, dt_, c, pb, pr)

        self.wd = wd
        self.ctx = dict(xT=xT, hT=hT, xr=xr, hr=hr, CH=CH, rms_rstd=rms_rstd, norm_mod=norm_mod,
                        resid_add=resid_add, ones_bf=ones_bf, r_ones=r_ones, gvec=gvec, r_gvec=r_gvec,
                        f32s=f32s, r_f32s=r_f32s, rstd=rstd, r_rstd=r_rstd, modT=modT, sqb=sqb, r_sq=r_sq)


        self.ARENA_BYTES = 84 * 1024
        LPW = 304
        self.LP = dict(BG=0, GM=16, GQK=272, SK=274, CV=278, HB=302)
        d = {}
        d["lp"] = inp("lp", [128, NL, LPW])
        d["cflags"] = inp("cflags", [128, 4])
        d["ident"] = inp("ident", [128, 128])
        d["blockones"] = inp("blockones", [128, 128])
        d["tri"] = inp("tri", [128, 4, 128])
        d["ropeP"] = inp("ropeP", [128, 128])
        d["ropeCS"] = inp("ropeCS", [128, 2, T])
        d["swamask"] = inp("swamask", [128, 8, 384])
        d["gmAB"] = inp("gmAB", [5, 2, T])
        d["fw1"] = inp("fw1", [33, NL, 64])
        d["fw2"] = inp("fw2", [64, NL, 64])
        d["fw3"] = inp("fw3", [64, NL, 256])
        d["fvec"] = inp("fvec", [64, NL, 3])
        d["fb3"] = inp("fb3", [1, NL, 256])
        d["featsT"] = inp("featsT", [33, T])
        d["window"] = inp("window", [128, 8, 256])
        d["dftF"] = inp("dftF", [2, T, T])
        d["dftG"] = inp("dftG", [2, T, T])
        d["sel"] = inp("sel", [4, 130])
        d["C0"] = inp("C0", [NL, 2, 128, 2, 65])
        d["m0rep"] = inp("m0rep", [128, NL, 2, 2])
        d["m0c"] = inp("m0c", [4, NL * 2])
        d["gkT"] = inp("gkT", [NL, 2, 128, 256])
        d["gvp"] = inp("gvp", [NL, 2, 128, 2, 192])
        d["skT"] = inp("skT", [NL, 2, 128, 256])
        d["svp"] = inp("svp", [NL, 2, 128, 2, 192])
        d["o_gk"] = outp("o_gk", [NL, 128, T])
        d["o_gv"] = outp("o_gv", [NL, T, 128])
        d["o_sk"] = outp("o_sk", [NL, 128, T])
        d["o_sv"] = outp("o_sv", [NL, T, 128])
        d["o_C"] = outp("o_C", [NL, 2, 4, 128, 2, 65])
        d["o_m"] = outp("o_m", [NL, 2, 4, 4])
        self.d = d
        if DEBUG:
            self.dbg_ymix = outp("dbg_ymix", [NL, 128, 8, T])
        k = {}
        k["lp"] = sb("lp", [128, NL, LPW]); k["cflags"] = sb("cflags", [128, 4])
        k["ident_f"] = sb("ident_f", [128, 128]); k["ident_bf"] = sb("ident_bf", [128, 128], BF16)
        k["blockones"] = sb("blockones", [128, 128], BF16)
        k["tri"] = sb("tri", [128, 4, 128]); k["ropeP"] = sb("ropeP", [128, 128])
        k["ones_f"] = sb("ones_f", [128, 128])
        k["fw1"] = sb("fw1", [33, NL, 64]); k["fw2"] = sb("fw2", [64, NL, 64]); k["fw3"] = sb("fw3", [64, NL, 256])
        k["fvec"] = sb("fvec", [64, NL, 3]); k["fb3"] = sb("fb3", [1, NL, 256]); k["fs"] = sb("fs", [64, NL, 4])
        k["sel"] = sb("sel", [4, 130]); k["m0rep"] = sb("m0rep", [128, NL, 2, 2]); k["m0c"] = sb("m0c", [4, NL * 2])
        k["Clo"] = sb("Clo", [128, 2, 2, 65]); k["Chi"] = sb("Chi", [128, 2, 2, 65])
        k["Cblo"] = sb("Cblo", [128, 2, 2, 66], BF16); k["Cbhi"] = sb("Cbhi", [128, 2, 2, 66], BF16)
        k["cvb"] = sb("cvb", [128, NL, 6, 2])
        self.k = k
        r_k = R("consts")
        self.r_k = r_k
        for nm in ("lp", "cflags", "tri", "ropeP", "fw1", "fw2", "fw3", "fvec", "fb3", "sel", "m0rep", "m0c"):
            S.dma("sp", k[nm][:], d[nm], writes=[r_k])
        S.dma("sp", k["ident_f"][:], d["ident"], writes=[r_k])
        S.dma("pool", k["ident_bf"][:], d["ident"], writes=[r_k])
        S.dma("pool", k["blockones"][:], d["blockones"], writes=[r_k])
        S.op("pool", lambda e: e.memset(k["ones_f"][:], 1.0), writes=[r_k])
        for nm in ("Clo", "Chi", "Cblo", "Cbhi"):
            S.op("pool", lambda e, nm=nm: e.memset(k[nm][:], 0.0), writes=[r_k])
        i2p = float(1.0 / (2 * math.pi))
        ts(S, "dve", k["fs"][:, :, 0:1], k["fvec"][:, :, 2:3], i2p, ALU.mult, [r_k], [r_k])
        tt(S, "dve", k["fs"][:, :, 1:2], k["fs"][:, :, 0:1], k["fvec"][:, :, 0:1], ALU.mult, [r_k], [r_k])
        tt(S, "dve", k["fs"][:, :, 2:3], k["fs"][:, :, 0:1], k["fvec"][:, :, 1:2], ALU.mult, [r_k], [r_k])
        CV = self.LP["CV"]
        for l in range(NL):
            cvv = k["lp"][:, l, CV:CV + 24].rearrange("p (a b) -> p a b", a=6)
            for (j, col) in ((0, 0), (1, 2)):
                ts(S, "dve", k["cvb"][:, l, :, j:j + 1], cvv[:, :, col:col + 1], k["cflags"][:, 2:3], ALU.mult,
                   [r_k], [r_k], s2=-1.0, op1=ALU.mult)

        for l in range(NL):
            ffn(l, 0, wd["w1_gate"], wd["w1_up"], wd["w1_down"])
            self.mixer(l)
            ffn(l, 2, wd["w2_gate"], wd["w2_up"], wd["w2_down"])

        gfo = NL * 24
        yv = yT_d.rearrange("(k p) t -> p k t", p=128)
        self.arena_reset()
        ost_ = self.carve([128, 2, 512], F32)
        ost = [ost_[:, 0, :], ost_[:, 1, :]]
        r_ost = RL(2, "ost")
        o_rr = 0
        for c in range(2):
            rms_rstd(c, lambda k: xT[:, k, CH(c)], [xr[k][c] for k in range(8)], 8, 1.0 / D, ones_bf[:])
            for k in range(8):
                i = self.f32_rr % 3
                self.f32_rr += 1
                tt(S, "dve", f32s[:, i, :], xT[:, k, CH(c)], rstd[:], ALU.mult, [xr[k][c], r_rstd], [r_f32s[i]])
                o = o_rr % 2
                o_rr += 1
                act(S, ost[o], f32s[:, i, :], AF.Copy, [r_f32s[i], r_gvec], [r_ost[o]],
                    scale=gvec[:, gfo + k:gfo + k + 1])
                S.dma("sp", yv[:, k, CH(c)], ost[o], reads=[r_ost[o]])
        S.finish()

    def mixer(self, l):
        S = self.S
        C = self.ctx
        hT, hr, CH = C["hT"], C["hr"], C["CH"]
        self.arena_reset()
        ymix = self.carve([128, 8, T], BF16)
        ymr = [[R("ym%d_%d" % (k, c)) for c in range(2)] for k in range(8)]
        base = self.aoff
        self.mod_require(l, 1)
        C["norm_mod"](l, 1)
        win = self.wd["w_in"][l].rearrange("(k p) n -> p k n", p=128)
        self.bank_pool = list(range(8))
        self.mix_mlstm(l, ymix, ymr, win)
        self.arena_reset(base)
        if STAGE >= 3:
            self.mix_attn(l, ymix, ymr, win, glob=True)
            self.arena_reset(base)
            self.mix_attn(l, ymix, ymr, win, glob=False)
            self.arena_reset(base)
        if STAGE >= 4:
            self.mix_hyena(l, ymix, ymr, win)
        self.prefetch(("wout", l, 0), self.parts_wout(l, 0))
        self.prefetch(("wout", l, 1), self.parts_wout(l, 1))
        wd_ = self.wd
        for g in range(2):
            self.prefetch(("gu", l, 2, g), self.parts_gu(wd_["w2_gate"], wd_["w2_up"], l, g))
        self.bank_pool = list(range(8))
        if DEBUG:
            f32s, r_f32s = C["f32s"], C["r_f32s"]
            for k in range(8):
                for c in range(2):
                    i = self.f32_rr % 3
                    self.f32_rr += 1
                    act(S, f32s[:, i, :], ymix[:, k, CH(c)], AF.Copy, [ymr[k][c]], [r_f32s[i]])
                    S.dma("sp", self.dbg_ymix[l, :, k, c * 512:(c + 1) * 512], f32s[:, i, :], reads=[r_f32s[i]])
        wov = self.wd["w_out"][l].rearrange("(k p) n -> p k n", p=128)
        for half in range(2):
            sl, sr = self.wload(self.parts_wout(l, half), key=("wout", l, half))
            s3 = self.slot_view(sl, 8)
            for j in range(4):
                dt_ = half * 4 + j
                for c in range(2):
                    pb, pr = self.bank()
                    groups = [(pb[:], s3[:, k, j * 128:(j + 1) * 128], ymix[:, k, CH(c)], k == 0, k == 7) for k in range(8)]
                    mm(S, groups, [sr] + [ymr[k][c] for k in range(8)], [pr])
                    C["resid_add"](l, 1, dt_, c, pb, pr)

    def parts_attn(self, l, glob):
        win = self.wd["w_in"][l].rearrange("(k p) n -> p k n", p=128)
        sv8 = lambda sl_: self.slot_view(sl_, 8)
        q0 = 1040 if glob else 1552
        k0, v0 = q0 + 256, q0 + 384
        return [(lambda sl_: sv8(sl_)[:, :, 0:256], win[:, :, q0:q0 + 256]),
                (lambda sl_: sv8(sl_)[:, :, 256:384], win[:, :, k0:k0 + 128]),
                (lambda sl_: sv8(sl_)[:, :, 384:448], win[:, :, k0 + 64:k0 + 128]),
                (lambda sl_: sv8(sl_)[:, :, 448:512], win[:, :, k0:k0 + 64]),
                (lambda sl_: sv8(sl_)[:, :, 512:640], win[:, :, v0:v0 + 128])]

    def parts_win(self, l, c0, n):
        win = self.wd["w_in"][l].rearrange("(k p) n -> p k n", p=128)
        return [(lambda sl_: self.slot_view(sl_, 8)[:, :, 0:n], win[:, :, c0:c0 + n])]

    def parts_wout(self, l, half):
        wov = self.wd["w_out"][l].rearrange("(k p) n -> p k n", p=128)
        return [(lambda sl_: self.slot_view(sl_, 8)[:, :, 0:512], wov[:, :, half * 512:(half + 1) * 512])]

    def parts_gu(self, wg, wu, l, g):
        wgv = wg[l].rearrange("(k p) n -> p k n", p=128)
        wuv = wu[l].rearrange("(k p) n -> p k n", p=128)
        c0 = g * 256
        return [(lambda sl_: self.slot_view(sl_, 8)[:, :, 0:256], wgv[:, :, c0:c0 + 256]),
                (lambda sl_: self.slot_view(sl_, 8)[:, :, 256:512], wuv[:, :, c0:c0 + 256])]

    def pbank(self):
        i = self.bank_pool[self.bank_rr % len(self.bank_pool)]
        self.bank_rr += 1
        return self.ps[i], self.psr[i]

    def proj_fm(self, c, s3, col0, pb):
        hT, CH = self.ctx["hT"], self.ctx["CH"]
        return [(pb[:], s3[:, k, col0:col0 + 128], hT[:, k, CH(c)], k == 0, k == 7) for k in range(8)]

    def proj_tok(self, tt_, s3, col0, ncols, pb, pc0):
        hT = self.ctx["hT"]
        return [(pb[:, pc0:pc0 + ncols], hT[:, k, tt_ * 128:(tt_ + 1) * 128], s3[:, k, col0:col0 + ncols], k == 0, k == 7)
                for k in range(8)]

    def mix_mlstm(self, l, ymix, ymr, win):
        S, k_, d = self.S, self.k, self.d
        C = self.ctx
        hr, CH = C["hr"], C["CH"]
        LP = self.LP
        lp = k_["lp"]
        r_k = self.r_k
        hall = [hr[k][c] for k in range(8) for c in range(2)]
        sv8 = lambda sl_: self.slot_view(sl_, 8)
        slA1, srA1 = self.wload(self.parts_win(l, 0, 512), key=("A1", l))
        slA2, srA2 = self.wload(self.parts_win(l, 512, 512), key=("A2", l))
        slG, srG = self.wload(self.parts_win(l, 1024, 16), key=("G", l))
        sA1, sA2, sG = sv8(slA1), sv8(slA2), sv8(slG)
        self.prefetch(("attn", l, True), self.parts_attn(l, True))
        if CUT == 1:
            return
        aqT = self.carve([128, 2, T], BF16)
        akp = self.carve([128, 4, T], BF16)
        ktok = self.carve([128, 8, 256], BF16)
        vaug = self.carve([128, 8, 4, 66], BF16)
        sgo = self.carve([128, 8, 256], BF16)
        gts = self.carve([128, 8, 16], F32)
        lf = self.carve([128, 8, 8], F32)
        cum = self.carve([128, 8, 16], F32)
        call = self.carve([128, 8, 8], F32)
        wall = self.carve([128, 8, 8], F32)
        wkl = self.carve([128, 8, 8], F32)
        wkall = self.carve([128, 8, 8], F32)
        wkm = self.carve([128, 8, 8], F32)
        dec = self.carve([128, 8, 4], F32)
        lfB = self.carve([128, 8, 128], F32)
        E = self.carve([128, 8, 128], F32)
        AT = self.carve([128, 2, 4, 128], BF16)
        hf = self.carve([128, 8, 256], F32)
        numt = self.carve([128, 2, 260], F32)
        h64 = self.carve([128, 2, 256], F32)
        dsm = self.carve([128, 2, 8], F32)
        kwp = self.carve([128, 2, 4, 192], BF16)
        yatok = self.carve([128, 8, 256], BF16)
        snap = self.carve([128, 2, 4, 130], F32)
        e0 = self.carve([128, 2, 2], F32)
        ssall = self.carve([128, 8, 4], F32)
        sq1 = self.carve([128, 2, 256], F32)
        mst = self.carve([128, 64], F32)
        scl = self.carve([128, 2, 8], F32)
        r_aq, r_akp = RL(2, "aq"), RL(2, "akp")
        r_ktok, r_vaug, r_sgo, r_gts = RL(8, "ktok"), RL(8, "vaug"), RL(8, "sgo"), R("gts")
        r_gate = R("gate")
        r_lfB, r_E, r_AT = RL(2, "lfB"), RL(2, "E"), RL(2, "AT")
        r_hf = RL(8, "hf")
        r_num, r_tmpn, r_h64, r_dsm, r_kwp = RL(2, "num"), RL(2, "tmpn"), RL(2, "h64"), RL(2, "dsm"), RL(2, "kwp")
        r_C, r_Cb = RL(2, "C"), RL(2, "Cb")
        r_snap = [[R("snap") for _ in range(4)] for _ in range(2)]
        r_ms, r_scl = R("mst"), R("scl")
        r_ya = RL(8, "ya")
        r_ss = R("ss")
        r_sq1 = RL(2, "sq1")
        S.op("pool", lambda e: e.memset(akp, 0.0), writes=r_akp)
        S.op("pool", lambda e: e.memset(kwp, 0.0), writes=r_kwp)
        S.op("pool", lambda e: e.memset(vaug[:, :, :, 64:65], 1.0), writes=r_vaug)
        if CUT == 2:
            return
        for t2 in range(2):
            for c in range(2):
                pb, pr = self.pbank()
                mm(S, self.proj_fm(c, sA1, t2 * 128, pb), [srA1] + hall, [pr])
                act(S, aqT[:, t2, CH(c)], pb[:], AF.Copy, [pr], [r_aq[c]])
        if CUT == 21:
            return
        for t2 in range(2):
            for c in range(2):
                pb, pr = self.pbank()
                mm(S, self.proj_fm(c, sA1, 256 + t2 * 128, pb), [srA1] + hall, [pr])
                act(S, akp[0:64, 2 * t2, CH(c)], pb[0:64, :], AF.Copy, [pr], [r_akp[c]])
                cp(S, "dve", akp[64:128, 2 * t2 + 1, CH(c)], pb[64:128, :], [pr], [r_akp[c]])
        if CUT == 22:
            return
        BG = LP["BG"]
        for t_ in range(8):
            p1, pr1 = self.pbank()
            p2, pr2 = self.pbank()
            g = self.proj_tok(t_, sA1, 256, 256, p1, 0) + self.proj_tok(t_, sA2, 0, 256, p1, 256)
            g += self.proj_tok(t_, sA2, 256, 256, p2, 0) + self.proj_tok(t_, sG, 0, 16, p2, 256)
            mm(S, g, [srA1, srA2, srG] + hall, [pr1, pr2])
            if CUT == 23:
                continue
            act(S, ktok[:, t_, :], p1[:, 0:256], AF.Copy, [pr1], [r_ktok[t_]])
            cp(S, "dve", vaug[:, t_, :, 0:64], p1[:, 256:512].rearrange("p (a b) -> p a b", a=4), [pr1], [r_vaug[t_]])
            if CUT == 24:
                continue
            act(S, sgo[:, t_, :], p2[:, 0:256], AF.Sigmoid, [pr2], [r_sgo[t_]])
            tt(S, "dve", gts[:, t_, :], p2[:, 256:272], lp[:, l, BG:BG + 16], ALU.add, [pr2, r_k], [r_gts])
            self.mod_pump()
        if CUT in (3, 23, 24):
            return
        ai, af = gts[:, :, 0:8], gts[:, :, 8:16]
        act(S, lf, af, AF.Exp, [r_gts], [r_gate], scale=-1.0)
        act(S, lf, lf, AF.Ln, [r_gate], [r_gate], bias=1.0)
        ts(S, "dve", lf, lf, -1.0, ALU.mult, [r_gate], [r_gate])
        tri = k_["tri"]
        pbc, prc = self.pbank()
        g = []
        for t_ in range(8):
            g.append((pbc[:, t_ * 16:t_ * 16 + 4], tri[:, 0, :], lf[:, t_, 0:4], True, True))
            g.append((pbc[:, t_ * 16 + 4:t_ * 16 + 8], tri[:, 1, :], lf[:, t_, 4:8], True, True))
            g.append((pbc[:, t_ * 16 + 8:t_ * 16 + 16], k_["ones_f"][:], lf[:, t_, 0:8], True, True))
        mm(S, g, [r_gate, r_k], [prc])
        cp(S, "dve", cum, pbc[:, 0:128].rearrange("p (a b) -> p a b", a=8), [prc], [r_gate])
        bc, bt = cum[:, :, 0:8], cum[:, :, 8:16]
        tt(S, "dve", call, ai, bc, ALU.subtract, [r_gts, r_gate], [r_gate])
        ts(S, "dve", call, call, LN8, ALU.add, [r_gate], [r_gate])
        act(S, wall, bc, AF.Exp, [r_gate], [r_gate])
        tt(S, "dve", wkl, call, bt, ALU.add, [r_gate], [r_gate])
        act(S, wkall, wkl, AF.Exp, [r_gate], [r_gate])
        ts(S, "dve", wkm, wkl, -LN8, ALU.add, [r_gate], [r_gate])
        for g_ in range(2):
            rows = slice(g_ * 64, (g_ + 1) * 64)
            act(S, dec[rows, :, :], cum[rows, :, 8 + g_:16:2], AF.Exp, [r_gate], [r_gate])
        if CUT == 4:
            return
        ident_f = k_["ident_f"]
        for dr in range(2):
            pbm, prm = self.pbank()
            g = [(pbm[0:4, t_:t_ + 1], lf[:, t_, dr * 4:dr * 4 + 4], k_["ones_f"][:, 0:1], True, True) for t_ in range(8)]
            mm(S, g, [r_gate, r_k], [prm])
            cp(S, "dve", mst[0:4, dr * 8:dr * 8 + 8], pbm[0:4, 0:8], [prm], [r_ms])
            for hh in range(2):
                pbt, prt = self.pbank()
                for q in range(4):
                    t_ = hh * 4 + q
                    S.op("pe", lambda e, t_=t_, q=q, pbt=pbt: e.transpose(pbt[0:4, q * 128:(q + 1) * 128],
                                                                          wkm[:, t_, dr * 4:dr * 4 + 4], ident_f[:]),
                         [r_gate, r_k], [prt])
                S.op("dve", lambda e, pbt=pbt, hh=hh: e.tensor_reduce(
                    out=mst[0:4, 16 + dr * 8 + hh * 4:16 + dr * 8 + hh * 4 + 4],
                    in_=pbt[0:4, :].rearrange("p (a b) -> p a b", a=4), axis=AX.X, op=ALU.max), [prt], [r_ms])
            bv_ = mst[0:4, dr * 8:dr * 8 + 8].rearrange("p (a b) -> p a b", a=4)
            av_ = mst[0:4, 16 + dr * 8:16 + dr * 8 + 8].rearrange("p (a b) -> p a b", a=4)
            fi, se = (0, 1) if dr == 0 else (1, 0)
            mf = mst[0:4, 32 + dr * 4:32 + dr * 4 + 4]
            ts(S, "dve", mf, bv_[:, :, fi], k_["m0c"][0:4, l * 2 + dr:l * 2 + dr + 1], ALU.add, [r_ms, r_k], [r_ms])
            tt(S, "dve", mf, mf, av_[:, :, fi], ALU.max, [r_ms], [r_ms])
            tt(S, "dve", mf, mf, bv_[:, :, se], ALU.add, [r_ms], [r_ms])
            tt(S, "dve", mf, mf, av_[:, :, se], ALU.max, [r_ms], [r_ms])
            S.dma("sp", d["o_m"][l, dr], mf, reads=[r_ms])
            en = mst[0:4, 40 + dr * 4:40 + dr * 4 + 4]
            act(S, en, mf, AF.Exp, [r_ms], [r_ms], scale=-1.0)
            rhs2 = mst[0:4, 48 + dr * 8:48 + dr * 8 + 8]
            tt(S, "dve", rhs2.rearrange("p (a b) -> p a b", a=4), en.unsqueeze(2).to_broadcast([4, 4, 2]),
               k_["sel"][0:4, 128:130].unsqueeze(1).to_broadcast([4, 4, 2]), ALU.mult, [r_ms, r_k], [r_ms])
            pbs, prs = self.pbank()
            mm(S, [(pbs[:, 0:8], k_["sel"][0:4, 0:128], rhs2, True, True)], [r_ms, r_k], [prs])
            cp(S, "dve", scl[:, dr, :], pbs[:, 0:8], [prs], [r_scl])
        if CUT == 5:
            return
        Clo, Chi, Cblo, Cbhi = k_["Clo"], k_["Chi"], k_["Cblo"], k_["Cbhi"]
        act(S, e0, k_["m0rep"][:, l, :, :], AF.Exp, [r_k], [r_gate])
        halves = ((slice(0, 64), Clo, Cblo), (slice(64, 128), Chi, Cbhi))
        for dr in range(2):
            S.dma("sp", Clo[:, dr, :, :], d["C0"][l, dr, :, :, :], writes=[r_C[dr]])
            tt(S, "dve", Clo[:, dr, :, :], Clo[:, dr, :, :], e0[:, dr, :].unsqueeze(2).to_broadcast([128, 2, 65]),
               ALU.mult, [r_C[dr], r_gate], [r_C[dr]])
            for (rows, Cx, Cbx) in halves:
                act(S, Cbx[rows, dr, :, 0:65], Clo[rows, dr, :, :], AF.Copy, [r_C[dr]], [r_Cb[dr]])
        keep = k_["cflags"][:, 1:2]

        tmpn2 = self.carve([128, 2, 2, 260], F32)
        r_tmpn2 = [[R("tn00"), R("tn01")], [R("tn10"), R("tn11")]]
        v3 = lambda ap: ap.rearrange("p (a b) -> p a b", a=4)

        def state_gen(dr, t_):
            tok = slice(t_ * 128, (t_ + 1) * 128)
            chs = slice(dr * 4, dr * 4 + 4)
            b_ = t_ % 2
            pst, prst = self.ps[3 + 4 * dr], self.psr[3 + 4 * dr]
            tt(S, "pool", kwp[:, dr, :, 64:128], ktok[:, t_, :].rearrange("p (a b) -> p a b", a=4),
               wkall[:, t_, chs].unsqueeze(2).to_broadcast([128, 4, 64]), ALU.mult, [r_ktok[t_], r_gate], [r_kwp[dr]])
            yield
            g = [(pst[:, h * 65:(h + 1) * 65], aqT[:, h // 2, tok], (Cblo if h % 2 == 0 else Cbhi)[:, dr, h // 2, 0:65], True, True)
                 for h in range(4)]
            for j in range(2):
                o = pst[:, 260 + j * 65:260 + (j + 1) * 65]
                g.append((o, kwp[:, dr, 2 * j, 64:192], vaug[:, t_, 2 * j, 0:65], True, False))
                g.append((o, kwp[:, dr, 2 * j + 1, 0:128], vaug[:, t_, 2 * j + 1, 0:65], False, True))
            mm(S, g, r_aq + [r_Cb[dr], r_kwp[dr], r_vaug[t_]], [prst])
            yield
            tt(S, "dve", v3(tmpn2[:, dr, b_, :]), v3(pst[:, 0:260]), wall[:, t_, chs].unsqueeze(2).to_broadcast([128, 4, 65]),
               ALU.mult, [prst, r_gate], [r_tmpn2[dr][b_]])
            yield
            tt(S, "dve", Clo[:, dr, :, :], Clo[:, dr, :, :],
               dec[:, t_, dr * 2:dr * 2 + 2].unsqueeze(2).to_broadcast([128, 2, 65]), ALU.mult,
               [r_C[dr], r_gate], [r_C[dr]])
            yield
            tt(S, "dve", Clo[:, dr, :, :], Clo[:, dr, :, :], pst[:, 260:390].rearrange("p (a b) -> p a b", a=2),
               ALU.add, [r_C[dr], prst], [r_C[dr]])
            yield
            end = (t_ % 2 == 1) if dr == 0 else (t_ % 2 == 0)
            if end:
                sq_ = t_ // 2
                tt(S, "dve", snap[:, dr, sq_, :].rearrange("p (a b) -> p a b", a=2), Clo[:, dr, :, :],
                   scl[:, dr, sq_ * 2:sq_ * 2 + 2].unsqueeze(2).to_broadcast([128, 2, 65]), ALU.mult,
                   [r_C[dr], r_scl], [r_snap[dr][sq_]])
                yield
                S.dma("sp", d["o_C"][l, dr, sq_], snap[:, dr, sq_, :].rearrange("p (a b) -> p a b", a=2),
                      reads=[r_snap[dr][sq_]])
                yield
                ts(S, "dve", Clo[:, dr, :, :], Clo[:, dr, :, :], keep, ALU.mult, [r_C[dr], r_k], [r_C[dr]])
                yield
            for (rows, Cx, Cbx) in halves:
                act(S, Cbx[rows, dr, :, 0:65], Clo[rows, dr, :, :], AF.Copy, [r_C[dr]], [r_Cb[dr]])
                yield

        def out_gen(dr, t_):
            tok = slice(t_ * 128, (t_ + 1) * 128)
            chs = slice(dr * 4, dr * 4 + 4)
            b_ = t_ % 2
            pbe, pre = self.ps[0 + 4 * dr], self.psr[0 + 4 * dr]
            pbs_, prs_ = self.ps[1 + 4 * dr], self.psr[1 + 4 * dr]
            pbi, pri = self.ps[2 + 4 * dr], self.psr[2 + 4 * dr]
            act(S, lfB[:, chs, :], lf[:, t_, chs].unsqueeze(2).to_broadcast([128, 4, 128]), AF.Copy, [r_gate], [r_lfB[dr]])
            yield
            g = []
            for h in range(4):
                o = pbe[:, h * 128:(h + 1) * 128]
                g.append((o, lfB[:, dr * 4 + h, :], tri[:, dr, :], True, False))
                g.append((o, ident_f[:], tri[:, 2 + dr, :], False, True))
            mm(S, g, [r_lfB[dr], r_k], [pre])
            yield
            g = [(pbs_[:, h * 128:(h + 1) * 128], akp[:, h, tok], aqT[:, h // 2, tok], True, True) for h in range(4)]
            mm(S, g, r_akp + r_aq, [prs_])
            yield
            for h in range(4):
                act(S, E[:, dr * 4 + h, :], pbe[:, h * 128:(h + 1) * 128], AF.Exp, [pre, r_gate], [r_E[dr]],
                    bias=call[:, t_, dr * 4 + h:dr * 4 + h + 1])
                yield
            tt(S, "dve", AT[:, dr, :, :], pbs_[:].rearrange("p (a b) -> p a b", a=4), E[:, chs, :], ALU.mult,
               [prs_, r_E[dr]], [r_AT[dr]])
            yield
            g = [(pbi[:, h * 65:(h + 1) * 65], AT[:, dr, h, :], vaug[:, t_, h, 0:65], True, True) for h in range(4)]
            mm(S, g, [r_AT[dr], r_vaug[t_]], [pri])
            yield
            tt(S, "dve", numt[:, dr, :], pbi[:, 0:260], tmpn2[:, dr, b_, :], ALU.add, [pri, r_tmpn2[dr][b_]], [r_num[dr]])
            yield
            nv = v3(numt[:, dr, :])
            dn, rd = dsm[:, dr, 0:4], dsm[:, dr, 4:8]
            ts(S, "dve", dn, nv[:, :, 64], -1.0, ALU.mult, [r_num[dr]], [r_dsm[dr]], s2=1.0, op1=ALU.max)
            yield
            tt(S, "dve", dn, dn, nv[:, :, 64], ALU.max, [r_num[dr], r_dsm[dr]], [r_dsm[dr]])
            yield
            recip(S, rd, dn, [r_dsm[dr]], [r_dsm[dr]])
            yield
            first = (t_ <= 3) if dr == 0 else (t_ >= 4)
            if first:
                tt(S, "dve", v3(hf[:, t_, :]), nv[:, :, 0:64], rd.unsqueeze(2).to_broadcast([128, 4, 64]), ALU.mult,
                   [r_num[dr], r_dsm[dr]], [r_hf[t_]])
                yield
            else:
                tt(S, "dve", v3(h64[:, dr, :]), nv[:, :, 0:64], rd.unsqueeze(2).to_broadcast([128, 4, 64]), ALU.mult,
                   [r_num[dr], r_dsm[dr]], [r_h64[dr]])
                yield
                tt(S, "dve", hf[:, t_, :], hf[:, t_, :], h64[:, dr, :], ALU.add, [r_h64[dr], r_hf[t_]], [r_hf[t_]])
                yield

        def zip_run(gens):
            gens = list(gens)
            while gens:
                for g__ in list(gens):
                    try:
                        next(g__)
                    except StopIteration:
                        gens.remove(g__)

        def pump_gen(npump):
            for _ in range(npump):
                for _ in range(5):
                    yield
                self.mod_pump(bank=(self.ps[2], self.psr[2], 300))
                yield

        order = [list(range(8)), list(range(7, -1, -1))]
        zip_run([state_gen(0, order[0][0]), state_gen(1, order[1][0])])
        for i in range(8):
            gens = [out_gen(0, order[0][i]), out_gen(1, order[1][i])]
            if i + 1 < 8:
                gens = [state_gen(0, order[0][i + 1]), state_gen(1, order[1][i + 1])] + gens
            gens.append(pump_gen(3))
            zip_run(gens)
        GM = LP["GM"]
        for t_ in range(8):
            b_ = t_ % 2
            tt(S, "dve", sq1[:, b_, :], hf[:, t_, :], hf[:, t_, :], ALU.mult, [r_hf[t_]], [r_sq1[b_]])
            S.op("dve", lambda e, t_=t_, b_=b_: e.tensor_reduce(out=ssall[:, t_, :],
                                                              in_=sq1[:, b_, :].rearrange("p (a b) -> p a b", a=4),
                                                              axis=AX.X, op=ALU.add), [r_sq1[b_]], [r_ss])
        act(S, ssall, ssall, AF.Ln, [r_ss], [r_ss], bias=EPS, scale=1.0 / 64)
        act(S, ssall, ssall, AF.Exp, [r_ss], [r_ss], scale=-0.5)
        ident_bf = k_["ident_bf"]
        for t_ in range(8):
            b_ = t_ % 2
            tt(S, "dve", sq1[:, b_, :].rearrange("p (a b) -> p a b", a=4), hf[:, t_, :].rearrange("p (a b) -> p a b", a=4),
               ssall[:, t_, :].unsqueeze(2).to_broadcast([128, 4, 64]), ALU.mult, [r_hf[t_], r_ss], [r_sq1[b_]])
            tt(S, "pool", h64[:, b_, :], sgo[:, t_, :], lp[:, l, GM:GM + 256], ALU.mult, [r_sgo[t_], r_k], [r_h64[b_]])
            tt(S, "dve", yatok[:, t_, :], sq1[:, b_, :], h64[:, b_, :], ALU.mult, [r_h64[b_], r_sq1[b_]], [r_ya[t_]])
            self.mod_pump()
            pbt, prt = self.pbank()
            pv = pbt[:].bitcast(BF16)
            for t2 in range(2):
                S.op("pe", lambda e, t2=t2, pv=pv, t_=t_: e.transpose(pv[:, t2 * 128:(t2 + 1) * 128],
                                                                      yatok[:, t_, t2 * 128:(t2 + 1) * 128], ident_bf[:]),
                     [r_ya[t_], r_k], [prt])
            c = t_ // 4
            for t2 in range(2):
                if t2 == 0:
                    act(S, ymix[:, t2, t_ * 128:(t_ + 1) * 128], pv[:, t2 * 128:(t2 + 1) * 128], AF.Copy, [prt], [ymr[t2][c]])
                else:
                    cp(S, "dve", ymix[:, t2, t_ * 128:(t_ + 1) * 128], pv[:, t2 * 128:(t2 + 1) * 128], [prt], [ymr[t2][c]])

    def mix_attn(self, l, ymix, ymr, win, glob):
        S, k_, d = self.S, self.k, self.d
        C = self.ctx
        hr, CH = C["hr"], C["CH"]
        LP = self.LP
        lp = k_["lp"]
        r_k = self.r_k
        hall = [hr[k][c] for k in range(8) for c in range(2)]
        sv8 = lambda sl_: self.slot_view(sl_, 8)
        q0 = 1040 if glob else 1552
        k0, v0 = q0 + 256, q0 + 384
        sl, sr = self.wload(self.parts_attn(l, glob), key=("attn", l, glob))
        if glob:
            self.prefetch(("attn", l, False), self.parts_attn(l, False))
        else:
            self.prefetch(("D1", l), self.parts_win(l, 2064, 512))
            self.prefetch(("D2", l), self.parts_win(l, 2576, 256))
        s3 = sv8(sl)
        ymb = 2 if glob else 4
        cs = self.carve([128, 2, T], F32)
        qpad = self.carve([128, 4, T], BF16)
        kfull = self.carve([128, 2, 1280], BF16)
        vpad = self.carve([128, 10, 2, 192], BF16)
        kout = self.carve([128, T], F32)
        vout = self.carve([128, 8, 128], F32)
        PT = self.carve([128, 3, 512], BF16)
        rden = self.carve([128, 2, 512], F32)
        raw = self.carve([128, 2, 512], F32)
        tb = self.carve([128, 2, 512], F32)
        gm = self.carve([128, 2, T], BF16)
        smask = self.carve([128, 8, 384], BF16) if not glob else None
        es = self.carve([128, 4], F32)
        r_cs, r_qp, r_kf, r_vp, r_kout, r_vout = R("cs"), RL(2, "qp"), R("kf"), RL(10, "vp"), R("kout"), R("vout")
        r_PT, r_rden, r_raw, r_tb, r_gm, r_sm, r_es = RL(3, "PT"), RL(2, "rden"), RL(2, "raw"), RL(2, "tb"), R("gm"), R("sm"), R("es")
        S.dma("sp", cs, d["ropeCS"], writes=[r_cs])
        S.op("pool", lambda e: e.memset(qpad, 0.0), writes=r_qp)
        S.op("pool", lambda e: e.memset(vpad[:, 2:10, :, :], 0.0), writes=r_vp[2:])
        kTd, vpd = (d["gkT"], d["gvp"]) if glob else (d["skT"], d["svp"])
        for x in range(2):
            S.dma("pool", kfull[:, x, 0:256], kTd[l, x], writes=[r_kf])
            S.dma("pool", vpad[:, x, :, :], vpd[l, x], writes=[r_vp[x]])
        if glob:
            S.dma("pool", gm[0:5, :, :], d["gmAB"], writes=[r_gm])
        if not glob:
            S.dma("pool", smask, d["swamask"], writes=[r_sm])
            SK = LP["SK"]
            act(S, es, lp[:, l, SK:SK + 4], AF.Exp, [r_k], [r_es])
        GQK = LP["GQK"]
        ropeP = k_["ropeP"]
        rstd2 = self.carve([128, 2, 512], F32)
        r_rstd2 = RL(2, "rstd2")

        def prelude(ti, c, b_):
            col0 = ti * 128
            pb, pr = self.pbank()
            mm(S, self.proj_fm(c, s3, col0, pb), [sr] + hall, [pr])
            yield
            rw = raw[:, b_, :]
            if glob:
                act(S, rw, pb[:], AF.Copy, [pr], [r_raw[b_]])
                yield
                i = self.sq_rr % 2
                self.sq_rr += 1
                sqb, r_sq = C["sqb"], C["r_sq"]
                act(S, sqb[:, i, :], rw, AF.Square, [r_raw[b_]], [r_sq[i]])
                yield
                p2, pr2 = self.pbank()
                mm(S, [(p2[:], k_["blockones"][:], sqb[:, i, :], True, True)], [r_sq[i], r_k], [pr2])
                yield
                rstd, r_rstd = rstd2[:, b_, :], r_rstd2[b_]
                act(S, rstd, p2[:], AF.Ln, [pr2], [r_rstd], bias=EPS, scale=1.0 / 64)
                yield
                act(S, rstd, rstd, AF.Exp, [r_rstd], [r_rstd], scale=-0.5)
                yield
                tt(S, "dve", rw, rw, rstd, ALU.mult, [r_raw[b_], r_rstd], [r_raw[b_]])
                yield
                gcol = GQK + (0 if ti < 2 else 1)
                act(S, rw, rw, AF.Copy, [r_raw[b_], r_k], [r_raw[b_]], scale=lp[:, l, gcol:gcol + 1])
                yield
            else:
                act(S, rw, pb[:], AF.Copy, [pr], [r_raw[b_]])
                yield
            p3, pr3 = self.pbank()
            mm(S, [(p3[:], ropeP[:], rw, True, True)], [r_raw[b_], r_k], [pr3])
            yield
            ta, tb_ = rw, tb[:, b_, :]
            tt(S, "dve", ta, rw, cs[:, 0, CH(c)], ALU.mult, [r_raw[b_], r_cs], [r_raw[b_]])
            yield
            tt(S, "dve", tb_, p3[:], cs[:, 1, CH(c)], ALU.mult, [pr3, r_cs], [r_tb[b_]])
            yield
            if ti < 2:
                for g_ in range(2):
                    rows = slice(g_ * 64, (g_ + 1) * 64)
                    tt(S, "dve", qpad[rows, 2 * ti + g_, CH(c)], ta[rows, :], tb_[rows, :], ALU.add, [r_tb[b_], r_raw[b_]], [r_qp[c]])
                    yield
            elif ti == 2:
                tt(S, "dve", kout[:, CH(c)], ta, tb_, ALU.add, [r_tb[b_], r_raw[b_]], [r_kout])
                yield
                act(S, kfull[:, 0, 256 + c * 512:256 + (c + 1) * 512], kout[:, CH(c)], AF.Copy, [r_kout], [r_kf])
                yield
            else:
                tt(S, "dve", kfull[:, 1, 256 + c * 512:256 + (c + 1) * 512], ta, tb_, ALU.add, [r_tb[b_], r_raw[b_]], [r_kf])
                yield

        its = [(ti, c) for ti in range(4) for c in range(2)]
        for i0 in range(0, 8, 2):
            gens = [prelude(its[i0][0], its[i0][1], 0), prelude(its[i0 + 1][0], its[i0 + 1][1], 1)]
            while gens:
                for g__ in list(gens):
                    try:
                        next(g__)
                    except StopIteration:
                        gens.remove(g__)
        S.dma("sp", (d["o_gk"] if glob else d["o_sk"])[l], kout, reads=[r_kout])
        for t_ in range(8):
            pb, pr = self.pbank()
            mm(S, self.proj_tok(t_, s3, 512, 128, pb, 0), [sr] + hall, [pr])
            act(S, vout[:, t_, :], pb[:, 0:128], AF.Copy, [pr], [r_vout])
            cp(S, "dve", vpad[:, 2 + t_, :, 64:128], pb[:, 0:128].rearrange("p (a b) -> p a b", a=2), [pr], [r_vp[2 + t_]])
        S.dma("sp", (d["o_gv"] if glob else d["o_sv"])[l].rearrange("(t p) n -> p t n", p=128), vout, reads=[r_vout])
        self.bank_pool = [0, 1, 2, 3]
        ones_bf = C["ones_bf"]
        r_ones = C["r_ones"]
        ident_bf = k_["ident_bf"]
        ctxb = k_["cflags"][:, 0:1]
        pt_rr = 0
        acc_rr = 0
        for h in range(4):
            kv, g_ = h // 2, h % 2
            kx = 0 if g_ == kv else 1
            rows = slice(g_ * 64, (g_ + 1) * 64)
            vw = slice(64, 192) if g_ == 0 else slice(0, 128)
            for c in range(2):
                if glob:
                    tiles = [(mt, 0, 512) for mt in range(10)]
                else:
                    tiles = [(0, 0, 512), (1, 0, 512)]
                    for j in range(8):
                        lo, hi = max((j - 1) * 128, c * 512), min((j + 2) * 128, (c + 1) * 512)
                        if hi > lo:
                            tiles.append((2 + j, lo - c * 512, hi - c * 512))
                ai_ = 4 + 2 * (acc_rr % 2)
                acc_rr += 1
                pn, prn, pd, prd = self.ps[ai_], self.psr[ai_], self.ps[ai_ + 1], self.psr[ai_ + 1]
                pend = []

                def qk(idx):
                    mt, lo, hi = tiles[idx]
                    pb, pr = self.pbank()
                    qs = slice(c * 512 + lo, c * 512 + hi)
                    g = [(pb[:, lo:hi], kfull[:, kx, mt * 128:(mt + 1) * 128], qpad[:, h, qs], True, mt < 2)]
                    rd = [r_kf, r_qp[c]]
                    if mt >= 2:
                        if glob:
                            g.append((pb[:, lo:hi], gm[0:5, 0, (mt - 2) * 128:(mt - 1) * 128], gm[0:5, 1, qs], False, True))
                            rd.append(r_gm)
                        else:
                            j = mt - 2
                            m0_ = c * 512 + lo - (j - 1) * 128
                            g.append((pb[:, lo:hi], ident_bf[:], smask[:, j, m0_:m0_ + (hi - lo)], False, True))
                            rd += [r_sm, r_k]
                    mm(S, g, rd, [pr])
                    return pb, pr

                nt = len(tiles)
                look = 2
                for idx in range(min(look, nt)):
                    pend.append(qk(idx))
                for idx in range(nt):
                    mt, lo, hi = tiles[idx]
                    pb, pr = pend.pop(0)
                    if idx + look < nt:
                        pend.append(qk(idx + look))
                    pi = pt_rr % 3
                    pt_rr += 1
                    act(S, PT[:, pi, lo:hi], pb[:, lo:hi], AF.Exp, [pr, r_k], [r_PT[pi]],
                        bias=(ctxb if mt < 2 else 0.0), scale=0.125)
                    g = [(pn[:, lo:hi], vpad[:, mt, kv, vw], PT[:, pi, lo:hi], idx == 0, idx == nt - 1),
                         (pd[:, lo:hi], ones_bf[:], PT[:, pi, lo:hi], idx == 0, idx == nt - 1)]
                    mm(S, g, [r_vp[mt], r_PT[pi], r_ones], [prn, prd])
                ri = (h * 2 + c) % 2
                if glob:
                    recip(S, rden[rows, ri, :], pd[rows, :], [prd], [r_rden[ri]])
                else:
                    ts(S, "dve", rden[rows, ri, :], pd[rows, :], es[rows, h:h + 1], ALU.add, [prd, r_es], [r_rden[ri]])
                    recip(S, rden[rows, ri, :], rden[rows, ri, :], [r_rden[ri]], [r_rden[ri]])
                tt(S, "dve", ymix[rows, ymb + kv, CH(c)], pn[rows, :], rden[rows, ri, :], ALU.mult, [prn, r_rden[ri]],
                   [ymr[ymb + kv][c]])
        self.bank_pool = list(range(8))

    def mix_hyena(self, l, ymix, ymr, win):
        S, k_, d = self.S, self.k, self.d
        C = self.ctx
        hr, CH = C["hr"], C["CH"]
        LP = self.LP
        lp = k_["lp"]
        r_k = self.r_k
        hall = [hr[k][c] for k in range(8) for c in range(2)]
        sv8 = lambda sl_: self.slot_view(sl_, 8)
        TWO_PI = float(2 * math.pi)
        xoff = self.aoff
        feats = self.carve([128, T], F32)
        z1 = self.carve([128, T], F32)
        z2 = self.carve([128, T], F32)
        wnd = self.carve([128, 8, 256], F32)
        rr = self.carve([128, 512], F32)
        ii = self.carve([128, 512], I32)
        kf = self.carve([128, 512], F32)
        xend = self.aoff
        self.aoff = xoff
        raw = self.carve([128, 3, T], F32)
        uct = self.carve([128, 2, T], F32)
        pa = self.carve([128, 2, 256], F32)
        pq = self.carve([128, 2, 256], F32)
        yt = self.carve([128, 2, 256], F32)
        assert self.aoff <= xend
        self.aoff = xend
        r_X = R("X")
        x0 = self.carve([128, 2, T], F32)
        zf = self.carve([128, 2, T], F32)
        zbf = self.carve([128, 2, T], BF16)
        zh = self.carve([128, 8, 512], BF16)
        ZH = self.carve([128, 2, 512], F32)
        Y = self.carve([128, 8, 2, 256], BF16)
        pa2 = self.carve([128, 2, 256], F32)
        yt2 = ZH[:, :, 0:256]
        r_x0, r_zf, r_zbf = RL(2, "x0"), RL(2, "zf"), RL(2, "zbf")
        r_zh = RL(8, "zh")
        r_ZH, r_Y, r_pa, r_pq, r_yt = R("ZH"), RL(8, "Y"), R("pa"), R("pq"), RL(2, "yt")
        S.dma("sp", feats[0:33, :], d["featsT"], writes=[r_X])
        S.dma("sp", wnd, d["window"], writes=[r_X])
        fs = k_["fs"]

        def sin_layer(pb, pr, dst, bcol):
            ts(S, "dve", rr[0:64, :], pb[0:64, :], fs[:, l, 0:1], ALU.mult, [pr, r_k], [r_X], s2=fs[:, l, bcol:bcol + 1],
               op1=ALU.add)
            cp(S, "dve", ii[0:64, :], rr[0:64, :], [r_X], [r_X])
            cp(S, "dve", kf[0:64, :], ii[0:64, :], [r_X], [r_X])
            tt(S, "dve", rr[0:64, :], rr[0:64, :], kf[0:64, :], ALU.subtract, [r_X], [r_X])
            act(S, dst, rr[0:64, :], AF.Sin, [r_X], [r_X], scale=TWO_PI)

        def zip2(gens):
            gens = list(gens)
            while gens:
                for g__ in list(gens):
                    try:
                        next(g__)
                    except StopIteration:
                        gens.remove(g__)

        r_fc = [[R("fc%d%d" % (a_, c_)) for c_ in range(2)] for a_ in range(2)]

        def sin_chain(layer, c):
            cs_ = slice(c * 256, (c + 1) * 256)
            for hh in range(2):
                q_ = slice(c * 512 + hh * 256, c * 512 + (hh + 1) * 256)
                pb, pr = self.pbank()
                if layer == 0:
                    mm(S, [(pb[0:64, 0:256], k_["fw1"][0:33, l, :], feats[0:33, q_], True, True)], [r_X, r_k], [pr])
                else:
                    mm(S, [(pb[0:64, 0:256], k_["fw2"][0:64, l, :], z1[0:64, q_], True, True)], [r_fc[0][c], r_k], [pr])
                yield
                bcol = 1 + layer
                rg = R("tmp")
                ts(S, "dve", rr[0:64, cs_], pb[0:64, 0:256], fs[:, l, 0:1], ALU.mult, [pr, r_k], [r_fc[1][c]],
                   s2=fs[:, l, bcol:bcol + 1], op1=ALU.add)
                yield
                cp(S, "dve", ii[0:64, cs_], rr[0:64, cs_], [r_fc[1][c]], [r_fc[1][c]])
                yield
                cp(S, "dve", kf[0:64, cs_], ii[0:64, cs_], [r_fc[1][c]], [r_fc[1][c]])
                yield
                tt(S, "dve", rr[0:64, cs_], rr[0:64, cs_], kf[0:64, cs_], ALU.subtract, [r_fc[1][c]], [r_fc[1][c]])
                yield
                dst = (z1 if layer == 0 else z2)[0:64, q_]
                act(S, dst, rr[0:64, cs_], AF.Sin, [r_fc[1][c]], [r_fc[0][c] if layer == 0 else r_X], scale=TWO_PI)
                yield

        for c in range(2):
            S.op("dve", lambda e, c=c: e.memset(rr[0:64, c * 256:c * 256 + 1], 0.0), [r_X], [r_fc[1][c], r_fc[0][c]])
        zip2([sin_chain(0, 0), sin_chain(0, 1)])
        zip2([sin_chain(1, 0), sin_chain(1, 1)])
        for c in range(2):
            S.op("dve", lambda e, c=c: e.memset(rr[0:64, c * 256:c * 256 + 1], 0.0), [r_fc[1][c], r_fc[0][c]], [r_X])
        for t_ in range(8):
            pb, pr = self.pbank()
            tok = slice(t_ * 128, (t_ + 1) * 128)
            mm(S, [(pb[:, 0:256], z2[0:64, tok], k_["fw3"][0:64, l, :], True, False),
                   (pb[:, 0:256], k_["ones_f"][0:1, :], k_["fb3"][0:1, l, :], False, True)], [r_X, r_k], [pr])
            tt(S, "dve", zh[:, t_, 256:512], pb[:, 0:256], wnd[:, t_, :], ALU.mult, [pr, r_X], [r_zh[t_]])
        slD1, srD1 = self.wload(self.parts_win(l, 2064, 512), key=("D1", l))
        slD2, srD2 = self.wload(self.parts_win(l, 2576, 256), key=("D2", l))
        sD1, sD2 = sv8(slD1), sv8(slD2)
        Fd, Gd = d["dftF"], d["dftG"]
        fv = lambda m: Fd[m].rearrange("(k p) n -> p k n", p=128)
        gv = lambda m: Gd[m].rearrange("(k p) n -> p k n", p=128)

        def parts_dft(vw, q):
            return [(lambda sl_: sv8(sl_)[:, :, 0:256], vw(0)[:, :, q * 256:(q + 1) * 256]),
                    (lambda sl_: sv8(sl_)[:, :, 256:512], vw(1)[:, :, q * 256:(q + 1) * 256])]

        def zip_run(gens):
            gens = list(gens)
            while gens:
                for g__ in list(gens):
                    try:
                        next(g__)
                    except StopIteration:
                        gens.remove(g__)

        for q in range(2):
            self.prefetch(("F", l, q), parts_dft(fv, q))
        CV = LP["CV"]
        cvb = k_["cvb"]
        r_raw = RL(3, "hraw")
        r_uct = RL(2, "uct")

        def conv_chain(ct, ui):
            s3, col, srr = ((sD1, ct * 128, srD1), (sD1, 256 + ct * 128, srD1), (sD2, ct * 128, srD2))[ui]
            rr_ = r_raw[ui]
            fx = [r_X] if ct == 0 else []
            for c in range(2):
                pb, pr = self.pbank()
                mm(S, self.proj_fm(c, s3, col, pb), [srr] + hall, [pr])
                yield
                act(S, raw[:, ui, CH(c)], pb[:], AF.Copy, [pr], [rr_] + fx)
                yield
            tile = ui * 2 + ct
            cw = lambda j: lp[:, l, CV + tile * 4 + j:CV + tile * 4 + j + 1]
            u = raw[:, ui, :]
            if ui == 0:
                dst, wr = x0[:, ct, :], [r_x0[ct]]
            else:
                dst, wr = uct[:, ui - 1, :], [r_uct[ui - 1]] + fx
            rd = [rr_, r_k]
            act(S, dst, u, AF.Identity, rd, wr, bias=cw(3), scale=cw(1))
            yield
            for (o, a_, sc) in ((dst[:, 1:T], u[:, 0:T - 1], cw(0)), (dst[:, 0:T - 1], u[:, 1:T], cw(2)),
                                (dst[:, 256:T:256], u[:, 255:T - 1:256], cvb[:, l, tile, 0:1]),
                                (dst[:, 255:T - 1:256], u[:, 256:T:256], cvb[:, l, tile, 1:2])):
                S.op("dve", lambda e, o=o, a_=a_, sc=sc: e.scalar_tensor_tensor(out=o, in0=a_, scalar=sc, in1=o,
                                                                              op0=ALU.mult, op1=ALU.add), rd + wr[:1], wr[:1])
                yield

        for ct in range(2):
            zip_run([conv_chain(ct, ui) for ui in range(3)])
            tt(S, "dve", zf[:, ct, :], uct[:, 0, :], uct[:, 1, :], ALU.mult, r_uct, [r_zf[ct]])
            act(S, zbf[:, ct, :], zf[:, ct, :], AF.Copy, [r_zf[ct]], [r_zbf[ct]])
        for q in range(2, 4):
            self.prefetch(("F", l, q), parts_dft(fv, q))
        ident_bf = k_["ident_bf"]
        for t_ in range(8):
            pb, pr = self.pbank()
            pv = pb[:].bitcast(BF16)
            for ct in range(2):
                S.op("pe", lambda e, ct=ct, pv=pv, t_=t_: e.transpose(pv[:, ct * 128:(ct + 1) * 128],
                                                                      zbf[:, ct, t_ * 128:(t_ + 1) * 128], ident_bf[:]),
                     [r_zbf[ct], r_k], [pr])
            cp(S, "dve", zh[:, t_, 0:256], pv[:, 0:256], [pr], [r_zh[t_]])
        ZHs = [ZH, zbf.rearrange("p a b -> p (a b)").bitcast(F32).rearrange("p (a b) -> p a b", a=2)]
        r_ZHs = [r_ZH, R("ZHb")]
        PAs, PQs = [pa, pa2], [pq, yt]
        r_PA, r_PQ = RL(2, "PA"), RL(2, "PQ")
        seen = set()

        def ft_chain(ft, fj, s3, sr):
            bi = ft % 2
            Zb, rz = ZHs[bi], r_ZHs[bi]
            PA, PQ, rpa, rpq = PAs[bi], PQs[bi], r_PA[bi], r_PQ[bi]
            fresh = bi not in seen
            seen.add(bi)
            fz = (r_zbf if (fresh and bi == 1) else [])
            fx = ([r_X] if fresh else [])
            pre_, prr = self.pbank()
            pim, pri = self.pbank()
            g = [(pre_[:], s3[:, t_, fj * 128:(fj + 1) * 128], zh[:, t_, :], t_ == 0, t_ == 7) for t_ in range(8)]
            g += [(pim[:], s3[:, t_, 256 + fj * 128:256 + (fj + 1) * 128], zh[:, t_, :], t_ == 0, t_ == 7) for t_ in range(8)]
            mm(S, g, [sr] + r_zh, [prr, pri])
            yield
            act(S, Zb[:, 0, :], pre_[:], AF.Copy, [prr], [rz] + fz)
            yield
            act(S, Zb[:, 1, :], pim[:], AF.Copy, [pri], [rz])
            yield
            Zr, Hr, Zi, Hi = Zb[:, 0, 0:256], Zb[:, 0, 256:512], Zb[:, 1, 0:256], Zb[:, 1, 256:512]
            tt(S, "dve", PA[:, 0, :], Zr, Hr, ALU.mult, [rz], [rpa] + fx)
            yield
            tt(S, "dve", PQ[:, 0, :], Zr, Hi, ALU.mult, [rz], [rpq] + fx)
            yield
            tt(S, "dve", PA[:, 1, :], Zi, Hi, ALU.mult, [rz], [rpa])
            yield
            tt(S, "dve", PQ[:, 1, :], Zi, Hr, ALU.mult, [rz], [rpq])
            yield
            tt(S, "dve", Y[:, ft, 0, :], PA[:, 0, :], PA[:, 1, :], ALU.subtract, [rpa], [r_Y[ft]])
            yield
            tt(S, "dve", Y[:, ft, 1, :], PQ[:, 0, :], PQ[:, 1, :], ALU.add, [rpq], [r_Y[ft]])
            yield

        for q in range(4):
            sl, sr = self.wload(parts_dft(fv, q), key=("F", l, q))
            s3 = sv8(sl)
            zip_run([ft_chain(q * 2 + fj, fj, s3, sr) for fj in range(2)])
        HB = LP["HB"]
        for q in range(2):
            self.prefetch(("G", l, q), parts_dft(gv, q))
        for q in range(4):
            sl, sr = self.wload(parts_dft(gv, q), key=("G", l, q))
            if q + 2 < 4:
                self.prefetch(("G", l, q + 2), parts_dft(gv, q + 2))
            s3 = sv8(sl)
            ns = slice(q * 256, (q + 1) * 256)
            for ct in range(2):
                pb, pr = self.pbank()
                g = []
                for ft in range(8):
                    g.append((pb[:, 0:256], Y[:, ft, 0, ct * 128:(ct + 1) * 128], s3[:, ft, 0:256], ft == 0, False))
                    g.append((pb[:, 0:256], Y[:, ft, 1, ct * 128:(ct + 1) * 128], s3[:, ft, 256:512], False, ft == 7))
                mm(S, g, [sr] + r_Y, [pr])
                ytb, ryt = (yt2[:, ct, :], r_yt[ct])
                act(S, ytb, pb[:, 0:256], AF.Copy, [pr], [ryt] + ([r_PA[1], r_ZHs[0]] if q == 0 else []))
                S.op("dve", lambda e, ytb=ytb, ct=ct: e.scalar_tensor_tensor(out=ytb, in0=zf[:, ct, ns],
                                                                           scalar=lp[:, l, HB + ct:HB + ct + 1], in1=ytb,
                                                                           op0=ALU.mult, op1=ALU.add),
                     [r_zf[ct], ryt, r_k], [ryt])
                tt(S, "dve", ymix[:, 6 + ct, ns], x0[:, ct, ns], ytb, ALU.mult, [r_x0[ct], ryt], [ymr[6 + ct][q // 2]])


def fm(v):
    return np.ascontiguousarray(np.asarray(v, np.float32).reshape(8, 128).T)


def _consts(kind):
    c = {}
    p = np.arange(128)
    t = np.arange(T)
    ident = np.eye(128, dtype=np.float32)
    c["ident"] = ident
    bo = np.zeros((128, 128), np.float32)
    bo[:64, :64] = 1
    bo[64:, 64:] = 1
    c["blockones"] = bo
    r_, t_ = np.meshgrid(p, p, indexing="ij")
    tri = np.zeros((128, 4, 128), np.float32)
    tri[:, 0, :] = (r_ <= t_)
    tri[:, 1, :] = (r_ >= t_)
    tri[:, 2, :] = np.where(r_ <= t_, 0.0, NEG)
    tri[:, 3, :] = np.where(r_ >= t_, 0.0, NEG)
    c["tri"] = tri
    P = np.zeros((128, 128), np.float32)
    for b in range(0, 128, 32):
        for i in range(16):
            P[b + i + 16, b + i] = -1.0
            P[b + i, b + i + 16] = 1.0
    c["ropeP"] = P
    cs = np.zeros((128, 2, T), np.float32)
    if kind == "s":
        dd = p % 64
        inv = (10000.0 ** (-(dd % 16).astype(np.float32) / np.float32(16))).astype(np.float32)
        row = (t // 64).astype(np.float32)
        col = (t % 64).astype(np.float32)
        pos = np.where((dd // 32)[:, None] == 0, row[None, :], col[None, :]).astype(np.float32)
        ang = (pos * inv[:, None]).astype(np.float32)
        cs[:, 0, :] = np.cos(ang)
        cs[:, 1, :] = np.sin(ang)
    else:
        cs[:, 0, :] = 1.0
    c["ropeCS"] = cs
    sm = np.full((128, 8, 384), NEG, np.float32)
    for j in range(8):
        m = j * 128 + p[:, None]
        q = (j - 1) * 128 + np.arange(384)[None, :]
        inr = (q >= 0) & (q < T)
        if kind == "s":
            ok = (np.abs(m - q) <= 128) & inr
        else:
            ok = ((m // 256) == (q // 256)) & inr
        sm[:, j, :] = np.where(ok, 0.0, NEG)
    c["swamask"] = sm
    gm = np.zeros((5, 2, T), np.float32)
    if kind == "p":
        gm[0, 0, :] = 1.0
        gm[0, 1, :] = -BIG
        for s_ in range(4):
            gm[1 + s_, 0, :] = (t // 256 == s_)
            gm[1 + s_, 1, :] = BIG * (t // 256 == s_)
    c["gmAB"] = gm
    c["cflags"] = np.tile(np.array([[0.0, 1.0, 0.0, 0.0]] if kind == "s" else [[NEG, 0.0, 1.0, 0.0]], np.float32), (128, 1))
    sel = np.zeros((4, 130), np.float32)
    for h in range(4):
        sel[h, 0:128] = ((p >= 64).astype(int) == (h % 2))
        sel[h, 128 + h // 2] = 1.0
    c["sel"] = sel
    L = T if kind == "s" else 256
    rep = T // L
    pos = np.arange(L, dtype=np.float32)
    t01 = pos / np.float32(max(L - 1, 1))
    lin = np.linspace(1e-4, 15.0, 16, dtype=np.float32)
    ang = (np.float32(2.0 * math.pi / L) * pos[:, None] * lin[None, :]).astype(np.float32)
    feats = np.concatenate([t01[:, None], np.cos(ang), -np.sin(ang)], -1).astype(np.float32)
    c["featsT"] = np.ascontiguousarray(np.tile(feats, (rep, 1)).T)
    centre = L // 2
    dist = np.abs(pos - centre) / np.float32(max(centre, 1))
    deltas = np.abs(np.linspace(math.log(0.01) / 1.5, math.log(0.01) / 0.3, 256, dtype=np.float32))
    wnd = np.exp(-dist[:, None] * deltas[None, :]).astype(np.float32)
    c["window"] = np.ascontiguousarray(np.tile(wnd, (rep, 1)).reshape(8, 128, 256).transpose(1, 0, 2))
    N = 2 * L
    tt_ = np.arange(L, dtype=np.float64)
    ff = np.arange(L, dtype=np.float64)
    th = math.pi * (2 * ff + 1) / N
    Fc = np.cos(tt_[:, None] * th[None, :])
    Fs = -np.sin(tt_[:, None] * th[None, :])
    Gc = (2.0 / N) * np.cos(th[:, None] * (tt_[None, :] + L // 2))
    Gs = -(2.0 / N) * np.sin(th[:, None] * (tt_[None, :] + L // 2))
    dF = np.zeros((2, T, T), np.float32)
    dG = np.zeros((2, T, T), np.float32)
    for s_ in range(rep):
        sl = slice(s_ * L, (s_ + 1) * L)
        dF[0, sl, sl] = Fc
        dF[1, sl, sl] = Fs
        dG[0, sl, sl] = Gc
        dG[1, sl, sl] = Gs
    c["dftF"] = dF
    c["dftG"] = dG
    return c


def host_inputs(inp, cores=None):
    f = lambda a: np.ascontiguousarray(np.asarray(a, dtype=np.float32))
    A = {k: np.asarray(v) for k, v in inp.items()}
    shared = {}
    for nm in ("w_ada", "w1_gate", "w1_up", "w1_down", "w_in", "w_out", "w2_gate", "w2_up", "w2_down"):
        shared[nm] = f(A[nm])
    shared["b_adaT"] = f(A["b_ada"].reshape(NL, 72, 128).transpose(0, 2, 1))
    gv = []
    for l in range(NL):
        gv += [fm(A["g_ff1"][l]), fm(A["g_mix"][l]), fm(A["g_ff2"][l])]
    gv.append(fm(A["g_final"]))
    shared["gvec"] = f(np.concatenate(gv, axis=1))
    lp = np.zeros((128, NL, 304), np.float32)
    p = np.arange(128)
    for l in range(NL):
        lp[:, l, 0:16] = A["b_gates"][l][None, :]
        lp[:, l, 16:272] = A["g_mlstm"][l][None, :]
        lp[:, l, 272] = A["g_qnorm"][l][p % 64]
        lp[:, l, 273] = A["g_knorm"][l][p % 64]
        lp[:, l, 274:278] = A["sinks"][l][None, :]
        for i in range(6):
            ch = i * 128 + p
            lp[:, l, 278 + i * 4 + 0] = A["conv_w"][l][0, ch]
            lp[:, l, 278 + i * 4 + 1] = A["conv_w"][l][1, ch]
            lp[:, l, 278 + i * 4 + 2] = A["conv_w"][l][2, ch]
            lp[:, l, 278 + i * 4 + 3] = A["conv_b"][l][ch]
        for ct in range(2):
            lp[:, l, 302 + ct] = A["hyena_bias"][l][ct * 128 + p]
    shared["lp"] = lp
    shared["fw1"] = f(A["filt_w1"].transpose(1, 0, 2))
    shared["fw2"] = f(A["filt_w2"].transpose(1, 0, 2))
    shared["fw3"] = f(A["filt_w3"].transpose(1, 0, 2))
    shared["fvec"] = f(np.stack([A["filt_b1"], A["filt_b2"], A["filt_freq"]], -1).transpose(1, 0, 2))
    shared["fb3"] = f(A["filt_b3"][None, :, :])
    cst = {"s": _consts("s"), "p": _consts("p")}
    maps = []
    xs, xp = A["x_sample"], A["x_prompt"]
    for core in (range(8) if cores is None else cores):
        m = dict(shared)
        kind = "s" if core < 4 else "p"
        m.update(cst[kind])
        C0 = np.zeros((NL, 2, 128, 2, 65), np.float32)
        m0rep = np.zeros((128, NL, 2, 2), np.float32)
        m0c = np.zeros((4, NL * 2), np.float32)
        kT = {n: np.zeros((NL, 2, 128, 256), np.float32) for n in ("gkT", "skT")}
        vp = {n: np.zeros((NL, 2, 128, 2, 192), np.float32) for n in ("gvp", "svp")}
        if core < 4:
            b = core
            m["xT"] = f(xs[b].T)
            m["cvec"] = fm(A["c"][b])
            sC, sn, smm = A["state_mlstm_C"][b], A["state_mlstm_n"][b], A["state_mlstm_m"][b]
            for g_ in range(2):
                for pr_ in range(2):
                    h = 2 * pr_ + g_
                    C0[:, :, g_ * 64:(g_ + 1) * 64, pr_, 0:64] = sC[:, :, h]
                    C0[:, :, g_ * 64:(g_ + 1) * 64, pr_, 64] = sn[:, :, h]
                    m0rep[g_ * 64:(g_ + 1) * 64, :, :, pr_] = smm[None, :, :, h]
            for l in range(NL):
                for dr in range(2):
                    m0c[:, l * 2 + dr] = smm[l, dr, :]
            for (kn, vn, ck_, cv_) in (("gkT", "gvp", "cache_gattn_k", "cache_gattn_v"), ("skT", "svp", "cache_swa_k", "cache_swa_v")):
                ck, cv = A[ck_][b], A[cv_][b]
                t1 = ck.transpose(0, 2, 3, 1).reshape(NL, 128, 256)
                t2 = ck[:, :, ::-1, :].transpose(0, 2, 3, 1).reshape(NL, 128, 256)
                kT[kn][:, 0] = t1
                kT[kn][:, 1] = t2
                vp[vn][:, :, :, :, 64:128] = cv.reshape(NL, 2, 128, 2, 64)
        else:
            j = core - 4
            m["xT"] = f(xp[4 * j:4 * j + 4].reshape(T, D).T)
            m["cvec"] = fm(A["c_ctx"])
        m["C0"], m["m0rep"], m["m0c"] = C0, m0rep, m0c
        m.update(kT)
        m.update(vp)
        maps.append(m)
    return maps


_PROG = None


def get_prog():
    global _PROG
    if _PROG is None:
        _PROG = Prog()
    return _PROG


def run_device(inputs, trace=False, cores=None):
    prog = get_prog()
    maps = host_inputs(inputs, cores)
    maps = [{k: np.ascontiguousarray(v, dtype=np.float32) for k, v in m.items() if k in prog.din} for m in maps]
    for m in maps:
        for k, shp in prog.din.items():
            assert m[k].shape == shp, (k, m[k].shape, shp)
    res = run_bass_kernel_spmd(prog.nc, maps, core_ids=list(range(len(maps))), trace=trace)
    return res


def assemble(res):
    r = res.results
    yp = np.zeros((16, 256, D), np.float32)
    ys = np.zeros((4, T, D), np.float32)
    nC = np.zeros((16, NL, 2, 4, 64, 64), np.float32)
    nn = np.zeros((16, NL, 2, 4, 64), np.float32)
    nm = np.zeros((16, NL, 2, 4), np.float32)
    kv = {n: np.zeros((16, NL, 256, 2, 64), np.float32) for n in ("o_gk", "o_gv", "o_sk", "o_sv")}
    for core in range(8):
        y = np.ascontiguousarray(r[core]["yT"].T)
        if core < 4:
            ys[core] = y
            continue
        j = core - 4
        yp[4 * j:4 * j + 4] = y.reshape(4, 256, D)
        for n in ("o_gk", "o_sk"):
            a = r[core][n].reshape(NL, 2, 64, 4, 256)
            kv[n][4 * j:4 * j + 4] = a.transpose(3, 0, 4, 1, 2)
        for n in ("o_gv", "o_sv"):
            a = r[core][n].reshape(NL, 4, 256, 2, 64)
            kv[n][4 * j:4 * j + 4] = a.transpose(1, 0, 2, 3, 4)
        oc = r[core]["o_C"].reshape(NL, 2, 4, 2, 64, 2, 65)
        oc = oc.transpose(2, 0, 1, 5, 3, 4, 6).reshape(4, NL, 2, 4, 64, 65)
        nC[4 * j:4 * j + 4] = oc[..., 0:64]
        nn[4 * j:4 * j + 4] = oc[..., 64]
        nm[4 * j:4 * j + 4] = r[core]["o_m"].transpose(3, 0, 1, 2)
    return (yp, ys, nC, nn, nm, kv["o_gk"], kv["o_gv"], kv["o_sk"], kv["o_sv"])


def kernel(**inputs):
    res = run_device(inputs)
    return assemble(res)
```

```python
import os
import math
import numpy as np
import concourse.bass as bass
import concourse.mybir as mybir
from concourse.bass_utils import run_bass_kernel_spmd

F32 = mybir.dt.float32
BF16 = mybir.dt.bfloat16
I32 = mybir.dt.int32
AF = mybir.ActivationFunctionType
ALU = mybir.AluOpType
AX = mybir.AxisListType

D = 1024
T = 1024
DFF = 2816
NFF = 22
NL = 2
NIN = 2832
EPS = 1e-6
NEG = -30000.0
BIG = 29952.0
LN8 = math.log(0.125)
SLOT = 8 * 640
STAGE = int(os.environ.get("MK_STAGE", "99"))
DEBUG = bool(os.environ.get("MK_DEBUG"))
CUT = int(os.environ.get("MK_CUT", "99"))


class R:
    __slots__ = ("name", "w", "rd", "excl")

    def __init__(self, name="", excl=False):
        self.name = name
        self.w = None
        self.rd = []
        self.excl = excl


def RL(n, name=""):
    return [R("%s%d" % (name, i)) for i in range(n)]


class Sched:
    NLANES = {"sp": 8, "pool": 3, "poolw": 5}

    def __init__(self, nc):
        self.nc = nc
        self.eng = {"pe": nc.tensor, "act": nc.scalar, "dve": nc.vector,
                    "pool": nc.gpsimd, "sp": nc.sync}
        self.sem = {}
        self.cnt = {}
        for k in self.eng:
            self.sem[k] = nc.alloc_semaphore(name="s_" + k)
            self.cnt[k] = 0
        self.lanes = {}
        for q, n in self.NLANES.items():
            self.lanes[q] = []
            for i in range(n):
                key = "d_%s%d" % (q, i)
                self.sem[key] = nc.alloc_semaphore(name=key)
                self.cnt[key] = 0
                self.lanes[q].append(key)
        self.lane_rr = {q: 0 for q in self.NLANES}
        self.eng_of = {"sp": "sp", "pool": "pool", "poolw": "pool"}
        self.waited = {k: {} for k in self.eng}
        self.nwaits = 0
        self.nops = 0

    def _wait(self, e, key, val):
        if key == "pe" and e == "pe":
            return
        w = self.waited[e]
        if w.get(key, 0) >= val:
            return
        self.eng[e].wait_ge(self.sem[key], val)
        w[key] = val
        self.nwaits += 1

    def _deps(self, e, reads, writes):
        deps = {}
        for r in reads:
            if r.w is not None:
                k, v = r.w
                if deps.get(k, 0) < v:
                    deps[k] = v
            if r.excl:
                for (k, v) in r.rd:
                    if k != e and deps.get(k, 0) < v:
                        deps[k] = v
        for w in writes:
            if w.w is not None:
                k, v = w.w
                if deps.get(k, 0) < v:
                    deps[k] = v
            for (k, v) in w.rd:
                if deps.get(k, 0) < v:
                    deps[k] = v
        for k, v in deps.items():
            self._wait(e, k, v)

    def _commit(self, tok, reads, writes):
        for r in reads:
            r.rd.append(tok)
            if len(r.rd) > 48:
                mx = {}
                for k, v in r.rd:
                    if mx.get(k, 0) < v:
                        mx[k] = v
                r.rd = list(mx.items())
        for w in writes:
            w.w = tok
            w.rd = []

    def op(self, e, fn, reads=(), writes=()):
        self._deps(e, reads, writes)
        inst = fn(self.eng[e])
        self.cnt[e] += 1
        inst.then_inc(self.sem[e], 1)
        self._commit((e, self.cnt[e]), reads, writes)
        self.nops += 1

    def dma(self, q, out, in_, reads=(), writes=()):
        lanes = self.lanes[q]
        e = self.eng_of[q]
        key = lanes[self.lane_rr[q] % len(lanes)]
        self.lane_rr[q] += 1
        self._wait(e, key, self.cnt[key])
        self._deps(e, reads, writes)
        inst = self.eng[e].dma_start(out=out, in_=in_)
        self.cnt[key] += 16
        inst.then_inc(self.sem[key], 16)
        self._commit((key, self.cnt[key]), reads, writes)
        self.nops += 1

    def barrier(self):
        for e in self.eng:
            for k in self.sem:
                if self.cnt[k] > 0 and not k.startswith("d_poolw"):
                    self._wait(e, k, self.cnt[k])

    def finish(self):
        for k in self.sem:
            if self.cnt[k] > 0 and k != "sp":
                self._wait("sp", k, self.cnt[k])


def act(S, out, in_, func, reads, writes, bias=0.0, scale=1.0):
    S.op("act", lambda e: e.activation(out=out, in_=in_, func=func, bias=bias, scale=scale), reads, writes)


def tt(S, eng, out, a, b, op, reads, writes):
    S.op(eng, lambda e: e.tensor_tensor(out=out, in0=a, in1=b, op=op), reads, writes)


def ts(S, eng, out, a, s1, op0, reads, writes, s2=None, op1=None):
    if op1 is None:
        S.op(eng, lambda e: e.tensor_scalar(out=out, in0=a, scalar1=s1, scalar2=None, op0=op0), reads, writes)
    else:
        S.op(eng, lambda e: e.tensor_scalar(out=out, in0=a, scalar1=s1, scalar2=s2, op0=op0, op1=op1), reads, writes)


def cp(S, eng, out, in_, reads, writes):
    S.op(eng, lambda e: e.tensor_copy(out=out, in_=in_), reads, writes)


def recip(S, out, in_, reads, writes):
    S.op("dve", lambda e: e.reciprocal(out=out, in_=in_), reads, writes)


def mm(S, groups, reads, writes):
    def fn(e):
        inst = None
        for (o, l, r, st, sp_) in groups:
            inst = e.matmul(o, lhsT=l, rhs=r, start=st, stop=sp_)
        return inst
    S.op("pe", fn, reads, writes)


class Prog:
    def __init__(self):
        nc = bass.Bass("TRN2", target_bir_lowering=False)
        self.nc = nc
        self.S = Sched(nc)
        self.din = {}
        self.dout = {}
        self._build()

    def inp(self, name, shape):
        t = self.nc.dram_tensor(name, list(shape), F32, kind="ExternalInput").ap()
        self.din[name] = tuple(shape)
        return t

    def outp(self, name, shape):
        t = self.nc.dram_tensor(name, list(shape), F32, kind="ExternalOutput").ap()
        self.dout[name] = tuple(shape)
        return t

    def sb(self, name, shape, dt=F32):
        return self.nc.alloc_sbuf_tensor("sb_" + name, list(shape), dt)

    def arena_reset(self, to=0):
        self.aoff = to
        self.S.barrier()

    def carve(self, shape, dt=F32):
        n = 1
        for s in shape[1:]:
            n *= s
        nb = n * (4 if dt in (F32, I32) else 2)
        nb = (nb + 31) // 32 * 32
        off = self.aoff
        self.aoff += nb
        assert self.aoff <= self.ARENA_BYTES, (self.aoff, self.ARENA_BYTES)
        v = self.arena[:, off // 2:(off + nb) // 2]
        if dt != BF16:
            v = v.bitcast(dt)
        v = v[:, 0:n]
        if len(shape) == 3:
            v = v.rearrange("p (a b) -> p a b", a=shape[1])
        elif len(shape) == 4:
            v = v.rearrange("p (a b c) -> p a b c", a=shape[1], b=shape[2])
        return v

    def bank(self):
        return self.pbank()

    def prefetch(self, key, parts):
        self.pre[key] = self.wload(parts)

    def wload(self, parts, key=None):
        if key is not None and key in self.pre:
            return self.pre.pop(key)
        i = self.slot_rr % len(self.slots)
        self.slot_rr += 1
        sl, r = self.slots[i], self.slotr[i]
        for (dst_fn, src) in parts:
            self.S.dma("poolw", dst_fn(sl), src, writes=[r])
        return sl, r

    def slot_view(self, sl, kk):
        return sl[:, 0:kk * self.slot_w].rearrange("p (k n) -> p k n", k=kk)

    def _build(self):
        nc, S = self.nc, self.S
        inp, outp, sb = self.inp, self.outp, self.sb
        xT_d = inp("xT", [D, T])
        cvec_d = inp("cvec", [128, 8])
        w_ada = inp("w_ada", [NL, D, 9 * D])
        b_adaT = inp("b_adaT", [NL, 128, 72])
        gvec_d = inp("gvec", [128, NL * 24 + 8])
        wd = {}
        for nm, shp in (("w1_gate", [NL, D, DFF]), ("w1_up", [NL, D, DFF]), ("w1_down", [NL, DFF, D]),
                        ("w_in", [NL, D, NIN]), ("w_out", [NL, D, D]),
                        ("w2_gate", [NL, D, DFF]), ("w2_up", [NL, D, DFF]), ("w2_down", [NL, DFF, D])):
            wd[nm] = inp(nm, shp)
        yT_d = outp("yT", [D, T])

        xT = sb("xT", [128, 8, T], F32)
        hT = sb("hT", [128, 8, T], BF16)
        self.ARENA_BYTES = 84 * 1024
        self.arena = sb("arena", [128, self.ARENA_BYTES // 2], BF16)
        self.slot_w = 640
        self.slots = [sb("slot%d" % i, [128, SLOT], BF16) for i in range(4)]
        self.slotr = RL(4, "slot")
        self.slot_rr = 0
        self.pre = {}
        self.ps = [nc.alloc_psum_tensor("ps%d" % i, [128, 512], F32) for i in range(8)]
        self.psr = [R("ps%d" % i, excl=True) for i in range(8)]
        self.bank_rr = 0
        self.bank_pool = list(range(8))
        ones_bf = sb("ones_bf", [128, 128], BF16)
        cvec = sb("cvec_sb", [128, 8], F32)
        sc_bf = sb("sc_bf", [128, 8], BF16)
        modT = sb("modT", [128, NL, 72], F32)
        badaT = sb("badaT", [128, NL, 72], F32)
        gvec = sb("gvec_sb", [128, NL * 24 + 8], F32)
        Acoef = sb("Acoef", [128, NL, 3, 8], F32)
        Gcoef = sb("Gcoef", [128, NL, 3, 8], F32)
        sqb = sb("sqb", [128, 2, 512], BF16)
        f32s = sb("f32s", [128, 3, 512], F32)
        rstd = sb("rstd", [128, 512], F32)
        r_ones, r_cvec, r_sc, r_gvec, r_rstd = R("ones"), R("cvec"), R("sc"), R("gvec"), R("rstd")
        r_mod = RL(NL, "mod")
        r_bada = R("bada")
        r_coef = RL(NL, "coef")
        r_sq = RL(2, "sq")
        r_f32s = RL(3, "f32s")
        xr = [[R("x%d_%d" % (k, c)) for c in range(2)] for k in range(8)]
        hr = [[R("h%d_%d" % (k, c)) for c in range(2)] for k in range(8)]
        self.sq_rr = 0
        self.f32_rr = 0

        def CH(c):
            return slice(c * 512, (c + 1) * 512)

        xv = xT_d.rearrange("(k p) t -> p k t", p=128)
        for k in range(8):
            S.dma("sp", xT[:, k, :], xv[:, k, :], writes=[xr[k][0], xr[k][1]])
        S.dma("sp", cvec[:], cvec_d, writes=[r_cvec])
        S.dma("sp", gvec[:], gvec_d, writes=[r_gvec])
        for l in range(NL):
            S.dma("sp", badaT[:, l, :], b_adaT[l], writes=[r_bada])
        S.op("dve", lambda e: e.memset(ones_bf[:], 1.0), writes=[r_ones])
        act(S, sc_bf[:], cvec[:], AF.Silu, [r_cvec], [r_sc])

        mslots = [sb("mslot%d" % i, [128, 8, 256], BF16) for i in range(2)]
        r_ms = RL(2, "mslot")
        r_modls = [[R("mod%d_%d" % (l, s_)) for s_ in range(3)] for l in range(NL)]
        jobs = [(l, q) for l in range(NL) for q in range(36)]
        st = {"dma": 0, "mm": 0, "fin": set()}

        def mod_dma(j):
            l, q = jobs[j]
            wv = w_ada[l].rearrange("(k p) n -> p k n", p=128)
            S.dma("poolw", mslots[j % 2][:], wv[:, :, q * 256:(q + 1) * 256], writes=[r_ms[j % 2]])

        def mod_mm(j, bank=None):
            l, q = jobs[j]
            if bank is None:
                pb, pr = self.bank()
                c0 = 0
            else:
                pb, pr, c0 = bank
            groups = []
            for jj in range(2):
                for k in range(8):
                    groups.append((pb[:, c0 + jj:c0 + jj + 1], mslots[j % 2][:, k, jj * 128:(jj + 1) * 128], sc_bf[:, k:k + 1],
                                   k == 0, k == 7))
            mm(S, groups, [r_ms[j % 2], r_sc], [pr])
            s_ = q // 12
            tt(S, "dve", modT[:, l, q * 2:q * 2 + 2], pb[:, c0:c0 + 2], badaT[:, l, q * 2:q * 2 + 2], ALU.add,
               [pr, r_bada], [r_modls[l][s_]])

        def mod_pump(n=1, bank=None):
            for _ in range(n):
                if st["dma"] < len(jobs) and st["dma"] - st["mm"] < 2:
                    mod_dma(st["dma"])
                    st["dma"] += 1
                if st["mm"] < st["dma"] - 1 or (st["dma"] == len(jobs) and st["mm"] < st["dma"]):
                    mod_mm(st["mm"], bank)
                    st["mm"] += 1

        def mod_require(l, s_):
            last = l * 36 + s_ * 12 + 11
            while st["mm"] <= last:
                mod_pump()
            if (l, s_) in st["fin"]:
                return
            st["fin"].add((l, s_))
            r = r_modls[l][s_]
            ts(S, "dve", Acoef[:, l, s_, :], modT[:, l, (3 * s_ + 1) * 8:(3 * s_ + 2) * 8], 1.0, ALU.add, [r], [r])
            tt(S, "dve", Acoef[:, l, s_, :], Acoef[:, l, s_, :], gvec[:, l * 24 + s_ * 8:l * 24 + s_ * 8 + 8],
               ALU.mult, [r, r_gvec], [r])
            ts(S, "dve", Gcoef[:, l, s_, :], modT[:, l, (3 * s_ + 2) * 8:(3 * s_ + 3) * 8],
               0.5 if s_ != 1 else 1.0, ALU.mult, [r], [r])

        self.mod_pump = mod_pump
        self.mod_require = mod_require

        def rms_rstd(c, src_fn, src_regs, nk, inv_n, ones_l):
            pb, pr = self.bank()
            for k in range(nk):
                i = self.sq_rr % 2
                self.sq_rr += 1
                act(S, sqb[:, i, :], src_fn(k), AF.Square, [src_regs[k]], [r_sq[i]])
                mm(S, [(pb[:], ones_l, sqb[:, i, :], k == 0, k == nk - 1)], [r_sq[i], r_ones], [pr])
            act(S, rstd[:], pb[:], AF.Ln, [pr], [r_rstd], bias=EPS, scale=inv_n)
            act(S, rstd[:], rstd[:], AF.Exp, [r_rstd], [r_rstd], scale=-0.5)

        def norm_mod(l, s):
            for c in range(2):
                rms_rstd(c, lambda k: xT[:, k, CH(c)], [xr[k][c] for k in range(8)], 8, 1.0 / D, ones_bf[:])
                for k in range(8):
                    i = self.f32_rr % 3
                    self.f32_rr += 1
                    tt(S, "dve", f32s[:, i, :], xT[:, k, CH(c)], rstd[:], ALU.mult,
                       [xr[k][c], r_rstd], [r_f32s[i]])
                    act(S, hT[:, k, CH(c)], f32s[:, i, :], AF.Identity, [r_f32s[i], r_modls[l][s]],
                        [hr[k][c]], bias=modT[:, l, 3 * s * 8 + k:3 * s * 8 + k + 1],
                        scale=Acoef[:, l, s, k:k + 1])

        def resid_add(l, s, dt_, c, pb, pr):
            i = self.f32_rr % 3
            self.f32_rr += 1
            act(S, f32s[:, i, :], pb[:], AF.Copy, [pr, r_modls[l][s]], [r_f32s[i]], scale=Gcoef[:, l, s, dt_:dt_ + 1])
            tt(S, "dve", xT[:, dt_, CH(c)], xT[:, dt_, CH(c)], f32s[:, i, :], ALU.add,
               [xr[dt_][c], r_f32s[i]], [xr[dt_][c]])

        def ffn(l, s, wg, wu, wdn):
            self.arena_reset()
            aT = self.carve([128, NFF, T], BF16)
            ar = [[R("a%d_%d" % (j, c)) for c in range(2)] for j in range(NFF)]
            sg = [self.carve([128, 512], F32) for _ in range(3)]
            r_sg = RL(3, "sg")
            sg_rr = 0
            mod_require(l, s)
            norm_mod(l, s)
            wgv = wg[l].rearrange("(k p) n -> p k n", p=128)
            wuv = wu[l].rearrange("(k p) n -> p k n", p=128)
            for g in range(NFF // 2):
                c0 = g * 256
                sl, sr = self.wload(self.parts_gu(wg, wu, l, g), key=("gu", l, s, g))
                s3 = self.slot_view(sl, 8)
                if l == 0:
                    mod_pump(1)
                for jj in range(2):
                    j = g * 2 + jj
                    for c in range(2):
                        pg, prg = self.bank()
                        pu, pru = self.bank()
                        groups = []
                        for k in range(8):
                            groups.append((pg[:], s3[:, k, jj * 128:(jj + 1) * 128], hT[:, k, CH(c)], k == 0, k == 7))
                        for k in range(8):
                            groups.append((pu[:], s3[:, k, 256 + jj * 128:256 + (jj + 1) * 128], hT[:, k, CH(c)],
                                           k == 0, k == 7))
                        mm(S, groups, [sr] + [hr[k][c] for k in range(8)], [prg, pru])
                        i = sg_rr % 3
                        sg_rr += 1
                        act(S, sg[i], pg[:], AF.Silu, [prg], [r_sg[i]])
                        tt(S, "dve", aT[:, j, CH(c)], sg[i], pu[:], ALU.mult, [r_sg[i], pru], [ar[j][c]])
            wdv = wdn[l].rearrange("(j p) n -> p j n", p=128)
            for dt_ in range(8):
                sl, sr = self.wload([(lambda sl_: sl_[:, 0:NFF * 128].rearrange("p (j n) -> p j n", j=NFF),
                                      wdv[:, :, dt_ * 128:(dt_ + 1) * 128])])
                if dt_ == 7:
                    if s == 0:
                        self.prefetch(("A1", l), self.parts_win(l, 0, 512))
                        self.prefetch(("A2", l), self.parts_win(l, 512, 512))
                        self.prefetch(("G", l), self.parts_win(l, 1024, 16))
                    elif l + 1 < NL:
                        for g_ in range(2):
                            self.prefetch(("gu", l + 1, 0, g_), self.parts_gu(wd["w1_gate"], wd["w1_up"], l + 1, g_))
                s3 = sl[:, 0:NFF * 128].rearrange("p (j n) -> p j n", j=NFF)
                if l == 0:
                    mod_pump(1)
                for c in range(2):
                    pb, pr = self.bank()
                    groups = [(pb[:], s3[:, j, :], aT[:, j, CH(c)], j == 0, j == NFF - 1) for j in range(NFF)]
                    mm(S, groups, [sr] + [ar[j][c] for j in range(NFF)], [pr])
                    resid_add(l, s, dt_, c, pb, pr)

        self.wd = wd
        self.ctx = dict(xT=xT, hT=hT, xr=xr, hr=hr, CH=CH, rms_rstd=rms_rstd, norm_mod=norm_mod,
                        resid_add=resid_add, ones_bf=ones_bf, r_ones=r_ones, gvec=gvec, r_gvec=r_gvec,
                        f32s=f32s, r_f32s=r_f32s, rstd=rstd, r_rstd=r_rstd, modT=modT, sqb=sqb, r_sq=r_sq)


        self.ARENA_BYTES = 84 * 1024
        LPW = 304
        self.LP = dict(BG=0, GM=16, GQK=272, SK=274, CV=278, HB=302)
        d = {}
        d["lp"] = inp("lp", [128, NL, LPW])
        d["cflags"] = inp("cflags", [128, 4])
        d["ident"] = inp("ident", [128, 128])
        d["blockones"] = inp("blockones", [128, 128])
        d["tri"] = inp("tri", [128, 4, 128])
        d["ropeP"] = inp("ropeP", [128, 128])
        d["ropeCS"] = inp("ropeCS", [128, 2, T])
        d["swamask"] = inp("swamask", [128, 8, 384])
        d["gmAB"] = inp("gmAB", [5, 2, T])
        d["fw1"] = inp("fw1", [33, NL, 64])
        d["fw2"] = inp("fw2", [64, NL, 64])
        d["fw3"] = inp("fw3", [64, NL, 256])
        d["fvec"] = inp("fvec", [64, NL, 3])
        d["fb3"] = inp("fb3", [1, NL, 256])
        d["featsT"] = inp("featsT", [33, T])
        d["window"] = inp("window", [128, 8, 256])
        d["dftF"] = inp("dftF", [2, T, T])
        d["dftG"] = inp("dftG", [2, T, T])
        d["sel"] = inp("sel", [4, 130])
        d["C0"] = inp("C0", [NL, 2, 128, 2, 65])
        d["m0rep"] = inp("m0rep", [128, NL, 2, 2])
        d["m0c"] = inp("m0c", [4, NL * 2])
        d["gkT"] = inp("gkT", [NL, 2, 128, 256])
        d["gvp"] = inp("gvp", [NL, 2, 128, 2, 192])
        d["skT"] = inp("skT", [NL, 2, 128, 256])
        d["svp"] = inp("svp", [NL, 2, 128, 2, 192])
        d["o_gk"] = outp("o_gk", [NL, 128, T])
        d["o_gv"] = outp("o_gv", [NL, T, 128])
        d["o_sk"] = outp("o_sk", [NL, 128, T])
        d["o_sv"] = outp("o_sv", [NL, T, 128])
        d["o_C"] = outp("o_C", [NL, 2, 4, 128, 2, 65])
        d["o_m"] = outp("o_m", [NL, 2, 4, 4])
        self.d = d
        if DEBUG:
            self.dbg_ymix = outp("dbg_ymix", [NL, 128, 8, T])
        k = {}
        k["lp"] = sb("lp", [128, NL, LPW]); k["cflags"] = sb("cflags", [128, 4])
        k["ident_f"] = sb("ident_f", [128, 128]); k["ident_bf"] = sb("ident_bf", [128, 128], BF16)
        k["blockones"] = sb("blockones", [128, 128], BF16)
        k["tri"] = sb("tri", [128, 4, 128]); k["ropeP"] = sb("ropeP", [128, 128])
        k["ones_f"] = sb("ones_f", [128, 128])
        k["fw1"] = sb("fw1", [33, NL, 64]); k["fw2"] = sb("fw2", [64, NL, 64]); k["fw3"] = sb("fw3", [64, NL, 256])
        k["fvec"] = sb("fvec", [64, NL, 3]); k["fb3"] = sb("fb3", [1, NL, 256]); k["fs"] = sb("fs", [64, NL, 4])
        k["sel"] = sb("sel", [4, 130]); k["m0rep"] = sb("m0rep", [128, NL, 2, 2]); k["m0c"] = sb("m0c", [4, NL * 2])
        k["Clo"] = sb("Clo", [128, 2, 2, 65]); k["Chi"] = sb("Chi", [128, 2, 2, 65])
        k["Cblo"] = sb("Cblo", [128, 2, 2, 66], BF16); k["Cbhi"] = sb("Cbhi", [128, 2, 2, 66], BF16)
        k["cvb"] = sb("cvb", [128, NL, 6, 2])
        self.k = k
        r_k = R("consts")
        self.r_k = r_k
        for nm in ("lp", "cflags", "tri", "ropeP", "fw1", "fw2", "fw3", "fvec", "fb3", "sel", "m0rep", "m0c"):
            S.dma("sp", k[nm][:], d[nm], writes=[r_k])
        S.dma("sp", k["ident_f"][:], d["ident"], writes=[r_k])
        S.dma("pool", k["ident_bf"][:], d["ident"], writes=[r_k])
        S.dma("pool", k["blockones"][:], d["blockones"], writes=[r_k])
        S.op("pool", lambda e: e.memset(k["ones_f"][:], 1.0), writes=[r_k])
        for nm in ("Clo", "Chi", "Cblo", "Cbhi"):
            S.op("pool", lambda e, nm=nm: e.memset(k[nm][:], 0.0), writes=[r_k])
        i2p = float(1.0 / (2 * math.pi))
        ts(S, "dve", k["fs"][:, :, 0:1], k["fvec"][:, :, 2:3], i2p, ALU.mult, [r_k], [r_k])
        tt(S, "dve", k["fs"][:, :, 1:2], k["fs"][:, :, 0:1], k["fvec"][:, :, 0:1], ALU.mult, [r_k], [r_k])
        tt(S, "dve", k["fs"][:, :, 2:3], k["fs"][:, :, 0:1], k["fvec"][:, :, 1:2], ALU.mult, [r_k], [r_k])
        CV = self.LP["CV"]
        for l in range(NL):
            cvv = k["lp"][:, l, CV:CV + 24].rearrange("p (a b) -> p a b", a=6)
            for (j, col) in ((0, 0), (1, 2)):
                ts(S, "dve", k["cvb"][:, l, :, j:j + 1], cvv[:, :, col:col + 1], k["cflags"][:, 2:3], ALU.mult,
                   [r_k], [r_k], s2=-1.0, op1=ALU.mult)

        for l in range(NL):
            ffn(l, 0, wd["w1_gate"], wd["w1_up"], wd["w1_down"])
            self.mixer(l)
            ffn(l, 2, wd["w2_gate"], wd["w2_up"], wd["w2_down"])

        gfo = NL * 24
        yv = yT_d.rearrange("(k p) t -> p k t", p=128)
        self.arena_reset()
        ost_ = self.carve([128, 2, 512], F32)
        ost = [ost_[:, 0, :], ost_[:, 1, :]]
        r_ost = RL(2, "ost")
        o_rr = 0
        for c in range(2):
            rms_rstd(c, lambda k: xT[:, k, CH(c)], [xr[k][c] for k in range(8)], 8, 1.0 / D, ones_bf[:])
            for k in range(8):
                i = self.f32_rr % 3
                self.f32_rr += 1
                tt(S, "dve", f32s[:, i, :], xT[:, k, CH(c)], rstd[:], ALU.mult, [xr[k][c], r_rstd], [r_f32s[i]])
                o = o_rr % 2
                o_rr += 1
                act(S, ost[o], f32s[:, i, :], AF.Copy, [r_f32s[i], r_gvec], [r_ost[o]],
                    scale=gvec[:, gfo + k:gfo + k + 1])
                S.dma("sp", yv[:, k, CH(c)], ost[o], reads=[r_ost[o]])
        S.finish()

    def mixer(self, l):
        S = self.S
        C = self.ctx
        hT, hr, CH = C["hT"], C["hr"], C["CH"]
        self.arena_reset()
        ymix = self.carve([128, 8, T], BF16)
        ymr = [[R("ym%d_%d" % (k, c)) for c in range(2)] for k in range(8)]
        base = self.aoff
        self.mod_require(l, 1)
        C["norm_mod"](l, 1)
        win = self.wd["w_in"][l].rearrange("(k p) n -> p k n", p=128)
        self.bank_pool = list(range(8))
        self.mix_mlstm(l, ymix, ymr, win)
        self.arena_reset(base)
        if STAGE >= 3:
            self.mix_attn(l, ymix, ymr, win, glob=True)
            self.arena_reset(base)
            self.mix_attn(l, ymix, ymr, win, glob=False)
            self.arena_reset(base)
        if STAGE >= 4:
            self.mix_hyena(l, ymix, ymr, win)
        self.prefetch(("wout", l, 0), self.parts_wout(l, 0))
        self.prefetch(("wout", l, 1), self.parts_wout(l, 1))
        wd_ = self.wd
        for g in range(2):
            self.prefetch(("gu", l, 2, g), self.parts_gu(wd_["w2_gate"], wd_["w2_up"], l, g))
        self.bank_pool = list(range(8))
        if DEBUG:
            f32s, r_f32s = C["f32s"], C["r_f32s"]
            for k in range(8):
                for c in range(2):
                    i = self.f32_rr % 3
                    self.f32_rr += 1
                    act(S, f32s[:, i, :], ymix[:, k, CH(c)], AF.Copy, [ymr[k][c]], [r_f32s[i]])
                    S.dma("sp", self.dbg_ymix[l, :, k, c * 512:(c + 1) * 512], f32s[:, i, :], reads=[r_f32s[i]])
        wov = self.wd["w_out"][l].rearrange("(k p) n -> p k n", p=128)
        for half in range(2):
            sl, sr = self.wload(self.parts_wout(l, half), key=("wout", l, half))
            s3 = self.slot_view(sl, 8)
            for j in range(4):
                dt_ = half * 4 + j
                for c in range(2):
                    pb, pr = self.bank()
                    groups = [(pb[:], s3[:, k, j * 128:(j + 1) * 128], ymix[:, k, CH(c)], k == 0, k == 7) for k in range(8)]
                    mm(S, groups, [sr] + [ymr[k][c] for k in range(8)], [pr])
                    C["resid_add"](l, 1, dt_, c, pb, pr)

    def parts_attn(self, l, glob):
        win = self.wd["w_in"][l].rearrange("(k p) n -> p k n", p=128)
        sv8 = lambda sl_: self.slot_view(sl_, 8)
        q0 = 1040 if glob else 1552
        k0, v0 = q0 + 256, q0 + 384
        return [(lambda sl_: sv8(sl_)[:, :, 0:256], win[:, :, q0:q0 + 256]),
                (lambda sl_: sv8(sl_)[:, :, 256:384], win[:, :, k0:k0 + 128]),
                (lambda sl_: sv8(sl_)[:, :, 384:448], win[:, :, k0 + 64:k0 + 128]),
                (lambda sl_: sv8(sl_)[:, :, 448:512], win[:, :, k0:k0 + 64]),
                (lambda sl_: sv8(sl_)[:, :, 512:640], win[:, :, v0:v0 + 128])]

    def parts_win(self, l, c0, n):
        win = self.wd["w_in"][l].rearrange("(k p) n -> p k n", p=128)
        return [(lambda sl_: self.slot_view(sl_, 8)[:, :, 0:n], win[:, :, c0:c0 + n])]

    def parts_wout(self, l, half):
        wov = self.wd["w_out"][l].rearrange("(k p) n -> p k n", p=128)
        return [(lambda sl_: self.slot_view(sl_, 8)[:, :, 0:512], wov[:, :, half * 512:(half + 1) * 512])]

    def parts_gu(self, wg, wu, l, g):
        wgv = wg[l].rearrange("(k p) n -> p k n", p=128)
        wuv = wu[l].rearrange("(k p) n -> p k n", p=128)
        c0 = g * 256
        return [(lambda sl_: self.slot_view(sl_, 8)[:, :, 0:256], wgv[:, :, c0:c0 + 256]),
                (lambda sl_: self.slot_view(sl_, 8)[:, :, 256:512], wuv[:, :, c0:c0 + 256])]

    def pbank(self):
        i = self.bank_pool[self.bank_rr % len(self.bank_pool)]
        self.bank_rr += 1
        return self.ps[i], self.psr[i]

    def proj_fm(self, c, s3, col0, pb):
        hT, CH = self.ctx["hT"], self.ctx["CH"]
        return [(pb[:], s3[:, k, col0:col0 + 128], hT[:, k, CH(c)], k == 0, k == 7) for k in range(8)]

    def proj_tok(self, tt_, s3, col0, ncols, pb, pc0):
        hT = self.ctx["hT"]
        return [(pb[:, pc0:pc0 + ncols], hT[:, k, tt_ * 128:(tt_ + 1) * 128], s3[:, k, col0:col0 + ncols], k == 0, k == 7)
                for k in range(8)]

    def mix_mlstm(self, l, ymix, ymr, win):
        S, k_, d = self.S, self.k, self.d
        C = self.ctx
        hr, CH = C["hr"], C["CH"]
        LP = self.LP
        lp = k_["lp"]
        r_k = self.r_k
        hall = [hr[k][c] for k in range(8) for c in range(2)]
        sv8 = lambda sl_: self.slot_view(sl_, 8)
        slA1, srA1 = self.wload(self.parts_win(l, 0, 512), key=("A1", l))
        slA2, srA2 = self.wload(self.parts_win(l, 512, 512), key=("A2", l))
        slG, srG = self.wload(self.parts_win(l, 1024, 16), key=("G", l))
        sA1, sA2, sG = sv8(slA1), sv8(slA2), sv8(slG)
        self.prefetch(("attn", l, True), self.parts_attn(l, True))
        if CUT == 1:
            return
        aqT = self.carve([128, 2, T], BF16)
        akp = self.carve([128, 4, T], BF16)
        ktok = self.carve([128, 8, 256], BF16)
        vaug = self.carve([128, 8, 4, 66], BF16)
        sgo = self.carve([128, 8, 256], BF16)
        gts = self.carve([128, 8, 16], F32)
        lf = self.carve([128, 8, 8], F32)
        cum = self.carve([128, 8, 16], F32)
        call = self.carve([128, 8, 8], F32)
        wall = self.carve([128, 8, 8], F32)
        wkl = self.carve([128, 8, 8], F32)
        wkall = self.carve([128, 8, 8], F32)
        wkm = self.carve([128, 8, 8], F32)
        dec = self.carve([128, 8, 4], F32)
        lfB = self.carve([128, 8, 128], F32)
        E = self.carve([128, 8, 128], F32)
        AT = self.carve([128, 2, 4, 128], BF16)
        hf = self.carve([128, 8, 256], F32)
        numt = self.carve([128, 2, 260], F32)
        h64 = self.carve([128, 2, 256], F32)
        dsm = self.carve([128, 2, 8], F32)
        kwp = self.carve([128, 2, 4, 192], BF16)
        yatok = self.carve([128, 8, 256], BF16)
        snap = self.carve([128, 2, 4, 130], F32)
        e0 = self.carve([128, 2, 2], F32)
        ssall = self.carve([128, 8, 4], F32)
        sq1 = self.carve([128, 2, 256], F32)
        mst = self.carve([128, 64], F32)
        scl = self.carve([128, 2, 8], F32)
        r_aq, r_akp = RL(2, "aq"), RL(2, "akp")
        r_ktok, r_vaug, r_sgo, r_gts = RL(8, "ktok"), RL(8, "vaug"), RL(8, "sgo"), R("gts")
        r_gate = R("gate")
        r_lfB, r_E, r_AT = RL(2, "lfB"), RL(2, "E"), RL(2, "AT")
        r_hf = RL(8, "hf")
        r_num, r_tmpn, r_h64, r_dsm, r_kwp = RL(2, "num"), RL(2, "tmpn"), RL(2, "h64"), RL(2, "dsm"), RL(2, "kwp")
        r_C, r_Cb = RL(2, "C"), RL(2, "Cb")
        r_snap = [[R("snap") for _ in range(4)] for _ in range(2)]
        r_ms, r_scl = R("mst"), R("scl")
        r_ya = RL(8, "ya")
        r_ss = R("ss")
        r_sq1 = RL(2, "sq1")
        S.op("pool", lambda e: e.memset(akp, 0.0), writes=r_akp)
        S.op("pool", lambda e: e.memset(kwp, 0.0), writes=r_kwp)
        S.op("pool", lambda e: e.memset(vaug[:, :, :, 64:65], 1.0), writes=r_vaug)
        if CUT == 2:
            return
        for t2 in range(2):
            for c in range(2):
                pb, pr = self.pbank()
                mm(S, self.proj_fm(c, sA1, t2 * 128, pb), [srA1] + hall, [pr])
                act(S, aqT[:, t2, CH(c)], pb[:], AF.Copy, [pr], [r_aq[c]])
        if CUT == 21:
            return
        for t2 in range(2):
            for c in range(2):
                pb, pr = self.pbank()
                mm(S, self.proj_fm(c, sA1, 256 + t2 * 128, pb), [srA1] + hall, [pr])
                act(S, akp[0:64, 2 * t2, CH(c)], pb[0:64, :], AF.Copy, [pr], [r_akp[c]])
                cp(S, "dve", akp[64:128, 2 * t2 + 1, CH(c)], pb[64:128, :], [pr], [r_akp[c]])
        if CUT == 22:
            return
        BG = LP["BG"]
        for t_ in range(8):
            p1, pr1 = self.pbank()
            p2, pr2 = self.pbank()
            g = self.proj_tok(t_, sA1, 256, 256, p1, 0) + self.proj_tok(t_, sA2, 0, 256, p1, 256)
            g += self.proj_tok(t_, sA2, 256, 256, p2, 0) + self.proj_tok(t_, sG, 0, 16, p2, 256)
            mm(S, g, [srA1, srA2, srG] + hall, [pr1, pr2])
            if CUT == 23:
                continue
            act(S, ktok[:, t_, :], p1[:, 0:256], AF.Copy, [pr1], [r_ktok[t_]])
            cp(S, "dve", vaug[:, t_, :, 0:64], p1[:, 256:512].rearrange("p (a b) -> p a b", a=4), [pr1], [r_vaug[t_]])
            if CUT == 24:
                continue
            act(S, sgo[:, t_, :], p2[:, 0:256], AF.Sigmoid, [pr2], [r_sgo[t_]])
            tt(S, "pool", sgo[:, t_, :], sgo[:, t_, :], lp[:, l, LP["GM"]:LP["GM"] + 256], ALU.mult, [r_sgo[t_], r_k], [r_sgo[t_]])
            tt(S, "dve", gts[:, t_, :], p2[:, 256:272], lp[:, l, BG:BG + 16], ALU.add, [pr2, r_k], [r_gts])
            self.mod_pump()
        if CUT in (3, 23, 24):
            return
        ai, af = gts[:, :, 0:8], gts[:, :, 8:16]
        act(S, lf, af, AF.Exp, [r_gts], [r_gate], scale=-1.0)
        act(S, lf, lf, AF.Ln, [r_gate], [r_gate], bias=1.0)
        ts(S, "dve", lf, lf, -1.0, ALU.mult, [r_gate], [r_gate])
        tri = k_["tri"]
        pbc, prc = self.pbank()
        g = []
        for t_ in range(8):
            g.append((pbc[:, t_ * 16:t_ * 16 + 4], tri[:, 0, :], lf[:, t_, 0:4], True, True))
            g.append((pbc[:, t_ * 16 + 4:t_ * 16 + 8], tri[:, 1, :], lf[:, t_, 4:8], True, True))
            g.append((pbc[:, t_ * 16 + 8:t_ * 16 + 16], k_["ones_f"][:], lf[:, t_, 0:8], True, True))
        mm(S, g, [r_gate, r_k], [prc])
        cp(S, "dve", cum, pbc[:, 0:128].rearrange("p (a b) -> p a b", a=8), [prc], [r_gate])
        bc, bt = cum[:, :, 0:8], cum[:, :, 8:16]
        tt(S, "dve", call, ai, bc, ALU.subtract, [r_gts, r_gate], [r_gate])
        ts(S, "dve", call, call, LN8, ALU.add, [r_gate], [r_gate])
        act(S, wall, bc, AF.Exp, [r_gate], [r_gate])
        tt(S, "dve", wkl, call, bt, ALU.add, [r_gate], [r_gate])
        act(S, wkall, wkl, AF.Exp, [r_gate], [r_gate])
        ts(S, "dve", wkm, wkl, -LN8, ALU.add, [r_gate], [r_gate])
        for g_ in range(2):
            rows = slice(g_ * 64, (g_ + 1) * 64)
            act(S, dec[rows, :, :], cum[rows, :, 8 + g_:16:2], AF.Exp, [r_gate], [r_gate])
        if CUT == 4:
            return
        ident_f = k_["ident_f"]
        for dr in range(2):
            pbm, prm = self.pbank()
            g = [(pbm[0:4, t_:t_ + 1], lf[:, t_, dr * 4:dr * 4 + 4], k_["ones_f"][:, 0:1], True, True) for t_ in range(8)]
            mm(S, g, [r_gate, r_k], [prm])
            cp(S, "dve", mst[0:4, dr * 8:dr * 8 + 8], pbm[0:4, 0:8], [prm], [r_ms])
            for hh in range(2):
                pbt, prt = self.pbank()
                for q in range(4):
                    t_ = hh * 4 + q
                    S.op("pe", lambda e, t_=t_, q=q, pbt=pbt: e.transpose(pbt[0:4, q * 128:(q + 1) * 128],
                                                                          wkm[:, t_, dr * 4:dr * 4 + 4], ident_f[:]),
                         [r_gate, r_k], [prt])
                S.op("dve", lambda e, pbt=pbt, hh=hh: e.tensor_reduce(
                    out=mst[0:4, 16 + dr * 8 + hh * 4:16 + dr * 8 + hh * 4 + 4],
                    in_=pbt[0:4, :].rearrange("p (a b) -> p a b", a=4), axis=AX.X, op=ALU.max), [prt], [r_ms])
            bv_ = mst[0:4, dr * 8:dr * 8 + 8].rearrange("p (a b) -> p a b", a=4)
            av_ = mst[0:4, 16 + dr * 8:16 + dr * 8 + 8].rearrange("p (a b) -> p a b", a=4)
            fi, se = (0, 1) if dr == 0 else (1, 0)
            mf = mst[0:4, 32 + dr * 4:32 + dr * 4 + 4]
            ts(S, "dve", mf, bv_[:, :, fi], k_["m0c"][0:4, l * 2 + dr:l * 2 + dr + 1], ALU.add, [r_ms, r_k], [r_ms])
            tt(S, "dve", mf, mf, av_[:, :, fi], ALU.max, [r_ms], [r_ms])
            tt(S, "dve", mf, mf, bv_[:, :, se], ALU.add, [r_ms], [r_ms])
            tt(S, "dve", mf, mf, av_[:, :, se], ALU.max, [r_ms], [r_ms])
            S.dma("sp", d["o_m"][l, dr], mf, reads=[r_ms])
            en = mst[0:4, 40 + dr * 4:40 + dr * 4 + 4]
            act(S, en, mf, AF.Exp, [r_ms], [r_ms], scale=-1.0)
            rhs2 = mst[0:4, 48 + dr * 8:48 + dr * 8 + 8]
            tt(S, "dve", rhs2.rearrange("p (a b) -> p a b", a=4), en.unsqueeze(2).to_broadcast([4, 4, 2]),
               k_["sel"][0:4, 128:130].unsqueeze(1).to_broadcast([4, 4, 2]), ALU.mult, [r_ms, r_k], [r_ms])
            pbs, prs = self.pbank()
            mm(S, [(pbs[:, 0:8], k_["sel"][0:4, 0:128], rhs2, True, True)], [r_ms, r_k], [prs])
            cp(S, "dve", scl[:, dr, :], pbs[:, 0:8], [prs], [r_scl])
        if CUT == 5:
            return
        Clo, Chi, Cblo, Cbhi = k_["Clo"], k_["Chi"], k_["Cblo"], k_["Cbhi"]
        act(S, e0, k_["m0rep"][:, l, :, :], AF.Exp, [r_k], [r_gate])
        halves = ((slice(0, 64), Clo, Cblo), (slice(64, 128), Chi, Cbhi))
        for dr in range(2):
            S.dma("sp", Clo[:, dr, :, :], d["C0"][l, dr, :, :, :], writes=[r_C[dr]])
            tt(S, "dve", Clo[:, dr, :, :], Clo[:, dr, :, :], e0[:, dr, :].unsqueeze(2).to_broadcast([128, 2, 65]),
               ALU.mult, [r_C[dr], r_gate], [r_C[dr]])
            for (rows, Cx, Cbx) in halves:
                act(S, Cbx[rows, dr, :, 0:65], Clo[rows, dr, :, :], AF.Copy, [r_C[dr]], [r_Cb[dr]])
        keep = k_["cflags"][:, 1:2]

        tmpn2 = self.carve([128, 2, 2, 260], F32)
        r_tmpn2 = [[R("tn00"), R("tn01")], [R("tn10"), R("tn11")]]
        v3 = lambda ap: ap.rearrange("p (a b) -> p a b", a=4)

        def state_gen(dr, t_):
            tok = slice(t_ * 128, (t_ + 1) * 128)
            chs = slice(dr * 4, dr * 4 + 4)
            b_ = t_ % 2
            pst, prst = self.ps[3 + 4 * dr], self.psr[3 + 4 * dr]
            tt(S, "pool", kwp[:, dr, :, 64:128], ktok[:, t_, :].rearrange("p (a b) -> p a b", a=4),
               wkall[:, t_, chs].unsqueeze(2).to_broadcast([128, 4, 64]), ALU.mult, [r_ktok[t_], r_gate], [r_kwp[dr]])
            yield
            g = [(pst[:, h * 65:(h + 1) * 65], aqT[:, h // 2, tok], (Cblo if h % 2 == 0 else Cbhi)[:, dr, h // 2, 0:65], True, True)
                 for h in range(4)]
            for j in range(2):
                o = pst[:, 260 + j * 65:260 + (j + 1) * 65]
                g.append((o, kwp[:, dr, 2 * j, 64:192], vaug[:, t_, 2 * j, 0:65], True, False))
                g.append((o, kwp[:, dr, 2 * j + 1, 0:128], vaug[:, t_, 2 * j + 1, 0:65], False, True))
            mm(S, g, r_aq + [r_Cb[dr], r_kwp[dr], r_vaug[t_]], [prst])
            yield
            tt(S, "dve", v3(tmpn2[:, dr, b_, :]), v3(pst[:, 0:260]), wall[:, t_, chs].unsqueeze(2).to_broadcast([128, 4, 65]),
               ALU.mult, [prst, r_gate], [r_tmpn2[dr][b_]])
            yield
            tt(S, "dve", Clo[:, dr, :, :], Clo[:, dr, :, :],
               dec[:, t_, dr * 2:dr * 2 + 2].unsqueeze(2).to_broadcast([128, 2, 65]), ALU.mult,
               [r_C[dr], r_gate], [r_C[dr]])
            yield
            tt(S, "dve", Clo[:, dr, :, :], Clo[:, dr, :, :], pst[:, 260:390].rearrange("p (a b) -> p a b", a=2),
               ALU.add, [r_C[dr], prst], [r_C[dr]])
            yield
            end = (t_ % 2 == 1) if dr == 0 else (t_ % 2 == 0)
            if end:
                sq_ = t_ // 2
                tt(S, "dve", snap[:, dr, sq_, :].rearrange("p (a b) -> p a b", a=2), Clo[:, dr, :, :],
                   scl[:, dr, sq_ * 2:sq_ * 2 + 2].unsqueeze(2).to_broadcast([128, 2, 65]), ALU.mult,
                   [r_C[dr], r_scl], [r_snap[dr][sq_]])
                yield
                S.dma("sp", d["o_C"][l, dr, sq_], snap[:, dr, sq_, :].rearrange("p (a b) -> p a b", a=2),
                      reads=[r_snap[dr][sq_]])
                yield
                ts(S, "dve", Clo[:, dr, :, :], Clo[:, dr, :, :], keep, ALU.mult, [r_C[dr], r_k], [r_C[dr]])
                yield
            for (rows, Cx, Cbx) in halves:
                act(S, Cbx[rows, dr, :, 0:65], Clo[rows, dr, :, :], AF.Copy, [r_C[dr]], [r_Cb[dr]])
                yield

        def out_gen(dr, t_):
            tok = slice(t_ * 128, (t_ + 1) * 128)
            chs = slice(dr * 4, dr * 4 + 4)
            b_ = t_ % 2
            pbe, pre = self.ps[0 + 4 * dr], self.psr[0 + 4 * dr]
            pbs_, prs_ = self.ps[1 + 4 * dr], self.psr[1 + 4 * dr]
            pbi, pri = self.ps[2 + 4 * dr], self.psr[2 + 4 * dr]
            act(S, lfB[:, chs, :], lf[:, t_, chs].unsqueeze(2).to_broadcast([128, 4, 128]), AF.Copy, [r_gate], [r_lfB[dr]])
            yield
            g = []
            for h in range(4):
                o = pbe[:, h * 128:(h + 1) * 128]
                g.append((o, lfB[:, dr * 4 + h, :], tri[:, dr, :], True, False))
                g.append((o, ident_f[:], tri[:, 2 + dr, :], False, True))
            mm(S, g, [r_lfB[dr], r_k], [pre])
            yield
            g = [(pbs_[:, h * 128:(h + 1) * 128], akp[:, h, tok], aqT[:, h // 2, tok], True, True) for h in range(4)]
            mm(S, g, r_akp + r_aq, [prs_])
            yield
            for h in range(4):
                act(S, E[:, dr * 4 + h, :], pbe[:, h * 128:(h + 1) * 128], AF.Exp, [pre, r_gate], [r_E[dr]],
                    bias=call[:, t_, dr * 4 + h:dr * 4 + h + 1])
                yield
            tt(S, "dve", AT[:, dr, :, :], pbs_[:].rearrange("p (a b) -> p a b", a=4), E[:, chs, :], ALU.mult,
               [prs_, r_E[dr]], [r_AT[dr]])
            yield
            g = [(pbi[:, h * 65:(h + 1) * 65], AT[:, dr, h, :], vaug[:, t_, h, 0:65], True, True) for h in range(4)]
            mm(S, g, [r_AT[dr], r_vaug[t_]], [pri])
            yield
            tt(S, "dve", numt[:, dr, :], pbi[:, 0:260], tmpn2[:, dr, b_, :], ALU.add, [pri, r_tmpn2[dr][b_]], [r_num[dr]])
            yield
            nv = v3(numt[:, dr, :])
            dn, rd = dsm[:, dr, 0:4], dsm[:, dr, 4:8]
            ts(S, "dve", dn, nv[:, :, 64], -1.0, ALU.mult, [r_num[dr]], [r_dsm[dr]], s2=1.0, op1=ALU.max)
            yield
            tt(S, "dve", dn, dn, nv[:, :, 64], ALU.max, [r_num[dr], r_dsm[dr]], [r_dsm[dr]])
            yield
            recip(S, rd, dn, [r_dsm[dr]], [r_dsm[dr]])
            yield
            first = (t_ <= 3) if dr == 0 else (t_ >= 4)
            if first:
                tt(S, "dve", v3(hf[:, t_, :]), nv[:, :, 0:64], rd.unsqueeze(2).to_broadcast([128, 4, 64]), ALU.mult,
                   [r_num[dr], r_dsm[dr]], [r_hf[t_]])
                yield
            else:
                tt(S, "dve", v3(h64[:, dr, :]), nv[:, :, 0:64], rd.unsqueeze(2).to_broadcast([128, 4, 64]), ALU.mult,
                   [r_num[dr], r_dsm[dr]], [r_h64[dr]])
                yield
                tt(S, "dve", hf[:, t_, :], hf[:, t_, :], h64[:, dr, :], ALU.add, [r_h64[dr], r_hf[t_]], [r_hf[t_]])
                yield

        def zip_run(gens):
            gens = list(gens)
            while gens:
                for g__ in list(gens):
                    try:
                        next(g__)
                    except StopIteration:
                        gens.remove(g__)

        def pump_gen(npump):
            for _ in range(npump):
                for _ in range(5):
                    yield
                self.mod_pump(bank=(self.ps[2], self.psr[2], 300))
                yield

        order = [list(range(8)), list(range(7, -1, -1))]
        zip_run([state_gen(0, order[0][0]), state_gen(1, order[1][0])])
        for i in range(8):
            gens = [out_gen(0, order[0][i]), out_gen(1, order[1][i])]
            if i + 1 < 8:
                gens = [state_gen(0, order[0][i + 1]), state_gen(1, order[1][i + 1])] + gens
            gens.append(pump_gen(1))
            zip_run(gens)
        GM = LP["GM"]
        for t_ in range(8):
            b_ = t_ % 2
            tt(S, "dve", sq1[:, b_, :], hf[:, t_, :], hf[:, t_, :], ALU.mult, [r_hf[t_]], [r_sq1[b_]])
            S.op("dve", lambda e, t_=t_, b_=b_: e.tensor_reduce(out=ssall[:, t_, :],
                                                              in_=sq1[:, b_, :].rearrange("p (a b) -> p a b", a=4),
                                                              axis=AX.X, op=ALU.add), [r_sq1[b_]], [r_ss])
        act(S, ssall, ssall, AF.Ln, [r_ss], [r_ss], bias=EPS, scale=1.0 / 64)
        act(S, ssall, ssall, AF.Exp, [r_ss], [r_ss], scale=-0.5)
        ident_bf = k_["ident_bf"]
        for t_ in range(8):
            b_ = t_ % 2
            tt(S, "dve", sq1[:, b_, :].rearrange("p (a b) -> p a b", a=4), hf[:, t_, :].rearrange("p (a b) -> p a b", a=4),
               ssall[:, t_, :].unsqueeze(2).to_broadcast([128, 4, 64]), ALU.mult, [r_hf[t_], r_ss], [r_sq1[b_]])
            tt(S, "dve", yatok[:, t_, :], sq1[:, b_, :], sgo[:, t_, :], ALU.mult, [r_sgo[t_], r_sq1[b_]], [r_ya[t_]])
            self.mod_pump()
            pbt, prt = self.pbank()
            pv = pbt[:].bitcast(BF16)
            for t2 in range(2):
                S.op("pe", lambda e, t2=t2, pv=pv, t_=t_: e.transpose(pv[:, t2 * 128:(t2 + 1) * 128],
                                                                      yatok[:, t_, t2 * 128:(t2 + 1) * 128], ident_bf[:]),
                     [r_ya[t_], r_k], [prt])
            c = t_ // 4
            for t2 in range(2):
                if t2 == 0:
                    act(S, ymix[:, t2, t_ * 128:(t_ + 1) * 128], pv[:, t2 * 128:(t2 + 1) * 128], AF.Copy, [prt], [ymr[t2][c]])
                else:
                    cp(S, "dve", ymix[:, t2, t_ * 128:(t_ + 1) * 128], pv[:, t2 * 128:(t2 + 1) * 128], [prt], [ymr[t2][c]])

    def mix_attn(self, l, ymix, ymr, win, glob):
        S, k_, d = self.S, self.k, self.d
        C = self.ctx
        hr, CH = C["hr"], C["CH"]
        LP = self.LP
        lp = k_["lp"]
        r_k = self.r_k
        hall = [hr[k][c] for k in range(8) for c in range(2)]
        sv8 = lambda sl_: self.slot_view(sl_, 8)
        q0 = 1040 if glob else 1552
        k0, v0 = q0 + 256, q0 + 384
        sl, sr = self.wload(self.parts_attn(l, glob), key=("attn", l, glob))
        if glob:
            self.prefetch(("attn", l, False), self.parts_attn(l, False))
        else:
            self.prefetch(("D1", l), self.parts_win(l, 2064, 512))
            self.prefetch(("D2", l), self.parts_win(l, 2576, 256))
        s3 = sv8(sl)
        ymb = 2 if glob else 4
        cs = self.carve([128, 2, T], F32)
        qpad = self.carve([128, 4, T], BF16)
        kfull = self.carve([128, 2, 1280], BF16)
        vpad = self.carve([128, 10, 2, 192], BF16)
        kout = self.carve([128, T], F32)
        vout = self.carve([128, 8, 128], F32)
        PT = self.carve([128, 3, 512], BF16)
        rden = self.carve([128, 2, 512], F32)
        raw = self.carve([128, 2, 512], F32)
        tb = self.carve([128, 2, 512], F32)
        gm = self.carve([128, 2, T], BF16)
        smask = self.carve([128, 8, 384], BF16) if not glob else None
        es = self.carve([128, 4], F32)
        r_cs, r_qp, r_kf, r_vp, r_kout, r_vout = R("cs"), RL(2, "qp"), R("kf"), RL(10, "vp"), R("kout"), R("vout")
        r_PT, r_rden, r_raw, r_tb, r_gm, r_sm, r_es = RL(3, "PT"), RL(2, "rden"), RL(2, "raw"), RL(2, "tb"), R("gm"), R("sm"), R("es")
        S.dma("sp", cs, d["ropeCS"], writes=[r_cs])
        S.op("pool", lambda e: e.memset(qpad, 0.0), writes=r_qp)
        S.op("pool", lambda e: e.memset(vpad[:, 2:10, :, :], 0.0), writes=r_vp[2:])
        kTd, vpd = (d["gkT"], d["gvp"]) if glob else (d["skT"], d["svp"])
        for x in range(2):
            S.dma("pool", kfull[:, x, 0:256], kTd[l, x], writes=[r_kf])
            S.dma("pool", vpad[:, x, :, :], vpd[l, x], writes=[r_vp[x]])
        if glob:
            S.dma("pool", gm[0:5, :, :], d["gmAB"], writes=[r_gm])
        if not glob:
            S.dma("pool", smask, d["swamask"], writes=[r_sm])
            SK = LP["SK"]
            act(S, es, lp[:, l, SK:SK + 4], AF.Exp, [r_k], [r_es])
        GQK = LP["GQK"]
        ropeP = k_["ropeP"]
        rstd2 = self.carve([128, 2, 512], F32)
        r_rstd2 = RL(2, "rstd2")

        def prelude(ti, c, b_):
            col0 = ti * 128
            pb, pr = self.pbank()
            mm(S, self.proj_fm(c, s3, col0, pb), [sr] + hall, [pr])
            yield
            rw = raw[:, b_, :]
            if glob:
                act(S, rw, pb[:], AF.Copy, [pr], [r_raw[b_]])
                yield
                i = self.sq_rr % 2
                self.sq_rr += 1
                sqb, r_sq = C["sqb"], C["r_sq"]
                act(S, sqb[:, i, :], rw, AF.Square, [r_raw[b_]], [r_sq[i]])
                yield
                p2, pr2 = self.pbank()
                mm(S, [(p2[:], k_["blockones"][:], sqb[:, i, :], True, True)], [r_sq[i], r_k], [pr2])
                yield
                rstd, r_rstd = rstd2[:, b_, :], r_rstd2[b_]
                act(S, rstd, p2[:], AF.Ln, [pr2], [r_rstd], bias=EPS, scale=1.0 / 64)
                yield
                act(S, rstd, rstd, AF.Exp, [r_rstd], [r_rstd], scale=-0.5)
                yield
                tt(S, "dve", rw, rw, rstd, ALU.mult, [r_raw[b_], r_rstd], [r_raw[b_]])
                yield
                gcol = GQK + (0 if ti < 2 else 1)
                act(S, rw, rw, AF.Copy, [r_raw[b_], r_k], [r_raw[b_]], scale=lp[:, l, gcol:gcol + 1])
                yield
            else:
                act(S, rw, pb[:], AF.Copy, [pr], [r_raw[b_]])
                yield
            p3, pr3 = self.pbank()
            mm(S, [(p3[:], ropeP[:], rw, True, True)], [r_raw[b_], r_k], [pr3])
            yield
            ta, tb_ = rw, tb[:, b_, :]
            tt(S, "dve", ta, rw, cs[:, 0, CH(c)], ALU.mult, [r_raw[b_], r_cs], [r_raw[b_]])
            yield
            tt(S, "dve", tb_, p3[:], cs[:, 1, CH(c)], ALU.mult, [pr3, r_cs], [r_tb[b_]])
            yield
            if ti < 2:
                for g_ in range(2):
                    rows = slice(g_ * 64, (g_ + 1) * 64)
                    tt(S, "dve", qpad[rows, 2 * ti + g_, CH(c)], ta[rows, :], tb_[rows, :], ALU.add, [r_tb[b_], r_raw[b_]], [r_qp[c]])
                    yield
            elif ti == 2:
                tt(S, "dve", kout[:, CH(c)], ta, tb_, ALU.add, [r_tb[b_], r_raw[b_]], [r_kout])
                yield
                act(S, kfull[:, 0, 256 + c * 512:256 + (c + 1) * 512], kout[:, CH(c)], AF.Copy, [r_kout], [r_kf])
                yield
            else:
                tt(S, "dve", kfull[:, 1, 256 + c * 512:256 + (c + 1) * 512], ta, tb_, ALU.add, [r_tb[b_], r_raw[b_]], [r_kf])
                yield

        its = [(ti, c) for ti in range(4) for c in range(2)]
        for i0 in range(0, 8, 2):
            gens = [prelude(its[i0][0], its[i0][1], 0), prelude(its[i0 + 1][0], its[i0 + 1][1], 1)]
            while gens:
                for g__ in list(gens):
                    try:
                        next(g__)
                    except StopIteration:
                        gens.remove(g__)
        S.dma("sp", (d["o_gk"] if glob else d["o_sk"])[l], kout, reads=[r_kout])
        for t_ in range(8):
            pb, pr = self.pbank()
            mm(S, self.proj_tok(t_, s3, 512, 128, pb, 0), [sr] + hall, [pr])
            act(S, vout[:, t_, :], pb[:, 0:128], AF.Copy, [pr], [r_vout])
            cp(S, "dve", vpad[:, 2 + t_, :, 64:128], pb[:, 0:128].rearrange("p (a b) -> p a b", a=2), [pr], [r_vp[2 + t_]])
        S.dma("sp", (d["o_gv"] if glob else d["o_sv"])[l].rearrange("(t p) n -> p t n", p=128), vout, reads=[r_vout])
        self.bank_pool = [0, 1, 2, 3]
        ones_bf = C["ones_bf"]
        r_ones = C["r_ones"]
        ident_bf = k_["ident_bf"]
        ctxb = k_["cflags"][:, 0:1]
        pt_rr = 0
        acc_rr = 0
        for h in range(4):
            kv, g_ = h // 2, h % 2
            kx = 0 if g_ == kv else 1
            rows = slice(g_ * 64, (g_ + 1) * 64)
            vw = slice(64, 192) if g_ == 0 else slice(0, 128)
            for c in range(2):
                if glob:
                    tiles = [(mt, 0, 512) for mt in range(10)]
                else:
                    tiles = [(0, 0, 512), (1, 0, 512)]
                    for j in range(8):
                        lo, hi = max((j - 1) * 128, c * 512), min((j + 2) * 128, (c + 1) * 512)
                        if hi > lo:
                            tiles.append((2 + j, lo - c * 512, hi - c * 512))
                ai_ = 4 + 2 * (acc_rr % 2)
                acc_rr += 1
                pn, prn, pd, prd = self.ps[ai_], self.psr[ai_], self.ps[ai_ + 1], self.psr[ai_ + 1]
                pend = []

                def qk(idx):
                    mt, lo, hi = tiles[idx]
                    pb, pr = self.pbank()
                    qs = slice(c * 512 + lo, c * 512 + hi)
                    g = [(pb[:, lo:hi], kfull[:, kx, mt * 128:(mt + 1) * 128], qpad[:, h, qs], True, mt < 2)]
                    rd = [r_kf, r_qp[c]]
                    if mt >= 2:
                        if glob:
                            g.append((pb[:, lo:hi], gm[0:5, 0, (mt - 2) * 128:(mt - 1) * 128], gm[0:5, 1, qs], False, True))
                            rd.append(r_gm)
                        else:
                            j = mt - 2
                            m0_ = c * 512 + lo - (j - 1) * 128
                            g.append((pb[:, lo:hi], ident_bf[:], smask[:, j, m0_:m0_ + (hi - lo)], False, True))
                            rd += [r_sm, r_k]
                    mm(S, g, rd, [pr])
                    return pb, pr

                nt = len(tiles)
                look = 2
                for idx in range(min(look, nt)):
                    pend.append(qk(idx))
                for idx in range(nt):
                    mt, lo, hi = tiles[idx]
                    pb, pr = pend.pop(0)
                    if idx + look < nt:
                        pend.append(qk(idx + look))
                    pi = pt_rr % 3
                    pt_rr += 1
                    act(S, PT[:, pi, lo:hi], pb[:, lo:hi], AF.Exp, [pr, r_k], [r_PT[pi]],
                        bias=(ctxb if mt < 2 else 0.0), scale=0.125)
                    g = [(pn[:, lo:hi], vpad[:, mt, kv, vw], PT[:, pi, lo:hi], idx == 0, idx == nt - 1),
                         (pd[:, lo:hi], ones_bf[:], PT[:, pi, lo:hi], idx == 0, idx == nt - 1)]
                    mm(S, g, [r_vp[mt], r_PT[pi], r_ones], [prn, prd])
                ri = (h * 2 + c) % 2
                if glob:
                    recip(S, rden[rows, ri, :], pd[rows, :], [prd], [r_rden[ri]])
                else:
                    ts(S, "dve", rden[rows, ri, :], pd[rows, :], es[rows, h:h + 1], ALU.add, [prd, r_es], [r_rden[ri]])
                    recip(S, rden[rows, ri, :], rden[rows, ri, :], [r_rden[ri]], [r_rden[ri]])
                tt(S, "dve", ymix[rows, ymb + kv, CH(c)], pn[rows, :], rden[rows, ri, :], ALU.mult, [prn, r_rden[ri]],
                   [ymr[ymb + kv][c]])
        self.bank_pool = list(range(8))

    def mix_hyena(self, l, ymix, ymr, win):
        S, k_, d = self.S, self.k, self.d
        C = self.ctx
        hr, CH = C["hr"], C["CH"]
        LP = self.LP
        lp = k_["lp"]
        r_k = self.r_k
        hall = [hr[k][c] for k in range(8) for c in range(2)]
        sv8 = lambda sl_: self.slot_view(sl_, 8)
        TWO_PI = float(2 * math.pi)
        xoff = self.aoff
        feats = self.carve([128, T], F32)
        z1 = self.carve([128, T], F32)
        z2 = self.carve([128, T], F32)
        wnd = self.carve([128, 8, 256], F32)
        rr = self.carve([128, 512], F32)
        ii = self.carve([128, 512], I32)
        kf = self.carve([128, 512], F32)
        xend = self.aoff
        self.aoff = xoff
        raw = self.carve([128, 3, T], F32)
        uct = self.carve([128, 2, T], F32)
        pa = self.carve([128, 2, 256], F32)
        pq = self.carve([128, 2, 256], F32)
        yt = self.carve([128, 2, 256], F32)
        assert self.aoff <= xend
        self.aoff = xend
        r_X = R("X")
        x0 = self.carve([128, 2, T], F32)
        zf = self.carve([128, 2, T], F32)
        zbf = self.carve([128, 2, T], BF16)
        zh = self.carve([128, 8, 512], BF16)
        ZH = self.carve([128, 2, 512], F32)
        Y = self.carve([128, 8, 2, 256], BF16)
        pa2 = self.carve([128, 2, 256], F32)
        yt2 = ZH[:, :, 0:256]
        r_x0, r_zf, r_zbf = RL(2, "x0"), RL(2, "zf"), RL(2, "zbf")
        r_zh = RL(8, "zh")
        r_ZH, r_Y, r_pa, r_pq, r_yt = R("ZH"), RL(8, "Y"), R("pa"), R("pq"), RL(2, "yt")
        S.dma("sp", feats[0:33, :], d["featsT"], writes=[r_X])
        S.dma("sp", wnd, d["window"], writes=[r_X])
        fs = k_["fs"]

        def sin_layer(pb, pr, dst, bcol):
            ts(S, "dve", rr[0:64, :], pb[0:64, :], fs[:, l, 0:1], ALU.mult, [pr, r_k], [r_X], s2=fs[:, l, bcol:bcol + 1],
               op1=ALU.add)
            cp(S, "dve", ii[0:64, :], rr[0:64, :], [r_X], [r_X])
            cp(S, "dve", kf[0:64, :], ii[0:64, :], [r_X], [r_X])
            tt(S, "dve", rr[0:64, :], rr[0:64, :], kf[0:64, :], ALU.subtract, [r_X], [r_X])
            act(S, dst, rr[0:64, :], AF.Sin, [r_X], [r_X], scale=TWO_PI)

        def zip2(gens):
            gens = list(gens)
            while gens:
                for g__ in list(gens):
                    try:
                        next(g__)
                    except StopIteration:
                        gens.remove(g__)

        r_fc = [[R("fc%d%d" % (a_, c_)) for c_ in range(2)] for a_ in range(2)]

        def sin_chain(layer, c):
            cs_ = slice(c * 256, (c + 1) * 256)
            for hh in range(2):
                q_ = slice(c * 512 + hh * 256, c * 512 + (hh + 1) * 256)
                pb, pr = self.pbank()
                if layer == 0:
                    mm(S, [(pb[0:64, 0:256], k_["fw1"][0:33, l, :], feats[0:33, q_], True, True)], [r_X, r_k], [pr])
                else:
                    mm(S, [(pb[0:64, 0:256], k_["fw2"][0:64, l, :], z1[0:64, q_], True, True)], [r_fc[0][c], r_k], [pr])
                yield
                bcol = 1 + layer
                rg = R("tmp")
                ts(S, "dve", rr[0:64, cs_], pb[0:64, 0:256], fs[:, l, 0:1], ALU.mult, [pr, r_k], [r_fc[1][c]],
                   s2=fs[:, l, bcol:bcol + 1], op1=ALU.add)
                yield
                cp(S, "dve", ii[0:64, cs_], rr[0:64, cs_], [r_fc[1][c]], [r_fc[1][c]])
                yield
                cp(S, "dve", kf[0:64, cs_], ii[0:64, cs_], [r_fc[1][c]], [r_fc[1][c]])
                yield
                tt(S, "dve", rr[0:64, cs_], rr[0:64, cs_], kf[0:64, cs_], ALU.subtract, [r_fc[1][c]], [r_fc[1][c]])
                yield
                dst = (z1 if layer == 0 else z2)[0:64, q_]
                act(S, dst, rr[0:64, cs_], AF.Sin, [r_fc[1][c]], [r_fc[0][c] if layer == 0 else r_X], scale=TWO_PI)
                yield

        for c in range(2):
            S.op("dve", lambda e, c=c: e.memset(rr[0:64, c * 256:c * 256 + 1], 0.0), [r_X], [r_fc[1][c], r_fc[0][c]])
        zip2([sin_chain(0, 0), sin_chain(0, 1)])
        zip2([sin_chain(1, 0), sin_chain(1, 1)])
        for c in range(2):
            S.op("dve", lambda e, c=c: e.memset(rr[0:64, c * 256:c * 256 + 1], 0.0), [r_fc[1][c], r_fc[0][c]], [r_X])
        for t_ in range(8):
            pb, pr = self.pbank()
            tok = slice(t_ * 128, (t_ + 1) * 128)
            mm(S, [(pb[:, 0:256], z2[0:64, tok], k_["fw3"][0:64, l, :], True, False),
                   (pb[:, 0:256], k_["ones_f"][0:1, :], k_["fb3"][0:1, l, :], False, True)], [r_X, r_k], [pr])
            tt(S, "dve", zh[:, t_, 256:512], pb[:, 0:256], wnd[:, t_, :], ALU.mult, [pr, r_X], [r_zh[t_]])
        slD1, srD1 = self.wload(self.parts_win(l, 2064, 512), key=("D1", l))
        slD2, srD2 = self.wload(self.parts_win(l, 2576, 256), key=("D2", l))
        sD1, sD2 = sv8(slD1), sv8(slD2)
        Fd, Gd = d["dftF"], d["dftG"]
        fv = lambda m: Fd[m].rearrange("(k p) n -> p k n", p=128)
        gv = lambda m: Gd[m].rearrange("(k p) n -> p k n", p=128)

        def parts_dft(vw, q):
            return [(lambda sl_: sv8(sl_)[:, :, 0:256], vw(0)[:, :, q * 256:(q + 1) * 256]),
                    (lambda sl_: sv8(sl_)[:, :, 256:512], vw(1)[:, :, q * 256:(q + 1) * 256])]

        def zip_run(gens):
            gens = list(gens)
            while gens:
                for g__ in list(gens):
                    try:
                        next(g__)
                    except StopIteration:
                        gens.remove(g__)

        for q in range(2):
            self.prefetch(("F", l, q), parts_dft(fv, q))
        CV = LP["CV"]
        cvb = k_["cvb"]
        r_raw = RL(3, "hraw")
        r_uct = RL(2, "uct")

        def conv_chain(ct, ui):
            s3, col, srr = ((sD1, ct * 128, srD1), (sD1, 256 + ct * 128, srD1), (sD2, ct * 128, srD2))[ui]
            rr_ = r_raw[ui]
            fx = [r_X] if ct == 0 else []
            for c in range(2):
                pb, pr = self.pbank()
                mm(S, self.proj_fm(c, s3, col, pb), [srr] + hall, [pr])
                yield
                act(S, raw[:, ui, CH(c)], pb[:], AF.Copy, [pr], [rr_] + fx)
                yield
            tile = ui * 2 + ct
            cw = lambda j: lp[:, l, CV + tile * 4 + j:CV + tile * 4 + j + 1]
            u = raw[:, ui, :]
            if ui == 0:
                dst, wr = x0[:, ct, :], [r_x0[ct]]
            else:
                dst, wr = uct[:, ui - 1, :], [r_uct[ui - 1]] + fx
            rd = [rr_, r_k]
            act(S, dst, u, AF.Identity, rd, wr, bias=cw(3), scale=cw(1))
            yield
            for (o, a_, sc) in ((dst[:, 1:T], u[:, 0:T - 1], cw(0)), (dst[:, 0:T - 1], u[:, 1:T], cw(2)),
                                (dst[:, 256:T:256], u[:, 255:T - 1:256], cvb[:, l, tile, 0:1]),
                                (dst[:, 255:T - 1:256], u[:, 256:T:256], cvb[:, l, tile, 1:2])):
                S.op("dve", lambda e, o=o, a_=a_, sc=sc: e.scalar_tensor_tensor(out=o, in0=a_, scalar=sc, in1=o,
                                                                              op0=ALU.mult, op1=ALU.add), rd + wr[:1], wr[:1])
                yield

        for ct in range(2):
            zip_run([conv_chain(ct, ui) for ui in range(3)])
            tt(S, "dve", zf[:, ct, :], uct[:, 0, :], uct[:, 1, :], ALU.mult, r_uct, [r_zf[ct]])
            act(S, zbf[:, ct, :], zf[:, ct, :], AF.Copy, [r_zf[ct]], [r_zbf[ct]])
        for q in range(2, 4):
            self.prefetch(("F", l, q), parts_dft(fv, q))
        ident_bf = k_["ident_bf"]
        for t_ in range(8):
            pb, pr = self.pbank()
            pv = pb[:].bitcast(BF16)
            for ct in range(2):
                S.op("pe", lambda e, ct=ct, pv=pv, t_=t_: e.transpose(pv[:, ct * 128:(ct + 1) * 128],
                                                                      zbf[:, ct, t_ * 128:(t_ + 1) * 128], ident_bf[:]),
                     [r_zbf[ct], r_k], [pr])
            cp(S, "dve", zh[:, t_, 0:256], pv[:, 0:256], [pr], [r_zh[t_]])
        ZHs = [ZH, zbf.rearrange("p a b -> p (a b)").bitcast(F32).rearrange("p (a b) -> p a b", a=2)]
        r_ZHs = [r_ZH, R("ZHb")]
        PAs, PQs = [pa, pa2], [pq, yt]
        r_PA, r_PQ = RL(2, "PA"), RL(2, "PQ")
        seen = set()

        def ft_chain(ft, fj, s3, sr):
            bi = ft % 2
            Zb, rz = ZHs[bi], r_ZHs[bi]
            PA, PQ, rpa, rpq = PAs[bi], PQs[bi], r_PA[bi], r_PQ[bi]
            fresh = bi not in seen
            seen.add(bi)
            fz = (r_zbf if (fresh and bi == 1) else [])
            fx = ([r_X] if fresh else [])
            pre_, prr = self.pbank()
            pim, pri = self.pbank()
            g = [(pre_[:], s3[:, t_, fj * 128:(fj + 1) * 128], zh[:, t_, :], t_ == 0, t_ == 7) for t_ in range(8)]
            g += [(pim[:], s3[:, t_, 256 + fj * 128:256 + (fj + 1) * 128], zh[:, t_, :], t_ == 0, t_ == 7) for t_ in range(8)]
            mm(S, g, [sr] + r_zh, [prr, pri])
            yield
            act(S, Zb[:, 0, :], pre_[:], AF.Copy, [prr], [rz] + fz)
            yield
            act(S, Zb[:, 1, :], pim[:], AF.Copy, [pri], [rz])
            yield
            Zr, Hr, Zi, Hi = Zb[:, 0, 0:256], Zb[:, 0, 256:512], Zb[:, 1, 0:256], Zb[:, 1, 256:512]
            tt(S, "dve", PA[:, 0, :], Zr, Hr, ALU.mult, [rz], [rpa] + fx)
            yield
            tt(S, "dve", PQ[:, 0, :], Zr, Hi, ALU.mult, [rz], [rpq] + fx)
            yield
            tt(S, "dve", PA[:, 1, :], Zi, Hi, ALU.mult, [rz], [rpa])
            yield
            tt(S, "dve", PQ[:, 1, :], Zi, Hr, ALU.mult, [rz], [rpq])
            yield
            tt(S, "dve", Y[:, ft, 0, :], PA[:, 0, :], PA[:, 1, :], ALU.subtract, [rpa], [r_Y[ft]])
            yield
            tt(S, "dve", Y[:, ft, 1, :], PQ[:, 0, :], PQ[:, 1, :], ALU.add, [rpq], [r_Y[ft]])
            yield

        for q in range(4):
            sl, sr = self.wload(parts_dft(fv, q), key=("F", l, q))
            s3 = sv8(sl)
            zip_run([ft_chain(q * 2 + fj, fj, s3, sr) for fj in range(2)])
        HB = LP["HB"]
        for q in range(2):
            self.prefetch(("G", l, q), parts_dft(gv, q))
        for q in range(4):
            sl, sr = self.wload(parts_dft(gv, q), key=("G", l, q))
            if q + 2 < 4:
                self.prefetch(("G", l, q + 2), parts_dft(gv, q + 2))
            s3 = sv8(sl)
            ns = slice(q * 256, (q + 1) * 256)
            for ct in range(2):
                pb, pr = self.pbank()
                g = []
                for ft in range(8):
                    g.append((pb[:, 0:256], Y[:, ft, 0, ct * 128:(ct + 1) * 128], s3[:, ft, 0:256], ft == 0, False))
                    g.append((pb[:, 0:256], Y[:, ft, 1, ct * 128:(ct + 1) * 128], s3[:, ft, 256:512], False, ft == 7))
                mm(S, g, [sr] + r_Y, [pr])
                ytb, ryt = (yt2[:, ct, :], r_yt[ct])
                act(S, ytb, pb[:, 0:256], AF.Copy, [pr], [ryt] + ([r_PA[1], r_ZHs[0]] if q == 0 else []))
                S.op("dve", lambda e, ytb=ytb, ct=ct: e.scalar_tensor_tensor(out=ytb, in0=zf[:, ct, ns],
                                                                           scalar=lp[:, l, HB + ct:HB + ct + 1], in1=ytb,
                                                                           op0=ALU.mult, op1=ALU.add),
                     [r_zf[ct], ryt, r_k], [ryt])
                tt(S, "dve", ymix[:, 6 + ct, ns], x0[:, ct, ns], ytb, ALU.mult, [r_x0[ct], ryt], [ymr[6 + ct][q // 2]])


def fm(v):
    return np.ascontiguousarray(np.asarray(v, np.float32).reshape(8, 128).T)


def _consts(kind):
    c = {}
    p = np.arange(128)
    t = np.arange(T)
    ident = np.eye(128, dtype=np.float32)
    c["ident"] = ident
    bo = np.zeros((128, 128), np.float32)
    bo[:64, :64] = 1
    bo[64:, 64:] = 1
    c["blockones"] = bo
    r_, t_ = np.meshgrid(p, p, indexing="ij")
    tri = np.zeros((128, 4, 128), np.float32)
    tri[:, 0, :] = (r_ <= t_)
    tri[:, 1, :] = (r_ >= t_)
    tri[:, 2, :] = np.where(r_ <= t_, 0.0, NEG)
    tri[:, 3, :] = np.where(r_ >= t_, 0.0, NEG)
    c["tri"] = tri
    P = np.zeros((128, 128), np.float32)
    for b in range(0, 128, 32):
        for i in range(16):
            P[b + i + 16, b + i] = -1.0
            P[b + i, b + i + 16] = 1.0
    c["ropeP"] = P
    cs = np.zeros((128, 2, T), np.float32)
    if kind == "s":
        dd = p % 64
        inv = (10000.0 ** (-(dd % 16).astype(np.float32) / np.float32(16))).astype(np.float32)
        row = (t // 64).astype(np.float32)
        col = (t % 64).astype(np.float32)
        pos = np.where((dd // 32)[:, None] == 0, row[None, :], col[None, :]).astype(np.float32)
        ang = (pos * inv[:, None]).astype(np.float32)
        cs[:, 0, :] = np.cos(ang)
        cs[:, 1, :] = np.sin(ang)
    else:
        cs[:, 0, :] = 1.0
    c["ropeCS"] = cs
    sm = np.full((128, 8, 384), NEG, np.float32)
    for j in range(8):
        m = j * 128 + p[:, None]
        q = (j - 1) * 128 + np.arange(384)[None, :]
        inr = (q >= 0) & (q < T)
        if kind == "s":
            ok = (np.abs(m - q) <= 128) & inr
        else:
            ok = ((m // 256) == (q // 256)) & inr
        sm[:, j, :] = np.where(ok, 0.0, NEG)
    c["swamask"] = sm
    gm = np.zeros((5, 2, T), np.float32)
    if kind == "p":
        gm[0, 0, :] = 1.0
        gm[0, 1, :] = -BIG
        for s_ in range(4):
            gm[1 + s_, 0, :] = (t // 256 == s_)
            gm[1 + s_, 1, :] = BIG * (t // 256 == s_)
    c["gmAB"] = gm
    c["cflags"] = np.tile(np.array([[0.0, 1.0, 0.0, 0.0]] if kind == "s" else [[NEG, 0.0, 1.0, 0.0]], np.float32), (128, 1))
    sel = np.zeros((4, 130), np.float32)
    for h in range(4):
        sel[h, 0:128] = ((p >= 64).astype(int) == (h % 2))
        sel[h, 128 + h // 2] = 1.0
    c["sel"] = sel
    L = T if kind == "s" else 256
    rep = T // L
    pos = np.arange(L, dtype=np.float32)
    t01 = pos / np.float32(max(L - 1, 1))
    lin = np.linspace(1e-4, 15.0, 16, dtype=np.float32)
    ang = (np.float32(2.0 * math.pi / L) * pos[:, None] * lin[None, :]).astype(np.float32)
    feats = np.concatenate([t01[:, None], np.cos(ang), -np.sin(ang)], -1).astype(np.float32)
    c["featsT"] = np.ascontiguousarray(np.tile(feats, (rep, 1)).T)
    centre = L // 2
    dist = np.abs(pos - centre) / np.float32(max(centre, 1))
    deltas = np.abs(np.linspace(math.log(0.01) / 1.5, math.log(0.01) / 0.3, 256, dtype=np.float32))
    wnd = np.exp(-dist[:, None] * deltas[None, :]).astype(np.float32)
    c["window"] = np.ascontiguousarray(np.tile(wnd, (rep, 1)).reshape(8, 128, 256).transpose(1, 0, 2))
    N = 2 * L
    tt_ = np.arange(L, dtype=np.float64)
    ff = np.arange(L, dtype=np.float64)
    th = math.pi * (2 * ff + 1) / N
    Fc = np.cos(tt_[:, None] * th[None, :])
    Fs = -np.sin(tt_[:, None] * th[None, :])
    Gc = (2.0 / N) * np.cos(th[:, None] * (tt_[None, :] + L // 2))
    Gs = -(2.0 / N) * np.sin(th[:, None] * (tt_[None, :] + L // 2))
    dF = np.zeros((2, T, T), np.float32)
    dG = np.zeros((2, T, T), np.float32)
    for s_ in range(rep):
        sl = slice(s_ * L, (s_ + 1) * L)
        dF[0, sl, sl] = Fc
        dF[1, sl, sl] = Fs
        dG[0, sl, sl] = Gc
        dG[1, sl, sl] = Gs
    c["dftF"] = dF
    c["dftG"] = dG
    return c


def host_inputs(inp, cores=None):
    f = lambda a: np.ascontiguousarray(np.asarray(a, dtype=np.float32))
    A = {k: np.asarray(v) for k, v in inp.items()}
    shared = {}
    for nm in ("w_ada", "w1_gate", "w1_up", "w1_down", "w_in", "w_out", "w2_gate", "w2_up", "w2_down"):
        shared[nm] = f(A[nm])
    shared["b_adaT"] = f(A["b_ada"].reshape(NL, 72, 128).transpose(0, 2, 1))
    gv = []
    for l in range(NL):
        gv += [fm(A["g_ff1"][l]), fm(A["g_mix"][l]), fm(A["g_ff2"][l])]
    gv.append(fm(A["g_final"]))
    shared["gvec"] = f(np.concatenate(gv, axis=1))
    lp = np.zeros((128, NL, 304), np.float32)
    p = np.arange(128)
    for l in range(NL):
        lp[:, l, 0:16] = A["b_gates"][l][None, :]
        lp[:, l, 16:272] = A["g_mlstm"][l][None, :]
        lp[:, l, 272] = A["g_qnorm"][l][p % 64]
        lp[:, l, 273] = A["g_knorm"][l][p % 64]
        lp[:, l, 274:278] = A["sinks"][l][None, :]
        for i in range(6):
            ch = i * 128 + p
            lp[:, l, 278 + i * 4 + 0] = A["conv_w"][l][0, ch]
            lp[:, l, 278 + i * 4 + 1] = A["conv_w"][l][1, ch]
            lp[:, l, 278 + i * 4 + 2] = A["conv_w"][l][2, ch]
            lp[:, l, 278 + i * 4 + 3] = A["conv_b"][l][ch]
        for ct in range(2):
            lp[:, l, 302 + ct] = A["hyena_bias"][l][ct * 128 + p]
    shared["lp"] = lp
    shared["fw1"] = f(A["filt_w1"].transpose(1, 0, 2))
    shared["fw2"] = f(A["filt_w2"].transpose(1, 0, 2))
    shared["fw3"] = f(A["filt_w3"].transpose(1, 0, 2))
    shared["fvec"] = f(np.stack([A["filt_b1"], A["filt_b2"], A["filt_freq"]], -1).transpose(1, 0, 2))
    shared["fb3"] = f(A["filt_b3"][None, :, :])
    cst = {"s": _consts("s"), "p": _consts("p")}
    maps = []
    xs, xp = A["x_sample"], A["x_prompt"]
    for core in (range(8) if cores is None else cores):
        m = dict(shared)
        kind = "s" if core < 4 else "p"
        m.update(cst[kind])
        C0 = np.zeros((NL, 2, 128, 2, 65), np.float32)
        m0rep = np.zeros((128, NL, 2, 2), np.float32)
        m0c = np.zeros((4, NL * 2), np.float32)
        kT = {n: np.zeros((NL, 2, 128, 256), np.float32) for n in ("gkT", "skT")}
        vp = {n: np.zeros((NL, 2, 128, 2, 192), np.float32) for n in ("gvp", "svp")}
        if core < 4:
            b = core
            m["xT"] = f(xs[b].T)
            m["cvec"] = fm(A["c"][b])
            sC, sn, smm = A["state_mlstm_C"][b], A["state_mlstm_n"][b], A["state_mlstm_m"][b]
            for g_ in range(2):
                for pr_ in range(2):
                    h = 2 * pr_ + g_
                    C0[:, :, g_ * 64:(g_ + 1) * 64, pr_, 0:64] = sC[:, :, h]
                    C0[:, :, g_ * 64:(g_ + 1) * 64, pr_, 64] = sn[:, :, h]
                    m0rep[g_ * 64:(g_ + 1) * 64, :, :, pr_] = smm[None, :, :, h]
            for l in range(NL):
                for dr in range(2):
                    m0c[:, l * 2 + dr] = smm[l, dr, :]
            for (kn, vn, ck_, cv_) in (("gkT", "gvp", "cache_gattn_k", "cache_gattn_v"), ("skT", "svp", "cache_swa_k", "cache_swa_v")):
                ck, cv = A[ck_][b], A[cv_][b]
                t1 = ck.transpose(0, 2, 3, 1).reshape(NL, 128, 256)
                t2 = ck[:, :, ::-1, :].transpose(0, 2, 3, 1).reshape(NL, 128, 256)
                kT[kn][:, 0] = t1
                kT[kn][:, 1] = t2
                vp[vn][:, :, :, :, 64:128] = cv.reshape(NL, 2, 128, 2, 64)
        else:
            j = core - 4
            m["xT"] = f(xp[4 * j:4 * j + 4].reshape(T, D).T)
            m["cvec"] = fm(A["c_ctx"])
        m["C0"], m["m0rep"], m["m0c"] = C0, m0rep, m0c
        m.update(kT)
        m.update(vp)
        maps.append(m)
    return maps


_PROG = None


def get_prog():
    global _PROG
    if _PROG is None:
        _PROG = Prog()
    return _PROG


def run_device(inputs, trace=False, cores=None):
    prog = get_prog()
    maps = host_inputs(inputs, cores)
    maps = [{k: np.ascontiguousarray(v, dtype=np.float32) for k, v in m.items() if k in prog.din} for m in maps]
    for m in maps:
        for k, shp in prog.din.items():
            assert m[k].shape == shp, (k, m[k].shape, shp)
    res = run_bass_kernel_spmd(prog.nc, maps, core_ids=list(range(len(maps))), trace=trace)
    return res


def assemble(res):
    r = res.results
    yp = np.zeros((16, 256, D), np.float32)
    ys = np.zeros((4, T, D), np.float32)
    nC = np.zeros((16, NL, 2, 4, 64, 64), np.float32)
    nn = np.zeros((16, NL, 2, 4, 64), np.float32)
    nm = np.zeros((16, NL, 2, 4), np.float32)
    kv = {n: np.zeros((16, NL, 256, 2, 64), np.float32) for n in ("o_gk", "o_gv", "o_sk", "o_sv")}
    for core in range(8):
        y = np.ascontiguousarray(r[core]["yT"].T)
        if core < 4:
            ys[core] = y
            continue
        j = core - 4
        yp[4 * j:4 * j + 4] = y.reshape(4, 256, D)
        for n in ("o_gk", "o_sk"):
            a = r[core][n].reshape(NL, 2, 64, 4, 256)
            kv[n][4 * j:4 * j + 4] = a.transpose(3, 0, 4, 1, 2)
        for n in ("o_gv", "o_sv"):
            a = r[core][n].reshape(NL, 4, 256, 2, 64)
            kv[n][4 * j:4 * j + 4] = a.transpose(1, 0, 2, 3, 4)
        oc = r[core]["o_C"].reshape(NL, 2, 4, 2, 64, 2, 65)
        oc = oc.transpose(2, 0, 1, 5, 3, 4, 6).reshape(4, NL, 2, 4, 64, 65)
        nC[4 * j:4 * j + 4] = oc[..., 0:64]
        nn[4 * j:4 * j + 4] = oc[..., 64]
        nm[4 * j:4 * j + 4] = r[core]["o_m"].transpose(3, 0, 1, 2)
    return (yp, ys, nC, nn, nm, kv["o_gk"], kv["o_gv"], kv["o_sk"], kv["o_sv"])


def kernel(**inputs):
    res = run_device(inputs)
    return assemble(res)
```

```python
import os
import math
import numpy as np
import concourse.bass as bass
import concourse.mybir as mybir
from concourse.bass_utils import run_bass_kernel_spmd

F32 = mybir.dt.float32
BF16 = mybir.dt.bfloat16
I32 = mybir.dt.int32
AF = mybir.ActivationFunctionType
ALU = mybir.AluOpType
AX = mybir.AxisListType

D = 1024
T = 1024
DFF = 2816
NFF = 22
NL = 2
NIN = 2832
EPS = 1e-6
NEG = -30000.0
BIG = 29952.0
LN8 = math.log(0.125)
SLOT = 8 * 640
STAGE = int(os.environ.get("MK_STAGE", "99"))
DEBUG = bool(os.environ.get("MK_DEBUG"))
CUT = int(os.environ.get("MK_CUT", "99"))


class R:
    __slots__ = ("name", "w", "rd", "excl")

    def __init__(self, name="", excl=False):
        self.name = name
        self.w = None
        self.rd = []
        self.excl = excl


def RL(n, name=""):
    return [R("%s%d" % (name, i)) for i in range(n)]


class Sched:
    NLANES = {"sp": 8, "pool": 3, "poolw": 5}

    def __init__(self, nc):
        self.nc = nc
        self.eng = {"pe": nc.tensor, "act": nc.scalar, "dve": nc.vector,
                    "pool": nc.gpsimd, "sp": nc.sync}
        self.sem = {}
        self.cnt = {}
        for k in self.eng:
            self.sem[k] = nc.alloc_semaphore(name="s_" + k)
            self.cnt[k] = 0
        self.lanes = {}
        for q, n in self.NLANES.items():
            self.lanes[q] = []
            for i in range(n):
                key = "d_%s%d" % (q, i)
                self.sem[key] = nc.alloc_semaphore(name=key)
                self.cnt[key] = 0
                self.lanes[q].append(key)
        self.lane_rr = {q: 0 for q in self.NLANES}
        self.eng_of = {"sp": "sp", "pool": "pool", "poolw": "pool"}
        self.waited = {k: {} for k in self.eng}
        self.nwaits = 0
        self.nops = 0

    def _wait(self, e, key, val):
        if key == "pe" and e == "pe":
            return
        w = self.waited[e]
        if w.get(key, 0) >= val:
            return
        self.eng[e].wait_ge(self.sem[key], val)
        w[key] = val
        self.nwaits += 1

    def _deps(self, e, reads, writes):
        deps = {}
        for r in reads:
            if r.w is not None:
                k, v = r.w
                if deps.get(k, 0) < v:
                    deps[k] = v
            if r.excl:
                for (k, v) in r.rd:
                    if k != e and deps.get(k, 0) < v:
                        deps[k] = v
        for w in writes:
            if w.w is not None:
                k, v = w.w
                if deps.get(k, 0) < v:
                    deps[k] = v
            for (k, v) in w.rd:
                if deps.get(k, 0) < v:
                    deps[k] = v
        for k, v in deps.items():
            self._wait(e, k, v)

    def _commit(self, tok, reads, writes):
        for r in reads:
            r.rd.append(tok)
            if len(r.rd) > 48:
                mx = {}
                for k, v in r.rd:
                    if mx.get(k, 0) < v:
                        mx[k] = v
                r.rd = list(mx.items())
        for w in writes:
            w.w = tok
            w.rd = []

    def op(self, e, fn, reads=(), writes=()):
        self._deps(e, reads, writes)
        inst = fn(self.eng[e])
        self.cnt[e] += 1
        inst.then_inc(self.sem[e], 1)
        self._commit((e, self.cnt[e]), reads, writes)
        self.nops += 1

    def dma(self, q, out, in_, reads=(), writes=()):
        lanes = self.lanes[q]
        e = self.eng_of[q]
        key = lanes[self.lane_rr[q] % len(lanes)]
        self.lane_rr[q] += 1
        self._wait(e, key, self.cnt[key])
        self._deps(e, reads, writes)
        inst = self.eng[e].dma_start(out=out, in_=in_)
        self.cnt[key] += 16
        inst.then_inc(self.sem[key], 16)
        self._commit((key, self.cnt[key]), reads, writes)
        self.nops += 1

    def barrier(self):
        for e in self.eng:
            for k in self.sem:
                if self.cnt[k] > 0 and not k.startswith("d_poolw"):
                    self._wait(e, k, self.cnt[k])

    def finish(self):
        for k in self.sem:
            if self.cnt[k] > 0 and k != "sp":
                self._wait("sp", k, self.cnt[k])


def act(S, out, in_, func, reads, writes, bias=0.0, scale=1.0):
    S.op("act", lambda e: e.activation(out=out, in_=in_, func=func, bias=bias, scale=scale), reads, writes)


def tt(S, eng, out, a, b, op, reads, writes):
    S.op(eng, lambda e: e.tensor_tensor(out=out, in0=a, in1=b, op=op), reads, writes)


def ts(S, eng, out, a, s1, op0, reads, writes, s2=None, op1=None):
    if op1 is None:
        S.op(eng, lambda e: e.tensor_scalar(out=out, in0=a, scalar1=s1, scalar2=None, op0=op0), reads, writes)
    else:
        S.op(eng, lambda e: e.tensor_scalar(out=out, in0=a, scalar1=s1, scalar2=s2, op0=op0, op1=op1), reads, writes)


def cp(S, eng, out, in_, reads, writes):
    S.op(eng, lambda e: e.tensor_copy(out=out, in_=in_), reads, writes)


def recip(S, out, in_, reads, writes):
    S.op("dve", lambda e: e.reciprocal(out=out, in_=in_), reads, writes)


def mm(S, groups, reads, writes):
    def fn(e):
        inst = None
        for (o, l, r, st, sp_) in groups:
            inst = e.matmul(o, lhsT=l, rhs=r, start=st, stop=sp_)
        return inst
    S.op("pe", fn, reads, writes)


class Prog:
    def __init__(self):
        nc = bass.Bass("TRN2", target_bir_lowering=False)
        self.nc = nc
        self.S = Sched(nc)
        self.din = {}
        self.dout = {}
        self._build()

    def inp(self, name, shape):
        t = self.nc.dram_tensor(name, list(shape), F32, kind="ExternalInput").ap()
        self.din[name] = tuple(shape)
        return t

    def outp(self, name, shape):
        t = self.nc.dram_tensor(name, list(shape), F32, kind="ExternalOutput").ap()
        self.dout[name] = tuple(shape)
        return t

    def sb(self, name, shape, dt=F32):
        return self.nc.alloc_sbuf_tensor("sb_" + name, list(shape), dt)

    def arena_reset(self, to=0):
        self.aoff = to
        self.S.barrier()

    def carve(self, shape, dt=F32):
        n = 1
        for s in shape[1:]:
            n *= s
        nb = n * (4 if dt in (F32, I32) else 2)
        nb = (nb + 31) // 32 * 32
        off = self.aoff
        self.aoff += nb
        assert self.aoff <= self.ARENA_BYTES, (self.aoff, self.ARENA_BYTES)
        v = self.arena[:, off // 2:(off + nb) // 2]
        if dt != BF16:
            v = v.bitcast(dt)
        v = v[:, 0:n]
        if len(shape) == 3:
            v = v.rearrange("p (a b) -> p a b", a=shape[1])
        elif len(shape) == 4:
            v = v.rearrange("p (a b c) -> p a b c", a=shape[1], b=shape[2])
        return v

    def bank(self):
        return self.pbank()

    def prefetch(self, key, parts):
        self.pre[key] = self.wload(parts)

    def wload(self, parts, key=None):
        if key is not None and key in self.pre:
            return self.pre.pop(key)
        i = self.slot_rr % len(self.slots)
        self.slot_rr += 1
        sl, r = self.slots[i], self.slotr[i]
        for (dst_fn, src) in parts:
            self.S.dma("poolw", dst_fn(sl), src, writes=[r])
        return sl, r

    def slot_view(self, sl, kk):
        return sl[:, 0:kk * self.slot_w].rearrange("p (k n) -> p k n", k=kk)

    def _build(self):
        nc, S = self.nc, self.S
        inp, outp, sb = self.inp, self.outp, self.sb
        xT_d = inp("xT", [D, T])
        cvec_d = inp("cvec", [128, 8])
        w_ada = inp("w_ada", [NL, D, 9 * D])
        b_adaT = inp("b_adaT", [NL, 128, 72])
        gvec_d = inp("gvec", [128, NL * 24 + 8])
        wd = {}
        for nm, shp in (("w1_gate", [NL, D, DFF]), ("w1_up", [NL, D, DFF]), ("w1_down", [NL, DFF, D]),
                        ("w_in", [NL, D, NIN]), ("w_out", [NL, D, D]),
                        ("w2_gate", [NL, D, DFF]), ("w2_up", [NL, D, DFF]), ("w2_down", [NL, DFF, D])):
            wd[nm] = inp(nm, shp)
        yT_d = outp("yT", [D, T])

        xT = sb("xT", [128, 8, T], F32)
        hT = sb("hT", [128, 8, T], BF16)
        self.ARENA_BYTES = 84 * 1024
        self.arena = sb("arena", [128, self.ARENA_BYTES // 2], BF16)
        self.slot_w = 640
        self.slots = [sb("slot%d" % i, [128, SLOT], BF16) for i in range(4)]
        self.slotr = RL(4, "slot")
        self.slot_rr = 0
        self.pre = {}
        self.ps = [nc.alloc_psum_tensor("ps%d" % i, [128, 512], F32) for i in range(8)]
        self.psr = [R("ps%d" % i, excl=True) for i in range(8)]
        self.bank_rr = 0
        self.bank_pool = list(range(8))
        ones_bf = sb("ones_bf", [128, 128], BF16)
        cvec = sb("cvec_sb", [128, 8], F32)
        sc_bf = sb("sc_bf", [128, 8], BF16)
        modT = sb("modT", [128, NL, 72], F32)
        badaT = sb("badaT", [128, NL, 72], F32)
        gvec = sb("gvec_sb", [128, NL * 24 + 8], F32)
        Acoef = sb("Acoef", [128, NL, 3, 8], F32)
        Gcoef = sb("Gcoef", [128, NL, 3, 8], F32)
        sqb = sb("sqb", [128, 2, 512], BF16)
        f32s = sb("f32s", [128, 3, 512], F32)
        rstd = sb("rstd", [128, 512], F32)
        r_ones, r_cvec, r_sc, r_gvec, r_rstd = R("ones"), R("cvec"), R("sc"), R("gvec"), R("rstd")
        r_mod = RL(NL, "mod")
        r_bada = R("bada")
        r_coef = RL(NL, "coef")
        r_sq = RL(2, "sq")
        r_f32s = RL(3, "f32s")
        xr = [[R("x%d_%d" % (k, c)) for c in range(2)] for k in range(8)]
        hr = [[R("h%d_%d" % (k, c)) for c in range(2)] for k in range(8)]
        self.sq_rr = 0
        self.f32_rr = 0

        def CH(c):
            return slice(c * 512, (c + 1) * 512)

        xv = xT_d.rearrange("(k p) t -> p k t", p=128)
        for k in range(8):
            S.dma("sp", xT[:, k, :], xv[:, k, :], writes=[xr[k][0], xr[k][1]])
        S.dma("sp", cvec[:], cvec_d, writes=[r_cvec])
        S.dma("sp", gvec[:], gvec_d, writes=[r_gvec])
        for l in range(NL):
            S.dma("sp", badaT[:, l, :], b_adaT[l], writes=[r_bada])
        S.op("dve", lambda e: e.memset(ones_bf[:], 1.0), writes=[r_ones])
        act(S, sc_bf[:], cvec[:], AF.Silu, [r_cvec], [r_sc])

        mslots = [sb("mslot%d" % i, [128, 8, 256], BF16) for i in range(2)]
        r_ms = RL(2, "mslot")
        r_modls = [[R("mod%d_%d" % (l, s_)) for s_ in range(3)] for l in range(NL)]
        jobs = [(l, q) for l in range(NL) for q in range(36)]
        st = {"dma": 0, "mm": 0, "fin": set()}

        def mod_dma(j):
            l, q = jobs[j]
            wv = w_ada[l].rearrange("(k p) n -> p k n", p=128)
            S.dma("poolw", mslots[j % 2][:], wv[:, :, q * 256:(q + 1) * 256], writes=[r_ms[j % 2]])

        def mod_mm(j, bank=None):
            l, q = jobs[j]
            if bank is None:
                pb, pr = self.bank()
                c0 = 0
            else:
                pb, pr, c0 = bank
            groups = []
            for jj in range(2):
                for k in range(8):
                    groups.append((pb[:, c0 + jj:c0 + jj + 1], mslots[j % 2][:, k, jj * 128:(jj + 1) * 128], sc_bf[:, k:k + 1],
                                   k == 0, k == 7))
            mm(S, groups, [r_ms[j % 2], r_sc], [pr])
            s_ = q // 12
            tt(S, "dve", modT[:, l, q * 2:q * 2 + 2], pb[:, c0:c0 + 2], badaT[:, l, q * 2:q * 2 + 2], ALU.add,
               [pr, r_bada], [r_modls[l][s_]])

        def mod_pump(n=1, bank=None):
            for _ in range(n):
                if st["dma"] < len(jobs) and st["dma"] - st["mm"] < 2:
                    mod_dma(st["dma"])
                    st["dma"] += 1
                if st["mm"] < st["dma"] - 1 or (st["dma"] == len(jobs) and st["mm"] < st["dma"]):
                    mod_mm(st["mm"], bank)
                    st["mm"] += 1

        def mod_require(l, s_):
            last = l * 36 + s_ * 12 + 11
            while st["mm"] <= last:
                mod_pump()
            if (l, s_) in st["fin"]:
                return
            st["fin"].add((l, s_))
            r = r_modls[l][s_]
            ts(S, "dve", Acoef[:, l, s_, :], modT[:, l, (3 * s_ + 1) * 8:(3 * s_ + 2) * 8], 1.0, ALU.add, [r], [r])
            tt(S, "dve", Acoef[:, l, s_, :], Acoef[:, l, s_, :], gvec[:, l * 24 + s_ * 8:l * 24 + s_ * 8 + 8],
               ALU.mult, [r, r_gvec], [r])
            ts(S, "dve", Gcoef[:, l, s_, :], modT[:, l, (3 * s_ + 2) * 8:(3 * s_ + 3) * 8],
               0.5 if s_ != 1 else 1.0, ALU.mult, [r], [r])

        self.mod_pump = mod_pump
        self.mod_require = mod_require

        def rms_rstd(c, src_fn, src_regs, nk, inv_n, ones_l):
            pb, pr = self.bank()
            for k in range(nk):
                i = self.sq_rr % 2
                self.sq_rr += 1
                act(S, sqb[:, i, :], src_fn(k), AF.Square, [src_regs[k]], [r_sq[i]])
                mm(S, [(pb[:], ones_l, sqb[:, i, :], k == 0, k == nk - 1)], [r_sq[i], r_ones], [pr])
            act(S, rstd[:], pb[:], AF.Ln, [pr], [r_rstd], bias=EPS, scale=inv_n)
            act(S, rstd[:], rstd[:], AF.Exp, [r_rstd], [r_rstd], scale=-0.5)

        def norm_mod(l, s):
            for c in range(2):
                rms_rstd(c, lambda k: xT[:, k, CH(c)], [xr[k][c] for k in range(8)], 8, 1.0 / D, ones_bf[:])
                for k in range(8):
                    i = self.f32_rr % 3
                    self.f32_rr += 1
                    tt(S, "dve", f32s[:, i, :], xT[:, k, CH(c)], rstd[:], ALU.mult,
                       [xr[k][c], r_rstd], [r_f32s[i]])
                    act(S, hT[:, k, CH(c)], f32s[:, i, :], AF.Identity, [r_f32s[i], r_modls[l][s]],
                        [hr[k][c]], bias=modT[:, l, 3 * s * 8 + k:3 * s * 8 + k + 1],
                        scale=Acoef[:, l, s, k:k + 1])

        def resid_add(l, s, dt_, c, pb, pr):
            i = self.f32_rr % 3
            self.f32_rr += 1
            act(S, f32s[:, i, :], pb[:], AF.Copy, [pr, r_modls[l][s]], [r_f32s[i]], scale=Gcoef[:, l, s, dt_:dt_ + 1])
            tt(S, "dve", xT[:, dt_, CH(c)], xT[:, dt_, CH(c)], f32s[:, i, :], ALU.add,
               [xr[dt_][c], r_f32s[i]], [xr[dt_][c]])

        def ffn(l, s, wg, wu, wdn):
            self.arena_reset()
            aT = self.carve([128, NFF, T], BF16)
            ar = [[R("a%d_%d" % (j, c)) for c in range(2)] for j in range(NFF)]
            sg = [self.carve([128, 512], F32) for _ in range(3)]
            r_sg = RL(3, "sg")
            sg_rr = 0
            mod_require(l, s)
            norm_mod(l, s)
            wgv = wg[l].rearrange("(k p) n -> p k n", p=128)
            wuv = wu[l].rearrange("(k p) n -> p k n", p=128)
            for g in range(NFF // 2):
                c0 = g * 256
                sl, sr = self.wload(self.parts_gu(wg, wu, l, g), key=("gu", l, s, g))
                s3 = self.slot_view(sl, 8)
                if l == 0:
                    mod_pump(1)
                for jj in range(2):
                    j = g * 2 + jj
                    for c in range(2):
                        pg, prg = self.bank()
                        pu, pru = self.bank()
                        groups = []
                        for k in range(8):
                            groups.append((pg[:], s3[:, k, jj * 128:(jj + 1) * 128], hT[:, k, CH(c)], k == 0, k == 7))
                        for k in range(8):
                            groups.append((pu[:], s3[:, k, 256 + jj * 128:256 + (jj + 1) * 128], hT[:, k, CH(c)],
                                           k == 0, k == 7))
                        mm(S, groups, [sr] + [hr[k][c] for k in range(8)], [prg, pru])
                        i = sg_rr % 3
                        sg_rr += 1
                        act(S, sg[i], pg[:], AF.Silu, [prg], [r_sg[i]])
                        tt(S, "dve", aT[:, j, CH(c)], sg[i], pu[:], ALU.mult, [r_sg[i], pru], [ar[j][c]])
            wdv = wdn[l].rearrange("(j p) n -> p j n", p=128)
            for dt_ in range(8):
                sl, sr = self.wload([(lambda sl_: sl_[:, 0:NFF * 128].rearrange("p (j n) -> p j n", j=NFF),
                                      wdv[:, :, dt_ * 128:(dt_ + 1) * 128])])
                if dt_ == 7:
                    if s == 0:
                        self.prefetch(("A1", l), self.parts_win(l, 0, 512))
                        self.prefetch(("A2", l), self.parts_win(l, 512, 512))
                        self.prefetch(("G", l), self.parts_win(l, 1024, 16))
                    elif l + 1 < NL:
                        for g_ in range(2):
                            self.prefetch(("gu", l + 1, 0, g_), self.parts_gu(wd["w1_gate"], wd["w1_up"], l + 1, g_))
                s3 = sl[:, 0:NFF * 128].rearrange("p (j n) -> p j n", j=NFF)
                if l == 0:
                    mod_pump(1)
                for c in range(2):
                    pb, pr = self.bank()
                    groups = [(pb[:], s3[:, j, :], aT[:, j, CH(c)], j == 0, j == NFF - 1) for j in range(NFF)]
                    mm(S, groups, [sr] + [ar[j][c] for j in range(NFF)], [pr])
                    resid_add(l, s, dt_, c, pb, pr)

        self.wd = wd
        self.ctx = dict(xT=xT, hT=hT, xr=xr, hr=hr, CH=CH, rms_rstd=rms_rstd, norm_mod=norm_mod,
                        resid_add=resid_add, ones_bf=ones_bf, r_ones=r_ones, gvec=gvec, r_gvec=r_gvec,
                        f32s=f32s, r_f32s=r_f32s, rstd=rstd, r_rstd=r_rstd, modT=modT, sqb=sqb, r_sq=r_sq)


        self.ARENA_BYTES = 84 * 1024
        LPW = 304
        self.LP = dict(BG=0, GM=16, GQK=272, SK=274, CV=278, HB=302)
        d = {}
        d["lp"] = inp("lp", [128, NL, LPW])
        d["cflags"] = inp("cflags", [128, 4])
        d["ident"] = inp("ident", [128, 128])
        d["blockones"] = inp("blockones", [128, 128])
        d["tri"] = inp("tri", [128, 4, 128])
        d["ropeP"] = inp("ropeP", [128, 128])
        d["ropeCS"] = inp("ropeCS", [128, 2, T])
        d["swamask"] = inp("swamask", [128, 8, 384])
        d["gmAB"] = inp("gmAB", [5, 2, T])
        d["fw1"] = inp("fw1", [33, NL, 64])
        d["fw2"] = inp("fw2", [64, NL, 64])
        d["fw3"] = inp("fw3", [64, NL, 256])
        d["fvec"] = inp("fvec", [64, NL, 3])
        d["fb3"] = inp("fb3", [1, NL, 256])
        d["featsT"] = inp("featsT", [33, T])
        d["window"] = inp("window", [128, 8, 256])
        d["dftF"] = inp("dftF", [2, T, T])
        d["dftG"] = inp("dftG", [2, T, T])
        d["sel"] = inp("sel", [4, 130])
        d["C0"] = inp("C0", [NL, 2, 128, 2, 65])
        d["m0rep"] = inp("m0rep", [128, NL, 2, 2])
        d["m0c"] = inp("m0c", [4, NL * 2])
        d["gkT"] = inp("gkT", [NL, 2, 128, 256])
        d["gvp"] = inp("gvp", [NL, 2, 128, 2, 192])
        d["skT"] = inp("skT", [NL, 2, 128, 256])
        d["svp"] = inp("svp", [NL, 2, 128, 2, 192])
        d["o_gk"] = outp("o_gk", [NL, 128, T])
        d["o_gv"] = outp("o_gv", [NL, T, 128])
        d["o_sk"] = outp("o_sk", [NL, 128, T])
        d["o_sv"] = outp("o_sv", [NL, T, 128])
        d["o_C"] = outp("o_C", [NL, 2, 4, 128, 2, 65])
        d["o_m"] = outp("o_m", [NL, 2, 4, 4])
        self.d = d
        if DEBUG:
            self.dbg_ymix = outp("dbg_ymix", [NL, 128, 8, T])
        k = {}
        k["lp"] = sb("lp", [128, NL, LPW]); k["cflags"] = sb("cflags", [128, 4])
        k["ident_f"] = sb("ident_f", [128, 128]); k["ident_bf"] = sb("ident_bf", [128, 128], BF16)
        k["blockones"] = sb("blockones", [128, 128], BF16)
        k["tri"] = sb("tri", [128, 4, 128]); k["ropeP"] = sb("ropeP", [128, 128])
        k["ones_f"] = sb("ones_f", [128, 128])
        k["fw1"] = sb("fw1", [33, NL, 64]); k["fw2"] = sb("fw2", [64, NL, 64]); k["fw3"] = sb("fw3", [64, NL, 256])
        k["fvec"] = sb("fvec", [64, NL, 3]); k["fb3"] = sb("fb3", [1, NL, 256]); k["fs"] = sb("fs", [64, NL, 4])
        k["sel"] = sb("sel", [4, 130]); k["m0rep"] = sb("m0rep", [128, NL, 2, 2]); k["m0c"] = sb("m0c", [4, NL * 2])
        k["Clo"] = sb("Clo", [128, 2, 2, 65]); k["Chi"] = sb("Chi", [128, 2, 2, 65])
        k["Cblo"] = sb("Cblo", [128, 2, 2, 66], BF16); k["Cbhi"] = sb("Cbhi", [128, 2, 2, 66], BF16)
        k["cvb"] = sb("cvb", [128, NL, 6, 2])
        self.k = k
        r_k = R("consts")
        self.r_k = r_k
        for nm in ("lp", "cflags", "tri", "ropeP", "fw1", "fw2", "fw3", "fvec", "fb3", "sel", "m0rep", "m0c"):
            S.dma("sp", k[nm][:], d[nm], writes=[r_k])
        S.dma("sp", k["ident_f"][:], d["ident"], writes=[r_k])
        S.dma("pool", k["ident_bf"][:], d["ident"], writes=[r_k])
        S.dma("pool", k["blockones"][:], d["blockones"], writes=[r_k])
        S.op("pool", lambda e: e.memset(k["ones_f"][:], 1.0), writes=[r_k])
        for nm in ("Clo", "Chi", "Cblo", "Cbhi"):
            S.op("pool", lambda e, nm=nm: e.memset(k[nm][:], 0.0), writes=[r_k])
        i2p = float(1.0 / (2 * math.pi))
        ts(S, "dve", k["fs"][:, :, 0:1], k["fvec"][:, :, 2:3], i2p, ALU.mult, [r_k], [r_k])
        tt(S, "dve", k["fs"][:, :, 1:2], k["fs"][:, :, 0:1], k["fvec"][:, :, 0:1], ALU.mult, [r_k], [r_k])
        tt(S, "dve", k["fs"][:, :, 2:3], k["fs"][:, :, 0:1], k["fvec"][:, :, 1:2], ALU.mult, [r_k], [r_k])
        CV = self.LP["CV"]
        for l in range(NL):
            cvv = k["lp"][:, l, CV:CV + 24].rearrange("p (a b) -> p a b", a=6)
            for (j, col) in ((0, 0), (1, 2)):
                ts(S, "dve", k["cvb"][:, l, :, j:j + 1], cvv[:, :, col:col + 1], k["cflags"][:, 2:3], ALU.mult,
                   [r_k], [r_k], s2=-1.0, op1=ALU.mult)

        for l in range(NL):
            ffn(l, 0, wd["w1_gate"], wd["w1_up"], wd["w1_down"])
            self.mixer(l)
            ffn(l, 2, wd["w2_gate"], wd["w2_up"], wd["w2_down"])

        gfo = NL * 24
        yv = yT_d.rearrange("(k p) t -> p k t", p=128)
        self.arena_reset()
        ost_ = self.carve([128, 2, 512], F32)
        ost = [ost_[:, 0, :], ost_[:, 1, :]]
        r_ost = RL(2, "ost")
        o_rr = 0
        for c in range(2):
            rms_rstd(c, lambda k: xT[:, k, CH(c)], [xr[k][c] for k in range(8)], 8, 1.0 / D, ones_bf[:])
            for k in range(8):
                i = self.f32_rr % 3
                self.f32_rr += 1
                tt(S, "dve", f32s[:, i, :], xT[:, k, CH(c)], rstd[:], ALU.mult, [xr[k][c], r_rstd], [r_f32s[i]])
                o = o_rr % 2
                o_rr += 1
                act(S, ost[o], f32s[:, i, :], AF.Copy, [r_f32s[i], r_gvec], [r_ost[o]],
                    scale=gvec[:, gfo + k:gfo + k + 1])
                S.dma("sp", yv[:, k, CH(c)], ost[o], reads=[r_ost[o]])
        S.finish()

    def mixer(self, l):
        S = self.S
        C = self.ctx
        hT, hr, CH = C["hT"], C["hr"], C["CH"]
        self.arena_reset()
        ymix = self.carve([128, 8, T], BF16)
        ymr = [[R("ym%d_%d" % (k, c)) for c in range(2)] for k in range(8)]
        base = self.aoff
        self.mod_require(l, 1)
        C["norm_mod"](l, 1)
        win = self.wd["w_in"][l].rearrange("(k p) n -> p k n", p=128)
        self.bank_pool = list(range(8))
        self.mix_mlstm(l, ymix, ymr, win)
        self.arena_reset(base)
        if STAGE >= 3:
            self.mix_attn(l, ymix, ymr, win, glob=True)
            self.arena_reset(base)
            self.mix_attn(l, ymix, ymr, win, glob=False)
            self.arena_reset(base)
        if STAGE >= 4:
            self.mix_hyena(l, ymix, ymr, win)
        self.prefetch(("wout", l, 0), self.parts_wout(l, 0))
        self.prefetch(("wout", l, 1), self.parts_wout(l, 1))
        wd_ = self.wd
        for g in range(2):
            self.prefetch(("gu", l, 2, g), self.parts_gu(wd_["w2_gate"], wd_["w2_up"], l, g))
        self.bank_pool = list(range(8))
        if DEBUG:
            f32s, r_f32s = C["f32s"], C["r_f32s"]
            for k in range(8):
                for c in range(2):
                    i = self.f32_rr % 3
                    self.f32_rr += 1
                    act(S, f32s[:, i, :], ymix[:, k, CH(c)], AF.Copy, [ymr[k][c]], [r_f32s[i]])
                    S.dma("sp", self.dbg_ymix[l, :, k, c * 512:(c + 1) * 512], f32s[:, i, :], reads=[r_f32s[i]])
        wov = self.wd["w_out"][l].rearrange("(k p) n -> p k n", p=128)
        for half in range(2):
            sl, sr = self.wload(self.parts_wout(l, half), key=("wout", l, half))
            s3 = self.slot_view(sl, 8)
            for j in range(4):
                dt_ = half * 4 + j
                for c in range(2):
                    pb, pr = self.bank()
                    groups = [(pb[:], s3[:, k, j * 128:(j + 1) * 128], ymix[:, k, CH(c)], k == 0, k == 7) for k in range(8)]
                    mm(S, groups, [sr] + [ymr[k][c] for k in range(8)], [pr])
                    C["resid_add"](l, 1, dt_, c, pb, pr)

    def parts_attn(self, l, glob):
        win = self.wd["w_in"][l].rearrange("(k p) n -> p k n", p=128)
        sv8 = lambda sl_: self.slot_view(sl_, 8)
        q0 = 1040 if glob else 1552
        k0, v0 = q0 + 256, q0 + 384
        return [(lambda sl_: sv8(sl_)[:, :, 0:256], win[:, :, q0:q0 + 256]),
                (lambda sl_: sv8(sl_)[:, :, 256:384], win[:, :, k0:k0 + 128]),
                (lambda sl_: sv8(sl_)[:, :, 384:448], win[:, :, k0 + 64:k0 + 128]),
                (lambda sl_: sv8(sl_)[:, :, 448:512], win[:, :, k0:k0 + 64]),
                (lambda sl_: sv8(sl_)[:, :, 512:640], win[:, :, v0:v0 + 128])]

    def parts_win(self, l, c0, n):
        win = self.wd["w_in"][l].rearrange("(k p) n -> p k n", p=128)
        return [(lambda sl_: self.slot_view(sl_, 8)[:, :, 0:n], win[:, :, c0:c0 + n])]

    def parts_wout(self, l, half):
        wov = self.wd["w_out"][l].rearrange("(k p) n -> p k n", p=128)
        return [(lambda sl_: self.slot_view(sl_, 8)[:, :, 0:512], wov[:, :, half * 512:(half + 1) * 512])]

    def parts_gu(self, wg, wu, l, g):
        wgv = wg[l].rearrange("(k p) n -> p k n", p=128)
        wuv = wu[l].rearrange("(k p) n -> p k n", p=128)
        c0 = g * 256
        return [(lambda sl_: self.slot_view(sl_, 8)[:, :, 0:256], wgv[:, :, c0:c0 + 256]),
                (lambda sl_: self.slot_view(sl_, 8)[:, :, 256:512], wuv[:, :, c0:c0 + 256])]

    def pbank(self):
        i = self.bank_pool[self.bank_rr % len(self.bank_pool)]
        self.bank_rr += 1
        return self.ps[i], self.psr[i]

    def proj_fm(self, c, s3, col0, pb):
        hT, CH = self.ctx["hT"], self.ctx["CH"]
        return [(pb[:], s3[:, k, col0:col0 + 128], hT[:, k, CH(c)], k == 0, k == 7) for k in range(8)]

    def proj_tok(self, tt_, s3, col0, ncols, pb, pc0):
        hT = self.ctx["hT"]
        return [(pb[:, pc0:pc0 + ncols], hT[:, k, tt_ * 128:(tt_ + 1) * 128], s3[:, k, col0:col0 + ncols], k == 0, k == 7)
                for k in range(8)]

    def mix_mlstm(self, l, ymix, ymr, win):
        S, k_, d = self.S, self.k, self.d
        C = self.ctx
        hr, CH = C["hr"], C["CH"]
        LP = self.LP
        lp = k_["lp"]
        r_k = self.r_k
        hall = [hr[k][c] for k in range(8) for c in range(2)]
        sv8 = lambda sl_: self.slot_view(sl_, 8)
        slA1, srA1 = self.wload(self.parts_win(l, 0, 512), key=("A1", l))
        slA2, srA2 = self.wload(self.parts_win(l, 512, 512), key=("A2", l))
        slG, srG = self.wload(self.parts_win(l, 1024, 16), key=("G", l))
        sA1, sA2, sG = sv8(slA1), sv8(slA2), sv8(slG)
        self.prefetch(("attn", l, True), self.parts_attn(l, True))
        if CUT == 1:
            return
        aqT = self.carve([128, 2, T], BF16)
        akp = self.carve([128, 4, T], BF16)
        ktok = self.carve([128, 8, 256], BF16)
        vaug = self.carve([128, 8, 4, 66], BF16)
        sgo = self.carve([128, 8, 256], BF16)
        gts = self.carve([128, 8, 16], F32)
        lf = self.carve([128, 8, 8], F32)
        cum = self.carve([128, 8, 16], F32)
        call = self.carve([128, 8, 8], F32)
        wall = self.carve([128, 8, 8], F32)
        wkl = self.carve([128, 8, 8], F32)
        wkall = self.carve([128, 8, 8], F32)
        wkm = self.carve([128, 8, 8], F32)
        dec = self.carve([128, 8, 4], F32)
        lfB = self.carve([128, 8, 128], F32)
        E = self.carve([128, 8, 128], F32)
        AT = self.carve([128, 2, 4, 128], BF16)
        hf = self.carve([128, 8, 256], F32)
        numt = self.carve([128, 2, 260], F32)
        h64 = self.carve([128, 2, 256], F32)
        dsm = self.carve([128, 2, 8], F32)
        kwp = self.carve([128, 2, 4, 192], BF16)
        yatok = self.carve([128, 8, 256], BF16)
        snap = self.carve([128, 2, 4, 130], F32)
        e0 = self.carve([128, 2, 2], F32)
        ssall = self.carve([128, 8, 4], F32)
        sq1 = self.carve([128, 2, 256], F32)
        mst = self.carve([128, 64], F32)
        scl = self.carve([128, 2, 8], F32)
        r_aq, r_akp = RL(2, "aq"), RL(2, "akp")
        r_ktok, r_vaug, r_sgo, r_gts = RL(8, "ktok"), RL(8, "vaug"), RL(8, "sgo"), R("gts")
        r_gate = R("gate")
        r_lfB, r_E, r_AT = RL(2, "lfB"), RL(2, "E"), RL(2, "AT")
        r_hf = RL(8, "hf")
        r_num, r_tmpn, r_h64, r_dsm, r_kwp = RL(2, "num"), RL(2, "tmpn"), RL(2, "h64"), RL(2, "dsm"), RL(2, "kwp")
        r_C, r_Cb = RL(2, "C"), RL(2, "Cb")
        r_snap = [[R("snap") for _ in range(4)] for _ in range(2)]
        r_ms, r_scl = R("mst"), R("scl")
        r_ya = RL(8, "ya")
        r_ss = R("ss")
        r_sq1 = RL(2, "sq1")
        S.op("dve", lambda e: e.memset(vaug[:, :, :, 64:65], 1.0), writes=r_vaug)
        S.op("dve", lambda e: e.memset(akp, 0.0), writes=r_akp)
        S.op("dve", lambda e: e.memset(kwp, 0.0), writes=r_kwp)
        if CUT == 2:
            return
        for t2 in range(2):
            for c in range(2):
                pb, pr = self.pbank()
                mm(S, self.proj_fm(c, sA1, t2 * 128, pb), [srA1] + hall, [pr])
                act(S, aqT[:, t2, CH(c)], pb[:], AF.Copy, [pr], [r_aq[c]])
        if CUT == 21:
            return
        for t2 in range(2):
            for c in range(2):
                pb, pr = self.pbank()
                mm(S, self.proj_fm(c, sA1, 256 + t2 * 128, pb), [srA1] + hall, [pr])
                act(S, akp[0:64, 2 * t2, CH(c)], pb[0:64, :], AF.Copy, [pr], [r_akp[c]])
                cp(S, "dve", akp[64:128, 2 * t2 + 1, CH(c)], pb[64:128, :], [pr], [r_akp[c]])
        if CUT == 22:
            return
        BG = LP["BG"]
        for t_ in range(8):
            p1, pr1 = self.pbank()
            p2, pr2 = self.pbank()
            g = self.proj_tok(t_, sA1, 256, 256, p1, 0) + self.proj_tok(t_, sA2, 0, 256, p1, 256)
            g += self.proj_tok(t_, sA2, 256, 256, p2, 0) + self.proj_tok(t_, sG, 0, 16, p2, 256)
            mm(S, g, [srA1, srA2, srG] + hall, [pr1, pr2])
            if CUT == 23:
                continue
            act(S, ktok[:, t_, :], p1[:, 0:256], AF.Copy, [pr1], [r_ktok[t_]])
            cp(S, "dve", vaug[:, t_, :, 0:64], p1[:, 256:512].rearrange("p (a b) -> p a b", a=4), [pr1], [r_vaug[t_]])
            if CUT == 24:
                continue
            act(S, sgo[:, t_, :], p2[:, 0:256], AF.Sigmoid, [pr2], [r_sgo[t_]])
            tt(S, "pool", sgo[:, t_, :], sgo[:, t_, :], lp[:, l, LP["GM"]:LP["GM"] + 256], ALU.mult, [r_sgo[t_], r_k], [r_sgo[t_]])
            tt(S, "dve", gts[:, t_, :], p2[:, 256:272], lp[:, l, BG:BG + 16], ALU.add, [pr2, r_k], [r_gts])
            self.mod_pump()
        if CUT in (3, 23, 24):
            return
        ai, af = gts[:, :, 0:8], gts[:, :, 8:16]
        act(S, lf, af, AF.Exp, [r_gts], [r_gate], scale=-1.0)
        act(S, lf, lf, AF.Ln, [r_gate], [r_gate], bias=1.0)
        ts(S, "dve", lf, lf, -1.0, ALU.mult, [r_gate], [r_gate])
        tri = k_["tri"]
        pbc, prc = self.pbank()
        g = []
        for t_ in range(8):
            g.append((pbc[:, t_ * 16:t_ * 16 + 4], tri[:, 0, :], lf[:, t_, 0:4], True, True))
            g.append((pbc[:, t_ * 16 + 4:t_ * 16 + 8], tri[:, 1, :], lf[:, t_, 4:8], True, True))
            g.append((pbc[:, t_ * 16 + 8:t_ * 16 + 16], k_["ones_f"][:], lf[:, t_, 0:8], True, True))
        mm(S, g, [r_gate, r_k], [prc])
        cp(S, "dve", cum, pbc[:, 0:128].rearrange("p (a b) -> p a b", a=8), [prc], [r_gate])
        bc, bt = cum[:, :, 0:8], cum[:, :, 8:16]
        tt(S, "dve", call, ai, bc, ALU.subtract, [r_gts, r_gate], [r_gate])
        ts(S, "dve", call, call, LN8, ALU.add, [r_gate], [r_gate])
        act(S, wall, bc, AF.Exp, [r_gate], [r_gate])
        tt(S, "dve", wkl, call, bt, ALU.add, [r_gate], [r_gate])
        act(S, wkall, wkl, AF.Exp, [r_gate], [r_gate])
        ts(S, "dve", wkm, wkl, -LN8, ALU.add, [r_gate], [r_gate])
        for g_ in range(2):
            rows = slice(g_ * 64, (g_ + 1) * 64)
            act(S, dec[rows, :, :], cum[rows, :, 8 + g_:16:2], AF.Exp, [r_gate], [r_gate])
        if CUT == 4:
            return
        ident_f = k_["ident_f"]
        for dr in range(2):
            pbm, prm = self.pbank()
            g = [(pbm[0:4, t_:t_ + 1], lf[:, t_, dr * 4:dr * 4 + 4], k_["ones_f"][:, 0:1], True, True) for t_ in range(8)]
            mm(S, g, [r_gate, r_k], [prm])
            cp(S, "dve", mst[0:4, dr * 8:dr * 8 + 8], pbm[0:4, 0:8], [prm], [r_ms])
            for hh in range(2):
                pbt, prt = self.pbank()
                for q in range(4):
                    t_ = hh * 4 + q
                    S.op("pe", lambda e, t_=t_, q=q, pbt=pbt: e.transpose(pbt[0:4, q * 128:(q + 1) * 128],
                                                                          wkm[:, t_, dr * 4:dr * 4 + 4], ident_f[:]),
                         [r_gate, r_k], [prt])
                S.op("dve", lambda e, pbt=pbt, hh=hh: e.tensor_reduce(
                    out=mst[0:4, 16 + dr * 8 + hh * 4:16 + dr * 8 + hh * 4 + 4],
                    in_=pbt[0:4, :].rearrange("p (a b) -> p a b", a=4), axis=AX.X, op=ALU.max), [prt], [r_ms])
            bv_ = mst[0:4, dr * 8:dr * 8 + 8].rearrange("p (a b) -> p a b", a=4)
            av_ = mst[0:4, 16 + dr * 8:16 + dr * 8 + 8].rearrange("p (a b) -> p a b", a=4)
            fi, se = (0, 1) if dr == 0 else (1, 0)
            mf = mst[0:4, 32 + dr * 4:32 + dr * 4 + 4]
            ts(S, "dve", mf, bv_[:, :, fi], k_["m0c"][0:4, l * 2 + dr:l * 2 + dr + 1], ALU.add, [r_ms, r_k], [r_ms])
            tt(S, "dve", mf, mf, av_[:, :, fi], ALU.max, [r_ms], [r_ms])
            tt(S, "dve", mf, mf, bv_[:, :, se], ALU.add, [r_ms], [r_ms])
            tt(S, "dve", mf, mf, av_[:, :, se], ALU.max, [r_ms], [r_ms])
            S.dma("sp", d["o_m"][l, dr], mf, reads=[r_ms])
            en = mst[0:4, 40 + dr * 4:40 + dr * 4 + 4]
            act(S, en, mf, AF.Exp, [r_ms], [r_ms], scale=-1.0)
            rhs2 = mst[0:4, 48 + dr * 8:48 + dr * 8 + 8]
            tt(S, "dve", rhs2.rearrange("p (a b) -> p a b", a=4), en.unsqueeze(2).to_broadcast([4, 4, 2]),
               k_["sel"][0:4, 128:130].unsqueeze(1).to_broadcast([4, 4, 2]), ALU.mult, [r_ms, r_k], [r_ms])
            pbs, prs = self.pbank()
            mm(S, [(pbs[:, 0:8], k_["sel"][0:4, 0:128], rhs2, True, True)], [r_ms, r_k], [prs])
            cp(S, "dve", scl[:, dr, :], pbs[:, 0:8], [prs], [r_scl])
        if CUT == 5:
            return
        Clo, Chi, Cblo, Cbhi = k_["Clo"], k_["Chi"], k_["Cblo"], k_["Cbhi"]
        act(S, e0, k_["m0rep"][:, l, :, :], AF.Exp, [r_k], [r_gate])
        halves = ((slice(0, 64), Clo, Cblo), (slice(64, 128), Chi, Cbhi))
        for dr in range(2):
            S.dma("sp", Clo[:, dr, :, :], d["C0"][l, dr, :, :, :], writes=[r_C[dr]])
            tt(S, "dve", Clo[:, dr, :, :], Clo[:, dr, :, :], e0[:, dr, :].unsqueeze(2).to_broadcast([128, 2, 65]),
               ALU.mult, [r_C[dr], r_gate], [r_C[dr]])
            for (rows, Cx, Cbx) in halves:
                act(S, Cbx[rows, dr, :, 0:65], Clo[rows, dr, :, :], AF.Copy, [r_C[dr]], [r_Cb[dr]])
        keep = k_["cflags"][:, 1:2]

        tmpn2 = self.carve([128, 2, 2, 260], F32)
        r_tmpn2 = [[R("tn00"), R("tn01")], [R("tn10"), R("tn11")]]
        v3 = lambda ap: ap.rearrange("p (a b) -> p a b", a=4)

        def state_gen(dr, t_):
            tok = slice(t_ * 128, (t_ + 1) * 128)
            chs = slice(dr * 4, dr * 4 + 4)
            b_ = t_ % 2
            pst, prst = self.ps[3 + 4 * dr], self.psr[3 + 4 * dr]
            tt(S, "pool", kwp[:, dr, :, 64:128], ktok[:, t_, :].rearrange("p (a b) -> p a b", a=4),
               wkall[:, t_, chs].unsqueeze(2).to_broadcast([128, 4, 64]), ALU.mult, [r_ktok[t_], r_gate], [r_kwp[dr]])
            yield
            g = [(pst[:, h * 65:(h + 1) * 65], aqT[:, h // 2, tok], (Cblo if h % 2 == 0 else Cbhi)[:, dr, h // 2, 0:65], True, True)
                 for h in range(4)]
            for j in range(2):
                o = pst[:, 260 + j * 65:260 + (j + 1) * 65]
                g.append((o, kwp[:, dr, 2 * j, 64:192], vaug[:, t_, 2 * j, 0:65], True, False))
                g.append((o, kwp[:, dr, 2 * j + 1, 0:128], vaug[:, t_, 2 * j + 1, 0:65], False, True))
            mm(S, g, r_aq + [r_Cb[dr], r_kwp[dr], r_vaug[t_]], [prst])
            yield
            tt(S, "dve", v3(tmpn2[:, dr, b_, :]), v3(pst[:, 0:260]), wall[:, t_, chs].unsqueeze(2).to_broadcast([128, 4, 65]),
               ALU.mult, [prst, r_gate], [r_tmpn2[dr][b_]])
            yield
            tt(S, "dve", Clo[:, dr, :, :], Clo[:, dr, :, :],
               dec[:, t_, dr * 2:dr * 2 + 2].unsqueeze(2).to_broadcast([128, 2, 65]), ALU.mult,
               [r_C[dr], r_gate], [r_C[dr]])
            yield
            tt(S, "dve", Clo[:, dr, :, :], Clo[:, dr, :, :], pst[:, 260:390].rearrange("p (a b) -> p a b", a=2),
               ALU.add, [r_C[dr], prst], [r_C[dr]])
            yield
            end = (t_ % 2 == 1) if dr == 0 else (t_ % 2 == 0)
            if end:
                sq_ = t_ // 2
                tt(S, "dve", snap[:, dr, sq_, :].rearrange("p (a b) -> p a b", a=2), Clo[:, dr, :, :],
                   scl[:, dr, sq_ * 2:sq_ * 2 + 2].unsqueeze(2).to_broadcast([128, 2, 65]), ALU.mult,
                   [r_C[dr], r_scl], [r_snap[dr][sq_]])
                yield
                S.dma("sp", d["o_C"][l, dr, sq_], snap[:, dr, sq_, :].rearrange("p (a b) -> p a b", a=2),
                      reads=[r_snap[dr][sq_]])
                yield
                ts(S, "dve", Clo[:, dr, :, :], Clo[:, dr, :, :], keep, ALU.mult, [r_C[dr], r_k], [r_C[dr]])
                yield
            for (rows, Cx, Cbx) in halves:
                act(S, Cbx[rows, dr, :, 0:65], Clo[rows, dr, :, :], AF.Copy, [r_C[dr]], [r_Cb[dr]])
                yield

        def out_gen(dr, t_):
            tok = slice(t_ * 128, (t_ + 1) * 128)
            chs = slice(dr * 4, dr * 4 + 4)
            b_ = t_ % 2
            pbe, pre = self.ps[0 + 4 * dr], self.psr[0 + 4 * dr]
            pbs_, prs_ = self.ps[1 + 4 * dr], self.psr[1 + 4 * dr]
            pbi, pri = self.ps[2 + 4 * dr], self.psr[2 + 4 * dr]
            act(S, lfB[:, chs, :], lf[:, t_, chs].unsqueeze(2).to_broadcast([128, 4, 128]), AF.Copy, [r_gate], [r_lfB[dr]])
            yield
            g = []
            for h in range(4):
                o = pbe[:, h * 128:(h + 1) * 128]
                g.append((o, lfB[:, dr * 4 + h, :], tri[:, dr, :], True, False))
                g.append((o, ident_f[:], tri[:, 2 + dr, :], False, True))
            mm(S, g, [r_lfB[dr], r_k], [pre])
            yield
            g = [(pbs_[:, h * 128:(h + 1) * 128], akp[:, h, tok], aqT[:, h // 2, tok], True, True) for h in range(4)]
            mm(S, g, r_akp + r_aq, [prs_])
            yield
            for h in range(4):
                act(S, E[:, dr * 4 + h, :], pbe[:, h * 128:(h + 1) * 128], AF.Exp, [pre, r_gate], [r_E[dr]],
                    bias=call[:, t_, dr * 4 + h:dr * 4 + h + 1])
                yield
            tt(S, "dve", AT[:, dr, :, :], pbs_[:].rearrange("p (a b) -> p a b", a=4), E[:, chs, :], ALU.mult,
               [prs_, r_E[dr]], [r_AT[dr]])
            yield
            g = [(pbi[:, h * 65:(h + 1) * 65], AT[:, dr, h, :], vaug[:, t_, h, 0:65], True, True) for h in range(4)]
            mm(S, g, [r_AT[dr], r_vaug[t_]], [pri])
            yield
            tt(S, "dve", numt[:, dr, :], pbi[:, 0:260], tmpn2[:, dr, b_, :], ALU.add, [pri, r_tmpn2[dr][b_]], [r_num[dr]])
            yield
            nv = v3(numt[:, dr, :])
            dn, rd = dsm[:, dr, 0:4], dsm[:, dr, 4:8]
            ts(S, "dve", dn, nv[:, :, 64], -1.0, ALU.mult, [r_num[dr]], [r_dsm[dr]], s2=1.0, op1=ALU.max)
            yield
            tt(S, "dve", dn, dn, nv[:, :, 64], ALU.max, [r_num[dr], r_dsm[dr]], [r_dsm[dr]])
            yield
            recip(S, rd, dn, [r_dsm[dr]], [r_dsm[dr]])
            yield
            first = (t_ <= 3) if dr == 0 else (t_ >= 4)
            if first:
                tt(S, "dve", v3(hf[:, t_, :]), nv[:, :, 0:64], rd.unsqueeze(2).to_broadcast([128, 4, 64]), ALU.mult,
                   [r_num[dr], r_dsm[dr]], [r_hf[t_]])
                yield
            else:
                tt(S, "dve", v3(h64[:, dr, :]), nv[:, :, 0:64], rd.unsqueeze(2).to_broadcast([128, 4, 64]), ALU.mult,
                   [r_num[dr], r_dsm[dr]], [r_h64[dr]])
                yield
                tt(S, "dve", hf[:, t_, :], hf[:, t_, :], h64[:, dr, :], ALU.add, [r_h64[dr], r_hf[t_]], [r_hf[t_]])
                yield

        def zip_run(gens):
            gens = list(gens)
            while gens:
                for g__ in list(gens):
                    try:
                        next(g__)
                    except StopIteration:
                        gens.remove(g__)

        def pump_gen(npump):
            for _ in range(npump):
                for _ in range(5):
                    yield
                self.mod_pump(bank=(self.ps[2], self.psr[2], 300))
                yield

        order = [list(range(8)), list(range(7, -1, -1))]
        zip_run([state_gen(0, order[0][0]), state_gen(1, order[1][0])])
        for i in range(8):
            gens = [out_gen(0, order[0][i]), out_gen(1, order[1][i])]
            if i + 1 < 8:
                gens = [state_gen(0, order[0][i + 1]), state_gen(1, order[1][i + 1])] + gens
            gens.append(pump_gen(1))
            zip_run(gens)
        GM = LP["GM"]
        for t_ in range(8):
            b_ = t_ % 2
            tt(S, "dve", sq1[:, b_, :], hf[:, t_, :], hf[:, t_, :], ALU.mult, [r_hf[t_]], [r_sq1[b_]])
            S.op("dve", lambda e, t_=t_, b_=b_: e.tensor_reduce(out=ssall[:, t_, :],
                                                              in_=sq1[:, b_, :].rearrange("p (a b) -> p a b", a=4),
                                                              axis=AX.X, op=ALU.add), [r_sq1[b_]], [r_ss])
        act(S, ssall, ssall, AF.Ln, [r_ss], [r_ss], bias=EPS, scale=1.0 / 64)
        act(S, ssall, ssall, AF.Exp, [r_ss], [r_ss], scale=-0.5)
        ident_bf = k_["ident_bf"]
        for t_ in range(8):
            b_ = t_ % 2
            tt(S, "dve", sq1[:, b_, :].rearrange("p (a b) -> p a b", a=4), hf[:, t_, :].rearrange("p (a b) -> p a b", a=4),
               ssall[:, t_, :].unsqueeze(2).to_broadcast([128, 4, 64]), ALU.mult, [r_hf[t_], r_ss], [r_sq1[b_]])
            tt(S, "dve", yatok[:, t_, :], sq1[:, b_, :], sgo[:, t_, :], ALU.mult, [r_sgo[t_], r_sq1[b_]], [r_ya[t_]])
            self.mod_pump()
            pbt, prt = self.pbank()
            pv = pbt[:].bitcast(BF16)
            for t2 in range(2):
                S.op("pe", lambda e, t2=t2, pv=pv, t_=t_: e.transpose(pv[:, t2 * 128:(t2 + 1) * 128],
                                                                      yatok[:, t_, t2 * 128:(t2 + 1) * 128], ident_bf[:]),
                     [r_ya[t_], r_k], [prt])
            c = t_ // 4
            for t2 in range(2):
                if t2 == 0:
                    act(S, ymix[:, t2, t_ * 128:(t_ + 1) * 128], pv[:, t2 * 128:(t2 + 1) * 128], AF.Copy, [prt], [ymr[t2][c]])
                else:
                    cp(S, "dve", ymix[:, t2, t_ * 128:(t_ + 1) * 128], pv[:, t2 * 128:(t2 + 1) * 128], [prt], [ymr[t2][c]])

    def mix_attn(self, l, ymix, ymr, win, glob):
        S, k_, d = self.S, self.k, self.d
        C = self.ctx
        hr, CH = C["hr"], C["CH"]
        LP = self.LP
        lp = k_["lp"]
        r_k = self.r_k
        hall = [hr[k][c] for k in range(8) for c in range(2)]
        sv8 = lambda sl_: self.slot_view(sl_, 8)
        q0 = 1040 if glob else 1552
        k0, v0 = q0 + 256, q0 + 384
        sl, sr = self.wload(self.parts_attn(l, glob), key=("attn", l, glob))
        if glob:
            self.prefetch(("attn", l, False), self.parts_attn(l, False))
        else:
            self.prefetch(("D1", l), self.parts_win(l, 2064, 512))
            self.prefetch(("D2", l), self.parts_win(l, 2576, 256))
        s3 = sv8(sl)
        ymb = 2 if glob else 4
        cs = self.carve([128, 2, T], F32)
        qpad = self.carve([128, 4, T], BF16)
        kfull = self.carve([128, 2, 1280], BF16)
        vpad = self.carve([128, 10, 2, 192], BF16)
        kout = self.carve([128, T], F32)
        vout = self.carve([128, 8, 128], F32)
        PT = self.carve([128, 3, 512], BF16)
        rden = self.carve([128, 2, 512], F32)
        raw = self.carve([128, 2, 512], F32)
        tb = self.carve([128, 2, 512], F32)
        gm = self.carve([128, 2, T], BF16)
        smask = self.carve([128, 8, 384], BF16) if not glob else None
        es = self.carve([128, 4], F32)
        r_cs, r_qp, r_kf, r_vp, r_kout, r_vout = R("cs"), RL(2, "qp"), R("kf"), RL(10, "vp"), R("kout"), R("vout")
        r_PT, r_rden, r_raw, r_tb, r_gm, r_sm, r_es = RL(3, "PT"), RL(2, "rden"), RL(2, "raw"), RL(2, "tb"), R("gm"), R("sm"), R("es")
        S.dma("sp", cs, d["ropeCS"], writes=[r_cs])
        S.op("dve", lambda e: e.memset(qpad, 0.0), writes=r_qp)
        S.op("dve", lambda e: e.memset(vpad[:, 2:10, :, :], 0.0), writes=r_vp[2:])
        kTd, vpd = (d["gkT"], d["gvp"]) if glob else (d["skT"], d["svp"])
        for x in range(2):
            S.dma("pool", kfull[:, x, 0:256], kTd[l, x], writes=[r_kf])
            S.dma("pool", vpad[:, x, :, :], vpd[l, x], writes=[r_vp[x]])
        if glob:
            S.dma("pool", gm[0:5, :, :], d["gmAB"], writes=[r_gm])
        if not glob:
            S.dma("pool", smask, d["swamask"], writes=[r_sm])
            SK = LP["SK"]
            act(S, es, lp[:, l, SK:SK + 4], AF.Exp, [r_k], [r_es])
        GQK = LP["GQK"]
        ropeP = k_["ropeP"]
        rstd2 = self.carve([128, 2, 512], F32)
        r_rstd2 = RL(2, "rstd2")

        def prelude(ti, c, b_):
            col0 = ti * 128
            pb, pr = self.pbank()
            mm(S, self.proj_fm(c, s3, col0, pb), [sr] + hall, [pr])
            yield
            rw = raw[:, b_, :]
            if glob:
                i = self.sq_rr % 2
                self.sq_rr += 1
                sqb, r_sq = C["sqb"], C["r_sq"]
                act(S, sqb[:, i, :], pb[:], AF.Square, [pr], [r_sq[i]])
                yield
                cp(S, "dve", rw, pb[:], [pr], [r_raw[b_]])
                yield
                p2, pr2 = self.pbank()
                mm(S, [(p2[:], k_["blockones"][:], sqb[:, i, :], True, True)], [r_sq[i], r_k], [pr2])
                yield
                rstd, r_rstd = rstd2[:, b_, :], r_rstd2[b_]
                act(S, rstd, p2[:], AF.Ln, [pr2], [r_rstd], bias=EPS, scale=1.0 / 64)
                yield
                act(S, rstd, rstd, AF.Exp, [r_rstd], [r_rstd], scale=-0.5)
                yield
                tt(S, "dve", rw, rw, rstd, ALU.mult, [r_raw[b_], r_rstd], [r_raw[b_]])
                yield
                gcol = GQK + (0 if ti < 2 else 1)
                act(S, rw, rw, AF.Copy, [r_raw[b_], r_k], [r_raw[b_]], scale=lp[:, l, gcol:gcol + 1])
                yield
            else:
                act(S, rw, pb[:], AF.Copy, [pr], [r_raw[b_]])
                yield
            p3, pr3 = self.pbank()
            mm(S, [(p3[:], ropeP[:], rw, True, True)], [r_raw[b_], r_k], [pr3])
            yield
            ta, tb_ = rw, tb[:, b_, :]
            tt(S, "dve", ta, rw, cs[:, 0, CH(c)], ALU.mult, [r_raw[b_], r_cs], [r_raw[b_]])
            yield
            tt(S, "dve", tb_, p3[:], cs[:, 1, CH(c)], ALU.mult, [pr3, r_cs], [r_tb[b_]])
            yield
            if ti < 2:
                for g_ in range(2):
                    rows = slice(g_ * 64, (g_ + 1) * 64)
                    tt(S, "dve", qpad[rows, 2 * ti + g_, CH(c)], ta[rows, :], tb_[rows, :], ALU.add, [r_tb[b_], r_raw[b_]], [r_qp[c]])
                    yield
            elif ti == 2:
                tt(S, "dve", kout[:, CH(c)], ta, tb_, ALU.add, [r_tb[b_], r_raw[b_]], [r_kout])
                yield
                act(S, kfull[:, 0, 256 + c * 512:256 + (c + 1) * 512], kout[:, CH(c)], AF.Copy, [r_kout], [r_kf])
                yield
            else:
                tt(S, "dve", kfull[:, 1, 256 + c * 512:256 + (c + 1) * 512], ta, tb_, ALU.add, [r_tb[b_], r_raw[b_]], [r_kf])
                yield

        its = [(ti, c) for ti in range(4) for c in range(2)]
        for i0 in range(0, 8, 2):
            gens = [prelude(its[i0][0], its[i0][1], 0), prelude(its[i0 + 1][0], its[i0 + 1][1], 1)]
            while gens:
                for g__ in list(gens):
                    try:
                        next(g__)
                    except StopIteration:
                        gens.remove(g__)
        S.dma("sp", (d["o_gk"] if glob else d["o_sk"])[l], kout, reads=[r_kout])
        for t_ in range(8):
            pb, pr = self.pbank()
            mm(S, self.proj_tok(t_, s3, 512, 128, pb, 0), [sr] + hall, [pr])
            act(S, vout[:, t_, :], pb[:, 0:128], AF.Copy, [pr], [r_vout])
            cp(S, "dve", vpad[:, 2 + t_, :, 64:128], pb[:, 0:128].rearrange("p (a b) -> p a b", a=2), [pr], [r_vp[2 + t_]])
        S.dma("sp", (d["o_gv"] if glob else d["o_sv"])[l].rearrange("(t p) n -> p t n", p=128), vout, reads=[r_vout])
        self.bank_pool = [0, 1, 2, 3]
        ones_bf = C["ones_bf"]
        r_ones = C["r_ones"]
        ident_bf = k_["ident_bf"]
        ctxb = k_["cflags"][:, 0:1]
        pt_rr = 0
        acc_rr = 0
        for h in range(4):
            kv, g_ = h // 2, h % 2
            kx = 0 if g_ == kv else 1
            rows = slice(g_ * 64, (g_ + 1) * 64)
            vw = slice(64, 192) if g_ == 0 else slice(0, 128)
            for c in range(2):
                if glob:
                    tiles = [(mt, 0, 512) for mt in range(10)]
                else:
                    tiles = [(0, 0, 512), (1, 0, 512)]
                    for j in range(8):
                        lo, hi = max((j - 1) * 128, c * 512), min((j + 2) * 128, (c + 1) * 512)
                        if hi > lo:
                            tiles.append((2 + j, lo - c * 512, hi - c * 512))
                ai_ = 4 + 2 * (acc_rr % 2)
                acc_rr += 1
                pn, prn, pd, prd = self.ps[ai_], self.psr[ai_], self.ps[ai_ + 1], self.psr[ai_ + 1]
                pend = []

                def qk(idx):
                    mt, lo, hi = tiles[idx]
                    pb, pr = self.pbank()
                    qs = slice(c * 512 + lo, c * 512 + hi)
                    g = [(pb[:, lo:hi], kfull[:, kx, mt * 128:(mt + 1) * 128], qpad[:, h, qs], True, mt < 2)]
                    rd = [r_kf, r_qp[c]]
                    if mt >= 2:
                        if glob:
                            g.append((pb[:, lo:hi], gm[0:5, 0, (mt - 2) * 128:(mt - 1) * 128], gm[0:5, 1, qs], False, True))
                            rd.append(r_gm)
                        else:
                            j = mt - 2
                            m0_ = c * 512 + lo - (j - 1) * 128
                            g.append((pb[:, lo:hi], ident_bf[:], smask[:, j, m0_:m0_ + (hi - lo)], False, True))
                            rd += [r_sm, r_k]
                    mm(S, g, rd, [pr])
                    return pb, pr

                nt = len(tiles)
                look = 2
                for idx in range(min(look, nt)):
                    pend.append(qk(idx))
                for idx in range(nt):
                    mt, lo, hi = tiles[idx]
                    pb, pr = pend.pop(0)
                    if idx + look < nt:
                        pend.append(qk(idx + look))
                    pi = pt_rr % 3
                    pt_rr += 1
                    act(S, PT[:, pi, lo:hi], pb[:, lo:hi], AF.Exp, [pr, r_k], [r_PT[pi]],
                        bias=(ctxb if mt < 2 else 0.0), scale=0.125)
                    g = [(pn[:, lo:hi], vpad[:, mt, kv, vw], PT[:, pi, lo:hi], idx == 0, idx == nt - 1),
                         (pd[:, lo:hi], ones_bf[:], PT[:, pi, lo:hi], idx == 0, idx == nt - 1)]
                    mm(S, g, [r_vp[mt], r_PT[pi], r_ones], [prn, prd])
                ri = (h * 2 + c) % 2
                if glob:
                    recip(S, rden[rows, ri, :], pd[rows, :], [prd], [r_rden[ri]])
                else:
                    ts(S, "dve", rden[rows, ri, :], pd[rows, :], es[rows, h:h + 1], ALU.add, [prd, r_es], [r_rden[ri]])
                    recip(S, rden[rows, ri, :], rden[rows, ri, :], [r_rden[ri]], [r_rden[ri]])
                tt(S, "dve", ymix[rows, ymb + kv, CH(c)], pn[rows, :], rden[rows, ri, :], ALU.mult, [prn, r_rden[ri]],
                   [ymr[ymb + kv][c]])
        self.bank_pool = list(range(8))

    def mix_hyena(self, l, ymix, ymr, win):
        S, k_, d = self.S, self.k, self.d
        C = self.ctx
        hr, CH = C["hr"], C["CH"]
        LP = self.LP
        lp = k_["lp"]
        r_k = self.r_k
        hall = [hr[k][c] for k in range(8) for c in range(2)]
        sv8 = lambda sl_: self.slot_view(sl_, 8)
        TWO_PI = float(2 * math.pi)
        xoff = self.aoff
        feats = self.carve([128, T], F32)
        z1 = self.carve([128, T], F32)
        z2 = self.carve([128, T], F32)
        wnd = self.carve([128, 8, 256], F32)
        rr = self.carve([128, 512], F32)
        ii = self.carve([128, 512], I32)
        kf = self.carve([128, 512], F32)
        xend = self.aoff
        self.aoff = xoff
        raw = self.carve([128, 3, T], F32)
        uct = self.carve([128, 2, T], F32)
        pa = self.carve([128, 2, 256], F32)
        pq = self.carve([128, 2, 256], F32)
        yt = self.carve([128, 2, 256], F32)
        assert self.aoff <= xend
        self.aoff = xend
        r_X = R("X")
        x0 = self.carve([128, 2, T], F32)
        zf = self.carve([128, 2, T], F32)
        zbf = self.carve([128, 2, T], BF16)
        zh = self.carve([128, 8, 512], BF16)
        ZH = self.carve([128, 2, 512], F32)
        Y = self.carve([128, 8, 2, 256], BF16)
        pa2 = self.carve([128, 2, 256], F32)
        yt2 = ZH[:, :, 0:256]
        r_x0, r_zf, r_zbf = RL(2, "x0"), RL(2, "zf"), RL(2, "zbf")
        r_zh = RL(8, "zh")
        r_ZH, r_Y, r_pa, r_pq, r_yt = R("ZH"), RL(8, "Y"), R("pa"), R("pq"), RL(2, "yt")
        S.dma("sp", feats[0:33, :], d["featsT"], writes=[r_X])
        S.dma("sp", wnd, d["window"], writes=[r_X])
        fs = k_["fs"]

        def sin_layer(pb, pr, dst, bcol):
            ts(S, "dve", rr[0:64, :], pb[0:64, :], fs[:, l, 0:1], ALU.mult, [pr, r_k], [r_X], s2=fs[:, l, bcol:bcol + 1],
               op1=ALU.add)
            cp(S, "dve", ii[0:64, :], rr[0:64, :], [r_X], [r_X])
            cp(S, "dve", kf[0:64, :], ii[0:64, :], [r_X], [r_X])
            tt(S, "dve", rr[0:64, :], rr[0:64, :], kf[0:64, :], ALU.subtract, [r_X], [r_X])
            act(S, dst, rr[0:64, :], AF.Sin, [r_X], [r_X], scale=TWO_PI)

        def zip2(gens):
            gens = list(gens)
            while gens:
                for g__ in list(gens):
                    try:
                        next(g__)
                    except StopIteration:
                        gens.remove(g__)

        r_fc = [[R("fc%d%d" % (a_, c_)) for c_ in range(2)] for a_ in range(2)]

        def sin_chain(layer, c):
            cs_ = slice(c * 256, (c + 1) * 256)
            for hh in range(2):
                q_ = slice(c * 512 + hh * 256, c * 512 + (hh + 1) * 256)
                pb, pr = self.pbank()
                if layer == 0:
                    mm(S, [(pb[0:64, 0:256], k_["fw1"][0:33, l, :], feats[0:33, q_], True, True)], [r_X, r_k], [pr])
                else:
                    mm(S, [(pb[0:64, 0:256], k_["fw2"][0:64, l, :], z1[0:64, q_], True, True)], [r_fc[0][c], r_k], [pr])
                yield
                bcol = 1 + layer
                rg = R("tmp")
                ts(S, "dve", rr[0:64, cs_], pb[0:64, 0:256], fs[:, l, 0:1], ALU.mult, [pr, r_k], [r_fc[1][c]],
                   s2=fs[:, l, bcol:bcol + 1], op1=ALU.add)
                yield
                cp(S, "dve", ii[0:64, cs_], rr[0:64, cs_], [r_fc[1][c]], [r_fc[1][c]])
                yield
                cp(S, "dve", kf[0:64, cs_], ii[0:64, cs_], [r_fc[1][c]], [r_fc[1][c]])
                yield
                tt(S, "dve", rr[0:64, cs_], rr[0:64, cs_], kf[0:64, cs_], ALU.subtract, [r_fc[1][c]], [r_fc[1][c]])
                yield
                dst = (z1 if layer == 0 else z2)[0:64, q_]
                act(S, dst, rr[0:64, cs_], AF.Sin, [r_fc[1][c]], [r_fc[0][c] if layer == 0 else r_X], scale=TWO_PI)
                yield

        for c in range(2):
            S.op("dve", lambda e, c=c: e.memset(rr[0:64, c * 256:c * 256 + 1], 0.0), [r_X], [r_fc[1][c], r_fc[0][c]])
        zip2([sin_chain(0, 0), sin_chain(0, 1)])
        zip2([sin_chain(1, 0), sin_chain(1, 1)])
        for c in range(2):
            S.op("dve", lambda e, c=c: e.memset(rr[0:64, c * 256:c * 256 + 1], 0.0), [r_fc[1][c], r_fc[0][c]], [r_X])
        for t_ in range(8):
            pb, pr = self.pbank()
            tok = slice(t_ * 128, (t_ + 1) * 128)
            mm(S, [(pb[:, 0:256], z2[0:64, tok], k_["fw3"][0:64, l, :], True, False),
                   (pb[:, 0:256], k_["ones_f"][0:1, :], k_["fb3"][0:1, l, :], False, True)], [r_X, r_k], [pr])
            tt(S, "dve", zh[:, t_, 256:512], pb[:, 0:256], wnd[:, t_, :], ALU.mult, [pr, r_X], [r_zh[t_]])
        slD1, srD1 = self.wload(self.parts_win(l, 2064, 512), key=("D1", l))
        slD2, srD2 = self.wload(self.parts_win(l, 2576, 256), key=("D2", l))
        sD1, sD2 = sv8(slD1), sv8(slD2)
        Fd, Gd = d["dftF"], d["dftG"]
        fv = lambda m: Fd[m].rearrange("(k p) n -> p k n", p=128)
        gv = lambda m: Gd[m].rearrange("(k p) n -> p k n", p=128)

        def parts_dft(vw, q):
            return [(lambda sl_: sv8(sl_)[:, :, 0:256], vw(0)[:, :, q * 256:(q + 1) * 256]),
                    (lambda sl_: sv8(sl_)[:, :, 256:512], vw(1)[:, :, q * 256:(q + 1) * 256])]

        def zip_run(gens):
            gens = list(gens)
            while gens:
                for g__ in list(gens):
                    try:
                        next(g__)
                    except StopIteration:
                        gens.remove(g__)

        for q in range(2):
            self.prefetch(("F", l, q), parts_dft(fv, q))
        CV = LP["CV"]
        cvb = k_["cvb"]
        r_raw = RL(3, "hraw")
        r_uct = RL(2, "uct")

        def conv_chain(ct, ui):
            s3, col, srr = ((sD1, ct * 128, srD1), (sD1, 256 + ct * 128, srD1), (sD2, ct * 128, srD2))[ui]
            rr_ = r_raw[ui]
            fx = [r_X] if ct == 0 else []
            for c in range(2):
                pb, pr = self.pbank()
                mm(S, self.proj_fm(c, s3, col, pb), [srr] + hall, [pr])
                yield
                act(S, raw[:, ui, CH(c)], pb[:], AF.Copy, [pr], [rr_] + fx)
                yield
            tile = ui * 2 + ct
            cw = lambda j: lp[:, l, CV + tile * 4 + j:CV + tile * 4 + j + 1]
            u = raw[:, ui, :]
            if ui == 0:
                dst, wr = x0[:, ct, :], [r_x0[ct]]
            else:
                dst, wr = uct[:, ui - 1, :], [r_uct[ui - 1]] + fx
            rd = [rr_, r_k]
            act(S, dst, u, AF.Identity, rd, wr, bias=cw(3), scale=cw(1))
            yield
            for (o, a_, sc) in ((dst[:, 1:T], u[:, 0:T - 1], cw(0)), (dst[:, 0:T - 1], u[:, 1:T], cw(2)),
                                (dst[:, 256:T:256], u[:, 255:T - 1:256], cvb[:, l, tile, 0:1]),
                                (dst[:, 255:T - 1:256], u[:, 256:T:256], cvb[:, l, tile, 1:2])):
                S.op("dve", lambda e, o=o, a_=a_, sc=sc: e.scalar_tensor_tensor(out=o, in0=a_, scalar=sc, in1=o,
                                                                              op0=ALU.mult, op1=ALU.add), rd + wr[:1], wr[:1])
                yield

        for ct in range(2):
            zip_run([conv_chain(ct, ui) for ui in range(3)])
            tt(S, "dve", zf[:, ct, :], uct[:, 0, :], uct[:, 1, :], ALU.mult, r_uct, [r_zf[ct]])
            act(S, zbf[:, ct, :], zf[:, ct, :], AF.Copy, [r_zf[ct]], [r_zbf[ct]])
        for q in range(2, 4):
            self.prefetch(("F", l, q), parts_dft(fv, q))
        ident_bf = k_["ident_bf"]
        for t_ in range(8):
            pb, pr = self.pbank()
            pv = pb[:].bitcast(BF16)
            for ct in range(2):
                S.op("pe", lambda e, ct=ct, pv=pv, t_=t_: e.transpose(pv[:, ct * 128:(ct + 1) * 128],
                                                                      zbf[:, ct, t_ * 128:(t_ + 1) * 128], ident_bf[:]),
                     [r_zbf[ct], r_k], [pr])
            cp(S, "dve", zh[:, t_, 0:256], pv[:, 0:256], [pr], [r_zh[t_]])
        ZHs = [ZH, zbf.rearrange("p a b -> p (a b)").bitcast(F32).rearrange("p (a b) -> p a b", a=2)]
        r_ZHs = [r_ZH, R("ZHb")]
        PAs, PQs = [pa, pa2], [pq, yt]
        r_PA, r_PQ = RL(2, "PA"), RL(2, "PQ")
        seen = set()

        def ft_chain(ft, fj, s3, sr):
            bi = ft % 2
            Zb, rz = ZHs[bi], r_ZHs[bi]
            PA, PQ, rpa, rpq = PAs[bi], PQs[bi], r_PA[bi], r_PQ[bi]
            fresh = bi not in seen
            seen.add(bi)
            fz = (r_zbf if (fresh and bi == 1) else [])
            fx = ([r_X] if fresh else [])
            pre_, prr = self.pbank()
            pim, pri = self.pbank()
            g = [(pre_[:], s3[:, t_, fj * 128:(fj + 1) * 128], zh[:, t_, :], t_ == 0, t_ == 7) for t_ in range(8)]
            g += [(pim[:], s3[:, t_, 256 + fj * 128:256 + (fj + 1) * 128], zh[:, t_, :], t_ == 0, t_ == 7) for t_ in range(8)]
            mm(S, g, [sr] + r_zh, [prr, pri])
            yield
            act(S, Zb[:, 0, :], pre_[:], AF.Copy, [prr], [rz] + fz)
            yield
            act(S, Zb[:, 1, :], pim[:], AF.Copy, [pri], [rz])
            yield
            Zr, Hr, Zi, Hi = Zb[:, 0, 0:256], Zb[:, 0, 256:512], Zb[:, 1, 0:256], Zb[:, 1, 256:512]
            tt(S, "dve", PA[:, 0, :], Zr, Hr, ALU.mult, [rz], [rpa] + fx)
            yield
            tt(S, "dve", PQ[:, 0, :], Zr, Hi, ALU.mult, [rz], [rpq] + fx)
            yield
            tt(S, "dve", PA[:, 1, :], Zi, Hi, ALU.mult, [rz], [rpa])
            yield
            tt(S, "dve", PQ[:, 1, :], Zi, Hr, ALU.mult, [rz], [rpq])
            yield
            tt(S, "dve", Y[:, ft, 0, :], PA[:, 0, :], PA[:, 1, :], ALU.subtract, [rpa], [r_Y[ft]])
            yield
            tt(S, "dve", Y[:, ft, 1, :], PQ[:, 0, :], PQ[:, 1, :], ALU.add, [rpq], [r_Y[ft]])
            yield

        for q in range(4):
            sl, sr = self.wload(parts_dft(fv, q), key=("F", l, q))
            s3 = sv8(sl)
            zip_run([ft_chain(q * 2 + fj, fj, s3, sr) for fj in range(2)])
        HB = LP["HB"]
        for q in range(2):
            self.prefetch(("G", l, q), parts_dft(gv, q))
        for q in range(4):
            sl, sr = self.wload(parts_dft(gv, q), key=("G", l, q))
            if q + 2 < 4:
                self.prefetch(("G", l, q + 2), parts_dft(gv, q + 2))
            s3 = sv8(sl)
            ns = slice(q * 256, (q + 1) * 256)
            for ct in range(2):
                pb, pr = self.pbank()
                g = []
                for ft in range(8):
                    g.append((pb[:, 0:256], Y[:, ft, 0, ct * 128:(ct + 1) * 128], s3[:, ft, 0:256], ft == 0, False))
                    g.append((pb[:, 0:256], Y[:, ft, 1, ct * 128:(ct + 1) * 128], s3[:, ft, 256:512], False, ft == 7))
                mm(S, g, [sr] + r_Y, [pr])
                ytb, ryt = (yt2[:, ct, :], r_yt[ct])
                act(S, ytb, pb[:, 0:256], AF.Copy, [pr], [ryt] + ([r_PA[1], r_ZHs[0]] if q == 0 else []))
                S.op("dve", lambda e, ytb=ytb, ct=ct: e.scalar_tensor_tensor(out=ytb, in0=zf[:, ct, ns],
                                                                           scalar=lp[:, l, HB + ct:HB + ct + 1], in1=ytb,
                                                                           op0=ALU.mult, op1=ALU.add),
                     [r_zf[ct], ryt, r_k], [ryt])
                tt(S, "dve", ymix[:, 6 + ct, ns], x0[:, ct, ns], ytb, ALU.mult, [r_x0[ct], ryt], [ymr[6 + ct][q // 2]])


def fm(v):
    return np.ascontiguousarray(np.asarray(v, np.float32).reshape(8, 128).T)


def _consts(kind):
    c = {}
    p = np.arange(128)
    t = np.arange(T)
    ident = np.eye(128, dtype=np.float32)
    c["ident"] = ident
    bo = np.zeros((128, 128), np.float32)
    bo[:64, :64] = 1
    bo[64:, 64:] = 1
    c["blockones"] = bo
    r_, t_ = np.meshgrid(p, p, indexing="ij")
    tri = np.zeros((128, 4, 128), np.float32)
    tri[:, 0, :] = (r_ <= t_)
    tri[:, 1, :] = (r_ >= t_)
    tri[:, 2, :] = np.where(r_ <= t_, 0.0, NEG)
    tri[:, 3, :] = np.where(r_ >= t_, 0.0, NEG)
    c["tri"] = tri
    P = np.zeros((128, 128), np.float32)
    for b in range(0, 128, 32):
        for i in range(16):
            P[b + i + 16, b + i] = -1.0
            P[b + i, b + i + 16] = 1.0
    c["ropeP"] = P
    cs = np.zeros((128, 2, T), np.float32)
    if kind == "s":
        dd = p % 64
        inv = (10000.0 ** (-(dd % 16).astype(np.float32) / np.float32(16))).astype(np.float32)
        row = (t // 64).astype(np.float32)
        col = (t % 64).astype(np.float32)
        pos = np.where((dd // 32)[:, None] == 0, row[None, :], col[None, :]).astype(np.float32)
        ang = (pos * inv[:, None]).astype(np.float32)
        cs[:, 0, :] = np.cos(ang)
        cs[:, 1, :] = np.sin(ang)
    else:
        cs[:, 0, :] = 1.0
    c["ropeCS"] = cs
    sm = np.full((128, 8, 384), NEG, np.float32)
    for j in range(8):
        m = j * 128 + p[:, None]
        q = (j - 1) * 128 + np.arange(384)[None, :]
        inr = (q >= 0) & (q < T)
        if kind == "s":
            ok = (np.abs(m - q) <= 128) & inr
        else:
            ok = ((m // 256) == (q // 256)) & inr
        sm[:, j, :] = np.where(ok, 0.0, NEG)
    c["swamask"] = sm
    gm = np.zeros((5, 2, T), np.float32)
    if kind == "p":
        gm[0, 0, :] = 1.0
        gm[0, 1, :] = -BIG
        for s_ in range(4):
            gm[1 + s_, 0, :] = (t // 256 == s_)
            gm[1 + s_, 1, :] = BIG * (t // 256 == s_)
    c["gmAB"] = gm
    c["cflags"] = np.tile(np.array([[0.0, 1.0, 0.0, 0.0]] if kind == "s" else [[NEG, 0.0, 1.0, 0.0]], np.float32), (128, 1))
    sel = np.zeros((4, 130), np.float32)
    for h in range(4):
        sel[h, 0:128] = ((p >= 64).astype(int) == (h % 2))
        sel[h, 128 + h // 2] = 1.0
    c["sel"] = sel
    L = T if kind == "s" else 256
    rep = T // L
    pos = np.arange(L, dtype=np.float32)
    t01 = pos / np.float32(max(L - 1, 1))
    lin = np.linspace(1e-4, 15.0, 16, dtype=np.float32)
    ang = (np.float32(2.0 * math.pi / L) * pos[:, None] * lin[None, :]).astype(np.float32)
    feats = np.concatenate([t01[:, None], np.cos(ang), -np.sin(ang)], -1).astype(np.float32)
    c["featsT"] = np.ascontiguousarray(np.tile(feats, (rep, 1)).T)
    centre = L // 2
    dist = np.abs(pos - centre) / np.float32(max(centre, 1))
    deltas = np.abs(np.linspace(math.log(0.01) / 1.5, math.log(0.01) / 0.3, 256, dtype=np.float32))
    wnd = np.exp(-dist[:, None] * deltas[None, :]).astype(np.float32)
    c["window"] = np.ascontiguousarray(np.tile(wnd, (rep, 1)).reshape(8, 128, 256).transpose(1, 0, 2))
    N = 2 * L
    tt_ = np.arange(L, dtype=np.float64)
    ff = np.arange(L, dtype=np.float64)
    th = math.pi * (2 * ff + 1) / N
    Fc = np.cos(tt_[:, None] * th[None, :])
    Fs = -np.sin(tt_[:, None] * th[None, :])
    Gc = (2.0 / N) * np.cos(th[:, None] * (tt_[None, :] + L // 2))
    Gs = -(2.0 / N) * np.sin(th[:, None] * (tt_[None, :] + L // 2))
    dF = np.zeros((2, T, T), np.float32)
    dG = np.zeros((2, T, T), np.float32)
    for s_ in range(rep):
        sl = slice(s_ * L, (s_ + 1) * L)
        dF[0, sl, sl] = Fc
        dF[1, sl, sl] = Fs
        dG[0, sl, sl] = Gc
        dG[1, sl, sl] = Gs
    c["dftF"] = dF
    c["dftG"] = dG
    return c


def host_inputs(inp, cores=None):
    f = lambda a: np.ascontiguousarray(np.asarray(a, dtype=np.float32))
    A = {k: np.asarray(v) for k, v in inp.items()}
    shared = {}
    for nm in ("w_ada", "w1_gate", "w1_up", "w1_down", "w_in", "w_out", "w2_gate", "w2_up", "w2_down"):
        shared[nm] = f(A[nm])
    shared["b_adaT"] = f(A["b_ada"].reshape(NL, 72, 128).transpose(0, 2, 1))
    gv = []
    for l in range(NL):
        gv += [fm(A["g_ff1"][l]), fm(A["g_mix"][l]), fm(A["g_ff2"][l])]
    gv.append(fm(A["g_final"]))
    shared["gvec"] = f(np.concatenate(gv, axis=1))
    lp = np.zeros((128, NL, 304), np.float32)
    p = np.arange(128)
    for l in range(NL):
        lp[:, l, 0:16] = A["b_gates"][l][None, :]
        lp[:, l, 16:272] = A["g_mlstm"][l][None, :]
        lp[:, l, 272] = A["g_qnorm"][l][p % 64]
        lp[:, l, 273] = A["g_knorm"][l][p % 64]
        lp[:, l, 274:278] = A["sinks"][l][None, :]
        for i in range(6):
            ch = i * 128 + p
            lp[:, l, 278 + i * 4 + 0] = A["conv_w"][l][0, ch]
            lp[:, l, 278 + i * 4 + 1] = A["conv_w"][l][1, ch]
            lp[:, l, 278 + i * 4 + 2] = A["conv_w"][l][2, ch]
            lp[:, l, 278 + i * 4 + 3] = A["conv_b"][l][ch]
        for ct in range(2):
            lp[:, l, 302 + ct] = A["hyena_bias"][l][ct * 128 + p]
    shared["lp"] = lp
    shared["fw1"] = f(A["filt_w1"].transpose(1, 0, 2))
    shared["fw2"] = f(A["filt_w2"].transpose(1, 0, 2))
    shared["fw3"] = f(A["filt_w3"].transpose(1, 0, 2))
    shared["fvec"] = f(np.stack([A["filt_b1"], A["filt_b2"], A["filt_freq"]], -1).transpose(1, 0, 2))
    shared["fb3"] = f(A["filt_b3"][None, :, :])
    cst = {"s": _consts("s"), "p": _consts("p")}
    maps = []
    xs, xp = A["x_sample"], A["x_prompt"]
    for core in (range(8) if cores is None else cores):
        m = dict(shared)
        kind = "s" if core < 4 else "p"
        m.update(cst[kind])
        C0 = np.zeros((NL, 2, 128, 2, 65), np.float32)
        m0rep = np.zeros((128, NL, 2, 2), np.float32)
        m0c = np.zeros((4, NL * 2), np.float32)
        kT = {n: np.zeros((NL, 2, 128, 256), np.float32) for n in ("gkT", "skT")}
        vp = {n: np.zeros((NL, 2, 128, 2, 192), np.float32) for n in ("gvp", "svp")}
        if core < 4:
            b = core
            m["xT"] = f(xs[b].T)
            m["cvec"] = fm(A["c"][b])
            sC, sn, smm = A["state_mlstm_C"][b], A["state_mlstm_n"][b], A["state_mlstm_m"][b]
            for g_ in range(2):
                for pr_ in range(2):
                    h = 2 * pr_ + g_
                    C0[:, :, g_ * 64:(g_ + 1) * 64, pr_, 0:64] = sC[:, :, h]
                    C0[:, :, g_ * 64:(g_ + 1) * 64, pr_, 64] = sn[:, :, h]
                    m0rep[g_ * 64:(g_ + 1) * 64, :, :, pr_] = smm[None, :, :, h]
            for l in range(NL):
                for dr in range(2):
                    m0c[:, l * 2 + dr] = smm[l, dr, :]
            for (kn, vn, ck_, cv_) in (("gkT", "gvp", "cache_gattn_k", "cache_gattn_v"), ("skT", "svp", "cache_swa_k", "cache_swa_v")):
                ck, cv = A[ck_][b], A[cv_][b]
                t1 = ck.transpose(0, 2, 3, 1).reshape(NL, 128, 256)
                t2 = ck[:, :, ::-1, :].transpose(0, 2, 3, 1).reshape(NL, 128, 256)
                kT[kn][:, 0] = t1
                kT[kn][:, 1] = t2
                vp[vn][:, :, :, :, 64:128] = cv.reshape(NL, 2, 128, 2, 64)
        else:
            j = core - 4
            m["xT"] = f(xp[4 * j:4 * j + 4].reshape(T, D).T)
            m["cvec"] = fm(A["c_ctx"])
        m["C0"], m["m0rep"], m["m0c"] = C0, m0rep, m0c
        m.update(kT)
        m.update(vp)
        maps.append(m)
    return maps


_PROG = None


def get_prog():
    global _PROG
    if _PROG is None:
        _PROG = Prog()
    return _PROG


def run_device(inputs, trace=False, cores=None):
    prog = get_prog()
    maps = host_inputs(inputs, cores)
    maps = [{k: np.ascontiguousarray(v, dtype=np.float32) for k, v in m.items() if k in prog.din} for m in maps]
    for m in maps:
        for k, shp in prog.din.items():
            assert m[k].shape == shp, (k, m[k].shape, shp)
    res = run_bass_kernel_spmd(prog.nc, maps, core_ids=list(range(len(maps))), trace=trace)
    return res


def assemble(res):
    r = res.results
    yp = np.zeros((16, 256, D), np.float32)
    ys = np.zeros((4, T, D), np.float32)
    nC = np.zeros((16, NL, 2, 4, 64, 64), np.float32)
    nn = np.zeros((16, NL, 2, 4, 64), np.float32)
    nm = np.zeros((16, NL, 2, 4), np.float32)
    kv = {n: np.zeros((16, NL, 256, 2, 64), np.float32) for n in ("o_gk", "o_gv", "o_sk", "o_sv")}
    for core in range(8):
        y = np.ascontiguousarray(r[core]["yT"].T)
        if core < 4:
            ys[core] = y
            continue
        j = core - 4
        yp[4 * j:4 * j + 4] = y.reshape(4, 256, D)
        for n in ("o_gk", "o_sk"):
            a = r[core][n].reshape(NL, 2, 64, 4, 256)
            kv[n][4 * j:4 * j + 4] = a.transpose(3, 0, 4, 1, 2)
        for n in ("o_gv", "o_sv"):
            a = r[core][n].reshape(NL, 4, 256, 2, 64)
            kv[n][4 * j:4 * j + 4] = a.transpose(1, 0, 2, 3, 4)
        oc = r[core]["o_C"].reshape(NL, 2, 4, 2, 64, 2, 65)
        oc = oc.transpose(2, 0, 1, 5, 3, 4, 6).reshape(4, NL, 2, 4, 64, 65)
        nC[4 * j:4 * j + 4] = oc[..., 0:64]
        nn[4 * j:4 * j + 4] = oc[..., 64]
        nm[4 * j:4 * j + 4] = r[core]["o_m"].transpose(3, 0, 1, 2)
    return (yp, ys, nC, nn, nm, kv["o_gk"], kv["o_gv"], kv["o_sk"], kv["o_sv"])


def kernel(**inputs):
    res = run_device(inputs)
    return assemble(res)
```

```python
import os
import math
import numpy as np
import concourse.bass as bass
import concourse.mybir as mybir
from concourse.bass_utils import run_bass_kernel_spmd

F32 = mybir.dt.float32
BF16 = mybir.dt.bfloat16
I32 = mybir.dt.int32
AF = mybir.ActivationFunctionType
ALU = mybir.AluOpType
AX = mybir.AxisListType

D = 1024
T = 1024
DFF = 2816
NFF = 22
NL = 2
NIN = 2832
EPS = 1e-6
NEG = -30000.0
BIG = 29952.0
LN8 = math.log(0.125)
SLOT = 8 * 640
STAGE = int(os.environ.get("MK_STAGE", "99"))
DEBUG = bool(os.environ.get("MK_DEBUG"))
CUT = int(os.environ.get("MK_CUT", "99"))


class R:
    __slots__ = ("name", "w", "rd", "excl")

    def __init__(self, name="", excl=False):
        self.name = name
        self.w = None
        self.rd = []
        self.excl = excl


def RL(n, name=""):
    return [R("%s%d" % (name, i)) for i in range(n)]


class Sched:
    NLANES = {"sp": 8, "pool": 3, "poolw": 5}

    def __init__(self, nc):
        self.nc = nc
        self.eng = {"pe": nc.tensor, "act": nc.scalar, "dve": nc.vector,
                    "pool": nc.gpsimd, "sp": nc.sync}
        self.sem = {}
        self.cnt = {}
        for k in self.eng:
            self.sem[k] = nc.alloc_semaphore(name="s_" + k)
            self.cnt[k] = 0
        self.lanes = {}
        for q, n in self.NLANES.items():
            self.lanes[q] = []
            for i in range(n):
                key = "d_%s%d" % (q, i)
                self.sem[key] = nc.alloc_semaphore(name=key)
                self.cnt[key] = 0
                self.lanes[q].append(key)
        self.lane_rr = {q: 0 for q in self.NLANES}
        self.eng_of = {"sp": "sp", "pool": "pool", "poolw": "pool"}
        self.waited = {k: {} for k in self.eng}
        self.nwaits = 0
        self.nops = 0

    def _wait(self, e, key, val):
        if key == "pe" and e == "pe":
            return
        w = self.waited[e]
        if w.get(key, 0) >= val:
            return
        self.eng[e].wait_ge(self.sem[key], val)
        w[key] = val
        self.nwaits += 1

    def _deps(self, e, reads, writes):
        deps = {}
        for r in reads:
            if r.w is not None:
                k, v = r.w
                if deps.get(k, 0) < v:
                    deps[k] = v
            if r.excl:
                for (k, v) in r.rd:
                    if k != e and deps.get(k, 0) < v:
                        deps[k] = v
        for w in writes:
            if w.w is not None:
                k, v = w.w
                if deps.get(k, 0) < v:
                    deps[k] = v
            for (k, v) in w.rd:
                if deps.get(k, 0) < v:
                    deps[k] = v
        for k, v in deps.items():
            self._wait(e, k, v)

    def _commit(self, tok, reads, writes):
        for r in reads:
            r.rd.append(tok)
            if len(r.rd) > 48:
                mx = {}
                for k, v in r.rd:
                    if mx.get(k, 0) < v:
                        mx[k] = v
                r.rd = list(mx.items())
        for w in writes:
            w.w = tok
            w.rd = []

    def op(self, e, fn, reads=(), writes=()):
        self._deps(e, reads, writes)
        inst = fn(self.eng[e])
        self.cnt[e] += 1
        inst.then_inc(self.sem[e], 1)
        self._commit((e, self.cnt[e]), reads, writes)
        self.nops += 1

    def dma(self, q, out, in_, reads=(), writes=()):
        lanes = self.lanes[q]
        e = self.eng_of[q]
        key = lanes[self.lane_rr[q] % len(lanes)]
        self.lane_rr[q] += 1
        self._wait(e, key, self.cnt[key])
        self._deps(e, reads, writes)
        inst = self.eng[e].dma_start(out=out, in_=in_)
        self.cnt[key] += 16
        inst.then_inc(self.sem[key], 16)
        self._commit((key, self.cnt[key]), reads, writes)
        self.nops += 1

    def barrier(self):
        for e in self.eng:
            for k in self.sem:
                if self.cnt[k] > 0 and not k.startswith("d_poolw"):
                    self._wait(e, k, self.cnt[k])

    def finish(self):
        for k in self.sem:
            if self.cnt[k] > 0 and k != "sp":
                self._wait("sp", k, self.cnt[k])


def act(S, out, in_, func, reads, writes, bias=0.0, scale=1.0):
    S.op("act", lambda e: e.activation(out=out, in_=in_, func=func, bias=bias, scale=scale), reads, writes)


def tt(S, eng, out, a, b, op, reads, writes):
    S.op(eng, lambda e: e.tensor_tensor(out=out, in0=a, in1=b, op=op), reads, writes)


def ts(S, eng, out, a, s1, op0, reads, writes, s2=None, op1=None):
    if op1 is None:
        S.op(eng, lambda e: e.tensor_scalar(out=out, in0=a, scalar1=s1, scalar2=None, op0=op0), reads, writes)
    else:
        S.op(eng, lambda e: e.tensor_scalar(out=out, in0=a, scalar1=s1, scalar2=s2, op0=op0, op1=op1), reads, writes)


def cp(S, eng, out, in_, reads, writes):
    S.op(eng, lambda e: e.tensor_copy(out=out, in_=in_), reads, writes)


def recip(S, out, in_, reads, writes):
    S.op("dve", lambda e: e.reciprocal(out=out, in_=in_), reads, writes)


def mm(S, groups, reads, writes):
    def fn(e):
        inst = None
        for (o, l, r, st, sp_) in groups:
            inst = e.matmul(o, lhsT=l, rhs=r, start=st, stop=sp_)
        return inst
    S.op("pe", fn, reads, writes)


class Prog:
    def __init__(self):
        nc = bass.Bass("TRN2", target_bir_lowering=False)
        self.nc = nc
        self.S = Sched(nc)
        self.din = {}
        self.dout = {}
        self._build()

    def inp(self, name, shape):
        t = self.nc.dram_tensor(name, list(shape), F32, kind="ExternalInput").ap()
        self.din[name] = tuple(shape)
        return t

    def outp(self, name, shape):
        t = self.nc.dram_tensor(name, list(shape), F32, kind="ExternalOutput").ap()
        self.dout[name] = tuple(shape)
        return t

    def sb(self, name, shape, dt=F32):
        return self.nc.alloc_sbuf_tensor("sb_" + name, list(shape), dt)

    def arena_reset(self, to=0):
        self.aoff = to
        self.S.barrier()

    def carve(self, shape, dt=F32):
        n = 1
        for s in shape[1:]:
            n *= s
        nb = n * (4 if dt in (F32, I32) else 2)
        nb = (nb + 31) // 32 * 32
        off = self.aoff
        self.aoff += nb
        assert self.aoff <= self.ARENA_BYTES, (self.aoff, self.ARENA_BYTES)
        v = self.arena[:, off // 2:(off + nb) // 2]
        if dt != BF16:
            v = v.bitcast(dt)
        v = v[:, 0:n]
        if len(shape) == 3:
            v = v.rearrange("p (a b) -> p a b", a=shape[1])
        elif len(shape) == 4:
            v = v.rearrange("p (a b c) -> p a b c", a=shape[1], b=shape[2])
        return v

    def bank(self):
        return self.pbank()

    def prefetch(self, key, parts):
        self.pre[key] = self.wload(parts)

    def wload(self, parts, key=None):
        if key is not None and key in self.pre:
            return self.pre.pop(key)
        i = self.slot_rr % len(self.slots)
        self.slot_rr += 1
        sl, r = self.slots[i], self.slotr[i]
        for (dst_fn, src) in parts:
            self.S.dma("poolw", dst_fn(sl), src, writes=[r])
        return sl, r

    def slot_view(self, sl, kk):
        return sl[:, 0:kk * self.slot_w].rearrange("p (k n) -> p k n", k=kk)

    def _build(self):
        nc, S = self.nc, self.S
        inp, outp, sb = self.inp, self.outp, self.sb
        xT_d = inp("xT", [D, T])
        cvec_d = inp("cvec", [128, 8])
        w_ada = inp("w_ada", [NL, D, 9 * D])
        b_adaT = inp("b_adaT", [NL, 128, 72])
        gvec_d = inp("gvec", [128, NL * 24 + 8])
        wd = {}
        for nm, shp in (("w1_gate", [NL, D, DFF]), ("w1_up", [NL, D, DFF]), ("w1_down", [NL, DFF, D]),
                        ("w_in", [NL, D, NIN]), ("w_out", [NL, D, D]),
                        ("w2_gate", [NL, D, DFF]), ("w2_up", [NL, D, DFF]), ("w2_down", [NL, DFF, D])):
            wd[nm] = inp(nm, shp)
        yT_d = outp("yT", [D, T])

        xT = sb("xT", [128, 8, T], F32)
        hT = sb("hT", [128, 8, T], BF16)
        self.ARENA_BYTES = 84 * 1024
        self.arena = sb("arena", [128, self.ARENA_BYTES // 2], BF16)
        self.slot_w = 640
        self.slots = [sb("slot%d" % i, [128, SLOT], BF16) for i in range(4)]
        self.slotr = RL(4, "slot")
        self.slot_rr = 0
        self.pre = {}
        self.ps = [nc.alloc_psum_tensor("ps%d" % i, [128, 512], F32) for i in range(8)]
        self.psr = [R("ps%d" % i, excl=True) for i in range(8)]
        self.bank_rr = 0
        self.bank_pool = list(range(8))
        ones_bf = sb("ones_bf", [128, 128], BF16)
        cvec = sb("cvec_sb", [128, 8], F32)
        sc_bf = sb("sc_bf", [128, 8], BF16)
        modT = sb("modT", [128, NL, 72], F32)
        badaT = sb("badaT", [128, NL, 72], F32)
        gvec = sb("gvec_sb", [128, NL * 24 + 8], F32)
        Acoef = sb("Acoef", [128, NL, 3, 8], F32)
        Gcoef = sb("Gcoef", [128, NL, 3, 8], F32)
        sqb = sb("sqb", [128, 2, 512], BF16)
        f32s = sb("f32s", [128, 3, 512], F32)
        rstd = sb("rstd", [128, 512], F32)
        r_ones, r_cvec, r_sc, r_gvec, r_rstd = R("ones"), R("cvec"), R("sc"), R("gvec"), R("rstd")
        r_mod = RL(NL, "mod")
        r_bada = R("bada")
        r_coef = RL(NL, "coef")
        r_sq = RL(2, "sq")
        r_f32s = RL(3, "f32s")
        xr = [[R("x%d_%d" % (k, c)) for c in range(2)] for k in range(8)]
        hr = [[R("h%d_%d" % (k, c)) for c in range(2)] for k in range(8)]
        self.sq_rr = 0
        self.f32_rr = 0

        def CH(c):
            return slice(c * 512, (c + 1) * 512)

        xv = xT_d.rearrange("(k p) t -> p k t", p=128)
        for k in range(8):
            S.dma("sp", xT[:, k, :], xv[:, k, :], writes=[xr[k][0], xr[k][1]])
        S.dma("sp", cvec[:], cvec_d, writes=[r_cvec])
        S.dma("sp", gvec[:], gvec_d, writes=[r_gvec])
        for l in range(NL):
            S.dma("sp", badaT[:, l, :], b_adaT[l], writes=[r_bada])
        S.op("dve", lambda e: e.memset(ones_bf[:], 1.0), writes=[r_ones])
        act(S, sc_bf[:], cvec[:], AF.Silu, [r_cvec], [r_sc])

        mslots = [sb("mslot%d" % i, [128, 8, 256], BF16) for i in range(2)]
        r_ms = RL(2, "mslot")
        r_modls = [[R("mod%d_%d" % (l, s_)) for s_ in range(3)] for l in range(NL)]
        jobs = [(l, q) for l in range(NL) for q in range(36)]
        st = {"dma": 0, "mm": 0, "fin": set()}

        def mod_dma(j):
            l, q = jobs[j]
            wv = w_ada[l].rearrange("(k p) n -> p k n", p=128)
            S.dma("poolw", mslots[j % 2][:], wv[:, :, q * 256:(q + 1) * 256], writes=[r_ms[j % 2]])

        def mod_mm(j, bank=None):
            l, q = jobs[j]
            if bank is None:
                pb, pr = self.bank()
                c0 = 0
            else:
                pb, pr, c0 = bank
            groups = []
            for jj in range(2):
                for k in range(8):
                    groups.append((pb[:, c0 + jj:c0 + jj + 1], mslots[j % 2][:, k, jj * 128:(jj + 1) * 128], sc_bf[:, k:k + 1],
                                   k == 0, k == 7))
            mm(S, groups, [r_ms[j % 2], r_sc], [pr])
            s_ = q // 12
            tt(S, "dve", modT[:, l, q * 2:q * 2 + 2], pb[:, c0:c0 + 2], badaT[:, l, q * 2:q * 2 + 2], ALU.add,
               [pr, r_bada], [r_modls[l][s_]])

        def mod_pump(n=1, bank=None):
            for _ in range(n):
                if st["dma"] < len(jobs) and st["dma"] - st["mm"] < 2:
                    mod_dma(st["dma"])
                    st["dma"] += 1
                if st["mm"] < st["dma"] - 1 or (st["dma"] == len(jobs) and st["mm"] < st["dma"]):
                    mod_mm(st["mm"], bank)
                    st["mm"] += 1

        def mod_require(l, s_):
            last = l * 36 + s_ * 12 + 11
            while st["mm"] <= last:
                mod_pump()
            if (l, s_) in st["fin"]:
                return
            st["fin"].add((l, s_))
            r = r_modls[l][s_]
            ts(S, "dve", Acoef[:, l, s_, :], modT[:, l, (3 * s_ + 1) * 8:(3 * s_ + 2) * 8], 1.0, ALU.add, [r], [r])
            tt(S, "dve", Acoef[:, l, s_, :], Acoef[:, l, s_, :], gvec[:, l * 24 + s_ * 8:l * 24 + s_ * 8 + 8],
               ALU.mult, [r, r_gvec], [r])
            ts(S, "dve", Gcoef[:, l, s_, :], modT[:, l, (3 * s_ + 2) * 8:(3 * s_ + 3) * 8],
               0.5 if s_ != 1 else 1.0, ALU.mult, [r], [r])

        self.mod_pump = mod_pump
        self.mod_require = mod_require

        def rms_rstd(c, src_fn, src_regs, nk, inv_n, ones_l):
            pb, pr = self.bank()
            for k in range(nk):
                i = self.sq_rr % 2
                self.sq_rr += 1
                act(S, sqb[:, i, :], src_fn(k), AF.Square, [src_regs[k]], [r_sq[i]])
                mm(S, [(pb[:], ones_l, sqb[:, i, :], k == 0, k == nk - 1)], [r_sq[i], r_ones], [pr])
            act(S, rstd[:], pb[:], AF.Ln, [pr], [r_rstd], bias=EPS, scale=inv_n)
            act(S, rstd[:], rstd[:], AF.Exp, [r_rstd], [r_rstd], scale=-0.5)

        def norm_mod(l, s):
            for c in range(2):
                rms_rstd(c, lambda k: xT[:, k, CH(c)], [xr[k][c] for k in range(8)], 8, 1.0 / D, ones_bf[:])
                for k in range(8):
                    i = self.f32_rr % 3
                    self.f32_rr += 1
                    tt(S, "dve", f32s[:, i, :], xT[:, k, CH(c)], rstd[:], ALU.mult,
                       [xr[k][c], r_rstd], [r_f32s[i]])
                    act(S, hT[:, k, CH(c)], f32s[:, i, :], AF.Identity, [r_f32s[i], r_modls[l][s]],
                        [hr[k][c]], bias=modT[:, l, 3 * s * 8 + k:3 * s * 8 + k + 1],
                        scale=Acoef[:, l, s, k:k + 1])

        def resid_add(l, s, dt_, c, pb, pr):
            i = self.f32_rr % 3
            self.f32_rr += 1
            act(S, f32s[:, i, :], pb[:], AF.Copy, [pr, r_modls[l][s]], [r_f32s[i]], scale=Gcoef[:, l, s, dt_:dt_ + 1])
            tt(S, "dve", xT[:, dt_, CH(c)], xT[:, dt_, CH(c)], f32s[:, i, :], ALU.add,
               [xr[dt_][c], r_f32s[i]], [xr[dt_][c]])

        def ffn(l, s, wg, wu, wdn):
            self.arena_reset()
            aT = self.carve([128, NFF, T], BF16)
            ar = [[R("a%d_%d" % (j, c)) for c in range(2)] for j in range(NFF)]
            sg = [self.carve([128, 512], F32) for _ in range(3)]
            r_sg = RL(3, "sg")
            sg_rr = 0
            mod_require(l, s)
            norm_mod(l, s)
            wgv = wg[l].rearrange("(k p) n -> p k n", p=128)
            wuv = wu[l].rearrange("(k p) n -> p k n", p=128)
            for g in range(NFF // 2):
                c0 = g * 256
                sl, sr = self.wload(self.parts_gu(wg, wu, l, g), key=("gu", l, s, g))
                s3 = self.slot_view(sl, 8)
                if l == 0:
                    mod_pump(1)
                for jj in range(2):
                    j = g * 2 + jj
                    if l == 0 and s == 2 and jj == 1:
                        mod_pump(1)
                    for c in range(2):
                        pg, prg = self.bank()
                        pu, pru = self.bank()
                        groups = []
                        for k in range(8):
                            groups.append((pg[:], s3[:, k, jj * 128:(jj + 1) * 128], hT[:, k, CH(c)], k == 0, k == 7))
                        for k in range(8):
                            groups.append((pu[:], s3[:, k, 256 + jj * 128:256 + (jj + 1) * 128], hT[:, k, CH(c)],
                                           k == 0, k == 7))
                        mm(S, groups, [sr] + [hr[k][c] for k in range(8)], [prg, pru])
                        i = sg_rr % 3
                        sg_rr += 1
                        act(S, sg[i], pg[:], AF.Silu, [prg], [r_sg[i]])
                        tt(S, "dve", aT[:, j, CH(c)], sg[i], pu[:], ALU.mult, [r_sg[i], pru], [ar[j][c]])
            wdv = wdn[l].rearrange("(j p) n -> p j n", p=128)
            for dt_ in range(8):
                sl, sr = self.wload([(lambda sl_: sl_[:, 0:NFF * 128].rearrange("p (j n) -> p j n", j=NFF),
                                      wdv[:, :, dt_ * 128:(dt_ + 1) * 128])])
                if dt_ == 7:
                    if s == 0:
                        self.prefetch(("A1", l), self.parts_win(l, 0, 512))
                        self.prefetch(("A2", l), self.parts_win(l, 512, 512))
                        self.prefetch(("G", l), self.parts_win(l, 1024, 16))
                    elif l + 1 < NL:
                        for g_ in range(2):
                            self.prefetch(("gu", l + 1, 0, g_), self.parts_gu(wd["w1_gate"], wd["w1_up"], l + 1, g_))
                s3 = sl[:, 0:NFF * 128].rearrange("p (j n) -> p j n", j=NFF)
                if l == 0:
                    mod_pump(1)
                for c in range(2):
                    pb, pr = self.bank()
                    groups = [(pb[:], s3[:, j, :], aT[:, j, CH(c)], j == 0, j == NFF - 1) for j in range(NFF)]
                    mm(S, groups, [sr] + [ar[j][c] for j in range(NFF)], [pr])
                    resid_add(l, s, dt_, c, pb, pr)

        self.wd = wd
        self.ctx = dict(xT=xT, hT=hT, xr=xr, hr=hr, CH=CH, rms_rstd=rms_rstd, norm_mod=norm_mod,
                        resid_add=resid_add, ones_bf=ones_bf, r_ones=r_ones, gvec=gvec, r_gvec=r_gvec,
                        f32s=f32s, r_f32s=r_f32s, rstd=rstd, r_rstd=r_rstd, modT=modT, sqb=sqb, r_sq=r_sq)


        self.ARENA_BYTES = 84 * 1024
        LPW = 304
        self.LP = dict(BG=0, GM=16, GQK=272, SK=274, CV=278, HB=302)
        d = {}
        d["lp"] = inp("lp", [128, NL, LPW])
        d["cflags"] = inp("cflags", [128, 4])
        d["ident"] = inp("ident", [128, 128])
        d["blockones"] = inp("blockones", [128, 128])
        d["tri"] = inp("tri", [128, 4, 128])
        d["ropeP"] = inp("ropeP", [128, 128])
        d["ropeCS"] = inp("ropeCS", [128, 2, T])
        d["swamask"] = inp("swamask", [128, 8, 384])
        d["gmAB"] = inp("gmAB", [5, 2, T])
        d["fw1"] = inp("fw1", [33, NL, 64])
        d["fw2"] = inp("fw2", [64, NL, 64])
        d["fw3"] = inp("fw3", [64, NL, 256])
        d["fvec"] = inp("fvec", [64, NL, 3])
        d["fb3"] = inp("fb3", [1, NL, 256])
        d["featsT"] = inp("featsT", [33, T])
        d["window"] = inp("window", [128, 8, 256])
        d["dftF"] = inp("dftF", [2, T, T])
        d["dftG"] = inp("dftG", [2, T, T])
        d["sel"] = inp("sel", [4, 130])
        d["C0"] = inp("C0", [NL, 2, 128, 2, 65])
        d["m0rep"] = inp("m0rep", [128, NL, 2, 2])
        d["m0c"] = inp("m0c", [4, NL * 2])
        d["gkT"] = inp("gkT", [NL, 2, 128, 256])
        d["gvp"] = inp("gvp", [NL, 2, 128, 2, 192])
        d["skT"] = inp("skT", [NL, 2, 128, 256])
        d["svp"] = inp("svp", [NL, 2, 128, 2, 192])
        d["o_gk"] = outp("o_gk", [NL, 128, T])
        d["o_gv"] = outp("o_gv", [NL, T, 128])
        d["o_sk"] = outp("o_sk", [NL, 128, T])
        d["o_sv"] = outp("o_sv", [NL, T, 128])
        d["o_C"] = outp("o_C", [NL, 2, 4, 128, 2, 65])
        d["o_m"] = outp("o_m", [NL, 2, 4, 4])
        self.d = d
        if DEBUG:
            self.dbg_ymix = outp("dbg_ymix", [NL, 128, 8, T])
        k = {}
        k["lp"] = sb("lp", [128, NL, LPW]); k["cflags"] = sb("cflags", [128, 4])
        k["ident_f"] = sb("ident_f", [128, 128]); k["ident_bf"] = sb("ident_bf", [128, 128], BF16)
        k["blockones"] = sb("blockones", [128, 128], BF16)
        k["tri"] = sb("tri", [128, 4, 128]); k["ropeP"] = sb("ropeP", [128, 128])
        k["ones_f"] = sb("ones_f", [128, 128])
        k["fw1"] = sb("fw1", [33, NL, 64]); k["fw2"] = sb("fw2", [64, NL, 64]); k["fw3"] = sb("fw3", [64, NL, 256])
        k["fvec"] = sb("fvec", [64, NL, 3]); k["fb3"] = sb("fb3", [1, NL, 256]); k["fs"] = sb("fs", [64, NL, 4])
        k["sel"] = sb("sel", [4, 130]); k["m0rep"] = sb("m0rep", [128, NL, 2, 2]); k["m0c"] = sb("m0c", [4, NL * 2])
        k["Clo"] = sb("Clo", [128, 2, 2, 65]); k["Chi"] = sb("Chi", [128, 2, 2, 65])
        k["Cblo"] = sb("Cblo", [128, 2, 2, 66], BF16); k["Cbhi"] = sb("Cbhi", [128, 2, 2, 66], BF16)
        k["cvb"] = sb("cvb", [128, NL, 6, 2])
        self.k = k
        r_k = R("consts")
        self.r_k = r_k
        for nm in ("lp", "cflags", "tri", "ropeP", "fw1", "fw2", "fw3", "fvec", "fb3", "sel", "m0rep", "m0c"):
            S.dma("sp", k[nm][:], d[nm], writes=[r_k])
        S.dma("sp", k["ident_f"][:], d["ident"], writes=[r_k])
        S.dma("pool", k["ident_bf"][:], d["ident"], writes=[r_k])
        S.dma("pool", k["blockones"][:], d["blockones"], writes=[r_k])
        S.op("pool", lambda e: e.memset(k["ones_f"][:], 1.0), writes=[r_k])
        for nm in ("Clo", "Chi", "Cblo", "Cbhi"):
            S.op("pool", lambda e, nm=nm: e.memset(k[nm][:], 0.0), writes=[r_k])
        i2p = float(1.0 / (2 * math.pi))
        ts(S, "dve", k["fs"][:, :, 0:1], k["fvec"][:, :, 2:3], i2p, ALU.mult, [r_k], [r_k])
        tt(S, "dve", k["fs"][:, :, 1:2], k["fs"][:, :, 0:1], k["fvec"][:, :, 0:1], ALU.mult, [r_k], [r_k])
        tt(S, "dve", k["fs"][:, :, 2:3], k["fs"][:, :, 0:1], k["fvec"][:, :, 1:2], ALU.mult, [r_k], [r_k])
        CV = self.LP["CV"]
        for l in range(NL):
            cvv = k["lp"][:, l, CV:CV + 24].rearrange("p (a b) -> p a b", a=6)
            for (j, col) in ((0, 0), (1, 2)):
                ts(S, "dve", k["cvb"][:, l, :, j:j + 1], cvv[:, :, col:col + 1], k["cflags"][:, 2:3], ALU.mult,
                   [r_k], [r_k], s2=-1.0, op1=ALU.mult)

        for l in range(NL):
            ffn(l, 0, wd["w1_gate"], wd["w1_up"], wd["w1_down"])
            self.mixer(l)
            ffn(l, 2, wd["w2_gate"], wd["w2_up"], wd["w2_down"])

        gfo = NL * 24
        yv = yT_d.rearrange("(k p) t -> p k t", p=128)
        self.arena_reset()
        ost_ = self.carve([128, 2, 512], F32)
        ost = [ost_[:, 0, :], ost_[:, 1, :]]
        r_ost = RL(2, "ost")
        o_rr = 0
        for c in range(2):
            rms_rstd(c, lambda k: xT[:, k, CH(c)], [xr[k][c] for k in range(8)], 8, 1.0 / D, ones_bf[:])
            for k in range(8):
                i = self.f32_rr % 3
                self.f32_rr += 1
                tt(S, "dve", f32s[:, i, :], xT[:, k, CH(c)], rstd[:], ALU.mult, [xr[k][c], r_rstd], [r_f32s[i]])
                o = o_rr % 2
                o_rr += 1
                act(S, ost[o], f32s[:, i, :], AF.Copy, [r_f32s[i], r_gvec], [r_ost[o]],
                    scale=gvec[:, gfo + k:gfo + k + 1])
                S.dma("sp", yv[:, k, CH(c)], ost[o], reads=[r_ost[o]])
        S.finish()

    def mixer(self, l):
        S = self.S
        C = self.ctx
        hT, hr, CH = C["hT"], C["hr"], C["CH"]
        self.arena_reset()
        ymix = self.carve([128, 8, T], BF16)
        ymr = [[R("ym%d_%d" % (k, c)) for c in range(2)] for k in range(8)]
        base = self.aoff
        self.mod_require(l, 1)
        C["norm_mod"](l, 1)
        win = self.wd["w_in"][l].rearrange("(k p) n -> p k n", p=128)
        self.bank_pool = list(range(8))
        self.mix_mlstm(l, ymix, ymr, win)
        self.arena_reset(base)
        if STAGE >= 3:
            self.mix_attn(l, ymix, ymr, win, glob=True)
            self.arena_reset(base)
            self.mix_attn(l, ymix, ymr, win, glob=False)
            self.arena_reset(base)
        if STAGE >= 4:
            self.mix_hyena(l, ymix, ymr, win)
        self.prefetch(("wout", l, 0), self.parts_wout(l, 0))
        self.prefetch(("wout", l, 1), self.parts_wout(l, 1))
        wd_ = self.wd
        for g in range(2):
            self.prefetch(("gu", l, 2, g), self.parts_gu(wd_["w2_gate"], wd_["w2_up"], l, g))
        self.bank_pool = list(range(8))
        if DEBUG:
            f32s, r_f32s = C["f32s"], C["r_f32s"]
            for k in range(8):
                for c in range(2):
                    i = self.f32_rr % 3
                    self.f32_rr += 1
                    act(S, f32s[:, i, :], ymix[:, k, CH(c)], AF.Copy, [ymr[k][c]], [r_f32s[i]])
                    S.dma("sp", self.dbg_ymix[l, :, k, c * 512:(c + 1) * 512], f32s[:, i, :], reads=[r_f32s[i]])
        wov = self.wd["w_out"][l].rearrange("(k p) n -> p k n", p=128)
        for half in range(2):
            sl, sr = self.wload(self.parts_wout(l, half), key=("wout", l, half))
            s3 = self.slot_view(sl, 8)
            for j in range(4):
                dt_ = half * 4 + j
                for c in range(2):
                    pb, pr = self.bank()
                    groups = [(pb[:], s3[:, k, j * 128:(j + 1) * 128], ymix[:, k, CH(c)], k == 0, k == 7) for k in range(8)]
                    mm(S, groups, [sr] + [ymr[k][c] for k in range(8)], [pr])
                    C["resid_add"](l, 1, dt_, c, pb, pr)

    def parts_attn(self, l, glob):
        win = self.wd["w_in"][l].rearrange("(k p) n -> p k n", p=128)
        sv8 = lambda sl_: self.slot_view(sl_, 8)
        q0 = 1040 if glob else 1552
        k0, v0 = q0 + 256, q0 + 384
        return [(lambda sl_: sv8(sl_)[:, :, 0:256], win[:, :, q0:q0 + 256]),
                (lambda sl_: sv8(sl_)[:, :, 256:384], win[:, :, k0:k0 + 128]),
                (lambda sl_: sv8(sl_)[:, :, 384:448], win[:, :, k0 + 64:k0 + 128]),
                (lambda sl_: sv8(sl_)[:, :, 448:512], win[:, :, k0:k0 + 64]),
                (lambda sl_: sv8(sl_)[:, :, 512:640], win[:, :, v0:v0 + 128])]

    def parts_win(self, l, c0, n):
        win = self.wd["w_in"][l].rearrange("(k p) n -> p k n", p=128)
        return [(lambda sl_: self.slot_view(sl_, 8)[:, :, 0:n], win[:, :, c0:c0 + n])]

    def parts_wout(self, l, half):
        wov = self.wd["w_out"][l].rearrange("(k p) n -> p k n", p=128)
        return [(lambda sl_: self.slot_view(sl_, 8)[:, :, 0:512], wov[:, :, half * 512:(half + 1) * 512])]

    def parts_gu(self, wg, wu, l, g):
        wgv = wg[l].rearrange("(k p) n -> p k n", p=128)
        wuv = wu[l].rearrange("(k p) n -> p k n", p=128)
        c0 = g * 256
        return [(lambda sl_: self.slot_view(sl_, 8)[:, :, 0:256], wgv[:, :, c0:c0 + 256]),
                (lambda sl_: self.slot_view(sl_, 8)[:, :, 256:512], wuv[:, :, c0:c0 + 256])]

    def pbank(self):
        i = self.bank_pool[self.bank_rr % len(self.bank_pool)]
        self.bank_rr += 1
        return self.ps[i], self.psr[i]

    def proj_fm(self, c, s3, col0, pb):
        hT, CH = self.ctx["hT"], self.ctx["CH"]
        return [(pb[:], s3[:, k, col0:col0 + 128], hT[:, k, CH(c)], k == 0, k == 7) for k in range(8)]

    def proj_tok(self, tt_, s3, col0, ncols, pb, pc0):
        hT = self.ctx["hT"]
        return [(pb[:, pc0:pc0 + ncols], hT[:, k, tt_ * 128:(tt_ + 1) * 128], s3[:, k, col0:col0 + ncols], k == 0, k == 7)
                for k in range(8)]

    def mix_mlstm(self, l, ymix, ymr, win):
        S, k_, d = self.S, self.k, self.d
        C = self.ctx
        hr, CH = C["hr"], C["CH"]
        LP = self.LP
        lp = k_["lp"]
        r_k = self.r_k
        hall = [hr[k][c] for k in range(8) for c in range(2)]
        sv8 = lambda sl_: self.slot_view(sl_, 8)
        slA1, srA1 = self.wload(self.parts_win(l, 0, 512), key=("A1", l))
        slA2, srA2 = self.wload(self.parts_win(l, 512, 512), key=("A2", l))
        slG, srG = self.wload(self.parts_win(l, 1024, 16), key=("G", l))
        sA1, sA2, sG = sv8(slA1), sv8(slA2), sv8(slG)
        self.prefetch(("attn", l, True), self.parts_attn(l, True))
        if CUT == 1:
            return
        aqT = self.carve([128, 2, T], BF16)
        akp = self.carve([128, 4, T], BF16)
        ktok = self.carve([128, 8, 256], BF16)
        vaug = self.carve([128, 8, 4, 66], BF16)
        sgo = self.carve([128, 8, 256], BF16)
        gts = self.carve([128, 8, 16], F32)
        lf = self.carve([128, 8, 8], F32)
        cum = self.carve([128, 8, 16], F32)
        call = self.carve([128, 8, 8], F32)
        wall = self.carve([128, 8, 8], F32)
        wkl = self.carve([128, 8, 8], F32)
        wkall = self.carve([128, 8, 8], F32)
        wkm = self.carve([128, 8, 8], F32)
        dec = self.carve([128, 8, 4], F32)
        lfB = self.carve([128, 8, 128], F32)
        E = self.carve([128, 8, 128], F32)
        AT = self.carve([128, 2, 4, 128], BF16)
        hf = self.carve([128, 8, 256], F32)
        numt = self.carve([128, 2, 260], F32)
        h64 = self.carve([128, 2, 256], F32)
        dsm = self.carve([128, 2, 8], F32)
        kwp = self.carve([128, 2, 4, 192], BF16)
        yatok = self.carve([128, 8, 256], BF16)
        snap = self.carve([128, 2, 4, 130], F32)
        e0 = self.carve([128, 2, 2], F32)
        ssall = self.carve([128, 8, 4], F32)
        sq1 = self.carve([128, 2, 256], F32)
        mst = self.carve([128, 64], F32)
        scl = self.carve([128, 2, 8], F32)
        r_aq, r_akp = RL(2, "aq"), RL(2, "akp")
        r_ktok, r_vaug, r_sgo, r_gts = RL(8, "ktok"), RL(8, "vaug"), RL(8, "sgo"), R("gts")
        r_gate = R("gate")
        r_lfB, r_E, r_AT = RL(2, "lfB"), RL(2, "E"), RL(2, "AT")
        r_hf = RL(8, "hf")
        r_num, r_tmpn, r_h64, r_dsm, r_kwp = RL(2, "num"), RL(2, "tmpn"), RL(2, "h64"), RL(2, "dsm"), RL(2, "kwp")
        r_C, r_Cb = RL(2, "C"), RL(2, "Cb")
        r_snap = [[R("snap") for _ in range(4)] for _ in range(2)]
        r_ms, r_scl = R("mst"), R("scl")
        r_ya = RL(8, "ya")
        r_ss = R("ss")
        r_sq1 = RL(2, "sq1")
        S.op("pool", lambda e: e.memset(akp, 0.0), writes=r_akp)
        S.op("pool", lambda e: e.memset(kwp, 0.0), writes=r_kwp)
        S.op("pool", lambda e: e.memset(vaug[:, :, :, 64:65], 1.0), writes=r_vaug)
        if CUT == 2:
            return
        for t2 in range(2):
            for c in range(2):
                pb, pr = self.pbank()
                mm(S, self.proj_fm(c, sA1, t2 * 128, pb), [srA1] + hall, [pr])
                act(S, aqT[:, t2, CH(c)], pb[:], AF.Copy, [pr], [r_aq[c]])
        if CUT == 21:
            return
        for t2 in range(2):
            for c in range(2):
                pb, pr = self.pbank()
                mm(S, self.proj_fm(c, sA1, 256 + t2 * 128, pb), [srA1] + hall, [pr])
                act(S, akp[0:64, 2 * t2, CH(c)], pb[0:64, :], AF.Copy, [pr], [r_akp[c]])
                cp(S, "dve", akp[64:128, 2 * t2 + 1, CH(c)], pb[64:128, :], [pr], [r_akp[c]])
        if CUT == 22:
            return
        BG = LP["BG"]
        for t_ in range(8):
            p1, pr1 = self.pbank()
            p2, pr2 = self.pbank()
            g = self.proj_tok(t_, sA1, 256, 256, p1, 0) + self.proj_tok(t_, sA2, 0, 256, p1, 256)
            g += self.proj_tok(t_, sA2, 256, 256, p2, 0) + self.proj_tok(t_, sG, 0, 16, p2, 256)
            mm(S, g, [srA1, srA2, srG] + hall, [pr1, pr2])
            if CUT == 23:
                continue
            act(S, ktok[:, t_, :], p1[:, 0:256], AF.Copy, [pr1], [r_ktok[t_]])
            cp(S, "dve", vaug[:, t_, :, 0:64], p1[:, 256:512].rearrange("p (a b) -> p a b", a=4), [pr1], [r_vaug[t_]])
            if CUT == 24:
                continue
            act(S, sgo[:, t_, :], p2[:, 0:256], AF.Sigmoid, [pr2], [r_sgo[t_]])
            tt(S, "pool", sgo[:, t_, :], sgo[:, t_, :], lp[:, l, LP["GM"]:LP["GM"] + 256], ALU.mult, [r_sgo[t_], r_k], [r_sgo[t_]])
            tt(S, "dve", gts[:, t_, :], p2[:, 256:272], lp[:, l, BG:BG + 16], ALU.add, [pr2, r_k], [r_gts])
        if CUT in (3, 23, 24):
            return
        ai, af = gts[:, :, 0:8], gts[:, :, 8:16]
        act(S, lf, af, AF.Exp, [r_gts], [r_gate], scale=-1.0)
        act(S, lf, lf, AF.Ln, [r_gate], [r_gate], bias=1.0)
        ts(S, "dve", lf, lf, -1.0, ALU.mult, [r_gate], [r_gate])
        tri = k_["tri"]
        pbc, prc = self.pbank()
        g = []
        for t_ in range(8):
            g.append((pbc[:, t_ * 16:t_ * 16 + 4], tri[:, 0, :], lf[:, t_, 0:4], True, True))
            g.append((pbc[:, t_ * 16 + 4:t_ * 16 + 8], tri[:, 1, :], lf[:, t_, 4:8], True, True))
            g.append((pbc[:, t_ * 16 + 8:t_ * 16 + 16], k_["ones_f"][:], lf[:, t_, 0:8], True, True))
        mm(S, g, [r_gate, r_k], [prc])
        cp(S, "dve", cum, pbc[:, 0:128].rearrange("p (a b) -> p a b", a=8), [prc], [r_gate])
        bc, bt = cum[:, :, 0:8], cum[:, :, 8:16]
        tt(S, "dve", call, ai, bc, ALU.subtract, [r_gts, r_gate], [r_gate])
        ts(S, "dve", call, call, LN8, ALU.add, [r_gate], [r_gate])
        act(S, wall, bc, AF.Exp, [r_gate], [r_gate])
        tt(S, "dve", wkl, call, bt, ALU.add, [r_gate], [r_gate])
        act(S, wkall, wkl, AF.Exp, [r_gate], [r_gate])
        ts(S, "dve", wkm, wkl, -LN8, ALU.add, [r_gate], [r_gate])
        for g_ in range(2):
            rows = slice(g_ * 64, (g_ + 1) * 64)
            act(S, dec[rows, :, :], cum[rows, :, 8 + g_:16:2], AF.Exp, [r_gate], [r_gate])
        if CUT == 4:
            return
        ident_f = k_["ident_f"]
        for dr in range(2):
            pbm, prm = self.pbank()
            g = [(pbm[0:4, t_:t_ + 1], lf[:, t_, dr * 4:dr * 4 + 4], k_["ones_f"][:, 0:1], True, True) for t_ in range(8)]
            mm(S, g, [r_gate, r_k], [prm])
            cp(S, "dve", mst[0:4, dr * 8:dr * 8 + 8], pbm[0:4, 0:8], [prm], [r_ms])
            for hh in range(2):
                pbt, prt = self.pbank()
                for q in range(4):
                    t_ = hh * 4 + q
                    S.op("pe", lambda e, t_=t_, q=q, pbt=pbt: e.transpose(pbt[0:4, q * 128:(q + 1) * 128],
                                                                          wkm[:, t_, dr * 4:dr * 4 + 4], ident_f[:]),
                         [r_gate, r_k], [prt])
                S.op("dve", lambda e, pbt=pbt, hh=hh: e.tensor_reduce(
                    out=mst[0:4, 16 + dr * 8 + hh * 4:16 + dr * 8 + hh * 4 + 4],
                    in_=pbt[0:4, :].rearrange("p (a b) -> p a b", a=4), axis=AX.X, op=ALU.max), [prt], [r_ms])
            bv_ = mst[0:4, dr * 8:dr * 8 + 8].rearrange("p (a b) -> p a b", a=4)
            av_ = mst[0:4, 16 + dr * 8:16 + dr * 8 + 8].rearrange("p (a b) -> p a b", a=4)
            fi, se = (0, 1) if dr == 0 else (1, 0)
            mf = mst[0:4, 32 + dr * 4:32 + dr * 4 + 4]
            ts(S, "dve", mf, bv_[:, :, fi], k_["m0c"][0:4, l * 2 + dr:l * 2 + dr + 1], ALU.add, [r_ms, r_k], [r_ms])
            tt(S, "dve", mf, mf, av_[:, :, fi], ALU.max, [r_ms], [r_ms])
            tt(S, "dve", mf, mf, bv_[:, :, se], ALU.add, [r_ms], [r_ms])
            tt(S, "dve", mf, mf, av_[:, :, se], ALU.max, [r_ms], [r_ms])
            S.dma("sp", d["o_m"][l, dr], mf, reads=[r_ms])
            en = mst[0:4, 40 + dr * 4:40 + dr * 4 + 4]
            act(S, en, mf, AF.Exp, [r_ms], [r_ms], scale=-1.0)
            rhs2 = mst[0:4, 48 + dr * 8:48 + dr * 8 + 8]
            tt(S, "dve", rhs2.rearrange("p (a b) -> p a b", a=4), en.unsqueeze(2).to_broadcast([4, 4, 2]),
               k_["sel"][0:4, 128:130].unsqueeze(1).to_broadcast([4, 4, 2]), ALU.mult, [r_ms, r_k], [r_ms])
            pbs, prs = self.pbank()
            mm(S, [(pbs[:, 0:8], k_["sel"][0:4, 0:128], rhs2, True, True)], [r_ms, r_k], [prs])
            cp(S, "dve", scl[:, dr, :], pbs[:, 0:8], [prs], [r_scl])
        if CUT == 5:
            return
        Clo, Chi, Cblo, Cbhi = k_["Clo"], k_["Chi"], k_["Cblo"], k_["Cbhi"]
        act(S, e0, k_["m0rep"][:, l, :, :], AF.Exp, [r_k], [r_gate])
        halves = ((slice(0, 64), Clo, Cblo), (slice(64, 128), Chi, Cbhi))
        for dr in range(2):
            S.dma("sp", Clo[:, dr, :, :], d["C0"][l, dr, :, :, :], writes=[r_C[dr]])
            tt(S, "dve", Clo[:, dr, :, :], Clo[:, dr, :, :], e0[:, dr, :].unsqueeze(2).to_broadcast([128, 2, 65]),
               ALU.mult, [r_C[dr], r_gate], [r_C[dr]])
            for (rows, Cx, Cbx) in halves:
                act(S, Cbx[rows, dr, :, 0:65], Clo[rows, dr, :, :], AF.Copy, [r_C[dr]], [r_Cb[dr]])
        keep = k_["cflags"][:, 1:2]

        tmpn2 = self.carve([128, 2, 2, 260], F32)
        r_tmpn2 = [[R("tn00"), R("tn01")], [R("tn10"), R("tn11")]]
        v3 = lambda ap: ap.rearrange("p (a b) -> p a b", a=4)

        def state_gen(dr, t_):
            tok = slice(t_ * 128, (t_ + 1) * 128)
            chs = slice(dr * 4, dr * 4 + 4)
            b_ = t_ % 2
            pst, prst = self.ps[3 + 4 * dr], self.psr[3 + 4 * dr]
            tt(S, "pool", kwp[:, dr, :, 64:128], ktok[:, t_, :].rearrange("p (a b) -> p a b", a=4),
               wkall[:, t_, chs].unsqueeze(2).to_broadcast([128, 4, 64]), ALU.mult, [r_ktok[t_], r_gate], [r_kwp[dr]])
            yield
            g = [(pst[:, h * 65:(h + 1) * 65], aqT[:, h // 2, tok], (Cblo if h % 2 == 0 else Cbhi)[:, dr, h // 2, 0:65], True, True)
                 for h in range(4)]
            for j in range(2):
                o = pst[:, 260 + j * 65:260 + (j + 1) * 65]
                g.append((o, kwp[:, dr, 2 * j, 64:192], vaug[:, t_, 2 * j, 0:65], True, False))
                g.append((o, kwp[:, dr, 2 * j + 1, 0:128], vaug[:, t_, 2 * j + 1, 0:65], False, True))
            mm(S, g, r_aq + [r_Cb[dr], r_kwp[dr], r_vaug[t_]], [prst])
            yield
            tt(S, "dve", v3(tmpn2[:, dr, b_, :]), v3(pst[:, 0:260]), wall[:, t_, chs].unsqueeze(2).to_broadcast([128, 4, 65]),
               ALU.mult, [prst, r_gate], [r_tmpn2[dr][b_]])
            yield
            tt(S, "dve", Clo[:, dr, :, :], Clo[:, dr, :, :],
               dec[:, t_, dr * 2:dr * 2 + 2].unsqueeze(2).to_broadcast([128, 2, 65]), ALU.mult,
               [r_C[dr], r_gate], [r_C[dr]])
            yield
            tt(S, "dve", Clo[:, dr, :, :], Clo[:, dr, :, :], pst[:, 260:390].rearrange("p (a b) -> p a b", a=2),
               ALU.add, [r_C[dr], prst], [r_C[dr]])
            yield
            end = (t_ % 2 == 1) if dr == 0 else (t_ % 2 == 0)
            if end:
                sq_ = t_ // 2
                tt(S, "dve", snap[:, dr, sq_, :].rearrange("p (a b) -> p a b", a=2), Clo[:, dr, :, :],
                   scl[:, dr, sq_ * 2:sq_ * 2 + 2].unsqueeze(2).to_broadcast([128, 2, 65]), ALU.mult,
                   [r_C[dr], r_scl], [r_snap[dr][sq_]])
                yield
                S.dma("sp", d["o_C"][l, dr, sq_], snap[:, dr, sq_, :].rearrange("p (a b) -> p a b", a=2),
                      reads=[r_snap[dr][sq_]])
                yield
                ts(S, "dve", Clo[:, dr, :, :], Clo[:, dr, :, :], keep, ALU.mult, [r_C[dr], r_k], [r_C[dr]])
                yield
            for (rows, Cx, Cbx) in halves:
                act(S, Cbx[rows, dr, :, 0:65], Clo[rows, dr, :, :], AF.Copy, [r_C[dr]], [r_Cb[dr]])
                yield

        def out_gen(dr, t_):
            tok = slice(t_ * 128, (t_ + 1) * 128)
            chs = slice(dr * 4, dr * 4 + 4)
            b_ = t_ % 2
            pbe, pre = self.ps[0 + 4 * dr], self.psr[0 + 4 * dr]
            pbs_, prs_ = self.ps[1 + 4 * dr], self.psr[1 + 4 * dr]
            pbi, pri = self.ps[2 + 4 * dr], self.psr[2 + 4 * dr]
            act(S, lfB[:, chs, :], lf[:, t_, chs].unsqueeze(2).to_broadcast([128, 4, 128]), AF.Copy, [r_gate], [r_lfB[dr]])
            yield
            g = []
            for h in range(4):
                o = pbe[:, h * 128:(h + 1) * 128]
                g.append((o, lfB[:, dr * 4 + h, :], tri[:, dr, :], True, False))
                g.append((o, ident_f[:], tri[:, 2 + dr, :], False, True))
            mm(S, g, [r_lfB[dr], r_k], [pre])
            yield
            g = [(pbs_[:, h * 128:(h + 1) * 128], akp[:, h, tok], aqT[:, h // 2, tok], True, True) for h in range(4)]
            mm(S, g, r_akp + r_aq, [prs_])
            yield
            for h in range(4):
                act(S, E[:, dr * 4 + h, :], pbe[:, h * 128:(h + 1) * 128], AF.Exp, [pre, r_gate], [r_E[dr]],
                    bias=call[:, t_, dr * 4 + h:dr * 4 + h + 1])
                yield
            tt(S, "dve", AT[:, dr, :, :], pbs_[:].rearrange("p (a b) -> p a b", a=4), E[:, chs, :], ALU.mult,
               [prs_, r_E[dr]], [r_AT[dr]])
            yield
            g = [(pbi[:, h * 65:(h + 1) * 65], AT[:, dr, h, :], vaug[:, t_, h, 0:65], True, True) for h in range(4)]
            mm(S, g, [r_AT[dr], r_vaug[t_]], [pri])
            yield
            tt(S, "dve", numt[:, dr, :], pbi[:, 0:260], tmpn2[:, dr, b_, :], ALU.add, [pri, r_tmpn2[dr][b_]], [r_num[dr]])
            yield
            nv = v3(numt[:, dr, :])
            dn, rd = dsm[:, dr, 0:4], dsm[:, dr, 4:8]
            ts(S, "dve", dn, nv[:, :, 64], -1.0, ALU.mult, [r_num[dr]], [r_dsm[dr]], s2=1.0, op1=ALU.max)
            yield
            tt(S, "dve", dn, dn, nv[:, :, 64], ALU.max, [r_num[dr], r_dsm[dr]], [r_dsm[dr]])
            yield
            recip(S, rd, dn, [r_dsm[dr]], [r_dsm[dr]])
            yield
            first = (t_ <= 3) if dr == 0 else (t_ >= 4)
            if first:
                tt(S, "dve", v3(hf[:, t_, :]), nv[:, :, 0:64], rd.unsqueeze(2).to_broadcast([128, 4, 64]), ALU.mult,
                   [r_num[dr], r_dsm[dr]], [r_hf[t_]])
                yield
            else:
                tt(S, "dve", v3(h64[:, dr, :]), nv[:, :, 0:64], rd.unsqueeze(2).to_broadcast([128, 4, 64]), ALU.mult,
                   [r_num[dr], r_dsm[dr]], [r_h64[dr]])
                yield
                tt(S, "dve", hf[:, t_, :], hf[:, t_, :], h64[:, dr, :], ALU.add, [r_h64[dr], r_hf[t_]], [r_hf[t_]])
                yield

        def zip_run(gens):
            gens = list(gens)
            while gens:
                for g__ in list(gens):
                    try:
                        next(g__)
                    except StopIteration:
                        gens.remove(g__)

        def pump_gen(npump):
            for _ in range(npump):
                for _ in range(5):
                    yield
                self.mod_pump(bank=(self.ps[2], self.psr[2], 300))
                yield

        order = [list(range(8)), list(range(7, -1, -1))]
        zip_run([state_gen(0, order[0][0]), state_gen(1, order[1][0])])
        for i in range(8):
            gens = [out_gen(0, order[0][i]), out_gen(1, order[1][i])]
            if i + 1 < 8:
                gens = [state_gen(0, order[0][i + 1]), state_gen(1, order[1][i + 1])] + gens
            gens.append(pump_gen(1))
            zip_run(gens)
        GM = LP["GM"]
        for t_ in range(8):
            b_ = t_ % 2
            tt(S, "dve", sq1[:, b_, :], hf[:, t_, :], hf[:, t_, :], ALU.mult, [r_hf[t_]], [r_sq1[b_]])
            S.op("dve", lambda e, t_=t_, b_=b_: e.tensor_reduce(out=ssall[:, t_, :],
                                                              in_=sq1[:, b_, :].rearrange("p (a b) -> p a b", a=4),
                                                              axis=AX.X, op=ALU.add), [r_sq1[b_]], [r_ss])
        act(S, ssall, ssall, AF.Ln, [r_ss], [r_ss], bias=EPS, scale=1.0 / 64)
        act(S, ssall, ssall, AF.Exp, [r_ss], [r_ss], scale=-0.5)
        ident_bf = k_["ident_bf"]
        for t_ in range(8):
            b_ = t_ % 2
            tt(S, "dve", sq1[:, b_, :].rearrange("p (a b) -> p a b", a=4), hf[:, t_, :].rearrange("p (a b) -> p a b", a=4),
               ssall[:, t_, :].unsqueeze(2).to_broadcast([128, 4, 64]), ALU.mult, [r_hf[t_], r_ss], [r_sq1[b_]])
            tt(S, "dve", yatok[:, t_, :], sq1[:, b_, :], sgo[:, t_, :], ALU.mult, [r_sgo[t_], r_sq1[b_]], [r_ya[t_]])
            self.mod_pump()
            pbt, prt = self.pbank()
            pv = pbt[:].bitcast(BF16)
            for t2 in range(2):
                S.op("pe", lambda e, t2=t2, pv=pv, t_=t_: e.transpose(pv[:, t2 * 128:(t2 + 1) * 128],
                                                                      yatok[:, t_, t2 * 128:(t2 + 1) * 128], ident_bf[:]),
                     [r_ya[t_], r_k], [prt])
            c = t_ // 4
            for t2 in range(2):
                if t2 == 0:
                    act(S, ymix[:, t2, t_ * 128:(t_ + 1) * 128], pv[:, t2 * 128:(t2 + 1) * 128], AF.Copy, [prt], [ymr[t2][c]])
                else:
                    cp(S, "dve", ymix[:, t2, t_ * 128:(t_ + 1) * 128], pv[:, t2 * 128:(t2 + 1) * 128], [prt], [ymr[t2][c]])

    def mix_attn(self, l, ymix, ymr, win, glob):
        S, k_, d = self.S, self.k, self.d
        C = self.ctx
        hr, CH = C["hr"], C["CH"]
        LP = self.LP
        lp = k_["lp"]
        r_k = self.r_k
        hall = [hr[k][c] for k in range(8) for c in range(2)]
        sv8 = lambda sl_: self.slot_view(sl_, 8)
        q0 = 1040 if glob else 1552
        k0, v0 = q0 + 256, q0 + 384
        sl, sr = self.wload(self.parts_attn(l, glob), key=("attn", l, glob))
        if glob:
            self.prefetch(("attn", l, False), self.parts_attn(l, False))
        else:
            self.prefetch(("D1", l), self.parts_win(l, 2064, 512))
            self.prefetch(("D2", l), self.parts_win(l, 2576, 256))
        s3 = sv8(sl)
        ymb = 2 if glob else 4
        cs = self.carve([128, 2, T], F32)
        qpad = self.carve([128, 4, T], BF16)
        kfull = self.carve([128, 2, 1280], BF16)
        vpad = self.carve([128, 10, 2, 192], BF16)
        kout = self.carve([128, T], F32)
        vout = self.carve([128, 8, 128], F32)
        PT = self.carve([128, 3, 512], BF16)
        rden = self.carve([128, 2, 512], F32)
        raw = self.carve([128, 2, 512], F32)
        tb = self.carve([128, 2, 512], F32)
        gm = self.carve([128, 2, T], BF16)
        smask = self.carve([128, 8, 384], BF16) if not glob else None
        es = self.carve([128, 4], F32)
        r_cs, r_qp, r_kf, r_vp, r_kout, r_vout = R("cs"), RL(2, "qp"), R("kf"), RL(10, "vp"), R("kout"), R("vout")
        r_PT, r_rden, r_raw, r_tb, r_gm, r_sm, r_es = RL(3, "PT"), RL(2, "rden"), RL(2, "raw"), RL(2, "tb"), R("gm"), R("sm"), R("es")
        S.dma("sp", cs, d["ropeCS"], writes=[r_cs])
        S.op("dve", lambda e: e.memset(qpad, 0.0), writes=r_qp)
        S.op("dve", lambda e: e.memset(vpad[:, 2:10, :, :], 0.0), writes=r_vp[2:])
        kTd, vpd = (d["gkT"], d["gvp"]) if glob else (d["skT"], d["svp"])
        for x in range(2):
            S.dma("pool", kfull[:, x, 0:256], kTd[l, x], writes=[r_kf])
            S.dma("pool", vpad[:, x, :, :], vpd[l, x], writes=[r_vp[x]])
        if glob:
            S.dma("pool", gm[0:5, :, :], d["gmAB"], writes=[r_gm])
        if not glob:
            S.dma("pool", smask, d["swamask"], writes=[r_sm])
            SK = LP["SK"]
            act(S, es, lp[:, l, SK:SK + 4], AF.Exp, [r_k], [r_es])
        GQK = LP["GQK"]
        ropeP = k_["ropeP"]
        rstd2 = self.carve([128, 2, 512], F32)
        r_rstd2 = RL(2, "rstd2")

        def prelude(ti, c, b_):
            col0 = ti * 128
            pb, pr = self.pbank()
            mm(S, self.proj_fm(c, s3, col0, pb), [sr] + hall, [pr])
            yield
            rw = raw[:, b_, :]
            if glob:
                i = self.sq_rr % 2
                self.sq_rr += 1
                sqb, r_sq = C["sqb"], C["r_sq"]
                act(S, sqb[:, i, :], pb[:], AF.Square, [pr], [r_sq[i]])
                yield
                cp(S, "dve", rw, pb[:], [pr], [r_raw[b_]])
                yield
                p2, pr2 = self.pbank()
                mm(S, [(p2[:], k_["blockones"][:], sqb[:, i, :], True, True)], [r_sq[i], r_k], [pr2])
                yield
                rstd, r_rstd = rstd2[:, b_, :], r_rstd2[b_]
                act(S, rstd, p2[:], AF.Ln, [pr2], [r_rstd], bias=EPS, scale=1.0 / 64)
                yield
                act(S, rstd, rstd, AF.Exp, [r_rstd], [r_rstd], scale=-0.5)
                yield
                tt(S, "dve", rw, rw, rstd, ALU.mult, [r_raw[b_], r_rstd], [r_raw[b_]])
                yield
                gcol = GQK + (0 if ti < 2 else 1)
                act(S, rw, rw, AF.Copy, [r_raw[b_], r_k], [r_raw[b_]], scale=lp[:, l, gcol:gcol + 1])
                yield
            else:
                act(S, rw, pb[:], AF.Copy, [pr], [r_raw[b_]])
                yield
            p3, pr3 = self.pbank()
            mm(S, [(p3[:], ropeP[:], rw, True, True)], [r_raw[b_], r_k], [pr3])
            yield
            ta, tb_ = rw, tb[:, b_, :]
            tt(S, "dve", ta, rw, cs[:, 0, CH(c)], ALU.mult, [r_raw[b_], r_cs], [r_raw[b_]])
            yield
            tt(S, "dve", tb_, p3[:], cs[:, 1, CH(c)], ALU.mult, [pr3, r_cs], [r_tb[b_]])
            yield
            if ti < 2:
                for g_ in range(2):
                    rows = slice(g_ * 64, (g_ + 1) * 64)
                    tt(S, "dve", qpad[rows, 2 * ti + g_, CH(c)], ta[rows, :], tb_[rows, :], ALU.add, [r_tb[b_], r_raw[b_]], [r_qp[c]])
                    yield
            elif ti == 2:
                tt(S, "dve", kout[:, CH(c)], ta, tb_, ALU.add, [r_tb[b_], r_raw[b_]], [r_kout])
                yield
                act(S, kfull[:, 0, 256 + c * 512:256 + (c + 1) * 512], kout[:, CH(c)], AF.Copy, [r_kout], [r_kf])
                yield
            else:
                tt(S, "dve", kfull[:, 1, 256 + c * 512:256 + (c + 1) * 512], ta, tb_, ALU.add, [r_tb[b_], r_raw[b_]], [r_kf])
                yield

        its = [(ti, c) for ti in range(4) for c in range(2)]
        for i0 in range(0, 8, 2):
            gens = [prelude(its[i0][0], its[i0][1], 0), prelude(its[i0 + 1][0], its[i0 + 1][1], 1)]
            while gens:
                for g__ in list(gens):
                    try:
                        next(g__)
                    except StopIteration:
                        gens.remove(g__)
        S.dma("sp", (d["o_gk"] if glob else d["o_sk"])[l], kout, reads=[r_kout])
        for t_ in range(8):
            pb, pr = self.pbank()
            mm(S, self.proj_tok(t_, s3, 512, 128, pb, 0), [sr] + hall, [pr])
            act(S, vout[:, t_, :], pb[:, 0:128], AF.Copy, [pr], [r_vout])
            cp(S, "dve", vpad[:, 2 + t_, :, 64:128], pb[:, 0:128].rearrange("p (a b) -> p a b", a=2), [pr], [r_vp[2 + t_]])
        S.dma("sp", (d["o_gv"] if glob else d["o_sv"])[l].rearrange("(t p) n -> p t n", p=128), vout, reads=[r_vout])
        self.bank_pool = [0, 1, 2, 3]
        ones_bf = C["ones_bf"]
        r_ones = C["r_ones"]
        ident_bf = k_["ident_bf"]
        ctxb = k_["cflags"][:, 0:1]
        pt_rr = 0
        acc_rr = 0
        for h in range(4):
            kv, g_ = h // 2, h % 2
            kx = 0 if g_ == kv else 1
            rows = slice(g_ * 64, (g_ + 1) * 64)
            vw = slice(64, 192) if g_ == 0 else slice(0, 128)
            for c in range(2):
                if glob:
                    tiles = [(mt, 0, 512) for mt in range(10)]
                else:
                    tiles = [(0, 0, 512), (1, 0, 512)]
                    for j in range(8):
                        lo, hi = max((j - 1) * 128, c * 512), min((j + 2) * 128, (c + 1) * 512)
                        if hi > lo:
                            tiles.append((2 + j, lo - c * 512, hi - c * 512))
                ai_ = 4 + 2 * (acc_rr % 2)
                acc_rr += 1
                pn, prn, pd, prd = self.ps[ai_], self.psr[ai_], self.ps[ai_ + 1], self.psr[ai_ + 1]
                pend = []

                def qk(idx):
                    mt, lo, hi = tiles[idx]
                    pb, pr = self.pbank()
                    qs = slice(c * 512 + lo, c * 512 + hi)
                    g = [(pb[:, lo:hi], kfull[:, kx, mt * 128:(mt + 1) * 128], qpad[:, h, qs], True, mt < 2)]
                    rd = [r_kf, r_qp[c]]
                    if mt >= 2:
                        if glob:
                            g.append((pb[:, lo:hi], gm[0:5, 0, (mt - 2) * 128:(mt - 1) * 128], gm[0:5, 1, qs], False, True))
                            rd.append(r_gm)
                        else:
                            j = mt - 2
                            m0_ = c * 512 + lo - (j - 1) * 128
                            g.append((pb[:, lo:hi], ident_bf[:], smask[:, j, m0_:m0_ + (hi - lo)], False, True))
                            rd += [r_sm, r_k]
                    mm(S, g, rd, [pr])
                    return pb, pr

                nt = len(tiles)
                look = 2
                for idx in range(min(look, nt)):
                    pend.append(qk(idx))
                for idx in range(nt):
                    mt, lo, hi = tiles[idx]
                    pb, pr = pend.pop(0)
                    if idx + look < nt:
                        pend.append(qk(idx + look))
                    pi = pt_rr % 3
                    pt_rr += 1
                    act(S, PT[:, pi, lo:hi], pb[:, lo:hi], AF.Exp, [pr, r_k], [r_PT[pi]],
                        bias=(ctxb if mt < 2 else 0.0), scale=0.125)
                    g = [(pn[:, lo:hi], vpad[:, mt, kv, vw], PT[:, pi, lo:hi], idx == 0, idx == nt - 1),
                         (pd[:, lo:hi], ones_bf[:], PT[:, pi, lo:hi], idx == 0, idx == nt - 1)]
                    mm(S, g, [r_vp[mt], r_PT[pi], r_ones], [prn, prd])
                ri = (h * 2 + c) % 2
                if glob:
                    recip(S, rden[rows, ri, :], pd[rows, :], [prd], [r_rden[ri]])
                else:
                    ts(S, "dve", rden[rows, ri, :], pd[rows, :], es[rows, h:h + 1], ALU.add, [prd, r_es], [r_rden[ri]])
                    recip(S, rden[rows, ri, :], rden[rows, ri, :], [r_rden[ri]], [r_rden[ri]])
                tt(S, "dve", ymix[rows, ymb + kv, CH(c)], pn[rows, :], rden[rows, ri, :], ALU.mult, [prn, r_rden[ri]],
                   [ymr[ymb + kv][c]])
        self.bank_pool = list(range(8))

    def mix_hyena(self, l, ymix, ymr, win):
        S, k_, d = self.S, self.k, self.d
        C = self.ctx
        hr, CH = C["hr"], C["CH"]
        LP = self.LP
        lp = k_["lp"]
        r_k = self.r_k
        hall = [hr[k][c] for k in range(8) for c in range(2)]
        sv8 = lambda sl_: self.slot_view(sl_, 8)
        TWO_PI = float(2 * math.pi)
        xoff = self.aoff
        feats = self.carve([128, T], F32)
        z1 = self.carve([128, T], F32)
        z2 = self.carve([128, T], F32)
        wnd = self.carve([128, 8, 256], F32)
        rr = self.carve([128, 512], F32)
        ii = self.carve([128, 512], I32)
        kf = self.carve([128, 512], F32)
        xend = self.aoff
        self.aoff = xoff
        raw = self.carve([128, 3, T], F32)
        uct = self.carve([128, 2, T], F32)
        pa = self.carve([128, 2, 256], F32)
        pq = self.carve([128, 2, 256], F32)
        yt = self.carve([128, 2, 256], F32)
        assert self.aoff <= xend
        self.aoff = xend
        r_X = R("X")
        x0 = self.carve([128, 2, T], F32)
        zf = self.carve([128, 2, T], F32)
        zbf = self.carve([128, 2, T], BF16)
        zh = self.carve([128, 8, 512], BF16)
        ZH = self.carve([128, 2, 512], F32)
        Y = self.carve([128, 8, 2, 256], BF16)
        pa2 = self.carve([128, 2, 256], F32)
        yt2 = ZH[:, :, 0:256]
        r_x0, r_zf, r_zbf = RL(2, "x0"), RL(2, "zf"), RL(2, "zbf")
        r_zh = RL(8, "zh")
        r_ZH, r_Y, r_pa, r_pq, r_yt = R("ZH"), RL(8, "Y"), R("pa"), R("pq"), RL(2, "yt")
        S.dma("sp", feats[0:33, :], d["featsT"], writes=[r_X])
        S.dma("sp", wnd, d["window"], writes=[r_X])
        fs = k_["fs"]

        def sin_layer(pb, pr, dst, bcol):
            ts(S, "dve", rr[0:64, :], pb[0:64, :], fs[:, l, 0:1], ALU.mult, [pr, r_k], [r_X], s2=fs[:, l, bcol:bcol + 1],
               op1=ALU.add)
            cp(S, "dve", ii[0:64, :], rr[0:64, :], [r_X], [r_X])
            cp(S, "dve", kf[0:64, :], ii[0:64, :], [r_X], [r_X])
            tt(S, "dve", rr[0:64, :], rr[0:64, :], kf[0:64, :], ALU.subtract, [r_X], [r_X])
            act(S, dst, rr[0:64, :], AF.Sin, [r_X], [r_X], scale=TWO_PI)

        def zip2(gens):
            gens = list(gens)
            while gens:
                for g__ in list(gens):
                    try:
                        next(g__)
                    except StopIteration:
                        gens.remove(g__)

        r_fc = [[R("fc%d%d" % (a_, c_)) for c_ in range(2)] for a_ in range(2)]

        def sin_chain(layer, c):
            cs_ = slice(c * 256, (c + 1) * 256)
            for hh in range(2):
                q_ = slice(c * 512 + hh * 256, c * 512 + (hh + 1) * 256)
                pb, pr = self.pbank()
                if layer == 0:
                    mm(S, [(pb[0:64, 0:256], k_["fw1"][0:33, l, :], feats[0:33, q_], True, True)], [r_X, r_k], [pr])
                else:
                    mm(S, [(pb[0:64, 0:256], k_["fw2"][0:64, l, :], z1[0:64, q_], True, True)], [r_fc[0][c], r_k], [pr])
                yield
                bcol = 1 + layer
                rg = R("tmp")
                ts(S, "dve", rr[0:64, cs_], pb[0:64, 0:256], fs[:, l, 0:1], ALU.mult, [pr, r_k], [r_fc[1][c]],
                   s2=fs[:, l, bcol:bcol + 1], op1=ALU.add)
                yield
                cp(S, "dve", ii[0:64, cs_], rr[0:64, cs_], [r_fc[1][c]], [r_fc[1][c]])
                yield
                cp(S, "dve", kf[0:64, cs_], ii[0:64, cs_], [r_fc[1][c]], [r_fc[1][c]])
                yield
                tt(S, "dve", rr[0:64, cs_], rr[0:64, cs_], kf[0:64, cs_], ALU.subtract, [r_fc[1][c]], [r_fc[1][c]])
                yield
                dst = (z1 if layer == 0 else z2)[0:64, q_]
                act(S, dst, rr[0:64, cs_], AF.Sin, [r_fc[1][c]], [r_fc[0][c] if layer == 0 else r_X], scale=TWO_PI)
                yield

        for c in range(2):
            S.op("dve", lambda e, c=c: e.memset(rr[0:64, c * 256:c * 256 + 1], 0.0), [r_X], [r_fc[1][c], r_fc[0][c]])
        zip2([sin_chain(0, 0), sin_chain(0, 1)])
        zip2([sin_chain(1, 0), sin_chain(1, 1)])
        for c in range(2):
            S.op("dve", lambda e, c=c: e.memset(rr[0:64, c * 256:c * 256 + 1], 0.0), [r_fc[1][c], r_fc[0][c]], [r_X])
        for t_ in range(8):
            pb, pr = self.pbank()
            tok = slice(t_ * 128, (t_ + 1) * 128)
            mm(S, [(pb[:, 0:256], z2[0:64, tok], k_["fw3"][0:64, l, :], True, False),
                   (pb[:, 0:256], k_["ones_f"][0:1, :], k_["fb3"][0:1, l, :], False, True)], [r_X, r_k], [pr])
            tt(S, "dve", zh[:, t_, 256:512], pb[:, 0:256], wnd[:, t_, :], ALU.mult, [pr, r_X], [r_zh[t_]])
        slD1, srD1 = self.wload(self.parts_win(l, 2064, 512), key=("D1", l))
        slD2, srD2 = self.wload(self.parts_win(l, 2576, 256), key=("D2", l))
        sD1, sD2 = sv8(slD1), sv8(slD2)
        Fd, Gd = d["dftF"], d["dftG"]
        fv = lambda m: Fd[m].rearrange("(k p) n -> p k n", p=128)
        gv = lambda m: Gd[m].rearrange("(k p) n -> p k n", p=128)

        def parts_dft(vw, q):
            return [(lambda sl_: sv8(sl_)[:, :, 0:256], vw(0)[:, :, q * 256:(q + 1) * 256]),
                    (lambda sl_: sv8(sl_)[:, :, 256:512], vw(1)[:, :, q * 256:(q + 1) * 256])]

        def zip_run(gens):
            gens = list(gens)
            while gens:
                for g__ in list(gens):
                    try:
                        next(g__)
                    except StopIteration:
                        gens.remove(g__)

        for q in range(2):
            self.prefetch(("F", l, q), parts_dft(fv, q))
        CV = LP["CV"]
        cvb = k_["cvb"]
        r_raw = RL(3, "hraw")
        r_uct = RL(2, "uct")

        def conv_chain(ct, ui):
            s3, col, srr = ((sD1, ct * 128, srD1), (sD1, 256 + ct * 128, srD1), (sD2, ct * 128, srD2))[ui]
            rr_ = r_raw[ui]
            fx = [r_X] if ct == 0 else []
            for c in range(2):
                pb, pr = self.pbank()
                mm(S, self.proj_fm(c, s3, col, pb), [srr] + hall, [pr])
                yield
                act(S, raw[:, ui, CH(c)], pb[:], AF.Copy, [pr], [rr_] + fx)
                yield
            tile = ui * 2 + ct
            cw = lambda j: lp[:, l, CV + tile * 4 + j:CV + tile * 4 + j + 1]
            u = raw[:, ui, :]
            if ui == 0:
                dst, wr = x0[:, ct, :], [r_x0[ct]]
            else:
                dst, wr = uct[:, ui - 1, :], [r_uct[ui - 1]] + fx
            rd = [rr_, r_k]
            act(S, dst, u, AF.Identity, rd, wr, bias=cw(3), scale=cw(1))
            yield
            for (o, a_, sc) in ((dst[:, 1:T], u[:, 0:T - 1], cw(0)), (dst[:, 0:T - 1], u[:, 1:T], cw(2)),
                                (dst[:, 256:T:256], u[:, 255:T - 1:256], cvb[:, l, tile, 0:1]),
                                (dst[:, 255:T - 1:256], u[:, 256:T:256], cvb[:, l, tile, 1:2])):
                S.op("dve", lambda e, o=o, a_=a_, sc=sc: e.scalar_tensor_tensor(out=o, in0=a_, scalar=sc, in1=o,
                                                                              op0=ALU.mult, op1=ALU.add), rd + wr[:1], wr[:1])
                yield

        for ct in range(2):
            zip_run([conv_chain(ct, ui) for ui in range(3)])
            tt(S, "dve", zf[:, ct, :], uct[:, 0, :], uct[:, 1, :], ALU.mult, r_uct, [r_zf[ct]])
            act(S, zbf[:, ct, :], zf[:, ct, :], AF.Copy, [r_zf[ct]], [r_zbf[ct]])
        for q in range(2, 4):
            self.prefetch(("F", l, q), parts_dft(fv, q))
        ident_bf = k_["ident_bf"]
        for t_ in range(8):
            pb, pr = self.pbank()
            pv = pb[:].bitcast(BF16)
            for ct in range(2):
                S.op("pe", lambda e, ct=ct, pv=pv, t_=t_: e.transpose(pv[:, ct * 128:(ct + 1) * 128],
                                                                      zbf[:, ct, t_ * 128:(t_ + 1) * 128], ident_bf[:]),
                     [r_zbf[ct], r_k], [pr])
            cp(S, "dve", zh[:, t_, 0:256], pv[:, 0:256], [pr], [r_zh[t_]])
        ZHs = [ZH, zbf.rearrange("p a b -> p (a b)").bitcast(F32).rearrange("p (a b) -> p a b", a=2)]
        r_ZHs = [r_ZH, R("ZHb")]
        PAs, PQs = [pa, pa2], [pq, yt]
        r_PA, r_PQ = RL(2, "PA"), RL(2, "PQ")
        seen = set()

        def ft_chain(ft, fj, s3, sr):
            bi = ft % 2
            Zb, rz = ZHs[bi], r_ZHs[bi]
            PA, PQ, rpa, rpq = PAs[bi], PQs[bi], r_PA[bi], r_PQ[bi]
            fresh = bi not in seen
            seen.add(bi)
            fz = (r_zbf if (fresh and bi == 1) else [])
            fx = ([r_X] if fresh else [])
            pre_, prr = self.pbank()
            pim, pri = self.pbank()
            g = [(pre_[:], s3[:, t_, fj * 128:(fj + 1) * 128], zh[:, t_, :], t_ == 0, t_ == 7) for t_ in range(8)]
            g += [(pim[:], s3[:, t_, 256 + fj * 128:256 + (fj + 1) * 128], zh[:, t_, :], t_ == 0, t_ == 7) for t_ in range(8)]
            mm(S, g, [sr] + r_zh, [prr, pri])
            yield
            act(S, Zb[:, 0, :], pre_[:], AF.Copy, [prr], [rz] + fz)
            yield
            act(S, Zb[:, 1, :], pim[:], AF.Copy, [pri], [rz])
            yield
            Zr, Hr, Zi, Hi = Zb[:, 0, 0:256], Zb[:, 0, 256:512], Zb[:, 1, 0:256], Zb[:, 1, 256:512]
            tt(S, "dve", PA[:, 0, :], Zr, Hr, ALU.mult, [rz], [rpa] + fx)
            yield
            tt(S, "dve", PQ[:, 0, :], Zr, Hi, ALU.mult, [rz], [rpq] + fx)
            yield
            tt(S, "dve", PA[:, 1, :], Zi, Hi, ALU.mult, [rz], [rpa])
            yield
            tt(S, "dve", PQ[:, 1, :], Zi, Hr, ALU.mult, [rz], [rpq])
            yield
            tt(S, "dve", Y[:, ft, 0, :], PA[:, 0, :], PA[:, 1, :], ALU.subtract, [rpa], [r_Y[ft]])
            yield
            tt(S, "dve", Y[:, ft, 1, :], PQ[:, 0, :], PQ[:, 1, :], ALU.add, [rpq], [r_Y[ft]])
            yield

        for q in range(4):
            sl, sr = self.wload(parts_dft(fv, q), key=("F", l, q))
            s3 = sv8(sl)
            zip_run([ft_chain(q * 2 + fj, fj, s3, sr) for fj in range(2)])
        HB = LP["HB"]
        for q in range(2):
            self.prefetch(("G", l, q), parts_dft(gv, q))
        for q in range(4):
            sl, sr = self.wload(parts_dft(gv, q), key=("G", l, q))
            if q + 2 < 4:
                self.prefetch(("G", l, q + 2), parts_dft(gv, q + 2))
            s3 = sv8(sl)
            ns = slice(q * 256, (q + 1) * 256)
            for ct in range(2):
                pb, pr = self.pbank()
                g = []
                for ft in range(8):
                    g.append((pb[:, 0:256], Y[:, ft, 0, ct * 128:(ct + 1) * 128], s3[:, ft, 0:256], ft == 0, False))
                    g.append((pb[:, 0:256], Y[:, ft, 1, ct * 128:(ct + 1) * 128], s3[:, ft, 256:512], False, ft == 7))
                mm(S, g, [sr] + r_Y, [pr])
                ytb, ryt = (yt2[:, ct, :], r_yt[ct])
                act(S, ytb, pb[:, 0:256], AF.Copy, [pr], [ryt] + ([r_PA[1], r_ZHs[0]] if q == 0 else []))
                S.op("dve", lambda e, ytb=ytb, ct=ct: e.scalar_tensor_tensor(out=ytb, in0=zf[:, ct, ns],
                                                                           scalar=lp[:, l, HB + ct:HB + ct + 1], in1=ytb,
                                                                           op0=ALU.mult, op1=ALU.add),
                     [r_zf[ct], ryt, r_k], [ryt])
                tt(S, "dve", ymix[:, 6 + ct, ns], x0[:, ct, ns], ytb, ALU.mult, [r_x0[ct], ryt], [ymr[6 + ct][q // 2]])


def fm(v):
    return np.ascontiguousarray(np.asarray(v, np.float32).reshape(8, 128).T)


def _consts(kind):
    c = {}
    p = np.arange(128)
    t = np.arange(T)
    ident = np.eye(128, dtype=np.float32)
    c["ident"] = ident
    bo = np.zeros((128, 128), np.float32)
    bo[:64, :64] = 1
    bo[64:, 64:] = 1
    c["blockones"] = bo
    r_, t_ = np.meshgrid(p, p, indexing="ij")
    tri = np.zeros((128, 4, 128), np.float32)
    tri[:, 0, :] = (r_ <= t_)
    tri[:, 1, :] = (r_ >= t_)
    tri[:, 2, :] = np.where(r_ <= t_, 0.0, NEG)
    tri[:, 3, :] = np.where(r_ >= t_, 0.0, NEG)
    c["tri"] = tri
    P = np.zeros((128, 128), np.float32)
    for b in range(0, 128, 32):
        for i in range(16):
            P[b + i + 16, b + i] = -1.0
            P[b + i, b + i + 16] = 1.0
    c["ropeP"] = P
    cs = np.zeros((128, 2, T), np.float32)
    if kind == "s":
        dd = p % 64
        inv = (10000.0 ** (-(dd % 16).astype(np.float32) / np.float32(16))).astype(np.float32)
        row = (t // 64).astype(np.float32)
        col = (t % 64).astype(np.float32)
        pos = np.where((dd // 32)[:, None] == 0, row[None, :], col[None, :]).astype(np.float32)
        ang = (pos * inv[:, None]).astype(np.float32)
        cs[:, 0, :] = np.cos(ang)
        cs[:, 1, :] = np.sin(ang)
    else:
        cs[:, 0, :] = 1.0
    c["ropeCS"] = cs
    sm = np.full((128, 8, 384), NEG, np.float32)
    for j in range(8):
        m = j * 128 + p[:, None]
        q = (j - 1) * 128 + np.arange(384)[None, :]
        inr = (q >= 0) & (q < T)
        if kind == "s":
            ok = (np.abs(m - q) <= 128) & inr
        else:
            ok = ((m // 256) == (q // 256)) & inr
        sm[:, j, :] = np.where(ok, 0.0, NEG)
    c["swamask"] = sm
    gm = np.zeros((5, 2, T), np.float32)
    if kind == "p":
        gm[0, 0, :] = 1.0
        gm[0, 1, :] = -BIG
        for s_ in range(4):
            gm[1 + s_, 0, :] = (t // 256 == s_)
            gm[1 + s_, 1, :] = BIG * (t // 256 == s_)
    c["gmAB"] = gm
    c["cflags"] = np.tile(np.array([[0.0, 1.0, 0.0, 0.0]] if kind == "s" else [[NEG, 0.0, 1.0, 0.0]], np.float32), (128, 1))
    sel = np.zeros((4, 130), np.float32)
    for h in range(4):
        sel[h, 0:128] = ((p >= 64).astype(int) == (h % 2))
        sel[h, 128 + h // 2] = 1.0
    c["sel"] = sel
    L = T if kind == "s" else 256
    rep = T // L
    pos = np.arange(L, dtype=np.float32)
    t01 = pos / np.float32(max(L - 1, 1))
    lin = np.linspace(1e-4, 15.0, 16, dtype=np.float32)
    ang = (np.float32(2.0 * math.pi / L) * pos[:, None] * lin[None, :]).astype(np.float32)
    feats = np.concatenate([t01[:, None], np.cos(ang), -np.sin(ang)], -1).astype(np.float32)
    c["featsT"] = np.ascontiguousarray(np.tile(feats, (rep, 1)).T)
    centre = L // 2
    dist = np.abs(pos - centre) / np.float32(max(centre, 1))
    deltas = np.abs(np.linspace(math.log(0.01) / 1.5, math.log(0.01) / 0.3, 256, dtype=np.float32))
    wnd = np.exp(-dist[:, None] * deltas[None, :]).astype(np.float32)
    c["window"] = np.ascontiguousarray(np.tile(wnd, (rep, 1)).reshape(8, 128, 256).transpose(1, 0, 2))
    N = 2 * L
    tt_ = np.arange(L, dtype=np.float64)
    ff = np.arange(L, dtype=np.float64)
    th = math.pi * (2 * ff + 1) / N
    Fc = np.cos(tt_[:, None] * th[None, :])
    Fs = -np.sin(tt_[:, None] * th[None, :])
    Gc = (2.0 / N) * np.cos(th[:, None] * (tt_[None, :] + L // 2))
    Gs = -(2.0 / N) * np.sin(th[:, None] * (tt_[None, :] + L // 2))
    dF = np.zeros((2, T, T), np.float32)
    dG = np.zeros((2, T, T), np.float32)
    for s_ in range(rep):
        sl = slice(s_ * L, (s_ + 1) * L)
        dF[0, sl, sl] = Fc
        dF[1, sl, sl] = Fs
        dG[0, sl, sl] = Gc
        dG[1, sl, sl] = Gs
    c["dftF"] = dF
    c["dftG"] = dG
    return c


def host_inputs(inp, cores=None):
    f = lambda a: np.ascontiguousarray(np.asarray(a, dtype=np.float32))
    A = {k: np.asarray(v) for k, v in inp.items()}
    shared = {}
    for nm in ("w_ada", "w1_gate", "w1_up", "w1_down", "w_in", "w_out", "w2_gate", "w2_up", "w2_down"):
        shared[nm] = f(A[nm])
    shared["b_adaT"] = f(A["b_ada"].reshape(NL, 72, 128).transpose(0, 2, 1))
    gv = []
    for l in range(NL):
        gv += [fm(A["g_ff1"][l]), fm(A["g_mix"][l]), fm(A["g_ff2"][l])]
    gv.append(fm(A["g_final"]))
    shared["gvec"] = f(np.concatenate(gv, axis=1))
    lp = np.zeros((128, NL, 304), np.float32)
    p = np.arange(128)
    for l in range(NL):
        lp[:, l, 0:16] = A["b_gates"][l][None, :]
        lp[:, l, 16:272] = A["g_mlstm"][l][None, :]
        lp[:, l, 272] = A["g_qnorm"][l][p % 64]
        lp[:, l, 273] = A["g_knorm"][l][p % 64]
        lp[:, l, 274:278] = A["sinks"][l][None, :]
        for i in range(6):
            ch = i * 128 + p
            lp[:, l, 278 + i * 4 + 0] = A["conv_w"][l][0, ch]
            lp[:, l, 278 + i * 4 + 1] = A["conv_w"][l][1, ch]
            lp[:, l, 278 + i * 4 + 2] = A["conv_w"][l][2, ch]
            lp[:, l, 278 + i * 4 + 3] = A["conv_b"][l][ch]
        for ct in range(2):
            lp[:, l, 302 + ct] = A["hyena_bias"][l][ct * 128 + p]
    shared["lp"] = lp
    shared["fw1"] = f(A["filt_w1"].transpose(1, 0, 2))
    shared["fw2"] = f(A["filt_w2"].transpose(1, 0, 2))
    shared["fw3"] = f(A["filt_w3"].transpose(1, 0, 2))
    shared["fvec"] = f(np.stack([A["filt_b1"], A["filt_b2"], A["filt_freq"]], -1).transpose(1, 0, 2))
    shared["fb3"] = f(A["filt_b3"][None, :, :])
    cst = {"s": _consts("s"), "p": _consts("p")}
    maps = []
    xs, xp = A["x_sample"], A["x_prompt"]
    for core in (range(8) if cores is None else cores):
        m = dict(shared)
        kind = "s" if core < 4 else "p"
        m.update(cst[kind])
        C0 = np.zeros((NL, 2, 128, 2, 65), np.float32)
        m0rep = np.zeros((128, NL, 2, 2), np.float32)
        m0c = np.zeros((4, NL * 2), np.float32)
        kT = {n: np.zeros((NL, 2, 128, 256), np.float32) for n in ("gkT", "skT")}
        vp = {n: np.zeros((NL, 2, 128, 2, 192), np.float32) for n in ("gvp", "svp")}
        if core < 4:
            b = core
            m["xT"] = f(xs[b].T)
            m["cvec"] = fm(A["c"][b])
            sC, sn, smm = A["state_mlstm_C"][b], A["state_mlstm_n"][b], A["state_mlstm_m"][b]
            for g_ in range(2):
                for pr_ in range(2):
                    h = 2 * pr_ + g_
                    C0[:, :, g_ * 64:(g_ + 1) * 64, pr_, 0:64] = sC[:, :, h]
                    C0[:, :, g_ * 64:(g_ + 1) * 64, pr_, 64] = sn[:, :, h]
                    m0rep[g_ * 64:(g_ + 1) * 64, :, :, pr_] = smm[None, :, :, h]
            for l in range(NL):
                for dr in range(2):
                    m0c[:, l * 2 + dr] = smm[l, dr, :]
            for (kn, vn, ck_, cv_) in (("gkT", "gvp", "cache_gattn_k", "cache_gattn_v"), ("skT", "svp", "cache_swa_k", "cache_swa_v")):
                ck, cv = A[ck_][b], A[cv_][b]
                t1 = ck.transpose(0, 2, 3, 1).reshape(NL, 128, 256)
                t2 = ck[:, :, ::-1, :].transpose(0, 2, 3, 1).reshape(NL, 128, 256)
                kT[kn][:, 0] = t1
                kT[kn][:, 1] = t2
                vp[vn][:, :, :, :, 64:128] = cv.reshape(NL, 2, 128, 2, 64)
        else:
            j = core - 4
            m["xT"] = f(xp[4 * j:4 * j + 4].reshape(T, D).T)
            m["cvec"] = fm(A["c_ctx"])
        m["C0"], m["m0rep"], m["m0c"] = C0, m0rep, m0c
        m.update(kT)
        m.update(vp)
        maps.append(m)
    return maps


_PROG = None


def get_prog():
    global _PROG
    if _PROG is None:
        _PROG = Prog()
    return _PROG


def run_device(inputs, trace=False, cores=None):
    prog = get_prog()
    maps = host_inputs(inputs, cores)
    maps = [{k: np.ascontiguousarray(v, dtype=np.float32) for k, v in m.items() if k in prog.din} for m in maps]
    for m in maps:
        for k, shp in prog.din.items():
            assert m[k].shape == shp, (k, m[k].shape, shp)
    res = run_bass_kernel_spmd(prog.nc, maps, core_ids=list(range(len(maps))), trace=trace)
    return res


def assemble(res):
    r = res.results
    yp = np.zeros((16, 256, D), np.float32)
    ys = np.zeros((4, T, D), np.float32)
    nC = np.zeros((16, NL, 2, 4, 64, 64), np.float32)
    nn = np.zeros((16, NL, 2, 4, 64), np.float32)
    nm = np.zeros((16, NL, 2, 4), np.float32)
    kv = {n: np.zeros((16, NL, 256, 2, 64), np.float32) for n in ("o_gk", "o_gv", "o_sk", "o_sv")}
    for core in range(8):
        y = np.ascontiguousarray(r[core]["yT"].T)
        if core < 4:
            ys[core] = y
            continue
        j = core - 4
        yp[4 * j:4 * j + 4] = y.reshape(4, 256, D)
        for n in ("o_gk", "o_sk"):
            a = r[core][n].reshape(NL, 2, 64, 4, 256)
            kv[n][4 * j:4 * j + 4] = a.transpose(3, 0, 4, 1, 2)
        for n in ("o_gv", "o_sv"):
            a = r[core][n].reshape(NL, 4, 256, 2, 64)
            kv[n][4 * j:4 * j + 4] = a.transpose(1, 0, 2, 3, 4)
        oc = r[core]["o_C"].reshape(NL, 2, 4, 2, 64, 2, 65)
        oc = oc.transpose(2, 0, 1, 5, 3, 4, 6).reshape(4, NL, 2, 4, 64, 65)
        nC[4 * j:4 * j + 4] = oc[..., 0:64]
        nn[4 * j:4 * j + 4] = oc[..., 64]
        nm[4 * j:4 * j + 4] = r[core]["o_m"].transpose(3, 0, 1, 2)
    return (yp, ys, nC, nn, nm, kv["o_gk"], kv["o_gv"], kv["o_sk"], kv["o_sv"])


def kernel(**inputs):
    res = run_device(inputs)
    return assemble(res)
```
